# Optimizing a Trainium2 kernel written in Bass

```python
import math
import jax
import jax.numpy as jnp
from jax import lax
import numpy as np

D_MODEL = 1024
BATCH = 8
SEQ = 4096
DEPTH = 4

CTX_LEN = 256
GRID_W = 64
HEAD_DIM = 64
W_HY = D_MODEL // 4
W_RW = D_MODEL // 4
W_WA = D_MODEL // 4
W_FA = D_MODEL // 4
D_MIX = W_HY + W_RW + W_WA + W_FA
HY_EMB = 33
HY_BANDS = (HY_EMB - 1) // 2
HY_FFN = 64
HY_FAST_DECAY = 0.3
HY_SLOW_DECAY = 1.5
HY_DECAY_TARGET = 1e-2
RW_HEADS = W_RW // HEAD_DIM
RW_LORA_W = 64
RW_LORA_A = 64
RW_GN_EPS = 64e-5
WA_HEADS = W_WA // HEAD_DIM
WA_KV = WA_HEADS // 2
WINDOW = 128
Q_BLOCK = 128
FA_HEADS = W_FA // HEAD_DIM
FA_KV = FA_HEADS // 2
ROPE_THETA = 10000.0
NORM_EPS = 1e-6
NEG_INF = -1e30
SPLITS = (3 * W_HY, W_HY, 3 * W_RW + RW_LORA_W + RW_LORA_A, W_RW,
          W_WA + 2 * WA_KV * HEAD_DIM, W_WA, W_FA + 2 * FA_KV * HEAD_DIM, W_FA)
D_IN = sum(SPLITS)
SPLIT_IDX = tuple(int(i) for i in np.cumsum(SPLITS)[:-1])

kernel_name = 'hymba_style_flow_hybrid_block'


def rms_norm(x, g):
    xf = x.astype(jnp.float32)
    y = xf * lax.rsqrt(jnp.mean(xf * xf, axis=-1, keepdims=True) + NORM_EPS)
    return (y * g.astype(jnp.float32)).astype(x.dtype)


def short_conv(z, w):
    L = z.shape[1]
    zp = jnp.pad(z, ((0, 0), (1, 1), (0, 0)))
    return zp[:, :L] * w[0] + zp[:, 1:L + 1] * w[1] + zp[:, 2:] * w[2]


def heads(t):
    return t.reshape(t.shape[:-1] + (-1, HEAD_DIM))


def group_q(t, n_kv):
    return t.reshape(t.shape[:-1] + (n_kv, -1, HEAD_DIM))


def grid_rope_tables(rows):
    row = jnp.repeat(jnp.arange(rows, dtype=jnp.float32), GRID_W)
    col = jnp.tile(jnp.arange(GRID_W, dtype=jnp.float32), rows)
    n_freq = HEAD_DIM // 4
    inv_freq = ROPE_THETA ** (-jnp.arange(n_freq, dtype=jnp.float32) / n_freq)
    ang = jnp.stack([row[:, None] * inv_freq, col[:, None] * inv_freq], axis=1)
    return jnp.cos(ang), jnp.sin(ang)


def apply_rope(x, cos, sin):
    L = x.shape[1]
    bshape = (1, L) + (1,) * (x.ndim - 3) + cos.shape[1:]
    c = cos.reshape(bshape)
    s = sin.reshape(bshape)
    xr = x.astype(jnp.float32).reshape(x.shape[:-1] + (2, 2, HEAD_DIM // 4))
    xa, xb = xr[..., 0, :], xr[..., 1, :]
    out = jnp.stack([xa * c - xb * s, xb * c + xa * s], axis=-2)
    return out.reshape(x.shape).astype(x.dtype)


def hyena_filters(L, fw1, fb1, freq, fw2, fb2, fw3):
    f32 = lambda a: a.astype(jnp.float32)
    t = jnp.linspace(0.0, 1.0, L, dtype=jnp.float32)[:, None]
    w = (2.0 * math.pi / L) * jnp.arange(L, dtype=jnp.float32)[:, None]
    f = jnp.linspace(1e-4, HY_BANDS - 1, HY_BANDS, dtype=jnp.float32)[None, :]
    z = jnp.concatenate([t, jnp.cos(f * w), jnp.sin(f * w)], axis=-1)
    h = jnp.sin(f32(freq) * (z @ f32(fw1) + f32(fb1)))
    h = jnp.sin(f32(freq) * (h @ f32(fw2) + f32(fb2)))
    h = (h @ f32(fw3)).reshape(L, 2, 2, W_HY)
    max_decay = math.log(HY_DECAY_TARGET) / HY_FAST_DECAY
    min_decay = math.log(HY_DECAY_TARGET) / HY_SLOW_DECAY
    deltas = jnp.linspace(min_decay, max_decay, W_HY, dtype=jnp.float32)
    h = h * jnp.exp(-t * jnp.abs(deltas))[:, None, None, :]
    return h / jnp.sum(jnp.abs(h), axis=(0, 2), keepdims=True)


def two_sided_fftconv(u, h_fwd, h_bwd, bias):
    L = u.shape[1]
    k = jnp.concatenate([h_fwd, jnp.zeros_like(h_fwd[:1]), h_bwd[:0:-1]], axis=0)
    U = jnp.fft.rfft(u, n=2 * L, axis=1)
    K = jnp.fft.rfft(k, n=2 * L, axis=0)
    y = jnp.fft.irfft(U * K[None], n=2 * L, axis=1)[:, :L]
    return y + u * bias


def hyena_branch(z, conv_w, fw1, fb1, freq, fw2, fb2, fw3, bias):
    h = hyena_filters(z.shape[1], fw1, fb1, freq, fw2, fb2, fw3)
    zc = short_conv(z.astype(jnp.float32), conv_w.astype(jnp.float32))
    v, x1, x2 = jnp.split(zc, 3, axis=-1)
    bias = bias.astype(jnp.float32)
    y = x1 * two_sided_fftconv(v, h[:, 0, 0], h[:, 0, 1], bias[0])
    return x2 * two_sided_fftconv(y, h[:, 1, 0], h[:, 1, 1], bias[1])


def rwkv_prep(z, conv_w, w0, w_up, a0, a_up, k_k, k_a):
    z = short_conv(z.astype(jnp.float32), conv_w.astype(jnp.float32))
    r, k, v, w_low, a_low = jnp.split(
        z, [W_RW, 2 * W_RW, 3 * W_RW, 3 * W_RW + RW_LORA_W], axis=-1)
    kk = heads(k * k_k)
    kk = kk / jnp.maximum(jnp.linalg.norm(kk, axis=-1, keepdims=True), 1e-12)
    w_low = jnp.tanh(w_low)
    per_dir = []
    for d in range(2):
        w_log = -jax.nn.softplus(-(w0[d] + w_low @ w_up[d])) - 0.5
        a = jax.nn.sigmoid(a0[d] + a_low @ a_up[d])
        k_d = k * (1.0 + (a - 1.0) * k_a)
        per_dir.append((heads(jnp.exp(-jnp.exp(w_log))), heads(a), heads(k_d)))
    return heads(r), heads(v), kk, per_dir


def rwkv7_scan(s0, r, w, k, v, kk, a, reverse):
    def step(S, inp):
        r_t, w_t, k_t, v_t, kk_t, a_t = inp
        sa = jnp.einsum('bhvk,bhk->bhv', S, -kk_t)
        S = (S * w_t[:, :, None, :] + sa[..., None] * (kk_t * a_t)[:, :, None, :]
             + v_t[..., None] * k_t[:, :, None, :])
        return S, jnp.einsum('bhvk,bhk->bhv', S, r_t)
    xs = tuple(jnp.moveaxis(t, 1, 0) for t in (r, w, k, v, kk, a))
    S, ys = lax.scan(step, s0, xs, reverse=reverse)
    return S, jnp.moveaxis(ys, 0, 1)


def rwkv_readout(y, r, v, k_fwd, k_bwd, r_k, ln_g, ln_b):
    mu = jnp.mean(y, axis=-1, keepdims=True)
    var = jnp.mean(jnp.square(y - mu), axis=-1, keepdims=True)
    yn = ((y - mu) * lax.rsqrt(var + RW_GN_EPS)).reshape(y.shape[:-2] + (-1,)) * ln_g + ln_b
    bonus = jnp.sum(r * (k_fwd + k_bwd) * heads(r_k), axis=-1, keepdims=True) * v
    return yn + bonus.reshape(yn.shape)


def rwkv_mixer(z_ctx, z_lat, conv_w, w0, w_up, a0, a_up, k_k, k_a, r_k, ln_g, ln_b, with_ctx_out):
    rc, vc, kkc, dirs_c = rwkv_prep(z_ctx, conv_w, w0, w_up, a0, a_up, k_k, k_a)
    rl, vl, kkl, dirs_l = rwkv_prep(z_lat, conv_w, w0, w_up, a0, a_up, k_k, k_a)
    s0 = jnp.zeros((z_lat.shape[0], RW_HEADS, HEAD_DIM, HEAD_DIM), jnp.float32)
    ys_c, ys_l = [], []
    for d, reverse in enumerate((False, True)):
        wc, ac, kc = dirs_c[d]
        wl, al, kl = dirs_l[d]
        s_ctx, y_c = rwkv7_scan(s0, rc, wc, kc, vc, kkc, ac, reverse)
        _, y_l = rwkv7_scan(s_ctx, rl, wl, kl, vl, kkl, al, reverse)
        ys_c.append(y_c)
        ys_l.append(y_l)
    out_l = rwkv_readout(ys_l[0] + ys_l[1], rl, vl, dirs_l[0][2], dirs_l[1][2], r_k, ln_g, ln_b)
    if not with_ctx_out:
        return out_l, None
    out_c = rwkv_readout(ys_c[0] + ys_c[1], rc, vc, dirs_c[0][2], dirs_c[1][2], r_k, ln_g, ln_b)
    return out_l, out_c


def full_attention(q, k, v, sink):
    s = jnp.einsum('bqkgd,bskd->bkgqs', q, k, preferred_element_type=jnp.float32) * HEAD_DIM ** -0.5
    if sink is not None:
        B, KV, G, Q, _ = s.shape
        sk = jnp.broadcast_to(sink.astype(jnp.float32).reshape(1, KV, G, 1, 1), (B, KV, G, Q, 1))
        p = jax.nn.softmax(jnp.concatenate([s, sk], axis=-1), axis=-1)[..., :-1]
    else:
        p = jax.nn.softmax(s, axis=-1)
    return jnp.einsum('bkgqs,bskd->bqkgd', p, v.astype(jnp.float32))


def window_attention(q, k, v, kc, vc, sink):
    B, L, KV, G, HD = q.shape
    nb = L // Q_BLOCK
    side = WINDOW // Q_BLOCK
    band = (2 * side + 1) * Q_BLOCK
    qb = q.reshape(B, nb, Q_BLOCK, KV, G, HD)

    def banded(t):
        tp = jnp.pad(t, ((0, 0), (WINDOW, WINDOW), (0, 0), (0, 0))).reshape(
            B, nb + 2 * side, Q_BLOCK, KV, HD)
        return jnp.concatenate([tp[:, j:j + nb] for j in range(2 * side + 1)], axis=2)

    kw, vw = banded(k), banded(v)
    scale = HEAD_DIM ** -0.5
    s_loc = jnp.einsum('bnqkgd,bnskd->bnkgqs', qb, kw, preferred_element_type=jnp.float32) * scale
    qpos = jnp.arange(nb)[:, None] * Q_BLOCK + jnp.arange(Q_BLOCK)[None, :]
    kpos = jnp.arange(nb)[:, None] * Q_BLOCK - WINDOW + jnp.arange(band)[None, :]
    valid = ((jnp.abs(qpos[:, :, None] - kpos[:, None, :]) <= WINDOW)
             & (kpos[:, None, :] >= 0) & (kpos[:, None, :] < L))
    s_loc = jnp.where(valid[None, :, None, None], s_loc, NEG_INF)
    s_ctx = jnp.einsum('bnqkgd,bckd->bnkgqc', qb, kc, preferred_element_type=jnp.float32) * scale
    sk = jnp.broadcast_to(sink.astype(jnp.float32).reshape(1, 1, KV, G, 1, 1),
                          (B, nb, KV, G, Q_BLOCK, 1))
    p = jax.nn.softmax(jnp.concatenate([s_loc, s_ctx, sk], axis=-1), axis=-1)
    out = (jnp.einsum('bnkgqs,bnskd->bnqkgd', p[..., :band], vw.astype(jnp.float32))
           + jnp.einsum('bnkgqc,bckd->bnqkgd', p[..., band:-1], vc.astype(jnp.float32)))
    return out.reshape(B, L, KV * G * HD)


def dense_attention(q, k_all, v_all):
    B, L, KV, G, HD = q.shape
    nb = L // Q_BLOCK
    qb = jnp.moveaxis(q.reshape(B, nb, Q_BLOCK, KV, G, HD), 1, 0)
    out = lax.map(lambda qi: full_attention(qi, k_all, v_all, None), qb)
    return jnp.moveaxis(out, 0, 1).reshape(B, L, KV * G * HD)


def mixer_layer(x, ctx, c, c_ctx, mod_w, mod_b, norm_g, w_in, w_out,
                hy_conv, hy_fw1, hy_fb1, hy_freq, hy_fw2, hy_fb2, hy_fw3, hy_bias,
                rw_conv, rw_w0, rw_w_up, rw_a0, rw_a_up, rw_k_k, rw_k_a, rw_r_k, rw_ln_g, rw_ln_b,
                wa_sink, fa_q_norm, fa_k_norm, rope_cos, rope_sin, with_ctx_out):
    dt = x.dtype
    B, L, _ = x.shape
    C = ctx.shape[1]
    shift, scale, gate = jnp.split(jax.nn.silu(c) @ mod_w + mod_b, 3, axis=-1)
    shift_c, scale_c, gate_c = jnp.split(jax.nn.silu(c_ctx) @ mod_w + mod_b, 3, axis=-1)
    h_l = rms_norm(x, norm_g) * (1.0 + scale[:, None]) + shift[:, None]
    h_c = rms_norm(ctx, norm_g) * (1.0 + scale_c) + shift_c
    hy_l, hyg_l, rw_l, rwg_l, wa_l, wag_l, fa_l, fag_l = jnp.split(h_l @ w_in, SPLIT_IDX, axis=-1)
    hy_c, hyg_c, rw_c, rwg_c, wa_c, wag_c, fa_c, fag_c = jnp.split(h_c @ w_in, SPLIT_IDX, axis=-1)

    a_l = hyena_branch(hy_l, hy_conv, hy_fw1, hy_fb1, hy_freq, hy_fw2, hy_fb2, hy_fw3, hy_bias) * jax.nn.silu(hyg_l)

    b_l, b_c = rwkv_mixer(rw_c, rw_l, rw_conv, rw_w0, rw_w_up, rw_a0, rw_a_up, rw_k_k, rw_k_a,
                          rw_r_k, rw_ln_g, rw_ln_b, with_ctx_out)
    b_l = b_l * jax.nn.silu(rwg_l)

    wa_idx = [W_WA, W_WA + WA_KV * HEAD_DIM]
    q_w, k_w, v_w = jnp.split(wa_l, wa_idx, axis=-1)
    q_cw, k_cw, v_cw = jnp.split(wa_c, wa_idx, axis=-1)
    k_cw, v_cw = heads(k_cw), heads(v_cw)
    c_l = window_attention(apply_rope(group_q(q_w, WA_KV), rope_cos, rope_sin),
                           apply_rope(heads(k_w), rope_cos, rope_sin), heads(v_w),
                           k_cw, v_cw, wa_sink) * jax.nn.silu(wag_l)

    fa_idx = [W_FA, W_FA + FA_KV * HEAD_DIM]
    q_f, k_f, v_f = jnp.split(fa_l, fa_idx, axis=-1)
    q_cf, k_cf, v_cf = jnp.split(fa_c, fa_idx, axis=-1)
    k_cf, v_cf = rms_norm(heads(k_cf), fa_k_norm), heads(v_cf)
    q_f = apply_rope(rms_norm(group_q(q_f, FA_KV), fa_q_norm), rope_cos, rope_sin)
    k_f = apply_rope(rms_norm(heads(k_f), fa_k_norm), rope_cos, rope_sin)
    k_all = jnp.concatenate([k_cf, k_f], axis=1)
    v_all = jnp.concatenate([v_cf, heads(v_f)], axis=1)
    d_l = dense_attention(q_f, k_all, v_all) * jax.nn.silu(fag_l)

    mix_l = jnp.concatenate([a_l, b_l, c_l, d_l], axis=-1).astype(dt)
    x = x + (gate[:, None] * (mix_l @ w_out)).astype(dt)
    if not with_ctx_out:
        return x, None

    a_c = hyena_branch(hy_c, hy_conv, hy_fw1, hy_fb1, hy_freq, hy_fw2, hy_fb2, hy_fw3, hy_bias) * jax.nn.silu(hyg_c)
    b_c = b_c * jax.nn.silu(rwg_c)
    c_c = full_attention(group_q(q_cw, WA_KV), k_cw, v_cw, wa_sink).reshape(B, C, W_WA) * jax.nn.silu(wag_c)
    d_c = full_attention(rms_norm(group_q(q_cf, FA_KV), fa_q_norm), k_cf, v_cf, None).reshape(B, C, W_FA) * jax.nn.silu(fag_c)
    mix_c = jnp.concatenate([a_c, b_c, c_c, d_c], axis=-1).astype(ctx.dtype)
    ctx = ctx + (gate_c * (mix_c @ w_out)).astype(ctx.dtype)
    return x, ctx


def setup_inputs(seed: int = 0) -> dict:
    key = jax.random.key(seed)
    ks = jax.random.split(key, 31)
    f32 = jnp.float32

    def nrm(k, shape, s):
        return jax.random.normal(k, shape, f32) * s

    return {
        'x': nrm(ks[0], (BATCH, SEQ, D_MODEL), 1.0),
        'c': nrm(ks[1], (BATCH, D_MODEL), 1.0),
        'ctx': nrm(ks[2], (BATCH, CTX_LEN, D_MODEL), 1.0),
        'c_ctx': nrm(ks[3], (D_MODEL,), 1.0),
        'mod_w': nrm(ks[4], (DEPTH, D_MODEL, 3 * D_MODEL), 0.5 * D_MODEL ** -0.5),
        'mod_b': nrm(ks[5], (DEPTH, 3 * D_MODEL), 0.02),
        'norm_g': 1.0 + nrm(ks[6], (DEPTH, D_MODEL), 0.02),
        'w_in': nrm(ks[7], (DEPTH, D_MODEL, D_IN), D_MODEL ** -0.5),
        'w_out': nrm(ks[8], (DEPTH, D_MIX, D_MODEL), D_MIX ** -0.5),
        'hy_conv': nrm(ks[9], (DEPTH, 3, 3 * W_HY), 3 ** -0.5),
        'hy_fw1': nrm(ks[10], (DEPTH, HY_EMB, HY_FFN), HY_EMB ** -0.5),
        'hy_fb1': nrm(ks[11], (DEPTH, HY_FFN), 0.1),
        'hy_freq': 1.0 + nrm(ks[12], (DEPTH, HY_FFN), 0.1),
        'hy_fw2': nrm(ks[13], (DEPTH, HY_FFN, HY_FFN), HY_FFN ** -0.5),
        'hy_fb2': nrm(ks[14], (DEPTH, HY_FFN), 0.1),
        'hy_fw3': nrm(ks[15], (DEPTH, HY_FFN, 4 * W_HY), HY_FFN ** -0.5),
        'hy_bias': nrm(ks[16], (DEPTH, 2, W_HY), 1.0),
        'rw_conv': nrm(ks[17], (DEPTH, 3, 3 * W_RW + RW_LORA_W + RW_LORA_A), 3 ** -0.5),
        'rw_w0': jnp.linspace(-6.0, -1.0, W_RW, dtype=f32) + nrm(ks[18], (DEPTH, 2, W_RW), 0.1),
        'rw_w_up': nrm(ks[19], (DEPTH, 2, RW_LORA_W, W_RW), 0.1),
        'rw_a0': nrm(ks[20], (DEPTH, 2, W_RW), 0.1),
        'rw_a_up': nrm(ks[21], (DEPTH, 2, RW_LORA_A, W_RW), 0.1),
        'rw_k_k': 0.85 + nrm(ks[22], (DEPTH, W_RW), 0.02),
        'rw_k_a': 1.0 + nrm(ks[23], (DEPTH, W_RW), 0.02),
        'rw_r_k': nrm(ks[24], (DEPTH, W_RW), 0.1),
        'rw_ln_g': 1.0 + nrm(ks[25], (DEPTH, W_RW), 0.02),
        'rw_ln_b': nrm(ks[26], (DEPTH, W_RW), 0.02),
        'wa_sink': nrm(ks[27], (DEPTH, WA_HEADS), 0.5),
        'fa_q_norm': 1.0 + nrm(ks[28], (DEPTH, HEAD_DIM), 0.02),
        'fa_k_norm': 1.0 + nrm(ks[29], (DEPTH, HEAD_DIM), 0.02),
        'final_g': 1.0 + nrm(ks[30], (D_MODEL,), 0.02),
    }


def reference(x, c, ctx, c_ctx, mod_w, mod_b, norm_g, w_in, w_out,
              hy_conv, hy_fw1, hy_fb1, hy_freq, hy_fw2, hy_fb2, hy_fw3, hy_bias,
              rw_conv, rw_w0, rw_w_up, rw_a0, rw_a_up, rw_k_k, rw_k_a, rw_r_k, rw_ln_g, rw_ln_b,
              wa_sink, fa_q_norm, fa_k_norm, final_g):
    rows = x.shape[1] // GRID_W
    rope_cos, rope_sin = grid_rope_tables(rows)
    for l in range(DEPTH):
        x, ctx = mixer_layer(
            x, ctx, c, c_ctx, mod_w[l], mod_b[l], norm_g[l], w_in[l], w_out[l],
            hy_conv[l], hy_fw1[l], hy_fb1[l], hy_freq[l], hy_fw2[l], hy_fb2[l], hy_fw3[l], hy_bias[l],
            rw_conv[l], rw_w0[l], rw_w_up[l], rw_a0[l], rw_a_up[l], rw_k_k[l], rw_k_a[l], rw_r_k[l],
            rw_ln_g[l], rw_ln_b[l], wa_sink[l], fa_q_norm[l], fa_k_norm[l], rope_cos, rope_sin,
            l < DEPTH - 1)
    return rms_norm(x, final_g)
```

```python
import contextlib
import math
import numpy as np
import ml_dtypes
import concourse.bass as bass
import concourse.mybir as mybir
from concourse.bass_utils import run_bass_kernel_spmd

F32 = mybir.dt.float32
BF16 = mybir.dt.bfloat16
ALU = mybir.AluOpType
AF = mybir.ActivationFunctionType
AX = mybir.AxisListType

D = 1024
L = 4096
C = 256
T = L + C
NT = T // 128
DEPTH = 4
D_IN = 3712
HY0, HYG0, RW0, RWG0, WA0, WAG0, FA0, FAG0 = 0, 768, 1024, 1920, 2176, 2688, 2944, 3456
EPS = 1e-6
NSLOT = 12


class Buf:
    def __init__(self, name=""):
        self.name = name
        self.w = None
        self.r = {}

    def wdeps(self):
        return [self.w] if self.w is not None else []

    def rdeps(self):
        return list(self.r.values())

    def add_reader(self, tok):
        k = tok[:2]
        if k not in self.r or self.r[k][2] < tok[2]:
            self.r[k] = tok

    def set_writer(self, tok):
        self.w = tok
        self.r = {}


class Tile(Buf):
    def __init__(self, name, t):
        super().__init__(name)
        self.t = t

    def __getitem__(self, key):
        return self.t[key]


class KB:
    def __init__(self):
        self.nc = bass.Bass("TRN2", target_bir_lowering=False)
        nc = self.nc
        self.es = contextlib.ExitStack()
        self.eng = {"pe": nc.tensor, "act": nc.scalar, "dve": nc.vector, "pool": nc.gpsimd, "sp": nc.sync}
        self.sem = {}
        self.cnt = {}
        self.waited = {e: {} for e in self.eng}
        for e in self.eng:
            self.sem[e] = self.es.enter_context(nc.semaphore("s_" + e))
            self.cnt[e] = 0
        self.slots = {}
        self.slot_i = {}
        for q in ("sp", "act", "pool"):
            self.slots[q] = [[self.es.enter_context(nc.semaphore(f"d_{q}{i}")), 0] for i in range(NSLOT)]
            self.slot_i[q] = 0
        self.n_ins = 0

    def sb(self, name, shape, dt=F32):
        self.uid = getattr(self, "uid", 0) + 1
        name = f"{name}_{self.uid}"
        return Tile(name, self.es.enter_context(self.nc.sbuf_tensor(name, list(shape), dt)))

    def ps(self, name, shape, dt=F32):
        return Tile(name, self.es.enter_context(self.nc.psum_tensor(name, list(shape), dt)))

    def dram(self, name, shape, dt=F32, kind="Internal"):
        t = self.nc.dram_tensor(name, list(shape), dt, kind=kind)
        b = Tile(name, t.ap())
        return b

    def _tok_sem(self, tok):
        if tok[0] == "e":
            return ("e", tok[1]), self.sem[tok[1]], tok[2]
        return ("d", tok[1]), self.slots[tok[1][0]][tok[1][1]][0], tok[2]

    def _wait(self, e, toks):
        need = {}
        for tok in toks:
            if tok is None:
                continue
            key, sem, val = self._tok_sem(tok)
            if tok[0] == "e" and tok[1] == e and e == "pe":
                continue
            if self.waited[e].get(key, 0) >= val:
                continue
            if key not in need or need[key][1] < val:
                need[key] = (sem, val)
        for key, (sem, val) in need.items():
            self.eng[e].wait_ge(sem, val)
            self.waited[e][key] = val

    def op(self, e, fn, reads=(), writes=()):
        toks = []
        for b in reads:
            toks += b.wdeps()
        for b in writes:
            toks += b.wdeps() + b.rdeps()
        self._wait(e, toks)
        ins = fn(self.eng[e])
        self.cnt[e] += 1
        ins.then_inc(self.sem[e], 1)
        tok = ("e", e, self.cnt[e])
        for b in reads:
            b.add_reader(tok)
        for b in writes:
            b.set_writer(tok)
        self.n_ins += 1
        return ins

    def dma(self, q, out, in_, reads=(), writes=(), slow=False):
        i = self.slot_i[q]
        self.slot_i[q] = (i + 1) % NSLOT
        slot = self.slots[q][i]
        toks = []
        if slot[1] > 0:
            toks.append(("d", (q, i), slot[1]))
        for b in reads:
            toks += b.wdeps()
        for b in writes:
            toks += b.wdeps() + b.rdeps()
        self._wait(q, toks)
        if slow:
            ins = self.eng[q].dma_start(out=out, in_=in_, allow_slow_non_contiguous=True)
        else:
            ins = self.eng[q].dma_start(out=out, in_=in_)
        ins.then_inc(slot[0], 16)
        slot[1] += 16
        tok = ("d", (q, i), slot[1])
        for b in reads:
            b.add_reader(tok)
        for b in writes:
            b.set_writer(tok)
        self.n_ins += 1
        return ins

    def barrier(self):
        toks = [("e", e, self.cnt[e]) for e in self.eng if self.cnt[e] > 0]
        for q in self.slots:
            for i, s in enumerate(self.slots[q]):
                if s[1] > 0:
                    toks.append(("d", (q, i), s[1]))
        for e in self.eng:
            self._wait(e, toks)

    def finish(self):
        self.barrier()

    @contextlib.contextmanager
    def scope(self):
        es = contextlib.ExitStack()
        old = self.es
        self.es = es
        try:
            yield
        finally:
            self.barrier()
            self.es = old
            es.close()


def host_consts():
    cst = {}
    cst["ident_bf"] = np.eye(128, dtype=np.float32).astype(ml_dtypes.bfloat16)
    cst["ident_f"] = np.eye(128, dtype=np.float32)
    blk = np.zeros((128, 128), np.float32)
    blk[:64, :64] = 1.0
    blk[64:, 64:] = 1.0
    cst["blk64"] = blk
    cst["ones_f"] = np.ones((128, 128), np.float32)
    t = np.arange(L)
    row = (t // 64).astype(np.float32)
    col = (t % 64).astype(np.float32)
    inv = (10000.0 ** (-np.arange(16, dtype=np.float32) / 16)).astype(np.float32)
    cosT = np.zeros((128, L), np.float32)
    sinT = np.zeros((128, L), np.float32)
    perm = np.zeros((128, 128), np.float32)
    for p in range(128):
        d = p % 64
        sec, half, f = d // 32, (d % 32) // 16, d % 16
        pos = row if sec == 0 else col
        ang = (pos * inv[f]).astype(np.float32)
        cosT[p] = np.cos(ang)
        sinT[p] = np.sin(ang)
        if half == 0:
            perm[p + 16, p] = -1.0
        else:
            perm[p - 16, p] = 1.0
    cst["rope_cos"] = cosT
    cst["rope_sin"] = sinT
    cst["rope_perm"] = perm
    i = np.arange(128)[:, None]
    j = np.arange(384)[None, :]
    cst["wmask"] = np.where((j >= i) & (j <= i + 256), 0.0, -1e30).astype(np.float32)
    cst.update(hy_consts(L, 32, 32, "L"))
    cst.update(hy_consts(C, 2, 64, "C"))
    return cst


def hy_consts(Ls, A, cbw, tag):
    G = 128 // A
    N = 2 * Ls
    out = {}
    p = np.arange(128)
    f1 = np.arange(256)
    F = np.zeros((128, 2, 512), np.float64)
    for h in range(2):
        pp = h * 128 + p
        ang = 2 * np.pi * ((pp[:, None] * f1[None, :]) % 256) / 256
        F[:, h, 0:256] = np.cos(ang)
        F[:, h, 256:512] = -np.sin(ang)
    out["F256"] = F
    a_of_row = np.arange(128) // G
    th = 2 * np.pi * ((a_of_row[:, None] * f1[None, :]) % N) / N
    out["TWC"] = np.cos(th)
    out["TWS"] = np.sin(th)
    Dre = np.zeros((128, 128)); Dim = np.zeros((128, 128))
    E1 = np.zeros((128, 256)); E2 = np.zeros((128, 256))
    for a in range(A):
        for c in range(G):
            for f2 in range(A):
                ph = 2 * np.pi * ((a * f2) % A) / A
                Dre[a * G + c, c * A + f2] = np.cos(ph)
                Dim[a * G + c, c * A + f2] = -np.sin(ph)
                E1[c * A + f2, c * A + a] = np.cos(ph)
                E1[c * A + f2, 128 + c * A + a] = np.sin(ph)
                E2[c * A + f2, c * A + a] = -np.sin(ph)
                E2[c * A + f2, 128 + c * A + a] = np.cos(ph)
    out["Dre"] = Dre; out["Dim"] = Dim; out["nDim"] = -Dim; out["E1"] = E1; out["E2"] = E2
    a_of_col = np.arange(128) % A
    TW2C = np.zeros((128, 2, 128)); TW2S = np.zeros((128, 2, 128))
    IC = np.zeros((128, 2, 128)); IS = np.zeros((128, 2, 128))
    for ch in range(2):
        ff = ch * 128 + np.arange(128)
        th2 = 2 * np.pi * ((ff[:, None] * a_of_col[None, :]) % N) / N
        TW2C[:, ch, :] = np.cos(th2) / N
        TW2S[:, ch, :] = np.sin(th2) / N
        ph = 2 * np.pi * ((ff[:, None] * p[None, :]) % 256) / 256
        IC[:, ch, :] = np.cos(ph)
        IS[:, ch, :] = -np.sin(ph)
    out["TW2C"] = TW2C; out["TW2S"] = TW2S; out["IC"] = IC; out["IS"] = IS
    tp = np.arange(N)
    pos = np.where(tp < Ls, tp, N - tp).astype(np.float64)
    tn = (pos / (Ls - 1)).astype(np.float32)
    w = ((2.0 * math.pi / Ls) * pos).astype(np.float32)
    fb = np.linspace(1e-4, 15.0, 16, dtype=np.float32)
    zT = np.zeros((33, N), np.float32)
    zT[0] = tn
    zT[1:17] = np.cos(fb[:, None] * w[None, :])
    zT[17:33] = np.sin(fb[:, None] * w[None, :])
    out["zT"] = zT
    deltas = np.abs(np.linspace(math.log(1e-2) / 1.5, math.log(1e-2) / 0.3, 256, dtype=np.float32))
    dec = np.exp(-tn[:, None] * deltas[None, :]).astype(np.float32)
    dec[Ls, :] = 0.0
    nblk = 256 // cbw
    ngr = cbw // G
    DEC = np.zeros((nblk, 128, 2, ngr, A, G), np.float32)
    for h in range(2):
        for a in range(A):
            tpp = A * (h * 128 + p) + a
            for b in range(nblk):
                DEC[b, :, h, :, a, :] = dec[tpp, b * cbw:(b + 1) * cbw].reshape(128, ngr, G)
    out["DEC"] = DEC.reshape(nblk, 128, 2 * ngr * A * G)
    return {f"hy{tag}_{k}": np.ascontiguousarray(v.astype(np.float32)) for k, v in out.items()}

CONST_SPECS = None


def build(depth=DEPTH, dbg=()):
    kb = KB()
    nc = kb.nc
    cst = host_consts()
    def inp(name, shape, dt=F32):
        return kb.dram(name, shape, dt, kind="ExternalInput")

    x_in = inp("x", [L, D])
    c_in = inp("c", [D])
    ctx_in = inp("ctx", [C, D])
    cctx_in = inp("c_ctx", [D])
    W = {}
    wspec = {
        "mod_w": [DEPTH, D, 3 * D], "mod_b": [DEPTH, 3 * D], "norm_g": [DEPTH, D], "w_in": [DEPTH, D, D_IN],
        "w_out": [DEPTH, D, D], "wa_sink": [DEPTH, 4], "fa_q_norm": [DEPTH, 64], "fa_k_norm": [DEPTH, 64],
        "final_g": [D],
        "rw_conv": [DEPTH, 3, 896], "rw_w0": [DEPTH, 2, 256], "rw_w_up": [DEPTH, 2, 64, 256], "rw_a0": [DEPTH, 2, 256],
        "rw_a_up": [DEPTH, 2, 64, 256], "rw_k_k": [DEPTH, 256], "rw_k_a": [DEPTH, 256], "rw_r_k": [DEPTH, 256],
        "rw_ln_g": [DEPTH, 256], "rw_ln_b": [DEPTH, 256],
        "hy_conv": [DEPTH, 3, 768], "hy_fw1": [DEPTH, 33, 64], "hy_fb1": [DEPTH, 64], "hy_freq": [DEPTH, 64],
        "hy_fw2": [DEPTH, 64, 64], "hy_fb2": [DEPTH, 64], "hy_fw3": [DEPTH, 64, 1024], "hy_bias": [DEPTH, 2, 256],
    }
    for n, s in wspec.items():
        W[n] = inp(n, s)
    CT = {}
    for n, a in cst.items():
        CT[n] = inp("k_" + n, list(a.shape), BF16 if a.dtype == ml_dtypes.bfloat16 else F32)
    out = kb.dram("out", [L, D], F32, kind="ExternalOutput")
    xres = kb.dram("xres", [T, D], F32)
    mixT = kb.dram("mixT", [D, T], BF16)
    RS = {}
    for n in ("RT", "VT", "AL", "W0", "W1", "B0", "B1", "KD0", "KD1", "YF", "YB"):
        RS[n] = kb.dram("rs_" + n, [256, T])
    RS["VTOK"] = kb.dram("rs_VTOK", [T, 256])
    RS["SGT"] = kb.dram("rs_SGT", [256, T], BF16)
    HS = {"SG": kb.dram("hs_SG", [256, T], BF16),
          "UTL": kb.dram("hs_UTL", [3, 8, 128, 32 * 32]), "UTC": kb.dram("hs_UTC", [3, 4, 128, 2 * 64])}
    dbg_t = {}
    for n, s in dbg:
        dbg_t[n] = kb.dram("dbg_" + n, s, F32, kind="ExternalOutput")

    ident_bf = kb.sb("ident_bf", [128, 128], BF16)
    ident_f = kb.sb("ident_f", [128, 128])
    blk64 = kb.sb("blk64", [128, 128])
    ones_f = kb.sb("ones_f", [128, 128])
    for tl, n in ((ident_bf, "ident_bf"), (ident_f, "ident_f"), (blk64, "blk64"), (ones_f, "ones_f")):
        kb.dma("sp", tl[:], CT[n][:, :], reads=[CT[n]], writes=[tl])
    hT = G1 = SH = None
    GT = kb.sb("GT", [128, 2, D])
    PS = [kb.ps(f"ps{i}", [128, 512]) for i in range(8)]

    xres_b = [Buf(f"xres{i}") for i in range(NT)]

    def x_src(l, i):
        if l == 0:
            if i < 2:
                return ctx_in[i * 128:(i + 1) * 128, :], ctx_in
            return x_in[(i - 2) * 128:(i - 1) * 128, :], x_in
        return xres[i * 128:(i + 1) * 128, :], xres_b[i]

    def phase_mod(l):
        with kb.scope():
            cc = kb.sb("cc", [128, 2, 8])
            sc = kb.sb("sc", [128, 2, 8])
            mw = [kb.sb(f"mw{i}", [128, 8, 512]) for i in range(2)]
            mb = kb.sb("mb", [128, 3 * D])
            ng = kb.sb("ng", [128, D])
            modr = kb.sb("modr", [128, 2, 3 * D])
            kb.dma("sp", cc[:, 0, :], c_in.t.rearrange("(j p) -> p j", p=128), reads=[c_in], writes=[cc], slow=True)
            kb.dma("sp", cc[:, 1, :], cctx_in.t.rearrange("(j p) -> p j", p=128), reads=[cctx_in], writes=[cc], slow=True)
            kb.dma("sp", mb[:], W["mod_b"][l, :].partition_broadcast(128), reads=[W["mod_b"]], writes=[mb])
            kb.dma("sp", ng[:], W["norm_g"][l, :].partition_broadcast(128), reads=[W["norm_g"]], writes=[ng])
            kb.op("act", lambda e: e.activation(sc[:], cc[:], AF.Silu), reads=[cc], writes=[sc])
            for n in range(6):
                m = mw[n % 2]
                kb.dma("sp" if n % 2 == 0 else "pool", m[:],
                       W["mod_w"][l, :, n * 512:(n + 1) * 512].rearrange("(j p) n -> p j n", p=128),
                       reads=[W["mod_w"]], writes=[m])
                for i in range(2):
                    p = PS[(2 * n + i) % 8]
                    for j in range(8):
                        kb.op("pe", lambda e, p=p, i=i, j=j, m=m: e.matmul(
                            p[:, :], sc[:, i, j:j + 1].broadcast_to([128, 128]), m[:, j, :],
                            start=(j == 0), stop=(j == 7)), reads=[sc, m], writes=[p])
                    kb.op("dve", lambda e, p=p, i=i, n=n: e.tensor_tensor(
                        modr[:, i, n * 512:(n + 1) * 512], p[:, :], mb[:, n * 512:(n + 1) * 512], ALU.add),
                        reads=[p, mb], writes=[modr])
            for i in range(2):
                kb.op("dve", lambda e, i=i: e.scalar_tensor_tensor(
                    G1[:, i, :], modr[:, i, D:2 * D], 1.0, ng[:], ALU.add, ALU.mult), reads=[modr, ng], writes=[G1])
                kb.op("act", lambda e, i=i: e.copy(SH[:, i, :], modr[:, i, 0:D]), reads=[modr], writes=[SH])
                kb.op("act", lambda e, i=i: e.copy(GT[:, i, :], modr[:, i, 2 * D:3 * D]), reads=[modr], writes=[GT])

    def phase_norm(l):
        with kb.scope():
            xt = [kb.sb(f"xt{i}", [128, D]) for i in range(3)]
            junk = kb.sb("junk", [128, D])
            hf = [kb.sb(f"hf{i}", [128, D]) for i in range(2)]
            hb = [kb.sb(f"hb{i}", [128, D], BF16) for i in range(2)]
            st = [kb.sb(f"st{i}", [128, 4]) for i in range(2)]
            for i in range(NT):
                x, s, h, hbt = xt[i % 3], st[i % 2], hf[i % 2], hb[i % 2]
                sel = 1 if i < 2 else 0
                src, srcb = x_src(l, i)
                kb.dma("sp" if i % 2 == 0 else "pool", x[:], src, reads=[srcb], writes=[x])
                kb.op("act", lambda e, x=x, s=s: e.activation(junk[:], x[:], AF.Square, accum_out=s[:, 0:1]),
                      reads=[x], writes=[junk, s])
                kb.op("dve", lambda e, s=s: e.tensor_scalar(s[:, 1:2], s[:, 0:1], 1.0 / D, EPS, ALU.mult, ALU.add),
                      reads=[s], writes=[s])
                kb.op("act", lambda e, s=s: e.sqrt(s[:, 2:3], s[:, 1:2]), reads=[s], writes=[s])
                kb.op("dve", lambda e, s=s: e.reciprocal(s[:, 3:4], s[:, 2:3]), reads=[s], writes=[s])
                kb.op("dve", lambda e, x=x, s=s, h=h, sel=sel: e.scalar_tensor_tensor(
                    h[:], x[:], s[:, 3:4], G1[:, sel, :], ALU.mult, ALU.mult), reads=[x, s, G1], writes=[h])
                kb.op("pool", lambda e, h=h, hbt=hbt, sel=sel: e.tensor_tensor(hbt[:], h[:], SH[:, sel, :], ALU.add),
                      reads=[h, SH], writes=[hbt])
                p = PS[i % 4]
                pv = p[:, :].bitcast(BF16)
                for j in range(8):
                    kb.op("pe", lambda e, j=j, pv=pv, hbt=hbt: e.transpose(
                        pv[:, j * 128:(j + 1) * 128], hbt[:, j * 128:(j + 1) * 128], ident_bf[:]),
                        reads=[hbt, ident_bf], writes=[p])
                kb.op("act", lambda e, pv=pv, i=i: e.copy(
                    hT[:, :, i * 128:(i + 1) * 128], pv.rearrange("p (j t) -> p j t", j=8)), reads=[p], writes=[hT])

    def load_w(l, dst, col0, ncols, stage, q="sp"):
        kb.dma(q, stage[:, :, 0:ncols], W["w_in"][l, :, col0:col0 + ncols].rearrange("(j p) n -> p j n", p=128),
               reads=[W["w_in"]], writes=[stage])
        kb.op("pool", lambda e: e.tensor_copy(dst[:, :, 0:ncols], stage[:, :, 0:ncols]), reads=[stage], writes=[dst])

    def proj_fm(p, wt, c0, nc_, t0, nt):
        for j in range(8):
            kb.op("pe", lambda e, j=j: e.matmul(p[0:nc_, 0:nt], wt[:, j, c0:c0 + nc_], hT[:, j, t0:t0 + nt],
                                                start=(j == 0), stop=(j == 7)), reads=[wt, hT], writes=[p])

    def proj_tm(p, wt, c0, nc_, i):
        for j in range(8):
            kb.op("pe", lambda e, j=j: e.matmul(p[:, 0:nc_], hT[:, j, i * 128:(i + 1) * 128], wt[:, j, c0:c0 + nc_],
                                                start=(j == 0), stop=(j == 7)), reads=[wt, hT], writes=[p])

    TCH = [(t0, min(512, T - t0)) for t0 in range(0, T, 512)]

    def qk_prep(l, es_tiles, wt, c0, dst, dst_j, gvec, norm, rope):
        raw, sq, rs, rot = es_tiles
        for ci, (t0, nt) in enumerate(TCH):
            p = PS[ci % 2]
            proj_fm(p, wt, c0, 128, t0, nt)
            if norm:
                kb.op("act", lambda e, p=p, nt=nt: e.activation(sq[:, 0:nt], p[:, 0:nt], AF.Square), reads=[p], writes=[sq])
                p2 = PS[2 + ci % 2]
                kb.op("pe", lambda e, p2=p2, nt=nt: e.matmul(p2[:, 0:nt], blk64[:], sq[:, 0:nt], start=True, stop=True),
                      reads=[blk64, sq], writes=[p2])
                kb.op("dve", lambda e, p2=p2, nt=nt: e.tensor_scalar(rs[:, 0:nt], p2[:, 0:nt], 1.0 / 64, EPS, ALU.mult, ALU.add),
                      reads=[p2], writes=[rs])
                kb.op("act", lambda e, nt=nt: e.sqrt(rs[:, 0:nt], rs[:, 0:nt]), reads=[rs], writes=[rs])
                kb.op("dve", lambda e, nt=nt: e.reciprocal(rs[:, 0:nt], rs[:, 0:nt]), reads=[rs], writes=[rs])
                kb.op("dve", lambda e, p=p, nt=nt: e.scalar_tensor_tensor(
                    raw[:, 0:nt], p[:, 0:nt], gvec[:, 0:1], rs[:, 0:nt], ALU.mult, ALU.mult), reads=[p, gvec, rs], writes=[raw])
            else:
                kb.op("act", lambda e, p=p, nt=nt: e.copy(raw[:, 0:nt], p[:, 0:nt]), reads=[p], writes=[raw])
            lat0 = 0
            if t0 < C:
                lat0 = C - t0
                kb.op("pool", lambda e, t0=t0, lat0=lat0: e.tensor_copy(dst[:, dst_j, t0:t0 + lat0], raw[:, 0:lat0]),
                      reads=[raw], writes=[dst])
            if not rope:
                if nt > lat0:
                    kb.op("pool", lambda e, t0=t0, lat0=lat0, nt=nt: e.tensor_copy(
                        dst[:, dst_j, t0 + lat0:t0 + nt], raw[:, lat0:nt]), reads=[raw], writes=[dst])
                continue
            p3 = PS[4 + ci % 2]
            n_l = nt - lat0
            lp = t0 + lat0 - C
            kb.op("pe", lambda e, p3=p3, lat0=lat0, nt=nt: e.matmul(p3[:, lat0:nt], rope_perm[:], raw[:, lat0:nt], start=True, stop=True),
                  reads=[rope_perm, raw], writes=[p3])
            kb.op("dve", lambda e, p3=p3, lat0=lat0, nt=nt, lp=lp, n_l=n_l: e.tensor_tensor(
                rot[:, lat0:nt], p3[:, lat0:nt], rope_sin[:, lp:lp + n_l], ALU.mult), reads=[p3, rope_sin], writes=[rot])
            kb.op("pool", lambda e, lat0=lat0, nt=nt, lp=lp, n_l=n_l: e.tensor_tensor(
                raw[:, lat0:nt], raw[:, lat0:nt], rope_cos[:, lp:lp + n_l], ALU.mult), reads=[raw, rope_cos], writes=[raw])
            kb.op("dve", lambda e, t0=t0, lat0=lat0, nt=nt: e.tensor_tensor(
                dst[:, dst_j, t0 + lat0:t0 + nt], raw[:, lat0:nt], rot[:, lat0:nt], ALU.add), reads=[raw, rot], writes=[dst])

    rope_cos = rope_sin = rope_perm = None

    def phase_attn(l, dense, with_ctx):
        nonlocal rope_cos, rope_sin, rope_perm
        base = FA0 if dense else WA0
        gbase = FAG0 if dense else WAG0
        mrow = 768 if dense else 512
        with kb.scope():
            wt = kb.sb("wt", [128, 8, 768], BF16)
            gq = kb.sb("gq", [128, 1])
            gk = kb.sb("gk", [128, 1])
            sink = kb.sb("sink", [128, 4])
            if dense:
                for hh in range(2):
                    kb.dma("sp", gq[hh * 64:(hh + 1) * 64, :], W["fa_q_norm"][l, :].rearrange("(d o) -> d o", o=1),
                           reads=[W["fa_q_norm"]], writes=[gq], slow=True)
                    kb.dma("sp", gk[hh * 64:(hh + 1) * 64, :], W["fa_k_norm"][l, :].rearrange("(d o) -> d o", o=1),
                           reads=[W["fa_k_norm"]], writes=[gk], slow=True)
            else:
                kb.dma("sp", sink[:], W["wa_sink"][l, :].partition_broadcast(128), reads=[W["wa_sink"]], writes=[sink])
            QT = kb.sb("QT", [128, 2, T], BF16)
            KT = kb.sb("KT", [128, 1, T], BF16)
            VW = 65 if dense else 64
            Vt = kb.sb("Vt", [128, NT, 2, VW], BF16)
            SG = None
            with kb.scope():
                rope_cos = kb.sb("rope_cos", [128, L])
                rope_sin = kb.sb("rope_sin", [128, L])
                rope_perm = kb.sb("rope_perm", [128, 128])
                kb.dma("sp", rope_cos[:], CT["rope_cos"][:, :], reads=[CT["rope_cos"]], writes=[rope_cos])
                kb.dma("pool", rope_sin[:], CT["rope_sin"][:, :], reads=[CT["rope_sin"]], writes=[rope_sin])
                kb.dma("sp", rope_perm[:], CT["rope_perm"][:, :], reads=[CT["rope_perm"]], writes=[rope_perm])
                stage = kb.sb("wstage", [128, 8, 256])
                w4 = W["w_in"][l, :, base:base + 256].rearrange("(j p) (h d) -> p j h d", p=128, d=64)
                st4 = stage[:, :, 0:256].rearrange("p j (h d) -> p j h d", d=64)
                for hi, h in enumerate((0, 2, 1, 3)):
                    kb.dma("sp", st4[:, :, hi, :], w4[:, :, h, :], reads=[W["w_in"]], writes=[stage])
                kb.op("pool", lambda e: e.tensor_copy(wt[:, :, 0:256], stage[:]), reads=[stage], writes=[wt])
                kb.dma("pool", stage[:], W["w_in"][l, :, base + 256:base + 512].rearrange("(j p) n -> p j n", p=128),
                       reads=[W["w_in"]], writes=[stage])
                kb.op("pool", lambda e: e.tensor_copy(wt[:, :, 256:512], stage[:]), reads=[stage], writes=[wt])
                kb.dma("sp", stage[:], W["w_in"][l, :, gbase:gbase + 256].rearrange("(j p) n -> p j n", p=128),
                       reads=[W["w_in"]], writes=[stage])
                kb.op("pool", lambda e: e.tensor_copy(wt[:, :, 512:768], stage[:]), reads=[stage], writes=[wt])
                tl = (kb.sb("qraw", [128, 512]), kb.sb("qsq", [128, 512]), kb.sb("qrs", [128, 512]), kb.sb("qrot", [128, 512]))
                qk_prep(l, tl, wt, 0, QT, 0, gq, dense, True)
                qk_prep(l, tl, wt, 128, QT, 1, gq, dense, True)
                qk_prep(l, tl, wt, 256, KT, 0, gk, dense, True)
            if dense:
                kb.op("pool", lambda e: e.memset(Vt[:, :, :, 64:65], 1.0), writes=[Vt])
            for i in range(NT):
                p = PS[i % 2]
                proj_tm(p, wt, 384, 128, i)
                kb.op("act", lambda e, p=p, i=i: e.copy(Vt[:, i, :, 0:64], p[:, 0:128].rearrange("p (k d) -> p k d", d=64)),
                      reads=[p], writes=[Vt])
            with kb.scope():
                if dense:
                    attn_dense(l, wt, QT, KT, Vt, mrow, with_ctx)
                else:
                    attn_window(l, wt, QT, KT, Vt, sink, mrow, with_ctx)

    def attn_dense(l, wt, QT, KT, Vt, mrow, with_ctx):
        pt = [kb.sb(f"pt{i}", [128, 512], BF16) for i in range(3)]
        osb = [kb.sb(f"osb{i}", [128, 512]) for i in range(2)]
        rc = [kb.sb(f"rc{i}", [128, 512]) for i in range(2)]
        ob = [kb.sb(f"ob{i}", [128, 512], BF16) for i in range(2)]
        it = 0
        sgt = [kb.sb(f"sgt{i}", [64, 512], BF16) for i in range(2)]
        chunks = []
        if with_ctx:
            chunks.append((0, C, 0, 2))
        for t0 in range(C, T, 512):
            chunks.append((t0, 512, 0, NT))
        for h in range(4):
            kv, pr = h // 2, h % 2
            ks = slice(64 * kv, 64 * kv + 64)
            for (t0, nt, kb0, kb1) in chunks:
                po = PS[4 + it % 2]
                sg = sgt[it % 2]
                pg = PS[3]
                for j in range(8):
                    kb.op("pe", lambda e, j=j: e.matmul(
                        pg[0:64, 0:nt], wt[:, j, 512 + 64 * h:576 + 64 * h], hT[:, j, t0:t0 + nt], start=(j == 0), stop=(j == 7)),
                        reads=[wt, hT], writes=[pg])
                kb.op("act", lambda e: e.activation(sg[0:64, 0:nt], pg[0:64, 0:nt], AF.Silu), reads=[pg], writes=[sg])
                for kbi in range(kb0, kb1):
                    psS = PS[kbi % 3]
                    ptt = pt[kbi % 3]
                    kb.op("pe", lambda e, psS=psS, kbi=kbi: e.matmul(
                        psS[:, 0:nt], KT[ks, 0, kbi * 128:(kbi + 1) * 128], QT[ks, pr, t0:t0 + nt], start=True, stop=True),
                        reads=[KT, QT], writes=[psS])
                    kb.op("act", lambda e, psS=psS, ptt=ptt: e.activation(ptt[:, 0:nt], psS[:, 0:nt], AF.Exp, scale=0.125),
                          reads=[psS], writes=[ptt])
                    kb.op("pe", lambda e, po=po, ptt=ptt, kbi=kbi: e.matmul(
                        po[0:65, 0:nt], Vt[:, kbi, kv, 0:65], ptt[:, 0:nt], start=(kbi == kb0), stop=(kbi == kb1 - 1)),
                        reads=[Vt, ptt], writes=[po])
                o_s, r_c, o_b = osb[it % 2], rc[it % 2], ob[it % 2]
                kb.op("dve", lambda e: e.reciprocal(r_c[64:65, 0:nt], po[64:65, 0:nt]), reads=[po], writes=[r_c])
                kb.op("act", lambda e: e.copy(o_s[0:64, 0:nt], po[0:64, 0:nt]), reads=[po], writes=[o_s])
                pb = PS[6 + it % 2]
                kb.op("pe", lambda e: e.matmul(pb[0:64, 0:nt], ones_f[64:65, 0:64], r_c[64:65, 0:nt], start=True, stop=True),
                      reads=[ones_f, r_c], writes=[pb])
                kb.op("dve", lambda e: e.tensor_tensor(o_s[0:64, 0:nt], o_s[0:64, 0:nt], pb[0:64, 0:nt], ALU.mult),
                      reads=[o_s, pb], writes=[o_s])
                kb.op("pool", lambda e: e.tensor_tensor(o_b[0:64, 0:nt], o_s[0:64, 0:nt], sg[0:64, 0:nt], ALU.mult),
                      reads=[o_s, sg], writes=[o_b])
                kb.dma("pool", mixT[mrow + 64 * h:mrow + 64 * h + 64, t0:t0 + nt], o_b[0:64, 0:nt], reads=[o_b], writes=[Buf()])
                it += 1

    def attn_window(l, wt, QT, KT, Vt, sink, mrow, with_ctx):
        wmask = kb.sb("wmask", [128, 384])
        kb.dma("sp", wmask[:], CT["wmask"][:, :], reads=[CT["wmask"]], writes=[wmask])
        nsink = kb.sb("nsink", [128, 4])
        kb.op("dve", lambda e: e.tensor_scalar(nsink[:], sink[:], -1.0, None, ALU.mult), reads=[sink], writes=[nsink])
        S = [kb.sb(f"wS{i}", [128, 640]) for i in range(2)]
        P = [kb.sb(f"wP{i}", [128, 640]) for i in range(2)]
        Pn = [kb.sb(f"wPn{i}", [128, 640], BF16) for i in range(2)]
        PT = [kb.sb(f"wPT{i}", [128, 640], BF16) for i in range(2)]
        st = [kb.sb(f"wst{i}", [128, 8]) for i in range(2)]
        sgt = [kb.sb(f"wsg{i}", [64, 128], BF16) for i in range(2)]
        ob = [kb.sb(f"wob{i}", [64, 128], BF16) for i in range(2)]
        it = 0
        for i in range(0 if with_ctx else 2, NT):
            if i < 2:
                loc = []
            else:
                loc = list(range(max(2, i - 1), min(NT - 1, i + 1) + 1))
            nl = 128 * len(loc)
            m0 = 128 if (i >= 2 and i - 1 < 2) else 0
            nk = nl + C
            ktiles = loc + [0, 1]
            for h in range(4):
                kv, pr = h // 2, h % 2
                ks = slice(64 * kv, 64 * kv + 64)
                s_, p_, pn_, pt_, st_, sg, o_b = S[it % 2], P[it % 2], Pn[it % 2], PT[it % 2], st[it % 2], sgt[it % 2], ob[it % 2]
                psA, psB, psT, psO, psG = PS[it % 2], PS[2 + it % 2], PS[4 + it % 2], PS[6], PS[7]
                q_ap = QT[ks, pr, i * 128:(i + 1) * 128]
                if nl:
                    k0 = loc[0] * 128
                    kb.op("pe", lambda e: e.matmul(psA[:, 0:nl], q_ap, KT[ks, 0, k0:k0 + nl], start=True, stop=True),
                          reads=[QT, KT], writes=[psA])
                    kb.op("dve", lambda e: e.tensor_tensor(s_[:, 0:nl], psA[:, 0:nl], wmask[:, m0:m0 + nl], ALU.add),
                          reads=[psA, wmask], writes=[s_])
                kb.op("pe", lambda e: e.matmul(psB[:, 0:C], q_ap, KT[ks, 0, 0:C], start=True, stop=True),
                      reads=[QT, KT], writes=[psB])
                kb.op("act", lambda e: e.copy(s_[:, nl:nk], psB[:, 0:C]), reads=[psB], writes=[s_])
                kb.op("dve", lambda e: e.reduce_max(st_[:, 0:1], s_[:, 0:nk], AX.X), reads=[s_], writes=[st_])
                kb.op("dve", lambda e: e.tensor_scalar(st_[:, 1:2], st_[:, 0:1], -0.125, nsink[:, h:h + 1], ALU.mult, ALU.min),
                      reads=[st_, nsink], writes=[st_])
                kb.op("act", lambda e: e.activation(p_[:, 0:nk], s_[:, 0:nk], AF.Exp, bias=st_[:, 1:2], scale=0.125,
                                                    accum_out=st_[:, 2:3]), reads=[s_, st_], writes=[p_, st_])
                kb.op("act", lambda e: e.activation(st_[:, 3:4], sink[:, h:h + 1], AF.Exp, bias=st_[:, 1:2], scale=1.0),
                      reads=[sink, st_], writes=[st_])
                kb.op("dve", lambda e: e.tensor_tensor(st_[:, 4:5], st_[:, 2:3], st_[:, 3:4], ALU.add), reads=[st_], writes=[st_])
                kb.op("dve", lambda e: e.reciprocal(st_[:, 5:6], st_[:, 4:5]), reads=[st_], writes=[st_])
                kb.op("dve", lambda e: e.tensor_scalar(pn_[:, 0:nk], p_[:, 0:nk], st_[:, 5:6], None, ALU.mult),
                      reads=[p_, st_], writes=[pn_])
                pv = psT[:, :].bitcast(BF16)
                nb = nk // 128
                for b in range(nb):
                    kb.op("pe", lambda e, b=b: e.transpose(pv[:, b * 128:(b + 1) * 128], pn_[:, b * 128:(b + 1) * 128], ident_bf[:]),
                          reads=[pn_, ident_bf], writes=[psT])
                kb.op("act", lambda e: e.copy(pt_[:, 0:nk], pv[:, 0:nk]), reads=[psT], writes=[pt_])
                for b in range(nb):
                    kb.op("pe", lambda e, b=b: e.matmul(psO[0:64, 0:128], Vt[:, ktiles[b], kv, 0:64], pt_[:, b * 128:(b + 1) * 128],
                                                        start=(b == 0), stop=(b == nb - 1)), reads=[Vt, pt_], writes=[psO])
                for j in range(8):
                    kb.op("pe", lambda e, j=j: e.matmul(
                        psG[0:64, 0:128], wt[:, j, 512 + 64 * h:576 + 64 * h], hT[:, j, i * 128:(i + 1) * 128],
                        start=(j == 0), stop=(j == 7)), reads=[wt, hT], writes=[psG])
                kb.op("act", lambda e: e.activation(sg[:, :], psG[0:64, 0:128], AF.Silu), reads=[psG], writes=[sg])
                kb.op("dve", lambda e: e.tensor_tensor(o_b[:, :], psO[0:64, 0:128], sg[:, :], ALU.mult), reads=[psO, sg], writes=[o_b])
                kb.dma("sp", mixT[mrow + 64 * h:mrow + 64 * h + 64, i * 128:(i + 1) * 128], o_b[:, :], reads=[o_b], writes=[Buf()])
                it += 1

    def phase_out(l, last):
        with kb.scope():
            wo = kb.sb("wo", [128, 8, D], BF16)
            stage = kb.sb("wostage", [128, 8, 256])
            for q in range(4):
                kb.dma("sp", stage[:], W["w_out"][l, :, q * 256:(q + 1) * 256].rearrange("(j p) n -> p j n", p=128),
                       reads=[W["w_out"]], writes=[stage])
                kb.op("pool", lambda e, q=q: e.tensor_copy(wo[:, :, q * 256:(q + 1) * 256], stage[:]), reads=[stage], writes=[wo])
            fg = kb.sb("fg", [128, D])
            if last:
                kb.dma("sp", fg[:], W["final_g"][:].partition_broadcast(128), reads=[W["final_g"]], writes=[fg])
            mt = [kb.sb(f"mt{i}", [128, 8, 128], BF16) for i in range(2)]
            xt = [kb.sb(f"oxt{i}", [128, D]) for i in range(2)]
            xn = [kb.sb(f"oxn{i}", [128, D]) for i in range(2)]
            tmp = [kb.sb(f"otmp{i}", [128, 512]) for i in range(2)]
            st = [kb.sb(f"ost{i}", [128, 4]) for i in range(2)]
            junk = kb.sb("ojunk", [128, D])
            mixv = mixT.t.rearrange("(j p) t -> p j t", p=128)
            for it, i in enumerate(range(2 if last else 0, NT)):
                m, x, xo, s = mt[it % 2], xt[it % 2], xn[it % 2], st[it % 2]
                sel = 1 if i < 2 else 0
                kb.dma("sp", m[:], mixv[:, :, i * 128:(i + 1) * 128], reads=[mixT], writes=[m])
                src, srcb = x_src(l, i)
                kb.dma("pool", x[:], src, reads=[srcb], writes=[x])
                for hf in range(2):
                    p = PS[(2 * it + hf) % 8]
                    tp = tmp[hf]
                    for j in range(8):
                        kb.op("pe", lambda e, j=j, p=p, m=m, hf=hf: e.matmul(p[:, :], m[:, j, :], wo[:, j, hf * 512:(hf + 1) * 512],
                                                                     start=(j == 0), stop=(j == 7)), reads=[m, wo], writes=[p])
                    kb.op("dve", lambda e, p=p, tp=tp, hf=hf, sel=sel: e.tensor_tensor(
                        tp[:], p[:, :], GT[:, sel, hf * 512:(hf + 1) * 512], ALU.mult), reads=[p, GT], writes=[tp])
                    kb.op("pool", lambda e, tp=tp, hf=hf, x=x, xo=xo: e.tensor_tensor(
                        xo[:, hf * 512:(hf + 1) * 512], x[:, hf * 512:(hf + 1) * 512], tp[:], ALU.add), reads=[x, tp], writes=[xo])
                if not last:
                    kb.dma("sp", xres[i * 128:(i + 1) * 128, :], xo[:], reads=[xo], writes=[xres_b[i]])
                else:
                    kb.op("act", lambda e, xo=xo, s=s: e.activation(junk[:], xo[:], AF.Square, accum_out=s[:, 0:1]),
                          reads=[xo], writes=[junk, s])
                    kb.op("dve", lambda e, s=s: e.tensor_scalar(s[:, 1:2], s[:, 0:1], 1.0 / D, EPS, ALU.mult, ALU.add),
                          reads=[s], writes=[s])
                    kb.op("act", lambda e, s=s: e.sqrt(s[:, 2:3], s[:, 1:2]), reads=[s], writes=[s])
                    kb.op("dve", lambda e, s=s: e.reciprocal(s[:, 3:4], s[:, 2:3]), reads=[s], writes=[s])
                    kb.op("dve", lambda e, xo=xo, s=s, x=x: e.scalar_tensor_tensor(
                        x[:], xo[:], s[:, 3:4], fg[:], ALU.mult, ALU.mult), reads=[xo, s, fg], writes=[x])
                    kb.dma("sp", out[(i - 2) * 128:(i - 1) * 128, :], x[:], reads=[x], writes=[Buf()])


    def conv_tile(l, wt, cw, ncw, jt, Zraw, Zout):
        for ci, (t0, nt) in enumerate(TCH):
            p = PS[ci % 4]
            proj_fm(p, wt, 0, 128, t0, nt)
            kb.op("act", lambda e, p=p, t0=t0, nt=nt: e.copy(Zraw[:, 1 + t0:1 + t0 + nt], p[:, 0:nt]), reads=[p], writes=[Zraw])
        kb.op("dve", lambda e: e.tensor_scalar(Zout[:, :], Zraw[:, 1:T + 1], cw[:, jt, 1:2], None, ALU.mult), reads=[Zraw, cw], writes=[Zout])
        kb.op("dve", lambda e: e.scalar_tensor_tensor(Zout[:, :], Zraw[:, 0:T], cw[:, jt, 0:1], Zout[:, :], ALU.mult, ALU.add),
              reads=[Zraw, cw, Zout], writes=[Zout])
        kb.op("dve", lambda e: e.scalar_tensor_tensor(Zout[:, :], Zraw[:, 2:T + 2], cw[:, jt, 2:3], Zout[:, :], ALU.mult, ALU.add),
              reads=[Zraw, cw, Zout], writes=[Zout])
        kb.op("dve", lambda e: e.scalar_tensor_tensor(Zout[:, C - 1:C], Zraw[:, C + 1:C + 2], ncw[:, jt, 2:3], Zout[:, C - 1:C], ALU.mult, ALU.add),
              reads=[Zraw, ncw, Zout], writes=[Zout])
        kb.op("dve", lambda e: e.scalar_tensor_tensor(Zout[:, C:C + 1], Zraw[:, C:C + 1], ncw[:, jt, 0:1], Zout[:, C:C + 1], ALU.mult, ALU.add),
              reads=[Zraw, ncw, Zout], writes=[Zout])

    def colvec(name, src_ap, srcb, shape, rearr, **kw):
        t = kb.sb(name, shape)
        kb.dma("sp", t[:], src_ap.rearrange(rearr, **kw), reads=[srcb], writes=[t], slow=True)
        return t

    def phase_rwkv_prep(l):
        with kb.scope():
            stage = kb.sb("rstage", [128, 8, 128])
            wts = [kb.sb(f"rwt{i}", [128, 8, 128], BF16) for i in range(2)]
            cw = kb.sb("rcw", [128, 7, 3])
            for k in range(3):
                kb.dma("sp", cw[:, :, k], W["rw_conv"][l, k, :].rearrange("(j p) -> p j", p=128), reads=[W["rw_conv"]], writes=[cw], slow=True)
            ncw = kb.sb("rncw", [128, 7, 3])
            kb.op("dve", lambda e: e.tensor_scalar(ncw[:], cw[:], -1.0, None, ALU.mult), reads=[cw], writes=[ncw])
            kk_ = colvec("rkk", W["rw_k_k"][l, :], W["rw_k_k"], [128, 2], "(j p) -> p j", p=128)
            ka_ = colvec("rka", W["rw_k_a"][l, :], W["rw_k_a"], [128, 2], "(j p) -> p j", p=128)
            omka = kb.sb("romka", [128, 2])
            kb.op("dve", lambda e: e.tensor_scalar(omka[:], ka_[:], -1.0, 1.0, ALU.mult, ALU.add), reads=[ka_], writes=[omka])
            w0_ = kb.sb("rw0", [128, 2, 2])
            a0_ = kb.sb("ra0", [128, 2, 2])
            for d in range(2):
                kb.dma("sp", w0_[:, d, :], W["rw_w0"][l, d, :].rearrange("(j p) -> p j", p=128), reads=[W["rw_w0"]], writes=[w0_], slow=True)
                kb.dma("sp", a0_[:, d, :], W["rw_a0"][l, d, :].rearrange("(j p) -> p j", p=128), reads=[W["rw_a0"]], writes=[a0_], slow=True)
            wup = kb.sb("rwup", [128, 2, 256])
            kb.dma("sp", wup[0:64, :, :], W["rw_w_up"][l, :, :, :].rearrange("d k n -> k d n"), reads=[W["rw_w_up"]], writes=[wup])
            kb.dma("sp", wup[64:128, :, :], W["rw_a_up"][l, :, :, :].rearrange("d k n -> k d n"), reads=[W["rw_a_up"]], writes=[wup])
            Zraw = kb.sb("rZraw", [128, T + 2])
            Zout = kb.sb("rZout", [128, T])
            Z6 = kb.sb("rZ6", [128, T])
            kb.op("pool", lambda e: e.memset(Zraw[:, 0:1], 0.0), writes=[Zraw])
            kb.op("pool", lambda e: e.memset(Zraw[:, T + 1:T + 2], 0.0), writes=[Zraw])
            tA = [kb.sb(f"rtA{i}", [128, 512]) for i in range(2)]
            tB = [kb.sb(f"rtB{i}", [128, 512]) for i in range(2)]
            tC = [kb.sb(f"rtC{i}", [128, 512]) for i in range(2)]
            tD = [kb.sb(f"rtD{i}", [128, 512]) for i in range(2)]
            tE = [kb.sb(f"rtE{i}", [128, 512]) for i in range(2)]
            tG = [kb.sb(f"rtG{i}", [128, 512], BF16) for i in range(2)]
            vt_ = [kb.sb(f"rvt{i}", [128, 128]) for i in range(2)]
            order = [6, 0, 1, 4, 5, 2, 3, 7, 8]
            for oi, jt in enumerate(order):
                wt = wts[oi % 2]
                c0 = RW0 + jt * 128 if jt < 7 else RWG0 + (jt - 7) * 128
                load_w(l, wt, c0, 128, stage)
                if jt >= 7:
                    for ci, (t0, nt) in enumerate(TCH):
                        p = PS[ci % 4]
                        proj_fm(p, wt, 0, 128, t0, nt)
                        g = tG[ci % 2]
                        kb.op("act", lambda e, p=p, g=g, nt=nt: e.activation(g[:, 0:nt], p[:, 0:nt], AF.Silu), reads=[p], writes=[g])
                        kb.dma("sp", RS["SGT"][(jt - 7) * 128:(jt - 6) * 128, t0:t0 + nt], g[:, 0:nt], reads=[g], writes=[Buf()])
                    continue
                conv_tile(l, wt, cw, ncw, jt, Zraw, Z6 if jt == 6 else Zout)
                if jt == 6:
                    kb.op("act", lambda e: e.activation(Z6[0:64, :], Z6[0:64, :], AF.Tanh), reads=[Z6], writes=[Z6])
                elif jt in (0, 1):
                    kb.dma("sp", RS["RT"][jt * 128:(jt + 1) * 128, :], Zout[:, :], reads=[Zout], writes=[Buf()])
                elif jt in (4, 5):
                    kb.dma("sp", RS["VT"][(jt - 4) * 128:(jt - 3) * 128, :], Zout[:, :], reads=[Zout], writes=[Buf()])
                    for i in range(NT):
                        p = PS[4 + i % 2]
                        kb.op("pe", lambda e, p=p, i=i: e.transpose(p[:, 0:128], Zout[:, i * 128:(i + 1) * 128], ident_f[:]),
                              reads=[Zout, ident_f], writes=[p])
                        v = vt_[i % 2]
                        kb.op("act", lambda e, p=p, v=v: e.copy(v[:, :], p[:, 0:128]), reads=[p], writes=[v])
                        kb.dma("pool", RS["VTOK"][i * 128:(i + 1) * 128, (jt - 4) * 128:(jt - 3) * 128], v[:, :], reads=[v], writes=[Buf()])
                else:
                    pt = jt - 2
                    rows = slice(pt * 128, (pt + 1) * 128)
                    for ci, (t0, nt) in enumerate(TCH):
                        a_, b_, c_, d_, e_ = tA[ci % 2], tB[ci % 2], tC[ci % 2], tD[ci % 2], tE[ci % 2]
                        zc = Zout[:, t0:t0 + nt]
                        kb.op("dve", lambda e: e.tensor_scalar(a_[:, 0:nt], zc, kk_[:, pt:pt + 1], None, ALU.mult), reads=[Zout, kk_], writes=[a_])
                        kb.op("act", lambda e: e.activation(b_[:, 0:nt], a_[:, 0:nt], AF.Square), reads=[a_], writes=[b_])
                        p = PS[ci % 2]
                        kb.op("pe", lambda e: e.matmul(p[:, 0:nt], blk64[:], b_[:, 0:nt], start=True, stop=True), reads=[blk64, b_], writes=[p])
                        kb.op("act", lambda e: e.sqrt(b_[:, 0:nt], p[:, 0:nt]), reads=[p], writes=[b_])
                        kb.op("dve", lambda e: e.tensor_scalar(b_[:, 0:nt], b_[:, 0:nt], 1e-12, None, ALU.max), reads=[b_], writes=[b_])
                        kb.op("dve", lambda e: e.reciprocal(b_[:, 0:nt], b_[:, 0:nt]), reads=[b_], writes=[b_])
                        kb.op("dve", lambda e: e.scalar_tensor_tensor(a_[:, 0:nt], a_[:, 0:nt], -1.0, b_[:, 0:nt], ALU.mult, ALU.mult),
                              reads=[a_, b_], writes=[a_])
                        kb.dma("sp", RS["AL"][rows, t0:t0 + nt], a_[:, 0:nt], reads=[a_], writes=[Buf()])
                        for d in range(2):
                            pa = PS[2 + d]
                            kb.op("pe", lambda e: e.matmul(pa[:, 0:nt], wup[64:128, d, pt * 128:(pt + 1) * 128], Z6[64:128, t0:t0 + nt],
                                                           start=True, stop=True), reads=[wup, Z6], writes=[pa])
                            kb.op("act", lambda e: e.activation(c_[:, 0:nt], pa[:, 0:nt], AF.Sigmoid, bias=a0_[:, d, pt:pt + 1]),
                                  reads=[pa, a0_], writes=[c_])
                            kb.op("dve", lambda e: e.scalar_tensor_tensor(d_[:, 0:nt], c_[:, 0:nt], -1.0, a_[:, 0:nt], ALU.mult, ALU.mult),
                                  reads=[c_, a_], writes=[d_])
                            kb.dma("sp", RS[f"B{d}"][rows, t0:t0 + nt], d_[:, 0:nt], reads=[d_], writes=[Buf()])
                            kb.op("dve", lambda e: e.tensor_scalar(c_[:, 0:nt], c_[:, 0:nt], ka_[:, pt:pt + 1], omka[:, pt:pt + 1], ALU.mult, ALU.add),
                                  reads=[c_, ka_, omka], writes=[c_])
                            kb.op("dve", lambda e: e.tensor_tensor(e_[:, 0:nt], c_[:, 0:nt], zc, ALU.mult), reads=[c_, Zout], writes=[e_])
                            kb.dma("pool", RS[f"KD{d}"][rows, t0:t0 + nt], e_[:, 0:nt], reads=[e_], writes=[Buf()])
                            pw = PS[4 + d]
                            kb.op("pe", lambda e: e.matmul(pw[:, 0:nt], wup[0:64, d, pt * 128:(pt + 1) * 128], Z6[0:64, t0:t0 + nt],
                                                           start=True, stop=True), reads=[wup, Z6], writes=[pw])
                            kb.op("act", lambda e: e.activation(c_[:, 0:nt], pw[:, 0:nt], AF.Sigmoid, bias=w0_[:, d, pt:pt + 1]),
                                  reads=[pw, w0_], writes=[c_])
                            kb.op("act", lambda e: e.activation(d_[:, 0:nt], c_[:, 0:nt], AF.Exp, scale=-math.exp(-0.5)),
                                  reads=[c_], writes=[d_])
                            kb.dma("pool", RS[f"W{d}"][rows, t0:t0 + nt], d_[:, 0:nt], reads=[d_], writes=[Buf()])

    def phase_rwkv_scan(l):
        with kb.scope():
            ST = [kb.sb(f"ST{d}", [128, 2, 64]) for d in range(2)]
            for d in range(2):
                kb.op("pool", lambda e, d=d: e.memset(ST[d][:], 0.0), writes=[ST[d]])
            names = ("AL", "W", "B", "KD", "RT")
            ch = [[{n: kb.sb(f"c{n}{d}{i}", [128, 2, 128]) for n in names} for i in range(2)] for d in range(2)]
            vch = [[kb.sb(f"cV{d}{i}", [128, 256]) for i in range(2)] for d in range(2)]
            t1 = [kb.sb(f"st1{d}", [128, 2, 64]) for d in range(2)]
            t2 = [kb.sb(f"st2{d}", [128, 2, 64]) for d in range(2)]
            ysb = [kb.sb(f"ysb{d}", [64, 512]) for d in range(2)]
            psSA, psV, psY = [PS[0], PS[1]], [PS[2], PS[3]], [PS[4], PS[5]]
            border = [1, 0] + list(range(NT - 1, 1, -1))
            for ci in range(NT):
                cidx = [ci, border[ci]]
                cur = []
                for d in range(2):
                    c0 = cidx[d] * 128
                    tl_ = ch[d][ci % 2]
                    for n in names:
                        src = RS[n if n in ("AL", "RT") else f"{n}{d}"]
                        kb.dma("sp" if d == 0 else "pool", tl_[n][:],
                               src.t.rearrange("(pr q) t -> q pr t", q=128)[:, :, c0:c0 + 128], reads=[src], writes=[tl_[n]])
                    vv = vch[d][ci % 2]
                    kb.dma("sp" if d == 0 else "pool", vv[:], RS["VTOK"][c0:c0 + 128, :], reads=[RS["VTOK"]], writes=[vv])
                    cur.append((tl_, vv))
                for tl in range(128):
                    for d in range(2):
                        col = tl if d == 0 else 127 - tl
                        tl_, vv = cur[d]
                        S_, sa, pv, py = ST[d], psSA[d], psV[d], psY[d]
                        for pr in range(2):
                            for hp in range(2):
                                rows = slice(64 * hp, 64 * hp + 64)
                                kb.op("pe", lambda e, pr=pr, rows=rows: e.matmul(
                                    sa[rows, pr * 64:(pr + 1) * 64], tl_["AL"][rows, pr, col:col + 1].broadcast_to([64, 64]),
                                    S_[rows, pr, :], start=True, stop=True), reads=[tl_["AL"], S_], writes=[sa])
                        for pr in range(2):
                            for hp in range(2):
                                rows = slice(64 * hp, 64 * hp + 64)
                                h = 2 * pr + hp
                                kb.op("pe", lambda e, pr=pr, rows=rows, h=h: e.matmul(
                                    pv[rows, pr * 64:(pr + 1) * 64], ident_f[:, col:col + 1].broadcast_to([128, 64]),
                                    vv[:, h * 64:(h + 1) * 64], start=True, stop=True), reads=[ident_f, vv], writes=[pv])
                        for pr in range(2):
                            kb.op("dve", lambda e, pr=pr: e.tensor_scalar(
                                t1[d][:, pr, :], sa[:, pr * 64:(pr + 1) * 64], tl_["B"][:, pr, col:col + 1], None, ALU.mult),
                                reads=[sa, tl_["B"]], writes=[t1[d]])
                            kb.op("dve", lambda e, pr=pr: e.scalar_tensor_tensor(
                                t2[d][:, pr, :], pv[:, pr * 64:(pr + 1) * 64], tl_["KD"][:, pr, col:col + 1], t1[d][:, pr, :], ALU.mult, ALU.add),
                                reads=[pv, tl_["KD"], t1[d]], writes=[t2[d]])
                            kb.op("dve", lambda e, pr=pr: e.scalar_tensor_tensor(
                                S_[:, pr, :], S_[:, pr, :], tl_["W"][:, pr, col:col + 1], t2[d][:, pr, :], ALU.mult, ALU.add),
                                reads=[S_, tl_["W"], t2[d]], writes=[S_])
                        for pr in range(2):
                            for hp in range(2):
                                rows = slice(64 * hp, 64 * hp + 64)
                                h = 2 * pr + hp
                                kb.op("pe", lambda e, pr=pr, rows=rows, h=h: e.matmul(
                                    py[0:64, h * 128 + col:h * 128 + col + 1], S_[rows, pr, :], tl_["RT"][rows, pr, col:col + 1],
                                    start=True, stop=True), reads=[S_, tl_["RT"]], writes=[py])
                for d in range(2):
                    c0 = cidx[d] * 128
                    kb.op("act", lambda e, d=d: e.copy(ysb[d][:, :], psY[d][0:64, :]), reads=[psY[d]], writes=[ysb[d]])
                    dst = RS["YF" if d == 0 else "YB"]
                    kb.dma("sp", dst.t.rearrange("(h v) t -> v h t", v=64)[:, :, c0:c0 + 128],
                           ysb[d][:, :].rearrange("v (h t) -> v h t", h=4), reads=[ysb[d]], writes=[Buf()])

    def phase_rwkv_out(l, with_ctx):
        with kb.scope():
            rk_ = colvec("rrk", W["rw_r_k"][l, :], W["rw_r_k"], [128, 2], "(j p) -> p j", p=128)
            lg_ = colvec("rlg", W["rw_ln_g"][l, :], W["rw_ln_g"], [128, 2], "(j p) -> p j", p=128)
            lb_ = colvec("rlb", W["rw_ln_b"][l, :], W["rw_ln_b"], [128, 2], "(j p) -> p j", p=128)
            nm = ("YF", "YB", "RT", "KD0", "KD1", "VT")
            tl = [{n: kb.sb(f"o{n}{i}", [128, 512]) for n in nm} for i in range(2)]
            sg = [kb.sb(f"osg{i}", [128, 512], BF16) for i in range(2)]
            ob = [kb.sb(f"oob{i}", [128, 512], BF16) for i in range(2)]
            wk = [[kb.sb(f"owk{k}{i}", [128, 512]) for k in range(3)] for i in range(2)]
            it = 0
            for pr in range(2):
                rows = slice(pr * 128, (pr + 1) * 128)
                for (t0, nt) in TCH:
                    if not with_ctx and t0 + nt <= C:
                        continue
                    t_, s_, o_, (a_, b_, c_) = tl[it % 2], sg[it % 2], ob[it % 2], wk[it % 2]
                    for k, n in enumerate(nm):
                        kb.dma("sp" if k % 2 == 0 else "pool", t_[n][:, 0:nt], RS[n][rows, t0:t0 + nt], reads=[RS[n]], writes=[t_[n]])
                    kb.dma("sp", s_[:, 0:nt], RS["SGT"][rows, t0:t0 + nt], reads=[RS["SGT"]], writes=[s_])
                    y = t_["YF"]
                    kb.op("dve", lambda e: e.tensor_tensor(y[:, 0:nt], y[:, 0:nt], t_["YB"][:, 0:nt], ALU.add), reads=[y, t_["YB"]], writes=[y])
                    p1, p2, p3 = PS[(3 * it) % 8], PS[(3 * it + 1) % 8], PS[(3 * it + 2) % 8]
                    kb.op("pe", lambda e: e.matmul(p1[:, 0:nt], blk64[:], y[:, 0:nt], start=True, stop=True), reads=[blk64, y], writes=[p1])
                    kb.op("dve", lambda e: e.scalar_tensor_tensor(a_[:, 0:nt], p1[:, 0:nt], -1.0 / 64, y[:, 0:nt], ALU.mult, ALU.add),
                          reads=[p1, y], writes=[a_])
                    kb.op("act", lambda e: e.activation(b_[:, 0:nt], a_[:, 0:nt], AF.Square), reads=[a_], writes=[b_])
                    kb.op("pe", lambda e: e.matmul(p2[:, 0:nt], blk64[:], b_[:, 0:nt], start=True, stop=True), reads=[blk64, b_], writes=[p2])
                    kb.op("dve", lambda e: e.tensor_scalar(b_[:, 0:nt], p2[:, 0:nt], 1.0 / 64, 64e-5, ALU.mult, ALU.add), reads=[p2], writes=[b_])
                    kb.op("act", lambda e: e.sqrt(b_[:, 0:nt], b_[:, 0:nt]), reads=[b_], writes=[b_])
                    kb.op("dve", lambda e: e.reciprocal(b_[:, 0:nt], b_[:, 0:nt]), reads=[b_], writes=[b_])
                    kb.op("dve", lambda e: e.tensor_tensor(a_[:, 0:nt], a_[:, 0:nt], b_[:, 0:nt], ALU.mult), reads=[a_, b_], writes=[a_])
                    kb.op("dve", lambda e: e.tensor_scalar(a_[:, 0:nt], a_[:, 0:nt], lg_[:, pr:pr + 1], lb_[:, pr:pr + 1], ALU.mult, ALU.add),
                          reads=[a_, lg_, lb_], writes=[a_])
                    kb.op("pool", lambda e: e.tensor_tensor(c_[:, 0:nt], t_["KD0"][:, 0:nt], t_["KD1"][:, 0:nt], ALU.add),
                          reads=[t_["KD0"], t_["KD1"]], writes=[c_])
                    kb.op("dve", lambda e: e.scalar_tensor_tensor(c_[:, 0:nt], t_["RT"][:, 0:nt], rk_[:, pr:pr + 1], c_[:, 0:nt], ALU.mult, ALU.mult),
                          reads=[t_["RT"], rk_, c_], writes=[c_])
                    kb.op("pe", lambda e: e.matmul(p3[:, 0:nt], blk64[:], c_[:, 0:nt], start=True, stop=True), reads=[blk64, c_], writes=[p3])
                    kb.op("dve", lambda e: e.tensor_tensor(c_[:, 0:nt], p3[:, 0:nt], t_["VT"][:, 0:nt], ALU.mult), reads=[p3, t_["VT"]], writes=[c_])
                    kb.op("dve", lambda e: e.tensor_tensor(a_[:, 0:nt], a_[:, 0:nt], c_[:, 0:nt], ALU.add), reads=[a_, c_], writes=[a_])
                    kb.op("pool", lambda e: e.tensor_tensor(o_[:, 0:nt], a_[:, 0:nt], s_[:, 0:nt], ALU.mult), reads=[a_, s_], writes=[o_])
                    kb.dma("sp", mixT[256 + pr * 128:256 + (pr + 1) * 128, t0:t0 + nt], o_[:, 0:nt], reads=[o_], writes=[Buf()])
                    it += 1


    SEGS = {"L": dict(Ls=L, A=32, cbw=32, off=C, ut="UTL"), "C": dict(Ls=C, A=2, cbw=64, off=0, ut="UTC")}

    def phase_hyena_prep(l, with_ctx):
        with kb.scope():
            stage = kb.sb("hstage", [128, 8, 128])
            wts = [kb.sb(f"hwt{i}", [128, 8, 128], BF16) for i in range(2)]
            cw = kb.sb("hcw", [128, 6, 3])
            for k in range(3):
                kb.dma("sp", cw[:, :, k], W["hy_conv"][l, k, :].rearrange("(j p) -> p j", p=128), reads=[W["hy_conv"]], writes=[cw], slow=True)
            ncw = kb.sb("hncw", [128, 6, 3])
            kb.op("dve", lambda e: e.tensor_scalar(ncw[:], cw[:], -1.0, None, ALU.mult), reads=[cw], writes=[ncw])
            Zraw = kb.sb("hZraw", [128, T + 2])
            Zout = kb.sb("hZout", [128, T])
            kb.op("pool", lambda e: e.memset(Zraw[:, 0:1], 0.0), writes=[Zraw])
            kb.op("pool", lambda e: e.memset(Zraw[:, T + 1:T + 2], 0.0), writes=[Zraw])
            ub = kb.sb("hub", [128, 32 * 128])
            tG = [kb.sb(f"htG{i}", [128, 512], BF16) for i in range(2)]
            for oi, jt in enumerate(range(8)):
                wt = wts[oi % 2]
                c0 = HY0 + jt * 128 if jt < 6 else HYG0 + (jt - 6) * 128
                load_w(l, wt, c0, 128, stage)
                if jt >= 6:
                    for ci, (t0, nt) in enumerate(TCH):
                        p = PS[ci % 4]
                        proj_fm(p, wt, 0, 128, t0, nt)
                        g = tG[ci % 2]
                        kb.op("act", lambda e, p=p, g=g, nt=nt: e.activation(g[:, 0:nt], p[:, 0:nt], AF.Silu), reads=[p], writes=[g])
                        kb.dma("sp", HS["SG"][(jt - 6) * 128:(jt - 5) * 128, t0:t0 + nt], g[:, 0:nt], reads=[g], writes=[Buf()])
                    continue
                conv_tile(l, wt, cw, ncw, jt, Zraw, Zout)
                arr, half = jt // 2, jt % 2
                for sn in (("L", "C") if with_ctx else ("L",)):
                    sg = SEGS[sn]
                    A, cbw, off = sg["A"], sg["cbw"], sg["off"]
                    G = 128 // A
                    ncg = 128 // G
                    ubv = ub[:, 0:A * 128].rearrange("p (g a c) -> p g a c", g=ncg, a=A)
                    for a in range(A):
                        p = PS[4 + (a // 4) % 4]
                        kb.op("pe", lambda e, p=p, a=a, A=A, off=off: e.transpose(
                            p[:, (a % 4) * 128:(a % 4 + 1) * 128], Zout[:, off + a:off + a + 127 * A + 1:A], ident_f[:]),
                            reads=[Zout, ident_f], writes=[p])
                        if a % 4 == 3 or a == A - 1:
                            a0 = (a // 4) * 4
                            na = a - a0 + 1
                            kb.op("act", lambda e, p=p, a0=a0, na=na, G=G: e.copy(
                                ubv[:, :, a0:a0 + na, :], p[:, 0:na * 128].rearrange("p (a g c) -> p g a c", a=na, c=G)), reads=[p], writes=[ub])
                    nb = 128 // cbw
                    bsz = A * cbw
                    for b in range(nb):
                        dst = HS[sg["ut"]][arr, half * nb + b, :, :]
                        kb.dma("sp" if b % 2 == 0 else "pool", dst, ub[:, b * bsz:(b + 1) * bsz], reads=[ub], writes=[Buf()])

    def cmul(dre, dim_, sre, sim, tre, tim, conj, srcb, tabb, dstb, tmp):
        t1, t2 = tmp
        sh = tuple(slice(None) for _ in range(1))
        kb.op("dve", lambda e: e.tensor_tensor(t1, sre, tre, ALU.mult), reads=srcb + tabb, writes=[dstb[2]])
        kb.op("dve", lambda e: e.tensor_tensor(t2, sim, tim, ALU.mult), reads=srcb + tabb, writes=[dstb[3]])
        kb.op("pool", lambda e: e.tensor_tensor(dre, t1, t2, ALU.add if conj else ALU.subtract), reads=[dstb[2], dstb[3]], writes=[dstb[0]])
        kb.op("dve", lambda e: e.tensor_tensor(t1, sim, tre, ALU.mult), reads=srcb + tabb + [dstb[0]], writes=[dstb[2]])
        kb.op("dve", lambda e: e.tensor_tensor(t2, sre, tim, ALU.mult), reads=srcb + tabb + [dstb[0]], writes=[dstb[3]])
        kb.op("pool", lambda e: e.tensor_tensor(dim_, t1, t2, ALU.subtract if conj else ALU.add), reads=[dstb[2], dstb[3]], writes=[dstb[1]])

    def phase_hyena_main(l, with_ctx):
        PI = math.pi
        with kb.scope():
            fw1 = kb.sb("hfw1", [33, 64])
            fw2 = kb.sb("hfw2", [64, 64])
            fw3 = kb.sb("hfw3", [64, 1024])
            kb.dma("sp", fw1[:], W["hy_fw1"][l, :, :], reads=[W["hy_fw1"]], writes=[fw1])
            kb.dma("sp", fw2[:], W["hy_fw2"][l, :, :], reads=[W["hy_fw2"]], writes=[fw2])
            kb.dma("sp", fw3[:], W["hy_fw3"][l, :, :], reads=[W["hy_fw3"]], writes=[fw3])
            fb1 = colvec("hfb1", W["hy_fb1"][l, :], W["hy_fb1"], [64, 1], "(d o) -> d o", o=1)
            fb2 = colvec("hfb2", W["hy_fb2"][l, :], W["hy_fb2"], [64, 1], "(d o) -> d o", o=1)
            frq = colvec("hfrq", W["hy_freq"][l, :], W["hy_freq"], [64, 1], "(d o) -> d o", o=1)
            brow = kb.sb("hbrow", [1, 512])
            kb.dma("sp", brow[:], W["hy_bias"][l, :, :].rearrange("o c -> (o c)").rearrange("(x n) -> x n", x=1), reads=[W["hy_bias"]], writes=[brow])
            for sn in (("L", "C") if with_ctx else ("L",)):
                sg = SEGS[sn]
                Ls, A, cbw, off = sg["Ls"], sg["A"], sg["cbw"], sg["off"]
                G = 128 // A
                N = 2 * Ls
                ngr = cbw // G
                nblk = 256 // cbw
                pre = f"hy{sn}_"
                with kb.scope():
                    def ld(nm, shape):
                        t = kb.sb("k" + nm, shape)
                        src = CT[pre + nm]
                        kb.dma("sp", t[:], src.t, reads=[src], writes=[t])
                        return t
                    F256 = ld("F256", [128, 2, 512]); TWC = ld("TWC", [128, 256]); TWS = ld("TWS", [128, 256])
                    Dre = ld("Dre", [128, 128]); Dim = ld("Dim", [128, 128]); nDim = ld("nDim", [128, 128])
                    E1 = ld("E1", [128, 256]); E2 = ld("E2", [128, 256])
                    TW2C = ld("TW2C", [128, 2, 128]); TW2S = ld("TW2S", [128, 2, 128])
                    IC = ld("IC", [128, 2, 128]); IS = ld("IS", [128, 2, 128])
                    h2T = kb.sb("h2T", [64, N])
                    with kb.scope():
                        zT = kb.sb("zT", [33, N])
                        kb.dma("sp", zT[:], CT[pre + "zT"].t, reads=[CT[pre + "zT"]], writes=[zT])
                        h1T = kb.sb("h1T", [64, N])
                        arg = [kb.sb(f"harg{i}", [64, 512]) for i in range(2)]
                        wr = [kb.sb(f"hwr{i}", [64, 512]) for i in range(2)]
                        for (src, K_, wgt, bcol, dst) in ((zT, 33, fw1, fb1, h1T), (h1T, 64, fw2, fb2, h2T)):
                            for ci, n0 in enumerate(range(0, N, 512)):
                                p = PS[ci % 4]
                                ag = arg[ci % 2]
                                kb.op("pe", lambda e: e.matmul(p[0:64, :], wgt[0:K_, :], src[0:K_, n0:n0 + 512], start=True, stop=True),
                                      reads=[wgt, src], writes=[p])
                                kb.op("dve", lambda e: e.tensor_scalar(ag[:, :], p[0:64, :], bcol[:, 0:1], frq[:, 0:1], ALU.add, ALU.mult),
                                      reads=[p, bcol, frq], writes=[ag])
                                for _w in range(2):
                                    kb.op("dve", lambda e: e.tensor_scalar(wr[0][:, :], ag[:, :], PI, -2 * PI, ALU.is_gt, ALU.mult), reads=[ag], writes=[wr[0]])
                                    kb.op("dve", lambda e: e.tensor_scalar(wr[1][:, :], ag[:, :], -PI, 2 * PI, ALU.is_lt, ALU.mult), reads=[ag], writes=[wr[1]])
                                    kb.op("dve", lambda e: e.tensor_tensor(ag[:, :], ag[:, :], wr[0][:, :], ALU.add), reads=[ag, wr[0]], writes=[ag])
                                    kb.op("dve", lambda e: e.tensor_tensor(ag[:, :], ag[:, :], wr[1][:, :], ALU.add), reads=[ag, wr[1]], writes=[ag])
                                kb.op("act", lambda e: e.activation(dst[:, n0:n0 + 512], ag[:, :], AF.Sin), reads=[ag], writes=[dst])
                    KT = [kb.sb(f"KT{o}", [128, 2, ngr, A, G]) for o in range(2)]
                    KS = [kb.sb(f"KS{o}", [128, ngr, 512]) for o in range(2)]
                    DECt = kb.sb("DECt", [128, 2, ngr, A, G])
                    part = kb.sb("hpart", [128, cbw])
                    rn = kb.sb("hrn", [128, cbw])
                    ex = kb.sb("hex", [1, cbw])
                    uv = kb.sb("huv", [128, ngr, A * G]); x1 = kb.sb("hx1", [128, ngr, A * G]); x2 = kb.sb("hx2", [128, ngr, A * G])
                    u2 = kb.sb("hu2", [128, ngr, A * G]); res = kb.sb("hres", [128, A, cbw])
                    Bp = [kb.sb(f"hBp{i}", [128, 256]) for i in range(4)]
                    Bpb = [Buf() for _ in range(4)]
                    Yp = [kb.sb(f"hYp{i}", [128, 256]) for i in range(4)]
                    Ypb = [Buf() for _ in range(4)]
                    Gp = [kb.sb(f"hGp{i}", [128, 2, 128]) for i in range(4)]
                    Gpb = [Buf() for _ in range(4)]
                    Fm = kb.sb("hFm", [cbw, Ls])
                    sgm = kb.sb("hsgm", [cbw, Ls], BF16)
                    Fo = kb.sb("hFo", [cbw, Ls], BF16)

                    def fwd_fft(lhs_chunks, lhs_bufs, psB, psX):
                        n = len(lhs_chunks)
                        for i, (ap, hf) in enumerate(lhs_chunks):
                            kb.op("pe", lambda e, ap=ap, hf=hf, i=i: e.matmul(psB[:, :], ap, F256[:, hf, :], start=(i == 0), stop=(i == n - 1)),
                                  reads=lhs_bufs + [F256], writes=[psB])
                        cmul(Bp[0][:, :], Bp[1][:, :], psB[:, 0:256], psB[:, 256:512], TWC[:, :], TWS[:, :], True,
                             [psB], [TWC, TWS], Bpb, (Bp[2][:, :], Bp[3][:, :]))
                        kb.op("pe", lambda e: e.matmul(psX[:, 0:256], Dre[:, :], Bp[0][:, :], start=True, stop=False), reads=[Dre, Bpb[0]], writes=[psX])
                        kb.op("pe", lambda e: e.matmul(psX[:, 0:256], nDim[:, :], Bp[1][:, :], start=False, stop=True), reads=[nDim, Bpb[1]], writes=[psX])
                        kb.op("pe", lambda e: e.matmul(psX[:, 256:512], Dim[:, :], Bp[0][:, :], start=True, stop=False), reads=[Dim, Bpb[0]], writes=[psX])
                        kb.op("pe", lambda e: e.matmul(psX[:, 256:512], Dre[:, :], Bp[1][:, :], start=False, stop=True), reads=[Dre, Bpb[1]], writes=[psX])

                    def conv_group(src, src_b, g, o, mulv, mul_b, dst_ap, dst_b, it):
                        psB, psX, psG, psy = PS[it % 2], PS[2 + it % 2], PS[4 + it % 2], PS[6 + it % 2]
                        fwd_fft([(src[:, g, :], 0)], [src_b], psB, psX)
                        cmul(Yp[0][:, :], Yp[1][:, :], psX[:, 0:256], psX[:, 256:512], KS[o][:, g, 0:256], KS[o][:, g, 256:512], False,
                             [psX], [KS[o]], Ypb, (Yp[2][:, :], Yp[3][:, :]))
                        for chn in range(2):
                            fs = slice(chn * 128, (chn + 1) * 128)
                            kb.op("pe", lambda e, fs=fs, chn=chn: e.matmul(psG[:, chn * 256:(chn + 1) * 256], Yp[0][:, fs], E1[:, :], start=True, stop=False),
                                  reads=[Ypb[0], E1], writes=[psG])
                            kb.op("pe", lambda e, fs=fs, chn=chn: e.matmul(psG[:, chn * 256:(chn + 1) * 256], Yp[1][:, fs], E2[:, :], start=False, stop=True),
                                  reads=[Ypb[1], E2], writes=[psG])
                        pg = psG[:, :].rearrange("p (ch ri c) -> p ch ri c", ch=2, ri=2)
                        cmul(Gp[0][:, :, :], Gp[1][:, :, :], pg[:, :, 0, :], pg[:, :, 1, :], TW2C[:, :, :], TW2S[:, :, :], False,
                             [psG], [TW2C, TW2S], Gpb, (Gp[2][:, :, :], Gp[3][:, :, :]))
                        k = 0
                        for chn in range(2):
                            for (tab, gsrc, gb) in ((IC, Gp[0], Gpb[0]), (IS, Gp[1], Gpb[1])):
                                kb.op("pe", lambda e, chn=chn, tab=tab, gsrc=gsrc, k=k: e.matmul(
                                    psy[:, 0:128], tab[:, chn, :], gsrc[:, chn, :], start=(k == 0), stop=(k == 3)), reads=[tab, gb], writes=[psy])
                                k += 1
                        kb.op("dve", lambda e: e.tensor_tensor(dst_ap, psy[:, 0:128].rearrange("p (c a) -> p a c", a=A),
                                                               mulv[:, g, :].rearrange("p (a c) -> p a c", c=G), ALU.mult),
                              reads=[psy, mul_b], writes=[dst_b])

                    git = 0
                    for cb in range(nblk):
                        kb.dma("sp", DECt[:].rearrange("p h g a c -> p (h g a c)"), CT[pre + "DEC"][cb, :, :], reads=[CT[pre + "DEC"]], writes=[DECt])
                        for ai, tile_ in enumerate((uv, x1, x2)):
                            kb.dma("pool", tile_[:].rearrange("p g x -> p (g x)"), HS[sg["ut"]][ai, cb, :, :], reads=[HS[sg["ut"]]], writes=[tile_])
                        for o in range(2):
                            for hf in range(2):
                                col0 = o * 512 + hf * 256 + cb * cbw
                                npb = 512 // cbw
                                for a in range(A):
                                    p = PS[(a // npb) % 4]
                                    kb.op("pe", lambda e, p=p, a=a, hf=hf, col0=col0, npb=npb: e.matmul(
                                        p[:, (a % npb) * cbw:(a % npb + 1) * cbw], h2T[0:64, hf * 128 * A + a:hf * 128 * A + a + 127 * A + 1:A],
                                        fw3[0:64, col0:col0 + cbw], start=True, stop=True), reads=[h2T, fw3], writes=[p])
                                    if a % npb == npb - 1 or a == A - 1:
                                        a0 = (a // npb) * npb
                                        na = a - a0 + 1
                                        kb.op("dve", lambda e, p=p, a0=a0, na=na, hf=hf, o=o: e.tensor_tensor(
                                            KT[o][:, hf, :, a0:a0 + na, :], p[:, 0:na * cbw].rearrange("p (a g c) -> p g a c", a=na, c=G),
                                            DECt[:, hf, :, a0:a0 + na, :], ALU.mult), reads=[p, DECt], writes=[KT[o]])
                            kb.op("dve", lambda e, o=o: e.tensor_reduce(part[:, :].rearrange("p (g c) -> p g c", c=G),
                                                                        KT[o][:, :, :, :, :].rearrange("p h g a c -> p g c h a"), AX.XY, ALU.add,
                                                                        apply_absolute_value=True), reads=[KT[o]], writes=[part])
                            pe_ = PS[4]
                            kb.op("pe", lambda e, o=o: e.matmul(pe_[0:1, 0:cbw], h2T[0:64, 0:1], fw3[0:64, o * 512 + 256 + cb * cbw:o * 512 + 256 + (cb + 1) * cbw],
                                                                start=True, stop=True), reads=[h2T, fw3], writes=[pe_])
                            kb.op("act", lambda e: e.activation(ex[0:1, :], pe_[0:1, 0:cbw], AF.Abs), reads=[pe_], writes=[ex])
                            kb.op("dve", lambda e: e.tensor_tensor(part[0:1, :], part[0:1, :], ex[0:1, :], ALU.add), reads=[part, ex], writes=[part])
                            pt_ = PS[5]
                            kb.op("pe", lambda e: e.matmul(pt_[:, 0:cbw], ones_f[:, :], part[:, :], start=True, stop=True), reads=[ones_f, part], writes=[pt_])
                            kb.op("dve", lambda e: e.reciprocal(rn[:, :], pt_[:, 0:cbw]), reads=[pt_], writes=[rn])
                            for hf in range(2):
                                kb.op("dve", lambda e, o=o, hf=hf: e.tensor_tensor(
                                    KT[o][:, hf, :, :, :], KT[o][:, hf, :, :, :],
                                    rn[:, :].rearrange("p (g c) -> p g c", c=G).unsqueeze(2).broadcast_to([128, ngr, A, G]), ALU.mult),
                                    reads=[KT[o], rn], writes=[KT[o]])
                            kb.op("dve", lambda e, o=o: e.tensor_tensor(
                                KT[o][0:1, 0, :, 0, :], KT[o][0:1, 0, :, 0, :],
                                brow[0:1, o * 256 + cb * cbw:o * 256 + (cb + 1) * cbw].rearrange("p (g c) -> p g c", c=G), ALU.add),
                                reads=[KT[o], brow], writes=[KT[o]])
                            for g in range(ngr):
                                psB, psX = PS[git % 2], PS[2 + git % 2]
                                fwd_fft([(KT[o][:, 0, g, :, :].rearrange("p a c -> p (a c)"), 0),
                                         (KT[o][:, 1, g, :, :].rearrange("p a c -> p (a c)"), 1)], [KT[o]], psB, psX)
                                kb.op("act", lambda e, o=o, g=g, psX=psX: e.copy(KS[o][:, g, :], psX[:, :]), reads=[psX], writes=[KS[o]])
                                git += 1
                        for g in range(ngr):
                            conv_group(uv, uv, g, 0, x1, x1, u2[:, g, :].rearrange("p (a c) -> p a c", c=G), u2, git)
                            git += 1
                        for g in range(ngr):
                            conv_group(u2, u2, g, 1, x2, x2, res[:, :, g * G:(g + 1) * G], res, git)
                            git += 1
                        kb.dma("sp", sgm[:], HS["SG"][cb * cbw:(cb + 1) * cbw, off:off + Ls], reads=[HS["SG"]], writes=[sgm])
                        Fv = Fm[:, :].rearrange("c (p a) -> c p a", a=A)
                        for a in range(A):
                            p = PS[4 + (a // 4) % 4]
                            kb.op("pe", lambda e, p=p, a=a: e.transpose(p[0:cbw, (a % 4) * 128:(a % 4 + 1) * 128], res[:, a, :], ident_f[:]),
                                  reads=[res, ident_f], writes=[p])
                            if a % 4 == 3 or a == A - 1:
                                a0 = (a // 4) * 4
                                na = a - a0 + 1
                                kb.op("act", lambda e, p=p, a0=a0, na=na: e.copy(
                                    Fv[:, :, a0:a0 + na], p[0:cbw, 0:na * 128].rearrange("c (a p) -> c p a", p=128)), reads=[p], writes=[Fm])
                        kb.op("pool", lambda e: e.tensor_tensor(Fo[:, :], Fm[:, :], sgm[:, :], ALU.mult), reads=[Fm, sgm], writes=[Fo])
                        kb.dma("sp", mixT[cb * cbw:(cb + 1) * cbw, off:off + Ls], Fo[:, :], reads=[Fo], writes=[Buf()])

    dbgn = [n for n, _ in dbg]
    for l in range(depth):
        last = (l == DEPTH - 1)
        with kb.scope():
            hT = kb.sb("hT", [128, 8, T], BF16)
            G1 = kb.sb("G1", [128, 2, D])
            SH = kb.sb("SH", [128, 2, D])
            phase_mod(l)
            phase_norm(l)
            if "noattn" not in dbgn:
                phase_attn(l, False, not last)
                phase_attn(l, True, not last)
            if "norw" not in dbgn:
                phase_rwkv_prep(l)
            if "nohy" not in dbgn:
                phase_hyena_prep(l, not last)
            if "hT" in dbgn:
                tmp = kb.sb("dbghT", [128, T])
                for j in range(8):
                    kb.op("dve", lambda e, j=j, tmp=tmp: e.tensor_copy(tmp[:], hT[:, j, :]), reads=[hT], writes=[tmp])
                    kb.dma("sp", dbg_t["hT"][:, j, :], tmp[:], reads=[tmp], writes=[dbg_t["hT"]])
        if "norw" not in dbgn:
            phase_rwkv_scan(l)
            phase_rwkv_out(l, not last)
        if "nohy" not in dbgn:
            phase_hyena_main(l, not last)
        if "noout" not in dbgn:
            phase_out(l, last)
    for n, s_ in dbg:
        if n == "xres":
            with kb.scope():
                tx = kb.sb("dbgx", [128, D])
                for i in range(NT):
                    kb.dma("sp", tx[:], xres[i * 128:(i + 1) * 128, :], reads=[xres_b[i]], writes=[tx])
                    kb.dma("sp", dbg_t[n][i * 128:(i + 1) * 128, :], tx[:], reads=[tx], writes=[dbg_t[n]])
        if n == "mixT":
            with kb.scope():
                tmpb = kb.sb("dbgmb", [128, T], BF16)
                tmpf = kb.sb("dbgmf", [128, T])
                for j in range(8):
                    kb.dma("sp", tmpb[:], mixT[j * 128:(j + 1) * 128, :], reads=[mixT], writes=[tmpb])
                    kb.op("dve", lambda e, tmpb=tmpb, tmpf=tmpf: e.tensor_copy(tmpf[:], tmpb[:]), reads=[tmpb], writes=[tmpf])
                    kb.dma("sp", dbg_t[n][j * 128:(j + 1) * 128, :], tmpf[:], reads=[tmpf], writes=[dbg_t[n]])
    kb.finish()
    kb.es.close()
    return kb, cst


_PROG = {}


def kernel(**inputs):
    if "p" not in _PROG:
        _PROG["p"] = build()
    kb, cst = _PROG["p"]
    f = lambda a: np.ascontiguousarray(np.asarray(a, dtype=np.float32))
    shared = {}
    for n in inputs:
        if n in ("x", "c", "ctx", "c_ctx"):
            continue
        shared[n] = f(inputs[n])
    shared["c_ctx"] = f(inputs["c_ctx"])
    for n, a in cst.items():
        shared["k_" + n] = np.ascontiguousarray(a)
    x, c, ctx = f(inputs["x"]), f(inputs["c"]), f(inputs["ctx"])
    B = x.shape[0]
    in_maps = []
    for b in range(B):
        m = dict(shared)
        m["x"] = np.ascontiguousarray(x[b])
        m["c"] = np.ascontiguousarray(c[b])
        m["ctx"] = np.ascontiguousarray(ctx[b])
        in_maps.append(m)
    res = run_bass_kernel_spmd(kb.nc, in_maps, core_ids=list(range(B)))
    return np.stack([np.asarray(res.results[b]["out"], dtype=np.float32) for b in range(B)], axis=0)
```

```python
import contextlib
import math
import numpy as np
import ml_dtypes
import concourse.bass as bass
import concourse.mybir as mybir
from concourse.bass_utils import run_bass_kernel_spmd

F32 = mybir.dt.float32
BF16 = mybir.dt.bfloat16
ALU = mybir.AluOpType
AF = mybir.ActivationFunctionType
AX = mybir.AxisListType

D = 1024
L = 4096
C = 256
T = L + C
NT = T // 128
DEPTH = 4
D_IN = 3712
HY0, HYG0, RW0, RWG0, WA0, WAG0, FA0, FAG0 = 0, 768, 1024, 1920, 2176, 2688, 2944, 3456
EPS = 1e-6
NSLOT = 12
import os
RW_STAGE = int(os.environ.get('RW_STAGE', '99'))


class Buf:
    def __init__(self, name=""):
        self.name = name
        self.w = None
        self.r = {}

    def wdeps(self):
        return [self.w] if self.w is not None else []

    def rdeps(self):
        return list(self.r.values())

    def add_reader(self, tok):
        k = tok[:2]
        if k not in self.r or self.r[k][2] < tok[2]:
            self.r[k] = tok

    def set_writer(self, tok):
        self.w = tok
        self.r = {}


class Tile(Buf):
    def __init__(self, name, t):
        super().__init__(name)
        self.t = t

    def __getitem__(self, key):
        return self.t[key]


class KB:
    def __init__(self):
        self.nc = bass.Bass("TRN2", target_bir_lowering=False)
        nc = self.nc
        self.es = contextlib.ExitStack()
        self.eng = {"pe": nc.tensor, "act": nc.scalar, "dve": nc.vector, "pool": nc.gpsimd, "sp": nc.sync}
        self.sem = {}
        self.cnt = {}
        self.waited = {e: {} for e in self.eng}
        for e in self.eng:
            self.sem[e] = self.es.enter_context(nc.semaphore("s_" + e))
            self.cnt[e] = 0
        self.slots = {}
        self.slot_i = {}
        for q in ("sp", "act", "pool"):
            self.slots[q] = [[self.es.enter_context(nc.semaphore(f"d_{q}{i}")), 0] for i in range(NSLOT)]
            self.slot_i[q] = 0
        self.n_ins = 0

    def sb(self, name, shape, dt=F32):
        self.uid = getattr(self, "uid", 0) + 1
        name = f"{name}_{self.uid}"
        return Tile(name, self.es.enter_context(self.nc.sbuf_tensor(name, list(shape), dt)))

    def ps(self, name, shape, dt=F32):
        return Tile(name, self.es.enter_context(self.nc.psum_tensor(name, list(shape), dt)))

    def dram(self, name, shape, dt=F32, kind="Internal"):
        t = self.nc.dram_tensor(name, list(shape), dt, kind=kind)
        b = Tile(name, t.ap())
        return b

    def _tok_sem(self, tok):
        if tok[0] == "e":
            return ("e", tok[1]), self.sem[tok[1]], tok[2]
        return ("d", tok[1]), self.slots[tok[1][0]][tok[1][1]][0], tok[2]

    def _wait(self, e, toks):
        need = {}
        for tok in toks:
            if tok is None:
                continue
            key, sem, val = self._tok_sem(tok)
            if tok[0] == "e" and tok[1] == e and e == "pe":
                continue
            if self.waited[e].get(key, 0) >= val:
                continue
            if key not in need or need[key][1] < val:
                need[key] = (sem, val)
        for key, (sem, val) in need.items():
            self.eng[e].wait_ge(sem, val)
            self.waited[e][key] = val

    def op(self, e, fn, reads=(), writes=()):
        toks = []
        for b in reads:
            toks += b.wdeps()
        for b in writes:
            toks += b.wdeps() + b.rdeps()
        self._wait(e, toks)
        ins = fn(self.eng[e])
        self.cnt[e] += 1
        ins.then_inc(self.sem[e], 1)
        tok = ("e", e, self.cnt[e])
        for b in reads:
            b.add_reader(tok)
        for b in writes:
            b.set_writer(tok)
        self.n_ins += 1
        return ins

    def dma(self, q, out, in_, reads=(), writes=(), slow=False):
        i = self.slot_i[q]
        self.slot_i[q] = (i + 1) % NSLOT
        slot = self.slots[q][i]
        toks = []
        if slot[1] > 0:
            toks.append(("d", (q, i), slot[1]))
        for b in reads:
            toks += b.wdeps()
        for b in writes:
            toks += b.wdeps() + b.rdeps()
        self._wait(q, toks)
        if slow:
            ins = self.eng[q].dma_start(out=out, in_=in_, allow_slow_non_contiguous=True)
        else:
            ins = self.eng[q].dma_start(out=out, in_=in_)
        ins.then_inc(slot[0], 16)
        slot[1] += 16
        tok = ("d", (q, i), slot[1])
        for b in reads:
            b.add_reader(tok)
        for b in writes:
            b.set_writer(tok)
        self.n_ins += 1
        return ins

    def barrier(self):
        toks = [("e", e, self.cnt[e]) for e in self.eng if self.cnt[e] > 0]
        for q in self.slots:
            for i, s in enumerate(self.slots[q]):
                if s[1] > 0:
                    toks.append(("d", (q, i), s[1]))
        for e in self.eng:
            self._wait(e, toks)

    def finish(self):
        self.barrier()

    @contextlib.contextmanager
    def scope(self):
        es = contextlib.ExitStack()
        old = self.es
        self.es = es
        try:
            yield
        finally:
            self.barrier()
            self.es = old
            es.close()


def host_consts():
    cst = {}
    cst["ident_bf"] = np.eye(128, dtype=np.float32).astype(ml_dtypes.bfloat16)
    cst["ident_f"] = np.eye(128, dtype=np.float32)
    blk = np.zeros((128, 128), np.float32)
    blk[:64, :64] = 1.0
    blk[64:, 64:] = 1.0
    cst["blk64"] = blk
    cst["ones_f"] = np.ones((128, 128), np.float32)
    t = np.arange(L)
    row = (t // 64).astype(np.float32)
    col = (t % 64).astype(np.float32)
    inv = (10000.0 ** (-np.arange(16, dtype=np.float32) / 16)).astype(np.float32)
    cosT = np.zeros((128, L), np.float32)
    sinT = np.zeros((128, L), np.float32)
    perm = np.zeros((128, 128), np.float32)
    for p in range(128):
        d = p % 64
        sec, half, f = d // 32, (d % 32) // 16, d % 16
        pos = row if sec == 0 else col
        ang = (pos * inv[f]).astype(np.float32)
        cosT[p] = np.cos(ang)
        sinT[p] = np.sin(ang)
        if half == 0:
            perm[p + 16, p] = -1.0
        else:
            perm[p - 16, p] = 1.0
    cst["rope_cos"] = cosT
    cst["rope_sin"] = sinT
    cst["rope_perm"] = perm
    i = np.arange(128)[:, None]
    j = np.arange(384)[None, :]
    cst["wmask"] = np.where((j >= i) & (j <= i + 256), 0.0, -1e30).astype(np.float32)
    ii = np.arange(64)
    ms = np.zeros((128, 2, 64), np.float32); mts = np.zeros((128, 2, 64), np.float32); mti = np.zeros((128, 2, 64), np.float32)
    for hp in range(2):
        rows = slice(hp * 64, hp * 64 + 64)
        ms[rows, 0, :] = (ii[None, :] < ii[:, None]); ms[rows, 1, :] = (ii[None, :] > ii[:, None])
        mts[rows, 0, :] = (ii[:, None] < ii[None, :]); mts[rows, 1, :] = (ii[:, None] > ii[None, :])
        mti[rows, 0, :] = (ii[:, None] <= ii[None, :]); mti[rows, 1, :] = (ii[:, None] >= ii[None, :])
    cst["rw_ms"] = ms; cst["rw_mts"] = mts; cst["rw_mti"] = mti
    cst["rw_id2"] = np.concatenate([np.eye(64, dtype=np.float32)] * 2, 0)
    cst.update(hy_consts(L, 32, 32, "L"))
    cst.update(hy_consts(C, 2, 64, "C"))
    return cst


def hy_consts(Ls, A, cbw, tag):
    G = 128 // A
    N = 2 * Ls
    out = {}
    p = np.arange(128)
    f1 = np.arange(256)
    F = np.zeros((128, 2, 512), np.float64)
    for h in range(2):
        pp = h * 128 + p
        ang = 2 * np.pi * ((pp[:, None] * f1[None, :]) % 256) / 256
        F[:, h, 0:256] = np.cos(ang)
        F[:, h, 256:512] = -np.sin(ang)
    out["F256"] = F
    a_of_row = np.arange(128) // G
    th = 2 * np.pi * ((a_of_row[:, None] * f1[None, :]) % N) / N
    out["TWC"] = np.cos(th)
    out["TWS"] = np.sin(th)
    Dre = np.zeros((128, 128)); Dim = np.zeros((128, 128))
    E1 = np.zeros((128, 256)); E2 = np.zeros((128, 256))
    for a in range(A):
        for c in range(G):
            for f2 in range(A):
                ph = 2 * np.pi * ((a * f2) % A) / A
                Dre[a * G + c, c * A + f2] = np.cos(ph)
                Dim[a * G + c, c * A + f2] = -np.sin(ph)
                E1[c * A + f2, c * A + a] = np.cos(ph)
                E1[c * A + f2, 128 + c * A + a] = np.sin(ph)
                E2[c * A + f2, c * A + a] = -np.sin(ph)
                E2[c * A + f2, 128 + c * A + a] = np.cos(ph)
    out["Dre"] = Dre; out["Dim"] = Dim; out["nDim"] = -Dim; out["E1"] = E1; out["E2"] = E2
    a_of_col = np.arange(128) % A
    TW2C = np.zeros((128, 2, 128)); TW2S = np.zeros((128, 2, 128))
    IC = np.zeros((128, 2, 128)); IS = np.zeros((128, 2, 128))
    for ch in range(2):
        ff = ch * 128 + np.arange(128)
        th2 = 2 * np.pi * ((ff[:, None] * a_of_col[None, :]) % N) / N
        TW2C[:, ch, :] = np.cos(th2) / N
        TW2S[:, ch, :] = np.sin(th2) / N
        ph = 2 * np.pi * ((ff[:, None] * p[None, :]) % 256) / 256
        IC[:, ch, :] = np.cos(ph)
        IS[:, ch, :] = -np.sin(ph)
    out["TW2C"] = TW2C; out["TW2S"] = TW2S; out["IC"] = IC; out["IS"] = IS
    tp = np.arange(N)
    pos = np.where(tp < Ls, tp, N - tp).astype(np.float64)
    tn = (pos / (Ls - 1)).astype(np.float32)
    w = ((2.0 * math.pi / Ls) * pos).astype(np.float32)
    fb = np.linspace(1e-4, 15.0, 16, dtype=np.float32)
    zT = np.zeros((33, N), np.float32)
    zT[0] = tn
    zT[1:17] = np.cos(fb[:, None] * w[None, :])
    zT[17:33] = np.sin(fb[:, None] * w[None, :])
    out["zT"] = zT
    deltas = np.abs(np.linspace(math.log(1e-2) / 1.5, math.log(1e-2) / 0.3, 256, dtype=np.float32))
    dec = np.exp(-tn[:, None] * deltas[None, :]).astype(np.float32)
    dec[Ls, :] = 0.0
    nblk = 256 // cbw
    ngr = cbw // G
    DEC = np.zeros((nblk, 128, 2, ngr, A, G), np.float32)
    for h in range(2):
        for a in range(A):
            tpp = A * (h * 128 + p) + a
            for b in range(nblk):
                DEC[b, :, h, :, a, :] = dec[tpp, b * cbw:(b + 1) * cbw].reshape(128, ngr, G)
    out["DEC"] = DEC.reshape(nblk, 128, 2 * ngr * A * G)
    return {f"hy{tag}_{k}": np.ascontiguousarray(v.astype(np.float32)) for k, v in out.items()}

CONST_SPECS = None


def build(depth=DEPTH, dbg=()):
    kb = KB()
    nc = kb.nc
    cst = host_consts()
    def inp(name, shape, dt=F32):
        return kb.dram(name, shape, dt, kind="ExternalInput")

    x_in = inp("x", [L, D])
    c_in = inp("c", [D])
    ctx_in = inp("ctx", [C, D])
    cctx_in = inp("c_ctx", [D])
    W = {}
    wspec = {
        "mod_w": [DEPTH, D, 3 * D], "mod_b": [DEPTH, 3 * D], "norm_g": [DEPTH, D], "w_in": [DEPTH, D, D_IN],
        "w_out": [DEPTH, D, D], "wa_sink": [DEPTH, 4], "fa_q_norm": [DEPTH, 64], "fa_k_norm": [DEPTH, 64],
        "final_g": [D],
        "rw_conv": [DEPTH, 3, 896], "rw_w0": [DEPTH, 2, 256], "rw_w_up": [DEPTH, 2, 64, 256], "rw_a0": [DEPTH, 2, 256],
        "rw_a_up": [DEPTH, 2, 64, 256], "rw_k_k": [DEPTH, 256], "rw_k_a": [DEPTH, 256], "rw_r_k": [DEPTH, 256],
        "rw_ln_g": [DEPTH, 256], "rw_ln_b": [DEPTH, 256],
        "hy_conv": [DEPTH, 3, 768], "hy_fw1": [DEPTH, 33, 64], "hy_fb1": [DEPTH, 64], "hy_freq": [DEPTH, 64],
        "hy_fw2": [DEPTH, 64, 64], "hy_fb2": [DEPTH, 64], "hy_fw3": [DEPTH, 64, 1024], "hy_bias": [DEPTH, 2, 256],
    }
    for n, s in wspec.items():
        W[n] = inp(n, s)
    CT = {}
    for n, a in cst.items():
        CT[n] = inp("k_" + n, list(a.shape), BF16 if a.dtype == ml_dtypes.bfloat16 else F32)
    out = kb.dram("out", [L, D], F32, kind="ExternalOutput")
    xres = kb.dram("xres", [T, D], F32)
    mixT = kb.dram("mixT", [D, T], BF16)
    RS = {}
    for n in ("RT", "VT", "AL", "W0", "W1", "B0", "B1", "KD0", "KD1", "YF", "YB"):
        RS[n] = kb.dram("rs_" + n, [256, T])
    RS["VTOK"] = kb.dram("rs_VTOK", [T, 256])
    RS["SGT"] = kb.dram("rs_SGT", [256, T], BF16)
    HS = {"SG": kb.dram("hs_SG", [256, T], BF16),
          "UTL": kb.dram("hs_UTL", [3, 8, 128, 32 * 32]), "UTC": kb.dram("hs_UTC", [3, 4, 128, 2 * 64])}
    dbg_t = {}
    for n, s in dbg:
        dbg_t[n] = kb.dram("dbg_" + n, s, F32, kind="ExternalOutput")

    ident_bf = kb.sb("ident_bf", [128, 128], BF16)
    ident_f = kb.sb("ident_f", [128, 128])
    blk64 = kb.sb("blk64", [128, 128])
    ones_f = kb.sb("ones_f", [128, 128])
    for tl, n in ((ident_bf, "ident_bf"), (ident_f, "ident_f"), (blk64, "blk64"), (ones_f, "ones_f")):
        kb.dma("sp", tl[:], CT[n][:, :], reads=[CT[n]], writes=[tl])
    hT = G1 = SH = None
    GT = kb.sb("GT", [128, 2, D])
    PS = [kb.ps(f"ps{i}", [128, 512]) for i in range(8)]

    xres_b = [Buf(f"xres{i}") for i in range(NT)]

    def x_src(l, i):
        if l == 0:
            if i < 2:
                return ctx_in[i * 128:(i + 1) * 128, :], ctx_in
            return x_in[(i - 2) * 128:(i - 1) * 128, :], x_in
        return xres[i * 128:(i + 1) * 128, :], xres_b[i]

    def phase_mod(l):
        with kb.scope():
            cc = kb.sb("cc", [128, 2, 8])
            sc = kb.sb("sc", [128, 2, 8])
            mw = [kb.sb(f"mw{i}", [128, 8, 512]) for i in range(2)]
            mb = kb.sb("mb", [128, 3 * D])
            ng = kb.sb("ng", [128, D])
            modr = kb.sb("modr", [128, 2, 3 * D])
            kb.dma("sp", cc[:, 0, :], c_in.t.rearrange("(j p) -> p j", p=128), reads=[c_in], writes=[cc], slow=True)
            kb.dma("sp", cc[:, 1, :], cctx_in.t.rearrange("(j p) -> p j", p=128), reads=[cctx_in], writes=[cc], slow=True)
            kb.dma("sp", mb[:], W["mod_b"][l, :].partition_broadcast(128), reads=[W["mod_b"]], writes=[mb])
            kb.dma("sp", ng[:], W["norm_g"][l, :].partition_broadcast(128), reads=[W["norm_g"]], writes=[ng])
            kb.op("act", lambda e: e.activation(sc[:], cc[:], AF.Silu), reads=[cc], writes=[sc])
            for n in range(6):
                m = mw[n % 2]
                kb.dma("sp" if n % 2 == 0 else "pool", m[:],
                       W["mod_w"][l, :, n * 512:(n + 1) * 512].rearrange("(j p) n -> p j n", p=128),
                       reads=[W["mod_w"]], writes=[m])
                for i in range(2):
                    p = PS[(2 * n + i) % 8]
                    for j in range(8):
                        kb.op("pe", lambda e, p=p, i=i, j=j, m=m: e.matmul(
                            p[:, :], sc[:, i, j:j + 1].broadcast_to([128, 128]), m[:, j, :],
                            start=(j == 0), stop=(j == 7)), reads=[sc, m], writes=[p])
                    kb.op("dve", lambda e, p=p, i=i, n=n: e.tensor_tensor(
                        modr[:, i, n * 512:(n + 1) * 512], p[:, :], mb[:, n * 512:(n + 1) * 512], ALU.add),
                        reads=[p, mb], writes=[modr])
            for i in range(2):
                kb.op("dve", lambda e, i=i: e.scalar_tensor_tensor(
                    G1[:, i, :], modr[:, i, D:2 * D], 1.0, ng[:], ALU.add, ALU.mult), reads=[modr, ng], writes=[G1])
                kb.op("act", lambda e, i=i: e.copy(SH[:, i, :], modr[:, i, 0:D]), reads=[modr], writes=[SH])
                kb.op("act", lambda e, i=i: e.copy(GT[:, i, :], modr[:, i, 2 * D:3 * D]), reads=[modr], writes=[GT])

    def phase_norm(l):
        with kb.scope():
            xt = [kb.sb(f"xt{i}", [128, D]) for i in range(3)]
            junk = kb.sb("junk", [128, D])
            hf = [kb.sb(f"hf{i}", [128, D]) for i in range(2)]
            hb = [kb.sb(f"hb{i}", [128, D], BF16) for i in range(2)]
            st = [kb.sb(f"st{i}", [128, 4]) for i in range(2)]
            for i in range(NT):
                x, s, h, hbt = xt[i % 3], st[i % 2], hf[i % 2], hb[i % 2]
                sel = 1 if i < 2 else 0
                src, srcb = x_src(l, i)
                kb.dma("sp" if i % 2 == 0 else "pool", x[:], src, reads=[srcb], writes=[x])
                kb.op("act", lambda e, x=x, s=s: e.activation(junk[:], x[:], AF.Square, accum_out=s[:, 0:1]),
                      reads=[x], writes=[junk, s])
                kb.op("dve", lambda e, s=s: e.tensor_scalar(s[:, 1:2], s[:, 0:1], 1.0 / D, EPS, ALU.mult, ALU.add),
                      reads=[s], writes=[s])
                kb.op("act", lambda e, s=s: e.sqrt(s[:, 2:3], s[:, 1:2]), reads=[s], writes=[s])
                kb.op("dve", lambda e, s=s: e.reciprocal(s[:, 3:4], s[:, 2:3]), reads=[s], writes=[s])
                kb.op("dve", lambda e, x=x, s=s, h=h, sel=sel: e.scalar_tensor_tensor(
                    h[:], x[:], s[:, 3:4], G1[:, sel, :], ALU.mult, ALU.mult), reads=[x, s, G1], writes=[h])
                kb.op("pool", lambda e, h=h, hbt=hbt, sel=sel: e.tensor_tensor(hbt[:], h[:], SH[:, sel, :], ALU.add),
                      reads=[h, SH], writes=[hbt])
                p = PS[i % 4]
                pv = p[:, :].bitcast(BF16)
                for j in range(8):
                    kb.op("pe", lambda e, j=j, pv=pv, hbt=hbt: e.transpose(
                        pv[:, j * 128:(j + 1) * 128], hbt[:, j * 128:(j + 1) * 128], ident_bf[:]),
                        reads=[hbt, ident_bf], writes=[p])
                kb.op("act", lambda e, pv=pv, i=i: e.copy(
                    hT[:, :, i * 128:(i + 1) * 128], pv.rearrange("p (j t) -> p j t", j=8)), reads=[p], writes=[hT])

    def load_w(l, dst, col0, ncols, stage, q="sp"):
        kb.dma(q, stage[:, :, 0:ncols], W["w_in"][l, :, col0:col0 + ncols].rearrange("(j p) n -> p j n", p=128),
               reads=[W["w_in"]], writes=[stage])
        kb.op("pool", lambda e: e.tensor_copy(dst[:, :, 0:ncols], stage[:, :, 0:ncols]), reads=[stage], writes=[dst])

    def proj_fm(p, wt, c0, nc_, t0, nt):
        for j in range(8):
            kb.op("pe", lambda e, j=j: e.matmul(p[0:nc_, 0:nt], wt[:, j, c0:c0 + nc_], hT[:, j, t0:t0 + nt],
                                                start=(j == 0), stop=(j == 7)), reads=[wt, hT], writes=[p])

    def proj_tm(p, wt, c0, nc_, i):
        for j in range(8):
            kb.op("pe", lambda e, j=j: e.matmul(p[:, 0:nc_], hT[:, j, i * 128:(i + 1) * 128], wt[:, j, c0:c0 + nc_],
                                                start=(j == 0), stop=(j == 7)), reads=[wt, hT], writes=[p])

    TCH = [(t0, min(512, T - t0)) for t0 in range(0, T, 512)]

    def qk_prep(l, es_tiles, wt, c0, dst, dst_j, gvec, norm, rope):
        raw, sq, rs, rot = es_tiles
        for ci, (t0, nt) in enumerate(TCH):
            p = PS[ci % 2]
            proj_fm(p, wt, c0, 128, t0, nt)
            if norm:
                kb.op("act", lambda e, p=p, nt=nt: e.activation(sq[:, 0:nt], p[:, 0:nt], AF.Square), reads=[p], writes=[sq])
                p2 = PS[2 + ci % 2]
                kb.op("pe", lambda e, p2=p2, nt=nt: e.matmul(p2[:, 0:nt], blk64[:], sq[:, 0:nt], start=True, stop=True),
                      reads=[blk64, sq], writes=[p2])
                kb.op("dve", lambda e, p2=p2, nt=nt: e.tensor_scalar(rs[:, 0:nt], p2[:, 0:nt], 1.0 / 64, EPS, ALU.mult, ALU.add),
                      reads=[p2], writes=[rs])
                kb.op("act", lambda e, nt=nt: e.sqrt(rs[:, 0:nt], rs[:, 0:nt]), reads=[rs], writes=[rs])
                kb.op("dve", lambda e, nt=nt: e.reciprocal(rs[:, 0:nt], rs[:, 0:nt]), reads=[rs], writes=[rs])
                kb.op("dve", lambda e, p=p, nt=nt: e.scalar_tensor_tensor(
                    raw[:, 0:nt], p[:, 0:nt], gvec[:, 0:1], rs[:, 0:nt], ALU.mult, ALU.mult), reads=[p, gvec, rs], writes=[raw])
            else:
                kb.op("act", lambda e, p=p, nt=nt: e.copy(raw[:, 0:nt], p[:, 0:nt]), reads=[p], writes=[raw])
            lat0 = 0
            if t0 < C:
                lat0 = C - t0
                kb.op("pool", lambda e, t0=t0, lat0=lat0: e.tensor_copy(dst[:, dst_j, t0:t0 + lat0], raw[:, 0:lat0]),
                      reads=[raw], writes=[dst])
            if not rope:
                if nt > lat0:
                    kb.op("pool", lambda e, t0=t0, lat0=lat0, nt=nt: e.tensor_copy(
                        dst[:, dst_j, t0 + lat0:t0 + nt], raw[:, lat0:nt]), reads=[raw], writes=[dst])
                continue
            p3 = PS[4 + ci % 2]
            n_l = nt - lat0
            lp = t0 + lat0 - C
            kb.op("pe", lambda e, p3=p3, lat0=lat0, nt=nt: e.matmul(p3[:, lat0:nt], rope_perm[:], raw[:, lat0:nt], start=True, stop=True),
                  reads=[rope_perm, raw], writes=[p3])
            kb.op("dve", lambda e, p3=p3, lat0=lat0, nt=nt, lp=lp, n_l=n_l: e.tensor_tensor(
                rot[:, lat0:nt], p3[:, lat0:nt], rope_sin[:, lp:lp + n_l], ALU.mult), reads=[p3, rope_sin], writes=[rot])
            kb.op("pool", lambda e, lat0=lat0, nt=nt, lp=lp, n_l=n_l: e.tensor_tensor(
                raw[:, lat0:nt], raw[:, lat0:nt], rope_cos[:, lp:lp + n_l], ALU.mult), reads=[raw, rope_cos], writes=[raw])
            kb.op("dve", lambda e, t0=t0, lat0=lat0, nt=nt: e.tensor_tensor(
                dst[:, dst_j, t0 + lat0:t0 + nt], raw[:, lat0:nt], rot[:, lat0:nt], ALU.add), reads=[raw, rot], writes=[dst])

    rope_cos = rope_sin = rope_perm = None

    def phase_attn(l, dense, with_ctx):
        nonlocal rope_cos, rope_sin, rope_perm
        base = FA0 if dense else WA0
        gbase = FAG0 if dense else WAG0
        mrow = 768 if dense else 512
        with kb.scope():
            wt = kb.sb("wt", [128, 8, 768], BF16)
            gq = kb.sb("gq", [128, 1])
            gk = kb.sb("gk", [128, 1])
            sink = kb.sb("sink", [128, 4])
            if dense:
                for hh in range(2):
                    kb.dma("sp", gq[hh * 64:(hh + 1) * 64, :], W["fa_q_norm"][l, :].rearrange("(d o) -> d o", o=1),
                           reads=[W["fa_q_norm"]], writes=[gq], slow=True)
                    kb.dma("sp", gk[hh * 64:(hh + 1) * 64, :], W["fa_k_norm"][l, :].rearrange("(d o) -> d o", o=1),
                           reads=[W["fa_k_norm"]], writes=[gk], slow=True)
            else:
                kb.dma("sp", sink[:], W["wa_sink"][l, :].partition_broadcast(128), reads=[W["wa_sink"]], writes=[sink])
            QT = kb.sb("QT", [128, 2, T], BF16)
            KT = kb.sb("KT", [128, 1, T], BF16)
            VW = 65 if dense else 64
            Vt = kb.sb("Vt", [128, NT, 2, VW], BF16)
            SG = None
            with kb.scope():
                rope_cos = kb.sb("rope_cos", [128, L])
                rope_sin = kb.sb("rope_sin", [128, L])
                rope_perm = kb.sb("rope_perm", [128, 128])
                kb.dma("sp", rope_cos[:], CT["rope_cos"][:, :], reads=[CT["rope_cos"]], writes=[rope_cos])
                kb.dma("pool", rope_sin[:], CT["rope_sin"][:, :], reads=[CT["rope_sin"]], writes=[rope_sin])
                kb.dma("sp", rope_perm[:], CT["rope_perm"][:, :], reads=[CT["rope_perm"]], writes=[rope_perm])
                stage = kb.sb("wstage", [128, 8, 256])
                w4 = W["w_in"][l, :, base:base + 256].rearrange("(j p) (h d) -> p j h d", p=128, d=64)
                st4 = stage[:, :, 0:256].rearrange("p j (h d) -> p j h d", d=64)
                for hi, h in enumerate((0, 2, 1, 3)):
                    kb.dma("sp", st4[:, :, hi, :], w4[:, :, h, :], reads=[W["w_in"]], writes=[stage])
                kb.op("pool", lambda e: e.tensor_copy(wt[:, :, 0:256], stage[:]), reads=[stage], writes=[wt])
                kb.dma("pool", stage[:], W["w_in"][l, :, base + 256:base + 512].rearrange("(j p) n -> p j n", p=128),
                       reads=[W["w_in"]], writes=[stage])
                kb.op("pool", lambda e: e.tensor_copy(wt[:, :, 256:512], stage[:]), reads=[stage], writes=[wt])
                kb.dma("sp", stage[:], W["w_in"][l, :, gbase:gbase + 256].rearrange("(j p) n -> p j n", p=128),
                       reads=[W["w_in"]], writes=[stage])
                kb.op("pool", lambda e: e.tensor_copy(wt[:, :, 512:768], stage[:]), reads=[stage], writes=[wt])
                tl = (kb.sb("qraw", [128, 512]), kb.sb("qsq", [128, 512]), kb.sb("qrs", [128, 512]), kb.sb("qrot", [128, 512]))
                qk_prep(l, tl, wt, 0, QT, 0, gq, dense, True)
                qk_prep(l, tl, wt, 128, QT, 1, gq, dense, True)
                qk_prep(l, tl, wt, 256, KT, 0, gk, dense, True)
            if dense:
                kb.op("pool", lambda e: e.memset(Vt[:, :, :, 64:65], 1.0), writes=[Vt])
            for i in range(NT):
                p = PS[i % 2]
                proj_tm(p, wt, 384, 128, i)
                kb.op("act", lambda e, p=p, i=i: e.copy(Vt[:, i, :, 0:64], p[:, 0:128].rearrange("p (k d) -> p k d", d=64)),
                      reads=[p], writes=[Vt])
            with kb.scope():
                if dense:
                    attn_dense(l, wt, QT, KT, Vt, mrow, with_ctx)
                else:
                    attn_window(l, wt, QT, KT, Vt, sink, mrow, with_ctx)

    def attn_dense(l, wt, QT, KT, Vt, mrow, with_ctx):
        pt = [kb.sb(f"pt{i}", [128, 512], BF16) for i in range(3)]
        osb = [kb.sb(f"osb{i}", [128, 512]) for i in range(2)]
        rc = [kb.sb(f"rc{i}", [128, 512]) for i in range(2)]
        ob = [kb.sb(f"ob{i}", [128, 512], BF16) for i in range(2)]
        it = 0
        sgt = [kb.sb(f"sgt{i}", [64, 512], BF16) for i in range(2)]
        chunks = []
        if with_ctx:
            chunks.append((0, C, 0, 2))
        for t0 in range(C, T, 512):
            chunks.append((t0, 512, 0, NT))
        for h in range(4):
            kv, pr = h // 2, h % 2
            ks = slice(64 * kv, 64 * kv + 64)
            for (t0, nt, kb0, kb1) in chunks:
                po = PS[4 + it % 2]
                sg = sgt[it % 2]
                pg = PS[3]
                for j in range(8):
                    kb.op("pe", lambda e, j=j: e.matmul(
                        pg[0:64, 0:nt], wt[:, j, 512 + 64 * h:576 + 64 * h], hT[:, j, t0:t0 + nt], start=(j == 0), stop=(j == 7)),
                        reads=[wt, hT], writes=[pg])
                kb.op("act", lambda e: e.activation(sg[0:64, 0:nt], pg[0:64, 0:nt], AF.Silu), reads=[pg], writes=[sg])
                for kbi in range(kb0, kb1):
                    psS = PS[kbi % 3]
                    ptt = pt[kbi % 3]
                    kb.op("pe", lambda e, psS=psS, kbi=kbi: e.matmul(
                        psS[:, 0:nt], KT[ks, 0, kbi * 128:(kbi + 1) * 128], QT[ks, pr, t0:t0 + nt], start=True, stop=True),
                        reads=[KT, QT], writes=[psS])
                    kb.op("act", lambda e, psS=psS, ptt=ptt: e.activation(ptt[:, 0:nt], psS[:, 0:nt], AF.Exp, scale=0.125),
                          reads=[psS], writes=[ptt])
                    kb.op("pe", lambda e, po=po, ptt=ptt, kbi=kbi: e.matmul(
                        po[0:65, 0:nt], Vt[:, kbi, kv, 0:65], ptt[:, 0:nt], start=(kbi == kb0), stop=(kbi == kb1 - 1)),
                        reads=[Vt, ptt], writes=[po])
                o_s, r_c, o_b = osb[it % 2], rc[it % 2], ob[it % 2]
                kb.op("dve", lambda e: e.reciprocal(r_c[64:65, 0:nt], po[64:65, 0:nt]), reads=[po], writes=[r_c])
                kb.op("act", lambda e: e.copy(o_s[0:64, 0:nt], po[0:64, 0:nt]), reads=[po], writes=[o_s])
                pb = PS[6 + it % 2]
                kb.op("pe", lambda e: e.matmul(pb[0:64, 0:nt], ones_f[64:65, 0:64], r_c[64:65, 0:nt], start=True, stop=True),
                      reads=[ones_f, r_c], writes=[pb])
                kb.op("dve", lambda e: e.tensor_tensor(o_s[0:64, 0:nt], o_s[0:64, 0:nt], pb[0:64, 0:nt], ALU.mult),
                      reads=[o_s, pb], writes=[o_s])
                kb.op("pool", lambda e: e.tensor_tensor(o_b[0:64, 0:nt], o_s[0:64, 0:nt], sg[0:64, 0:nt], ALU.mult),
                      reads=[o_s, sg], writes=[o_b])
                kb.dma("pool", mixT[mrow + 64 * h:mrow + 64 * h + 64, t0:t0 + nt], o_b[0:64, 0:nt], reads=[o_b], writes=[Buf()])
                it += 1

    def attn_window(l, wt, QT, KT, Vt, sink, mrow, with_ctx):
        wmask = kb.sb("wmask", [128, 384])
        kb.dma("sp", wmask[:], CT["wmask"][:, :], reads=[CT["wmask"]], writes=[wmask])
        nsink = kb.sb("nsink", [128, 4])
        kb.op("dve", lambda e: e.tensor_scalar(nsink[:], sink[:], -1.0, None, ALU.mult), reads=[sink], writes=[nsink])
        S = [kb.sb(f"wS{i}", [128, 640]) for i in range(2)]
        P = [kb.sb(f"wP{i}", [128, 640]) for i in range(2)]
        Pn = [kb.sb(f"wPn{i}", [128, 640], BF16) for i in range(2)]
        PT = [kb.sb(f"wPT{i}", [128, 640], BF16) for i in range(2)]
        st = [kb.sb(f"wst{i}", [128, 8]) for i in range(2)]
        sgt = [kb.sb(f"wsg{i}", [64, 128], BF16) for i in range(2)]
        ob = [kb.sb(f"wob{i}", [64, 128], BF16) for i in range(2)]
        it = 0
        for i in range(0 if with_ctx else 2, NT):
            if i < 2:
                loc = []
            else:
                loc = list(range(max(2, i - 1), min(NT - 1, i + 1) + 1))
            nl = 128 * len(loc)
            m0 = 128 if (i >= 2 and i - 1 < 2) else 0
            nk = nl + C
            ktiles = loc + [0, 1]
            for h in range(4):
                kv, pr = h // 2, h % 2
                ks = slice(64 * kv, 64 * kv + 64)
                s_, p_, pn_, pt_, st_, sg, o_b = S[it % 2], P[it % 2], Pn[it % 2], PT[it % 2], st[it % 2], sgt[it % 2], ob[it % 2]
                psA, psB, psT, psO, psG = PS[it % 2], PS[2 + it % 2], PS[4 + it % 2], PS[6], PS[7]
                q_ap = QT[ks, pr, i * 128:(i + 1) * 128]
                if nl:
                    k0 = loc[0] * 128
                    kb.op("pe", lambda e: e.matmul(psA[:, 0:nl], q_ap, KT[ks, 0, k0:k0 + nl], start=True, stop=True),
                          reads=[QT, KT], writes=[psA])
                    kb.op("dve", lambda e: e.tensor_tensor(s_[:, 0:nl], psA[:, 0:nl], wmask[:, m0:m0 + nl], ALU.add),
                          reads=[psA, wmask], writes=[s_])
                kb.op("pe", lambda e: e.matmul(psB[:, 0:C], q_ap, KT[ks, 0, 0:C], start=True, stop=True),
                      reads=[QT, KT], writes=[psB])
                kb.op("act", lambda e: e.copy(s_[:, nl:nk], psB[:, 0:C]), reads=[psB], writes=[s_])
                kb.op("dve", lambda e: e.reduce_max(st_[:, 0:1], s_[:, 0:nk], AX.X), reads=[s_], writes=[st_])
                kb.op("dve", lambda e: e.tensor_scalar(st_[:, 1:2], st_[:, 0:1], -0.125, nsink[:, h:h + 1], ALU.mult, ALU.min),
                      reads=[st_, nsink], writes=[st_])
                kb.op("act", lambda e: e.activation(p_[:, 0:nk], s_[:, 0:nk], AF.Exp, bias=st_[:, 1:2], scale=0.125,
                                                    accum_out=st_[:, 2:3]), reads=[s_, st_], writes=[p_, st_])
                kb.op("act", lambda e: e.activation(st_[:, 3:4], sink[:, h:h + 1], AF.Exp, bias=st_[:, 1:2], scale=1.0),
                      reads=[sink, st_], writes=[st_])
                kb.op("dve", lambda e: e.tensor_tensor(st_[:, 4:5], st_[:, 2:3], st_[:, 3:4], ALU.add), reads=[st_], writes=[st_])
                kb.op("dve", lambda e: e.reciprocal(st_[:, 5:6], st_[:, 4:5]), reads=[st_], writes=[st_])
                kb.op("dve", lambda e: e.tensor_scalar(pn_[:, 0:nk], p_[:, 0:nk], st_[:, 5:6], None, ALU.mult),
                      reads=[p_, st_], writes=[pn_])
                pv = psT[:, :].bitcast(BF16)
                nb = nk // 128
                for b in range(nb):
                    kb.op("pe", lambda e, b=b: e.transpose(pv[:, b * 128:(b + 1) * 128], pn_[:, b * 128:(b + 1) * 128], ident_bf[:]),
                          reads=[pn_, ident_bf], writes=[psT])
                kb.op("act", lambda e: e.copy(pt_[:, 0:nk], pv[:, 0:nk]), reads=[psT], writes=[pt_])
                for b in range(nb):
                    kb.op("pe", lambda e, b=b: e.matmul(psO[0:64, 0:128], Vt[:, ktiles[b], kv, 0:64], pt_[:, b * 128:(b + 1) * 128],
                                                        start=(b == 0), stop=(b == nb - 1)), reads=[Vt, pt_], writes=[psO])
                for j in range(8):
                    kb.op("pe", lambda e, j=j: e.matmul(
                        psG[0:64, 0:128], wt[:, j, 512 + 64 * h:576 + 64 * h], hT[:, j, i * 128:(i + 1) * 128],
                        start=(j == 0), stop=(j == 7)), reads=[wt, hT], writes=[psG])
                kb.op("act", lambda e: e.activation(sg[:, :], psG[0:64, 0:128], AF.Silu), reads=[psG], writes=[sg])
                kb.op("dve", lambda e: e.tensor_tensor(o_b[:, :], psO[0:64, 0:128], sg[:, :], ALU.mult), reads=[psO, sg], writes=[o_b])
                kb.dma("sp", mixT[mrow + 64 * h:mrow + 64 * h + 64, i * 128:(i + 1) * 128], o_b[:, :], reads=[o_b], writes=[Buf()])
                it += 1

    def phase_out(l, last):
        with kb.scope():
            wo = kb.sb("wo", [128, 8, D], BF16)
            stage = kb.sb("wostage", [128, 8, 256])
            for q in range(4):
                kb.dma("sp", stage[:], W["w_out"][l, :, q * 256:(q + 1) * 256].rearrange("(j p) n -> p j n", p=128),
                       reads=[W["w_out"]], writes=[stage])
                kb.op("pool", lambda e, q=q: e.tensor_copy(wo[:, :, q * 256:(q + 1) * 256], stage[:]), reads=[stage], writes=[wo])
            fg = kb.sb("fg", [128, D])
            if last:
                kb.dma("sp", fg[:], W["final_g"][:].partition_broadcast(128), reads=[W["final_g"]], writes=[fg])
            mt = [kb.sb(f"mt{i}", [128, 8, 128], BF16) for i in range(2)]
            xt = [kb.sb(f"oxt{i}", [128, D]) for i in range(2)]
            xn = [kb.sb(f"oxn{i}", [128, D]) for i in range(2)]
            tmp = [kb.sb(f"otmp{i}", [128, 512]) for i in range(2)]
            st = [kb.sb(f"ost{i}", [128, 4]) for i in range(2)]
            junk = kb.sb("ojunk", [128, D])
            mixv = mixT.t.rearrange("(j p) t -> p j t", p=128)
            for it, i in enumerate(range(2 if last else 0, NT)):
                m, x, xo, s = mt[it % 2], xt[it % 2], xn[it % 2], st[it % 2]
                sel = 1 if i < 2 else 0
                kb.dma("sp", m[:], mixv[:, :, i * 128:(i + 1) * 128], reads=[mixT], writes=[m])
                src, srcb = x_src(l, i)
                kb.dma("pool", x[:], src, reads=[srcb], writes=[x])
                for hf in range(2):
                    p = PS[(2 * it + hf) % 8]
                    tp = tmp[hf]
                    for j in range(8):
                        kb.op("pe", lambda e, j=j, p=p, m=m, hf=hf: e.matmul(p[:, :], m[:, j, :], wo[:, j, hf * 512:(hf + 1) * 512],
                                                                     start=(j == 0), stop=(j == 7)), reads=[m, wo], writes=[p])
                    kb.op("dve", lambda e, p=p, tp=tp, hf=hf, sel=sel: e.tensor_tensor(
                        tp[:], p[:, :], GT[:, sel, hf * 512:(hf + 1) * 512], ALU.mult), reads=[p, GT], writes=[tp])
                    kb.op("pool", lambda e, tp=tp, hf=hf, x=x, xo=xo: e.tensor_tensor(
                        xo[:, hf * 512:(hf + 1) * 512], x[:, hf * 512:(hf + 1) * 512], tp[:], ALU.add), reads=[x, tp], writes=[xo])
                if not last:
                    kb.dma("sp", xres[i * 128:(i + 1) * 128, :], xo[:], reads=[xo], writes=[xres_b[i]])
                else:
                    kb.op("act", lambda e, xo=xo, s=s: e.activation(junk[:], xo[:], AF.Square, accum_out=s[:, 0:1]),
                          reads=[xo], writes=[junk, s])
                    kb.op("dve", lambda e, s=s: e.tensor_scalar(s[:, 1:2], s[:, 0:1], 1.0 / D, EPS, ALU.mult, ALU.add),
                          reads=[s], writes=[s])
                    kb.op("act", lambda e, s=s: e.sqrt(s[:, 2:3], s[:, 1:2]), reads=[s], writes=[s])
                    kb.op("dve", lambda e, s=s: e.reciprocal(s[:, 3:4], s[:, 2:3]), reads=[s], writes=[s])
                    kb.op("dve", lambda e, xo=xo, s=s, x=x: e.scalar_tensor_tensor(
                        x[:], xo[:], s[:, 3:4], fg[:], ALU.mult, ALU.mult), reads=[xo, s, fg], writes=[x])
                    kb.dma("sp", out[(i - 2) * 128:(i - 1) * 128, :], x[:], reads=[x], writes=[Buf()])


    def conv_tile(l, wt, cw, ncw, jt, Zraw, Zout):
        for ci, (t0, nt) in enumerate(TCH):
            p = PS[ci % 4]
            proj_fm(p, wt, 0, 128, t0, nt)
            kb.op("act", lambda e, p=p, t0=t0, nt=nt: e.copy(Zraw[:, 1 + t0:1 + t0 + nt], p[:, 0:nt]), reads=[p], writes=[Zraw])
        kb.op("dve", lambda e: e.tensor_scalar(Zout[:, :], Zraw[:, 1:T + 1], cw[:, jt, 1:2], None, ALU.mult), reads=[Zraw, cw], writes=[Zout])
        kb.op("dve", lambda e: e.scalar_tensor_tensor(Zout[:, :], Zraw[:, 0:T], cw[:, jt, 0:1], Zout[:, :], ALU.mult, ALU.add),
              reads=[Zraw, cw, Zout], writes=[Zout])
        kb.op("dve", lambda e: e.scalar_tensor_tensor(Zout[:, :], Zraw[:, 2:T + 2], cw[:, jt, 2:3], Zout[:, :], ALU.mult, ALU.add),
              reads=[Zraw, cw, Zout], writes=[Zout])
        kb.op("dve", lambda e: e.scalar_tensor_tensor(Zout[:, C - 1:C], Zraw[:, C + 1:C + 2], ncw[:, jt, 2:3], Zout[:, C - 1:C], ALU.mult, ALU.add),
              reads=[Zraw, ncw, Zout], writes=[Zout])
        kb.op("dve", lambda e: e.scalar_tensor_tensor(Zout[:, C:C + 1], Zraw[:, C:C + 1], ncw[:, jt, 0:1], Zout[:, C:C + 1], ALU.mult, ALU.add),
              reads=[Zraw, ncw, Zout], writes=[Zout])

    def colvec(name, src_ap, srcb, shape, rearr, **kw):
        t = kb.sb(name, shape)
        kb.dma("sp", t[:], src_ap.rearrange(rearr, **kw), reads=[srcb], writes=[t], slow=True)
        return t

    def phase_rwkv_prep(l):
        with kb.scope():
            stage = kb.sb("rstage", [128, 8, 128])
            wts = [kb.sb(f"rwt{i}", [128, 8, 128], BF16) for i in range(2)]
            cw = kb.sb("rcw", [128, 7, 3])
            for k in range(3):
                kb.dma("sp", cw[:, :, k], W["rw_conv"][l, k, :].rearrange("(j p) -> p j", p=128), reads=[W["rw_conv"]], writes=[cw], slow=True)
            ncw = kb.sb("rncw", [128, 7, 3])
            kb.op("dve", lambda e: e.tensor_scalar(ncw[:], cw[:], -1.0, None, ALU.mult), reads=[cw], writes=[ncw])
            kk_ = colvec("rkk", W["rw_k_k"][l, :], W["rw_k_k"], [128, 2], "(j p) -> p j", p=128)
            ka_ = colvec("rka", W["rw_k_a"][l, :], W["rw_k_a"], [128, 2], "(j p) -> p j", p=128)
            omka = kb.sb("romka", [128, 2])
            kb.op("dve", lambda e: e.tensor_scalar(omka[:], ka_[:], -1.0, 1.0, ALU.mult, ALU.add), reads=[ka_], writes=[omka])
            w0_ = kb.sb("rw0", [128, 2, 2])
            a0_ = kb.sb("ra0", [128, 2, 2])
            for d in range(2):
                kb.dma("sp", w0_[:, d, :], W["rw_w0"][l, d, :].rearrange("(j p) -> p j", p=128), reads=[W["rw_w0"]], writes=[w0_], slow=True)
                kb.dma("sp", a0_[:, d, :], W["rw_a0"][l, d, :].rearrange("(j p) -> p j", p=128), reads=[W["rw_a0"]], writes=[a0_], slow=True)
            wup = kb.sb("rwup", [128, 2, 256])
            kb.dma("sp", wup[0:64, :, :], W["rw_w_up"][l, :, :, :].rearrange("d k n -> k d n"), reads=[W["rw_w_up"]], writes=[wup])
            kb.dma("sp", wup[64:128, :, :], W["rw_a_up"][l, :, :, :].rearrange("d k n -> k d n"), reads=[W["rw_a_up"]], writes=[wup])
            Zraw = kb.sb("rZraw", [128, T + 2])
            Zout = kb.sb("rZout", [128, T])
            Z6 = kb.sb("rZ6", [128, T])
            kb.op("pool", lambda e: e.memset(Zraw[:, 0:1], 0.0), writes=[Zraw])
            kb.op("pool", lambda e: e.memset(Zraw[:, T + 1:T + 2], 0.0), writes=[Zraw])
            tA = [kb.sb(f"rtA{i}", [128, 512]) for i in range(2)]
            tB = [kb.sb(f"rtB{i}", [128, 512]) for i in range(2)]
            tC = [kb.sb(f"rtC{i}", [128, 512]) for i in range(2)]
            tD = [kb.sb(f"rtD{i}", [128, 512]) for i in range(2)]
            tE = [kb.sb(f"rtE{i}", [128, 512]) for i in range(2)]
            tG = [kb.sb(f"rtG{i}", [128, 512], BF16) for i in range(2)]
            vt_ = [kb.sb(f"rvt{i}", [128, 128]) for i in range(2)]
            order = [6, 0, 1, 4, 5, 2, 3, 7, 8]
            for oi, jt in enumerate(order):
                wt = wts[oi % 2]
                c0 = RW0 + jt * 128 if jt < 7 else RWG0 + (jt - 7) * 128
                load_w(l, wt, c0, 128, stage)
                if jt >= 7:
                    for ci, (t0, nt) in enumerate(TCH):
                        p = PS[ci % 4]
                        proj_fm(p, wt, 0, 128, t0, nt)
                        g = tG[ci % 2]
                        kb.op("act", lambda e, p=p, g=g, nt=nt: e.activation(g[:, 0:nt], p[:, 0:nt], AF.Silu), reads=[p], writes=[g])
                        kb.dma("sp", RS["SGT"][(jt - 7) * 128:(jt - 6) * 128, t0:t0 + nt], g[:, 0:nt], reads=[g], writes=[Buf()])
                    continue
                conv_tile(l, wt, cw, ncw, jt, Zraw, Z6 if jt == 6 else Zout)
                if jt == 6:
                    kb.op("act", lambda e: e.activation(Z6[0:64, :], Z6[0:64, :], AF.Tanh), reads=[Z6], writes=[Z6])
                elif jt in (0, 1):
                    kb.dma("sp", RS["RT"][jt * 128:(jt + 1) * 128, :], Zout[:, :], reads=[Zout], writes=[Buf()])
                elif jt in (4, 5):
                    kb.dma("sp", RS["VT"][(jt - 4) * 128:(jt - 3) * 128, :], Zout[:, :], reads=[Zout], writes=[Buf()])
                    for i in range(NT):
                        p = PS[4 + i % 2]
                        kb.op("pe", lambda e, p=p, i=i: e.transpose(p[:, 0:128], Zout[:, i * 128:(i + 1) * 128], ident_f[:]),
                              reads=[Zout, ident_f], writes=[p])
                        v = vt_[i % 2]
                        kb.op("act", lambda e, p=p, v=v: e.copy(v[:, :], p[:, 0:128]), reads=[p], writes=[v])
                        kb.dma("pool", RS["VTOK"][i * 128:(i + 1) * 128, (jt - 4) * 128:(jt - 3) * 128], v[:, :], reads=[v], writes=[Buf()])
                else:
                    pt = jt - 2
                    rows = slice(pt * 128, (pt + 1) * 128)
                    for ci, (t0, nt) in enumerate(TCH):
                        a_, b_, c_, d_, e_ = tA[ci % 2], tB[ci % 2], tC[ci % 2], tD[ci % 2], tE[ci % 2]
                        zc = Zout[:, t0:t0 + nt]
                        kb.op("dve", lambda e: e.tensor_scalar(a_[:, 0:nt], zc, kk_[:, pt:pt + 1], None, ALU.mult), reads=[Zout, kk_], writes=[a_])
                        kb.op("act", lambda e: e.activation(b_[:, 0:nt], a_[:, 0:nt], AF.Square), reads=[a_], writes=[b_])
                        p = PS[ci % 2]
                        kb.op("pe", lambda e: e.matmul(p[:, 0:nt], blk64[:], b_[:, 0:nt], start=True, stop=True), reads=[blk64, b_], writes=[p])
                        kb.op("act", lambda e: e.sqrt(b_[:, 0:nt], p[:, 0:nt]), reads=[p], writes=[b_])
                        kb.op("dve", lambda e: e.tensor_scalar(b_[:, 0:nt], b_[:, 0:nt], 1e-12, None, ALU.max), reads=[b_], writes=[b_])
                        kb.op("dve", lambda e: e.reciprocal(b_[:, 0:nt], b_[:, 0:nt]), reads=[b_], writes=[b_])
                        kb.op("dve", lambda e: e.scalar_tensor_tensor(a_[:, 0:nt], a_[:, 0:nt], -1.0, b_[:, 0:nt], ALU.mult, ALU.mult),
                              reads=[a_, b_], writes=[a_])
                        kb.dma("sp", RS["AL"][rows, t0:t0 + nt], a_[:, 0:nt], reads=[a_], writes=[Buf()])
                        for d in range(2):
                            pa = PS[2 + d]
                            kb.op("pe", lambda e: e.matmul(pa[:, 0:nt], wup[64:128, d, pt * 128:(pt + 1) * 128], Z6[64:128, t0:t0 + nt],
                                                           start=True, stop=True), reads=[wup, Z6], writes=[pa])
                            kb.op("act", lambda e: e.activation(c_[:, 0:nt], pa[:, 0:nt], AF.Sigmoid, bias=a0_[:, d, pt:pt + 1]),
                                  reads=[pa, a0_], writes=[c_])
                            kb.op("dve", lambda e: e.scalar_tensor_tensor(d_[:, 0:nt], c_[:, 0:nt], -1.0, a_[:, 0:nt], ALU.mult, ALU.mult),
                                  reads=[c_, a_], writes=[d_])
                            kb.dma("sp", RS[f"B{d}"][rows, t0:t0 + nt], d_[:, 0:nt], reads=[d_], writes=[Buf()])
                            kb.op("dve", lambda e: e.tensor_scalar(c_[:, 0:nt], c_[:, 0:nt], ka_[:, pt:pt + 1], omka[:, pt:pt + 1], ALU.mult, ALU.add),
                                  reads=[c_, ka_, omka], writes=[c_])
                            kb.op("dve", lambda e: e.tensor_tensor(e_[:, 0:nt], c_[:, 0:nt], zc, ALU.mult), reads=[c_, Zout], writes=[e_])
                            kb.dma("pool", RS[f"KD{d}"][rows, t0:t0 + nt], e_[:, 0:nt], reads=[e_], writes=[Buf()])
                            pw = PS[4 + d]
                            kb.op("pe", lambda e: e.matmul(pw[:, 0:nt], wup[0:64, d, pt * 128:(pt + 1) * 128], Z6[0:64, t0:t0 + nt],
                                                           start=True, stop=True), reads=[wup, Z6], writes=[pw])
                            kb.op("act", lambda e: e.activation(c_[:, 0:nt], pw[:, 0:nt], AF.Sigmoid, bias=w0_[:, d, pt:pt + 1]),
                                  reads=[pw, w0_], writes=[c_])
                            kb.op("dve", lambda e: e.tensor_scalar(d_[:, 0:nt], c_[:, 0:nt], -math.exp(-0.5), None, ALU.mult),
                                  reads=[c_], writes=[d_])
                            kb.dma("pool", RS[f"W{d}"][rows, t0:t0 + nt], d_[:, 0:nt], reads=[d_], writes=[Buf()])

    def phase_rwkv_scan(l):
        with kb.scope():
            ST = [kb.sb(f"ST{d}", [128, 2, 64]) for d in range(2)]
            for d in range(2):
                kb.op("pool", lambda e, d=d: e.memset(ST[d][:], 0.0), writes=[ST[d]])
            names = ("AL", "W", "B", "KD", "RT")
            ch = [[{n: kb.sb(f"c{n}{d}{i}", [128, 2, 128]) for n in names} for i in range(2)] for d in range(2)]
            vch = [[kb.sb(f"cV{d}{i}", [128, 256]) for i in range(2)] for d in range(2)]
            t1 = [kb.sb(f"st1{d}", [128, 2, 64]) for d in range(2)]
            t2 = [kb.sb(f"st2{d}", [128, 2, 64]) for d in range(2)]
            ysb = [kb.sb(f"ysb{d}", [64, 512]) for d in range(2)]
            psSA, psV, psY = [PS[0], PS[1]], [PS[2], PS[3]], [PS[4], PS[5]]
            border = [1, 0] + list(range(NT - 1, 1, -1))
            for ci in range(NT):
                cidx = [ci, border[ci]]
                cur = []
                for d in range(2):
                    c0 = cidx[d] * 128
                    tl_ = ch[d][ci % 2]
                    for n in names:
                        src = RS[n if n in ("AL", "RT") else f"{n}{d}"]
                        kb.dma("sp" if d == 0 else "pool", tl_[n][:],
                               src.t.rearrange("(pr q) t -> q pr t", q=128)[:, :, c0:c0 + 128], reads=[src], writes=[tl_[n]])
                    vv = vch[d][ci % 2]
                    kb.dma("sp" if d == 0 else "pool", vv[:], RS["VTOK"][c0:c0 + 128, :], reads=[RS["VTOK"]], writes=[vv])
                    cur.append((tl_, vv))
                for tl in range(128):
                    for d in range(2):
                        col = tl if d == 0 else 127 - tl
                        tl_, vv = cur[d]
                        S_, sa, pv, py = ST[d], psSA[d], psV[d], psY[d]
                        for pr in range(2):
                            for hp in range(2):
                                rows = slice(64 * hp, 64 * hp + 64)
                                kb.op("pe", lambda e, pr=pr, rows=rows: e.matmul(
                                    sa[rows, pr * 64:(pr + 1) * 64], tl_["AL"][rows, pr, col:col + 1].broadcast_to([64, 64]),
                                    S_[rows, pr, :], start=True, stop=True), reads=[tl_["AL"], S_], writes=[sa])
                        for pr in range(2):
                            for hp in range(2):
                                rows = slice(64 * hp, 64 * hp + 64)
                                h = 2 * pr + hp
                                kb.op("pe", lambda e, pr=pr, rows=rows, h=h: e.matmul(
                                    pv[rows, pr * 64:(pr + 1) * 64], ident_f[:, col:col + 1].broadcast_to([128, 64]),
                                    vv[:, h * 64:(h + 1) * 64], start=True, stop=True), reads=[ident_f, vv], writes=[pv])
                        for pr in range(2):
                            kb.op("dve", lambda e, pr=pr: e.tensor_scalar(
                                t1[d][:, pr, :], sa[:, pr * 64:(pr + 1) * 64], tl_["B"][:, pr, col:col + 1], None, ALU.mult),
                                reads=[sa, tl_["B"]], writes=[t1[d]])
                            kb.op("dve", lambda e, pr=pr: e.scalar_tensor_tensor(
                                t2[d][:, pr, :], pv[:, pr * 64:(pr + 1) * 64], tl_["KD"][:, pr, col:col + 1], t1[d][:, pr, :], ALU.mult, ALU.add),
                                reads=[pv, tl_["KD"], t1[d]], writes=[t2[d]])
                            kb.op("dve", lambda e, pr=pr: e.scalar_tensor_tensor(
                                S_[:, pr, :], S_[:, pr, :], tl_["W"][:, pr, col:col + 1], t2[d][:, pr, :], ALU.mult, ALU.add),
                                reads=[S_, tl_["W"], t2[d]], writes=[S_])
                        for pr in range(2):
                            for hp in range(2):
                                rows = slice(64 * hp, 64 * hp + 64)
                                h = 2 * pr + hp
                                kb.op("pe", lambda e, pr=pr, rows=rows, h=h: e.matmul(
                                    py[0:64, h * 128 + col:h * 128 + col + 1], S_[rows, pr, :], tl_["RT"][rows, pr, col:col + 1],
                                    start=True, stop=True), reads=[S_, tl_["RT"]], writes=[py])
                for d in range(2):
                    c0 = cidx[d] * 128
                    kb.op("act", lambda e, d=d: e.copy(ysb[d][:, :], psY[d][0:64, :]), reads=[psY[d]], writes=[ysb[d]])
                    dst = RS["YF" if d == 0 else "YB"]
                    kb.dma("sp", dst.t.rearrange("(h v) t -> v h t", v=64)[:, :, c0:c0 + 128],
                           ysb[d][:, :].rearrange("v (h t) -> v h t", h=4), reads=[ysb[d]], writes=[Buf()])


    def phase_rwkv_chunked(l):
        CH = 64
        NCH = T // CH
        with kb.scope():
            def ldc(nm, shape):
                t = kb.sb("k" + nm, shape)
                kb.dma("sp", t[:], CT[nm].t, reads=[CT[nm]], writes=[t])
                return t
            Ms = ldc("rw_ms", [128, 2, 64]); MTs = ldc("rw_mts", [128, 2, 64]); MTi = ldc("rw_mti", [128, 2, 64])
            id2 = ldc("rw_id2", [128, 64])
            ones = kb.sb("rones", [128, 64])
            kb.op("pool", lambda e: e.memset(ones[:], 1.0), writes=[ones])
            ST = kb.sb("cST", [128, 4, 64])
            kb.op("pool", lambda e: e.memset(ST[:], 0.0), writes=[ST])
            names = ("AL", "W", "B", "KD", "RT")
            def t4(nm, n=2, w=64):
                return [kb.sb(f"{nm}{i}", [128, 4, w]) for i in range(n)]
            IN = {n: t4("ci" + n) for n in names}
            VTK = t4("cVTK")
            CS = t4("cCS", 1)[0]; TOT = kb.sb("cTOT", [128, 4]); TMP = t4("cTMP", 1)[0]
            Epos = t4("cEp", 1)[0]; Eneg = t4("cEn", 1)[0]; Eprev = t4("cEv", 1)[0]; Etot = t4("cEt", 1)[0]; Wtot = kb.sb("cWt", [128, 4])
            Ab = t4("cAb", 1)[0]; Bb = t4("cBb", 1)[0]; Kb = t4("cKb", 1)[0]; Rb = t4("cRb", 1)[0]; Bt = t4("cBt", 1)[0]; Kt = t4("cKt", 1)[0]
            Q = t4("cQ"); P = t4("cP"); ArbT = t4("cArbT", 1)[0]; AkvT = t4("cAkvT", 1)[0]; ArkT = t4("cArkT", 1)[0]
            X = t4("cX", 2, 128); Btok = t4("cBtok", 1)[0]; Ktok = t4("cKtok", 1)[0]
            RAT = t4("cRAT", 1)[0]; McT = t4("cMcT", 1)[0]; NcS = t4("cNcS", 1)[0]; DG = t4("cDG", 1)[0]
            ysb = [kb.sb(f"cysb{d}", [64, 256]) for d in range(2)]
            border = [3, 2, 1, 0] + list(range(NCH - 1, 3, -1))
            DP = [(d, pr) for d in range(2) for pr in range(2)]
            HP = [slice(0, 64), slice(64, 128)]

            def mm_all(ps, col_fn, lhs_fn, rhs_fn, reads, start=True, stop=True, w=None):
                for dp in range(4):
                    for hp in range(2):
                        r = HP[hp]
                        c0, c1 = col_fn(dp)
                        kb.op("pe", lambda e, dp=dp, r=r, c0=c0, c1=c1: e.matmul(ps[r, c0:c1], lhs_fn(dp, r), rhs_fn(dp, r), start=start, stop=stop),
                              reads=reads, writes=[ps])

            for ci in range(NCH):
                cidx = [ci, border[ci]]
                i2 = ci % 2
                for d in range(2):
                    c0 = cidx[d] * CH
                    for n in names:
                        src = RS[n if n in ("AL", "RT") else f"{n}{d}"]
                        kb.dma("sp" if d == 0 else "pool", IN[n][i2][:, 2 * d:2 * d + 2, :],
                               src.t.rearrange("(pr q) t -> q pr t", q=128)[:, :, c0:c0 + CH], reads=[src], writes=[IN[n][i2]])
                    for hp in range(2):
                        kb.dma("sp" if d == 0 else "pool", VTK[i2][HP[hp], 2 * d:2 * d + 2, :],
                               RS["VTOK"][c0:c0 + CH, :].rearrange("t (pr hp v) -> t pr hp v", pr=2, hp=2)[:, :, hp, :],
                               reads=[RS["VTOK"]], writes=[VTK[i2]])
                al, lw, be, kd, rt, vt = IN["AL"][i2], IN["W"][i2], IN["B"][i2], IN["KD"][i2], IN["RT"][i2], VTK[i2]
                if RW_STAGE <= 1:
                    continue
                for dp in range(4):
                    kb.op("dve", lambda e, dp=dp: e.tensor_tensor_scan(CS[:, dp, :], ones[:, :], lw[:, dp, :], 0.0, ALU.mult, ALU.add),
                          reads=[ones, lw], writes=[CS])
                kb.op("dve", lambda e: e.tensor_copy(TOT[:, :], CS[:, :, CH - 1]), reads=[CS], writes=[TOT])
                kb.op("dve", lambda e: e.tensor_tensor(CS[:, 2:4, :], lw[:, 2:4, :], CS[:, 2:4, :], ALU.subtract), reads=[lw, CS], writes=[CS])
                kb.op("dve", lambda e: e.tensor_tensor(CS[:, 2:4, :], CS[:, 2:4, :], TOT[:, 2:4].unsqueeze(2).broadcast_to([128, 2, CH]), ALU.add),
                      reads=[CS, TOT], writes=[CS])
                kb.op("act", lambda e: e.activation(Epos[:], CS[:], AF.Exp), reads=[CS], writes=[Epos])
                kb.op("act", lambda e: e.activation(Eneg[:], CS[:], AF.Exp, scale=-1.0), reads=[CS], writes=[Eneg])
                kb.op("pool", lambda e: e.tensor_tensor(TMP[:], CS[:], lw[:], ALU.subtract), reads=[CS, lw], writes=[TMP])
                kb.op("act", lambda e: e.activation(Eprev[:], TMP[:], AF.Exp), reads=[TMP], writes=[Eprev])
                kb.op("dve", lambda e: e.tensor_tensor(Etot[:], TOT[:, :].unsqueeze(2).broadcast_to([128, 4, CH]), CS[:], ALU.subtract),
                      reads=[TOT, CS], writes=[Etot])
                kb.op("act", lambda e: e.activation(Etot[:], Etot[:], AF.Exp), reads=[Etot], writes=[Etot])
                kb.op("act", lambda e: e.activation(Wtot[:], TOT[:], AF.Exp), reads=[TOT], writes=[Wtot])
                kb.op("dve", lambda e: e.tensor_tensor(Ab[:], al[:], Eprev[:], ALU.mult), reads=[al, Eprev], writes=[Ab])
                kb.op("pool", lambda e: e.tensor_tensor(Bb[:], be[:], Eneg[:], ALU.mult), reads=[be, Eneg], writes=[Bb])
                kb.op("dve", lambda e: e.tensor_tensor(Kb[:], kd[:], Eneg[:], ALU.mult), reads=[kd, Eneg], writes=[Kb])
                kb.op("pool", lambda e: e.tensor_tensor(Rb[:], rt[:], Epos[:], ALU.mult), reads=[rt, Epos], writes=[Rb])
                kb.op("dve", lambda e: e.tensor_tensor(Bt[:], be[:], Etot[:], ALU.mult), reads=[be, Etot], writes=[Bt])
                kb.op("pool", lambda e: e.tensor_tensor(Kt[:], kd[:], Etot[:], ALU.mult), reads=[kd, Etot], writes=[Kt])
                if RW_STAGE <= 2:
                    continue
                PA, PB, PC, PT1, PD, PX, PPQ, PE_ = PS
                mm_all(PA, lambda dp: (dp * 128, dp * 128 + 64), lambda dp, r: Bb[r, dp, :], lambda dp, r: Ab[r, dp, :], [Bb, Ab])
                mm_all(PA, lambda dp: (dp * 128 + 64, dp * 128 + 128), lambda dp, r: Bb[r, dp, :], lambda dp, r: Rb[r, dp, :], [Bb, Rb])
                mm_all(PB, lambda dp: (dp * 128, dp * 128 + 64), lambda dp, r: Kb[r, dp, :], lambda dp, r: Ab[r, dp, :], [Kb, Ab])
                mm_all(PB, lambda dp: (dp * 128 + 64, dp * 128 + 128), lambda dp, r: Kb[r, dp, :], lambda dp, r: Rb[r, dp, :], [Kb, Rb])
                mm_all(PC, lambda dp: (dp * 64, dp * 64 + 64), lambda dp, r: Ab[r, dp, :], lambda dp, r: Bb[r, dp, :], [Ab, Bb])
                q0, p0 = Q[0], P[0]
                pav = PA[:, :].rearrange("p (dp x) -> p dp x", dp=4)
                pbv = PB[:, :].rearrange("p (dp x) -> p dp x", dp=4)
                def mk(m):
                    return m[:, :, :].unsqueeze(2).broadcast_to([128, 2, 2, 64])
                def v4(ap):
                    return ap.rearrange("p (d pr) x -> p d pr x", d=2)
                kb.op("dve", lambda e: e.tensor_tensor(v4(q0[:]), v4(pav[:, :, 0:64]), mk(MTs), ALU.mult), reads=[PA, MTs], writes=[q0])
                kb.op("dve", lambda e: e.tensor_tensor(v4(ArbT[:]), v4(pav[:, :, 64:128]), mk(MTi), ALU.mult), reads=[PA, MTi], writes=[ArbT])
                kb.op("dve", lambda e: e.tensor_tensor(v4(AkvT[:]), v4(pbv[:, :, 0:64]), mk(MTs), ALU.mult), reads=[PB, MTs], writes=[AkvT])
                kb.op("dve", lambda e: e.tensor_tensor(v4(ArkT[:]), v4(pbv[:, :, 64:128]), mk(MTi), ALU.mult), reads=[PB, MTi], writes=[ArkT])
                kb.op("dve", lambda e: e.tensor_tensor(v4(p0[:]), v4(PC[:, 0:256].rearrange("p (dp x) -> p dp x", dp=4)), mk(Ms), ALU.mult),
                      reads=[PC, Ms], writes=[p0])
                if RW_STAGE <= 3:
                    continue
                def idb(r):
                    return ident_f[r, r.start:r.start + 64]
                mm_all(PT1, lambda dp: (dp * 128, dp * 128 + 64), lambda dp, r: Ab[r, dp, :], lambda dp, r: idb(r), [Ab, ident_f])
                mm_all(PT1, lambda dp: (dp * 128 + 64, dp * 128 + 128), lambda dp, r: Bt[r, dp, :], lambda dp, r: idb(r), [Bt, ident_f])
                mm_all(PC, lambda dp: (256 + dp * 64, 256 + dp * 64 + 64), lambda dp, r: Kt[r, dp, :], lambda dp, r: idb(r), [Kt, ident_f])
                x0 = X[0]
                pt1v = PT1[:, :].rearrange("p (dp x) -> p dp x", dp=4)
                kb.op("act", lambda e: e.copy(x0[:, :, 0:64], pt1v[:, :, 0:64]), reads=[PT1], writes=[x0])
                kb.op("act", lambda e: e.copy(Btok[:], pt1v[:, :, 64:128]), reads=[PT1], writes=[Btok])
                kb.op("act", lambda e: e.copy(Ktok[:], PC[:, 256:512].rearrange("p (dp x) -> p dp x", dp=4)), reads=[PC], writes=[Ktok])
                if RW_STAGE <= 4:
                    continue
                mm_all(PD, lambda dp: (dp * 64, dp * 64 + 64), lambda dp, r: AkvT[r, dp, :], lambda dp, r: vt[r, dp, :], [AkvT, vt])
                kb.op("act", lambda e: e.copy(x0[:, :, 64:128], PD[:, 0:256].rearrange("p (dp x) -> p dp x", dp=4)), reads=[PD], writes=[x0])
                if RW_STAGE <= 5:
                    continue
                qc, pc, xc = Q[0], P[0], X[0]
                for j in range(6):
                    qn, pn, xn = Q[(j + 1) % 2], P[(j + 1) % 2], X[(j + 1) % 2]
                    mm_all(PX, lambda dp: (dp * 128, dp * 128 + 128), lambda dp, r: qc[r, dp, :], lambda dp, r: xc[r, dp, :], [qc, xc])
                    kb.op("dve", lambda e, xn=xn, xc=xc: e.tensor_tensor(xn[:], xc[:], PX[:, :].rearrange("p (dp x) -> p dp x", dp=4), ALU.add),
                          reads=[xc, PX], writes=[xn])
                    if j < 5:
                        mm_all(PPQ, lambda dp: (dp * 64, dp * 64 + 64), lambda dp, r: qc[r, dp, :], lambda dp, r: pc[r, dp, :], [qc, pc])
                        mm_all(PPQ, lambda dp: (256 + dp * 64, 256 + dp * 64 + 64), lambda dp, r: pc[r, dp, :], lambda dp, r: qc[r, dp, :], [qc, pc])
                        kb.op("act", lambda e, pn=pn: e.copy(pn[:], PPQ[:, 0:256].rearrange("p (dp x) -> p dp x", dp=4)), reads=[PPQ], writes=[pn])
                        kb.op("act", lambda e, qn=qn: e.copy(qn[:], PPQ[:, 256:512].rearrange("p (dp x) -> p dp x", dp=4)), reads=[PPQ], writes=[qn])
                    qc, pc, xc = qn, pn, xn
                if RW_STAGE <= 6:
                    continue
                mm_all(PD, lambda dp: (256 + dp * 64, 256 + dp * 64 + 64), lambda dp, r: xc[r, dp, 0:64], lambda dp, r: ArbT[r, dp, :], [xc, ArbT])
                kb.op("dve", lambda e: e.tensor_tensor(RAT[:], Rb[:], PD[:, 256:512].rearrange("p (dp x) -> p dp x", dp=4), ALU.add),
                      reads=[Rb, PD], writes=[RAT])
                mm_all(PE_, lambda dp: (dp * 64, dp * 64 + 64), lambda dp, r: xc[r, dp, 0:64], lambda dp, r: Btok[r, dp, :], [xc, Btok])
                kb.op("pool", lambda e: e.tensor_tensor(DG[:], id2[:, :].unsqueeze(1).broadcast_to([128, 4, 64]),
                                                        Wtot[:, :].unsqueeze(2).broadcast_to([128, 4, 64]), ALU.mult), reads=[id2, Wtot], writes=[DG])
                kb.op("dve", lambda e: e.tensor_tensor(McT[:], DG[:], PE_[:, 0:256].rearrange("p (dp x) -> p dp x", dp=4), ALU.add),
                      reads=[DG, PE_], writes=[McT])
                for dp in range(4):
                    for hp in range(2):
                        r = HP[hp]
                        c0 = 256 + dp * 64
                        kb.op("pe", lambda e, dp=dp, r=r, c0=c0: e.matmul(PE_[r, c0:c0 + 64], Btok[r, dp, :], xc[r, dp, 64:128], start=True, stop=False),
                              reads=[Btok, xc], writes=[PE_])
                        kb.op("pe", lambda e, dp=dp, r=r, c0=c0: e.matmul(PE_[r, c0:c0 + 64], Ktok[r, dp, :], vt[r, dp, :], start=False, stop=True),
                              reads=[Ktok, vt], writes=[PE_])
                kb.op("act", lambda e: e.copy(NcS[:], PE_[:, 256:512].rearrange("p (dp x) -> p dp x", dp=4)), reads=[PE_], writes=[NcS])
                if RW_STAGE <= 7:
                    continue
                PYs = [PA, PT1]
                for dp in range(4):
                    for hp in range(2):
                        r = HP[hp]
                        PY = PYs[hp]
                        c0 = dp * 64
                        kb.op("pe", lambda e, dp=dp, r=r, c0=c0, PY=PY: e.matmul(PY[0:64, c0:c0 + 64], ST[r, dp, :], RAT[r, dp, :], start=True, stop=False),
                              reads=[ST, RAT], writes=[PY])
                        kb.op("pe", lambda e, dp=dp, r=r, c0=c0, PY=PY: e.matmul(PY[0:64, c0:c0 + 64], xc[r, dp, 64:128], ArbT[r, dp, :], start=False, stop=False),
                              reads=[xc, ArbT], writes=[PY])
                        kb.op("pe", lambda e, dp=dp, r=r, c0=c0, PY=PY: e.matmul(PY[0:64, c0:c0 + 64], vt[r, dp, :], ArkT[r, dp, :], start=False, stop=True),
                              reads=[vt, ArkT], writes=[PY])
                for d in range(2):
                    c0 = cidx[d] * CH
                    yv = ysb[d][:, :].rearrange("v (pr hp t) -> v pr hp t", pr=2, hp=2)
                    for hp in range(2):
                        kb.op("act", lambda e, d=d, hp=hp, yv=yv: e.copy(
                            yv[:, :, hp, :], PYs[hp][0:64, d * 128:(d + 1) * 128].rearrange("v (pr t) -> v pr t", pr=2)), reads=[PYs[hp]], writes=[ysb[d]])
                    dst = RS["YF" if d == 0 else "YB"]
                    kb.dma("sp", dst.t.rearrange("(h v) t -> v h t", v=64)[:, :, c0:c0 + CH],
                           ysb[d][:, :].rearrange("v (h t) -> v h t", h=4), reads=[ysb[d]], writes=[Buf()])
                if RW_STAGE <= 8:
                    continue
                PSS = PB
                mm_all(PSS, lambda dp: (dp * 64, dp * 64 + 64), lambda dp, r: McT[r, dp, :], lambda dp, r: ST[r, dp, :], [McT, ST])
                kb.op("dve", lambda e: e.tensor_tensor(ST[:], NcS[:], PSS[:, 0:256].rearrange("p (dp x) -> p dp x", dp=4), ALU.add),
                      reads=[NcS, PSS], writes=[ST])

    def phase_rwkv_out(l, with_ctx):
        with kb.scope():
            rk_ = colvec("rrk", W["rw_r_k"][l, :], W["rw_r_k"], [128, 2], "(j p) -> p j", p=128)
            lg_ = colvec("rlg", W["rw_ln_g"][l, :], W["rw_ln_g"], [128, 2], "(j p) -> p j", p=128)
            lb_ = colvec("rlb", W["rw_ln_b"][l, :], W["rw_ln_b"], [128, 2], "(j p) -> p j", p=128)
            nm = ("YF", "YB", "RT", "KD0", "KD1", "VT")
            tl = [{n: kb.sb(f"o{n}{i}", [128, 512]) for n in nm} for i in range(2)]
            sg = [kb.sb(f"osg{i}", [128, 512], BF16) for i in range(2)]
            ob = [kb.sb(f"oob{i}", [128, 512], BF16) for i in range(2)]
            wk = [[kb.sb(f"owk{k}{i}", [128, 512]) for k in range(3)] for i in range(2)]
            it = 0
            for pr in range(2):
                rows = slice(pr * 128, (pr + 1) * 128)
                for (t0, nt) in TCH:
                    if not with_ctx and t0 + nt <= C:
                        continue
                    t_, s_, o_, (a_, b_, c_) = tl[it % 2], sg[it % 2], ob[it % 2], wk[it % 2]
                    for k, n in enumerate(nm):
                        kb.dma("sp" if k % 2 == 0 else "pool", t_[n][:, 0:nt], RS[n][rows, t0:t0 + nt], reads=[RS[n]], writes=[t_[n]])
                    kb.dma("sp", s_[:, 0:nt], RS["SGT"][rows, t0:t0 + nt], reads=[RS["SGT"]], writes=[s_])
                    y = t_["YF"]
                    kb.op("dve", lambda e: e.tensor_tensor(y[:, 0:nt], y[:, 0:nt], t_["YB"][:, 0:nt], ALU.add), reads=[y, t_["YB"]], writes=[y])
                    p1, p2, p3 = PS[(3 * it) % 8], PS[(3 * it + 1) % 8], PS[(3 * it + 2) % 8]
                    kb.op("pe", lambda e: e.matmul(p1[:, 0:nt], blk64[:], y[:, 0:nt], start=True, stop=True), reads=[blk64, y], writes=[p1])
                    kb.op("dve", lambda e: e.scalar_tensor_tensor(a_[:, 0:nt], p1[:, 0:nt], -1.0 / 64, y[:, 0:nt], ALU.mult, ALU.add),
                          reads=[p1, y], writes=[a_])
                    kb.op("act", lambda e: e.activation(b_[:, 0:nt], a_[:, 0:nt], AF.Square), reads=[a_], writes=[b_])
                    kb.op("pe", lambda e: e.matmul(p2[:, 0:nt], blk64[:], b_[:, 0:nt], start=True, stop=True), reads=[blk64, b_], writes=[p2])
                    kb.op("dve", lambda e: e.tensor_scalar(b_[:, 0:nt], p2[:, 0:nt], 1.0 / 64, 64e-5, ALU.mult, ALU.add), reads=[p2], writes=[b_])
                    kb.op("act", lambda e: e.sqrt(b_[:, 0:nt], b_[:, 0:nt]), reads=[b_], writes=[b_])
                    kb.op("dve", lambda e: e.reciprocal(b_[:, 0:nt], b_[:, 0:nt]), reads=[b_], writes=[b_])
                    kb.op("dve", lambda e: e.tensor_tensor(a_[:, 0:nt], a_[:, 0:nt], b_[:, 0:nt], ALU.mult), reads=[a_, b_], writes=[a_])
                    kb.op("dve", lambda e: e.tensor_scalar(a_[:, 0:nt], a_[:, 0:nt], lg_[:, pr:pr + 1], lb_[:, pr:pr + 1], ALU.mult, ALU.add),
                          reads=[a_, lg_, lb_], writes=[a_])
                    kb.op("pool", lambda e: e.tensor_tensor(c_[:, 0:nt], t_["KD0"][:, 0:nt], t_["KD1"][:, 0:nt], ALU.add),
                          reads=[t_["KD0"], t_["KD1"]], writes=[c_])
                    kb.op("dve", lambda e: e.scalar_tensor_tensor(c_[:, 0:nt], t_["RT"][:, 0:nt], rk_[:, pr:pr + 1], c_[:, 0:nt], ALU.mult, ALU.mult),
                          reads=[t_["RT"], rk_, c_], writes=[c_])
                    kb.op("pe", lambda e: e.matmul(p3[:, 0:nt], blk64[:], c_[:, 0:nt], start=True, stop=True), reads=[blk64, c_], writes=[p3])
                    kb.op("dve", lambda e: e.tensor_tensor(c_[:, 0:nt], p3[:, 0:nt], t_["VT"][:, 0:nt], ALU.mult), reads=[p3, t_["VT"]], writes=[c_])
                    kb.op("dve", lambda e: e.tensor_tensor(a_[:, 0:nt], a_[:, 0:nt], c_[:, 0:nt], ALU.add), reads=[a_, c_], writes=[a_])
                    kb.op("pool", lambda e: e.tensor_tensor(o_[:, 0:nt], a_[:, 0:nt], s_[:, 0:nt], ALU.mult), reads=[a_, s_], writes=[o_])
                    kb.dma("sp", mixT[256 + pr * 128:256 + (pr + 1) * 128, t0:t0 + nt], o_[:, 0:nt], reads=[o_], writes=[Buf()])
                    it += 1


    SEGS = {"L": dict(Ls=L, A=32, cbw=32, off=C, ut="UTL"), "C": dict(Ls=C, A=2, cbw=64, off=0, ut="UTC")}

    def phase_hyena_prep(l, with_ctx):
        with kb.scope():
            stage = kb.sb("hstage", [128, 8, 128])
            wts = [kb.sb(f"hwt{i}", [128, 8, 128], BF16) for i in range(2)]
            cw = kb.sb("hcw", [128, 6, 3])
            for k in range(3):
                kb.dma("sp", cw[:, :, k], W["hy_conv"][l, k, :].rearrange("(j p) -> p j", p=128), reads=[W["hy_conv"]], writes=[cw], slow=True)
            ncw = kb.sb("hncw", [128, 6, 3])
            kb.op("dve", lambda e: e.tensor_scalar(ncw[:], cw[:], -1.0, None, ALU.mult), reads=[cw], writes=[ncw])
            Zraw = kb.sb("hZraw", [128, T + 2])
            Zout = kb.sb("hZout", [128, T])
            kb.op("pool", lambda e: e.memset(Zraw[:, 0:1], 0.0), writes=[Zraw])
            kb.op("pool", lambda e: e.memset(Zraw[:, T + 1:T + 2], 0.0), writes=[Zraw])
            ub = kb.sb("hub", [128, 32 * 128])
            tG = [kb.sb(f"htG{i}", [128, 512], BF16) for i in range(2)]
            for oi, jt in enumerate(range(8)):
                wt = wts[oi % 2]
                c0 = HY0 + jt * 128 if jt < 6 else HYG0 + (jt - 6) * 128
                load_w(l, wt, c0, 128, stage)
                if jt >= 6:
                    for ci, (t0, nt) in enumerate(TCH):
                        p = PS[ci % 4]
                        proj_fm(p, wt, 0, 128, t0, nt)
                        g = tG[ci % 2]
                        kb.op("act", lambda e, p=p, g=g, nt=nt: e.activation(g[:, 0:nt], p[:, 0:nt], AF.Silu), reads=[p], writes=[g])
                        kb.dma("sp", HS["SG"][(jt - 6) * 128:(jt - 5) * 128, t0:t0 + nt], g[:, 0:nt], reads=[g], writes=[Buf()])
                    continue
                conv_tile(l, wt, cw, ncw, jt, Zraw, Zout)
                arr, half = jt // 2, jt % 2
                for sn in (("L", "C") if with_ctx else ("L",)):
                    sg = SEGS[sn]
                    A, cbw, off = sg["A"], sg["cbw"], sg["off"]
                    G = 128 // A
                    ncg = 128 // G
                    ubv = ub[:, 0:A * 128].rearrange("p (g a c) -> p g a c", g=ncg, a=A)
                    for a in range(A):
                        p = PS[4 + (a // 4) % 4]
                        kb.op("pe", lambda e, p=p, a=a, A=A, off=off: e.transpose(
                            p[:, (a % 4) * 128:(a % 4 + 1) * 128], Zout[:, off + a:off + a + 127 * A + 1:A], ident_f[:]),
                            reads=[Zout, ident_f], writes=[p])
                        if a % 4 == 3 or a == A - 1:
                            a0 = (a // 4) * 4
                            na = a - a0 + 1
                            kb.op("act", lambda e, p=p, a0=a0, na=na, G=G: e.copy(
                                ubv[:, :, a0:a0 + na, :], p[:, 0:na * 128].rearrange("p (a g c) -> p g a c", a=na, c=G)), reads=[p], writes=[ub])
                    nb = 128 // cbw
                    bsz = A * cbw
                    for b in range(nb):
                        dst = HS[sg["ut"]][arr, half * nb + b, :, :]
                        kb.dma("sp" if b % 2 == 0 else "pool", dst, ub[:, b * bsz:(b + 1) * bsz], reads=[ub], writes=[Buf()])

    def cmul(dre, dim_, sre, sim, tre, tim, conj, srcb, tabb, dstb, tmp):
        t1, t2 = tmp
        sh = tuple(slice(None) for _ in range(1))
        kb.op("dve", lambda e: e.tensor_tensor(t1, sre, tre, ALU.mult), reads=srcb + tabb, writes=[dstb[2]])
        kb.op("dve", lambda e: e.tensor_tensor(t2, sim, tim, ALU.mult), reads=srcb + tabb, writes=[dstb[3]])
        kb.op("pool", lambda e: e.tensor_tensor(dre, t1, t2, ALU.add if conj else ALU.subtract), reads=[dstb[2], dstb[3]], writes=[dstb[0]])
        kb.op("dve", lambda e: e.tensor_tensor(t1, sim, tre, ALU.mult), reads=srcb + tabb + [dstb[0]], writes=[dstb[2]])
        kb.op("dve", lambda e: e.tensor_tensor(t2, sre, tim, ALU.mult), reads=srcb + tabb + [dstb[0]], writes=[dstb[3]])
        kb.op("pool", lambda e: e.tensor_tensor(dim_, t1, t2, ALU.subtract if conj else ALU.add), reads=[dstb[2], dstb[3]], writes=[dstb[1]])

    def phase_hyena_main(l, with_ctx):
        PI = math.pi
        with kb.scope():
            fw1 = kb.sb("hfw1", [33, 64])
            fw2 = kb.sb("hfw2", [64, 64])
            fw3 = kb.sb("hfw3", [64, 1024])
            kb.dma("sp", fw1[:], W["hy_fw1"][l, :, :], reads=[W["hy_fw1"]], writes=[fw1])
            kb.dma("sp", fw2[:], W["hy_fw2"][l, :, :], reads=[W["hy_fw2"]], writes=[fw2])
            kb.dma("sp", fw3[:], W["hy_fw3"][l, :, :], reads=[W["hy_fw3"]], writes=[fw3])
            fb1 = colvec("hfb1", W["hy_fb1"][l, :], W["hy_fb1"], [64, 1], "(d o) -> d o", o=1)
            fb2 = colvec("hfb2", W["hy_fb2"][l, :], W["hy_fb2"], [64, 1], "(d o) -> d o", o=1)
            frq = colvec("hfrq", W["hy_freq"][l, :], W["hy_freq"], [64, 1], "(d o) -> d o", o=1)
            brow = kb.sb("hbrow", [1, 512])
            kb.dma("sp", brow[:], W["hy_bias"][l, :, :].rearrange("o c -> (o c)").rearrange("(x n) -> x n", x=1), reads=[W["hy_bias"]], writes=[brow])
            for sn in (("L", "C") if with_ctx else ("L",)):
                sg = SEGS[sn]
                Ls, A, cbw, off = sg["Ls"], sg["A"], sg["cbw"], sg["off"]
                G = 128 // A
                N = 2 * Ls
                ngr = cbw // G
                nblk = 256 // cbw
                pre = f"hy{sn}_"
                with kb.scope():
                    def ld(nm, shape):
                        t = kb.sb("k" + nm, shape)
                        src = CT[pre + nm]
                        kb.dma("sp", t[:], src.t, reads=[src], writes=[t])
                        return t
                    F256 = ld("F256", [128, 2, 512]); TWC = ld("TWC", [128, 256]); TWS = ld("TWS", [128, 256])
                    Dre = ld("Dre", [128, 128]); Dim = ld("Dim", [128, 128]); nDim = ld("nDim", [128, 128])
                    E1 = ld("E1", [128, 256]); E2 = ld("E2", [128, 256])
                    TW2C = ld("TW2C", [128, 2, 128]); TW2S = ld("TW2S", [128, 2, 128])
                    IC = ld("IC", [128, 2, 128]); IS = ld("IS", [128, 2, 128])
                    h2T = kb.sb("h2T", [64, N])
                    with kb.scope():
                        zT = kb.sb("zT", [33, N])
                        kb.dma("sp", zT[:], CT[pre + "zT"].t, reads=[CT[pre + "zT"]], writes=[zT])
                        h1T = kb.sb("h1T", [64, N])
                        arg = [kb.sb(f"harg{i}", [64, 512]) for i in range(2)]
                        wr = [kb.sb(f"hwr{i}", [64, 512]) for i in range(2)]
                        for (src, K_, wgt, bcol, dst) in ((zT, 33, fw1, fb1, h1T), (h1T, 64, fw2, fb2, h2T)):
                            for ci, n0 in enumerate(range(0, N, 512)):
                                p = PS[ci % 4]
                                ag = arg[ci % 2]
                                kb.op("pe", lambda e: e.matmul(p[0:64, :], wgt[0:K_, :], src[0:K_, n0:n0 + 512], start=True, stop=True),
                                      reads=[wgt, src], writes=[p])
                                kb.op("dve", lambda e: e.tensor_scalar(ag[:, :], p[0:64, :], bcol[:, 0:1], frq[:, 0:1], ALU.add, ALU.mult),
                                      reads=[p, bcol, frq], writes=[ag])
                                for _w in range(2):
                                    kb.op("dve", lambda e: e.tensor_scalar(wr[0][:, :], ag[:, :], PI, -2 * PI, ALU.is_gt, ALU.mult), reads=[ag], writes=[wr[0]])
                                    kb.op("dve", lambda e: e.tensor_scalar(wr[1][:, :], ag[:, :], -PI, 2 * PI, ALU.is_lt, ALU.mult), reads=[ag], writes=[wr[1]])
                                    kb.op("dve", lambda e: e.tensor_tensor(ag[:, :], ag[:, :], wr[0][:, :], ALU.add), reads=[ag, wr[0]], writes=[ag])
                                    kb.op("dve", lambda e: e.tensor_tensor(ag[:, :], ag[:, :], wr[1][:, :], ALU.add), reads=[ag, wr[1]], writes=[ag])
                                kb.op("act", lambda e: e.activation(dst[:, n0:n0 + 512], ag[:, :], AF.Sin), reads=[ag], writes=[dst])
                    KT = [kb.sb(f"KT{o}", [128, 2, ngr, A, G]) for o in range(2)]
                    KS = [kb.sb(f"KS{o}", [128, ngr, 512]) for o in range(2)]
                    DECt = kb.sb("DECt", [128, 2, ngr, A, G])
                    part = kb.sb("hpart", [128, cbw])
                    rn = kb.sb("hrn", [128, cbw])
                    ex = kb.sb("hex", [1, cbw])
                    uv = kb.sb("huv", [128, ngr, A * G]); x1 = kb.sb("hx1", [128, ngr, A * G]); x2 = kb.sb("hx2", [128, ngr, A * G])
                    u2 = kb.sb("hu2", [128, ngr, A * G]); res = kb.sb("hres", [128, A, cbw])
                    Bp = [kb.sb(f"hBp{i}", [128, 256]) for i in range(4)]
                    Bpb = [Buf() for _ in range(4)]
                    Yp = [kb.sb(f"hYp{i}", [128, 256]) for i in range(4)]
                    Ypb = [Buf() for _ in range(4)]
                    Gp = [kb.sb(f"hGp{i}", [128, 2, 128]) for i in range(4)]
                    Gpb = [Buf() for _ in range(4)]
                    Fm = kb.sb("hFm", [cbw, Ls])
                    sgm = kb.sb("hsgm", [cbw, Ls], BF16)
                    Fo = kb.sb("hFo", [cbw, Ls], BF16)

                    def fwd_fft(lhs_chunks, lhs_bufs, psB, psX):
                        n = len(lhs_chunks)
                        for i, (ap, hf) in enumerate(lhs_chunks):
                            kb.op("pe", lambda e, ap=ap, hf=hf, i=i: e.matmul(psB[:, :], ap, F256[:, hf, :], start=(i == 0), stop=(i == n - 1)),
                                  reads=lhs_bufs + [F256], writes=[psB])
                        cmul(Bp[0][:, :], Bp[1][:, :], psB[:, 0:256], psB[:, 256:512], TWC[:, :], TWS[:, :], True,
                             [psB], [TWC, TWS], Bpb, (Bp[2][:, :], Bp[3][:, :]))
                        kb.op("pe", lambda e: e.matmul(psX[:, 0:256], Dre[:, :], Bp[0][:, :], start=True, stop=False), reads=[Dre, Bpb[0]], writes=[psX])
                        kb.op("pe", lambda e: e.matmul(psX[:, 0:256], nDim[:, :], Bp[1][:, :], start=False, stop=True), reads=[nDim, Bpb[1]], writes=[psX])
                        kb.op("pe", lambda e: e.matmul(psX[:, 256:512], Dim[:, :], Bp[0][:, :], start=True, stop=False), reads=[Dim, Bpb[0]], writes=[psX])
                        kb.op("pe", lambda e: e.matmul(psX[:, 256:512], Dre[:, :], Bp[1][:, :], start=False, stop=True), reads=[Dre, Bpb[1]], writes=[psX])

                    def conv_group(src, src_b, g, o, mulv, mul_b, dst_ap, dst_b, it):
                        psB, psX, psG, psy = PS[it % 2], PS[2 + it % 2], PS[4 + it % 2], PS[6 + it % 2]
                        fwd_fft([(src[:, g, :], 0)], [src_b], psB, psX)
                        cmul(Yp[0][:, :], Yp[1][:, :], psX[:, 0:256], psX[:, 256:512], KS[o][:, g, 0:256], KS[o][:, g, 256:512], False,
                             [psX], [KS[o]], Ypb, (Yp[2][:, :], Yp[3][:, :]))
                        for chn in range(2):
                            fs = slice(chn * 128, (chn + 1) * 128)
                            kb.op("pe", lambda e, fs=fs, chn=chn: e.matmul(psG[:, chn * 256:(chn + 1) * 256], Yp[0][:, fs], E1[:, :], start=True, stop=False),
                                  reads=[Ypb[0], E1], writes=[psG])
                            kb.op("pe", lambda e, fs=fs, chn=chn: e.matmul(psG[:, chn * 256:(chn + 1) * 256], Yp[1][:, fs], E2[:, :], start=False, stop=True),
                                  reads=[Ypb[1], E2], writes=[psG])
                        pg = psG[:, :].rearrange("p (ch ri c) -> p ch ri c", ch=2, ri=2)
                        cmul(Gp[0][:, :, :], Gp[1][:, :, :], pg[:, :, 0, :], pg[:, :, 1, :], TW2C[:, :, :], TW2S[:, :, :], False,
                             [psG], [TW2C, TW2S], Gpb, (Gp[2][:, :, :], Gp[3][:, :, :]))
                        k = 0
                        for chn in range(2):
                            for (tab, gsrc, gb) in ((IC, Gp[0], Gpb[0]), (IS, Gp[1], Gpb[1])):
                                kb.op("pe", lambda e, chn=chn, tab=tab, gsrc=gsrc, k=k: e.matmul(
                                    psy[:, 0:128], tab[:, chn, :], gsrc[:, chn, :], start=(k == 0), stop=(k == 3)), reads=[tab, gb], writes=[psy])
                                k += 1
                        kb.op("dve", lambda e: e.tensor_tensor(dst_ap, psy[:, 0:128].rearrange("p (c a) -> p a c", a=A),
                                                               mulv[:, g, :].rearrange("p (a c) -> p a c", c=G), ALU.mult),
                              reads=[psy, mul_b], writes=[dst_b])

                    git = 0
                    for cb in range(nblk):
                        kb.dma("sp", DECt[:].rearrange("p h g a c -> p (h g a c)"), CT[pre + "DEC"][cb, :, :], reads=[CT[pre + "DEC"]], writes=[DECt])
                        for ai, tile_ in enumerate((uv, x1, x2)):
                            kb.dma("pool", tile_[:].rearrange("p g x -> p (g x)"), HS[sg["ut"]][ai, cb, :, :], reads=[HS[sg["ut"]]], writes=[tile_])
                        for o in range(2):
                            for hf in range(2):
                                col0 = o * 512 + hf * 256 + cb * cbw
                                npb = 512 // cbw
                                for a in range(A):
                                    p = PS[(a // npb) % 4]
                                    kb.op("pe", lambda e, p=p, a=a, hf=hf, col0=col0, npb=npb: e.matmul(
                                        p[:, (a % npb) * cbw:(a % npb + 1) * cbw], h2T[0:64, hf * 128 * A + a:hf * 128 * A + a + 127 * A + 1:A],
                                        fw3[0:64, col0:col0 + cbw], start=True, stop=True), reads=[h2T, fw3], writes=[p])
                                    if a % npb == npb - 1 or a == A - 1:
                                        a0 = (a // npb) * npb
                                        na = a - a0 + 1
                                        kb.op("dve", lambda e, p=p, a0=a0, na=na, hf=hf, o=o: e.tensor_tensor(
                                            KT[o][:, hf, :, a0:a0 + na, :], p[:, 0:na * cbw].rearrange("p (a g c) -> p g a c", a=na, c=G),
                                            DECt[:, hf, :, a0:a0 + na, :], ALU.mult), reads=[p, DECt], writes=[KT[o]])
                            kb.op("dve", lambda e, o=o: e.tensor_reduce(part[:, :].rearrange("p (g c) -> p g c", c=G),
                                                                        KT[o][:, :, :, :, :].rearrange("p h g a c -> p g c h a"), AX.XY, ALU.add,
                                                                        apply_absolute_value=True), reads=[KT[o]], writes=[part])
                            pe_ = PS[4]
                            kb.op("pe", lambda e, o=o: e.matmul(pe_[0:1, 0:cbw], h2T[0:64, 0:1], fw3[0:64, o * 512 + 256 + cb * cbw:o * 512 + 256 + (cb + 1) * cbw],
                                                                start=True, stop=True), reads=[h2T, fw3], writes=[pe_])
                            kb.op("act", lambda e: e.activation(ex[0:1, :], pe_[0:1, 0:cbw], AF.Abs), reads=[pe_], writes=[ex])
                            kb.op("dve", lambda e: e.tensor_tensor(part[0:1, :], part[0:1, :], ex[0:1, :], ALU.add), reads=[part, ex], writes=[part])
                            pt_ = PS[5]
                            kb.op("pe", lambda e: e.matmul(pt_[:, 0:cbw], ones_f[:, :], part[:, :], start=True, stop=True), reads=[ones_f, part], writes=[pt_])
                            kb.op("dve", lambda e: e.reciprocal(rn[:, :], pt_[:, 0:cbw]), reads=[pt_], writes=[rn])
                            for hf in range(2):
                                kb.op("dve", lambda e, o=o, hf=hf: e.tensor_tensor(
                                    KT[o][:, hf, :, :, :], KT[o][:, hf, :, :, :],
                                    rn[:, :].rearrange("p (g c) -> p g c", c=G).unsqueeze(2).broadcast_to([128, ngr, A, G]), ALU.mult),
                                    reads=[KT[o], rn], writes=[KT[o]])
                            kb.op("dve", lambda e, o=o: e.tensor_tensor(
                                KT[o][0:1, 0, :, 0, :], KT[o][0:1, 0, :, 0, :],
                                brow[0:1, o * 256 + cb * cbw:o * 256 + (cb + 1) * cbw].rearrange("p (g c) -> p g c", c=G), ALU.add),
                                reads=[KT[o], brow], writes=[KT[o]])
                            for g in range(ngr):
                                psB, psX = PS[git % 2], PS[2 + git % 2]
                                fwd_fft([(KT[o][:, 0, g, :, :].rearrange("p a c -> p (a c)"), 0),
                                         (KT[o][:, 1, g, :, :].rearrange("p a c -> p (a c)"), 1)], [KT[o]], psB, psX)
                                kb.op("act", lambda e, o=o, g=g, psX=psX: e.copy(KS[o][:, g, :], psX[:, :]), reads=[psX], writes=[KS[o]])
                                git += 1
                        for g in range(ngr):
                            conv_group(uv, uv, g, 0, x1, x1, u2[:, g, :].rearrange("p (a c) -> p a c", c=G), u2, git)
                            git += 1
                        for g in range(ngr):
                            conv_group(u2, u2, g, 1, x2, x2, res[:, :, g * G:(g + 1) * G], res, git)
                            git += 1
                        kb.dma("sp", sgm[:], HS["SG"][cb * cbw:(cb + 1) * cbw, off:off + Ls], reads=[HS["SG"]], writes=[sgm])
                        Fv = Fm[:, :].rearrange("c (p a) -> c p a", a=A)
                        for a in range(A):
                            p = PS[4 + (a // 4) % 4]
                            kb.op("pe", lambda e, p=p, a=a: e.transpose(p[0:cbw, (a % 4) * 128:(a % 4 + 1) * 128], res[:, a, :], ident_f[:]),
                                  reads=[res, ident_f], writes=[p])
                            if a % 4 == 3 or a == A - 1:
                                a0 = (a // 4) * 4
                                na = a - a0 + 1
                                kb.op("act", lambda e, p=p, a0=a0, na=na: e.copy(
                                    Fv[:, :, a0:a0 + na], p[0:cbw, 0:na * 128].rearrange("c (a p) -> c p a", p=128)), reads=[p], writes=[Fm])
                        kb.op("pool", lambda e: e.tensor_tensor(Fo[:, :], Fm[:, :], sgm[:, :], ALU.mult), reads=[Fm, sgm], writes=[Fo])
                        kb.dma("sp", mixT[cb * cbw:(cb + 1) * cbw, off:off + Ls], Fo[:, :], reads=[Fo], writes=[Buf()])

    dbgn = [n for n, _ in dbg]
    for l in range(depth):
        last = (l == DEPTH - 1)
        with kb.scope():
            hT = kb.sb("hT", [128, 8, T], BF16)
            G1 = kb.sb("G1", [128, 2, D])
            SH = kb.sb("SH", [128, 2, D])
            phase_mod(l)
            phase_norm(l)
            if "noattn" not in dbgn:
                phase_attn(l, False, not last)
                phase_attn(l, True, not last)
            if "norw" not in dbgn:
                phase_rwkv_prep(l)
            if "nohy" not in dbgn:
                phase_hyena_prep(l, not last)
            if "hT" in dbgn:
                tmp = kb.sb("dbghT", [128, T])
                for j in range(8):
                    kb.op("dve", lambda e, j=j, tmp=tmp: e.tensor_copy(tmp[:], hT[:, j, :]), reads=[hT], writes=[tmp])
                    kb.dma("sp", dbg_t["hT"][:, j, :], tmp[:], reads=[tmp], writes=[dbg_t["hT"]])
        if "norw" not in dbgn:
            phase_rwkv_chunked(l)
            phase_rwkv_out(l, not last)
        if "nohy" not in dbgn:
            phase_hyena_main(l, not last)
        if "noout" not in dbgn:
            phase_out(l, last)
    for n, s_ in dbg:
        if n == "xres":
            with kb.scope():
                tx = kb.sb("dbgx", [128, D])
                for i in range(NT):
                    kb.dma("sp", tx[:], xres[i * 128:(i + 1) * 128, :], reads=[xres_b[i]], writes=[tx])
                    kb.dma("sp", dbg_t[n][i * 128:(i + 1) * 128, :], tx[:], reads=[tx], writes=[dbg_t[n]])
        if n == "mixT":
            with kb.scope():
                tmpb = kb.sb("dbgmb", [128, T], BF16)
                tmpf = kb.sb("dbgmf", [128, T])
                for j in range(8):
                    kb.dma("sp", tmpb[:], mixT[j * 128:(j + 1) * 128, :], reads=[mixT], writes=[tmpb])
                    kb.op("dve", lambda e, tmpb=tmpb, tmpf=tmpf: e.tensor_copy(tmpf[:], tmpb[:]), reads=[tmpb], writes=[tmpf])
                    kb.dma("sp", dbg_t[n][j * 128:(j + 1) * 128, :], tmpf[:], reads=[tmpf], writes=[dbg_t[n]])
    kb.finish()
    kb.es.close()
    return kb, cst


_PROG = {}


def kernel(**inputs):
    if "p" not in _PROG:
        _PROG["p"] = build()
    kb, cst = _PROG["p"]
    f = lambda a: np.ascontiguousarray(np.asarray(a, dtype=np.float32))
    shared = {}
    for n in inputs:
        if n in ("x", "c", "ctx", "c_ctx"):
            continue
        shared[n] = f(inputs[n])
    shared["c_ctx"] = f(inputs["c_ctx"])
    for n, a in cst.items():
        shared["k_" + n] = np.ascontiguousarray(a)
    x, c, ctx = f(inputs["x"]), f(inputs["c"]), f(inputs["ctx"])
    B = x.shape[0]
    in_maps = []
    for b in range(B):
        m = dict(shared)
        m["x"] = np.ascontiguousarray(x[b])
        m["c"] = np.ascontiguousarray(c[b])
        m["ctx"] = np.ascontiguousarray(ctx[b])
        in_maps.append(m)
    res = run_bass_kernel_spmd(kb.nc, in_maps, core_ids=list(range(B)))
    return np.stack([np.asarray(res.results[b]["out"], dtype=np.float32) for b in range(B)], axis=0)
```

```python
import contextlib
import math
import numpy as np
import ml_dtypes
import concourse.bass as bass
import concourse.mybir as mybir
from concourse.bass_utils import run_bass_kernel_spmd

F32 = mybir.dt.float32
BF16 = mybir.dt.bfloat16
F32R = mybir.dt.float32r
ALU = mybir.AluOpType
AF = mybir.ActivationFunctionType
AX = mybir.AxisListType

D = 1024
L = 4096
C = 256
T = L + C
NT = T // 128
DEPTH = 4
D_IN = 3712
HY0, HYG0, RW0, RWG0, WA0, WAG0, FA0, FAG0 = 0, 768, 1024, 1920, 2176, 2688, 2944, 3456
EPS = 1e-6
NSLOT = 12
import os
RW_STAGE = int(os.environ.get('RW_STAGE', '99'))


class Buf:
    def __init__(self, name=""):
        self.name = name
        self.w = None
        self.r = {}

    def wdeps(self):
        return [self.w] if self.w is not None else []

    def rdeps(self):
        return list(self.r.values())

    def add_reader(self, tok):
        k = tok[:2]
        if k not in self.r or self.r[k][2] < tok[2]:
            self.r[k] = tok

    def set_writer(self, tok):
        self.w = tok
        self.r = {}


class Tile(Buf):
    def __init__(self, name, t):
        super().__init__(name)
        self.t = t

    def __getitem__(self, key):
        return self.t[key]


class KB:
    def __init__(self):
        self.nc = bass.Bass("TRN2", target_bir_lowering=False)
        nc = self.nc
        self.es = contextlib.ExitStack()
        self.eng = {"pe": nc.tensor, "act": nc.scalar, "dve": nc.vector, "pool": nc.gpsimd, "sp": nc.sync}
        self.sem = {}
        self.cnt = {}
        self.waited = {e: {} for e in self.eng}
        for e in self.eng:
            self.sem[e] = self.es.enter_context(nc.semaphore("s_" + e))
            self.cnt[e] = 0
        self.slots = {}
        self.slot_i = {}
        for q in ("sp", "act", "pool"):
            self.slots[q] = [[self.es.enter_context(nc.semaphore(f"d_{q}{i}")), 0] for i in range(NSLOT)]
            self.slot_i[q] = 0
        self.n_ins = 0

    def sb(self, name, shape, dt=F32):
        self.uid = getattr(self, "uid", 0) + 1
        name = f"{name}_{self.uid}"
        return Tile(name, self.es.enter_context(self.nc.sbuf_tensor(name, list(shape), dt)))

    def ps(self, name, shape, dt=F32):
        return Tile(name, self.es.enter_context(self.nc.psum_tensor(name, list(shape), dt)))

    def dram(self, name, shape, dt=F32, kind="Internal"):
        t = self.nc.dram_tensor(name, list(shape), dt, kind=kind)
        b = Tile(name, t.ap())
        return b

    def _tok_sem(self, tok):
        if tok[0] == "e":
            return ("e", tok[1]), self.sem[tok[1]], tok[2]
        return ("d", tok[1]), self.slots[tok[1][0]][tok[1][1]][0], tok[2]

    def _wait(self, e, toks):
        need = {}
        for tok in toks:
            if tok is None:
                continue
            key, sem, val = self._tok_sem(tok)
            if tok[0] == "e" and tok[1] == e and e == "pe":
                continue
            if self.waited[e].get(key, 0) >= val:
                continue
            if key not in need or need[key][1] < val:
                need[key] = (sem, val)
        for key, (sem, val) in need.items():
            self.eng[e].wait_ge(sem, val)
            self.waited[e][key] = val

    def op(self, e, fn, reads=(), writes=()):
        toks = []
        for b in reads:
            toks += b.wdeps()
        for b in writes:
            toks += b.wdeps() + b.rdeps()
        self._wait(e, toks)
        ins = fn(self.eng[e])
        self.cnt[e] += 1
        ins.then_inc(self.sem[e], 1)
        tok = ("e", e, self.cnt[e])
        for b in reads:
            b.add_reader(tok)
        for b in writes:
            b.set_writer(tok)
        self.n_ins += 1
        return ins

    def dma(self, q, out, in_, reads=(), writes=(), slow=False):
        i = self.slot_i[q]
        self.slot_i[q] = (i + 1) % NSLOT
        slot = self.slots[q][i]
        toks = []
        if slot[1] > 0:
            toks.append(("d", (q, i), slot[1]))
        for b in reads:
            toks += b.wdeps()
        for b in writes:
            toks += b.wdeps() + b.rdeps()
        self._wait(q, toks)
        if slow:
            ins = self.eng[q].dma_start(out=out, in_=in_, allow_slow_non_contiguous=True)
        else:
            ins = self.eng[q].dma_start(out=out, in_=in_)
        ins.then_inc(slot[0], 16)
        slot[1] += 16
        tok = ("d", (q, i), slot[1])
        for b in reads:
            b.add_reader(tok)
        for b in writes:
            b.set_writer(tok)
        self.n_ins += 1
        return ins

    def barrier(self):
        toks = [("e", e, self.cnt[e]) for e in self.eng if self.cnt[e] > 0]
        for q in self.slots:
            for i, s in enumerate(self.slots[q]):
                if s[1] > 0:
                    toks.append(("d", (q, i), s[1]))
        for e in self.eng:
            self._wait(e, toks)

    def finish(self):
        self.barrier()

    @contextlib.contextmanager
    def scope(self):
        es = contextlib.ExitStack()
        old = self.es
        self.es = es
        try:
            yield
        finally:
            self.barrier()
            self.es = old
            es.close()


def host_consts():
    cst = {}
    cst["ident_bf"] = np.eye(128, dtype=np.float32).astype(ml_dtypes.bfloat16)
    cst["ident_f"] = np.eye(128, dtype=np.float32)
    blk = np.zeros((128, 128), np.float32)
    blk[:64, :64] = 1.0
    blk[64:, 64:] = 1.0
    cst["blk64"] = blk
    cst["ones_f"] = np.ones((128, 128), np.float32)
    t = np.arange(L)
    row = (t // 64).astype(np.float32)
    col = (t % 64).astype(np.float32)
    inv = (10000.0 ** (-np.arange(16, dtype=np.float32) / 16)).astype(np.float32)
    cosT = np.zeros((128, L), np.float32)
    sinT = np.zeros((128, L), np.float32)
    perm = np.zeros((128, 128), np.float32)
    for p in range(128):
        d = p % 64
        sec, half, f = d // 32, (d % 32) // 16, d % 16
        pos = row if sec == 0 else col
        ang = (pos * inv[f]).astype(np.float32)
        cosT[p] = np.cos(ang)
        sinT[p] = np.sin(ang)
        if half == 0:
            perm[p + 16, p] = -1.0
        else:
            perm[p - 16, p] = 1.0
    cst["rope_cos"] = cosT
    cst["rope_sin"] = sinT
    cst["rope_perm"] = perm
    i = np.arange(128)[:, None]
    j = np.arange(384)[None, :]
    cst["wmask"] = np.where((j >= i) & (j <= i + 256), 0.0, -1e30).astype(np.float32)
    ii = np.arange(64)
    ms = np.zeros((128, 2, 64), np.float32); mts = np.zeros((128, 2, 64), np.float32); mti = np.zeros((128, 2, 64), np.float32)
    for hp in range(2):
        rows = slice(hp * 64, hp * 64 + 64)
        ms[rows, 0, :] = (ii[None, :] < ii[:, None]); ms[rows, 1, :] = (ii[None, :] > ii[:, None])
        mts[rows, 0, :] = (ii[:, None] < ii[None, :]); mts[rows, 1, :] = (ii[:, None] > ii[None, :])
        mti[rows, 0, :] = (ii[:, None] <= ii[None, :]); mti[rows, 1, :] = (ii[:, None] >= ii[None, :])
    cst["rw_ms"] = ms; cst["rw_mts"] = mts; cst["rw_mti"] = mti
    cst["rw_id2"] = np.concatenate([np.eye(64, dtype=np.float32)] * 2, 0)
    cst.update(hy_consts(L, 32, 32, "L"))
    cst.update(hy_consts(C, 2, 64, "C"))
    return cst


def hy_consts(Ls, A, cbw, tag):
    G = 128 // A
    N = 2 * Ls
    out = {}
    p = np.arange(128)
    f1 = np.arange(256)
    F = np.zeros((128, 2, 512), np.float64)
    for h in range(2):
        pp = h * 128 + p
        ang = 2 * np.pi * ((pp[:, None] * f1[None, :]) % 256) / 256
        F[:, h, 0:256] = np.cos(ang)
        F[:, h, 256:512] = -np.sin(ang)
    out["F256"] = F
    a_of_row = np.arange(128) // G
    th = 2 * np.pi * ((a_of_row[:, None] * f1[None, :]) % N) / N
    out["TWC"] = np.cos(th)
    out["TWS"] = np.sin(th)
    Dre = np.zeros((128, 128)); Dim = np.zeros((128, 128))
    E1 = np.zeros((128, 256)); E2 = np.zeros((128, 256))
    for a in range(A):
        for c in range(G):
            for f2 in range(A):
                ph = 2 * np.pi * ((a * f2) % A) / A
                Dre[a * G + c, c * A + f2] = np.cos(ph)
                Dim[a * G + c, c * A + f2] = -np.sin(ph)
                E1[c * A + f2, c * A + a] = np.cos(ph)
                E1[c * A + f2, 128 + c * A + a] = np.sin(ph)
                E2[c * A + f2, c * A + a] = -np.sin(ph)
                E2[c * A + f2, 128 + c * A + a] = np.cos(ph)
    out["Dre"] = Dre; out["Dim"] = Dim; out["nDim"] = -Dim; out["E1"] = E1; out["E2"] = E2
    a_of_col = np.arange(128) % A
    TW2C = np.zeros((128, 2, 128)); TW2S = np.zeros((128, 2, 128))
    IC = np.zeros((128, 2, 128)); IS = np.zeros((128, 2, 128))
    for ch in range(2):
        ff = ch * 128 + np.arange(128)
        th2 = 2 * np.pi * ((ff[:, None] * a_of_col[None, :]) % N) / N
        TW2C[:, ch, :] = np.cos(th2) / N
        TW2S[:, ch, :] = np.sin(th2) / N
        ph = 2 * np.pi * ((ff[:, None] * p[None, :]) % 256) / 256
        IC[:, ch, :] = np.cos(ph)
        IS[:, ch, :] = -np.sin(ph)
    out["TW2C"] = TW2C; out["TW2S"] = TW2S; out["IC"] = IC; out["IS"] = IS
    tp = np.arange(N)
    pos = np.where(tp < Ls, tp, N - tp).astype(np.float64)
    tn = (pos / (Ls - 1)).astype(np.float32)
    w = ((2.0 * math.pi / Ls) * pos).astype(np.float32)
    fb = np.linspace(1e-4, 15.0, 16, dtype=np.float32)
    zT = np.zeros((33, N), np.float32)
    zT[0] = tn
    zT[1:17] = np.cos(fb[:, None] * w[None, :])
    zT[17:33] = np.sin(fb[:, None] * w[None, :])
    out["zT"] = zT
    deltas = np.abs(np.linspace(math.log(1e-2) / 1.5, math.log(1e-2) / 0.3, 256, dtype=np.float32))
    dec = np.exp(-tn[:, None] * deltas[None, :]).astype(np.float32)
    dec[Ls, :] = 0.0
    nblk = 256 // cbw
    ngr = cbw // G
    DEC = np.zeros((nblk, 128, 2, ngr, A, G), np.float32)
    for h in range(2):
        for a in range(A):
            tpp = A * (h * 128 + p) + a
            for b in range(nblk):
                DEC[b, :, h, :, a, :] = dec[tpp, b * cbw:(b + 1) * cbw].reshape(128, ngr, G)
    out["DEC"] = DEC.reshape(nblk, 128, 2 * ngr * A * G)
    return {f"hy{tag}_{k}": np.ascontiguousarray(v.astype(np.float32)) for k, v in out.items()}

CONST_SPECS = None


def build(depth=DEPTH, dbg=()):
    kb = KB()
    nc = kb.nc
    cst = host_consts()
    def inp(name, shape, dt=F32):
        return kb.dram(name, shape, dt, kind="ExternalInput")

    x_in = inp("x", [L, D])
    c_in = inp("c", [D])
    ctx_in = inp("ctx", [C, D])
    cctx_in = inp("c_ctx", [D])
    W = {}
    wspec = {
        "mod_w": [DEPTH, D, 3 * D], "mod_b": [DEPTH, 3 * D], "norm_g": [DEPTH, D], "w_in": [DEPTH, D, D_IN],
        "w_out": [DEPTH, D, D], "wa_sink": [DEPTH, 4], "fa_q_norm": [DEPTH, 64], "fa_k_norm": [DEPTH, 64],
        "final_g": [D],
        "rw_conv": [DEPTH, 3, 896], "rw_w0": [DEPTH, 2, 256], "rw_w_up": [DEPTH, 2, 64, 256], "rw_a0": [DEPTH, 2, 256],
        "rw_a_up": [DEPTH, 2, 64, 256], "rw_k_k": [DEPTH, 256], "rw_k_a": [DEPTH, 256], "rw_r_k": [DEPTH, 256],
        "rw_ln_g": [DEPTH, 256], "rw_ln_b": [DEPTH, 256],
        "hy_conv": [DEPTH, 3, 768], "hy_fw1": [DEPTH, 33, 64], "hy_fb1": [DEPTH, 64], "hy_freq": [DEPTH, 64],
        "hy_fw2": [DEPTH, 64, 64], "hy_fb2": [DEPTH, 64], "hy_fw3": [DEPTH, 64, 1024], "hy_bias": [DEPTH, 2, 256],
    }
    for n, s in wspec.items():
        W[n] = inp(n, s)
    CT = {}
    for n, a in cst.items():
        CT[n] = inp("k_" + n, list(a.shape), BF16 if a.dtype == ml_dtypes.bfloat16 else F32)
    out = kb.dram("out", [L, D], F32, kind="ExternalOutput")
    xres = kb.dram("xres", [T, D], F32)
    mixT = kb.dram("mixT", [D, T], BF16)
    RS = {}
    for n in ("RT", "VT", "AL", "W0", "W1", "B0", "B1", "KD0", "KD1", "YF", "YB"):
        RS[n] = kb.dram("rs_" + n, [256, T])
    RS["VTOK"] = kb.dram("rs_VTOK", [T, 256])
    RS["SGT"] = kb.dram("rs_SGT", [256, T], BF16)
    HS = {"SG": kb.dram("hs_SG", [256, T], BF16),
          "UTL": kb.dram("hs_UTL", [3, 8, 128, 32 * 32]), "UTC": kb.dram("hs_UTC", [3, 4, 128, 2 * 64])}
    dbg_t = {}
    for n, s in dbg:
        dbg_t[n] = kb.dram("dbg_" + n, s, F32, kind="ExternalOutput")

    ident_bf = kb.sb("ident_bf", [128, 128], BF16)
    ident_f = kb.sb("ident_f", [128, 128])
    blk64 = kb.sb("blk64", [128, 128])
    ones_f = kb.sb("ones_f", [128, 128])
    for tl, n in ((ident_bf, "ident_bf"), (ident_f, "ident_f"), (blk64, "blk64"), (ones_f, "ones_f")):
        kb.dma("sp", tl[:], CT[n][:, :], reads=[CT[n]], writes=[tl])
    hT = G1 = SH = None
    GT = kb.sb("GT", [128, 2, D])
    PS = [kb.ps(f"ps{i}", [128, 512]) for i in range(8)]

    xres_b = [Buf(f"xres{i}") for i in range(NT)]

    def x_src(l, i):
        if l == 0:
            if i < 2:
                return ctx_in[i * 128:(i + 1) * 128, :], ctx_in
            return x_in[(i - 2) * 128:(i - 1) * 128, :], x_in
        return xres[i * 128:(i + 1) * 128, :], xres_b[i]

    def phase_mod(l):
        with kb.scope():
            cc = kb.sb("cc", [128, 2, 8])
            sc = kb.sb("sc", [128, 2, 8])
            mw = [kb.sb(f"mw{i}", [128, 8, 512]) for i in range(2)]
            mb = kb.sb("mb", [128, 3 * D])
            ng = kb.sb("ng", [128, D])
            modr = kb.sb("modr", [128, 2, 3 * D])
            kb.dma("sp", cc[:, 0, :], c_in.t.rearrange("(j p) -> p j", p=128), reads=[c_in], writes=[cc], slow=True)
            kb.dma("sp", cc[:, 1, :], cctx_in.t.rearrange("(j p) -> p j", p=128), reads=[cctx_in], writes=[cc], slow=True)
            kb.dma("sp", mb[:], W["mod_b"][l, :].partition_broadcast(128), reads=[W["mod_b"]], writes=[mb])
            kb.dma("sp", ng[:], W["norm_g"][l, :].partition_broadcast(128), reads=[W["norm_g"]], writes=[ng])
            kb.op("act", lambda e: e.activation(sc[:], cc[:], AF.Silu), reads=[cc], writes=[sc])
            for n in range(6):
                m = mw[n % 2]
                kb.dma("sp" if n % 2 == 0 else "pool", m[:],
                       W["mod_w"][l, :, n * 512:(n + 1) * 512].rearrange("(j p) n -> p j n", p=128),
                       reads=[W["mod_w"]], writes=[m])
                for i in range(2):
                    p = PS[(2 * n + i) % 8]
                    for j in range(8):
                        kb.op("pe", lambda e, p=p, i=i, j=j, m=m: e.matmul(
                            p[:, :], sc[:, i, j:j + 1].broadcast_to([128, 128]), m[:, j, :],
                            start=(j == 0), stop=(j == 7)), reads=[sc, m], writes=[p])
                    kb.op("dve", lambda e, p=p, i=i, n=n: e.tensor_tensor(
                        modr[:, i, n * 512:(n + 1) * 512], p[:, :], mb[:, n * 512:(n + 1) * 512], ALU.add),
                        reads=[p, mb], writes=[modr])
            for i in range(2):
                kb.op("dve", lambda e, i=i: e.scalar_tensor_tensor(
                    G1[:, i, :], modr[:, i, D:2 * D], 1.0, ng[:], ALU.add, ALU.mult), reads=[modr, ng], writes=[G1])
                kb.op("act", lambda e, i=i: e.copy(SH[:, i, :], modr[:, i, 0:D]), reads=[modr], writes=[SH])
                kb.op("act", lambda e, i=i: e.copy(GT[:, i, :], modr[:, i, 2 * D:3 * D]), reads=[modr], writes=[GT])

    def phase_norm(l):
        with kb.scope():
            xt = [kb.sb(f"xt{i}", [128, D]) for i in range(3)]
            junk = kb.sb("junk", [128, D])
            hf = [kb.sb(f"hf{i}", [128, D]) for i in range(2)]
            hb = [kb.sb(f"hb{i}", [128, D], BF16) for i in range(2)]
            st = [kb.sb(f"st{i}", [128, 4]) for i in range(2)]
            for i in range(NT):
                x, s, h, hbt = xt[i % 3], st[i % 2], hf[i % 2], hb[i % 2]
                sel = 1 if i < 2 else 0
                src, srcb = x_src(l, i)
                kb.dma("sp" if i % 2 == 0 else "pool", x[:], src, reads=[srcb], writes=[x])
                kb.op("act", lambda e, x=x, s=s: e.activation(junk[:], x[:], AF.Square, accum_out=s[:, 0:1]),
                      reads=[x], writes=[junk, s])
                kb.op("dve", lambda e, s=s: e.tensor_scalar(s[:, 1:2], s[:, 0:1], 1.0 / D, EPS, ALU.mult, ALU.add),
                      reads=[s], writes=[s])
                kb.op("act", lambda e, s=s: e.sqrt(s[:, 2:3], s[:, 1:2]), reads=[s], writes=[s])
                kb.op("dve", lambda e, s=s: e.reciprocal(s[:, 3:4], s[:, 2:3]), reads=[s], writes=[s])
                kb.op("dve", lambda e, x=x, s=s, h=h, sel=sel: e.scalar_tensor_tensor(
                    h[:], x[:], s[:, 3:4], G1[:, sel, :], ALU.mult, ALU.mult), reads=[x, s, G1], writes=[h])
                kb.op("pool", lambda e, h=h, hbt=hbt, sel=sel: e.tensor_tensor(hbt[:], h[:], SH[:, sel, :], ALU.add),
                      reads=[h, SH], writes=[hbt])
                p = PS[i % 4]
                pv = p[:, :].bitcast(BF16)
                for j in range(8):
                    kb.op("pe", lambda e, j=j, pv=pv, hbt=hbt: e.transpose(
                        pv[:, j * 128:(j + 1) * 128], hbt[:, j * 128:(j + 1) * 128], ident_bf[:]),
                        reads=[hbt, ident_bf], writes=[p])
                kb.op("act", lambda e, pv=pv, i=i: e.copy(
                    hT[:, :, i * 128:(i + 1) * 128], pv.rearrange("p (j t) -> p j t", j=8)), reads=[p], writes=[hT])

    def load_w(l, dst, col0, ncols, stage, q="sp"):
        kb.dma(q, stage[:, :, 0:ncols], W["w_in"][l, :, col0:col0 + ncols].rearrange("(j p) n -> p j n", p=128),
               reads=[W["w_in"]], writes=[stage])
        kb.op("pool", lambda e: e.tensor_copy(dst[:, :, 0:ncols], stage[:, :, 0:ncols]), reads=[stage], writes=[dst])

    def proj_fm(p, wt, c0, nc_, t0, nt):
        for j in range(8):
            kb.op("pe", lambda e, j=j: e.matmul(p[0:nc_, 0:nt], wt[:, j, c0:c0 + nc_], hT[:, j, t0:t0 + nt],
                                                start=(j == 0), stop=(j == 7)), reads=[wt, hT], writes=[p])

    def proj_tm(p, wt, c0, nc_, i):
        for j in range(8):
            kb.op("pe", lambda e, j=j: e.matmul(p[:, 0:nc_], hT[:, j, i * 128:(i + 1) * 128], wt[:, j, c0:c0 + nc_],
                                                start=(j == 0), stop=(j == 7)), reads=[wt, hT], writes=[p])

    TCH = [(t0, min(512, T - t0)) for t0 in range(0, T, 512)]

    def qk_prep(l, es_tiles, wt, c0, dst, dst_j, gvec, norm, rope):
        raw, sq, rs, rot = es_tiles
        for ci, (t0, nt) in enumerate(TCH):
            p = PS[ci % 2]
            proj_fm(p, wt, c0, 128, t0, nt)
            if norm:
                kb.op("act", lambda e, p=p, nt=nt: e.activation(sq[:, 0:nt], p[:, 0:nt], AF.Square), reads=[p], writes=[sq])
                p2 = PS[2 + ci % 2]
                kb.op("pe", lambda e, p2=p2, nt=nt: e.matmul(p2[:, 0:nt], blk64[:], sq[:, 0:nt], start=True, stop=True),
                      reads=[blk64, sq], writes=[p2])
                kb.op("dve", lambda e, p2=p2, nt=nt: e.tensor_scalar(rs[:, 0:nt], p2[:, 0:nt], 1.0 / 64, EPS, ALU.mult, ALU.add),
                      reads=[p2], writes=[rs])
                kb.op("act", lambda e, nt=nt: e.sqrt(rs[:, 0:nt], rs[:, 0:nt]), reads=[rs], writes=[rs])
                kb.op("dve", lambda e, nt=nt: e.reciprocal(rs[:, 0:nt], rs[:, 0:nt]), reads=[rs], writes=[rs])
                kb.op("dve", lambda e, p=p, nt=nt: e.scalar_tensor_tensor(
                    raw[:, 0:nt], p[:, 0:nt], gvec[:, 0:1], rs[:, 0:nt], ALU.mult, ALU.mult), reads=[p, gvec, rs], writes=[raw])
            else:
                kb.op("act", lambda e, p=p, nt=nt: e.copy(raw[:, 0:nt], p[:, 0:nt]), reads=[p], writes=[raw])
            lat0 = 0
            if t0 < C:
                lat0 = C - t0
                kb.op("pool", lambda e, t0=t0, lat0=lat0: e.tensor_copy(dst[:, dst_j, t0:t0 + lat0], raw[:, 0:lat0]),
                      reads=[raw], writes=[dst])
            if not rope:
                if nt > lat0:
                    kb.op("pool", lambda e, t0=t0, lat0=lat0, nt=nt: e.tensor_copy(
                        dst[:, dst_j, t0 + lat0:t0 + nt], raw[:, lat0:nt]), reads=[raw], writes=[dst])
                continue
            p3 = PS[4 + ci % 2]
            n_l = nt - lat0
            lp = t0 + lat0 - C
            kb.op("pe", lambda e, p3=p3, lat0=lat0, nt=nt: e.matmul(p3[:, lat0:nt], rope_perm[:], raw[:, lat0:nt], start=True, stop=True),
                  reads=[rope_perm, raw], writes=[p3])
            kb.op("dve", lambda e, p3=p3, lat0=lat0, nt=nt, lp=lp, n_l=n_l: e.tensor_tensor(
                rot[:, lat0:nt], p3[:, lat0:nt], rope_sin[:, lp:lp + n_l], ALU.mult), reads=[p3, rope_sin], writes=[rot])
            kb.op("pool", lambda e, lat0=lat0, nt=nt, lp=lp, n_l=n_l: e.tensor_tensor(
                raw[:, lat0:nt], raw[:, lat0:nt], rope_cos[:, lp:lp + n_l], ALU.mult), reads=[raw, rope_cos], writes=[raw])
            kb.op("dve", lambda e, t0=t0, lat0=lat0, nt=nt: e.tensor_tensor(
                dst[:, dst_j, t0 + lat0:t0 + nt], raw[:, lat0:nt], rot[:, lat0:nt], ALU.add), reads=[raw, rot], writes=[dst])

    rope_cos = rope_sin = rope_perm = None

    def phase_attn(l, dense, with_ctx):
        nonlocal rope_cos, rope_sin, rope_perm
        base = FA0 if dense else WA0
        gbase = FAG0 if dense else WAG0
        mrow = 768 if dense else 512
        with kb.scope():
            wt = kb.sb("wt", [128, 8, 768], BF16)
            gq = kb.sb("gq", [128, 1])
            gk = kb.sb("gk", [128, 1])
            sink = kb.sb("sink", [128, 4])
            if dense:
                for hh in range(2):
                    kb.dma("sp", gq[hh * 64:(hh + 1) * 64, :], W["fa_q_norm"][l, :].rearrange("(d o) -> d o", o=1),
                           reads=[W["fa_q_norm"]], writes=[gq], slow=True)
                    kb.dma("sp", gk[hh * 64:(hh + 1) * 64, :], W["fa_k_norm"][l, :].rearrange("(d o) -> d o", o=1),
                           reads=[W["fa_k_norm"]], writes=[gk], slow=True)
            else:
                kb.dma("sp", sink[:], W["wa_sink"][l, :].partition_broadcast(128), reads=[W["wa_sink"]], writes=[sink])
            QT = kb.sb("QT", [128, 2, T], BF16)
            KT = kb.sb("KT", [128, 1, T], BF16)
            VW = 65 if dense else 64
            Vt = kb.sb("Vt", [128, NT, 2, VW], BF16)
            SG = None
            with kb.scope():
                rope_cos = kb.sb("rope_cos", [128, L])
                rope_sin = kb.sb("rope_sin", [128, L])
                rope_perm = kb.sb("rope_perm", [128, 128])
                kb.dma("sp", rope_cos[:], CT["rope_cos"][:, :], reads=[CT["rope_cos"]], writes=[rope_cos])
                kb.dma("pool", rope_sin[:], CT["rope_sin"][:, :], reads=[CT["rope_sin"]], writes=[rope_sin])
                kb.dma("sp", rope_perm[:], CT["rope_perm"][:, :], reads=[CT["rope_perm"]], writes=[rope_perm])
                stage = kb.sb("wstage", [128, 8, 256])
                w4 = W["w_in"][l, :, base:base + 256].rearrange("(j p) (h d) -> p j h d", p=128, d=64)
                st4 = stage[:, :, 0:256].rearrange("p j (h d) -> p j h d", d=64)
                for hi, h in enumerate((0, 2, 1, 3)):
                    kb.dma("sp", st4[:, :, hi, :], w4[:, :, h, :], reads=[W["w_in"]], writes=[stage])
                kb.op("pool", lambda e: e.tensor_copy(wt[:, :, 0:256], stage[:]), reads=[stage], writes=[wt])
                kb.dma("pool", stage[:], W["w_in"][l, :, base + 256:base + 512].rearrange("(j p) n -> p j n", p=128),
                       reads=[W["w_in"]], writes=[stage])
                kb.op("pool", lambda e: e.tensor_copy(wt[:, :, 256:512], stage[:]), reads=[stage], writes=[wt])
                kb.dma("sp", stage[:], W["w_in"][l, :, gbase:gbase + 256].rearrange("(j p) n -> p j n", p=128),
                       reads=[W["w_in"]], writes=[stage])
                kb.op("pool", lambda e: e.tensor_copy(wt[:, :, 512:768], stage[:]), reads=[stage], writes=[wt])
                tl = (kb.sb("qraw", [128, 512]), kb.sb("qsq", [128, 512]), kb.sb("qrs", [128, 512]), kb.sb("qrot", [128, 512]))
                qk_prep(l, tl, wt, 0, QT, 0, gq, dense, True)
                qk_prep(l, tl, wt, 128, QT, 1, gq, dense, True)
                qk_prep(l, tl, wt, 256, KT, 0, gk, dense, True)
            if dense:
                kb.op("pool", lambda e: e.memset(Vt[:, :, :, 64:65], 1.0), writes=[Vt])
            for i in range(NT):
                p = PS[i % 2]
                proj_tm(p, wt, 384, 128, i)
                kb.op("act", lambda e, p=p, i=i: e.copy(Vt[:, i, :, 0:64], p[:, 0:128].rearrange("p (k d) -> p k d", d=64)),
                      reads=[p], writes=[Vt])
            with kb.scope():
                if dense:
                    attn_dense(l, wt, QT, KT, Vt, mrow, with_ctx)
                else:
                    attn_window(l, wt, QT, KT, Vt, sink, mrow, with_ctx)

    def attn_dense(l, wt, QT, KT, Vt, mrow, with_ctx):
        pt = [kb.sb(f"pt{i}", [128, 512], BF16) for i in range(3)]
        osb = [kb.sb(f"osb{i}", [128, 512]) for i in range(2)]
        rc = [kb.sb(f"rc{i}", [128, 512]) for i in range(2)]
        ob = [kb.sb(f"ob{i}", [128, 512], BF16) for i in range(2)]
        it = 0
        sgt = [kb.sb(f"sgt{i}", [64, 512], BF16) for i in range(2)]
        chunks = []
        if with_ctx:
            chunks.append((0, C, 0, 2))
        for t0 in range(C, T, 512):
            chunks.append((t0, 512, 0, NT))
        for h in range(4):
            kv, pr = h // 2, h % 2
            ks = slice(64 * kv, 64 * kv + 64)
            for (t0, nt, kb0, kb1) in chunks:
                po = PS[4 + it % 2]
                sg = sgt[it % 2]
                pg = PS[3]
                for j in range(8):
                    kb.op("pe", lambda e, j=j: e.matmul(
                        pg[0:64, 0:nt], wt[:, j, 512 + 64 * h:576 + 64 * h], hT[:, j, t0:t0 + nt], start=(j == 0), stop=(j == 7)),
                        reads=[wt, hT], writes=[pg])
                kb.op("act", lambda e: e.activation(sg[0:64, 0:nt], pg[0:64, 0:nt], AF.Silu), reads=[pg], writes=[sg])
                def pv_(kbi):
                    ptt = pt[kbi % 3]
                    kb.op("pe", lambda e: e.matmul(
                        po[0:65, 0:nt], Vt[:, kbi, kv, 0:65], ptt[:, 0:nt], start=(kbi == kb0), stop=(kbi == kb1 - 1)),
                        reads=[Vt, ptt], writes=[po])
                for kbi in range(kb0, kb1):
                    psS = PS[kbi % 3]
                    ptt = pt[kbi % 3]
                    kb.op("pe", lambda e, psS=psS, kbi=kbi: e.matmul(
                        psS[:, 0:nt], KT[ks, 0, kbi * 128:(kbi + 1) * 128], QT[ks, pr, t0:t0 + nt], start=True, stop=True),
                        reads=[KT, QT], writes=[psS])
                    kb.op("act", lambda e, psS=psS, ptt=ptt: e.activation(ptt[:, 0:nt], psS[:, 0:nt], AF.Exp, scale=0.125),
                          reads=[psS], writes=[ptt])
                    if kbi - 1 >= kb0:
                        pv_(kbi - 1)
                pv_(kb1 - 1)
                o_s, r_c, o_b = osb[it % 2], rc[it % 2], ob[it % 2]
                kb.op("dve", lambda e: e.reciprocal(r_c[64:65, 0:nt], po[64:65, 0:nt]), reads=[po], writes=[r_c])
                kb.op("act", lambda e: e.copy(o_s[0:64, 0:nt], po[0:64, 0:nt]), reads=[po], writes=[o_s])
                pb = PS[6 + it % 2]
                kb.op("pe", lambda e: e.matmul(pb[0:64, 0:nt], ones_f[64:65, 0:64], r_c[64:65, 0:nt], start=True, stop=True),
                      reads=[ones_f, r_c], writes=[pb])
                kb.op("dve", lambda e: e.tensor_tensor(o_s[0:64, 0:nt], o_s[0:64, 0:nt], pb[0:64, 0:nt], ALU.mult),
                      reads=[o_s, pb], writes=[o_s])
                kb.op("pool", lambda e: e.tensor_tensor(o_b[0:64, 0:nt], o_s[0:64, 0:nt], sg[0:64, 0:nt], ALU.mult),
                      reads=[o_s, sg], writes=[o_b])
                kb.dma("pool", mixT[mrow + 64 * h:mrow + 64 * h + 64, t0:t0 + nt], o_b[0:64, 0:nt], reads=[o_b], writes=[Buf()])
                it += 1

    def attn_window(l, wt, QT, KT, Vt, sink, mrow, with_ctx):
        wmask = kb.sb("wmask", [128, 384])
        kb.dma("sp", wmask[:], CT["wmask"][:, :], reads=[CT["wmask"]], writes=[wmask])
        nsink = kb.sb("nsink", [128, 4])
        kb.op("dve", lambda e: e.tensor_scalar(nsink[:], sink[:], -1.0, None, ALU.mult), reads=[sink], writes=[nsink])
        S = [kb.sb(f"wS{i}", [128, 640]) for i in range(2)]
        P = [kb.sb(f"wP{i}", [128, 640]) for i in range(2)]
        Pn = [kb.sb(f"wPn{i}", [128, 640], BF16) for i in range(2)]
        PT = [kb.sb(f"wPT{i}", [128, 640], BF16) for i in range(2)]
        st = [kb.sb(f"wst{i}", [128, 8]) for i in range(2)]
        sgt = [kb.sb(f"wsg{i}", [64, 128], BF16) for i in range(2)]
        ob = [kb.sb(f"wob{i}", [64, 128], BF16) for i in range(2)]
        it = 0
        for i in range(0 if with_ctx else 2, NT):
            if i < 2:
                loc = []
            else:
                loc = list(range(max(2, i - 1), min(NT - 1, i + 1) + 1))
            nl = 128 * len(loc)
            m0 = 128 if (i >= 2 and i - 1 < 2) else 0
            nk = nl + C
            ktiles = loc + [0, 1]
            for h in range(4):
                kv, pr = h // 2, h % 2
                ks = slice(64 * kv, 64 * kv + 64)
                s_, p_, pn_, pt_, st_, sg, o_b = S[it % 2], P[it % 2], Pn[it % 2], PT[it % 2], st[it % 2], sgt[it % 2], ob[it % 2]
                psA, psB, psT, psO, psG = PS[it % 2], PS[2 + it % 2], PS[4 + it % 2], PS[6], PS[7]
                q_ap = QT[ks, pr, i * 128:(i + 1) * 128]
                if nl:
                    k0 = loc[0] * 128
                    kb.op("pe", lambda e: e.matmul(psA[:, 0:nl], q_ap, KT[ks, 0, k0:k0 + nl], start=True, stop=True),
                          reads=[QT, KT], writes=[psA])
                    kb.op("dve", lambda e: e.tensor_tensor(s_[:, 0:nl], psA[:, 0:nl], wmask[:, m0:m0 + nl], ALU.add),
                          reads=[psA, wmask], writes=[s_])
                kb.op("pe", lambda e: e.matmul(psB[:, 0:C], q_ap, KT[ks, 0, 0:C], start=True, stop=True),
                      reads=[QT, KT], writes=[psB])
                kb.op("act", lambda e: e.copy(s_[:, nl:nk], psB[:, 0:C]), reads=[psB], writes=[s_])
                kb.op("dve", lambda e: e.reduce_max(st_[:, 0:1], s_[:, 0:nk], AX.X), reads=[s_], writes=[st_])
                kb.op("dve", lambda e: e.tensor_scalar(st_[:, 1:2], st_[:, 0:1], -0.125, nsink[:, h:h + 1], ALU.mult, ALU.min),
                      reads=[st_, nsink], writes=[st_])
                kb.op("act", lambda e: e.activation(p_[:, 0:nk], s_[:, 0:nk], AF.Exp, bias=st_[:, 1:2], scale=0.125,
                                                    accum_out=st_[:, 2:3]), reads=[s_, st_], writes=[p_, st_])
                kb.op("act", lambda e: e.activation(st_[:, 3:4], sink[:, h:h + 1], AF.Exp, bias=st_[:, 1:2], scale=1.0),
                      reads=[sink, st_], writes=[st_])
                kb.op("dve", lambda e: e.tensor_tensor(st_[:, 4:5], st_[:, 2:3], st_[:, 3:4], ALU.add), reads=[st_], writes=[st_])
                kb.op("dve", lambda e: e.reciprocal(st_[:, 5:6], st_[:, 4:5]), reads=[st_], writes=[st_])
                kb.op("dve", lambda e: e.tensor_scalar(pn_[:, 0:nk], p_[:, 0:nk], st_[:, 5:6], None, ALU.mult),
                      reads=[p_, st_], writes=[pn_])
                pv = psT[:, :].bitcast(BF16)
                nb = nk // 128
                for b in range(nb):
                    kb.op("pe", lambda e, b=b: e.transpose(pv[:, b * 128:(b + 1) * 128], pn_[:, b * 128:(b + 1) * 128], ident_bf[:]),
                          reads=[pn_, ident_bf], writes=[psT])
                kb.op("act", lambda e: e.copy(pt_[:, 0:nk], pv[:, 0:nk]), reads=[psT], writes=[pt_])
                for b in range(nb):
                    kb.op("pe", lambda e, b=b: e.matmul(psO[0:64, 0:128], Vt[:, ktiles[b], kv, 0:64], pt_[:, b * 128:(b + 1) * 128],
                                                        start=(b == 0), stop=(b == nb - 1)), reads=[Vt, pt_], writes=[psO])
                for j in range(8):
                    kb.op("pe", lambda e, j=j: e.matmul(
                        psG[0:64, 0:128], wt[:, j, 512 + 64 * h:576 + 64 * h], hT[:, j, i * 128:(i + 1) * 128],
                        start=(j == 0), stop=(j == 7)), reads=[wt, hT], writes=[psG])
                kb.op("act", lambda e: e.activation(sg[:, :], psG[0:64, 0:128], AF.Silu), reads=[psG], writes=[sg])
                kb.op("dve", lambda e: e.tensor_tensor(o_b[:, :], psO[0:64, 0:128], sg[:, :], ALU.mult), reads=[psO, sg], writes=[o_b])
                kb.dma("sp", mixT[mrow + 64 * h:mrow + 64 * h + 64, i * 128:(i + 1) * 128], o_b[:, :], reads=[o_b], writes=[Buf()])
                it += 1

    def phase_out(l, last):
        with kb.scope():
            wo = kb.sb("wo", [128, 8, D], BF16)
            stage = kb.sb("wostage", [128, 8, 256])
            for q in range(4):
                kb.dma("sp", stage[:], W["w_out"][l, :, q * 256:(q + 1) * 256].rearrange("(j p) n -> p j n", p=128),
                       reads=[W["w_out"]], writes=[stage])
                kb.op("pool", lambda e, q=q: e.tensor_copy(wo[:, :, q * 256:(q + 1) * 256], stage[:]), reads=[stage], writes=[wo])
            fg = kb.sb("fg", [128, D])
            if last:
                kb.dma("sp", fg[:], W["final_g"][:].partition_broadcast(128), reads=[W["final_g"]], writes=[fg])
            mt = [kb.sb(f"mt{i}", [128, 8, 128], BF16) for i in range(2)]
            xt = [kb.sb(f"oxt{i}", [128, D]) for i in range(2)]
            xn = [kb.sb(f"oxn{i}", [128, D]) for i in range(2)]
            tmp = [kb.sb(f"otmp{i}", [128, 512]) for i in range(2)]
            st = [kb.sb(f"ost{i}", [128, 4]) for i in range(2)]
            junk = kb.sb("ojunk", [128, D])
            mixv = mixT.t.rearrange("(j p) t -> p j t", p=128)
            for it, i in enumerate(range(2 if last else 0, NT)):
                m, x, xo, s = mt[it % 2], xt[it % 2], xn[it % 2], st[it % 2]
                sel = 1 if i < 2 else 0
                kb.dma("sp", m[:], mixv[:, :, i * 128:(i + 1) * 128], reads=[mixT], writes=[m])
                src, srcb = x_src(l, i)
                kb.dma("pool", x[:], src, reads=[srcb], writes=[x])
                for hf in range(2):
                    p = PS[(2 * it + hf) % 8]
                    tp = tmp[hf]
                    for j in range(8):
                        kb.op("pe", lambda e, j=j, p=p, m=m, hf=hf: e.matmul(p[:, :], m[:, j, :], wo[:, j, hf * 512:(hf + 1) * 512],
                                                                     start=(j == 0), stop=(j == 7)), reads=[m, wo], writes=[p])
                    kb.op("dve", lambda e, p=p, tp=tp, hf=hf, sel=sel: e.tensor_tensor(
                        tp[:], p[:, :], GT[:, sel, hf * 512:(hf + 1) * 512], ALU.mult), reads=[p, GT], writes=[tp])
                    kb.op("pool", lambda e, tp=tp, hf=hf, x=x, xo=xo: e.tensor_tensor(
                        xo[:, hf * 512:(hf + 1) * 512], x[:, hf * 512:(hf + 1) * 512], tp[:], ALU.add), reads=[x, tp], writes=[xo])
                if not last:
                    kb.dma("sp", xres[i * 128:(i + 1) * 128, :], xo[:], reads=[xo], writes=[xres_b[i]])
                else:
                    kb.op("act", lambda e, xo=xo, s=s: e.activation(junk[:], xo[:], AF.Square, accum_out=s[:, 0:1]),
                          reads=[xo], writes=[junk, s])
                    kb.op("dve", lambda e, s=s: e.tensor_scalar(s[:, 1:2], s[:, 0:1], 1.0 / D, EPS, ALU.mult, ALU.add),
                          reads=[s], writes=[s])
                    kb.op("act", lambda e, s=s: e.sqrt(s[:, 2:3], s[:, 1:2]), reads=[s], writes=[s])
                    kb.op("dve", lambda e, s=s: e.reciprocal(s[:, 3:4], s[:, 2:3]), reads=[s], writes=[s])
                    kb.op("dve", lambda e, xo=xo, s=s, x=x: e.scalar_tensor_tensor(
                        x[:], xo[:], s[:, 3:4], fg[:], ALU.mult, ALU.mult), reads=[xo, s, fg], writes=[x])
                    kb.dma("sp", out[(i - 2) * 128:(i - 1) * 128, :], x[:], reads=[x], writes=[Buf()])


    def conv_tile(l, wt, cw, ncw, jt, Zraw, Zout):
        for ci, (t0, nt) in enumerate(TCH):
            p = PS[ci % 4]
            proj_fm(p, wt, 0, 128, t0, nt)
            kb.op("act", lambda e, p=p, t0=t0, nt=nt: e.copy(Zraw[:, 1 + t0:1 + t0 + nt], p[:, 0:nt]), reads=[p], writes=[Zraw])
        kb.op("dve", lambda e: e.tensor_scalar(Zout[:, :], Zraw[:, 1:T + 1], cw[:, jt, 1:2], None, ALU.mult), reads=[Zraw, cw], writes=[Zout])
        kb.op("dve", lambda e: e.scalar_tensor_tensor(Zout[:, :], Zraw[:, 0:T], cw[:, jt, 0:1], Zout[:, :], ALU.mult, ALU.add),
              reads=[Zraw, cw, Zout], writes=[Zout])
        kb.op("dve", lambda e: e.scalar_tensor_tensor(Zout[:, :], Zraw[:, 2:T + 2], cw[:, jt, 2:3], Zout[:, :], ALU.mult, ALU.add),
              reads=[Zraw, cw, Zout], writes=[Zout])
        kb.op("dve", lambda e: e.scalar_tensor_tensor(Zout[:, C - 1:C], Zraw[:, C + 1:C + 2], ncw[:, jt, 2:3], Zout[:, C - 1:C], ALU.mult, ALU.add),
              reads=[Zraw, ncw, Zout], writes=[Zout])
        kb.op("dve", lambda e: e.scalar_tensor_tensor(Zout[:, C:C + 1], Zraw[:, C:C + 1], ncw[:, jt, 0:1], Zout[:, C:C + 1], ALU.mult, ALU.add),
              reads=[Zraw, ncw, Zout], writes=[Zout])

    def colvec(name, src_ap, srcb, shape, rearr, **kw):
        t = kb.sb(name, shape)
        kb.dma("sp", t[:], src_ap.rearrange(rearr, **kw), reads=[srcb], writes=[t], slow=True)
        return t

    def phase_rwkv_prep(l):
        with kb.scope():
            stage = kb.sb("rstage", [128, 8, 128])
            wts = [kb.sb(f"rwt{i}", [128, 8, 128], BF16) for i in range(2)]
            cw = kb.sb("rcw", [128, 7, 3])
            for k in range(3):
                kb.dma("sp", cw[:, :, k], W["rw_conv"][l, k, :].rearrange("(j p) -> p j", p=128), reads=[W["rw_conv"]], writes=[cw], slow=True)
            ncw = kb.sb("rncw", [128, 7, 3])
            kb.op("dve", lambda e: e.tensor_scalar(ncw[:], cw[:], -1.0, None, ALU.mult), reads=[cw], writes=[ncw])
            kk_ = colvec("rkk", W["rw_k_k"][l, :], W["rw_k_k"], [128, 2], "(j p) -> p j", p=128)
            ka_ = colvec("rka", W["rw_k_a"][l, :], W["rw_k_a"], [128, 2], "(j p) -> p j", p=128)
            omka = kb.sb("romka", [128, 2])
            kb.op("dve", lambda e: e.tensor_scalar(omka[:], ka_[:], -1.0, 1.0, ALU.mult, ALU.add), reads=[ka_], writes=[omka])
            w0_ = kb.sb("rw0", [128, 2, 2])
            a0_ = kb.sb("ra0", [128, 2, 2])
            for d in range(2):
                kb.dma("sp", w0_[:, d, :], W["rw_w0"][l, d, :].rearrange("(j p) -> p j", p=128), reads=[W["rw_w0"]], writes=[w0_], slow=True)
                kb.dma("sp", a0_[:, d, :], W["rw_a0"][l, d, :].rearrange("(j p) -> p j", p=128), reads=[W["rw_a0"]], writes=[a0_], slow=True)
            wup = kb.sb("rwup", [128, 2, 256])
            kb.dma("sp", wup[0:64, :, :], W["rw_w_up"][l, :, :, :].rearrange("d k n -> k d n"), reads=[W["rw_w_up"]], writes=[wup])
            kb.dma("sp", wup[64:128, :, :], W["rw_a_up"][l, :, :, :].rearrange("d k n -> k d n"), reads=[W["rw_a_up"]], writes=[wup])
            Zraw = kb.sb("rZraw", [128, T + 2])
            Zout = kb.sb("rZout", [128, T])
            Z6 = kb.sb("rZ6", [128, T])
            kb.op("pool", lambda e: e.memset(Zraw[:, 0:1], 0.0), writes=[Zraw])
            kb.op("pool", lambda e: e.memset(Zraw[:, T + 1:T + 2], 0.0), writes=[Zraw])
            tA = [kb.sb(f"rtA{i}", [128, 512]) for i in range(2)]
            tB = [kb.sb(f"rtB{i}", [128, 512]) for i in range(2)]
            tC = [kb.sb(f"rtC{i}", [128, 512]) for i in range(2)]
            tD = [kb.sb(f"rtD{i}", [128, 512]) for i in range(2)]
            tE = [kb.sb(f"rtE{i}", [128, 512]) for i in range(2)]
            tG = [kb.sb(f"rtG{i}", [128, 512], BF16) for i in range(2)]
            vt_ = [kb.sb(f"rvt{i}", [128, 128]) for i in range(2)]
            order = [6, 0, 1, 4, 5, 2, 3, 7, 8]
            for oi, jt in enumerate(order):
                wt = wts[oi % 2]
                c0 = RW0 + jt * 128 if jt < 7 else RWG0 + (jt - 7) * 128
                load_w(l, wt, c0, 128, stage)
                if jt >= 7:
                    for ci, (t0, nt) in enumerate(TCH):
                        p = PS[ci % 4]
                        proj_fm(p, wt, 0, 128, t0, nt)
                        g = tG[ci % 2]
                        kb.op("act", lambda e, p=p, g=g, nt=nt: e.activation(g[:, 0:nt], p[:, 0:nt], AF.Silu), reads=[p], writes=[g])
                        kb.dma("sp", RS["SGT"][(jt - 7) * 128:(jt - 6) * 128, t0:t0 + nt], g[:, 0:nt], reads=[g], writes=[Buf()])
                    continue
                conv_tile(l, wt, cw, ncw, jt, Zraw, Z6 if jt == 6 else Zout)
                if jt == 6:
                    kb.op("act", lambda e: e.activation(Z6[0:64, :], Z6[0:64, :], AF.Tanh), reads=[Z6], writes=[Z6])
                elif jt in (0, 1):
                    kb.dma("sp", RS["RT"][jt * 128:(jt + 1) * 128, :], Zout[:, :], reads=[Zout], writes=[Buf()])
                elif jt in (4, 5):
                    kb.dma("sp", RS["VT"][(jt - 4) * 128:(jt - 3) * 128, :], Zout[:, :], reads=[Zout], writes=[Buf()])
                    for i in range(NT):
                        p = PS[4 + i % 2]
                        kb.op("pe", lambda e, p=p, i=i: e.transpose(p[:, 0:128], Zout[:, i * 128:(i + 1) * 128], ident_f[:]),
                              reads=[Zout, ident_f], writes=[p])
                        v = vt_[i % 2]
                        kb.op("act", lambda e, p=p, v=v: e.copy(v[:, :], p[:, 0:128]), reads=[p], writes=[v])
                        kb.dma("pool", RS["VTOK"][i * 128:(i + 1) * 128, (jt - 4) * 128:(jt - 3) * 128], v[:, :], reads=[v], writes=[Buf()])
                else:
                    pt = jt - 2
                    rows = slice(pt * 128, (pt + 1) * 128)
                    for ci, (t0, nt) in enumerate(TCH):
                        a_, b_, c_, d_, e_ = tA[ci % 2], tB[ci % 2], tC[ci % 2], tD[ci % 2], tE[ci % 2]
                        zc = Zout[:, t0:t0 + nt]
                        kb.op("dve", lambda e: e.tensor_scalar(a_[:, 0:nt], zc, kk_[:, pt:pt + 1], None, ALU.mult), reads=[Zout, kk_], writes=[a_])
                        kb.op("act", lambda e: e.activation(b_[:, 0:nt], a_[:, 0:nt], AF.Square), reads=[a_], writes=[b_])
                        p = PS[ci % 2]
                        kb.op("pe", lambda e: e.matmul(p[:, 0:nt], blk64[:], b_[:, 0:nt], start=True, stop=True), reads=[blk64, b_], writes=[p])
                        kb.op("act", lambda e: e.sqrt(b_[:, 0:nt], p[:, 0:nt]), reads=[p], writes=[b_])
                        kb.op("dve", lambda e: e.tensor_scalar(b_[:, 0:nt], b_[:, 0:nt], 1e-12, None, ALU.max), reads=[b_], writes=[b_])
                        kb.op("dve", lambda e: e.reciprocal(b_[:, 0:nt], b_[:, 0:nt]), reads=[b_], writes=[b_])
                        kb.op("dve", lambda e: e.scalar_tensor_tensor(a_[:, 0:nt], a_[:, 0:nt], -1.0, b_[:, 0:nt], ALU.mult, ALU.mult),
                              reads=[a_, b_], writes=[a_])
                        kb.dma("sp", RS["AL"][rows, t0:t0 + nt], a_[:, 0:nt], reads=[a_], writes=[Buf()])
                        for d in range(2):
                            pa = PS[2 + d]
                            kb.op("pe", lambda e: e.matmul(pa[:, 0:nt], wup[64:128, d, pt * 128:(pt + 1) * 128], Z6[64:128, t0:t0 + nt],
                                                           start=True, stop=True), reads=[wup, Z6], writes=[pa])
                            kb.op("act", lambda e: e.activation(c_[:, 0:nt], pa[:, 0:nt], AF.Sigmoid, bias=a0_[:, d, pt:pt + 1]),
                                  reads=[pa, a0_], writes=[c_])
                            kb.op("dve", lambda e: e.scalar_tensor_tensor(d_[:, 0:nt], c_[:, 0:nt], -1.0, a_[:, 0:nt], ALU.mult, ALU.mult),
                                  reads=[c_, a_], writes=[d_])
                            kb.dma("sp", RS[f"B{d}"][rows, t0:t0 + nt], d_[:, 0:nt], reads=[d_], writes=[Buf()])
                            kb.op("dve", lambda e: e.tensor_scalar(c_[:, 0:nt], c_[:, 0:nt], ka_[:, pt:pt + 1], omka[:, pt:pt + 1], ALU.mult, ALU.add),
                                  reads=[c_, ka_, omka], writes=[c_])
                            kb.op("dve", lambda e: e.tensor_tensor(e_[:, 0:nt], c_[:, 0:nt], zc, ALU.mult), reads=[c_, Zout], writes=[e_])
                            kb.dma("pool", RS[f"KD{d}"][rows, t0:t0 + nt], e_[:, 0:nt], reads=[e_], writes=[Buf()])
                            pw = PS[4 + d]
                            kb.op("pe", lambda e: e.matmul(pw[:, 0:nt], wup[0:64, d, pt * 128:(pt + 1) * 128], Z6[0:64, t0:t0 + nt],
                                                           start=True, stop=True), reads=[wup, Z6], writes=[pw])
                            kb.op("act", lambda e: e.activation(c_[:, 0:nt], pw[:, 0:nt], AF.Sigmoid, bias=w0_[:, d, pt:pt + 1]),
                                  reads=[pw, w0_], writes=[c_])
                            kb.op("dve", lambda e: e.tensor_scalar(d_[:, 0:nt], c_[:, 0:nt], -math.exp(-0.5), None, ALU.mult),
                                  reads=[c_], writes=[d_])
                            kb.dma("pool", RS[f"W{d}"][rows, t0:t0 + nt], d_[:, 0:nt], reads=[d_], writes=[Buf()])

    def phase_rwkv_scan(l):
        with kb.scope():
            ST = [kb.sb(f"ST{d}", [128, 2, 64]) for d in range(2)]
            for d in range(2):
                kb.op("pool", lambda e, d=d: e.memset(ST[d][:], 0.0), writes=[ST[d]])
            names = ("AL", "W", "B", "KD", "RT")
            ch = [[{n: kb.sb(f"c{n}{d}{i}", [128, 2, 128]) for n in names} for i in range(2)] for d in range(2)]
            vch = [[kb.sb(f"cV{d}{i}", [128, 256]) for i in range(2)] for d in range(2)]
            t1 = [kb.sb(f"st1{d}", [128, 2, 64]) for d in range(2)]
            t2 = [kb.sb(f"st2{d}", [128, 2, 64]) for d in range(2)]
            ysb = [kb.sb(f"ysb{d}", [64, 512]) for d in range(2)]
            psSA, psV, psY = [PS[0], PS[1]], [PS[2], PS[3]], [PS[4], PS[5]]
            border = [1, 0] + list(range(NT - 1, 1, -1))
            for ci in range(NT):
                cidx = [ci, border[ci]]
                cur = []
                for d in range(2):
                    c0 = cidx[d] * 128
                    tl_ = ch[d][ci % 2]
                    for n in names:
                        src = RS[n if n in ("AL", "RT") else f"{n}{d}"]
                        kb.dma("sp" if d == 0 else "pool", tl_[n][:],
                               src.t.rearrange("(pr q) t -> q pr t", q=128)[:, :, c0:c0 + 128], reads=[src], writes=[tl_[n]])
                    vv = vch[d][ci % 2]
                    kb.dma("sp" if d == 0 else "pool", vv[:], RS["VTOK"][c0:c0 + 128, :], reads=[RS["VTOK"]], writes=[vv])
                    cur.append((tl_, vv))
                for tl in range(128):
                    for d in range(2):
                        col = tl if d == 0 else 127 - tl
                        tl_, vv = cur[d]
                        S_, sa, pv, py = ST[d], psSA[d], psV[d], psY[d]
                        for pr in range(2):
                            for hp in range(2):
                                rows = slice(64 * hp, 64 * hp + 64)
                                kb.op("pe", lambda e, pr=pr, rows=rows: e.matmul(
                                    sa[rows, pr * 64:(pr + 1) * 64], tl_["AL"][rows, pr, col:col + 1].broadcast_to([64, 64]),
                                    S_[rows, pr, :], start=True, stop=True), reads=[tl_["AL"], S_], writes=[sa])
                        for pr in range(2):
                            for hp in range(2):
                                rows = slice(64 * hp, 64 * hp + 64)
                                h = 2 * pr + hp
                                kb.op("pe", lambda e, pr=pr, rows=rows, h=h: e.matmul(
                                    pv[rows, pr * 64:(pr + 1) * 64], ident_f[:, col:col + 1].broadcast_to([128, 64]),
                                    vv[:, h * 64:(h + 1) * 64], start=True, stop=True), reads=[ident_f, vv], writes=[pv])
                        for pr in range(2):
                            kb.op("dve", lambda e, pr=pr: e.tensor_scalar(
                                t1[d][:, pr, :], sa[:, pr * 64:(pr + 1) * 64], tl_["B"][:, pr, col:col + 1], None, ALU.mult),
                                reads=[sa, tl_["B"]], writes=[t1[d]])
                            kb.op("dve", lambda e, pr=pr: e.scalar_tensor_tensor(
                                t2[d][:, pr, :], pv[:, pr * 64:(pr + 1) * 64], tl_["KD"][:, pr, col:col + 1], t1[d][:, pr, :], ALU.mult, ALU.add),
                                reads=[pv, tl_["KD"], t1[d]], writes=[t2[d]])
                            kb.op("dve", lambda e, pr=pr: e.scalar_tensor_tensor(
                                S_[:, pr, :], S_[:, pr, :], tl_["W"][:, pr, col:col + 1], t2[d][:, pr, :], ALU.mult, ALU.add),
                                reads=[S_, tl_["W"], t2[d]], writes=[S_])
                        for pr in range(2):
                            for hp in range(2):
                                rows = slice(64 * hp, 64 * hp + 64)
                                h = 2 * pr + hp
                                kb.op("pe", lambda e, pr=pr, rows=rows, h=h: e.matmul(
                                    py[0:64, h * 128 + col:h * 128 + col + 1], S_[rows, pr, :], tl_["RT"][rows, pr, col:col + 1],
                                    start=True, stop=True), reads=[S_, tl_["RT"]], writes=[py])
                for d in range(2):
                    c0 = cidx[d] * 128
                    kb.op("act", lambda e, d=d: e.copy(ysb[d][:, :], psY[d][0:64, :]), reads=[psY[d]], writes=[ysb[d]])
                    dst = RS["YF" if d == 0 else "YB"]
                    kb.dma("sp", dst.t.rearrange("(h v) t -> v h t", v=64)[:, :, c0:c0 + 128],
                           ysb[d][:, :].rearrange("v (h t) -> v h t", h=4), reads=[ysb[d]], writes=[Buf()])


    def phase_rwkv_chunked(l):
        CH = 64
        NCH = T // CH
        with kb.scope():
            def ldc(nm, shape):
                t = kb.sb("k" + nm, shape)
                kb.dma("sp", t[:], CT[nm].t, reads=[CT[nm]], writes=[t])
                return t
            Ms = ldc("rw_ms", [128, 2, 64]); MTs = ldc("rw_mts", [128, 2, 64]); MTi = ldc("rw_mti", [128, 2, 64])
            id2 = ldc("rw_id2", [128, 64])
            ones = kb.sb("rones", [128, 64])
            kb.op("pool", lambda e: e.memset(ones[:], 1.0), writes=[ones])
            ST = kb.sb("cST", [128, 4, 64])
            kb.op("pool", lambda e: e.memset(ST[:], 0.0), writes=[ST])
            names = ("AL", "W", "B", "KD", "RT")
            def t4(nm, n=2, w=64):
                return [kb.sb(f"{nm}{i}", [128, 4, w]) for i in range(n)]
            IN = {n: t4("ci" + n) for n in names}
            VTK = t4("cVTK")
            CS = t4("cCS", 1)[0]; TOT = kb.sb("cTOT", [128, 4]); TMP = t4("cTMP", 1)[0]
            Epos = t4("cEp", 1)[0]; Eneg = t4("cEn", 1)[0]; Eprev = t4("cEv", 1)[0]; Etot = t4("cEt", 1)[0]; Wtot = kb.sb("cWt", [128, 4])
            Ab = t4("cAb", 1)[0]; Bb = t4("cBb", 1)[0]; Kb = t4("cKb", 1)[0]; Rb = t4("cRb", 1)[0]; Bt = t4("cBt", 1)[0]; Kt = t4("cKt", 1)[0]
            Q = t4("cQ"); P = t4("cP"); ArbT = t4("cArbT", 1)[0]; AkvT = t4("cAkvT", 1)[0]; ArkT = t4("cArkT", 1)[0]
            X = t4("cX", 2, 128); Btok = t4("cBtok", 1)[0]; Ktok = t4("cKtok", 1)[0]
            RAT = t4("cRAT", 1)[0]; McT = t4("cMcT", 1)[0]; NcS = t4("cNcS", 1)[0]; DG = t4("cDG", 1)[0]
            ysb = [kb.sb(f"cysb{d}", [64, 256]) for d in range(2)]
            border = [3, 2, 1, 0] + list(range(NCH - 1, 3, -1))
            DP = [(d, pr) for d in range(2) for pr in range(2)]
            HP = [slice(0, 64), slice(64, 128)]

            def mm_all(ps, col_fn, lhs_fn, rhs_fn, reads, start=True, stop=True, w=None):
                for dp in range(4):
                    for hp in range(2):
                        r = HP[hp]
                        c0, c1 = col_fn(dp)
                        kb.op("pe", lambda e, dp=dp, r=r, c0=c0, c1=c1: e.matmul(ps[r, c0:c1], lhs_fn(dp, r), rhs_fn(dp, r), start=start, stop=stop),
                              reads=reads, writes=[ps])

            for ci in range(NCH):
                cidx = [ci, border[ci]]
                i2 = ci % 2
                for d in range(2):
                    c0 = cidx[d] * CH
                    for n in names:
                        src = RS[n if n in ("AL", "RT") else f"{n}{d}"]
                        kb.dma("sp" if d == 0 else "pool", IN[n][i2][:, 2 * d:2 * d + 2, :],
                               src.t.rearrange("(pr q) t -> q pr t", q=128)[:, :, c0:c0 + CH], reads=[src], writes=[IN[n][i2]])
                    for hp in range(2):
                        kb.dma("sp" if d == 0 else "pool", VTK[i2][HP[hp], 2 * d:2 * d + 2, :],
                               RS["VTOK"][c0:c0 + CH, :].rearrange("t (pr hp v) -> t pr hp v", pr=2, hp=2)[:, :, hp, :],
                               reads=[RS["VTOK"]], writes=[VTK[i2]])
                al, lw, be, kd, rt, vt = IN["AL"][i2], IN["W"][i2], IN["B"][i2], IN["KD"][i2], IN["RT"][i2], VTK[i2]
                if RW_STAGE <= 1:
                    continue
                for dp in range(4):
                    kb.op("dve", lambda e, dp=dp: e.tensor_tensor_scan(CS[:, dp, :], ones[:, :], lw[:, dp, :], 0.0, ALU.mult, ALU.add),
                          reads=[ones, lw], writes=[CS])
                kb.op("dve", lambda e: e.tensor_copy(TOT[:, :], CS[:, :, CH - 1]), reads=[CS], writes=[TOT])
                kb.op("dve", lambda e: e.tensor_tensor(CS[:, 2:4, :], lw[:, 2:4, :], CS[:, 2:4, :], ALU.subtract), reads=[lw, CS], writes=[CS])
                kb.op("dve", lambda e: e.tensor_tensor(CS[:, 2:4, :], CS[:, 2:4, :], TOT[:, 2:4].unsqueeze(2).broadcast_to([128, 2, CH]), ALU.add),
                      reads=[CS, TOT], writes=[CS])
                kb.op("act", lambda e: e.activation(Epos[:], CS[:], AF.Exp), reads=[CS], writes=[Epos])
                kb.op("act", lambda e: e.activation(Eneg[:], CS[:], AF.Exp, scale=-1.0), reads=[CS], writes=[Eneg])
                kb.op("pool", lambda e: e.tensor_tensor(TMP[:], CS[:], lw[:], ALU.subtract), reads=[CS, lw], writes=[TMP])
                kb.op("act", lambda e: e.activation(Eprev[:], TMP[:], AF.Exp), reads=[TMP], writes=[Eprev])
                kb.op("dve", lambda e: e.tensor_tensor(Etot[:], TOT[:, :].unsqueeze(2).broadcast_to([128, 4, CH]), CS[:], ALU.subtract),
                      reads=[TOT, CS], writes=[Etot])
                kb.op("act", lambda e: e.activation(Etot[:], Etot[:], AF.Exp), reads=[Etot], writes=[Etot])
                kb.op("act", lambda e: e.activation(Wtot[:], TOT[:], AF.Exp), reads=[TOT], writes=[Wtot])
                kb.op("dve", lambda e: e.tensor_tensor(Ab[:], al[:], Eprev[:], ALU.mult), reads=[al, Eprev], writes=[Ab])
                kb.op("pool", lambda e: e.tensor_tensor(Bb[:], be[:], Eneg[:], ALU.mult), reads=[be, Eneg], writes=[Bb])
                kb.op("dve", lambda e: e.tensor_tensor(Kb[:], kd[:], Eneg[:], ALU.mult), reads=[kd, Eneg], writes=[Kb])
                kb.op("pool", lambda e: e.tensor_tensor(Rb[:], rt[:], Epos[:], ALU.mult), reads=[rt, Epos], writes=[Rb])
                kb.op("dve", lambda e: e.tensor_tensor(Bt[:], be[:], Etot[:], ALU.mult), reads=[be, Etot], writes=[Bt])
                kb.op("pool", lambda e: e.tensor_tensor(Kt[:], kd[:], Etot[:], ALU.mult), reads=[kd, Etot], writes=[Kt])
                if RW_STAGE <= 2:
                    continue
                PA, PB, PC, PT1, PD, PX, PPQ, PE_ = PS
                mm_all(PA, lambda dp: (dp * 128, dp * 128 + 64), lambda dp, r: Bb[r, dp, :], lambda dp, r: Ab[r, dp, :], [Bb, Ab])
                mm_all(PA, lambda dp: (dp * 128 + 64, dp * 128 + 128), lambda dp, r: Bb[r, dp, :], lambda dp, r: Rb[r, dp, :], [Bb, Rb])
                mm_all(PB, lambda dp: (dp * 128, dp * 128 + 64), lambda dp, r: Kb[r, dp, :], lambda dp, r: Ab[r, dp, :], [Kb, Ab])
                mm_all(PB, lambda dp: (dp * 128 + 64, dp * 128 + 128), lambda dp, r: Kb[r, dp, :], lambda dp, r: Rb[r, dp, :], [Kb, Rb])
                mm_all(PC, lambda dp: (dp * 64, dp * 64 + 64), lambda dp, r: Ab[r, dp, :], lambda dp, r: Bb[r, dp, :], [Ab, Bb])
                q0, p0 = Q[0], P[0]
                pav = PA[:, :].rearrange("p (dp x) -> p dp x", dp=4)
                pbv = PB[:, :].rearrange("p (dp x) -> p dp x", dp=4)
                def mk(m):
                    return m[:, :, :].unsqueeze(2).broadcast_to([128, 2, 2, 64])
                def v4(ap):
                    return ap.rearrange("p (d pr) x -> p d pr x", d=2)
                kb.op("dve", lambda e: e.tensor_tensor(v4(q0[:]), v4(pav[:, :, 0:64]), mk(MTs), ALU.mult), reads=[PA, MTs], writes=[q0])
                kb.op("dve", lambda e: e.tensor_tensor(v4(ArbT[:]), v4(pav[:, :, 64:128]), mk(MTi), ALU.mult), reads=[PA, MTi], writes=[ArbT])
                kb.op("dve", lambda e: e.tensor_tensor(v4(AkvT[:]), v4(pbv[:, :, 0:64]), mk(MTs), ALU.mult), reads=[PB, MTs], writes=[AkvT])
                kb.op("dve", lambda e: e.tensor_tensor(v4(ArkT[:]), v4(pbv[:, :, 64:128]), mk(MTi), ALU.mult), reads=[PB, MTi], writes=[ArkT])
                kb.op("dve", lambda e: e.tensor_tensor(v4(p0[:]), v4(PC[:, 0:256].rearrange("p (dp x) -> p dp x", dp=4)), mk(Ms), ALU.mult),
                      reads=[PC, Ms], writes=[p0])
                if RW_STAGE <= 3:
                    continue
                def idb(r):
                    return ident_f[r, r.start:r.start + 64]
                mm_all(PT1, lambda dp: (dp * 128, dp * 128 + 64), lambda dp, r: Ab[r, dp, :], lambda dp, r: idb(r), [Ab, ident_f])
                mm_all(PT1, lambda dp: (dp * 128 + 64, dp * 128 + 128), lambda dp, r: Bt[r, dp, :], lambda dp, r: idb(r), [Bt, ident_f])
                mm_all(PC, lambda dp: (256 + dp * 64, 256 + dp * 64 + 64), lambda dp, r: Kt[r, dp, :], lambda dp, r: idb(r), [Kt, ident_f])
                x0 = X[0]
                pt1v = PT1[:, :].rearrange("p (dp x) -> p dp x", dp=4)
                kb.op("act", lambda e: e.copy(x0[:, :, 0:64], pt1v[:, :, 0:64]), reads=[PT1], writes=[x0])
                kb.op("act", lambda e: e.copy(Btok[:], pt1v[:, :, 64:128]), reads=[PT1], writes=[Btok])
                kb.op("act", lambda e: e.copy(Ktok[:], PC[:, 256:512].rearrange("p (dp x) -> p dp x", dp=4)), reads=[PC], writes=[Ktok])
                if RW_STAGE <= 4:
                    continue
                mm_all(PD, lambda dp: (dp * 64, dp * 64 + 64), lambda dp, r: AkvT[r, dp, :], lambda dp, r: vt[r, dp, :], [AkvT, vt])
                kb.op("act", lambda e: e.copy(x0[:, :, 64:128], PD[:, 0:256].rearrange("p (dp x) -> p dp x", dp=4)), reads=[PD], writes=[x0])
                if RW_STAGE <= 5:
                    continue
                qc, pc, xc = Q[0], P[0], X[0]
                for j in range(6):
                    qn, pn, xn = Q[(j + 1) % 2], P[(j + 1) % 2], X[(j + 1) % 2]
                    mm_all(PX, lambda dp: (dp * 128, dp * 128 + 128), lambda dp, r: qc[r, dp, :], lambda dp, r: xc[r, dp, :], [qc, xc])
                    kb.op("dve", lambda e, xn=xn, xc=xc: e.tensor_tensor(xn[:], xc[:], PX[:, :].rearrange("p (dp x) -> p dp x", dp=4), ALU.add),
                          reads=[xc, PX], writes=[xn])
                    if j < 5:
                        mm_all(PPQ, lambda dp: (dp * 64, dp * 64 + 64), lambda dp, r: qc[r, dp, :], lambda dp, r: pc[r, dp, :], [qc, pc])
                        mm_all(PPQ, lambda dp: (256 + dp * 64, 256 + dp * 64 + 64), lambda dp, r: pc[r, dp, :], lambda dp, r: qc[r, dp, :], [qc, pc])
                        kb.op("act", lambda e, pn=pn: e.copy(pn[:], PPQ[:, 0:256].rearrange("p (dp x) -> p dp x", dp=4)), reads=[PPQ], writes=[pn])
                        kb.op("act", lambda e, qn=qn: e.copy(qn[:], PPQ[:, 256:512].rearrange("p (dp x) -> p dp x", dp=4)), reads=[PPQ], writes=[qn])
                    qc, pc, xc = qn, pn, xn
                if RW_STAGE <= 6:
                    continue
                mm_all(PD, lambda dp: (256 + dp * 64, 256 + dp * 64 + 64), lambda dp, r: xc[r, dp, 0:64], lambda dp, r: ArbT[r, dp, :], [xc, ArbT])
                kb.op("dve", lambda e: e.tensor_tensor(RAT[:], Rb[:], PD[:, 256:512].rearrange("p (dp x) -> p dp x", dp=4), ALU.add),
                      reads=[Rb, PD], writes=[RAT])
                mm_all(PE_, lambda dp: (dp * 64, dp * 64 + 64), lambda dp, r: xc[r, dp, 0:64], lambda dp, r: Btok[r, dp, :], [xc, Btok])
                kb.op("pool", lambda e: e.tensor_tensor(DG[:], id2[:, :].unsqueeze(1).broadcast_to([128, 4, 64]),
                                                        Wtot[:, :].unsqueeze(2).broadcast_to([128, 4, 64]), ALU.mult), reads=[id2, Wtot], writes=[DG])
                kb.op("dve", lambda e: e.tensor_tensor(McT[:], DG[:], PE_[:, 0:256].rearrange("p (dp x) -> p dp x", dp=4), ALU.add),
                      reads=[DG, PE_], writes=[McT])
                for dp in range(4):
                    for hp in range(2):
                        r = HP[hp]
                        c0 = 256 + dp * 64
                        kb.op("pe", lambda e, dp=dp, r=r, c0=c0: e.matmul(PE_[r, c0:c0 + 64], Btok[r, dp, :], xc[r, dp, 64:128], start=True, stop=False),
                              reads=[Btok, xc], writes=[PE_])
                        kb.op("pe", lambda e, dp=dp, r=r, c0=c0: e.matmul(PE_[r, c0:c0 + 64], Ktok[r, dp, :], vt[r, dp, :], start=False, stop=True),
                              reads=[Ktok, vt], writes=[PE_])
                kb.op("act", lambda e: e.copy(NcS[:], PE_[:, 256:512].rearrange("p (dp x) -> p dp x", dp=4)), reads=[PE_], writes=[NcS])
                if RW_STAGE <= 7:
                    continue
                PYs = [PA, PT1]
                for dp in range(4):
                    for hp in range(2):
                        r = HP[hp]
                        PY = PYs[hp]
                        c0 = dp * 64
                        kb.op("pe", lambda e, dp=dp, r=r, c0=c0, PY=PY: e.matmul(PY[0:64, c0:c0 + 64], ST[r, dp, :], RAT[r, dp, :], start=True, stop=False),
                              reads=[ST, RAT], writes=[PY])
                        kb.op("pe", lambda e, dp=dp, r=r, c0=c0, PY=PY: e.matmul(PY[0:64, c0:c0 + 64], xc[r, dp, 64:128], ArbT[r, dp, :], start=False, stop=False),
                              reads=[xc, ArbT], writes=[PY])
                        kb.op("pe", lambda e, dp=dp, r=r, c0=c0, PY=PY: e.matmul(PY[0:64, c0:c0 + 64], vt[r, dp, :], ArkT[r, dp, :], start=False, stop=True),
                              reads=[vt, ArkT], writes=[PY])
                for d in range(2):
                    c0 = cidx[d] * CH
                    yv = ysb[d][:, :].rearrange("v (pr hp t) -> v pr hp t", pr=2, hp=2)
                    for hp in range(2):
                        kb.op("act", lambda e, d=d, hp=hp, yv=yv: e.copy(
                            yv[:, :, hp, :], PYs[hp][0:64, d * 128:(d + 1) * 128].rearrange("v (pr t) -> v pr t", pr=2)), reads=[PYs[hp]], writes=[ysb[d]])
                    dst = RS["YF" if d == 0 else "YB"]
                    kb.dma("sp", dst.t.rearrange("(h v) t -> v h t", v=64)[:, :, c0:c0 + CH],
                           ysb[d][:, :].rearrange("v (h t) -> v h t", h=4), reads=[ysb[d]], writes=[Buf()])
                if RW_STAGE <= 8:
                    continue
                PSS = PB
                mm_all(PSS, lambda dp: (dp * 64, dp * 64 + 64), lambda dp, r: McT[r, dp, :], lambda dp, r: ST[r, dp, :], [McT, ST])
                kb.op("dve", lambda e: e.tensor_tensor(ST[:], NcS[:], PSS[:, 0:256].rearrange("p (dp x) -> p dp x", dp=4), ALU.add),
                      reads=[NcS, PSS], writes=[ST])

    def phase_rwkv_out(l, with_ctx):
        with kb.scope():
            rk_ = colvec("rrk", W["rw_r_k"][l, :], W["rw_r_k"], [128, 2], "(j p) -> p j", p=128)
            lg_ = colvec("rlg", W["rw_ln_g"][l, :], W["rw_ln_g"], [128, 2], "(j p) -> p j", p=128)
            lb_ = colvec("rlb", W["rw_ln_b"][l, :], W["rw_ln_b"], [128, 2], "(j p) -> p j", p=128)
            nm = ("YF", "YB", "RT", "KD0", "KD1", "VT")
            tl = [{n: kb.sb(f"o{n}{i}", [128, 512]) for n in nm} for i in range(2)]
            sg = [kb.sb(f"osg{i}", [128, 512], BF16) for i in range(2)]
            ob = [kb.sb(f"oob{i}", [128, 512], BF16) for i in range(2)]
            wk = [[kb.sb(f"owk{k}{i}", [128, 512]) for k in range(3)] for i in range(2)]
            it = 0
            for pr in range(2):
                rows = slice(pr * 128, (pr + 1) * 128)
                for (t0, nt) in TCH:
                    if not with_ctx and t0 + nt <= C:
                        continue
                    t_, s_, o_, (a_, b_, c_) = tl[it % 2], sg[it % 2], ob[it % 2], wk[it % 2]
                    for k, n in enumerate(nm):
                        kb.dma("sp" if k % 2 == 0 else "pool", t_[n][:, 0:nt], RS[n][rows, t0:t0 + nt], reads=[RS[n]], writes=[t_[n]])
                    kb.dma("sp", s_[:, 0:nt], RS["SGT"][rows, t0:t0 + nt], reads=[RS["SGT"]], writes=[s_])
                    y = t_["YF"]
                    kb.op("dve", lambda e: e.tensor_tensor(y[:, 0:nt], y[:, 0:nt], t_["YB"][:, 0:nt], ALU.add), reads=[y, t_["YB"]], writes=[y])
                    p1, p2, p3 = PS[(3 * it) % 8], PS[(3 * it + 1) % 8], PS[(3 * it + 2) % 8]
                    kb.op("pe", lambda e: e.matmul(p1[:, 0:nt], blk64[:], y[:, 0:nt], start=True, stop=True), reads=[blk64, y], writes=[p1])
                    kb.op("dve", lambda e: e.scalar_tensor_tensor(a_[:, 0:nt], p1[:, 0:nt], -1.0 / 64, y[:, 0:nt], ALU.mult, ALU.add),
                          reads=[p1, y], writes=[a_])
                    kb.op("act", lambda e: e.activation(b_[:, 0:nt], a_[:, 0:nt], AF.Square), reads=[a_], writes=[b_])
                    kb.op("pe", lambda e: e.matmul(p2[:, 0:nt], blk64[:], b_[:, 0:nt], start=True, stop=True), reads=[blk64, b_], writes=[p2])
                    kb.op("dve", lambda e: e.tensor_scalar(b_[:, 0:nt], p2[:, 0:nt], 1.0 / 64, 64e-5, ALU.mult, ALU.add), reads=[p2], writes=[b_])
                    kb.op("act", lambda e: e.sqrt(b_[:, 0:nt], b_[:, 0:nt]), reads=[b_], writes=[b_])
                    kb.op("dve", lambda e: e.reciprocal(b_[:, 0:nt], b_[:, 0:nt]), reads=[b_], writes=[b_])
                    kb.op("dve", lambda e: e.tensor_tensor(a_[:, 0:nt], a_[:, 0:nt], b_[:, 0:nt], ALU.mult), reads=[a_, b_], writes=[a_])
                    kb.op("dve", lambda e: e.tensor_scalar(a_[:, 0:nt], a_[:, 0:nt], lg_[:, pr:pr + 1], lb_[:, pr:pr + 1], ALU.mult, ALU.add),
                          reads=[a_, lg_, lb_], writes=[a_])
                    kb.op("pool", lambda e: e.tensor_tensor(c_[:, 0:nt], t_["KD0"][:, 0:nt], t_["KD1"][:, 0:nt], ALU.add),
                          reads=[t_["KD0"], t_["KD1"]], writes=[c_])
                    kb.op("dve", lambda e: e.scalar_tensor_tensor(c_[:, 0:nt], t_["RT"][:, 0:nt], rk_[:, pr:pr + 1], c_[:, 0:nt], ALU.mult, ALU.mult),
                          reads=[t_["RT"], rk_, c_], writes=[c_])
                    kb.op("pe", lambda e: e.matmul(p3[:, 0:nt], blk64[:], c_[:, 0:nt], start=True, stop=True), reads=[blk64, c_], writes=[p3])
                    kb.op("dve", lambda e: e.tensor_tensor(c_[:, 0:nt], p3[:, 0:nt], t_["VT"][:, 0:nt], ALU.mult), reads=[p3, t_["VT"]], writes=[c_])
                    kb.op("dve", lambda e: e.tensor_tensor(a_[:, 0:nt], a_[:, 0:nt], c_[:, 0:nt], ALU.add), reads=[a_, c_], writes=[a_])
                    kb.op("pool", lambda e: e.tensor_tensor(o_[:, 0:nt], a_[:, 0:nt], s_[:, 0:nt], ALU.mult), reads=[a_, s_], writes=[o_])
                    kb.dma("sp", mixT[256 + pr * 128:256 + (pr + 1) * 128, t0:t0 + nt], o_[:, 0:nt], reads=[o_], writes=[Buf()])
                    it += 1


    SEGS = {"L": dict(Ls=L, A=32, cbw=32, off=C, ut="UTL"), "C": dict(Ls=C, A=2, cbw=64, off=0, ut="UTC")}

    def phase_hyena_prep(l, with_ctx):
        with kb.scope():
            stage = kb.sb("hstage", [128, 8, 128])
            wts = [kb.sb(f"hwt{i}", [128, 8, 128], BF16) for i in range(2)]
            cw = kb.sb("hcw", [128, 6, 3])
            for k in range(3):
                kb.dma("sp", cw[:, :, k], W["hy_conv"][l, k, :].rearrange("(j p) -> p j", p=128), reads=[W["hy_conv"]], writes=[cw], slow=True)
            ncw = kb.sb("hncw", [128, 6, 3])
            kb.op("dve", lambda e: e.tensor_scalar(ncw[:], cw[:], -1.0, None, ALU.mult), reads=[cw], writes=[ncw])
            Zraw = kb.sb("hZraw", [128, T + 2])
            Zout = kb.sb("hZout", [128, T])
            kb.op("pool", lambda e: e.memset(Zraw[:, 0:1], 0.0), writes=[Zraw])
            kb.op("pool", lambda e: e.memset(Zraw[:, T + 1:T + 2], 0.0), writes=[Zraw])
            ub = kb.sb("hub", [128, 32 * 128])
            tG = [kb.sb(f"htG{i}", [128, 512], BF16) for i in range(2)]
            for oi, jt in enumerate(range(8)):
                wt = wts[oi % 2]
                c0 = HY0 + jt * 128 if jt < 6 else HYG0 + (jt - 6) * 128
                load_w(l, wt, c0, 128, stage)
                if jt >= 6:
                    for ci, (t0, nt) in enumerate(TCH):
                        p = PS[ci % 4]
                        proj_fm(p, wt, 0, 128, t0, nt)
                        g = tG[ci % 2]
                        kb.op("act", lambda e, p=p, g=g, nt=nt: e.activation(g[:, 0:nt], p[:, 0:nt], AF.Silu), reads=[p], writes=[g])
                        kb.dma("sp", HS["SG"][(jt - 6) * 128:(jt - 5) * 128, t0:t0 + nt], g[:, 0:nt], reads=[g], writes=[Buf()])
                    continue
                conv_tile(l, wt, cw, ncw, jt, Zraw, Zout)
                arr, half = jt // 2, jt % 2
                for sn in (("L", "C") if with_ctx else ("L",)):
                    sg = SEGS[sn]
                    A, cbw, off = sg["A"], sg["cbw"], sg["off"]
                    G = 128 // A
                    ncg = 128 // G
                    ubv = ub[:, 0:A * 128].rearrange("p (g a c) -> p g a c", g=ncg, a=A)
                    for a in range(A):
                        p = PS[4 + (a // 4) % 4]
                        kb.op("pe", lambda e, p=p, a=a, A=A, off=off: e.transpose(
                            p[:, (a % 4) * 128:(a % 4 + 1) * 128], Zout[:, off + a:off + a + 127 * A + 1:A], ident_f[:]),
                            reads=[Zout, ident_f], writes=[p])
                        if a % 4 == 3 or a == A - 1:
                            a0 = (a // 4) * 4
                            na = a - a0 + 1
                            kb.op("act", lambda e, p=p, a0=a0, na=na, G=G: e.copy(
                                ubv[:, :, a0:a0 + na, :], p[:, 0:na * 128].rearrange("p (a g c) -> p g a c", a=na, c=G)), reads=[p], writes=[ub])
                    nb = 128 // cbw
                    bsz = A * cbw
                    for b in range(nb):
                        dst = HS[sg["ut"]][arr, half * nb + b, :, :]
                        kb.dma("sp" if b % 2 == 0 else "pool", dst, ub[:, b * bsz:(b + 1) * bsz], reads=[ub], writes=[Buf()])

    def cmul(dre, dim_, sre, sim, tre, tim, conj, srcb, tabb, dstb, tmp):
        t1, t2 = tmp
        sh = tuple(slice(None) for _ in range(1))
        kb.op("dve", lambda e: e.tensor_tensor(t1, sre, tre, ALU.mult), reads=srcb + tabb, writes=[dstb[2]])
        kb.op("dve", lambda e: e.tensor_tensor(t2, sim, tim, ALU.mult), reads=srcb + tabb, writes=[dstb[3]])
        kb.op("pool", lambda e: e.tensor_tensor(dre, t1, t2, ALU.add if conj else ALU.subtract), reads=[dstb[2], dstb[3]], writes=[dstb[0]])
        kb.op("dve", lambda e: e.tensor_tensor(t1, sim, tre, ALU.mult), reads=srcb + tabb + [dstb[0]], writes=[dstb[2]])
        kb.op("dve", lambda e: e.tensor_tensor(t2, sre, tim, ALU.mult), reads=srcb + tabb + [dstb[0]], writes=[dstb[3]])
        kb.op("pool", lambda e: e.tensor_tensor(dim_, t1, t2, ALU.subtract if conj else ALU.add), reads=[dstb[2], dstb[3]], writes=[dstb[1]])

    def phase_hyena_main(l, with_ctx):
        PI = math.pi
        with kb.scope():
            fw1 = kb.sb("hfw1", [33, 64])
            fw2 = kb.sb("hfw2", [64, 64])
            fw3 = kb.sb("hfw3", [64, 1024])
            kb.dma("sp", fw1[:], W["hy_fw1"][l, :, :], reads=[W["hy_fw1"]], writes=[fw1])
            kb.dma("sp", fw2[:], W["hy_fw2"][l, :, :], reads=[W["hy_fw2"]], writes=[fw2])
            kb.dma("sp", fw3[:], W["hy_fw3"][l, :, :], reads=[W["hy_fw3"]], writes=[fw3])
            fb1 = colvec("hfb1", W["hy_fb1"][l, :], W["hy_fb1"], [64, 1], "(d o) -> d o", o=1)
            fb2 = colvec("hfb2", W["hy_fb2"][l, :], W["hy_fb2"], [64, 1], "(d o) -> d o", o=1)
            frq = colvec("hfrq", W["hy_freq"][l, :], W["hy_freq"], [64, 1], "(d o) -> d o", o=1)
            brow = kb.sb("hbrow", [1, 512])
            kb.dma("sp", brow[:], W["hy_bias"][l, :, :].rearrange("o c -> (o c)").rearrange("(x n) -> x n", x=1), reads=[W["hy_bias"]], writes=[brow])
            for sn in (("L", "C") if with_ctx else ("L",)):
                sg = SEGS[sn]
                Ls, A, cbw, off = sg["Ls"], sg["A"], sg["cbw"], sg["off"]
                G = 128 // A
                N = 2 * Ls
                ngr = cbw // G
                nblk = 256 // cbw
                pre = f"hy{sn}_"
                with kb.scope():
                    def ld(nm, shape):
                        t = kb.sb("k" + nm, shape)
                        src = CT[pre + nm]
                        kb.dma("sp", t[:], src.t, reads=[src], writes=[t])
                        return t
                    def ldr(nm, shape):
                        tr = kb.sb("r" + nm, shape, F32R)
                        with kb.scope():
                            t32 = ld(nm, shape)
                            kb.op("dve", lambda e: e.tensor_copy(tr[:], t32[:]), reads=[t32], writes=[tr])
                        return tr
                    F256 = ldr("F256", [128, 2, 512]); TWC = ld("TWC", [128, 256]); TWS = ld("TWS", [128, 256])
                    Dre = ldr("Dre", [128, 128]); Dim = ldr("Dim", [128, 128]); nDim = ldr("nDim", [128, 128])
                    E1 = ldr("E1", [128, 256]); E2 = ldr("E2", [128, 256])
                    TW2C = ld("TW2C", [128, 2, 128]); TW2S = ld("TW2S", [128, 2, 128])
                    IC = ldr("IC", [128, 2, 128]); IS = ldr("IS", [128, 2, 128])
                    h2T = kb.sb("h2T", [64, N])
                    with kb.scope():
                        zT = kb.sb("zT", [33, N])
                        kb.dma("sp", zT[:], CT[pre + "zT"].t, reads=[CT[pre + "zT"]], writes=[zT])
                        h1T = kb.sb("h1T", [64, N])
                        arg = [kb.sb(f"harg{i}", [64, 512]) for i in range(2)]
                        wr = [kb.sb(f"hwr{i}", [64, 512]) for i in range(2)]
                        for (src, K_, wgt, bcol, dst) in ((zT, 33, fw1, fb1, h1T), (h1T, 64, fw2, fb2, h2T)):
                            for ci, n0 in enumerate(range(0, N, 512)):
                                p = PS[ci % 4]
                                ag = arg[ci % 2]
                                kb.op("pe", lambda e: e.matmul(p[0:64, :], wgt[0:K_, :], src[0:K_, n0:n0 + 512], start=True, stop=True),
                                      reads=[wgt, src], writes=[p])
                                kb.op("dve", lambda e: e.tensor_scalar(ag[:, :], p[0:64, :], bcol[:, 0:1], frq[:, 0:1], ALU.add, ALU.mult),
                                      reads=[p, bcol, frq], writes=[ag])
                                for _w in range(2):
                                    kb.op("dve", lambda e: e.tensor_scalar(wr[0][:, :], ag[:, :], PI, -2 * PI, ALU.is_gt, ALU.mult), reads=[ag], writes=[wr[0]])
                                    kb.op("dve", lambda e: e.tensor_scalar(wr[1][:, :], ag[:, :], -PI, 2 * PI, ALU.is_lt, ALU.mult), reads=[ag], writes=[wr[1]])
                                    kb.op("dve", lambda e: e.tensor_tensor(ag[:, :], ag[:, :], wr[0][:, :], ALU.add), reads=[ag, wr[0]], writes=[ag])
                                    kb.op("dve", lambda e: e.tensor_tensor(ag[:, :], ag[:, :], wr[1][:, :], ALU.add), reads=[ag, wr[1]], writes=[ag])
                                kb.op("act", lambda e: e.activation(dst[:, n0:n0 + 512], ag[:, :], AF.Sin), reads=[ag], writes=[dst])
                    KT = [kb.sb(f"KT{o}", [128, 2, ngr, A, G]) for o in range(2)]
                    KTr = [kb.sb(f"KTr{o}", [128, 2, ngr, A, G], F32R) for o in range(2)]
                    uvr = kb.sb("huvr", [128, ngr, A * G], F32R)
                    KS = [kb.sb(f"KS{o}", [128, ngr, 512]) for o in range(2)]
                    DECt = kb.sb("DECt", [128, 2, ngr, A, G])
                    part = kb.sb("hpart", [128, cbw])
                    rn = kb.sb("hrn", [128, cbw])
                    ex = kb.sb("hex", [1, cbw])
                    uv = kb.sb("huv", [128, ngr, A * G]); x1 = kb.sb("hx1", [128, ngr, A * G]); x2 = kb.sb("hx2", [128, ngr, A * G])
                    u2 = kb.sb("hu2", [128, ngr, A * G], F32R); res = kb.sb("hres", [128, A, cbw])
                    dts = (F32R, F32R, F32, F32)
                    BpS = [[kb.sb(f"hBp{b}{i}", [128, 256], dts[i]) for i in range(4)] for b in range(2)]
                    BpbS = [[Buf() for _ in range(4)] for b in range(2)]
                    YpS = [[kb.sb(f"hYp{b}{i}", [128, 256], dts[i]) for i in range(4)] for b in range(2)]
                    YpbS = [[Buf() for _ in range(4)] for b in range(2)]
                    GpS = [[kb.sb(f"hGp{b}{i}", [128, 2, 128], dts[i]) for i in range(4)] for b in range(2)]
                    GpbS = [[Buf() for _ in range(4)] for b in range(2)]
                    fctr = [0]
                    Fm = kb.sb("hFm", [cbw, Ls])
                    sgm = kb.sb("hsgm", [cbw, Ls], BF16)

                    def fwd_fft(lhs_chunks, lhs_bufs, psB, psX):
                        n = len(lhs_chunks)
                        fctr[0] += 1
                        Bp, Bpb = BpS[fctr[0] % 2], BpbS[fctr[0] % 2]
                        for i, (ap, hf) in enumerate(lhs_chunks):
                            kb.op("pe", lambda e, ap=ap, hf=hf, i=i: e.matmul(psB[:, :], ap, F256[:, hf, :], start=(i == 0), stop=(i == n - 1)),
                                  reads=lhs_bufs + [F256], writes=[psB])
                        yield
                        cmul(Bp[0][:, :], Bp[1][:, :], psB[:, 0:256], psB[:, 256:512], TWC[:, :], TWS[:, :], True,
                             [psB], [TWC, TWS], Bpb, (Bp[2][:, :], Bp[3][:, :]))
                        yield
                        kb.op("pe", lambda e: e.matmul(psX[:, 0:256], Dre[:, :], Bp[0][:, :], start=True, stop=False), reads=[Dre, Bpb[0]], writes=[psX])
                        kb.op("pe", lambda e: e.matmul(psX[:, 0:256], nDim[:, :], Bp[1][:, :], start=False, stop=True), reads=[nDim, Bpb[1]], writes=[psX])
                        kb.op("pe", lambda e: e.matmul(psX[:, 256:512], Dim[:, :], Bp[0][:, :], start=True, stop=False), reads=[Dim, Bpb[0]], writes=[psX])
                        kb.op("pe", lambda e: e.matmul(psX[:, 256:512], Dre[:, :], Bp[1][:, :], start=False, stop=True), reads=[Dre, Bpb[1]], writes=[psX])

                    def conv_group(src, src_b, g, o, mulv, mul_b, dst_ap, dst_b, it):
                        psB, psX, psG, psy = PS[it % 2], PS[2 + it % 2], PS[4 + it % 2], PS[6 + it % 2]
                        Yp, Ypb, Gp, Gpb = YpS[it % 2], YpbS[it % 2], GpS[it % 2], GpbS[it % 2]
                        yield from fwd_fft([(src[:, g, :], 0)], [src_b], psB, psX)
                        yield
                        cmul(Yp[0][:, :], Yp[1][:, :], psX[:, 0:256], psX[:, 256:512], KS[o][:, g, 0:256], KS[o][:, g, 256:512], False,
                             [psX], [KS[o]], Ypb, (Yp[2][:, :], Yp[3][:, :]))
                        yield
                        for chn in range(2):
                            fs = slice(chn * 128, (chn + 1) * 128)
                            kb.op("pe", lambda e, fs=fs, chn=chn: e.matmul(psG[:, chn * 256:(chn + 1) * 256], Yp[0][:, fs], E1[:, :], start=True, stop=False),
                                  reads=[Ypb[0], E1], writes=[psG])
                            kb.op("pe", lambda e, fs=fs, chn=chn: e.matmul(psG[:, chn * 256:(chn + 1) * 256], Yp[1][:, fs], E2[:, :], start=False, stop=True),
                                  reads=[Ypb[1], E2], writes=[psG])
                        yield
                        pg = psG[:, :].rearrange("p (ch ri c) -> p ch ri c", ch=2, ri=2)
                        cmul(Gp[0][:, :, :], Gp[1][:, :, :], pg[:, :, 0, :], pg[:, :, 1, :], TW2C[:, :, :], TW2S[:, :, :], False,
                             [psG], [TW2C, TW2S], Gpb, (Gp[2][:, :, :], Gp[3][:, :, :]))
                        yield
                        k = 0
                        for chn in range(2):
                            for (tab, gsrc, gb) in ((IC, Gp[0], Gpb[0]), (IS, Gp[1], Gpb[1])):
                                kb.op("pe", lambda e, chn=chn, tab=tab, gsrc=gsrc, k=k: e.matmul(
                                    psy[:, 0:128], tab[:, chn, :], gsrc[:, chn, :], start=(k == 0), stop=(k == 3)), reads=[tab, gb], writes=[psy])
                                k += 1
                        yield
                        kb.op("dve", lambda e: e.tensor_tensor(dst_ap, psy[:, 0:128].rearrange("p (c a) -> p a c", a=A),
                                                               mulv[:, g, :].rearrange("p (a c) -> p a c", c=G), ALU.mult),
                              reads=[psy, mul_b], writes=[dst_b])

                    def lockstep(gens):
                        gens = list(gens)
                        while gens:
                            nxt = []
                            for g_ in gens:
                                try:
                                    next(g_)
                                    nxt.append(g_)
                                except StopIteration:
                                    pass
                            gens = nxt

                    def spec_group(o, g, it):
                        psB, psX = PS[it % 2], PS[2 + it % 2]
                        yield from fwd_fft([(KTr[o][:, 0, g, :, :].rearrange("p a c -> p (a c)"), 0),
                                            (KTr[o][:, 1, g, :, :].rearrange("p a c -> p (a c)"), 1)], [KTr[o]], psB, psX)
                        yield
                        kb.op("act", lambda e: e.copy(KS[o][:, g, :], psX[:, :]), reads=[psX], writes=[KS[o]])

                    git = 0
                    for cb in range(nblk):
                        kb.dma("sp", DECt[:].rearrange("p h g a c -> p (h g a c)"), CT[pre + "DEC"][cb, :, :], reads=[CT[pre + "DEC"]], writes=[DECt])
                        for ai, tile_ in enumerate((uv, x1, x2)):
                            kb.dma("pool", tile_[:].rearrange("p g x -> p (g x)"), HS[sg["ut"]][ai, cb, :, :], reads=[HS[sg["ut"]]], writes=[tile_])
                        for o in range(2):
                            for hf in range(2):
                                col0 = o * 512 + hf * 256 + cb * cbw
                                npb = 512 // cbw
                                for a in range(A):
                                    p = PS[(a // npb) % 4]
                                    kb.op("pe", lambda e, p=p, a=a, hf=hf, col0=col0, npb=npb: e.matmul(
                                        p[:, (a % npb) * cbw:(a % npb + 1) * cbw], h2T[0:64, hf * 128 * A + a:hf * 128 * A + a + 127 * A + 1:A],
                                        fw3[0:64, col0:col0 + cbw], start=True, stop=True), reads=[h2T, fw3], writes=[p])
                                    if a % npb == npb - 1 or a == A - 1:
                                        a0 = (a // npb) * npb
                                        na = a - a0 + 1
                                        kb.op("dve", lambda e, p=p, a0=a0, na=na, hf=hf, o=o: e.tensor_tensor(
                                            KT[o][:, hf, :, a0:a0 + na, :], p[:, 0:na * cbw].rearrange("p (a g c) -> p g a c", a=na, c=G),
                                            DECt[:, hf, :, a0:a0 + na, :], ALU.mult), reads=[p, DECt], writes=[KT[o]])
                            kb.op("dve", lambda e, o=o: e.tensor_reduce(part[:, :].rearrange("p (g c) -> p g c", c=G),
                                                                        KT[o][:, :, :, :, :].rearrange("p h g a c -> p g c h a"), AX.XY, ALU.add,
                                                                        apply_absolute_value=True), reads=[KT[o]], writes=[part])
                            pe_ = PS[4]
                            kb.op("pe", lambda e, o=o: e.matmul(pe_[0:1, 0:cbw], h2T[0:64, 0:1], fw3[0:64, o * 512 + 256 + cb * cbw:o * 512 + 256 + (cb + 1) * cbw],
                                                                start=True, stop=True), reads=[h2T, fw3], writes=[pe_])
                            kb.op("act", lambda e: e.activation(ex[0:1, :], pe_[0:1, 0:cbw], AF.Abs), reads=[pe_], writes=[ex])
                            kb.op("dve", lambda e: e.tensor_tensor(part[0:1, :], part[0:1, :], ex[0:1, :], ALU.add), reads=[part, ex], writes=[part])
                            pt_ = PS[5]
                            kb.op("pe", lambda e: e.matmul(pt_[:, 0:cbw], ones_f[:, :], part[:, :], start=True, stop=True), reads=[ones_f, part], writes=[pt_])
                            kb.op("dve", lambda e: e.reciprocal(rn[:, :], pt_[:, 0:cbw]), reads=[pt_], writes=[rn])
                            for hf in range(2):
                                kb.op("dve", lambda e, o=o, hf=hf: e.tensor_tensor(
                                    KTr[o][:, hf, :, :, :], KT[o][:, hf, :, :, :],
                                    rn[:, :].rearrange("p (g c) -> p g c", c=G).unsqueeze(2).broadcast_to([128, ngr, A, G]), ALU.mult),
                                    reads=[KT[o], rn], writes=[KTr[o]])
                            kb.op("dve", lambda e, o=o: e.tensor_tensor(
                                KTr[o][0:1, 0, :, 0, :], KTr[o][0:1, 0, :, 0, :].bitcast(F32),
                                brow[0:1, o * 256 + cb * cbw:o * 256 + (cb + 1) * cbw].rearrange("p (g c) -> p g c", c=G), ALU.add),
                                reads=[KTr[o], brow], writes=[KTr[o]])
                            for g in range(0, ngr, 2):
                                gg = [g] + ([g + 1] if g + 1 < ngr else [])
                                lockstep([spec_group(o, g_, git + k_) for k_, g_ in enumerate(gg)])
                                git += len(gg)
                        kb.op("act", lambda e: e.copy(uvr[:], uv[:]), reads=[uv], writes=[uvr])
                        for g in range(0, ngr, 2):
                            gg = [g] + ([g + 1] if g + 1 < ngr else [])
                            lockstep([conv_group(uvr, uvr, g_, 0, x1, x1, u2[:, g_, :].rearrange("p (a c) -> p a c", c=G), u2, git + k_)
                                      for k_, g_ in enumerate(gg)])
                            git += len(gg)
                        for g in range(0, ngr, 2):
                            gg = [g] + ([g + 1] if g + 1 < ngr else [])
                            lockstep([conv_group(u2, u2, g_, 1, x2, x2, res[:, :, g_ * G:(g_ + 1) * G], res, git + k_)
                                      for k_, g_ in enumerate(gg)])
                            git += len(gg)
                        kb.dma("sp", sgm[:], HS["SG"][cb * cbw:(cb + 1) * cbw, off:off + Ls], reads=[HS["SG"]], writes=[sgm])
                        Fv = Fm[:, :].rearrange("c (p a) -> c p a", a=A)
                        for a in range(A):
                            p = PS[4 + (a // 4) % 4]
                            kb.op("pe", lambda e, p=p, a=a: e.transpose(p[0:cbw, (a % 4) * 128:(a % 4 + 1) * 128], res[:, a, :], ident_f[:]),
                                  reads=[res, ident_f], writes=[p])
                            if a % 4 == 3 or a == A - 1:
                                a0 = (a // 4) * 4
                                na = a - a0 + 1
                                kb.op("act", lambda e, p=p, a0=a0, na=na: e.copy(
                                    Fv[:, :, a0:a0 + na], p[0:cbw, 0:na * 128].rearrange("c (a p) -> c p a", p=128)), reads=[p], writes=[Fm])
                        kb.op("pool", lambda e: e.tensor_tensor(sgm[:, :], Fm[:, :], sgm[:, :], ALU.mult), reads=[Fm, sgm], writes=[sgm])
                        kb.dma("sp", mixT[cb * cbw:(cb + 1) * cbw, off:off + Ls], sgm[:, :], reads=[sgm], writes=[Buf()])

    dbgn = [n for n, _ in dbg]
    for l in range(depth):
        last = (l == DEPTH - 1)
        with kb.scope():
            hT = kb.sb("hT", [128, 8, T], BF16)
            G1 = kb.sb("G1", [128, 2, D])
            SH = kb.sb("SH", [128, 2, D])
            phase_mod(l)
            phase_norm(l)
            if "noattn" not in dbgn:
                phase_attn(l, False, not last)
                phase_attn(l, True, not last)
            if "norw" not in dbgn:
                phase_rwkv_prep(l)
            if "nohy" not in dbgn:
                phase_hyena_prep(l, not last)
            if "hT" in dbgn:
                tmp = kb.sb("dbghT", [128, T])
                for j in range(8):
                    kb.op("dve", lambda e, j=j, tmp=tmp: e.tensor_copy(tmp[:], hT[:, j, :]), reads=[hT], writes=[tmp])
                    kb.dma("sp", dbg_t["hT"][:, j, :], tmp[:], reads=[tmp], writes=[dbg_t["hT"]])
        if "norw" not in dbgn:
            phase_rwkv_chunked(l)
            phase_rwkv_out(l, not last)
        if "nohy" not in dbgn:
            phase_hyena_main(l, not last)
        if "noout" not in dbgn:
            phase_out(l, last)
    for n, s_ in dbg:
        if n == "xres":
            with kb.scope():
                tx = kb.sb("dbgx", [128, D])
                for i in range(NT):
                    kb.dma("sp", tx[:], xres[i * 128:(i + 1) * 128, :], reads=[xres_b[i]], writes=[tx])
                    kb.dma("sp", dbg_t[n][i * 128:(i + 1) * 128, :], tx[:], reads=[tx], writes=[dbg_t[n]])
        if n == "mixT":
            with kb.scope():
                tmpb = kb.sb("dbgmb", [128, T], BF16)
                tmpf = kb.sb("dbgmf", [128, T])
                for j in range(8):
                    kb.dma("sp", tmpb[:], mixT[j * 128:(j + 1) * 128, :], reads=[mixT], writes=[tmpb])
                    kb.op("dve", lambda e, tmpb=tmpb, tmpf=tmpf: e.tensor_copy(tmpf[:], tmpb[:]), reads=[tmpb], writes=[tmpf])
                    kb.dma("sp", dbg_t[n][j * 128:(j + 1) * 128, :], tmpf[:], reads=[tmpf], writes=[dbg_t[n]])
    kb.finish()
    kb.es.close()
    return kb, cst


_PROG = {}


def kernel(**inputs):
    if "p" not in _PROG:
        _PROG["p"] = build()
    kb, cst = _PROG["p"]
    f = lambda a: np.ascontiguousarray(np.asarray(a, dtype=np.float32))
    shared = {}
    for n in inputs:
        if n in ("x", "c", "ctx", "c_ctx"):
            continue
        shared[n] = f(inputs[n])
    shared["c_ctx"] = f(inputs["c_ctx"])
    for n, a in cst.items():
        shared["k_" + n] = np.ascontiguousarray(a)
    x, c, ctx = f(inputs["x"]), f(inputs["c"]), f(inputs["ctx"])
    B = x.shape[0]
    in_maps = []
    for b in range(B):
        m = dict(shared)
        m["x"] = np.ascontiguousarray(x[b])
        m["c"] = np.ascontiguousarray(c[b])
        m["ctx"] = np.ascontiguousarray(ctx[b])
        in_maps.append(m)
    res = run_bass_kernel_spmd(kb.nc, in_maps, core_ids=list(range(B)))
    return np.stack([np.asarray(res.results[b]["out"], dtype=np.float32) for b in range(B)], axis=0)
```

```python
import contextlib
import math
import numpy as np
import ml_dtypes
import concourse.bass as bass
import concourse.mybir as mybir
from concourse.bass_utils import run_bass_kernel_spmd

F32 = mybir.dt.float32
BF16 = mybir.dt.bfloat16
F32R = mybir.dt.float32r
ALU = mybir.AluOpType
AF = mybir.ActivationFunctionType
AX = mybir.AxisListType

D = 1024
L = 4096
C = 256
T = L + C
NT = T // 128
DEPTH = 4
D_IN = 3712
HY0, HYG0, RW0, RWG0, WA0, WAG0, FA0, FAG0 = 0, 768, 1024, 1920, 2176, 2688, 2944, 3456
EPS = 1e-6
NSLOT = 12
import os
RW_STAGE = int(os.environ.get('RW_STAGE', '99'))
RW_V2 = int(os.environ.get('RW_V2', '3'))


class Buf:
    def __init__(self, name=""):
        self.name = name
        self.w = None
        self.r = {}

    def wdeps(self):
        return [self.w] if self.w is not None else []

    def rdeps(self):
        return list(self.r.values())

    def add_reader(self, tok):
        k = tok[:2]
        if k not in self.r or self.r[k][2] < tok[2]:
            self.r[k] = tok

    def set_writer(self, tok):
        self.w = tok
        self.r = {}


class Tile(Buf):
    def __init__(self, name, t):
        super().__init__(name)
        self.t = t

    def __getitem__(self, key):
        return self.t[key]


class KB:
    def __init__(self):
        self.nc = bass.Bass("TRN2", target_bir_lowering=False)
        nc = self.nc
        self.es = contextlib.ExitStack()
        self.eng = {"pe": nc.tensor, "act": nc.scalar, "dve": nc.vector, "pool": nc.gpsimd, "sp": nc.sync}
        self.sem = {}
        self.cnt = {}
        self.waited = {e: {} for e in self.eng}
        for e in self.eng:
            self.sem[e] = self.es.enter_context(nc.semaphore("s_" + e))
            self.cnt[e] = 0
        self.slots = {}
        self.slot_i = {}
        for q in ("sp", "act", "pool"):
            self.slots[q] = [[self.es.enter_context(nc.semaphore(f"d_{q}{i}")), 0] for i in range(NSLOT)]
            self.slot_i[q] = 0
        self.n_ins = 0

    def sb(self, name, shape, dt=F32):
        self.uid = getattr(self, "uid", 0) + 1
        name = f"{name}_{self.uid}"
        return Tile(name, self.es.enter_context(self.nc.sbuf_tensor(name, list(shape), dt)))

    def ps(self, name, shape, dt=F32):
        return Tile(name, self.es.enter_context(self.nc.psum_tensor(name, list(shape), dt)))

    def dram(self, name, shape, dt=F32, kind="Internal"):
        t = self.nc.dram_tensor(name, list(shape), dt, kind=kind)
        b = Tile(name, t.ap())
        return b

    def _tok_sem(self, tok):
        if tok[0] == "e":
            return ("e", tok[1]), self.sem[tok[1]], tok[2]
        return ("d", tok[1]), self.slots[tok[1][0]][tok[1][1]][0], tok[2]

    def _wait(self, e, toks):
        need = {}
        for tok in toks:
            if tok is None:
                continue
            key, sem, val = self._tok_sem(tok)
            if tok[0] == "e" and tok[1] == e and e == "pe":
                continue
            if self.waited[e].get(key, 0) >= val:
                continue
            if key not in need or need[key][1] < val:
                need[key] = (sem, val)
        for key, (sem, val) in need.items():
            self.eng[e].wait_ge(sem, val)
            self.waited[e][key] = val

    def op(self, e, fn, reads=(), writes=()):
        toks = []
        for b in reads:
            toks += b.wdeps()
        for b in writes:
            toks += b.wdeps() + b.rdeps()
        self._wait(e, toks)
        ins = fn(self.eng[e])
        self.cnt[e] += 1
        ins.then_inc(self.sem[e], 1)
        tok = ("e", e, self.cnt[e])
        for b in reads:
            b.add_reader(tok)
        for b in writes:
            b.set_writer(tok)
        self.n_ins += 1
        return ins

    def dma(self, q, out, in_, reads=(), writes=(), slow=False):
        i = self.slot_i[q]
        self.slot_i[q] = (i + 1) % NSLOT
        slot = self.slots[q][i]
        toks = []
        if slot[1] > 0:
            toks.append(("d", (q, i), slot[1]))
        for b in reads:
            toks += b.wdeps()
        for b in writes:
            toks += b.wdeps() + b.rdeps()
        self._wait(q, toks)
        if slow:
            ins = self.eng[q].dma_start(out=out, in_=in_, allow_slow_non_contiguous=True)
        else:
            ins = self.eng[q].dma_start(out=out, in_=in_)
        ins.then_inc(slot[0], 16)
        slot[1] += 16
        tok = ("d", (q, i), slot[1])
        for b in reads:
            b.add_reader(tok)
        for b in writes:
            b.set_writer(tok)
        self.n_ins += 1
        return ins

    def barrier(self):
        toks = [("e", e, self.cnt[e]) for e in self.eng if self.cnt[e] > 0]
        for q in self.slots:
            for i, s in enumerate(self.slots[q]):
                if s[1] > 0:
                    toks.append(("d", (q, i), s[1]))
        for e in self.eng:
            self._wait(e, toks)

    def finish(self):
        self.barrier()

    @contextlib.contextmanager
    def scope(self):
        es = contextlib.ExitStack()
        old = self.es
        self.es = es
        try:
            yield
        finally:
            self.barrier()
            self.es = old
            es.close()


def host_consts():
    cst = {}
    cst["ident_bf"] = np.eye(128, dtype=np.float32).astype(ml_dtypes.bfloat16)
    cst["ident_f"] = np.eye(128, dtype=np.float32)
    blk = np.zeros((128, 128), np.float32)
    blk[:64, :64] = 1.0
    blk[64:, 64:] = 1.0
    cst["blk64"] = blk
    cst["ones_f"] = np.ones((128, 128), np.float32)
    t = np.arange(L)
    row = (t // 64).astype(np.float32)
    col = (t % 64).astype(np.float32)
    inv = (10000.0 ** (-np.arange(16, dtype=np.float32) / 16)).astype(np.float32)
    cosT = np.zeros((128, L), np.float32)
    sinT = np.zeros((128, L), np.float32)
    perm = np.zeros((128, 128), np.float32)
    for p in range(128):
        d = p % 64
        sec, half, f = d // 32, (d % 32) // 16, d % 16
        pos = row if sec == 0 else col
        ang = (pos * inv[f]).astype(np.float32)
        cosT[p] = np.cos(ang)
        sinT[p] = np.sin(ang)
        if half == 0:
            perm[p + 16, p] = -1.0
        else:
            perm[p - 16, p] = 1.0
    cst["rope_cos"] = cosT
    cst["rope_sin"] = sinT
    cst["rope_perm"] = perm
    i = np.arange(128)[:, None]
    j = np.arange(384)[None, :]
    cst["wmask"] = np.where((j >= i) & (j <= i + 256), 0.0, -1e30).astype(np.float32)
    ii = np.arange(64)
    ms = np.zeros((128, 2, 64), np.float32); mts = np.zeros((128, 2, 64), np.float32); mti = np.zeros((128, 2, 64), np.float32)
    for hp in range(2):
        rows = slice(hp * 64, hp * 64 + 64)
        ms[rows, 0, :] = (ii[None, :] < ii[:, None]); ms[rows, 1, :] = (ii[None, :] > ii[:, None])
        mts[rows, 0, :] = (ii[:, None] < ii[None, :]); mts[rows, 1, :] = (ii[:, None] > ii[None, :])
        mti[rows, 0, :] = (ii[:, None] <= ii[None, :]); mti[rows, 1, :] = (ii[:, None] >= ii[None, :])
    cst["rw_ms"] = ms; cst["rw_mts"] = mts; cst["rw_mti"] = mti
    cst["rw_msb"] = np.concatenate([ms, ms], 2); cst["rw_mtsb"] = np.concatenate([mts, mts], 2)
    cst["rw_id2"] = np.concatenate([np.eye(64, dtype=np.float32)] * 2, 0)
    cst.update(hy_consts(L, 32, 32, "L"))
    cst.update(hy_consts(C, 2, 64, "C"))
    return cst


def hy_consts(Ls, A, cbw, tag):
    G = 128 // A
    N = 2 * Ls
    out = {}
    p = np.arange(128)
    f1 = np.arange(256)
    F = np.zeros((128, 2, 512), np.float64)
    for h in range(2):
        pp = h * 128 + p
        ang = 2 * np.pi * ((pp[:, None] * f1[None, :]) % 256) / 256
        F[:, h, 0:256] = np.cos(ang)
        F[:, h, 256:512] = -np.sin(ang)
    out["F256"] = F
    a_of_row = np.arange(128) // G
    th = 2 * np.pi * ((a_of_row[:, None] * f1[None, :]) % N) / N
    out["TWC"] = np.cos(th)
    out["TWS"] = np.sin(th)
    Dre = np.zeros((128, 128)); Dim = np.zeros((128, 128))
    E1 = np.zeros((128, 256)); E2 = np.zeros((128, 256))
    for a in range(A):
        for c in range(G):
            for f2 in range(A):
                ph = 2 * np.pi * ((a * f2) % A) / A
                Dre[a * G + c, c * A + f2] = np.cos(ph)
                Dim[a * G + c, c * A + f2] = -np.sin(ph)
                E1[c * A + f2, c * A + a] = np.cos(ph)
                E1[c * A + f2, 128 + c * A + a] = np.sin(ph)
                E2[c * A + f2, c * A + a] = -np.sin(ph)
                E2[c * A + f2, 128 + c * A + a] = np.cos(ph)
    out["Dre"] = Dre; out["Dim"] = Dim; out["nDim"] = -Dim; out["E1"] = E1; out["E2"] = E2
    a_of_col = np.arange(128) % A
    TW2C = np.zeros((128, 2, 128)); TW2S = np.zeros((128, 2, 128))
    IC = np.zeros((128, 2, 128)); IS = np.zeros((128, 2, 128))
    for ch in range(2):
        ff = ch * 128 + np.arange(128)
        th2 = 2 * np.pi * ((ff[:, None] * a_of_col[None, :]) % N) / N
        TW2C[:, ch, :] = np.cos(th2) / N
        TW2S[:, ch, :] = np.sin(th2) / N
        ph = 2 * np.pi * ((ff[:, None] * p[None, :]) % 256) / 256
        IC[:, ch, :] = np.cos(ph)
        IS[:, ch, :] = -np.sin(ph)
    out["TW2C"] = TW2C; out["TW2S"] = TW2S; out["IC"] = IC; out["IS"] = IS
    tp = np.arange(N)
    pos = np.where(tp < Ls, tp, N - tp).astype(np.float64)
    tn = (pos / (Ls - 1)).astype(np.float32)
    w = ((2.0 * math.pi / Ls) * pos).astype(np.float32)
    fb = np.linspace(1e-4, 15.0, 16, dtype=np.float32)
    zT = np.zeros((33, N), np.float32)
    zT[0] = tn
    zT[1:17] = np.cos(fb[:, None] * w[None, :])
    zT[17:33] = np.sin(fb[:, None] * w[None, :])
    out["zT"] = zT
    deltas = np.abs(np.linspace(math.log(1e-2) / 1.5, math.log(1e-2) / 0.3, 256, dtype=np.float32))
    dec = np.exp(-tn[:, None] * deltas[None, :]).astype(np.float32)
    dec[Ls, :] = 0.0
    nblk = 256 // cbw
    ngr = cbw // G
    DEC = np.zeros((nblk, 128, 2, ngr, A, G), np.float32)
    for h in range(2):
        for a in range(A):
            tpp = A * (h * 128 + p) + a
            for b in range(nblk):
                DEC[b, :, h, :, a, :] = dec[tpp, b * cbw:(b + 1) * cbw].reshape(128, ngr, G)
    out["DEC"] = DEC.reshape(nblk, 128, 2 * ngr * A * G)
    return {f"hy{tag}_{k}": np.ascontiguousarray(v.astype(np.float32)) for k, v in out.items()}

CONST_SPECS = None


def build(depth=DEPTH, dbg=()):
    kb = KB()
    nc = kb.nc
    cst = host_consts()
    def inp(name, shape, dt=F32):
        return kb.dram(name, shape, dt, kind="ExternalInput")

    x_in = inp("x", [L, D])
    c_in = inp("c", [D])
    ctx_in = inp("ctx", [C, D])
    cctx_in = inp("c_ctx", [D])
    W = {}
    wspec = {
        "mod_w": [DEPTH, D, 3 * D], "mod_b": [DEPTH, 3 * D], "norm_g": [DEPTH, D], "w_in": [DEPTH, D, D_IN],
        "w_out": [DEPTH, D, D], "wa_sink": [DEPTH, 4], "fa_q_norm": [DEPTH, 64], "fa_k_norm": [DEPTH, 64],
        "final_g": [D],
        "rw_conv": [DEPTH, 3, 896], "rw_w0": [DEPTH, 2, 256], "rw_w_up": [DEPTH, 2, 64, 256], "rw_a0": [DEPTH, 2, 256],
        "rw_a_up": [DEPTH, 2, 64, 256], "rw_k_k": [DEPTH, 256], "rw_k_a": [DEPTH, 256], "rw_r_k": [DEPTH, 256],
        "rw_ln_g": [DEPTH, 256], "rw_ln_b": [DEPTH, 256],
        "hy_conv": [DEPTH, 3, 768], "hy_fw1": [DEPTH, 33, 64], "hy_fb1": [DEPTH, 64], "hy_freq": [DEPTH, 64],
        "hy_fw2": [DEPTH, 64, 64], "hy_fb2": [DEPTH, 64], "hy_fw3": [DEPTH, 64, 1024], "hy_bias": [DEPTH, 2, 256],
    }
    for n, s in wspec.items():
        W[n] = inp(n, s)
    CT = {}
    for n, a in cst.items():
        CT[n] = inp("k_" + n, list(a.shape), BF16 if a.dtype == ml_dtypes.bfloat16 else F32)
    out = kb.dram("out", [L, D], F32, kind="ExternalOutput")
    xres = kb.dram("xres", [T, D], F32)
    mixT = kb.dram("mixT", [D, T], BF16)
    RS = {}
    for n in ("RT", "VT", "AL", "W0", "W1", "B0", "B1", "KD0", "KD1", "YF", "YB"):
        RS[n] = kb.dram("rs_" + n, [256, T])
    RS["VTOK"] = kb.dram("rs_VTOK", [T, 256])
    RS["SGT"] = kb.dram("rs_SGT", [256, T], BF16)
    HS = {"SG": kb.dram("hs_SG", [256, T], BF16),
          "UTL": kb.dram("hs_UTL", [3, 8, 128, 32 * 32]), "UTC": kb.dram("hs_UTC", [3, 4, 128, 2 * 64])}
    dbg_t = {}
    for n, s in dbg:
        dbg_t[n] = kb.dram("dbg_" + n, s, F32, kind="ExternalOutput")

    ident_bf = kb.sb("ident_bf", [128, 128], BF16)
    ident_f = kb.sb("ident_f", [128, 128])
    blk64 = kb.sb("blk64", [128, 128])
    ones_f = kb.sb("ones_f", [128, 128])
    for tl, n in ((ident_bf, "ident_bf"), (ident_f, "ident_f"), (blk64, "blk64"), (ones_f, "ones_f")):
        kb.dma("sp", tl[:], CT[n][:, :], reads=[CT[n]], writes=[tl])
    hT = G1 = SH = None
    GT = kb.sb("GT", [128, 2, D])
    PS = [kb.ps(f"ps{i}", [128, 512]) for i in range(8)]

    xres_b = [Buf(f"xres{i}") for i in range(NT)]

    def x_src(l, i):
        if l == 0:
            if i < 2:
                return ctx_in[i * 128:(i + 1) * 128, :], ctx_in
            return x_in[(i - 2) * 128:(i - 1) * 128, :], x_in
        return xres[i * 128:(i + 1) * 128, :], xres_b[i]

    def phase_mod(l):
        with kb.scope():
            cc = kb.sb("cc", [128, 2, 8])
            sc = kb.sb("sc", [128, 2, 8])
            mw = [kb.sb(f"mw{i}", [128, 8, 512]) for i in range(2)]
            mb = kb.sb("mb", [128, 3 * D])
            ng = kb.sb("ng", [128, D])
            modr = kb.sb("modr", [128, 2, 3 * D])
            kb.dma("sp", cc[:, 0, :], c_in.t.rearrange("(j p) -> p j", p=128), reads=[c_in], writes=[cc], slow=True)
            kb.dma("sp", cc[:, 1, :], cctx_in.t.rearrange("(j p) -> p j", p=128), reads=[cctx_in], writes=[cc], slow=True)
            kb.dma("sp", mb[:], W["mod_b"][l, :].partition_broadcast(128), reads=[W["mod_b"]], writes=[mb])
            kb.dma("sp", ng[:], W["norm_g"][l, :].partition_broadcast(128), reads=[W["norm_g"]], writes=[ng])
            kb.op("act", lambda e: e.activation(sc[:], cc[:], AF.Silu), reads=[cc], writes=[sc])
            for n in range(6):
                m = mw[n % 2]
                kb.dma("sp" if n % 2 == 0 else "pool", m[:],
                       W["mod_w"][l, :, n * 512:(n + 1) * 512].rearrange("(j p) n -> p j n", p=128),
                       reads=[W["mod_w"]], writes=[m])
                for i in range(2):
                    p = PS[(2 * n + i) % 8]
                    for j in range(8):
                        kb.op("pe", lambda e, p=p, i=i, j=j, m=m: e.matmul(
                            p[:, :], sc[:, i, j:j + 1].broadcast_to([128, 128]), m[:, j, :],
                            start=(j == 0), stop=(j == 7)), reads=[sc, m], writes=[p])
                    kb.op("dve", lambda e, p=p, i=i, n=n: e.tensor_tensor(
                        modr[:, i, n * 512:(n + 1) * 512], p[:, :], mb[:, n * 512:(n + 1) * 512], ALU.add),
                        reads=[p, mb], writes=[modr])
            for i in range(2):
                kb.op("dve", lambda e, i=i: e.scalar_tensor_tensor(
                    G1[:, i, :], modr[:, i, D:2 * D], 1.0, ng[:], ALU.add, ALU.mult), reads=[modr, ng], writes=[G1])
                kb.op("act", lambda e, i=i: e.copy(SH[:, i, :], modr[:, i, 0:D]), reads=[modr], writes=[SH])
                kb.op("act", lambda e, i=i: e.copy(GT[:, i, :], modr[:, i, 2 * D:3 * D]), reads=[modr], writes=[GT])

    def phase_norm(l):
        with kb.scope():
            xt = [kb.sb(f"xt{i}", [128, D]) for i in range(3)]
            junk = kb.sb("junk", [128, D])
            hf = [kb.sb(f"hf{i}", [128, D]) for i in range(2)]
            hb = [kb.sb(f"hb{i}", [128, D], BF16) for i in range(2)]
            st = [kb.sb(f"st{i}", [128, 4]) for i in range(2)]
            for i in range(NT):
                x, s, h, hbt = xt[i % 3], st[i % 2], hf[i % 2], hb[i % 2]
                sel = 1 if i < 2 else 0
                src, srcb = x_src(l, i)
                kb.dma("sp" if i % 2 == 0 else "pool", x[:], src, reads=[srcb], writes=[x])
                kb.op("act", lambda e, x=x, s=s: e.activation(junk[:], x[:], AF.Square, accum_out=s[:, 0:1]),
                      reads=[x], writes=[junk, s])
                kb.op("dve", lambda e, s=s: e.tensor_scalar(s[:, 1:2], s[:, 0:1], 1.0 / D, EPS, ALU.mult, ALU.add),
                      reads=[s], writes=[s])
                kb.op("act", lambda e, s=s: e.sqrt(s[:, 2:3], s[:, 1:2]), reads=[s], writes=[s])
                kb.op("dve", lambda e, s=s: e.reciprocal(s[:, 3:4], s[:, 2:3]), reads=[s], writes=[s])
                kb.op("dve", lambda e, x=x, s=s, h=h, sel=sel: e.scalar_tensor_tensor(
                    h[:], x[:], s[:, 3:4], G1[:, sel, :], ALU.mult, ALU.mult), reads=[x, s, G1], writes=[h])
                kb.op("pool", lambda e, h=h, hbt=hbt, sel=sel: e.tensor_tensor(hbt[:], h[:], SH[:, sel, :], ALU.add),
                      reads=[h, SH], writes=[hbt])
                p = PS[i % 4]
                pv = p[:, :].bitcast(BF16)
                for j in range(8):
                    kb.op("pe", lambda e, j=j, pv=pv, hbt=hbt: e.transpose(
                        pv[:, j * 128:(j + 1) * 128], hbt[:, j * 128:(j + 1) * 128], ident_bf[:]),
                        reads=[hbt, ident_bf], writes=[p])
                kb.op("act", lambda e, pv=pv, i=i: e.copy(
                    hT[:, :, i * 128:(i + 1) * 128], pv.rearrange("p (j t) -> p j t", j=8)), reads=[p], writes=[hT])

    def load_w(l, dst, col0, ncols, stage, q="sp"):
        kb.dma(q, stage[:, :, 0:ncols], W["w_in"][l, :, col0:col0 + ncols].rearrange("(j p) n -> p j n", p=128),
               reads=[W["w_in"]], writes=[stage])
        kb.op("pool", lambda e: e.tensor_copy(dst[:, :, 0:ncols], stage[:, :, 0:ncols]), reads=[stage], writes=[dst])

    def proj_fm(p, wt, c0, nc_, t0, nt):
        for j in range(8):
            kb.op("pe", lambda e, j=j: e.matmul(p[0:nc_, 0:nt], wt[:, j, c0:c0 + nc_], hT[:, j, t0:t0 + nt],
                                                start=(j == 0), stop=(j == 7)), reads=[wt, hT], writes=[p])

    def proj_tm(p, wt, c0, nc_, i):
        for j in range(8):
            kb.op("pe", lambda e, j=j: e.matmul(p[:, 0:nc_], hT[:, j, i * 128:(i + 1) * 128], wt[:, j, c0:c0 + nc_],
                                                start=(j == 0), stop=(j == 7)), reads=[wt, hT], writes=[p])

    TCH = [(t0, min(512, T - t0)) for t0 in range(0, T, 512)]

    def qk_prep(l, es_tiles, wt, c0, dst, dst_j, gvec, norm, rope):
        raw, sq, rs, rot = es_tiles
        for ci, (t0, nt) in enumerate(TCH):
            p = PS[ci % 2]
            proj_fm(p, wt, c0, 128, t0, nt)
            if norm:
                kb.op("act", lambda e, p=p, nt=nt: e.activation(sq[:, 0:nt], p[:, 0:nt], AF.Square), reads=[p], writes=[sq])
                p2 = PS[2 + ci % 2]
                kb.op("pe", lambda e, p2=p2, nt=nt: e.matmul(p2[:, 0:nt], blk64[:], sq[:, 0:nt], start=True, stop=True),
                      reads=[blk64, sq], writes=[p2])
                kb.op("dve", lambda e, p2=p2, nt=nt: e.tensor_scalar(rs[:, 0:nt], p2[:, 0:nt], 1.0 / 64, EPS, ALU.mult, ALU.add),
                      reads=[p2], writes=[rs])
                kb.op("act", lambda e, nt=nt: e.sqrt(rs[:, 0:nt], rs[:, 0:nt]), reads=[rs], writes=[rs])
                kb.op("dve", lambda e, nt=nt: e.reciprocal(rs[:, 0:nt], rs[:, 0:nt]), reads=[rs], writes=[rs])
                kb.op("dve", lambda e, p=p, nt=nt: e.scalar_tensor_tensor(
                    raw[:, 0:nt], p[:, 0:nt], gvec[:, 0:1], rs[:, 0:nt], ALU.mult, ALU.mult), reads=[p, gvec, rs], writes=[raw])
            else:
                kb.op("act", lambda e, p=p, nt=nt: e.copy(raw[:, 0:nt], p[:, 0:nt]), reads=[p], writes=[raw])
            lat0 = 0
            if t0 < C:
                lat0 = C - t0
                kb.op("pool", lambda e, t0=t0, lat0=lat0: e.tensor_copy(dst[:, dst_j, t0:t0 + lat0], raw[:, 0:lat0]),
                      reads=[raw], writes=[dst])
            if not rope:
                if nt > lat0:
                    kb.op("pool", lambda e, t0=t0, lat0=lat0, nt=nt: e.tensor_copy(
                        dst[:, dst_j, t0 + lat0:t0 + nt], raw[:, lat0:nt]), reads=[raw], writes=[dst])
                continue
            p3 = PS[4 + ci % 2]
            n_l = nt - lat0
            lp = t0 + lat0 - C
            kb.op("pe", lambda e, p3=p3, lat0=lat0, nt=nt: e.matmul(p3[:, lat0:nt], rope_perm[:], raw[:, lat0:nt], start=True, stop=True),
                  reads=[rope_perm, raw], writes=[p3])
            kb.op("dve", lambda e, p3=p3, lat0=lat0, nt=nt, lp=lp, n_l=n_l: e.tensor_tensor(
                rot[:, lat0:nt], p3[:, lat0:nt], rope_sin[:, lp:lp + n_l], ALU.mult), reads=[p3, rope_sin], writes=[rot])
            kb.op("pool", lambda e, lat0=lat0, nt=nt, lp=lp, n_l=n_l: e.tensor_tensor(
                raw[:, lat0:nt], raw[:, lat0:nt], rope_cos[:, lp:lp + n_l], ALU.mult), reads=[raw, rope_cos], writes=[raw])
            kb.op("dve", lambda e, t0=t0, lat0=lat0, nt=nt: e.tensor_tensor(
                dst[:, dst_j, t0 + lat0:t0 + nt], raw[:, lat0:nt], rot[:, lat0:nt], ALU.add), reads=[raw, rot], writes=[dst])

    rope_cos = rope_sin = rope_perm = None

    def phase_attn(l, dense, with_ctx):
        nonlocal rope_cos, rope_sin, rope_perm
        base = FA0 if dense else WA0
        gbase = FAG0 if dense else WAG0
        mrow = 768 if dense else 512
        with kb.scope():
            wt = kb.sb("wt", [128, 8, 768], BF16)
            gq = kb.sb("gq", [128, 1])
            gk = kb.sb("gk", [128, 1])
            sink = kb.sb("sink", [128, 4])
            if dense:
                for hh in range(2):
                    kb.dma("sp", gq[hh * 64:(hh + 1) * 64, :], W["fa_q_norm"][l, :].rearrange("(d o) -> d o", o=1),
                           reads=[W["fa_q_norm"]], writes=[gq], slow=True)
                    kb.dma("sp", gk[hh * 64:(hh + 1) * 64, :], W["fa_k_norm"][l, :].rearrange("(d o) -> d o", o=1),
                           reads=[W["fa_k_norm"]], writes=[gk], slow=True)
            else:
                kb.dma("sp", sink[:], W["wa_sink"][l, :].partition_broadcast(128), reads=[W["wa_sink"]], writes=[sink])
            QT = kb.sb("QT", [128, 2, T], BF16)
            KT = kb.sb("KT", [128, 1, T], BF16)
            VW = 65 if dense else 64
            Vt = kb.sb("Vt", [128, NT, 2, VW], BF16)
            SG = None
            with kb.scope():
                rope_cos = kb.sb("rope_cos", [128, L])
                rope_sin = kb.sb("rope_sin", [128, L])
                rope_perm = kb.sb("rope_perm", [128, 128])
                kb.dma("sp", rope_cos[:], CT["rope_cos"][:, :], reads=[CT["rope_cos"]], writes=[rope_cos])
                kb.dma("pool", rope_sin[:], CT["rope_sin"][:, :], reads=[CT["rope_sin"]], writes=[rope_sin])
                kb.dma("sp", rope_perm[:], CT["rope_perm"][:, :], reads=[CT["rope_perm"]], writes=[rope_perm])
                stage = kb.sb("wstage", [128, 8, 256])
                w4 = W["w_in"][l, :, base:base + 256].rearrange("(j p) (h d) -> p j h d", p=128, d=64)
                st4 = stage[:, :, 0:256].rearrange("p j (h d) -> p j h d", d=64)
                for hi, h in enumerate((0, 2, 1, 3)):
                    kb.dma("sp", st4[:, :, hi, :], w4[:, :, h, :], reads=[W["w_in"]], writes=[stage])
                kb.op("pool", lambda e: e.tensor_copy(wt[:, :, 0:256], stage[:]), reads=[stage], writes=[wt])
                kb.dma("pool", stage[:], W["w_in"][l, :, base + 256:base + 512].rearrange("(j p) n -> p j n", p=128),
                       reads=[W["w_in"]], writes=[stage])
                kb.op("pool", lambda e: e.tensor_copy(wt[:, :, 256:512], stage[:]), reads=[stage], writes=[wt])
                kb.dma("sp", stage[:], W["w_in"][l, :, gbase:gbase + 256].rearrange("(j p) n -> p j n", p=128),
                       reads=[W["w_in"]], writes=[stage])
                kb.op("pool", lambda e: e.tensor_copy(wt[:, :, 512:768], stage[:]), reads=[stage], writes=[wt])
                tl = (kb.sb("qraw", [128, 512]), kb.sb("qsq", [128, 512]), kb.sb("qrs", [128, 512]), kb.sb("qrot", [128, 512]))
                qk_prep(l, tl, wt, 0, QT, 0, gq, dense, True)
                qk_prep(l, tl, wt, 128, QT, 1, gq, dense, True)
                qk_prep(l, tl, wt, 256, KT, 0, gk, dense, True)
            if dense:
                kb.op("pool", lambda e: e.memset(Vt[:, :, :, 64:65], 1.0), writes=[Vt])
            for i in range(NT):
                p = PS[i % 2]
                proj_tm(p, wt, 384, 128, i)
                kb.op("act", lambda e, p=p, i=i: e.copy(Vt[:, i, :, 0:64], p[:, 0:128].rearrange("p (k d) -> p k d", d=64)),
                      reads=[p], writes=[Vt])
            with kb.scope():
                if dense:
                    attn_dense(l, wt, QT, KT, Vt, mrow, with_ctx)
                else:
                    attn_window(l, wt, QT, KT, Vt, sink, mrow, with_ctx)

    def attn_dense(l, wt, QT, KT, Vt, mrow, with_ctx):
        pt = [kb.sb(f"pt{i}", [128, 512], BF16) for i in range(3)]
        osb = [kb.sb(f"osb{i}", [128, 512]) for i in range(2)]
        rc = [kb.sb(f"rc{i}", [128, 512]) for i in range(2)]
        ob = [kb.sb(f"ob{i}", [128, 512], BF16) for i in range(2)]
        it = 0
        sgt = [kb.sb(f"sgt{i}", [64, 512], BF16) for i in range(2)]
        chunks = []
        if with_ctx:
            chunks.append((0, C, 0, 2))
        for t0 in range(C, T, 512):
            chunks.append((t0, 512, 0, NT))
        for h in range(4):
            kv, pr = h // 2, h % 2
            ks = slice(64 * kv, 64 * kv + 64)
            for (t0, nt, kb0, kb1) in chunks:
                po = PS[4 + it % 2]
                sg = sgt[it % 2]
                pg = PS[3]
                for j in range(8):
                    kb.op("pe", lambda e, j=j: e.matmul(
                        pg[0:64, 0:nt], wt[:, j, 512 + 64 * h:576 + 64 * h], hT[:, j, t0:t0 + nt], start=(j == 0), stop=(j == 7)),
                        reads=[wt, hT], writes=[pg])
                kb.op("act", lambda e: e.activation(sg[0:64, 0:nt], pg[0:64, 0:nt], AF.Silu), reads=[pg], writes=[sg])
                def pv_(kbi):
                    ptt = pt[kbi % 3]
                    kb.op("pe", lambda e: e.matmul(
                        po[0:65, 0:nt], Vt[:, kbi, kv, 0:65], ptt[:, 0:nt], start=(kbi == kb0), stop=(kbi == kb1 - 1)),
                        reads=[Vt, ptt], writes=[po])
                for kbi in range(kb0, kb1):
                    psS = PS[kbi % 3]
                    ptt = pt[kbi % 3]
                    kb.op("pe", lambda e, psS=psS, kbi=kbi: e.matmul(
                        psS[:, 0:nt], KT[ks, 0, kbi * 128:(kbi + 1) * 128], QT[ks, pr, t0:t0 + nt], start=True, stop=True),
                        reads=[KT, QT], writes=[psS])
                    kb.op("act", lambda e, psS=psS, ptt=ptt: e.activation(ptt[:, 0:nt], psS[:, 0:nt], AF.Exp, scale=0.125),
                          reads=[psS], writes=[ptt])
                    if kbi - 1 >= kb0:
                        pv_(kbi - 1)
                pv_(kb1 - 1)
                o_s, r_c, o_b = osb[it % 2], rc[it % 2], ob[it % 2]
                kb.op("dve", lambda e: e.reciprocal(r_c[64:65, 0:nt], po[64:65, 0:nt]), reads=[po], writes=[r_c])
                kb.op("act", lambda e: e.copy(o_s[0:64, 0:nt], po[0:64, 0:nt]), reads=[po], writes=[o_s])
                pb = PS[6 + it % 2]
                kb.op("pe", lambda e: e.matmul(pb[0:64, 0:nt], ones_f[64:65, 0:64], r_c[64:65, 0:nt], start=True, stop=True),
                      reads=[ones_f, r_c], writes=[pb])
                kb.op("dve", lambda e: e.tensor_tensor(o_s[0:64, 0:nt], o_s[0:64, 0:nt], pb[0:64, 0:nt], ALU.mult),
                      reads=[o_s, pb], writes=[o_s])
                kb.op("pool", lambda e: e.tensor_tensor(o_b[0:64, 0:nt], o_s[0:64, 0:nt], sg[0:64, 0:nt], ALU.mult),
                      reads=[o_s, sg], writes=[o_b])
                kb.dma("pool", mixT[mrow + 64 * h:mrow + 64 * h + 64, t0:t0 + nt], o_b[0:64, 0:nt], reads=[o_b], writes=[Buf()])
                it += 1

    def attn_window(l, wt, QT, KT, Vt, sink, mrow, with_ctx):
        wmask = kb.sb("wmask", [128, 384])
        kb.dma("sp", wmask[:], CT["wmask"][:, :], reads=[CT["wmask"]], writes=[wmask])
        nsink = kb.sb("nsink", [128, 4])
        kb.op("dve", lambda e: e.tensor_scalar(nsink[:], sink[:], -1.0, None, ALU.mult), reads=[sink], writes=[nsink])
        S = [kb.sb(f"wS{i}", [128, 640]) for i in range(2)]
        P = [kb.sb(f"wP{i}", [128, 640]) for i in range(2)]
        Pn = [kb.sb(f"wPn{i}", [128, 640], BF16) for i in range(2)]
        PT = [kb.sb(f"wPT{i}", [128, 640], BF16) for i in range(2)]
        st = [kb.sb(f"wst{i}", [128, 8]) for i in range(2)]
        sgt = [kb.sb(f"wsg{i}", [64, 128], BF16) for i in range(2)]
        ob = [kb.sb(f"wob{i}", [64, 128], BF16) for i in range(2)]
        it = 0
        for i in range(0 if with_ctx else 2, NT):
            if i < 2:
                loc = []
            else:
                loc = list(range(max(2, i - 1), min(NT - 1, i + 1) + 1))
            nl = 128 * len(loc)
            m0 = 128 if (i >= 2 and i - 1 < 2) else 0
            nk = nl + C
            ktiles = loc + [0, 1]
            for h in range(4):
                kv, pr = h // 2, h % 2
                ks = slice(64 * kv, 64 * kv + 64)
                s_, p_, pn_, pt_, st_, sg, o_b = S[it % 2], P[it % 2], Pn[it % 2], PT[it % 2], st[it % 2], sgt[it % 2], ob[it % 2]
                psA, psB, psT, psO, psG = PS[it % 2], PS[2 + it % 2], PS[4 + it % 2], PS[6], PS[7]
                q_ap = QT[ks, pr, i * 128:(i + 1) * 128]
                if nl:
                    k0 = loc[0] * 128
                    kb.op("pe", lambda e: e.matmul(psA[:, 0:nl], q_ap, KT[ks, 0, k0:k0 + nl], start=True, stop=True),
                          reads=[QT, KT], writes=[psA])
                    kb.op("dve", lambda e: e.tensor_tensor(s_[:, 0:nl], psA[:, 0:nl], wmask[:, m0:m0 + nl], ALU.add),
                          reads=[psA, wmask], writes=[s_])
                kb.op("pe", lambda e: e.matmul(psB[:, 0:C], q_ap, KT[ks, 0, 0:C], start=True, stop=True),
                      reads=[QT, KT], writes=[psB])
                kb.op("act", lambda e: e.copy(s_[:, nl:nk], psB[:, 0:C]), reads=[psB], writes=[s_])
                kb.op("dve", lambda e: e.reduce_max(st_[:, 0:1], s_[:, 0:nk], AX.X), reads=[s_], writes=[st_])
                kb.op("dve", lambda e: e.tensor_scalar(st_[:, 1:2], st_[:, 0:1], -0.125, nsink[:, h:h + 1], ALU.mult, ALU.min),
                      reads=[st_, nsink], writes=[st_])
                kb.op("act", lambda e: e.activation(p_[:, 0:nk], s_[:, 0:nk], AF.Exp, bias=st_[:, 1:2], scale=0.125,
                                                    accum_out=st_[:, 2:3]), reads=[s_, st_], writes=[p_, st_])
                kb.op("act", lambda e: e.activation(st_[:, 3:4], sink[:, h:h + 1], AF.Exp, bias=st_[:, 1:2], scale=1.0),
                      reads=[sink, st_], writes=[st_])
                kb.op("dve", lambda e: e.tensor_tensor(st_[:, 4:5], st_[:, 2:3], st_[:, 3:4], ALU.add), reads=[st_], writes=[st_])
                kb.op("dve", lambda e: e.reciprocal(st_[:, 5:6], st_[:, 4:5]), reads=[st_], writes=[st_])
                kb.op("dve", lambda e: e.tensor_scalar(pn_[:, 0:nk], p_[:, 0:nk], st_[:, 5:6], None, ALU.mult),
                      reads=[p_, st_], writes=[pn_])
                pv = psT[:, :].bitcast(BF16)
                nb = nk // 128
                for b in range(nb):
                    kb.op("pe", lambda e, b=b: e.transpose(pv[:, b * 128:(b + 1) * 128], pn_[:, b * 128:(b + 1) * 128], ident_bf[:]),
                          reads=[pn_, ident_bf], writes=[psT])
                kb.op("act", lambda e: e.copy(pt_[:, 0:nk], pv[:, 0:nk]), reads=[psT], writes=[pt_])
                for b in range(nb):
                    kb.op("pe", lambda e, b=b: e.matmul(psO[0:64, 0:128], Vt[:, ktiles[b], kv, 0:64], pt_[:, b * 128:(b + 1) * 128],
                                                        start=(b == 0), stop=(b == nb - 1)), reads=[Vt, pt_], writes=[psO])
                for j in range(8):
                    kb.op("pe", lambda e, j=j: e.matmul(
                        psG[0:64, 0:128], wt[:, j, 512 + 64 * h:576 + 64 * h], hT[:, j, i * 128:(i + 1) * 128],
                        start=(j == 0), stop=(j == 7)), reads=[wt, hT], writes=[psG])
                kb.op("act", lambda e: e.activation(sg[:, :], psG[0:64, 0:128], AF.Silu), reads=[psG], writes=[sg])
                kb.op("dve", lambda e: e.tensor_tensor(o_b[:, :], psO[0:64, 0:128], sg[:, :], ALU.mult), reads=[psO, sg], writes=[o_b])
                kb.dma("sp", mixT[mrow + 64 * h:mrow + 64 * h + 64, i * 128:(i + 1) * 128], o_b[:, :], reads=[o_b], writes=[Buf()])
                it += 1

    def phase_out(l, last):
        with kb.scope():
            wo = kb.sb("wo", [128, 8, D], BF16)
            stage = kb.sb("wostage", [128, 8, 256])
            for q in range(4):
                kb.dma("sp", stage[:], W["w_out"][l, :, q * 256:(q + 1) * 256].rearrange("(j p) n -> p j n", p=128),
                       reads=[W["w_out"]], writes=[stage])
                kb.op("pool", lambda e, q=q: e.tensor_copy(wo[:, :, q * 256:(q + 1) * 256], stage[:]), reads=[stage], writes=[wo])
            fg = kb.sb("fg", [128, D])
            if last:
                kb.dma("sp", fg[:], W["final_g"][:].partition_broadcast(128), reads=[W["final_g"]], writes=[fg])
            mt = [kb.sb(f"mt{i}", [128, 8, 128], BF16) for i in range(2)]
            xt = [kb.sb(f"oxt{i}", [128, D]) for i in range(2)]
            xn = [kb.sb(f"oxn{i}", [128, D]) for i in range(2)]
            tmp = [kb.sb(f"otmp{i}", [128, 512]) for i in range(2)]
            st = [kb.sb(f"ost{i}", [128, 4]) for i in range(2)]
            junk = kb.sb("ojunk", [128, D])
            mixv = mixT.t.rearrange("(j p) t -> p j t", p=128)
            for it, i in enumerate(range(2 if last else 0, NT)):
                m, x, xo, s = mt[it % 2], xt[it % 2], xn[it % 2], st[it % 2]
                sel = 1 if i < 2 else 0
                kb.dma("sp", m[:], mixv[:, :, i * 128:(i + 1) * 128], reads=[mixT], writes=[m])
                src, srcb = x_src(l, i)
                kb.dma("pool", x[:], src, reads=[srcb], writes=[x])
                for hf in range(2):
                    p = PS[(2 * it + hf) % 8]
                    tp = tmp[hf]
                    for j in range(8):
                        kb.op("pe", lambda e, j=j, p=p, m=m, hf=hf: e.matmul(p[:, :], m[:, j, :], wo[:, j, hf * 512:(hf + 1) * 512],
                                                                     start=(j == 0), stop=(j == 7)), reads=[m, wo], writes=[p])
                    kb.op("dve", lambda e, p=p, tp=tp, hf=hf, sel=sel: e.tensor_tensor(
                        tp[:], p[:, :], GT[:, sel, hf * 512:(hf + 1) * 512], ALU.mult), reads=[p, GT], writes=[tp])
                    kb.op("pool", lambda e, tp=tp, hf=hf, x=x, xo=xo: e.tensor_tensor(
                        xo[:, hf * 512:(hf + 1) * 512], x[:, hf * 512:(hf + 1) * 512], tp[:], ALU.add), reads=[x, tp], writes=[xo])
                if not last:
                    kb.dma("sp", xres[i * 128:(i + 1) * 128, :], xo[:], reads=[xo], writes=[xres_b[i]])
                else:
                    kb.op("act", lambda e, xo=xo, s=s: e.activation(junk[:], xo[:], AF.Square, accum_out=s[:, 0:1]),
                          reads=[xo], writes=[junk, s])
                    kb.op("dve", lambda e, s=s: e.tensor_scalar(s[:, 1:2], s[:, 0:1], 1.0 / D, EPS, ALU.mult, ALU.add),
                          reads=[s], writes=[s])
                    kb.op("act", lambda e, s=s: e.sqrt(s[:, 2:3], s[:, 1:2]), reads=[s], writes=[s])
                    kb.op("dve", lambda e, s=s: e.reciprocal(s[:, 3:4], s[:, 2:3]), reads=[s], writes=[s])
                    kb.op("dve", lambda e, xo=xo, s=s, x=x: e.scalar_tensor_tensor(
                        x[:], xo[:], s[:, 3:4], fg[:], ALU.mult, ALU.mult), reads=[xo, s, fg], writes=[x])
                    kb.dma("sp", out[(i - 2) * 128:(i - 1) * 128, :], x[:], reads=[x], writes=[Buf()])


    def conv_tile(l, wt, cw, ncw, jt, Zraw, Zout):
        for ci, (t0, nt) in enumerate(TCH):
            p = PS[ci % 4]
            proj_fm(p, wt, 0, 128, t0, nt)
            kb.op("act", lambda e, p=p, t0=t0, nt=nt: e.copy(Zraw[:, 1 + t0:1 + t0 + nt], p[:, 0:nt]), reads=[p], writes=[Zraw])
        kb.op("dve", lambda e: e.tensor_scalar(Zout[:, :], Zraw[:, 1:T + 1], cw[:, jt, 1:2], None, ALU.mult), reads=[Zraw, cw], writes=[Zout])
        kb.op("dve", lambda e: e.scalar_tensor_tensor(Zout[:, :], Zraw[:, 0:T], cw[:, jt, 0:1], Zout[:, :], ALU.mult, ALU.add),
              reads=[Zraw, cw, Zout], writes=[Zout])
        kb.op("dve", lambda e: e.scalar_tensor_tensor(Zout[:, :], Zraw[:, 2:T + 2], cw[:, jt, 2:3], Zout[:, :], ALU.mult, ALU.add),
              reads=[Zraw, cw, Zout], writes=[Zout])
        kb.op("dve", lambda e: e.scalar_tensor_tensor(Zout[:, C - 1:C], Zraw[:, C + 1:C + 2], ncw[:, jt, 2:3], Zout[:, C - 1:C], ALU.mult, ALU.add),
              reads=[Zraw, ncw, Zout], writes=[Zout])
        kb.op("dve", lambda e: e.scalar_tensor_tensor(Zout[:, C:C + 1], Zraw[:, C:C + 1], ncw[:, jt, 0:1], Zout[:, C:C + 1], ALU.mult, ALU.add),
              reads=[Zraw, ncw, Zout], writes=[Zout])

    def colvec(name, src_ap, srcb, shape, rearr, **kw):
        t = kb.sb(name, shape)
        kb.dma("sp", t[:], src_ap.rearrange(rearr, **kw), reads=[srcb], writes=[t], slow=True)
        return t

    def phase_rwkv_prep(l):
        with kb.scope():
            stage = kb.sb("rstage", [128, 8, 128])
            wts = [kb.sb(f"rwt{i}", [128, 8, 128], BF16) for i in range(2)]
            cw = kb.sb("rcw", [128, 7, 3])
            for k in range(3):
                kb.dma("sp", cw[:, :, k], W["rw_conv"][l, k, :].rearrange("(j p) -> p j", p=128), reads=[W["rw_conv"]], writes=[cw], slow=True)
            ncw = kb.sb("rncw", [128, 7, 3])
            kb.op("dve", lambda e: e.tensor_scalar(ncw[:], cw[:], -1.0, None, ALU.mult), reads=[cw], writes=[ncw])
            kk_ = colvec("rkk", W["rw_k_k"][l, :], W["rw_k_k"], [128, 2], "(j p) -> p j", p=128)
            ka_ = colvec("rka", W["rw_k_a"][l, :], W["rw_k_a"], [128, 2], "(j p) -> p j", p=128)
            omka = kb.sb("romka", [128, 2])
            kb.op("dve", lambda e: e.tensor_scalar(omka[:], ka_[:], -1.0, 1.0, ALU.mult, ALU.add), reads=[ka_], writes=[omka])
            w0_ = kb.sb("rw0", [128, 2, 2])
            a0_ = kb.sb("ra0", [128, 2, 2])
            for d in range(2):
                kb.dma("sp", w0_[:, d, :], W["rw_w0"][l, d, :].rearrange("(j p) -> p j", p=128), reads=[W["rw_w0"]], writes=[w0_], slow=True)
                kb.dma("sp", a0_[:, d, :], W["rw_a0"][l, d, :].rearrange("(j p) -> p j", p=128), reads=[W["rw_a0"]], writes=[a0_], slow=True)
            wup = kb.sb("rwup", [128, 2, 256])
            kb.dma("sp", wup[0:64, :, :], W["rw_w_up"][l, :, :, :].rearrange("d k n -> k d n"), reads=[W["rw_w_up"]], writes=[wup])
            kb.dma("sp", wup[64:128, :, :], W["rw_a_up"][l, :, :, :].rearrange("d k n -> k d n"), reads=[W["rw_a_up"]], writes=[wup])
            Zraw = kb.sb("rZraw", [128, T + 2])
            Zout = kb.sb("rZout", [128, T])
            Z6 = kb.sb("rZ6", [128, T])
            kb.op("pool", lambda e: e.memset(Zraw[:, 0:1], 0.0), writes=[Zraw])
            kb.op("pool", lambda e: e.memset(Zraw[:, T + 1:T + 2], 0.0), writes=[Zraw])
            tA = [kb.sb(f"rtA{i}", [128, 512]) for i in range(2)]
            tB = [kb.sb(f"rtB{i}", [128, 512]) for i in range(2)]
            tC = [kb.sb(f"rtC{i}", [128, 512]) for i in range(2)]
            tD = [kb.sb(f"rtD{i}", [128, 512]) for i in range(2)]
            tE = [kb.sb(f"rtE{i}", [128, 512]) for i in range(2)]
            tG = [kb.sb(f"rtG{i}", [128, 512], BF16) for i in range(2)]
            vt_ = [kb.sb(f"rvt{i}", [128, 128]) for i in range(2)]
            order = [6, 0, 1, 4, 5, 2, 3, 7, 8]
            for oi, jt in enumerate(order):
                wt = wts[oi % 2]
                c0 = RW0 + jt * 128 if jt < 7 else RWG0 + (jt - 7) * 128
                load_w(l, wt, c0, 128, stage)
                if jt >= 7:
                    for ci, (t0, nt) in enumerate(TCH):
                        p = PS[ci % 4]
                        proj_fm(p, wt, 0, 128, t0, nt)
                        g = tG[ci % 2]
                        kb.op("act", lambda e, p=p, g=g, nt=nt: e.activation(g[:, 0:nt], p[:, 0:nt], AF.Silu), reads=[p], writes=[g])
                        kb.dma("sp", RS["SGT"][(jt - 7) * 128:(jt - 6) * 128, t0:t0 + nt], g[:, 0:nt], reads=[g], writes=[Buf()])
                    continue
                conv_tile(l, wt, cw, ncw, jt, Zraw, Z6 if jt == 6 else Zout)
                if jt == 6:
                    kb.op("act", lambda e: e.activation(Z6[0:64, :], Z6[0:64, :], AF.Tanh), reads=[Z6], writes=[Z6])
                elif jt in (0, 1):
                    kb.dma("sp", RS["RT"][jt * 128:(jt + 1) * 128, :], Zout[:, :], reads=[Zout], writes=[Buf()])
                elif jt in (4, 5):
                    kb.dma("sp", RS["VT"][(jt - 4) * 128:(jt - 3) * 128, :], Zout[:, :], reads=[Zout], writes=[Buf()])
                    for i in range(NT):
                        p = PS[4 + i % 2]
                        kb.op("pe", lambda e, p=p, i=i: e.transpose(p[:, 0:128], Zout[:, i * 128:(i + 1) * 128], ident_f[:]),
                              reads=[Zout, ident_f], writes=[p])
                        v = vt_[i % 2]
                        kb.op("act", lambda e, p=p, v=v: e.copy(v[:, :], p[:, 0:128]), reads=[p], writes=[v])
                        kb.dma("pool", RS["VTOK"][i * 128:(i + 1) * 128, (jt - 4) * 128:(jt - 3) * 128], v[:, :], reads=[v], writes=[Buf()])
                else:
                    pt = jt - 2
                    rows = slice(pt * 128, (pt + 1) * 128)
                    for ci, (t0, nt) in enumerate(TCH):
                        a_, b_, c_, d_, e_ = tA[ci % 2], tB[ci % 2], tC[ci % 2], tD[ci % 2], tE[ci % 2]
                        zc = Zout[:, t0:t0 + nt]
                        kb.op("dve", lambda e: e.tensor_scalar(a_[:, 0:nt], zc, kk_[:, pt:pt + 1], None, ALU.mult), reads=[Zout, kk_], writes=[a_])
                        kb.op("act", lambda e: e.activation(b_[:, 0:nt], a_[:, 0:nt], AF.Square), reads=[a_], writes=[b_])
                        p = PS[ci % 2]
                        kb.op("pe", lambda e: e.matmul(p[:, 0:nt], blk64[:], b_[:, 0:nt], start=True, stop=True), reads=[blk64, b_], writes=[p])
                        kb.op("act", lambda e: e.sqrt(b_[:, 0:nt], p[:, 0:nt]), reads=[p], writes=[b_])
                        kb.op("dve", lambda e: e.tensor_scalar(b_[:, 0:nt], b_[:, 0:nt], 1e-12, None, ALU.max), reads=[b_], writes=[b_])
                        kb.op("dve", lambda e: e.reciprocal(b_[:, 0:nt], b_[:, 0:nt]), reads=[b_], writes=[b_])
                        kb.op("dve", lambda e: e.scalar_tensor_tensor(a_[:, 0:nt], a_[:, 0:nt], -1.0, b_[:, 0:nt], ALU.mult, ALU.mult),
                              reads=[a_, b_], writes=[a_])
                        kb.dma("sp", RS["AL"][rows, t0:t0 + nt], a_[:, 0:nt], reads=[a_], writes=[Buf()])
                        for d in range(2):
                            pa = PS[2 + d]
                            kb.op("pe", lambda e: e.matmul(pa[:, 0:nt], wup[64:128, d, pt * 128:(pt + 1) * 128], Z6[64:128, t0:t0 + nt],
                                                           start=True, stop=True), reads=[wup, Z6], writes=[pa])
                            kb.op("act", lambda e: e.activation(c_[:, 0:nt], pa[:, 0:nt], AF.Sigmoid, bias=a0_[:, d, pt:pt + 1]),
                                  reads=[pa, a0_], writes=[c_])
                            kb.op("dve", lambda e: e.scalar_tensor_tensor(d_[:, 0:nt], c_[:, 0:nt], -1.0, a_[:, 0:nt], ALU.mult, ALU.mult),
                                  reads=[c_, a_], writes=[d_])
                            kb.dma("sp", RS[f"B{d}"][rows, t0:t0 + nt], d_[:, 0:nt], reads=[d_], writes=[Buf()])
                            kb.op("dve", lambda e: e.tensor_scalar(c_[:, 0:nt], c_[:, 0:nt], ka_[:, pt:pt + 1], omka[:, pt:pt + 1], ALU.mult, ALU.add),
                                  reads=[c_, ka_, omka], writes=[c_])
                            kb.op("dve", lambda e: e.tensor_tensor(e_[:, 0:nt], c_[:, 0:nt], zc, ALU.mult), reads=[c_, Zout], writes=[e_])
                            kb.dma("pool", RS[f"KD{d}"][rows, t0:t0 + nt], e_[:, 0:nt], reads=[e_], writes=[Buf()])
                            pw = PS[4 + d]
                            kb.op("pe", lambda e: e.matmul(pw[:, 0:nt], wup[0:64, d, pt * 128:(pt + 1) * 128], Z6[0:64, t0:t0 + nt],
                                                           start=True, stop=True), reads=[wup, Z6], writes=[pw])
                            kb.op("act", lambda e: e.activation(c_[:, 0:nt], pw[:, 0:nt], AF.Sigmoid, bias=w0_[:, d, pt:pt + 1]),
                                  reads=[pw, w0_], writes=[c_])
                            kb.op("dve", lambda e: e.tensor_scalar(d_[:, 0:nt], c_[:, 0:nt], -math.exp(-0.5), None, ALU.mult),
                                  reads=[c_], writes=[d_])
                            kb.dma("pool", RS[f"W{d}"][rows, t0:t0 + nt], d_[:, 0:nt], reads=[d_], writes=[Buf()])

    def phase_rwkv_scan(l):
        with kb.scope():
            ST = [kb.sb(f"ST{d}", [128, 2, 64]) for d in range(2)]
            for d in range(2):
                kb.op("pool", lambda e, d=d: e.memset(ST[d][:], 0.0), writes=[ST[d]])
            names = ("AL", "W", "B", "KD", "RT")
            ch = [[{n: kb.sb(f"c{n}{d}{i}", [128, 2, 128]) for n in names} for i in range(2)] for d in range(2)]
            vch = [[kb.sb(f"cV{d}{i}", [128, 256]) for i in range(2)] for d in range(2)]
            t1 = [kb.sb(f"st1{d}", [128, 2, 64]) for d in range(2)]
            t2 = [kb.sb(f"st2{d}", [128, 2, 64]) for d in range(2)]
            ysb = [kb.sb(f"ysb{d}", [64, 512]) for d in range(2)]
            psSA, psV, psY = [PS[0], PS[1]], [PS[2], PS[3]], [PS[4], PS[5]]
            border = [1, 0] + list(range(NT - 1, 1, -1))
            for ci in range(NT):
                cidx = [ci, border[ci]]
                cur = []
                for d in range(2):
                    c0 = cidx[d] * 128
                    tl_ = ch[d][ci % 2]
                    for n in names:
                        src = RS[n if n in ("AL", "RT") else f"{n}{d}"]
                        kb.dma("sp" if d == 0 else "pool", tl_[n][:],
                               src.t.rearrange("(pr q) t -> q pr t", q=128)[:, :, c0:c0 + 128], reads=[src], writes=[tl_[n]])
                    vv = vch[d][ci % 2]
                    kb.dma("sp" if d == 0 else "pool", vv[:], RS["VTOK"][c0:c0 + 128, :], reads=[RS["VTOK"]], writes=[vv])
                    cur.append((tl_, vv))
                for tl in range(128):
                    for d in range(2):
                        col = tl if d == 0 else 127 - tl
                        tl_, vv = cur[d]
                        S_, sa, pv, py = ST[d], psSA[d], psV[d], psY[d]
                        for pr in range(2):
                            for hp in range(2):
                                rows = slice(64 * hp, 64 * hp + 64)
                                kb.op("pe", lambda e, pr=pr, rows=rows: e.matmul(
                                    sa[rows, pr * 64:(pr + 1) * 64], tl_["AL"][rows, pr, col:col + 1].broadcast_to([64, 64]),
                                    S_[rows, pr, :], start=True, stop=True), reads=[tl_["AL"], S_], writes=[sa])
                        for pr in range(2):
                            for hp in range(2):
                                rows = slice(64 * hp, 64 * hp + 64)
                                h = 2 * pr + hp
                                kb.op("pe", lambda e, pr=pr, rows=rows, h=h: e.matmul(
                                    pv[rows, pr * 64:(pr + 1) * 64], ident_f[:, col:col + 1].broadcast_to([128, 64]),
                                    vv[:, h * 64:(h + 1) * 64], start=True, stop=True), reads=[ident_f, vv], writes=[pv])
                        for pr in range(2):
                            kb.op("dve", lambda e, pr=pr: e.tensor_scalar(
                                t1[d][:, pr, :], sa[:, pr * 64:(pr + 1) * 64], tl_["B"][:, pr, col:col + 1], None, ALU.mult),
                                reads=[sa, tl_["B"]], writes=[t1[d]])
                            kb.op("dve", lambda e, pr=pr: e.scalar_tensor_tensor(
                                t2[d][:, pr, :], pv[:, pr * 64:(pr + 1) * 64], tl_["KD"][:, pr, col:col + 1], t1[d][:, pr, :], ALU.mult, ALU.add),
                                reads=[pv, tl_["KD"], t1[d]], writes=[t2[d]])
                            kb.op("dve", lambda e, pr=pr: e.scalar_tensor_tensor(
                                S_[:, pr, :], S_[:, pr, :], tl_["W"][:, pr, col:col + 1], t2[d][:, pr, :], ALU.mult, ALU.add),
                                reads=[S_, tl_["W"], t2[d]], writes=[S_])
                        for pr in range(2):
                            for hp in range(2):
                                rows = slice(64 * hp, 64 * hp + 64)
                                h = 2 * pr + hp
                                kb.op("pe", lambda e, pr=pr, rows=rows, h=h: e.matmul(
                                    py[0:64, h * 128 + col:h * 128 + col + 1], S_[rows, pr, :], tl_["RT"][rows, pr, col:col + 1],
                                    start=True, stop=True), reads=[S_, tl_["RT"]], writes=[py])
                for d in range(2):
                    c0 = cidx[d] * 128
                    kb.op("act", lambda e, d=d: e.copy(ysb[d][:, :], psY[d][0:64, :]), reads=[psY[d]], writes=[ysb[d]])
                    dst = RS["YF" if d == 0 else "YB"]
                    kb.dma("sp", dst.t.rearrange("(h v) t -> v h t", v=64)[:, :, c0:c0 + 128],
                           ysb[d][:, :].rearrange("v (h t) -> v h t", h=4), reads=[ysb[d]], writes=[Buf()])


    def phase_rwkv_chunked(l):
        CH = 64
        NCH = T // CH
        with kb.scope():
            def ldc(nm, shape):
                t = kb.sb("k" + nm, shape)
                kb.dma("sp", t[:], CT[nm].t, reads=[CT[nm]], writes=[t])
                return t
            Ms = ldc("rw_ms", [128, 2, 64]); MTs = ldc("rw_mts", [128, 2, 64]); MTi = ldc("rw_mti", [128, 2, 64])
            id2 = ldc("rw_id2", [128, 64])
            ones = kb.sb("rones", [128, 64])
            kb.op("pool", lambda e: e.memset(ones[:], 1.0), writes=[ones])
            ST = kb.sb("cST", [128, 4, 64])
            kb.op("pool", lambda e: e.memset(ST[:], 0.0), writes=[ST])
            names = ("AL", "W", "B", "KD", "RT")
            def t4(nm, n=2, w=64):
                return [kb.sb(f"{nm}{i}", [128, 4, w]) for i in range(n)]
            IN = {n: t4("ci" + n) for n in names}
            VTK = t4("cVTK")
            CS = t4("cCS", 1)[0]; TOT = kb.sb("cTOT", [128, 4]); TMP = t4("cTMP", 1)[0]
            Epos = t4("cEp", 1)[0]; Eneg = t4("cEn", 1)[0]; Eprev = t4("cEv", 1)[0]; Etot = t4("cEt", 1)[0]; Wtot = kb.sb("cWt", [128, 4])
            Ab = t4("cAb", 1)[0]; Bb = t4("cBb", 1)[0]; Kb = t4("cKb", 1)[0]; Rb = t4("cRb", 1)[0]; Bt = t4("cBt", 1)[0]; Kt = t4("cKt", 1)[0]
            Q = t4("cQ"); P = t4("cP"); ArbT = t4("cArbT", 1)[0]; AkvT = t4("cAkvT", 1)[0]; ArkT = t4("cArkT", 1)[0]
            X = t4("cX", 2, 128); Btok = t4("cBtok", 1)[0]; Ktok = t4("cKtok", 1)[0]
            RAT = t4("cRAT", 1)[0]; McT = t4("cMcT", 1)[0]; NcS = t4("cNcS", 1)[0]; DG = t4("cDG", 1)[0]
            ysb = [kb.sb(f"cysb{d}", [64, 256]) for d in range(2)]
            border = [3, 2, 1, 0] + list(range(NCH - 1, 3, -1))
            DP = [(d, pr) for d in range(2) for pr in range(2)]
            HP = [slice(0, 64), slice(64, 128)]

            def mm_all(ps, col_fn, lhs_fn, rhs_fn, reads, start=True, stop=True, w=None):
                for dp in range(4):
                    for hp in range(2):
                        r = HP[hp]
                        c0, c1 = col_fn(dp)
                        kb.op("pe", lambda e, dp=dp, r=r, c0=c0, c1=c1: e.matmul(ps[r, c0:c1], lhs_fn(dp, r), rhs_fn(dp, r), start=start, stop=stop),
                              reads=reads, writes=[ps])

            for ci in range(NCH):
                cidx = [ci, border[ci]]
                i2 = ci % 2
                for d in range(2):
                    c0 = cidx[d] * CH
                    for n in names:
                        src = RS[n if n in ("AL", "RT") else f"{n}{d}"]
                        kb.dma("sp" if d == 0 else "pool", IN[n][i2][:, 2 * d:2 * d + 2, :],
                               src.t.rearrange("(pr q) t -> q pr t", q=128)[:, :, c0:c0 + CH], reads=[src], writes=[IN[n][i2]])
                    for hp in range(2):
                        kb.dma("sp" if d == 0 else "pool", VTK[i2][HP[hp], 2 * d:2 * d + 2, :],
                               RS["VTOK"][c0:c0 + CH, :].rearrange("t (pr hp v) -> t pr hp v", pr=2, hp=2)[:, :, hp, :],
                               reads=[RS["VTOK"]], writes=[VTK[i2]])
                al, lw, be, kd, rt, vt = IN["AL"][i2], IN["W"][i2], IN["B"][i2], IN["KD"][i2], IN["RT"][i2], VTK[i2]
                if RW_STAGE <= 1:
                    continue
                for dp in range(4):
                    kb.op("dve", lambda e, dp=dp: e.tensor_tensor_scan(CS[:, dp, :], ones[:, :], lw[:, dp, :], 0.0, ALU.mult, ALU.add),
                          reads=[ones, lw], writes=[CS])
                kb.op("dve", lambda e: e.tensor_copy(TOT[:, :], CS[:, :, CH - 1]), reads=[CS], writes=[TOT])
                kb.op("dve", lambda e: e.tensor_tensor(CS[:, 2:4, :], lw[:, 2:4, :], CS[:, 2:4, :], ALU.subtract), reads=[lw, CS], writes=[CS])
                kb.op("dve", lambda e: e.tensor_tensor(CS[:, 2:4, :], CS[:, 2:4, :], TOT[:, 2:4].unsqueeze(2).broadcast_to([128, 2, CH]), ALU.add),
                      reads=[CS, TOT], writes=[CS])
                kb.op("act", lambda e: e.activation(Epos[:], CS[:], AF.Exp), reads=[CS], writes=[Epos])
                kb.op("act", lambda e: e.activation(Eneg[:], CS[:], AF.Exp, scale=-1.0), reads=[CS], writes=[Eneg])
                kb.op("pool", lambda e: e.tensor_tensor(TMP[:], CS[:], lw[:], ALU.subtract), reads=[CS, lw], writes=[TMP])
                kb.op("act", lambda e: e.activation(Eprev[:], TMP[:], AF.Exp), reads=[TMP], writes=[Eprev])
                kb.op("dve", lambda e: e.tensor_tensor(Etot[:], TOT[:, :].unsqueeze(2).broadcast_to([128, 4, CH]), CS[:], ALU.subtract),
                      reads=[TOT, CS], writes=[Etot])
                kb.op("act", lambda e: e.activation(Etot[:], Etot[:], AF.Exp), reads=[Etot], writes=[Etot])
                kb.op("act", lambda e: e.activation(Wtot[:], TOT[:], AF.Exp), reads=[TOT], writes=[Wtot])
                kb.op("dve", lambda e: e.tensor_tensor(Ab[:], al[:], Eprev[:], ALU.mult), reads=[al, Eprev], writes=[Ab])
                kb.op("pool", lambda e: e.tensor_tensor(Bb[:], be[:], Eneg[:], ALU.mult), reads=[be, Eneg], writes=[Bb])
                kb.op("dve", lambda e: e.tensor_tensor(Kb[:], kd[:], Eneg[:], ALU.mult), reads=[kd, Eneg], writes=[Kb])
                kb.op("pool", lambda e: e.tensor_tensor(Rb[:], rt[:], Epos[:], ALU.mult), reads=[rt, Epos], writes=[Rb])
                kb.op("dve", lambda e: e.tensor_tensor(Bt[:], be[:], Etot[:], ALU.mult), reads=[be, Etot], writes=[Bt])
                kb.op("pool", lambda e: e.tensor_tensor(Kt[:], kd[:], Etot[:], ALU.mult), reads=[kd, Etot], writes=[Kt])
                if RW_STAGE <= 2:
                    continue
                PA, PB, PC, PT1, PD, PX, PPQ, PE_ = PS
                mm_all(PA, lambda dp: (dp * 128, dp * 128 + 64), lambda dp, r: Bb[r, dp, :], lambda dp, r: Ab[r, dp, :], [Bb, Ab])
                mm_all(PA, lambda dp: (dp * 128 + 64, dp * 128 + 128), lambda dp, r: Bb[r, dp, :], lambda dp, r: Rb[r, dp, :], [Bb, Rb])
                mm_all(PB, lambda dp: (dp * 128, dp * 128 + 64), lambda dp, r: Kb[r, dp, :], lambda dp, r: Ab[r, dp, :], [Kb, Ab])
                mm_all(PB, lambda dp: (dp * 128 + 64, dp * 128 + 128), lambda dp, r: Kb[r, dp, :], lambda dp, r: Rb[r, dp, :], [Kb, Rb])
                mm_all(PC, lambda dp: (dp * 64, dp * 64 + 64), lambda dp, r: Ab[r, dp, :], lambda dp, r: Bb[r, dp, :], [Ab, Bb])
                q0, p0 = Q[0], P[0]
                pav = PA[:, :].rearrange("p (dp x) -> p dp x", dp=4)
                pbv = PB[:, :].rearrange("p (dp x) -> p dp x", dp=4)
                def mk(m):
                    return m[:, :, :].unsqueeze(2).broadcast_to([128, 2, 2, 64])
                def v4(ap):
                    return ap.rearrange("p (d pr) x -> p d pr x", d=2)
                kb.op("dve", lambda e: e.tensor_tensor(v4(q0[:]), v4(pav[:, :, 0:64]), mk(MTs), ALU.mult), reads=[PA, MTs], writes=[q0])
                kb.op("dve", lambda e: e.tensor_tensor(v4(ArbT[:]), v4(pav[:, :, 64:128]), mk(MTi), ALU.mult), reads=[PA, MTi], writes=[ArbT])
                kb.op("dve", lambda e: e.tensor_tensor(v4(AkvT[:]), v4(pbv[:, :, 0:64]), mk(MTs), ALU.mult), reads=[PB, MTs], writes=[AkvT])
                kb.op("dve", lambda e: e.tensor_tensor(v4(ArkT[:]), v4(pbv[:, :, 64:128]), mk(MTi), ALU.mult), reads=[PB, MTi], writes=[ArkT])
                kb.op("dve", lambda e: e.tensor_tensor(v4(p0[:]), v4(PC[:, 0:256].rearrange("p (dp x) -> p dp x", dp=4)), mk(Ms), ALU.mult),
                      reads=[PC, Ms], writes=[p0])
                if RW_STAGE <= 3:
                    continue
                def idb(r):
                    return ident_f[r, r.start:r.start + 64]
                mm_all(PT1, lambda dp: (dp * 128, dp * 128 + 64), lambda dp, r: Ab[r, dp, :], lambda dp, r: idb(r), [Ab, ident_f])
                mm_all(PT1, lambda dp: (dp * 128 + 64, dp * 128 + 128), lambda dp, r: Bt[r, dp, :], lambda dp, r: idb(r), [Bt, ident_f])
                mm_all(PC, lambda dp: (256 + dp * 64, 256 + dp * 64 + 64), lambda dp, r: Kt[r, dp, :], lambda dp, r: idb(r), [Kt, ident_f])
                x0 = X[0]
                pt1v = PT1[:, :].rearrange("p (dp x) -> p dp x", dp=4)
                kb.op("act", lambda e: e.copy(x0[:, :, 0:64], pt1v[:, :, 0:64]), reads=[PT1], writes=[x0])
                kb.op("act", lambda e: e.copy(Btok[:], pt1v[:, :, 64:128]), reads=[PT1], writes=[Btok])
                kb.op("act", lambda e: e.copy(Ktok[:], PC[:, 256:512].rearrange("p (dp x) -> p dp x", dp=4)), reads=[PC], writes=[Ktok])
                if RW_STAGE <= 4:
                    continue
                mm_all(PD, lambda dp: (dp * 64, dp * 64 + 64), lambda dp, r: AkvT[r, dp, :], lambda dp, r: vt[r, dp, :], [AkvT, vt])
                kb.op("act", lambda e: e.copy(x0[:, :, 64:128], PD[:, 0:256].rearrange("p (dp x) -> p dp x", dp=4)), reads=[PD], writes=[x0])
                if RW_STAGE <= 5:
                    continue
                qc, pc, xc = Q[0], P[0], X[0]
                for j in range(6):
                    qn, pn, xn = Q[(j + 1) % 2], P[(j + 1) % 2], X[(j + 1) % 2]
                    mm_all(PX, lambda dp: (dp * 128, dp * 128 + 128), lambda dp, r: qc[r, dp, :], lambda dp, r: xc[r, dp, :], [qc, xc])
                    kb.op("dve", lambda e, xn=xn, xc=xc: e.tensor_tensor(xn[:], xc[:], PX[:, :].rearrange("p (dp x) -> p dp x", dp=4), ALU.add),
                          reads=[xc, PX], writes=[xn])
                    if j < 5:
                        mm_all(PPQ, lambda dp: (dp * 64, dp * 64 + 64), lambda dp, r: qc[r, dp, :], lambda dp, r: pc[r, dp, :], [qc, pc])
                        mm_all(PPQ, lambda dp: (256 + dp * 64, 256 + dp * 64 + 64), lambda dp, r: pc[r, dp, :], lambda dp, r: qc[r, dp, :], [qc, pc])
                        kb.op("act", lambda e, pn=pn: e.copy(pn[:], PPQ[:, 0:256].rearrange("p (dp x) -> p dp x", dp=4)), reads=[PPQ], writes=[pn])
                        kb.op("act", lambda e, qn=qn: e.copy(qn[:], PPQ[:, 256:512].rearrange("p (dp x) -> p dp x", dp=4)), reads=[PPQ], writes=[qn])
                    qc, pc, xc = qn, pn, xn
                if RW_STAGE <= 6:
                    continue
                mm_all(PD, lambda dp: (256 + dp * 64, 256 + dp * 64 + 64), lambda dp, r: xc[r, dp, 0:64], lambda dp, r: ArbT[r, dp, :], [xc, ArbT])
                kb.op("dve", lambda e: e.tensor_tensor(RAT[:], Rb[:], PD[:, 256:512].rearrange("p (dp x) -> p dp x", dp=4), ALU.add),
                      reads=[Rb, PD], writes=[RAT])
                mm_all(PE_, lambda dp: (dp * 64, dp * 64 + 64), lambda dp, r: xc[r, dp, 0:64], lambda dp, r: Btok[r, dp, :], [xc, Btok])
                kb.op("pool", lambda e: e.tensor_tensor(DG[:], id2[:, :].unsqueeze(1).broadcast_to([128, 4, 64]),
                                                        Wtot[:, :].unsqueeze(2).broadcast_to([128, 4, 64]), ALU.mult), reads=[id2, Wtot], writes=[DG])
                kb.op("dve", lambda e: e.tensor_tensor(McT[:], DG[:], PE_[:, 0:256].rearrange("p (dp x) -> p dp x", dp=4), ALU.add),
                      reads=[DG, PE_], writes=[McT])
                for dp in range(4):
                    for hp in range(2):
                        r = HP[hp]
                        c0 = 256 + dp * 64
                        kb.op("pe", lambda e, dp=dp, r=r, c0=c0: e.matmul(PE_[r, c0:c0 + 64], Btok[r, dp, :], xc[r, dp, 64:128], start=True, stop=False),
                              reads=[Btok, xc], writes=[PE_])
                        kb.op("pe", lambda e, dp=dp, r=r, c0=c0: e.matmul(PE_[r, c0:c0 + 64], Ktok[r, dp, :], vt[r, dp, :], start=False, stop=True),
                              reads=[Ktok, vt], writes=[PE_])
                kb.op("act", lambda e: e.copy(NcS[:], PE_[:, 256:512].rearrange("p (dp x) -> p dp x", dp=4)), reads=[PE_], writes=[NcS])
                if RW_STAGE <= 7:
                    continue
                PYs = [PA, PT1]
                for dp in range(4):
                    for hp in range(2):
                        r = HP[hp]
                        PY = PYs[hp]
                        c0 = dp * 64
                        kb.op("pe", lambda e, dp=dp, r=r, c0=c0, PY=PY: e.matmul(PY[0:64, c0:c0 + 64], ST[r, dp, :], RAT[r, dp, :], start=True, stop=False),
                              reads=[ST, RAT], writes=[PY])
                        kb.op("pe", lambda e, dp=dp, r=r, c0=c0, PY=PY: e.matmul(PY[0:64, c0:c0 + 64], xc[r, dp, 64:128], ArbT[r, dp, :], start=False, stop=False),
                              reads=[xc, ArbT], writes=[PY])
                        kb.op("pe", lambda e, dp=dp, r=r, c0=c0, PY=PY: e.matmul(PY[0:64, c0:c0 + 64], vt[r, dp, :], ArkT[r, dp, :], start=False, stop=True),
                              reads=[vt, ArkT], writes=[PY])
                for d in range(2):
                    c0 = cidx[d] * CH
                    yv = ysb[d][:, :].rearrange("v (pr hp t) -> v pr hp t", pr=2, hp=2)
                    for hp in range(2):
                        kb.op("act", lambda e, d=d, hp=hp, yv=yv: e.copy(
                            yv[:, :, hp, :], PYs[hp][0:64, d * 128:(d + 1) * 128].rearrange("v (pr t) -> v pr t", pr=2)), reads=[PYs[hp]], writes=[ysb[d]])
                    dst = RS["YF" if d == 0 else "YB"]
                    kb.dma("sp", dst.t.rearrange("(h v) t -> v h t", v=64)[:, :, c0:c0 + CH],
                           ysb[d][:, :].rearrange("v (h t) -> v h t", h=4), reads=[ysb[d]], writes=[Buf()])
                if RW_STAGE <= 8:
                    continue
                PSS = PB
                mm_all(PSS, lambda dp: (dp * 64, dp * 64 + 64), lambda dp, r: McT[r, dp, :], lambda dp, r: ST[r, dp, :], [McT, ST])
                kb.op("dve", lambda e: e.tensor_tensor(ST[:], NcS[:], PSS[:, 0:256].rearrange("p (dp x) -> p dp x", dp=4), ALU.add),
                      reads=[NcS, PSS], writes=[ST])


    def phase_rwkv_chunked3(l):
        CH = 64
        NCH = T // CH
        with kb.scope():
            def ldc(nm, shape):
                t = kb.sb("k" + nm, shape)
                kb.dma("sp", t[:], CT[nm].t, reads=[CT[nm]], writes=[t])
                return t
            Ms = ldc("rw_ms", [128, 2, 64]); MTs = ldc("rw_mts", [128, 2, 64]); MTi = ldc("rw_mti", [128, 2, 64])
            id2 = ldc("rw_id2", [128, 64])
            ones = kb.sb("rones", [128, 64])
            kb.op("pool", lambda e: e.memset(ones[:], 1.0), writes=[ones])
            ST = kb.sb("cST", [128, 4, 64])
            kb.op("pool", lambda e: e.memset(ST[:], 0.0), writes=[ST])
            names = ("AL", "W", "B", "KD", "RT")
            import types
            def alloc_set(si):
                S = types.SimpleNamespace()
                def t4(nm, n=2, w=64):
                    return [kb.sb(f"{nm}s{si}_{i}", [128, 4, w]) for i in range(n)]
                S.IN = {n: t4("ci" + n, 1)[0] for n in names}
                S.VTK = t4("cVTK", 1)[0]
                S.CS = t4("cCS", 1)[0]; S.TOT = kb.sb(f"cTOT{si}", [128, 4]); S.TMP = t4("cTMP", 1)[0]
                S.Epos = t4("cEp", 1)[0]; S.Eneg = t4("cEn", 1)[0]; S.Eprev = t4("cEv", 1)[0]; S.Etot = t4("cEt", 1)[0]; S.Wtot = kb.sb(f"cWt{si}", [128, 4])
                S.Ab = t4("cAb", 1)[0]; S.Bb = t4("cBb", 1)[0]; S.Kb = t4("cKb", 1)[0]; S.Rb = t4("cRb", 1)[0]; S.Bt = t4("cBt", 1)[0]; S.Kt = t4("cKt", 1)[0]
                S.Q = t4("cQ"); S.P = t4("cP"); S.ArbT = t4("cArbT", 1)[0]; S.AkvT = t4("cAkvT", 1)[0]; S.ArkT = t4("cArkT", 1)[0]
                S.X = t4("cX", 2, 128); S.Btok = t4("cBtok", 1)[0]; S.Ktok = t4("cKtok", 1)[0]
                S.RAT = t4("cRAT", 1)[0]; S.McT = t4("cMcT", 1)[0]; S.NcS = t4("cNcS", 1)[0]; S.DG = t4("cDG", 1)[0]
                S.ysb = [kb.sb(f"cysb{si}_{d}", [64, 256]) for d in range(2)]
                S.banks = PS[4 * si:4 * si + 4]
                return S
            SETS = [alloc_set(0), alloc_set(1)]
            border = [3, 2, 1, 0] + list(range(NCH - 1, 3, -1))
            DP = [(d, pr) for d in range(2) for pr in range(2)]
            HP = [slice(0, 64), slice(64, 128)]

            def mm_all(ps, col_fn, lhs_fn, rhs_fn, reads, start=True, stop=True, w=None):
                for dp in range(4):
                    for hp in range(2):
                        r = HP[hp]
                        c0, c1 = col_fn(dp)
                        kb.op("pe", lambda e, dp=dp, r=r, c0=c0, c1=c1: e.matmul(ps[r, c0:c1], lhs_fn(dp, r), rhs_fn(dp, r), start=start, stop=stop),
                              reads=reads, writes=[ps])

            def chunk_gen(ci, S):
                cidx = [ci, border[ci]]
                IN, VTK = S.IN, S.VTK
                for d in range(2):
                    c0 = cidx[d] * CH
                    for n in names:
                        src = RS[n if n in ("AL", "RT") else f"{n}{d}"]
                        kb.dma("sp" if d == 0 else "pool", IN[n][:, 2 * d:2 * d + 2, :],
                               src.t.rearrange("(pr q) t -> q pr t", q=128)[:, :, c0:c0 + CH], reads=[src], writes=[IN[n]])
                    for hp in range(2):
                        kb.dma("sp" if d == 0 else "pool", VTK[HP[hp], 2 * d:2 * d + 2, :],
                               RS["VTOK"][c0:c0 + CH, :].rearrange("t (pr hp v) -> t pr hp v", pr=2, hp=2)[:, :, hp, :],
                               reads=[RS["VTOK"]], writes=[VTK])
                al, lw, be, kd, rt, vt = IN["AL"], IN["W"], IN["B"], IN["KD"], IN["RT"], VTK
                CS, TOT, TMP, Epos, Eneg, Eprev, Etot, Wtot = S.CS, S.TOT, S.TMP, S.Epos, S.Eneg, S.Eprev, S.Etot, S.Wtot
                Ab, Bb, Kb, Rb, Bt, Kt, Q, P, ArbT, AkvT, ArkT = S.Ab, S.Bb, S.Kb, S.Rb, S.Bt, S.Kt, S.Q, S.P, S.ArbT, S.AkvT, S.ArkT
                X, Btok, Ktok, RAT, McT, NcS, DG, ysb = S.X, S.Btok, S.Ktok, S.RAT, S.McT, S.NcS, S.DG, S.ysb
                yield
                for dp in range(4):
                    kb.op("dve", lambda e, dp=dp: e.tensor_tensor_scan(CS[:, dp, :], ones[:, :], lw[:, dp, :], 0.0, ALU.mult, ALU.add),
                          reads=[ones, lw], writes=[CS])
                kb.op("dve", lambda e: e.tensor_copy(TOT[:, :], CS[:, :, CH - 1]), reads=[CS], writes=[TOT])
                kb.op("dve", lambda e: e.tensor_tensor(CS[:, 2:4, :], lw[:, 2:4, :], CS[:, 2:4, :], ALU.subtract), reads=[lw, CS], writes=[CS])
                kb.op("dve", lambda e: e.tensor_tensor(CS[:, 2:4, :], CS[:, 2:4, :], TOT[:, 2:4].unsqueeze(2).broadcast_to([128, 2, CH]), ALU.add),
                      reads=[CS, TOT], writes=[CS])
                kb.op("act", lambda e: e.activation(Epos[:], CS[:], AF.Exp), reads=[CS], writes=[Epos])
                kb.op("act", lambda e: e.activation(Eneg[:], CS[:], AF.Exp, scale=-1.0), reads=[CS], writes=[Eneg])
                kb.op("pool", lambda e: e.tensor_tensor(TMP[:], CS[:], lw[:], ALU.subtract), reads=[CS, lw], writes=[TMP])
                kb.op("act", lambda e: e.activation(Eprev[:], TMP[:], AF.Exp), reads=[TMP], writes=[Eprev])
                kb.op("dve", lambda e: e.tensor_tensor(Etot[:], TOT[:, :].unsqueeze(2).broadcast_to([128, 4, CH]), CS[:], ALU.subtract),
                      reads=[TOT, CS], writes=[Etot])
                kb.op("act", lambda e: e.activation(Etot[:], Etot[:], AF.Exp), reads=[Etot], writes=[Etot])
                kb.op("act", lambda e: e.activation(Wtot[:], TOT[:], AF.Exp), reads=[TOT], writes=[Wtot])
                kb.op("dve", lambda e: e.tensor_tensor(Ab[:], al[:], Eprev[:], ALU.mult), reads=[al, Eprev], writes=[Ab])
                kb.op("pool", lambda e: e.tensor_tensor(Bb[:], be[:], Eneg[:], ALU.mult), reads=[be, Eneg], writes=[Bb])
                kb.op("dve", lambda e: e.tensor_tensor(Kb[:], kd[:], Eneg[:], ALU.mult), reads=[kd, Eneg], writes=[Kb])
                kb.op("pool", lambda e: e.tensor_tensor(Rb[:], rt[:], Epos[:], ALU.mult), reads=[rt, Epos], writes=[Rb])
                kb.op("dve", lambda e: e.tensor_tensor(Bt[:], be[:], Etot[:], ALU.mult), reads=[be, Etot], writes=[Bt])
                kb.op("pool", lambda e: e.tensor_tensor(Kt[:], kd[:], Etot[:], ALU.mult), reads=[kd, Etot], writes=[Kt])
                yield
                PA, PB, PC, PT1 = S.banks
                PD, PX, PPQ, PE_ = PA, PB, PC, PT1
                mm_all(PA, lambda dp: (dp * 128, dp * 128 + 64), lambda dp, r: Bb[r, dp, :], lambda dp, r: Ab[r, dp, :], [Bb, Ab])
                mm_all(PA, lambda dp: (dp * 128 + 64, dp * 128 + 128), lambda dp, r: Bb[r, dp, :], lambda dp, r: Rb[r, dp, :], [Bb, Rb])
                mm_all(PB, lambda dp: (dp * 128, dp * 128 + 64), lambda dp, r: Kb[r, dp, :], lambda dp, r: Ab[r, dp, :], [Kb, Ab])
                mm_all(PB, lambda dp: (dp * 128 + 64, dp * 128 + 128), lambda dp, r: Kb[r, dp, :], lambda dp, r: Rb[r, dp, :], [Kb, Rb])
                mm_all(PC, lambda dp: (dp * 64, dp * 64 + 64), lambda dp, r: Ab[r, dp, :], lambda dp, r: Bb[r, dp, :], [Ab, Bb])
                q0, p0 = Q[0], P[0]
                pav = PA[:, :].rearrange("p (dp x) -> p dp x", dp=4)
                pbv = PB[:, :].rearrange("p (dp x) -> p dp x", dp=4)
                def mk(m):
                    return m[:, :, :].unsqueeze(2).broadcast_to([128, 2, 2, 64])
                def v4(ap):
                    return ap.rearrange("p (d pr) x -> p d pr x", d=2)
                kb.op("dve", lambda e: e.tensor_tensor(v4(q0[:]), v4(pav[:, :, 0:64]), mk(MTs), ALU.mult), reads=[PA, MTs], writes=[q0])
                kb.op("dve", lambda e: e.tensor_tensor(v4(ArbT[:]), v4(pav[:, :, 64:128]), mk(MTi), ALU.mult), reads=[PA, MTi], writes=[ArbT])
                kb.op("dve", lambda e: e.tensor_tensor(v4(AkvT[:]), v4(pbv[:, :, 0:64]), mk(MTs), ALU.mult), reads=[PB, MTs], writes=[AkvT])
                kb.op("dve", lambda e: e.tensor_tensor(v4(ArkT[:]), v4(pbv[:, :, 64:128]), mk(MTi), ALU.mult), reads=[PB, MTi], writes=[ArkT])
                kb.op("dve", lambda e: e.tensor_tensor(v4(p0[:]), v4(PC[:, 0:256].rearrange("p (dp x) -> p dp x", dp=4)), mk(Ms), ALU.mult),
                      reads=[PC, Ms], writes=[p0])
                yield
                def idb(r):
                    return ident_f[r, r.start:r.start + 64]
                mm_all(PT1, lambda dp: (dp * 128, dp * 128 + 64), lambda dp, r: Ab[r, dp, :], lambda dp, r: idb(r), [Ab, ident_f])
                mm_all(PT1, lambda dp: (dp * 128 + 64, dp * 128 + 128), lambda dp, r: Bt[r, dp, :], lambda dp, r: idb(r), [Bt, ident_f])
                mm_all(PC, lambda dp: (256 + dp * 64, 256 + dp * 64 + 64), lambda dp, r: Kt[r, dp, :], lambda dp, r: idb(r), [Kt, ident_f])
                x0 = X[0]
                pt1v = PT1[:, :].rearrange("p (dp x) -> p dp x", dp=4)
                kb.op("act", lambda e: e.copy(x0[:, :, 0:64], pt1v[:, :, 0:64]), reads=[PT1], writes=[x0])
                kb.op("act", lambda e: e.copy(Btok[:], pt1v[:, :, 64:128]), reads=[PT1], writes=[Btok])
                kb.op("act", lambda e: e.copy(Ktok[:], PC[:, 256:512].rearrange("p (dp x) -> p dp x", dp=4)), reads=[PC], writes=[Ktok])
                yield
                mm_all(PD, lambda dp: (dp * 64, dp * 64 + 64), lambda dp, r: AkvT[r, dp, :], lambda dp, r: vt[r, dp, :], [AkvT, vt])
                kb.op("act", lambda e: e.copy(x0[:, :, 64:128], PD[:, 0:256].rearrange("p (dp x) -> p dp x", dp=4)), reads=[PD], writes=[x0])
                yield
                qc, pc, xc = Q[0], P[0], X[0]
                for j in range(6):
                    qn, pn, xn = Q[(j + 1) % 2], P[(j + 1) % 2], X[(j + 1) % 2]
                    mm_all(PX, lambda dp: (dp * 128, dp * 128 + 128), lambda dp, r: qc[r, dp, :], lambda dp, r: xc[r, dp, :], [qc, xc])
                    kb.op("dve", lambda e, xn=xn, xc=xc: e.tensor_tensor(xn[:], xc[:], PX[:, :].rearrange("p (dp x) -> p dp x", dp=4), ALU.add),
                          reads=[xc, PX], writes=[xn])
                    if j < 5:
                        mm_all(PPQ, lambda dp: (dp * 64, dp * 64 + 64), lambda dp, r: qc[r, dp, :], lambda dp, r: pc[r, dp, :], [qc, pc])
                        mm_all(PPQ, lambda dp: (256 + dp * 64, 256 + dp * 64 + 64), lambda dp, r: pc[r, dp, :], lambda dp, r: qc[r, dp, :], [qc, pc])
                        kb.op("act", lambda e, pn=pn: e.copy(pn[:], PPQ[:, 0:256].rearrange("p (dp x) -> p dp x", dp=4)), reads=[PPQ], writes=[pn])
                        kb.op("act", lambda e, qn=qn: e.copy(qn[:], PPQ[:, 256:512].rearrange("p (dp x) -> p dp x", dp=4)), reads=[PPQ], writes=[qn])
                    qc, pc, xc = qn, pn, xn
                    yield
                yield
                mm_all(PD, lambda dp: (256 + dp * 64, 256 + dp * 64 + 64), lambda dp, r: xc[r, dp, 0:64], lambda dp, r: ArbT[r, dp, :], [xc, ArbT])
                kb.op("dve", lambda e: e.tensor_tensor(RAT[:], Rb[:], PD[:, 256:512].rearrange("p (dp x) -> p dp x", dp=4), ALU.add),
                      reads=[Rb, PD], writes=[RAT])
                mm_all(PE_, lambda dp: (dp * 64, dp * 64 + 64), lambda dp, r: xc[r, dp, 0:64], lambda dp, r: Btok[r, dp, :], [xc, Btok])
                kb.op("pool", lambda e: e.tensor_tensor(DG[:], id2[:, :].unsqueeze(1).broadcast_to([128, 4, 64]),
                                                        Wtot[:, :].unsqueeze(2).broadcast_to([128, 4, 64]), ALU.mult), reads=[id2, Wtot], writes=[DG])
                kb.op("dve", lambda e: e.tensor_tensor(McT[:], DG[:], PE_[:, 0:256].rearrange("p (dp x) -> p dp x", dp=4), ALU.add),
                      reads=[DG, PE_], writes=[McT])
                for dp in range(4):
                    for hp in range(2):
                        r = HP[hp]
                        c0 = 256 + dp * 64
                        kb.op("pe", lambda e, dp=dp, r=r, c0=c0: e.matmul(PE_[r, c0:c0 + 64], Btok[r, dp, :], xc[r, dp, 64:128], start=True, stop=False),
                              reads=[Btok, xc], writes=[PE_])
                        kb.op("pe", lambda e, dp=dp, r=r, c0=c0: e.matmul(PE_[r, c0:c0 + 64], Ktok[r, dp, :], vt[r, dp, :], start=False, stop=True),
                              reads=[Ktok, vt], writes=[PE_])
                kb.op("act", lambda e: e.copy(NcS[:], PE_[:, 256:512].rearrange("p (dp x) -> p dp x", dp=4)), reads=[PE_], writes=[NcS])
                yield
                PYs = [PA, PB]
                for dp in range(4):
                    for hp in range(2):
                        r = HP[hp]
                        PY = PYs[hp]
                        c0 = dp * 64
                        kb.op("pe", lambda e, dp=dp, r=r, c0=c0, PY=PY: e.matmul(PY[0:64, c0:c0 + 64], ST[r, dp, :], RAT[r, dp, :], start=True, stop=False),
                              reads=[ST, RAT], writes=[PY])
                        kb.op("pe", lambda e, dp=dp, r=r, c0=c0, PY=PY: e.matmul(PY[0:64, c0:c0 + 64], xc[r, dp, 64:128], ArbT[r, dp, :], start=False, stop=False),
                              reads=[xc, ArbT], writes=[PY])
                        kb.op("pe", lambda e, dp=dp, r=r, c0=c0, PY=PY: e.matmul(PY[0:64, c0:c0 + 64], vt[r, dp, :], ArkT[r, dp, :], start=False, stop=True),
                              reads=[vt, ArkT], writes=[PY])
                for d in range(2):
                    c0 = cidx[d] * CH
                    yv = ysb[d][:, :].rearrange("v (pr hp t) -> v pr hp t", pr=2, hp=2)
                    for hp in range(2):
                        kb.op("act", lambda e, d=d, hp=hp, yv=yv: e.copy(
                            yv[:, :, hp, :], PYs[hp][0:64, d * 128:(d + 1) * 128].rearrange("v (pr t) -> v pr t", pr=2)), reads=[PYs[hp]], writes=[ysb[d]])
                    dst = RS["YF" if d == 0 else "YB"]
                    kb.dma("sp", dst.t.rearrange("(h v) t -> v h t", v=64)[:, :, c0:c0 + CH],
                           ysb[d][:, :].rearrange("v (h t) -> v h t", h=4), reads=[ysb[d]], writes=[Buf()])
                PSS = PT1
                mm_all(PSS, lambda dp: (dp * 64, dp * 64 + 64), lambda dp, r: McT[r, dp, :], lambda dp, r: ST[r, dp, :], [McT, ST])
                kb.op("dve", lambda e: e.tensor_tensor(ST[:], NcS[:], PSS[:, 0:256].rearrange("p (dp x) -> p dp x", dp=4), ALU.add),
                      reads=[NcS, PSS], writes=[ST])


            def lockstep(gens):
                gens = list(gens)
                while gens:
                    nxt = []
                    for g_ in gens:
                        try:
                            next(g_)
                            nxt.append(g_)
                        except StopIteration:
                            pass
                    gens = nxt
            for ci in range(0, NCH, 2):
                lockstep([chunk_gen(ci, SETS[0]), chunk_gen(ci + 1, SETS[1])])

    def phase_rwkv_chunked2(l):
        CH = 64
        NCH = T // CH
        with kb.scope():
            def ldc(nm, shape):
                t = kb.sb("k" + nm, shape)
                kb.dma("sp", t[:], CT[nm].t, reads=[CT[nm]], writes=[t])
                return t
            MsB = ldc("rw_msb", [128, 2, 128]); MTsB = ldc("rw_mtsb", [128, 2, 128]); MTi = ldc("rw_mti", [128, 2, 64])
            identr = kb.sb("cidr", [128, 128], F32R)
            kb.op("dve", lambda e: e.tensor_copy(identr[:], ident_f[:]), reads=[ident_f], writes=[identr])
            ones = kb.sb("rones", [128, 64])
            kb.op("pool", lambda e: e.memset(ones[:], 1.0), writes=[ones])
            def bd(nm, n=1, dt=F32R):
                ts = [kb.sb(f"{nm}{i}", [128, 4, 128], dt) for i in range(n)]
                for t in ts:
                    kb.op("pool", lambda e, t=t: e.memset(t[:].bitcast(F32) if dt == F32R else t[:], 0.0), writes=[t])
                return ts
            def t4(nm, n=1, w=64, dt=F32):
                return [kb.sb(f"{nm}{i}", [128, 4, w], dt) for i in range(n)]
            f32 = lambda ap: ap.bitcast(F32)
            names = ("AL", "W", "B", "KD", "RT")
            IN = {n: t4("di" + n, 2) for n in names}
            VT = bd("dVT", 2, F32)
            VTr = bd("dVTr")[0]
            ST = bd("dST")[0]
            CS = t4("dCS")[0]; TOT = kb.sb("dTOT", [128, 4]); TMP = t4("dTMP")[0]
            Epos = t4("dEp")[0]; Eneg = t4("dEn")[0]; Eprev = t4("dEv")[0]; Etot = t4("dEt")[0]; Wtot = kb.sb("dWt", [128, 4])
            Ab = bd("dAb")[0]; Bb = bd("dBb")[0]; Kb = bd("dKb")[0]; Bt = bd("dBt")[0]; Kt = bd("dKt")[0]
            Rb = t4("dRb", 1, 64, F32R)[0]
            Q = bd("dQ", 2); P = bd("dP", 2); AkvT = bd("dAkvT")[0]
            ArbT = t4("dArbT", 1, 64, F32R)[0]; ArkT = t4("dArkT", 1, 64, F32R)[0]; RAT = t4("dRAT", 1, 64, F32R)[0]
            X = [kb.sb(f"dX{i}", [128, 4, 256], F32R) for i in range(2)]
            Btok = bd("dBtok")[0]; Ktok = bd("dKtok")[0]; McT = bd("dMcT")[0]
            NcS = bd("dNcS", 1, F32)[0]; DG = bd("dDG", 1, F32)[0]
            ysb = [kb.sb(f"dysb{d}", [128, 2, 64]) for d in range(2)]
            border = [3, 2, 1, 0] + list(range(NCH - 1, 3, -1))
            H0, H1 = slice(0, 64), slice(64, 128)
            B0, B1, B2, B3, B4, B5, B6, B7 = PS

            def mm4(ps, c0, w, lhs, rhs, reads, start=True, stop=True):
                for dp in range(4):
                    kb.op("pe", lambda e, dp=dp: e.matmul(ps[:, c0 + dp * w:c0 + (dp + 1) * w], lhs(dp), rhs(dp), start=start, stop=stop),
                          reads=reads, writes=[ps])

            def v4(ap):
                return ap.rearrange("p (d pr) x -> p d pr x", d=2)

            def mk(m, w):
                return m[:, :, :].unsqueeze(2).broadcast_to([128, 2, 2, w])

            def pv(ps, c0, w):
                return ps[:, c0:c0 + 4 * w].rearrange("p (dp x) -> p dp x", dp=4)

            for ci in range(NCH):
                cidx = [ci, border[ci]]
                i2 = ci % 2
                vt = VT[i2]
                for d in range(2):
                    c0 = cidx[d] * CH
                    q_ = "sp" if d == 0 else "pool"
                    for n in names:
                        src = RS[n if n in ("AL", "RT") else f"{n}{d}"]
                        kb.dma(q_, IN[n][i2][:, 2 * d:2 * d + 2, :],
                               src.t.rearrange("(pr q) t -> q pr t", q=128)[:, :, c0:c0 + CH], reads=[src], writes=[IN[n][i2]])
                    for hp in range(2):
                        kb.dma(q_, vt[hp * 64:(hp + 1) * 64, 2 * d:2 * d + 2, hp * 64:(hp + 1) * 64],
                               RS["VTOK"][c0:c0 + CH, :].rearrange("t (pr hp v) -> t pr hp v", pr=2, hp=2)[:, :, hp, :],
                               reads=[RS["VTOK"]], writes=[vt])
                al, lw, be, kd, rt = IN["AL"][i2], IN["W"][i2], IN["B"][i2], IN["KD"][i2], IN["RT"][i2]
                kb.op("act", lambda e: e.copy(VTr[:], vt[:]), reads=[vt], writes=[VTr])
                for dp in range(4):
                    kb.op("dve", lambda e, dp=dp: e.tensor_tensor_scan(CS[:, dp, :], ones[:, :], lw[:, dp, :], 0.0, ALU.mult, ALU.add),
                          reads=[ones, lw], writes=[CS])
                kb.op("dve", lambda e: e.tensor_copy(TOT[:, :], CS[:, :, CH - 1]), reads=[CS], writes=[TOT])
                kb.op("dve", lambda e: e.tensor_tensor(CS[:, 2:4, :], lw[:, 2:4, :], CS[:, 2:4, :], ALU.subtract), reads=[lw, CS], writes=[CS])
                kb.op("dve", lambda e: e.tensor_tensor(CS[:, 2:4, :], CS[:, 2:4, :], TOT[:, 2:4].unsqueeze(2).broadcast_to([128, 2, CH]), ALU.add),
                      reads=[CS, TOT], writes=[CS])
                kb.op("act", lambda e: e.activation(Epos[:], CS[:], AF.Exp), reads=[CS], writes=[Epos])
                kb.op("act", lambda e: e.activation(Eneg[:], CS[:], AF.Exp, scale=-1.0), reads=[CS], writes=[Eneg])
                kb.op("pool", lambda e: e.tensor_tensor(TMP[:], CS[:], lw[:], ALU.subtract), reads=[CS, lw], writes=[TMP])
                kb.op("act", lambda e: e.activation(Eprev[:], TMP[:], AF.Exp), reads=[TMP], writes=[Eprev])
                kb.op("pool", lambda e: e.tensor_tensor(Etot[:], TOT[:, :].unsqueeze(2).broadcast_to([128, 4, CH]), CS[:], ALU.subtract),
                      reads=[TOT, CS], writes=[Etot])
                kb.op("act", lambda e: e.activation(Etot[:], Etot[:], AF.Exp), reads=[Etot], writes=[Etot])
                kb.op("act", lambda e: e.activation(Wtot[:], TOT[:], AF.Exp), reads=[TOT], writes=[Wtot])
                for k_, (dst, a_, b_) in enumerate(((Ab, al, Eprev), (Bb, be, Eneg), (Kb, kd, Eneg), (Bt, be, Etot), (Kt, kd, Etot))):
                    for hi, r in enumerate((H0, H1)):
                        eng = "dve" if (k_ + hi) % 2 == 0 else "pool"
                        kb.op(eng, lambda e, dst=dst, a_=a_, b_=b_, r=r: e.tensor_tensor(dst[r, :, r.start:r.start + 64], a_[r, :, :], b_[r, :, :], ALU.mult),
                              reads=[a_, b_], writes=[dst])
                kb.op("pool", lambda e: e.tensor_tensor(Rb[:], rt[:], Epos[:], ALU.mult), reads=[rt, Epos], writes=[Rb])
                mm4(B0, 0, 128, lambda dp: Bb[:, dp, :], lambda dp: Ab[:, dp, :], [Bb, Ab])
                mm4(B1, 0, 128, lambda dp: Kb[:, dp, :], lambda dp: Ab[:, dp, :], [Kb, Ab])
                mm4(B2, 0, 128, lambda dp: Ab[:, dp, :], lambda dp: Bb[:, dp, :], [Ab, Bb])
                mm4(B3, 0, 64, lambda dp: Bb[:, dp, :], lambda dp: Rb[:, dp, :], [Bb, Rb])
                mm4(B3, 256, 64, lambda dp: Kb[:, dp, :], lambda dp: Rb[:, dp, :], [Kb, Rb])
                q0, p0, x0 = Q[0], P[0], X[0]
                kb.op("dve", lambda e: e.tensor_tensor(v4(q0[:]), v4(pv(B0, 0, 128)), mk(MTsB, 128), ALU.mult), reads=[B0, MTsB], writes=[q0])
                kb.op("dve", lambda e: e.tensor_tensor(v4(AkvT[:]), v4(pv(B1, 0, 128)), mk(MTsB, 128), ALU.mult), reads=[B1, MTsB], writes=[AkvT])
                kb.op("dve", lambda e: e.tensor_tensor(v4(p0[:]), v4(pv(B2, 0, 128)), mk(MsB, 128), ALU.mult), reads=[B2, MsB], writes=[p0])
                kb.op("dve", lambda e: e.tensor_tensor(v4(ArbT[:]), v4(pv(B3, 0, 64)), mk(MTi, 64), ALU.mult), reads=[B3, MTi], writes=[ArbT])
                kb.op("dve", lambda e: e.tensor_tensor(v4(ArkT[:]), v4(pv(B3, 256, 64)), mk(MTi, 64), ALU.mult), reads=[B3, MTi], writes=[ArkT])
                mm4(B4, 0, 128, lambda dp: Ab[:, dp, :], lambda dp: identr[:, :], [Ab, identr])
                mm4(B6, 0, 128, lambda dp: Bt[:, dp, :], lambda dp: identr[:, :], [Bt, identr])
                mm4(B7, 0, 128, lambda dp: Kt[:, dp, :], lambda dp: identr[:, :], [Kt, identr])
                mm4(B5, 0, 128, lambda dp: AkvT[:, dp, :], lambda dp: VTr[:, dp, :], [AkvT, VTr])
                kb.op("act", lambda e: e.copy(x0[:, :, 0:128], pv(B4, 0, 128)), reads=[B4], writes=[x0])
                kb.op("act", lambda e: e.copy(Btok[:], pv(B6, 0, 128)), reads=[B6], writes=[Btok])
                kb.op("act", lambda e: e.copy(Ktok[:], pv(B7, 0, 128)), reads=[B7], writes=[Ktok])
                kb.op("act", lambda e: e.copy(x0[:, :, 128:256], pv(B5, 0, 128)), reads=[B5], writes=[x0])
                qc, pc, xc = Q[0], P[0], X[0]
                for j in range(6):
                    qn, pn, xn = Q[(j + 1) % 2], P[(j + 1) % 2], X[(j + 1) % 2]
                    for hf, bank in ((0, B4), (1, B5)):
                        for dq in range(2):
                            dp = hf * 2 + dq
                            kb.op("pe", lambda e, dp=dp, dq=dq, bank=bank: e.matmul(bank[:, dq * 256:(dq + 1) * 256], qc[:, dp, :], xc[:, dp, :],
                                                                                    start=True, stop=True), reads=[qc, xc], writes=[bank])
                        kb.op("dve", lambda e, hf=hf, bank=bank, xn=xn, xc=xc: e.tensor_tensor(
                            xn[:, 2 * hf:2 * hf + 2, :], f32(xc[:, 2 * hf:2 * hf + 2, :]), bank[:, :].rearrange("p (dq x) -> p dq x", dq=2), ALU.add),
                            reads=[xc, bank], writes=[xn])
                    if j < 5:
                        mm4(B6, 0, 128, lambda dp: qc[:, dp, :], lambda dp: pc[:, dp, :], [qc, pc])
                        mm4(B7, 0, 128, lambda dp: pc[:, dp, :], lambda dp: qc[:, dp, :], [qc, pc])
                        kb.op("act", lambda e, pn=pn: e.copy(pn[:], pv(B6, 0, 128)), reads=[B6], writes=[pn])
                        kb.op("act", lambda e, qn=qn: e.copy(qn[:], pv(B7, 0, 128)), reads=[B7], writes=[qn])
                    qc, pc, xc = qn, pn, xn
                mm4(B3, 0, 64, lambda dp: xc[:, dp, 0:128], lambda dp: ArbT[:, dp, :], [xc, ArbT])
                kb.op("dve", lambda e: e.tensor_tensor(RAT[:], f32(Rb[:]), pv(B3, 0, 64), ALU.add), reads=[Rb, B3], writes=[RAT])
                mm4(B2, 0, 128, lambda dp: xc[:, dp, 0:128], lambda dp: Btok[:, dp, :], [xc, Btok])
                kb.op("pool", lambda e: e.tensor_tensor(DG[:], ident_f[:, :].unsqueeze(1).broadcast_to([128, 4, 128]),
                                                        Wtot[:, :].unsqueeze(2).broadcast_to([128, 4, 128]), ALU.mult), reads=[ident_f, Wtot], writes=[DG])
                kb.op("dve", lambda e: e.tensor_tensor(McT[:], DG[:], pv(B2, 0, 128), ALU.add), reads=[DG, B2], writes=[McT])
                for dp in range(4):
                    kb.op("pe", lambda e, dp=dp: e.matmul(B0[:, dp * 128:(dp + 1) * 128], Btok[:, dp, :], xc[:, dp, 128:256], start=True, stop=False),
                          reads=[Btok, xc], writes=[B0])
                    kb.op("pe", lambda e, dp=dp: e.matmul(B0[:, dp * 128:(dp + 1) * 128], Ktok[:, dp, :], VTr[:, dp, :], start=False, stop=True),
                          reads=[Ktok, VTr], writes=[B0])
                kb.op("act", lambda e: e.copy(NcS[:], pv(B0, 0, 128)), reads=[B0], writes=[NcS])
                for dp in range(4):
                    c0 = dp * 64
                    kb.op("pe", lambda e, dp=dp, c0=c0: e.matmul(B1[:, c0:c0 + 64], ST[:, dp, :], RAT[:, dp, :], start=True, stop=False),
                          reads=[ST, RAT], writes=[B1])
                    kb.op("pe", lambda e, dp=dp, c0=c0: e.matmul(B1[:, c0:c0 + 64], xc[:, dp, 128:256], ArbT[:, dp, :], start=False, stop=False),
                          reads=[xc, ArbT], writes=[B1])
                    kb.op("pe", lambda e, dp=dp, c0=c0: e.matmul(B1[:, c0:c0 + 64], VTr[:, dp, :], ArkT[:, dp, :], start=False, stop=True),
                          reads=[VTr, ArkT], writes=[B1])
                for d in range(2):
                    c0 = cidx[d] * CH
                    kb.op("act", lambda e, d=d: e.copy(ysb[d][:, :, :], B1[:, d * 128:(d + 1) * 128].rearrange("p (pr t) -> p pr t", pr=2)),
                          reads=[B1], writes=[ysb[d]])
                    dst = RS["YF" if d == 0 else "YB"]
                    kb.dma("sp", dst.t.rearrange("(pr q) t -> q pr t", q=128)[:, :, c0:c0 + CH], ysb[d][:, :, :], reads=[ysb[d]], writes=[Buf()])
                mm4(B6, 0, 128, lambda dp: McT[:, dp, :], lambda dp: ST[:, dp, :], [McT, ST])
                kb.op("dve", lambda e: e.tensor_tensor(ST[:], NcS[:], pv(B6, 0, 128), ALU.add), reads=[NcS, B6], writes=[ST])

    def phase_rwkv_out(l, with_ctx):
        with kb.scope():
            rk_ = colvec("rrk", W["rw_r_k"][l, :], W["rw_r_k"], [128, 2], "(j p) -> p j", p=128)
            lg_ = colvec("rlg", W["rw_ln_g"][l, :], W["rw_ln_g"], [128, 2], "(j p) -> p j", p=128)
            lb_ = colvec("rlb", W["rw_ln_b"][l, :], W["rw_ln_b"], [128, 2], "(j p) -> p j", p=128)
            nm = ("YF", "YB", "RT", "KD0", "KD1", "VT")
            tl = [{n: kb.sb(f"o{n}{i}", [128, 512]) for n in nm} for i in range(2)]
            sg = [kb.sb(f"osg{i}", [128, 512], BF16) for i in range(2)]
            ob = [kb.sb(f"oob{i}", [128, 512], BF16) for i in range(2)]
            wk = [[kb.sb(f"owk{k}{i}", [128, 512]) for k in range(3)] for i in range(2)]
            it = 0
            for pr in range(2):
                rows = slice(pr * 128, (pr + 1) * 128)
                for (t0, nt) in TCH:
                    if not with_ctx and t0 + nt <= C:
                        continue
                    t_, s_, o_, (a_, b_, c_) = tl[it % 2], sg[it % 2], ob[it % 2], wk[it % 2]
                    for k, n in enumerate(nm):
                        kb.dma("sp" if k % 2 == 0 else "pool", t_[n][:, 0:nt], RS[n][rows, t0:t0 + nt], reads=[RS[n]], writes=[t_[n]])
                    kb.dma("sp", s_[:, 0:nt], RS["SGT"][rows, t0:t0 + nt], reads=[RS["SGT"]], writes=[s_])
                    y = t_["YF"]
                    kb.op("dve", lambda e: e.tensor_tensor(y[:, 0:nt], y[:, 0:nt], t_["YB"][:, 0:nt], ALU.add), reads=[y, t_["YB"]], writes=[y])
                    p1, p2, p3 = PS[(3 * it) % 8], PS[(3 * it + 1) % 8], PS[(3 * it + 2) % 8]
                    kb.op("pe", lambda e: e.matmul(p1[:, 0:nt], blk64[:], y[:, 0:nt], start=True, stop=True), reads=[blk64, y], writes=[p1])
                    kb.op("dve", lambda e: e.scalar_tensor_tensor(a_[:, 0:nt], p1[:, 0:nt], -1.0 / 64, y[:, 0:nt], ALU.mult, ALU.add),
                          reads=[p1, y], writes=[a_])
                    kb.op("act", lambda e: e.activation(b_[:, 0:nt], a_[:, 0:nt], AF.Square), reads=[a_], writes=[b_])
                    kb.op("pe", lambda e: e.matmul(p2[:, 0:nt], blk64[:], b_[:, 0:nt], start=True, stop=True), reads=[blk64, b_], writes=[p2])
                    kb.op("dve", lambda e: e.tensor_scalar(b_[:, 0:nt], p2[:, 0:nt], 1.0 / 64, 64e-5, ALU.mult, ALU.add), reads=[p2], writes=[b_])
                    kb.op("act", lambda e: e.sqrt(b_[:, 0:nt], b_[:, 0:nt]), reads=[b_], writes=[b_])
                    kb.op("dve", lambda e: e.reciprocal(b_[:, 0:nt], b_[:, 0:nt]), reads=[b_], writes=[b_])
                    kb.op("dve", lambda e: e.tensor_tensor(a_[:, 0:nt], a_[:, 0:nt], b_[:, 0:nt], ALU.mult), reads=[a_, b_], writes=[a_])
                    kb.op("dve", lambda e: e.tensor_scalar(a_[:, 0:nt], a_[:, 0:nt], lg_[:, pr:pr + 1], lb_[:, pr:pr + 1], ALU.mult, ALU.add),
                          reads=[a_, lg_, lb_], writes=[a_])
                    kb.op("pool", lambda e: e.tensor_tensor(c_[:, 0:nt], t_["KD0"][:, 0:nt], t_["KD1"][:, 0:nt], ALU.add),
                          reads=[t_["KD0"], t_["KD1"]], writes=[c_])
                    kb.op("dve", lambda e: e.scalar_tensor_tensor(c_[:, 0:nt], t_["RT"][:, 0:nt], rk_[:, pr:pr + 1], c_[:, 0:nt], ALU.mult, ALU.mult),
                          reads=[t_["RT"], rk_, c_], writes=[c_])
                    kb.op("pe", lambda e: e.matmul(p3[:, 0:nt], blk64[:], c_[:, 0:nt], start=True, stop=True), reads=[blk64, c_], writes=[p3])
                    kb.op("dve", lambda e: e.tensor_tensor(c_[:, 0:nt], p3[:, 0:nt], t_["VT"][:, 0:nt], ALU.mult), reads=[p3, t_["VT"]], writes=[c_])
                    kb.op("dve", lambda e: e.tensor_tensor(a_[:, 0:nt], a_[:, 0:nt], c_[:, 0:nt], ALU.add), reads=[a_, c_], writes=[a_])
                    kb.op("pool", lambda e: e.tensor_tensor(o_[:, 0:nt], a_[:, 0:nt], s_[:, 0:nt], ALU.mult), reads=[a_, s_], writes=[o_])
                    kb.dma("sp", mixT[256 + pr * 128:256 + (pr + 1) * 128, t0:t0 + nt], o_[:, 0:nt], reads=[o_], writes=[Buf()])
                    it += 1


    SEGS = {"L": dict(Ls=L, A=32, cbw=32, off=C, ut="UTL"), "C": dict(Ls=C, A=2, cbw=64, off=0, ut="UTC")}

    def phase_hyena_prep(l, with_ctx):
        with kb.scope():
            stage = kb.sb("hstage", [128, 8, 128])
            wts = [kb.sb(f"hwt{i}", [128, 8, 128], BF16) for i in range(2)]
            cw = kb.sb("hcw", [128, 6, 3])
            for k in range(3):
                kb.dma("sp", cw[:, :, k], W["hy_conv"][l, k, :].rearrange("(j p) -> p j", p=128), reads=[W["hy_conv"]], writes=[cw], slow=True)
            ncw = kb.sb("hncw", [128, 6, 3])
            kb.op("dve", lambda e: e.tensor_scalar(ncw[:], cw[:], -1.0, None, ALU.mult), reads=[cw], writes=[ncw])
            Zraw = kb.sb("hZraw", [128, T + 2])
            Zout = kb.sb("hZout", [128, T])
            kb.op("pool", lambda e: e.memset(Zraw[:, 0:1], 0.0), writes=[Zraw])
            kb.op("pool", lambda e: e.memset(Zraw[:, T + 1:T + 2], 0.0), writes=[Zraw])
            ub = kb.sb("hub", [128, 32 * 128])
            tG = [kb.sb(f"htG{i}", [128, 512], BF16) for i in range(2)]
            for oi, jt in enumerate(range(8)):
                wt = wts[oi % 2]
                c0 = HY0 + jt * 128 if jt < 6 else HYG0 + (jt - 6) * 128
                load_w(l, wt, c0, 128, stage)
                if jt >= 6:
                    for ci, (t0, nt) in enumerate(TCH):
                        p = PS[ci % 4]
                        proj_fm(p, wt, 0, 128, t0, nt)
                        g = tG[ci % 2]
                        kb.op("act", lambda e, p=p, g=g, nt=nt: e.activation(g[:, 0:nt], p[:, 0:nt], AF.Silu), reads=[p], writes=[g])
                        kb.dma("sp", HS["SG"][(jt - 6) * 128:(jt - 5) * 128, t0:t0 + nt], g[:, 0:nt], reads=[g], writes=[Buf()])
                    continue
                conv_tile(l, wt, cw, ncw, jt, Zraw, Zout)
                arr, half = jt // 2, jt % 2
                for sn in (("L", "C") if with_ctx else ("L",)):
                    sg = SEGS[sn]
                    A, cbw, off = sg["A"], sg["cbw"], sg["off"]
                    G = 128 // A
                    ncg = 128 // G
                    ubv = ub[:, 0:A * 128].rearrange("p (g a c) -> p g a c", g=ncg, a=A)
                    for a in range(A):
                        p = PS[4 + (a // 4) % 4]
                        kb.op("pe", lambda e, p=p, a=a, A=A, off=off: e.transpose(
                            p[:, (a % 4) * 128:(a % 4 + 1) * 128], Zout[:, off + a:off + a + 127 * A + 1:A], ident_f[:]),
                            reads=[Zout, ident_f], writes=[p])
                        if a % 4 == 3 or a == A - 1:
                            a0 = (a // 4) * 4
                            na = a - a0 + 1
                            kb.op("act", lambda e, p=p, a0=a0, na=na, G=G: e.copy(
                                ubv[:, :, a0:a0 + na, :], p[:, 0:na * 128].rearrange("p (a g c) -> p g a c", a=na, c=G)), reads=[p], writes=[ub])
                    nb = 128 // cbw
                    bsz = A * cbw
                    for b in range(nb):
                        dst = HS[sg["ut"]][arr, half * nb + b, :, :]
                        kb.dma("sp" if b % 2 == 0 else "pool", dst, ub[:, b * bsz:(b + 1) * bsz], reads=[ub], writes=[Buf()])

    def cmul(dre, dim_, sre, sim, tre, tim, conj, srcb, tabb, dstb, tmp):
        t1, t2 = tmp
        sh = tuple(slice(None) for _ in range(1))
        kb.op("dve", lambda e: e.tensor_tensor(t1, sre, tre, ALU.mult), reads=srcb + tabb, writes=[dstb[2]])
        kb.op("dve", lambda e: e.tensor_tensor(t2, sim, tim, ALU.mult), reads=srcb + tabb, writes=[dstb[3]])
        kb.op("pool", lambda e: e.tensor_tensor(dre, t1, t2, ALU.add if conj else ALU.subtract), reads=[dstb[2], dstb[3]], writes=[dstb[0]])
        kb.op("dve", lambda e: e.tensor_tensor(t1, sim, tre, ALU.mult), reads=srcb + tabb + [dstb[0]], writes=[dstb[2]])
        kb.op("dve", lambda e: e.tensor_tensor(t2, sre, tim, ALU.mult), reads=srcb + tabb + [dstb[0]], writes=[dstb[3]])
        kb.op("pool", lambda e: e.tensor_tensor(dim_, t1, t2, ALU.subtract if conj else ALU.add), reads=[dstb[2], dstb[3]], writes=[dstb[1]])

    def phase_hyena_main(l, with_ctx):
        PI = math.pi
        with kb.scope():
            fw1 = kb.sb("hfw1", [33, 64])
            fw2 = kb.sb("hfw2", [64, 64])
            fw3 = kb.sb("hfw3", [64, 1024])
            kb.dma("sp", fw1[:], W["hy_fw1"][l, :, :], reads=[W["hy_fw1"]], writes=[fw1])
            kb.dma("sp", fw2[:], W["hy_fw2"][l, :, :], reads=[W["hy_fw2"]], writes=[fw2])
            kb.dma("sp", fw3[:], W["hy_fw3"][l, :, :], reads=[W["hy_fw3"]], writes=[fw3])
            fb1 = colvec("hfb1", W["hy_fb1"][l, :], W["hy_fb1"], [64, 1], "(d o) -> d o", o=1)
            fb2 = colvec("hfb2", W["hy_fb2"][l, :], W["hy_fb2"], [64, 1], "(d o) -> d o", o=1)
            frq = colvec("hfrq", W["hy_freq"][l, :], W["hy_freq"], [64, 1], "(d o) -> d o", o=1)
            brow = kb.sb("hbrow", [1, 512])
            kb.dma("sp", brow[:], W["hy_bias"][l, :, :].rearrange("o c -> (o c)").rearrange("(x n) -> x n", x=1), reads=[W["hy_bias"]], writes=[brow])
            for sn in (("L", "C") if with_ctx else ("L",)):
                sg = SEGS[sn]
                Ls, A, cbw, off = sg["Ls"], sg["A"], sg["cbw"], sg["off"]
                G = 128 // A
                N = 2 * Ls
                ngr = cbw // G
                nblk = 256 // cbw
                pre = f"hy{sn}_"
                with kb.scope():
                    def ld(nm, shape):
                        t = kb.sb("k" + nm, shape)
                        src = CT[pre + nm]
                        kb.dma("sp", t[:], src.t, reads=[src], writes=[t])
                        return t
                    def ldr(nm, shape):
                        tr = kb.sb("r" + nm, shape, F32R)
                        with kb.scope():
                            t32 = ld(nm, shape)
                            kb.op("dve", lambda e: e.tensor_copy(tr[:], t32[:]), reads=[t32], writes=[tr])
                        return tr
                    F256 = ldr("F256", [128, 2, 512]); TWC = ld("TWC", [128, 256]); TWS = ld("TWS", [128, 256])
                    Dre = ldr("Dre", [128, 128]); Dim = ldr("Dim", [128, 128]); nDim = ldr("nDim", [128, 128])
                    E1 = ldr("E1", [128, 256]); E2 = ldr("E2", [128, 256])
                    TW2C = ld("TW2C", [128, 2, 128]); TW2S = ld("TW2S", [128, 2, 128])
                    IC = ldr("IC", [128, 2, 128]); IS = ldr("IS", [128, 2, 128])
                    h2T = kb.sb("h2T", [64, N])
                    with kb.scope():
                        zT = kb.sb("zT", [33, N])
                        kb.dma("sp", zT[:], CT[pre + "zT"].t, reads=[CT[pre + "zT"]], writes=[zT])
                        h1T = kb.sb("h1T", [64, N])
                        arg = [kb.sb(f"harg{i}", [64, 512]) for i in range(2)]
                        wr = [kb.sb(f"hwr{i}", [64, 512]) for i in range(2)]
                        for (src, K_, wgt, bcol, dst) in ((zT, 33, fw1, fb1, h1T), (h1T, 64, fw2, fb2, h2T)):
                            for ci, n0 in enumerate(range(0, N, 512)):
                                p = PS[ci % 4]
                                ag = arg[ci % 2]
                                kb.op("pe", lambda e: e.matmul(p[0:64, :], wgt[0:K_, :], src[0:K_, n0:n0 + 512], start=True, stop=True),
                                      reads=[wgt, src], writes=[p])
                                kb.op("dve", lambda e: e.tensor_scalar(ag[:, :], p[0:64, :], bcol[:, 0:1], frq[:, 0:1], ALU.add, ALU.mult),
                                      reads=[p, bcol, frq], writes=[ag])
                                for _w in range(2):
                                    kb.op("dve", lambda e: e.tensor_scalar(wr[0][:, :], ag[:, :], PI, -2 * PI, ALU.is_gt, ALU.mult), reads=[ag], writes=[wr[0]])
                                    kb.op("dve", lambda e: e.tensor_scalar(wr[1][:, :], ag[:, :], -PI, 2 * PI, ALU.is_lt, ALU.mult), reads=[ag], writes=[wr[1]])
                                    kb.op("dve", lambda e: e.tensor_tensor(ag[:, :], ag[:, :], wr[0][:, :], ALU.add), reads=[ag, wr[0]], writes=[ag])
                                    kb.op("dve", lambda e: e.tensor_tensor(ag[:, :], ag[:, :], wr[1][:, :], ALU.add), reads=[ag, wr[1]], writes=[ag])
                                kb.op("act", lambda e: e.activation(dst[:, n0:n0 + 512], ag[:, :], AF.Sin), reads=[ag], writes=[dst])
                    KT = [kb.sb(f"KT{o}", [128, 2, ngr, A, G]) for o in range(2)]
                    KTr = [kb.sb(f"KTr{o}", [128, 2, ngr, A, G], F32R) for o in range(2)]
                    uvr = kb.sb("huvr", [128, ngr, A * G], F32R)
                    KS = [kb.sb(f"KS{o}", [128, ngr, 512]) for o in range(2)]
                    DECt = kb.sb("DECt", [128, 2, ngr, A, G])
                    part = kb.sb("hpart", [128, cbw])
                    rn = kb.sb("hrn", [128, cbw])
                    ex = kb.sb("hex", [1, cbw])
                    uv = kb.sb("huv", [128, ngr, A * G]); x1 = kb.sb("hx1", [128, ngr, A * G]); x2 = kb.sb("hx2", [128, ngr, A * G])
                    u2 = kb.sb("hu2", [128, ngr, A * G], F32R); res = kb.sb("hres", [128, A, cbw])
                    dts = (F32R, F32R, F32, F32)
                    BpS = [[kb.sb(f"hBp{b}{i}", [128, 256], dts[i]) for i in range(4)] for b in range(2)]
                    BpbS = [[Buf() for _ in range(4)] for b in range(2)]
                    YpS = [[kb.sb(f"hYp{b}{i}", [128, 256], dts[i]) for i in range(4)] for b in range(2)]
                    YpbS = [[Buf() for _ in range(4)] for b in range(2)]
                    GpS = [[kb.sb(f"hGp{b}{i}", [128, 2, 128], dts[i]) for i in range(4)] for b in range(2)]
                    GpbS = [[Buf() for _ in range(4)] for b in range(2)]
                    fctr = [0]
                    Fm = kb.sb("hFm", [cbw, Ls])
                    sgm = kb.sb("hsgm", [cbw, Ls], BF16)

                    def fwd_fft(lhs_chunks, lhs_bufs, psB, psX):
                        n = len(lhs_chunks)
                        fctr[0] += 1
                        Bp, Bpb = BpS[fctr[0] % 2], BpbS[fctr[0] % 2]
                        for i, (ap, hf) in enumerate(lhs_chunks):
                            kb.op("pe", lambda e, ap=ap, hf=hf, i=i: e.matmul(psB[:, :], ap, F256[:, hf, :], start=(i == 0), stop=(i == n - 1)),
                                  reads=lhs_bufs + [F256], writes=[psB])
                        yield
                        cmul(Bp[0][:, :], Bp[1][:, :], psB[:, 0:256], psB[:, 256:512], TWC[:, :], TWS[:, :], True,
                             [psB], [TWC, TWS], Bpb, (Bp[2][:, :], Bp[3][:, :]))
                        yield
                        kb.op("pe", lambda e: e.matmul(psX[:, 0:256], Dre[:, :], Bp[0][:, :], start=True, stop=False), reads=[Dre, Bpb[0]], writes=[psX])
                        kb.op("pe", lambda e: e.matmul(psX[:, 0:256], nDim[:, :], Bp[1][:, :], start=False, stop=True), reads=[nDim, Bpb[1]], writes=[psX])
                        kb.op("pe", lambda e: e.matmul(psX[:, 256:512], Dim[:, :], Bp[0][:, :], start=True, stop=False), reads=[Dim, Bpb[0]], writes=[psX])
                        kb.op("pe", lambda e: e.matmul(psX[:, 256:512], Dre[:, :], Bp[1][:, :], start=False, stop=True), reads=[Dre, Bpb[1]], writes=[psX])

                    def conv_group(src, src_b, g, o, mulv, mul_b, dst_ap, dst_b, it):
                        psB, psX, psG, psy = PS[it % 2], PS[2 + it % 2], PS[4 + it % 2], PS[6 + it % 2]
                        Yp, Ypb, Gp, Gpb = YpS[it % 2], YpbS[it % 2], GpS[it % 2], GpbS[it % 2]
                        yield from fwd_fft([(src[:, g, :], 0)], [src_b], psB, psX)
                        yield
                        cmul(Yp[0][:, :], Yp[1][:, :], psX[:, 0:256], psX[:, 256:512], KS[o][:, g, 0:256], KS[o][:, g, 256:512], False,
                             [psX], [KS[o]], Ypb, (Yp[2][:, :], Yp[3][:, :]))
                        yield
                        for chn in range(2):
                            fs = slice(chn * 128, (chn + 1) * 128)
                            kb.op("pe", lambda e, fs=fs, chn=chn: e.matmul(psG[:, chn * 256:(chn + 1) * 256], Yp[0][:, fs], E1[:, :], start=True, stop=False),
                                  reads=[Ypb[0], E1], writes=[psG])
                            kb.op("pe", lambda e, fs=fs, chn=chn: e.matmul(psG[:, chn * 256:(chn + 1) * 256], Yp[1][:, fs], E2[:, :], start=False, stop=True),
                                  reads=[Ypb[1], E2], writes=[psG])
                        yield
                        pg = psG[:, :].rearrange("p (ch ri c) -> p ch ri c", ch=2, ri=2)
                        cmul(Gp[0][:, :, :], Gp[1][:, :, :], pg[:, :, 0, :], pg[:, :, 1, :], TW2C[:, :, :], TW2S[:, :, :], False,
                             [psG], [TW2C, TW2S], Gpb, (Gp[2][:, :, :], Gp[3][:, :, :]))
                        yield
                        k = 0
                        for chn in range(2):
                            for (tab, gsrc, gb) in ((IC, Gp[0], Gpb[0]), (IS, Gp[1], Gpb[1])):
                                kb.op("pe", lambda e, chn=chn, tab=tab, gsrc=gsrc, k=k: e.matmul(
                                    psy[:, 0:128], tab[:, chn, :], gsrc[:, chn, :], start=(k == 0), stop=(k == 3)), reads=[tab, gb], writes=[psy])
                                k += 1
                        yield
                        kb.op("dve", lambda e: e.tensor_tensor(dst_ap, psy[:, 0:128].rearrange("p (c a) -> p a c", a=A),
                                                               mulv[:, g, :].rearrange("p (a c) -> p a c", c=G), ALU.mult),
                              reads=[psy, mul_b], writes=[dst_b])

                    def lockstep(gens):
                        gens = list(gens)
                        while gens:
                            nxt = []
                            for g_ in gens:
                                try:
                                    next(g_)
                                    nxt.append(g_)
                                except StopIteration:
                                    pass
                            gens = nxt

                    def spec_group(o, g, it):
                        psB, psX = PS[it % 2], PS[2 + it % 2]
                        yield from fwd_fft([(KTr[o][:, 0, g, :, :].rearrange("p a c -> p (a c)"), 0),
                                            (KTr[o][:, 1, g, :, :].rearrange("p a c -> p (a c)"), 1)], [KTr[o]], psB, psX)
                        yield
                        kb.op("act", lambda e: e.copy(KS[o][:, g, :], psX[:, :]), reads=[psX], writes=[KS[o]])

                    git = 0
                    for cb in range(nblk):
                        kb.dma("sp", DECt[:].rearrange("p h g a c -> p (h g a c)"), CT[pre + "DEC"][cb, :, :], reads=[CT[pre + "DEC"]], writes=[DECt])
                        for ai, tile_ in enumerate((uv, x1, x2)):
                            kb.dma("pool", tile_[:].rearrange("p g x -> p (g x)"), HS[sg["ut"]][ai, cb, :, :], reads=[HS[sg["ut"]]], writes=[tile_])
                        for o in range(2):
                            for hf in range(2):
                                col0 = o * 512 + hf * 256 + cb * cbw
                                npb = 512 // cbw
                                for a in range(A):
                                    p = PS[(a // npb) % 4]
                                    kb.op("pe", lambda e, p=p, a=a, hf=hf, col0=col0, npb=npb: e.matmul(
                                        p[:, (a % npb) * cbw:(a % npb + 1) * cbw], h2T[0:64, hf * 128 * A + a:hf * 128 * A + a + 127 * A + 1:A],
                                        fw3[0:64, col0:col0 + cbw], start=True, stop=True), reads=[h2T, fw3], writes=[p])
                                    if a % npb == npb - 1 or a == A - 1:
                                        a0 = (a // npb) * npb
                                        na = a - a0 + 1
                                        kb.op("dve", lambda e, p=p, a0=a0, na=na, hf=hf, o=o: e.tensor_tensor(
                                            KT[o][:, hf, :, a0:a0 + na, :], p[:, 0:na * cbw].rearrange("p (a g c) -> p g a c", a=na, c=G),
                                            DECt[:, hf, :, a0:a0 + na, :], ALU.mult), reads=[p, DECt], writes=[KT[o]])
                            kb.op("dve", lambda e, o=o: e.tensor_reduce(part[:, :].rearrange("p (g c) -> p g c", c=G),
                                                                        KT[o][:, :, :, :, :].rearrange("p h g a c -> p g c h a"), AX.XY, ALU.add,
                                                                        apply_absolute_value=True), reads=[KT[o]], writes=[part])
                            pe_ = PS[4]
                            kb.op("pe", lambda e, o=o: e.matmul(pe_[0:1, 0:cbw], h2T[0:64, 0:1], fw3[0:64, o * 512 + 256 + cb * cbw:o * 512 + 256 + (cb + 1) * cbw],
                                                                start=True, stop=True), reads=[h2T, fw3], writes=[pe_])
                            kb.op("act", lambda e: e.activation(ex[0:1, :], pe_[0:1, 0:cbw], AF.Abs), reads=[pe_], writes=[ex])
                            kb.op("dve", lambda e: e.tensor_tensor(part[0:1, :], part[0:1, :], ex[0:1, :], ALU.add), reads=[part, ex], writes=[part])
                            pt_ = PS[5]
                            kb.op("pe", lambda e: e.matmul(pt_[:, 0:cbw], ones_f[:, :], part[:, :], start=True, stop=True), reads=[ones_f, part], writes=[pt_])
                            kb.op("dve", lambda e: e.reciprocal(rn[:, :], pt_[:, 0:cbw]), reads=[pt_], writes=[rn])
                            for hf in range(2):
                                kb.op("dve", lambda e, o=o, hf=hf: e.tensor_tensor(
                                    KTr[o][:, hf, :, :, :], KT[o][:, hf, :, :, :],
                                    rn[:, :].rearrange("p (g c) -> p g c", c=G).unsqueeze(2).broadcast_to([128, ngr, A, G]), ALU.mult),
                                    reads=[KT[o], rn], writes=[KTr[o]])
                            kb.op("dve", lambda e, o=o: e.tensor_tensor(
                                KTr[o][0:1, 0, :, 0, :], KTr[o][0:1, 0, :, 0, :].bitcast(F32),
                                brow[0:1, o * 256 + cb * cbw:o * 256 + (cb + 1) * cbw].rearrange("p (g c) -> p g c", c=G), ALU.add),
                                reads=[KTr[o], brow], writes=[KTr[o]])
                            for g in range(0, ngr, 2):
                                gg = [g] + ([g + 1] if g + 1 < ngr else [])
                                lockstep([spec_group(o, g_, git + k_) for k_, g_ in enumerate(gg)])
                                git += len(gg)
                        kb.op("act", lambda e: e.copy(uvr[:], uv[:]), reads=[uv], writes=[uvr])
                        for g in range(0, ngr, 2):
                            gg = [g] + ([g + 1] if g + 1 < ngr else [])
                            lockstep([conv_group(uvr, uvr, g_, 0, x1, x1, u2[:, g_, :].rearrange("p (a c) -> p a c", c=G), u2, git + k_)
                                      for k_, g_ in enumerate(gg)])
                            git += len(gg)
                        for g in range(0, ngr, 2):
                            gg = [g] + ([g + 1] if g + 1 < ngr else [])
                            lockstep([conv_group(u2, u2, g_, 1, x2, x2, res[:, :, g_ * G:(g_ + 1) * G], res, git + k_)
                                      for k_, g_ in enumerate(gg)])
                            git += len(gg)
                        kb.dma("sp", sgm[:], HS["SG"][cb * cbw:(cb + 1) * cbw, off:off + Ls], reads=[HS["SG"]], writes=[sgm])
                        Fv = Fm[:, :].rearrange("c (p a) -> c p a", a=A)
                        for a in range(A):
                            p = PS[4 + (a // 4) % 4]
                            kb.op("pe", lambda e, p=p, a=a: e.transpose(p[0:cbw, (a % 4) * 128:(a % 4 + 1) * 128], res[:, a, :], ident_f[:]),
                                  reads=[res, ident_f], writes=[p])
                            if a % 4 == 3 or a == A - 1:
                                a0 = (a // 4) * 4
                                na = a - a0 + 1
                                kb.op("act", lambda e, p=p, a0=a0, na=na: e.copy(
                                    Fv[:, :, a0:a0 + na], p[0:cbw, 0:na * 128].rearrange("c (a p) -> c p a", p=128)), reads=[p], writes=[Fm])
                        kb.op("pool", lambda e: e.tensor_tensor(sgm[:, :], Fm[:, :], sgm[:, :], ALU.mult), reads=[Fm, sgm], writes=[sgm])
                        kb.dma("sp", mixT[cb * cbw:(cb + 1) * cbw, off:off + Ls], sgm[:, :], reads=[sgm], writes=[Buf()])

    dbgn = [n for n, _ in dbg]
    for l in range(depth):
        last = (l == DEPTH - 1)
        with kb.scope():
            hT = kb.sb("hT", [128, 8, T], BF16)
            G1 = kb.sb("G1", [128, 2, D])
            SH = kb.sb("SH", [128, 2, D])
            phase_mod(l)
            phase_norm(l)
            if "noattn" not in dbgn:
                phase_attn(l, False, not last)
                phase_attn(l, True, not last)
            if "norw" not in dbgn:
                phase_rwkv_prep(l)
            if "nohy" not in dbgn:
                phase_hyena_prep(l, not last)
            if "hT" in dbgn:
                tmp = kb.sb("dbghT", [128, T])
                for j in range(8):
                    kb.op("dve", lambda e, j=j, tmp=tmp: e.tensor_copy(tmp[:], hT[:, j, :]), reads=[hT], writes=[tmp])
                    kb.dma("sp", dbg_t["hT"][:, j, :], tmp[:], reads=[tmp], writes=[dbg_t["hT"]])
        if "norw" not in dbgn:
            {0: phase_rwkv_chunked, 1: phase_rwkv_chunked2, 3: phase_rwkv_chunked3}[RW_V2](l)
            phase_rwkv_out(l, not last)
        if "nohy" not in dbgn:
            phase_hyena_main(l, not last)
        if "noout" not in dbgn:
            phase_out(l, last)
    for n, s_ in dbg:
        if n == "xres":
            with kb.scope():
                tx = kb.sb("dbgx", [128, D])
                for i in range(NT):
                    kb.dma("sp", tx[:], xres[i * 128:(i + 1) * 128, :], reads=[xres_b[i]], writes=[tx])
                    kb.dma("sp", dbg_t[n][i * 128:(i + 1) * 128, :], tx[:], reads=[tx], writes=[dbg_t[n]])
        if n == "mixT":
            with kb.scope():
                tmpb = kb.sb("dbgmb", [128, T], BF16)
                tmpf = kb.sb("dbgmf", [128, T])
                for j in range(8):
                    kb.dma("sp", tmpb[:], mixT[j * 128:(j + 1) * 128, :], reads=[mixT], writes=[tmpb])
                    kb.op("dve", lambda e, tmpb=tmpb, tmpf=tmpf: e.tensor_copy(tmpf[:], tmpb[:]), reads=[tmpb], writes=[tmpf])
                    kb.dma("sp", dbg_t[n][j * 128:(j + 1) * 128, :], tmpf[:], reads=[tmpf], writes=[dbg_t[n]])
    kb.finish()
    kb.es.close()
    return kb, cst


_PROG = {}


def kernel(**inputs):
    if "p" not in _PROG:
        _PROG["p"] = build()
    kb, cst = _PROG["p"]
    f = lambda a: np.ascontiguousarray(np.asarray(a, dtype=np.float32))
    shared = {}
    for n in inputs:
        if n in ("x", "c", "ctx", "c_ctx"):
            continue
        shared[n] = f(inputs[n])
    shared["c_ctx"] = f(inputs["c_ctx"])
    for n, a in cst.items():
        shared["k_" + n] = np.ascontiguousarray(a)
    x, c, ctx = f(inputs["x"]), f(inputs["c"]), f(inputs["ctx"])
    B = x.shape[0]
    in_maps = []
    for b in range(B):
        m = dict(shared)
        m["x"] = np.ascontiguousarray(x[b])
        m["c"] = np.ascontiguousarray(c[b])
        m["ctx"] = np.ascontiguousarray(ctx[b])
        in_maps.append(m)
    res = run_bass_kernel_spmd(kb.nc, in_maps, core_ids=list(range(B)))
    return np.stack([np.asarray(res.results[b]["out"], dtype=np.float32) for b in range(B)], axis=0)
```

```python
import contextlib
import math
import numpy as np
import ml_dtypes
import concourse.bass as bass
import concourse.mybir as mybir
from concourse.bass_utils import run_bass_kernel_spmd

F32 = mybir.dt.float32
BF16 = mybir.dt.bfloat16
F32R = mybir.dt.float32r
ALU = mybir.AluOpType
AF = mybir.ActivationFunctionType
AX = mybir.AxisListType

D = 1024
L = 4096
C = 256
T = L + C
NT = T // 128
DEPTH = 4
D_IN = 3712
HY0, HYG0, RW0, RWG0, WA0, WAG0, FA0, FAG0 = 0, 768, 1024, 1920, 2176, 2688, 2944, 3456
EPS = 1e-6
NSLOT = 12
import os
RW_STAGE = int(os.environ.get('RW_STAGE', '99'))
RW_V2 = int(os.environ.get('RW_V2', '3'))


class Buf:
    def __init__(self, name=""):
        self.name = name
        self.w = None
        self.r = {}

    def wdeps(self):
        return [self.w] if self.w is not None else []

    def rdeps(self):
        return list(self.r.values())

    def add_reader(self, tok):
        k = tok[:2]
        if k not in self.r or self.r[k][2] < tok[2]:
            self.r[k] = tok

    def set_writer(self, tok):
        self.w = tok
        self.r = {}


class Tile(Buf):
    def __init__(self, name, t):
        super().__init__(name)
        self.t = t

    def __getitem__(self, key):
        return self.t[key]


class KB:
    def __init__(self):
        self.nc = bass.Bass("TRN2", target_bir_lowering=False)
        nc = self.nc
        self.es = contextlib.ExitStack()
        self.eng = {"pe": nc.tensor, "act": nc.scalar, "dve": nc.vector, "pool": nc.gpsimd, "sp": nc.sync}
        self.sem = {}
        self.cnt = {}
        self.waited = {e: {} for e in self.eng}
        for e in self.eng:
            self.sem[e] = self.es.enter_context(nc.semaphore("s_" + e))
            self.cnt[e] = 0
        self.slots = {}
        self.slot_i = {}
        for q in ("sp", "act", "pool"):
            self.slots[q] = [[self.es.enter_context(nc.semaphore(f"d_{q}{i}")), 0] for i in range(NSLOT)]
            self.slot_i[q] = 0
        self.n_ins = 0

    def sb(self, name, shape, dt=F32):
        self.uid = getattr(self, "uid", 0) + 1
        name = f"{name}_{self.uid}"
        return Tile(name, self.es.enter_context(self.nc.sbuf_tensor(name, list(shape), dt)))

    def ps(self, name, shape, dt=F32):
        return Tile(name, self.es.enter_context(self.nc.psum_tensor(name, list(shape), dt)))

    def dram(self, name, shape, dt=F32, kind="Internal"):
        t = self.nc.dram_tensor(name, list(shape), dt, kind=kind)
        b = Tile(name, t.ap())
        return b

    def _tok_sem(self, tok):
        if tok[0] == "e":
            return ("e", tok[1]), self.sem[tok[1]], tok[2]
        return ("d", tok[1]), self.slots[tok[1][0]][tok[1][1]][0], tok[2]

    def _wait(self, e, toks):
        need = {}
        for tok in toks:
            if tok is None:
                continue
            key, sem, val = self._tok_sem(tok)
            if tok[0] == "e" and tok[1] == e and e == "pe":
                continue
            if self.waited[e].get(key, 0) >= val:
                continue
            if key not in need or need[key][1] < val:
                need[key] = (sem, val)
        for key, (sem, val) in need.items():
            self.eng[e].wait_ge(sem, val)
            self.waited[e][key] = val

    def op(self, e, fn, reads=(), writes=()):
        toks = []
        for b in reads:
            toks += b.wdeps()
        for b in writes:
            toks += b.wdeps() + b.rdeps()
        self._wait(e, toks)
        ins = fn(self.eng[e])
        self.cnt[e] += 1
        ins.then_inc(self.sem[e], 1)
        tok = ("e", e, self.cnt[e])
        for b in reads:
            b.add_reader(tok)
        for b in writes:
            b.set_writer(tok)
        self.n_ins += 1
        return ins

    def dma(self, q, out, in_, reads=(), writes=(), slow=False):
        i = self.slot_i[q]
        self.slot_i[q] = (i + 1) % NSLOT
        slot = self.slots[q][i]
        toks = []
        if slot[1] > 0:
            toks.append(("d", (q, i), slot[1]))
        for b in reads:
            toks += b.wdeps()
        for b in writes:
            toks += b.wdeps() + b.rdeps()
        self._wait(q, toks)
        if slow:
            ins = self.eng[q].dma_start(out=out, in_=in_, allow_slow_non_contiguous=True)
        else:
            ins = self.eng[q].dma_start(out=out, in_=in_)
        ins.then_inc(slot[0], 16)
        slot[1] += 16
        tok = ("d", (q, i), slot[1])
        for b in reads:
            b.add_reader(tok)
        for b in writes:
            b.set_writer(tok)
        self.n_ins += 1
        return ins

    def barrier(self):
        toks = [("e", e, self.cnt[e]) for e in self.eng if self.cnt[e] > 0]
        for q in self.slots:
            for i, s in enumerate(self.slots[q]):
                if s[1] > 0:
                    toks.append(("d", (q, i), s[1]))
        for e in self.eng:
            self._wait(e, toks)

    def finish(self):
        self.barrier()

    @contextlib.contextmanager
    def scope(self):
        es = contextlib.ExitStack()
        old = self.es
        self.es = es
        try:
            yield
        finally:
            self.barrier()
            self.es = old
            es.close()


def host_consts():
    cst = {}
    cst["ident_bf"] = np.eye(128, dtype=np.float32).astype(ml_dtypes.bfloat16)
    cst["ident_f"] = np.eye(128, dtype=np.float32)
    blk = np.zeros((128, 128), np.float32)
    blk[:64, :64] = 1.0
    blk[64:, 64:] = 1.0
    cst["blk64"] = blk
    cst["ones_f"] = np.ones((128, 128), np.float32)
    t = np.arange(L)
    row = (t // 64).astype(np.float32)
    col = (t % 64).astype(np.float32)
    inv = (10000.0 ** (-np.arange(16, dtype=np.float32) / 16)).astype(np.float32)
    cosT = np.zeros((128, L), np.float32)
    sinT = np.zeros((128, L), np.float32)
    perm = np.zeros((128, 128), np.float32)
    for p in range(128):
        d = p % 64
        sec, half, f = d // 32, (d % 32) // 16, d % 16
        pos = row if sec == 0 else col
        ang = (pos * inv[f]).astype(np.float32)
        cosT[p] = np.cos(ang)
        sinT[p] = np.sin(ang)
        if half == 0:
            perm[p + 16, p] = -1.0
        else:
            perm[p - 16, p] = 1.0
    cst["rope_cos"] = cosT
    cst["rope_sin"] = sinT
    cst["rope_perm"] = perm
    i = np.arange(128)[:, None]
    j = np.arange(384)[None, :]
    cst["wmask"] = np.where((j >= i) & (j <= i + 256), 0.0, -1e30).astype(np.float32)
    ii = np.arange(64)
    ms = np.zeros((128, 2, 64), np.float32); mts = np.zeros((128, 2, 64), np.float32); mti = np.zeros((128, 2, 64), np.float32)
    for hp in range(2):
        rows = slice(hp * 64, hp * 64 + 64)
        ms[rows, 0, :] = (ii[None, :] < ii[:, None]); ms[rows, 1, :] = (ii[None, :] > ii[:, None])
        mts[rows, 0, :] = (ii[:, None] < ii[None, :]); mts[rows, 1, :] = (ii[:, None] > ii[None, :])
        mti[rows, 0, :] = (ii[:, None] <= ii[None, :]); mti[rows, 1, :] = (ii[:, None] >= ii[None, :])
    cst["rw_ms"] = ms; cst["rw_mts"] = mts; cst["rw_mti"] = mti
    cst["rw_msb"] = np.concatenate([ms, ms], 2); cst["rw_mtsb"] = np.concatenate([mts, mts], 2)
    cst["rw_id2"] = np.concatenate([np.eye(64, dtype=np.float32)] * 2, 0)
    cst.update(hy_consts(L, 32, 32, "L"))
    cst.update(hy_consts(C, 2, 64, "C"))
    return cst


def hy_consts(Ls, A, cbw, tag):
    G = 128 // A
    N = 2 * Ls
    out = {}
    p = np.arange(128)
    f1 = np.arange(256)
    F = np.zeros((128, 2, 512), np.float64)
    for h in range(2):
        pp = h * 128 + p
        ang = 2 * np.pi * ((pp[:, None] * f1[None, :]) % 256) / 256
        F[:, h, 0:256] = np.cos(ang)
        F[:, h, 256:512] = -np.sin(ang)
    out["F256"] = F
    a_of_row = np.arange(128) // G
    th = 2 * np.pi * ((a_of_row[:, None] * f1[None, :]) % N) / N
    out["TWC"] = np.cos(th)
    out["TWS"] = np.sin(th)
    Dre = np.zeros((128, 128)); Dim = np.zeros((128, 128))
    E1 = np.zeros((128, 256)); E2 = np.zeros((128, 256))
    for a in range(A):
        for c in range(G):
            for f2 in range(A):
                ph = 2 * np.pi * ((a * f2) % A) / A
                Dre[a * G + c, c * A + f2] = np.cos(ph)
                Dim[a * G + c, c * A + f2] = -np.sin(ph)
                E1[c * A + f2, c * A + a] = np.cos(ph)
                E1[c * A + f2, 128 + c * A + a] = np.sin(ph)
                E2[c * A + f2, c * A + a] = -np.sin(ph)
                E2[c * A + f2, 128 + c * A + a] = np.cos(ph)
    out["Dre"] = Dre; out["Dim"] = Dim; out["nDim"] = -Dim; out["E1"] = E1; out["E2"] = E2
    a_of_col = np.arange(128) % A
    TW2C = np.zeros((128, 2, 128)); TW2S = np.zeros((128, 2, 128))
    IC = np.zeros((128, 2, 128)); IS = np.zeros((128, 2, 128))
    for ch in range(2):
        ff = ch * 128 + np.arange(128)
        th2 = 2 * np.pi * ((ff[:, None] * a_of_col[None, :]) % N) / N
        TW2C[:, ch, :] = np.cos(th2) / N
        TW2S[:, ch, :] = np.sin(th2) / N
        ph = 2 * np.pi * ((ff[:, None] * p[None, :]) % 256) / 256
        IC[:, ch, :] = np.cos(ph)
        IS[:, ch, :] = -np.sin(ph)
    out["TW2C"] = TW2C; out["TW2S"] = TW2S; out["IC"] = IC; out["IS"] = IS
    tp = np.arange(N)
    pos = np.where(tp < Ls, tp, N - tp).astype(np.float64)
    tn = (pos / (Ls - 1)).astype(np.float32)
    w = ((2.0 * math.pi / Ls) * pos).astype(np.float32)
    fb = np.linspace(1e-4, 15.0, 16, dtype=np.float32)
    zT = np.zeros((33, N), np.float32)
    zT[0] = tn
    zT[1:17] = np.cos(fb[:, None] * w[None, :])
    zT[17:33] = np.sin(fb[:, None] * w[None, :])
    out["zT"] = zT
    deltas = np.abs(np.linspace(math.log(1e-2) / 1.5, math.log(1e-2) / 0.3, 256, dtype=np.float32))
    dec = np.exp(-tn[:, None] * deltas[None, :]).astype(np.float32)
    dec[Ls, :] = 0.0
    nblk = 256 // cbw
    ngr = cbw // G
    DEC = np.zeros((nblk, 128, 2, ngr, A, G), np.float32)
    for h in range(2):
        for a in range(A):
            tpp = A * (h * 128 + p) + a
            for b in range(nblk):
                DEC[b, :, h, :, a, :] = dec[tpp, b * cbw:(b + 1) * cbw].reshape(128, ngr, G)
    out["DEC"] = DEC.reshape(nblk, 128, 2 * ngr * A * G)
    return {f"hy{tag}_{k}": np.ascontiguousarray(v.astype(np.float32)) for k, v in out.items()}

CONST_SPECS = None


def build(depth=DEPTH, dbg=()):
    kb = KB()
    nc = kb.nc
    cst = host_consts()
    def inp(name, shape, dt=F32):
        return kb.dram(name, shape, dt, kind="ExternalInput")

    x_in = inp("x", [L, D])
    c_in = inp("c", [D])
    ctx_in = inp("ctx", [C, D])
    cctx_in = inp("c_ctx", [D])
    W = {}
    wspec = {
        "mod_w": [DEPTH, D, 3 * D], "mod_b": [DEPTH, 3 * D], "norm_g": [DEPTH, D], "w_in": [DEPTH, D, D_IN],
        "w_out": [DEPTH, D, D], "wa_sink": [DEPTH, 4], "fa_q_norm": [DEPTH, 64], "fa_k_norm": [DEPTH, 64],
        "final_g": [D],
        "rw_conv": [DEPTH, 3, 896], "rw_w0": [DEPTH, 2, 256], "rw_w_up": [DEPTH, 2, 64, 256], "rw_a0": [DEPTH, 2, 256],
        "rw_a_up": [DEPTH, 2, 64, 256], "rw_k_k": [DEPTH, 256], "rw_k_a": [DEPTH, 256], "rw_r_k": [DEPTH, 256],
        "rw_ln_g": [DEPTH, 256], "rw_ln_b": [DEPTH, 256],
        "hy_conv": [DEPTH, 3, 768], "hy_fw1": [DEPTH, 33, 64], "hy_fb1": [DEPTH, 64], "hy_freq": [DEPTH, 64],
        "hy_fw2": [DEPTH, 64, 64], "hy_fb2": [DEPTH, 64], "hy_fw3": [DEPTH, 64, 1024], "hy_bias": [DEPTH, 2, 256],
    }
    for n, s in wspec.items():
        W[n] = inp(n, s)
    CT = {}
    for n, a in cst.items():
        CT[n] = inp("k_" + n, list(a.shape), BF16 if a.dtype == ml_dtypes.bfloat16 else F32)
    out = kb.dram("out", [L, D], F32, kind="ExternalOutput")
    xres = kb.dram("xres", [T, D], F32)
    mixT = kb.dram("mixT", [D, T], BF16)
    RS = {}
    for n in ("RT", "VT", "AL", "W0", "W1", "B0", "B1", "KD0", "KD1", "YF", "YB"):
        RS[n] = kb.dram("rs_" + n, [256, T])
    RS["VTOK"] = kb.dram("rs_VTOK", [T, 256])
    RS["SGT"] = kb.dram("rs_SGT", [256, T], BF16)
    HS = {"SG": kb.dram("hs_SG", [256, T], BF16),
          "UTL": kb.dram("hs_UTL", [3, 8, 128, 32 * 32]), "UTC": kb.dram("hs_UTC", [3, 4, 128, 2 * 64])}
    dbg_t = {}
    for n, s in dbg:
        dbg_t[n] = kb.dram("dbg_" + n, s, F32, kind="ExternalOutput")

    ident_bf = kb.sb("ident_bf", [128, 128], BF16)
    ident_f = kb.sb("ident_f", [128, 128])
    blk64 = kb.sb("blk64", [128, 128])
    ones_f = kb.sb("ones_f", [128, 128])
    for tl, n in ((ident_bf, "ident_bf"), (ident_f, "ident_f"), (blk64, "blk64"), (ones_f, "ones_f")):
        kb.dma("sp", tl[:], CT[n][:, :], reads=[CT[n]], writes=[tl])
    hT = G1 = SH = None
    GT = kb.sb("GT", [128, 2, D])
    PS = [kb.ps(f"ps{i}", [128, 512]) for i in range(8)]

    xres_b = [Buf(f"xres{i}") for i in range(NT)]

    def x_src(l, i):
        if l == 0:
            if i < 2:
                return ctx_in[i * 128:(i + 1) * 128, :], ctx_in
            return x_in[(i - 2) * 128:(i - 1) * 128, :], x_in
        return xres[i * 128:(i + 1) * 128, :], xres_b[i]

    def phase_mod(l):
        with kb.scope():
            cc = kb.sb("cc", [128, 2, 8])
            sc = kb.sb("sc", [128, 2, 8])
            mw = [kb.sb(f"mw{i}", [128, 8, 512]) for i in range(2)]
            mb = kb.sb("mb", [128, 3 * D])
            ng = kb.sb("ng", [128, D])
            modr = kb.sb("modr", [128, 2, 3 * D])
            kb.dma("sp", cc[:, 0, :], c_in.t.rearrange("(j p) -> p j", p=128), reads=[c_in], writes=[cc], slow=True)
            kb.dma("sp", cc[:, 1, :], cctx_in.t.rearrange("(j p) -> p j", p=128), reads=[cctx_in], writes=[cc], slow=True)
            kb.dma("sp", mb[:], W["mod_b"][l, :].partition_broadcast(128), reads=[W["mod_b"]], writes=[mb])
            kb.dma("sp", ng[:], W["norm_g"][l, :].partition_broadcast(128), reads=[W["norm_g"]], writes=[ng])
            kb.op("act", lambda e: e.activation(sc[:], cc[:], AF.Silu), reads=[cc], writes=[sc])
            for n in range(6):
                m = mw[n % 2]
                kb.dma("sp" if n % 2 == 0 else "pool", m[:],
                       W["mod_w"][l, :, n * 512:(n + 1) * 512].rearrange("(j p) n -> p j n", p=128),
                       reads=[W["mod_w"]], writes=[m])
                for i in range(2):
                    p = PS[(2 * n + i) % 8]
                    for j in range(8):
                        kb.op("pe", lambda e, p=p, i=i, j=j, m=m: e.matmul(
                            p[:, :], sc[:, i, j:j + 1].broadcast_to([128, 128]), m[:, j, :],
                            start=(j == 0), stop=(j == 7)), reads=[sc, m], writes=[p])
                    kb.op("dve", lambda e, p=p, i=i, n=n: e.tensor_tensor(
                        modr[:, i, n * 512:(n + 1) * 512], p[:, :], mb[:, n * 512:(n + 1) * 512], ALU.add),
                        reads=[p, mb], writes=[modr])
            for i in range(2):
                kb.op("dve", lambda e, i=i: e.scalar_tensor_tensor(
                    G1[:, i, :], modr[:, i, D:2 * D], 1.0, ng[:], ALU.add, ALU.mult), reads=[modr, ng], writes=[G1])
                kb.op("act", lambda e, i=i: e.copy(SH[:, i, :], modr[:, i, 0:D]), reads=[modr], writes=[SH])
                kb.op("act", lambda e, i=i: e.copy(GT[:, i, :], modr[:, i, 2 * D:3 * D]), reads=[modr], writes=[GT])

    def phase_norm(l):
        with kb.scope():
            xt = [kb.sb(f"xt{i}", [128, D]) for i in range(3)]
            junk = kb.sb("junk", [128, D])
            hf = [kb.sb(f"hf{i}", [128, D]) for i in range(2)]
            hb = [kb.sb(f"hb{i}", [128, D], BF16) for i in range(2)]
            st = [kb.sb(f"st{i}", [128, 4]) for i in range(2)]
            for i in range(NT):
                x, s, h, hbt = xt[i % 3], st[i % 2], hf[i % 2], hb[i % 2]
                sel = 1 if i < 2 else 0
                src, srcb = x_src(l, i)
                kb.dma("sp" if i % 2 == 0 else "pool", x[:], src, reads=[srcb], writes=[x])
                kb.op("act", lambda e, x=x, s=s: e.activation(junk[:], x[:], AF.Square, accum_out=s[:, 0:1]),
                      reads=[x], writes=[junk, s])
                kb.op("dve", lambda e, s=s: e.tensor_scalar(s[:, 1:2], s[:, 0:1], 1.0 / D, EPS, ALU.mult, ALU.add),
                      reads=[s], writes=[s])
                kb.op("act", lambda e, s=s: e.sqrt(s[:, 2:3], s[:, 1:2]), reads=[s], writes=[s])
                kb.op("dve", lambda e, s=s: e.reciprocal(s[:, 3:4], s[:, 2:3]), reads=[s], writes=[s])
                kb.op("dve", lambda e, x=x, s=s, h=h, sel=sel: e.scalar_tensor_tensor(
                    h[:], x[:], s[:, 3:4], G1[:, sel, :], ALU.mult, ALU.mult), reads=[x, s, G1], writes=[h])
                kb.op("pool", lambda e, h=h, hbt=hbt, sel=sel: e.tensor_tensor(hbt[:], h[:], SH[:, sel, :], ALU.add),
                      reads=[h, SH], writes=[hbt])
                p = PS[i % 4]
                pv = p[:, :].bitcast(BF16)
                for j in range(8):
                    kb.op("pe", lambda e, j=j, pv=pv, hbt=hbt: e.transpose(
                        pv[:, j * 128:(j + 1) * 128], hbt[:, j * 128:(j + 1) * 128], ident_bf[:]),
                        reads=[hbt, ident_bf], writes=[p])
                kb.op("act", lambda e, pv=pv, i=i: e.copy(
                    hT[:, :, i * 128:(i + 1) * 128], pv.rearrange("p (j t) -> p j t", j=8)), reads=[p], writes=[hT])

    def load_w(l, dst, col0, ncols, stage, q="sp"):
        kb.dma(q, stage[:, :, 0:ncols], W["w_in"][l, :, col0:col0 + ncols].rearrange("(j p) n -> p j n", p=128),
               reads=[W["w_in"]], writes=[stage])
        kb.op("pool", lambda e: e.tensor_copy(dst[:, :, 0:ncols], stage[:, :, 0:ncols]), reads=[stage], writes=[dst])

    def proj_fm(p, wt, c0, nc_, t0, nt):
        for j in range(8):
            kb.op("pe", lambda e, j=j: e.matmul(p[0:nc_, 0:nt], wt[:, j, c0:c0 + nc_], hT[:, j, t0:t0 + nt],
                                                start=(j == 0), stop=(j == 7)), reads=[wt, hT], writes=[p])

    def proj_tm(p, wt, c0, nc_, i):
        for j in range(8):
            kb.op("pe", lambda e, j=j: e.matmul(p[:, 0:nc_], hT[:, j, i * 128:(i + 1) * 128], wt[:, j, c0:c0 + nc_],
                                                start=(j == 0), stop=(j == 7)), reads=[wt, hT], writes=[p])

    TCH = [(t0, min(512, T - t0)) for t0 in range(0, T, 512)]

    def qk_prep(l, es_tiles, wt, c0, dst, dst_j, gvec, norm, rope):
        raw, sq, rs, rot = es_tiles
        for ci, (t0, nt) in enumerate(TCH):
            p = PS[ci % 2]
            proj_fm(p, wt, c0, 128, t0, nt)
            if norm:
                kb.op("act", lambda e, p=p, nt=nt: e.activation(sq[:, 0:nt], p[:, 0:nt], AF.Square), reads=[p], writes=[sq])
                p2 = PS[2 + ci % 2]
                kb.op("pe", lambda e, p2=p2, nt=nt: e.matmul(p2[:, 0:nt], blk64[:], sq[:, 0:nt], start=True, stop=True),
                      reads=[blk64, sq], writes=[p2])
                kb.op("dve", lambda e, p2=p2, nt=nt: e.tensor_scalar(rs[:, 0:nt], p2[:, 0:nt], 1.0 / 64, EPS, ALU.mult, ALU.add),
                      reads=[p2], writes=[rs])
                kb.op("act", lambda e, nt=nt: e.sqrt(rs[:, 0:nt], rs[:, 0:nt]), reads=[rs], writes=[rs])
                kb.op("dve", lambda e, nt=nt: e.reciprocal(rs[:, 0:nt], rs[:, 0:nt]), reads=[rs], writes=[rs])
                kb.op("dve", lambda e, p=p, nt=nt: e.scalar_tensor_tensor(
                    raw[:, 0:nt], p[:, 0:nt], gvec[:, 0:1], rs[:, 0:nt], ALU.mult, ALU.mult), reads=[p, gvec, rs], writes=[raw])
            else:
                kb.op("act", lambda e, p=p, nt=nt: e.copy(raw[:, 0:nt], p[:, 0:nt]), reads=[p], writes=[raw])
            lat0 = 0
            if t0 < C:
                lat0 = C - t0
                kb.op("pool", lambda e, t0=t0, lat0=lat0: e.tensor_copy(dst[:, dst_j, t0:t0 + lat0], raw[:, 0:lat0]),
                      reads=[raw], writes=[dst])
            if not rope:
                if nt > lat0:
                    kb.op("pool", lambda e, t0=t0, lat0=lat0, nt=nt: e.tensor_copy(
                        dst[:, dst_j, t0 + lat0:t0 + nt], raw[:, lat0:nt]), reads=[raw], writes=[dst])
                continue
            p3 = PS[4 + ci % 2]
            n_l = nt - lat0
            lp = t0 + lat0 - C
            kb.op("pe", lambda e, p3=p3, lat0=lat0, nt=nt: e.matmul(p3[:, lat0:nt], rope_perm[:], raw[:, lat0:nt], start=True, stop=True),
                  reads=[rope_perm, raw], writes=[p3])
            kb.op("dve", lambda e, p3=p3, lat0=lat0, nt=nt, lp=lp, n_l=n_l: e.tensor_tensor(
                rot[:, lat0:nt], p3[:, lat0:nt], rope_sin[:, lp:lp + n_l], ALU.mult), reads=[p3, rope_sin], writes=[rot])
            kb.op("pool", lambda e, lat0=lat0, nt=nt, lp=lp, n_l=n_l: e.tensor_tensor(
                raw[:, lat0:nt], raw[:, lat0:nt], rope_cos[:, lp:lp + n_l], ALU.mult), reads=[raw, rope_cos], writes=[raw])
            kb.op("dve", lambda e, t0=t0, lat0=lat0, nt=nt: e.tensor_tensor(
                dst[:, dst_j, t0 + lat0:t0 + nt], raw[:, lat0:nt], rot[:, lat0:nt], ALU.add), reads=[raw, rot], writes=[dst])

    rope_cos = rope_sin = rope_perm = None

    def phase_attn(l, dense, with_ctx):
        nonlocal rope_cos, rope_sin, rope_perm
        base = FA0 if dense else WA0
        gbase = FAG0 if dense else WAG0
        mrow = 768 if dense else 512
        with kb.scope():
            wt = kb.sb("wt", [128, 8, 768], BF16)
            gq = kb.sb("gq", [128, 1])
            gk = kb.sb("gk", [128, 1])
            sink = kb.sb("sink", [128, 4])
            if dense:
                for hh in range(2):
                    kb.dma("sp", gq[hh * 64:(hh + 1) * 64, :], W["fa_q_norm"][l, :].rearrange("(d o) -> d o", o=1),
                           reads=[W["fa_q_norm"]], writes=[gq], slow=True)
                    kb.dma("sp", gk[hh * 64:(hh + 1) * 64, :], W["fa_k_norm"][l, :].rearrange("(d o) -> d o", o=1),
                           reads=[W["fa_k_norm"]], writes=[gk], slow=True)
            else:
                kb.dma("sp", sink[:], W["wa_sink"][l, :].partition_broadcast(128), reads=[W["wa_sink"]], writes=[sink])
            QT = kb.sb("QT", [128, 2, T], BF16)
            KT = kb.sb("KT", [128, 1, T], BF16)
            VW = 65 if dense else 64
            Vt = kb.sb("Vt", [128, NT, 2, VW], BF16)
            SG = None
            with kb.scope():
                rope_cos = kb.sb("rope_cos", [128, L])
                rope_sin = kb.sb("rope_sin", [128, L])
                rope_perm = kb.sb("rope_perm", [128, 128])
                kb.dma("sp", rope_cos[:], CT["rope_cos"][:, :], reads=[CT["rope_cos"]], writes=[rope_cos])
                kb.dma("pool", rope_sin[:], CT["rope_sin"][:, :], reads=[CT["rope_sin"]], writes=[rope_sin])
                kb.dma("sp", rope_perm[:], CT["rope_perm"][:, :], reads=[CT["rope_perm"]], writes=[rope_perm])
                stage = kb.sb("wstage", [128, 8, 256])
                w4 = W["w_in"][l, :, base:base + 256].rearrange("(j p) (h d) -> p j h d", p=128, d=64)
                st4 = stage[:, :, 0:256].rearrange("p j (h d) -> p j h d", d=64)
                for hi, h in enumerate((0, 2, 1, 3)):
                    kb.dma("sp", st4[:, :, hi, :], w4[:, :, h, :], reads=[W["w_in"]], writes=[stage])
                kb.op("pool", lambda e: e.tensor_copy(wt[:, :, 0:256], stage[:]), reads=[stage], writes=[wt])
                kb.dma("pool", stage[:], W["w_in"][l, :, base + 256:base + 512].rearrange("(j p) n -> p j n", p=128),
                       reads=[W["w_in"]], writes=[stage])
                kb.op("pool", lambda e: e.tensor_copy(wt[:, :, 256:512], stage[:]), reads=[stage], writes=[wt])
                kb.dma("sp", stage[:], W["w_in"][l, :, gbase:gbase + 256].rearrange("(j p) n -> p j n", p=128),
                       reads=[W["w_in"]], writes=[stage])
                kb.op("pool", lambda e: e.tensor_copy(wt[:, :, 512:768], stage[:]), reads=[stage], writes=[wt])
                tl = (kb.sb("qraw", [128, 512]), kb.sb("qsq", [128, 512]), kb.sb("qrs", [128, 512]), kb.sb("qrot", [128, 512]))
                qk_prep(l, tl, wt, 0, QT, 0, gq, dense, True)
                qk_prep(l, tl, wt, 128, QT, 1, gq, dense, True)
                qk_prep(l, tl, wt, 256, KT, 0, gk, dense, True)
            if dense:
                kb.op("pool", lambda e: e.memset(Vt[:, :, :, 64:65], 1.0), writes=[Vt])
            for i in range(NT):
                p = PS[i % 2]
                proj_tm(p, wt, 384, 128, i)
                kb.op("act", lambda e, p=p, i=i: e.copy(Vt[:, i, :, 0:64], p[:, 0:128].rearrange("p (k d) -> p k d", d=64)),
                      reads=[p], writes=[Vt])
            with kb.scope():
                if dense:
                    attn_dense(l, wt, QT, KT, Vt, mrow, with_ctx)
                else:
                    attn_window(l, wt, QT, KT, Vt, sink, mrow, with_ctx)

    def attn_dense(l, wt, QT, KT, Vt, mrow, with_ctx):
        pt = [kb.sb(f"pt{i}", [128, 512], BF16) for i in range(4)]
        osb = [kb.sb(f"osb{i}", [128, 512]) for i in range(2)]
        rc = [kb.sb(f"rc{i}", [128, 512]) for i in range(2)]
        ob = [kb.sb(f"ob{i}", [128, 512], BF16) for i in range(2)]
        it = 0
        sgt = [kb.sb(f"sgt{i}", [64, 512], BF16) for i in range(2)]
        chunks = []
        if with_ctx:
            chunks.append((0, C, 0, 2))
        for t0 in range(C, T, 512):
            chunks.append((t0, 512, 0, NT))
        for h in range(4):
            kv, pr = h // 2, h % 2
            ks = slice(64 * kv, 64 * kv + 64)
            for (t0, nt, kb0, kb1) in chunks:
                po = PS[4 + it % 2]
                sg = sgt[it % 2]
                pg = PS[6 + it % 2]
                for j in range(8):
                    kb.op("pe", lambda e, j=j: e.matmul(
                        pg[0:64, 0:nt], wt[:, j, 512 + 64 * h:576 + 64 * h], hT[:, j, t0:t0 + nt], start=(j == 0), stop=(j == 7)),
                        reads=[wt, hT], writes=[pg])
                kb.op("act", lambda e: e.activation(sg[0:64, 0:nt], pg[0:64, 0:nt], AF.Silu), reads=[pg], writes=[sg])
                def pv_(kbi):
                    ptt = pt[kbi % 4]
                    kb.op("pe", lambda e: e.matmul(
                        po[0:65, 0:nt], Vt[:, kbi, kv, 0:65], ptt[:, 0:nt], start=(kbi == kb0), stop=(kbi == kb1 - 1)),
                        reads=[Vt, ptt], writes=[po])
                LA = 2
                for kbi in range(kb0, kb1):
                    psS = PS[kbi % 4]
                    ptt = pt[kbi % 4]
                    kb.op("pe", lambda e, psS=psS, kbi=kbi: e.matmul(
                        psS[:, 0:nt], KT[ks, 0, kbi * 128:(kbi + 1) * 128], QT[ks, pr, t0:t0 + nt], start=True, stop=True),
                        reads=[KT, QT], writes=[psS])
                    kb.op("act", lambda e, psS=psS, ptt=ptt: e.activation(ptt[:, 0:nt], psS[:, 0:nt], AF.Exp, scale=0.125),
                          reads=[psS], writes=[ptt])
                    if kbi - LA >= kb0:
                        pv_(kbi - LA)
                for kbi in range(max(kb0, kb1 - LA), kb1):
                    pv_(kbi)
                o_s, r_c, o_b = osb[it % 2], rc[it % 2], ob[it % 2]
                kb.op("dve", lambda e: e.reciprocal(r_c[64:65, 0:nt], po[64:65, 0:nt]), reads=[po], writes=[r_c])
                kb.op("act", lambda e: e.copy(o_s[0:64, 0:nt], po[0:64, 0:nt]), reads=[po], writes=[o_s])
                pb = PS[6 + it % 2]
                kb.op("pe", lambda e: e.matmul(pb[0:64, 0:nt], ones_f[64:65, 0:64], r_c[64:65, 0:nt], start=True, stop=True),
                      reads=[ones_f, r_c], writes=[pb])
                kb.op("dve", lambda e: e.tensor_tensor(o_s[0:64, 0:nt], o_s[0:64, 0:nt], pb[0:64, 0:nt], ALU.mult),
                      reads=[o_s, pb], writes=[o_s])
                kb.op("pool", lambda e: e.tensor_tensor(o_b[0:64, 0:nt], o_s[0:64, 0:nt], sg[0:64, 0:nt], ALU.mult),
                      reads=[o_s, sg], writes=[o_b])
                kb.dma("pool", mixT[mrow + 64 * h:mrow + 64 * h + 64, t0:t0 + nt], o_b[0:64, 0:nt], reads=[o_b], writes=[Buf()])
                it += 1

    def attn_window(l, wt, QT, KT, Vt, sink, mrow, with_ctx):
        wmask = kb.sb("wmask", [128, 384])
        kb.dma("sp", wmask[:], CT["wmask"][:, :], reads=[CT["wmask"]], writes=[wmask])
        nsink = kb.sb("nsink", [128, 4])
        kb.op("dve", lambda e: e.tensor_scalar(nsink[:], sink[:], -1.0, None, ALU.mult), reads=[sink], writes=[nsink])
        S = [kb.sb(f"wS{i}", [128, 640]) for i in range(2)]
        P = [kb.sb(f"wP{i}", [128, 640]) for i in range(2)]
        Pn = [kb.sb(f"wPn{i}", [128, 640], BF16) for i in range(2)]
        PT = [kb.sb(f"wPT{i}", [128, 640], BF16) for i in range(2)]
        st = [kb.sb(f"wst{i}", [128, 8]) for i in range(2)]
        sgt = [kb.sb(f"wsg{i}", [64, 128], BF16) for i in range(2)]
        ob = [kb.sb(f"wob{i}", [64, 128], BF16) for i in range(2)]
        it = 0
        for i in range(0 if with_ctx else 2, NT):
            if i < 2:
                loc = []
            else:
                loc = list(range(max(2, i - 1), min(NT - 1, i + 1) + 1))
            nl = 128 * len(loc)
            m0 = 128 if (i >= 2 and i - 1 < 2) else 0
            nk = nl + C
            ktiles = loc + [0, 1]
            def unit(h, it):
                kv, pr = h // 2, h % 2
                ks = slice(64 * kv, 64 * kv + 64)
                s_, p_, pn_, pt_, st_, sg, o_b = S[it % 2], P[it % 2], Pn[it % 2], PT[it % 2], st[it % 2], sgt[it % 2], ob[it % 2]
                psA, psB, psT, psOG = PS[it % 2], PS[2 + it % 2], PS[4 + it % 2], PS[6 + it % 2]
                psO, psG = psOG, psOG
                q_ap = QT[ks, pr, i * 128:(i + 1) * 128]
                if nl:
                    k0 = loc[0] * 128
                    kb.op("pe", lambda e: e.matmul(psA[:, 0:nl], q_ap, KT[ks, 0, k0:k0 + nl], start=True, stop=True),
                          reads=[QT, KT], writes=[psA])
                    kb.op("dve", lambda e: e.tensor_tensor(s_[:, 0:nl], psA[:, 0:nl], wmask[:, m0:m0 + nl], ALU.add),
                          reads=[psA, wmask], writes=[s_])
                kb.op("pe", lambda e: e.matmul(psB[:, 0:C], q_ap, KT[ks, 0, 0:C], start=True, stop=True),
                      reads=[QT, KT], writes=[psB])
                kb.op("act", lambda e: e.copy(s_[:, nl:nk], psB[:, 0:C]), reads=[psB], writes=[s_])
                yield
                kb.op("dve", lambda e: e.reduce_max(st_[:, 0:1], s_[:, 0:nk], AX.X), reads=[s_], writes=[st_])
                kb.op("dve", lambda e: e.tensor_scalar(st_[:, 1:2], st_[:, 0:1], -0.125, nsink[:, h:h + 1], ALU.mult, ALU.min),
                      reads=[st_, nsink], writes=[st_])
                kb.op("act", lambda e: e.activation(p_[:, 0:nk], s_[:, 0:nk], AF.Exp, bias=st_[:, 1:2], scale=0.125,
                                                    accum_out=st_[:, 2:3]), reads=[s_, st_], writes=[p_, st_])
                kb.op("act", lambda e: e.activation(st_[:, 3:4], sink[:, h:h + 1], AF.Exp, bias=st_[:, 1:2], scale=1.0),
                      reads=[sink, st_], writes=[st_])
                kb.op("dve", lambda e: e.tensor_tensor(st_[:, 4:5], st_[:, 2:3], st_[:, 3:4], ALU.add), reads=[st_], writes=[st_])
                kb.op("dve", lambda e: e.reciprocal(st_[:, 5:6], st_[:, 4:5]), reads=[st_], writes=[st_])
                kb.op("dve", lambda e: e.tensor_scalar(pn_[:, 0:nk], p_[:, 0:nk], st_[:, 5:6], None, ALU.mult),
                      reads=[p_, st_], writes=[pn_])
                yield
                pv = psT[:, :].bitcast(BF16)
                nb = nk // 128
                for b in range(nb):
                    kb.op("pe", lambda e, b=b: e.transpose(pv[:, b * 128:(b + 1) * 128], pn_[:, b * 128:(b + 1) * 128], ident_bf[:]),
                          reads=[pn_, ident_bf], writes=[psT])
                yield
                kb.op("act", lambda e: e.copy(pt_[:, 0:nk], pv[:, 0:nk]), reads=[psT], writes=[pt_])
                yield
                for b in range(nb):
                    kb.op("pe", lambda e, b=b: e.matmul(psO[0:64, 0:128], Vt[:, ktiles[b], kv, 0:64], pt_[:, b * 128:(b + 1) * 128],
                                                        start=(b == 0), stop=(b == nb - 1)), reads=[Vt, pt_], writes=[psO])
                for j in range(8):
                    kb.op("pe", lambda e, j=j: e.matmul(
                        psG[0:64, 128:256], wt[:, j, 512 + 64 * h:576 + 64 * h], hT[:, j, i * 128:(i + 1) * 128],
                        start=(j == 0), stop=(j == 7)), reads=[wt, hT], writes=[psG])
                yield
                kb.op("act", lambda e: e.activation(sg[:, :], psG[0:64, 128:256], AF.Silu), reads=[psG], writes=[sg])
                kb.op("dve", lambda e: e.tensor_tensor(o_b[:, :], psO[0:64, 0:128], sg[:, :], ALU.mult), reads=[psO, sg], writes=[o_b])
                kb.dma("sp", mixT[mrow + 64 * h:mrow + 64 * h + 64, i * 128:(i + 1) * 128], o_b[:, :], reads=[o_b], writes=[Buf()])

            for h0 in (0, 2):
                gens = [unit(h0, it), unit(h0 + 1, it + 1)]
                it += 2
                while gens:
                    nxt = []
                    for g_ in gens:
                        try:
                            next(g_)
                            nxt.append(g_)
                        except StopIteration:
                            pass
                    gens = nxt

    def phase_out(l, last):
        with kb.scope():
            wo = kb.sb("wo", [128, 8, D], BF16)
            stage = kb.sb("wostage", [128, 8, 256])
            for q in range(4):
                kb.dma("sp", stage[:], W["w_out"][l, :, q * 256:(q + 1) * 256].rearrange("(j p) n -> p j n", p=128),
                       reads=[W["w_out"]], writes=[stage])
                kb.op("pool", lambda e, q=q: e.tensor_copy(wo[:, :, q * 256:(q + 1) * 256], stage[:]), reads=[stage], writes=[wo])
            fg = kb.sb("fg", [128, D])
            if last:
                kb.dma("sp", fg[:], W["final_g"][:].partition_broadcast(128), reads=[W["final_g"]], writes=[fg])
            mt = [kb.sb(f"mt{i}", [128, 8, 128], BF16) for i in range(2)]
            xt = [kb.sb(f"oxt{i}", [128, D]) for i in range(2)]
            xn = [kb.sb(f"oxn{i}", [128, D]) for i in range(2)]
            tmp = [kb.sb(f"otmp{i}", [128, 512]) for i in range(2)]
            st = [kb.sb(f"ost{i}", [128, 4]) for i in range(2)]
            junk = kb.sb("ojunk", [128, D])
            mixv = mixT.t.rearrange("(j p) t -> p j t", p=128)
            for it, i in enumerate(range(2 if last else 0, NT)):
                m, x, xo, s = mt[it % 2], xt[it % 2], xn[it % 2], st[it % 2]
                sel = 1 if i < 2 else 0
                kb.dma("sp", m[:], mixv[:, :, i * 128:(i + 1) * 128], reads=[mixT], writes=[m])
                src, srcb = x_src(l, i)
                kb.dma("pool", x[:], src, reads=[srcb], writes=[x])
                for hf in range(2):
                    p = PS[(2 * it + hf) % 8]
                    tp = tmp[hf]
                    for j in range(8):
                        kb.op("pe", lambda e, j=j, p=p, m=m, hf=hf: e.matmul(p[:, :], m[:, j, :], wo[:, j, hf * 512:(hf + 1) * 512],
                                                                     start=(j == 0), stop=(j == 7)), reads=[m, wo], writes=[p])
                    kb.op("dve", lambda e, p=p, tp=tp, hf=hf, sel=sel: e.tensor_tensor(
                        tp[:], p[:, :], GT[:, sel, hf * 512:(hf + 1) * 512], ALU.mult), reads=[p, GT], writes=[tp])
                    kb.op("pool", lambda e, tp=tp, hf=hf, x=x, xo=xo: e.tensor_tensor(
                        xo[:, hf * 512:(hf + 1) * 512], x[:, hf * 512:(hf + 1) * 512], tp[:], ALU.add), reads=[x, tp], writes=[xo])
                if not last:
                    kb.dma("sp", xres[i * 128:(i + 1) * 128, :], xo[:], reads=[xo], writes=[xres_b[i]])
                else:
                    kb.op("act", lambda e, xo=xo, s=s: e.activation(junk[:], xo[:], AF.Square, accum_out=s[:, 0:1]),
                          reads=[xo], writes=[junk, s])
                    kb.op("dve", lambda e, s=s: e.tensor_scalar(s[:, 1:2], s[:, 0:1], 1.0 / D, EPS, ALU.mult, ALU.add),
                          reads=[s], writes=[s])
                    kb.op("act", lambda e, s=s: e.sqrt(s[:, 2:3], s[:, 1:2]), reads=[s], writes=[s])
                    kb.op("dve", lambda e, s=s: e.reciprocal(s[:, 3:4], s[:, 2:3]), reads=[s], writes=[s])
                    kb.op("dve", lambda e, xo=xo, s=s, x=x: e.scalar_tensor_tensor(
                        x[:], xo[:], s[:, 3:4], fg[:], ALU.mult, ALU.mult), reads=[xo, s, fg], writes=[x])
                    kb.dma("sp", out[(i - 2) * 128:(i - 1) * 128, :], x[:], reads=[x], writes=[Buf()])


    def conv_tile(l, wt, cw, ncw, jt, Zraw, Zout):
        for ci, (t0, nt) in enumerate(TCH):
            p = PS[ci % 4]
            proj_fm(p, wt, 0, 128, t0, nt)
            kb.op("act", lambda e, p=p, t0=t0, nt=nt: e.copy(Zraw[:, 1 + t0:1 + t0 + nt], p[:, 0:nt]), reads=[p], writes=[Zraw])
        kb.op("dve", lambda e: e.tensor_scalar(Zout[:, :], Zraw[:, 1:T + 1], cw[:, jt, 1:2], None, ALU.mult), reads=[Zraw, cw], writes=[Zout])
        kb.op("dve", lambda e: e.scalar_tensor_tensor(Zout[:, :], Zraw[:, 0:T], cw[:, jt, 0:1], Zout[:, :], ALU.mult, ALU.add),
              reads=[Zraw, cw, Zout], writes=[Zout])
        kb.op("dve", lambda e: e.scalar_tensor_tensor(Zout[:, :], Zraw[:, 2:T + 2], cw[:, jt, 2:3], Zout[:, :], ALU.mult, ALU.add),
              reads=[Zraw, cw, Zout], writes=[Zout])
        kb.op("dve", lambda e: e.scalar_tensor_tensor(Zout[:, C - 1:C], Zraw[:, C + 1:C + 2], ncw[:, jt, 2:3], Zout[:, C - 1:C], ALU.mult, ALU.add),
              reads=[Zraw, ncw, Zout], writes=[Zout])
        kb.op("dve", lambda e: e.scalar_tensor_tensor(Zout[:, C:C + 1], Zraw[:, C:C + 1], ncw[:, jt, 0:1], Zout[:, C:C + 1], ALU.mult, ALU.add),
              reads=[Zraw, ncw, Zout], writes=[Zout])

    def colvec(name, src_ap, srcb, shape, rearr, **kw):
        t = kb.sb(name, shape)
        kb.dma("sp", t[:], src_ap.rearrange(rearr, **kw), reads=[srcb], writes=[t], slow=True)
        return t

    def phase_rwkv_prep(l):
        with kb.scope():
            stage = kb.sb("rstage", [128, 8, 128])
            wts = [kb.sb(f"rwt{i}", [128, 8, 128], BF16) for i in range(2)]
            cw = kb.sb("rcw", [128, 7, 3])
            for k in range(3):
                kb.dma("sp", cw[:, :, k], W["rw_conv"][l, k, :].rearrange("(j p) -> p j", p=128), reads=[W["rw_conv"]], writes=[cw], slow=True)
            ncw = kb.sb("rncw", [128, 7, 3])
            kb.op("dve", lambda e: e.tensor_scalar(ncw[:], cw[:], -1.0, None, ALU.mult), reads=[cw], writes=[ncw])
            kk_ = colvec("rkk", W["rw_k_k"][l, :], W["rw_k_k"], [128, 2], "(j p) -> p j", p=128)
            ka_ = colvec("rka", W["rw_k_a"][l, :], W["rw_k_a"], [128, 2], "(j p) -> p j", p=128)
            omka = kb.sb("romka", [128, 2])
            kb.op("dve", lambda e: e.tensor_scalar(omka[:], ka_[:], -1.0, 1.0, ALU.mult, ALU.add), reads=[ka_], writes=[omka])
            w0_ = kb.sb("rw0", [128, 2, 2])
            a0_ = kb.sb("ra0", [128, 2, 2])
            for d in range(2):
                kb.dma("sp", w0_[:, d, :], W["rw_w0"][l, d, :].rearrange("(j p) -> p j", p=128), reads=[W["rw_w0"]], writes=[w0_], slow=True)
                kb.dma("sp", a0_[:, d, :], W["rw_a0"][l, d, :].rearrange("(j p) -> p j", p=128), reads=[W["rw_a0"]], writes=[a0_], slow=True)
            wup = kb.sb("rwup", [128, 2, 256])
            kb.dma("sp", wup[0:64, :, :], W["rw_w_up"][l, :, :, :].rearrange("d k n -> k d n"), reads=[W["rw_w_up"]], writes=[wup])
            kb.dma("sp", wup[64:128, :, :], W["rw_a_up"][l, :, :, :].rearrange("d k n -> k d n"), reads=[W["rw_a_up"]], writes=[wup])
            Zraw = kb.sb("rZraw", [128, T + 2])
            Zout = kb.sb("rZout", [128, T])
            Z6 = kb.sb("rZ6", [128, T])
            kb.op("pool", lambda e: e.memset(Zraw[:, 0:1], 0.0), writes=[Zraw])
            kb.op("pool", lambda e: e.memset(Zraw[:, T + 1:T + 2], 0.0), writes=[Zraw])
            tA = [kb.sb(f"rtA{i}", [128, 512]) for i in range(2)]
            tB = [kb.sb(f"rtB{i}", [128, 512]) for i in range(2)]
            tC = [kb.sb(f"rtC{i}", [128, 512]) for i in range(2)]
            tD = [kb.sb(f"rtD{i}", [128, 512]) for i in range(2)]
            tE = [kb.sb(f"rtE{i}", [128, 512]) for i in range(2)]
            tG = [kb.sb(f"rtG{i}", [128, 512], BF16) for i in range(2)]
            vt_ = [kb.sb(f"rvt{i}", [128, 128]) for i in range(2)]
            order = [6, 0, 1, 4, 5, 2, 3, 7, 8]
            for oi, jt in enumerate(order):
                wt = wts[oi % 2]
                c0 = RW0 + jt * 128 if jt < 7 else RWG0 + (jt - 7) * 128
                load_w(l, wt, c0, 128, stage)
                if jt >= 7:
                    for ci, (t0, nt) in enumerate(TCH):
                        p = PS[ci % 4]
                        proj_fm(p, wt, 0, 128, t0, nt)
                        g = tG[ci % 2]
                        kb.op("act", lambda e, p=p, g=g, nt=nt: e.activation(g[:, 0:nt], p[:, 0:nt], AF.Silu), reads=[p], writes=[g])
                        kb.dma("sp", RS["SGT"][(jt - 7) * 128:(jt - 6) * 128, t0:t0 + nt], g[:, 0:nt], reads=[g], writes=[Buf()])
                    continue
                conv_tile(l, wt, cw, ncw, jt, Zraw, Z6 if jt == 6 else Zout)
                if jt == 6:
                    kb.op("act", lambda e: e.activation(Z6[0:64, :], Z6[0:64, :], AF.Tanh), reads=[Z6], writes=[Z6])
                elif jt in (0, 1):
                    kb.dma("sp", RS["RT"][jt * 128:(jt + 1) * 128, :], Zout[:, :], reads=[Zout], writes=[Buf()])
                elif jt in (4, 5):
                    kb.dma("sp", RS["VT"][(jt - 4) * 128:(jt - 3) * 128, :], Zout[:, :], reads=[Zout], writes=[Buf()])
                    for i in range(NT):
                        p = PS[4 + i % 2]
                        kb.op("pe", lambda e, p=p, i=i: e.transpose(p[:, 0:128], Zout[:, i * 128:(i + 1) * 128], ident_f[:]),
                              reads=[Zout, ident_f], writes=[p])
                        v = vt_[i % 2]
                        kb.op("act", lambda e, p=p, v=v: e.copy(v[:, :], p[:, 0:128]), reads=[p], writes=[v])
                        kb.dma("pool", RS["VTOK"][i * 128:(i + 1) * 128, (jt - 4) * 128:(jt - 3) * 128], v[:, :], reads=[v], writes=[Buf()])
                else:
                    pt = jt - 2
                    rows = slice(pt * 128, (pt + 1) * 128)
                    for ci, (t0, nt) in enumerate(TCH):
                        a_, b_, c_, d_, e_ = tA[ci % 2], tB[ci % 2], tC[ci % 2], tD[ci % 2], tE[ci % 2]
                        zc = Zout[:, t0:t0 + nt]
                        kb.op("dve", lambda e: e.tensor_scalar(a_[:, 0:nt], zc, kk_[:, pt:pt + 1], None, ALU.mult), reads=[Zout, kk_], writes=[a_])
                        kb.op("act", lambda e: e.activation(b_[:, 0:nt], a_[:, 0:nt], AF.Square), reads=[a_], writes=[b_])
                        p = PS[ci % 2]
                        kb.op("pe", lambda e: e.matmul(p[:, 0:nt], blk64[:], b_[:, 0:nt], start=True, stop=True), reads=[blk64, b_], writes=[p])
                        kb.op("act", lambda e: e.sqrt(b_[:, 0:nt], p[:, 0:nt]), reads=[p], writes=[b_])
                        kb.op("dve", lambda e: e.tensor_scalar(b_[:, 0:nt], b_[:, 0:nt], 1e-12, None, ALU.max), reads=[b_], writes=[b_])
                        kb.op("dve", lambda e: e.reciprocal(b_[:, 0:nt], b_[:, 0:nt]), reads=[b_], writes=[b_])
                        kb.op("dve", lambda e: e.scalar_tensor_tensor(a_[:, 0:nt], a_[:, 0:nt], -1.0, b_[:, 0:nt], ALU.mult, ALU.mult),
                              reads=[a_, b_], writes=[a_])
                        kb.dma("sp", RS["AL"][rows, t0:t0 + nt], a_[:, 0:nt], reads=[a_], writes=[Buf()])
                        for d in range(2):
                            pa = PS[2 + d]
                            kb.op("pe", lambda e: e.matmul(pa[:, 0:nt], wup[64:128, d, pt * 128:(pt + 1) * 128], Z6[64:128, t0:t0 + nt],
                                                           start=True, stop=True), reads=[wup, Z6], writes=[pa])
                            kb.op("act", lambda e: e.activation(c_[:, 0:nt], pa[:, 0:nt], AF.Sigmoid, bias=a0_[:, d, pt:pt + 1]),
                                  reads=[pa, a0_], writes=[c_])
                            kb.op("dve", lambda e: e.scalar_tensor_tensor(d_[:, 0:nt], c_[:, 0:nt], -1.0, a_[:, 0:nt], ALU.mult, ALU.mult),
                                  reads=[c_, a_], writes=[d_])
                            kb.dma("sp", RS[f"B{d}"][rows, t0:t0 + nt], d_[:, 0:nt], reads=[d_], writes=[Buf()])
                            kb.op("dve", lambda e: e.tensor_scalar(c_[:, 0:nt], c_[:, 0:nt], ka_[:, pt:pt + 1], omka[:, pt:pt + 1], ALU.mult, ALU.add),
                                  reads=[c_, ka_, omka], writes=[c_])
                            kb.op("dve", lambda e: e.tensor_tensor(e_[:, 0:nt], c_[:, 0:nt], zc, ALU.mult), reads=[c_, Zout], writes=[e_])
                            kb.dma("pool", RS[f"KD{d}"][rows, t0:t0 + nt], e_[:, 0:nt], reads=[e_], writes=[Buf()])
                            pw = PS[4 + d]
                            kb.op("pe", lambda e: e.matmul(pw[:, 0:nt], wup[0:64, d, pt * 128:(pt + 1) * 128], Z6[0:64, t0:t0 + nt],
                                                           start=True, stop=True), reads=[wup, Z6], writes=[pw])
                            kb.op("act", lambda e: e.activation(c_[:, 0:nt], pw[:, 0:nt], AF.Sigmoid, bias=w0_[:, d, pt:pt + 1]),
                                  reads=[pw, w0_], writes=[c_])
                            kb.op("dve", lambda e: e.tensor_scalar(d_[:, 0:nt], c_[:, 0:nt], -math.exp(-0.5), None, ALU.mult),
                                  reads=[c_], writes=[d_])
                            kb.dma("pool", RS[f"W{d}"][rows, t0:t0 + nt], d_[:, 0:nt], reads=[d_], writes=[Buf()])

    def phase_rwkv_scan(l):
        with kb.scope():
            ST = [kb.sb(f"ST{d}", [128, 2, 64]) for d in range(2)]
            for d in range(2):
                kb.op("pool", lambda e, d=d: e.memset(ST[d][:], 0.0), writes=[ST[d]])
            names = ("AL", "W", "B", "KD", "RT")
            ch = [[{n: kb.sb(f"c{n}{d}{i}", [128, 2, 128]) for n in names} for i in range(2)] for d in range(2)]
            vch = [[kb.sb(f"cV{d}{i}", [128, 256]) for i in range(2)] for d in range(2)]
            t1 = [kb.sb(f"st1{d}", [128, 2, 64]) for d in range(2)]
            t2 = [kb.sb(f"st2{d}", [128, 2, 64]) for d in range(2)]
            ysb = [kb.sb(f"ysb{d}", [64, 512]) for d in range(2)]
            psSA, psV, psY = [PS[0], PS[1]], [PS[2], PS[3]], [PS[4], PS[5]]
            border = [1, 0] + list(range(NT - 1, 1, -1))
            for ci in range(NT):
                cidx = [ci, border[ci]]
                cur = []
                for d in range(2):
                    c0 = cidx[d] * 128
                    tl_ = ch[d][ci % 2]
                    for n in names:
                        src = RS[n if n in ("AL", "RT") else f"{n}{d}"]
                        kb.dma("sp" if d == 0 else "pool", tl_[n][:],
                               src.t.rearrange("(pr q) t -> q pr t", q=128)[:, :, c0:c0 + 128], reads=[src], writes=[tl_[n]])
                    vv = vch[d][ci % 2]
                    kb.dma("sp" if d == 0 else "pool", vv[:], RS["VTOK"][c0:c0 + 128, :], reads=[RS["VTOK"]], writes=[vv])
                    cur.append((tl_, vv))
                for tl in range(128):
                    for d in range(2):
                        col = tl if d == 0 else 127 - tl
                        tl_, vv = cur[d]
                        S_, sa, pv, py = ST[d], psSA[d], psV[d], psY[d]
                        for pr in range(2):
                            for hp in range(2):
                                rows = slice(64 * hp, 64 * hp + 64)
                                kb.op("pe", lambda e, pr=pr, rows=rows: e.matmul(
                                    sa[rows, pr * 64:(pr + 1) * 64], tl_["AL"][rows, pr, col:col + 1].broadcast_to([64, 64]),
                                    S_[rows, pr, :], start=True, stop=True), reads=[tl_["AL"], S_], writes=[sa])
                        for pr in range(2):
                            for hp in range(2):
                                rows = slice(64 * hp, 64 * hp + 64)
                                h = 2 * pr + hp
                                kb.op("pe", lambda e, pr=pr, rows=rows, h=h: e.matmul(
                                    pv[rows, pr * 64:(pr + 1) * 64], ident_f[:, col:col + 1].broadcast_to([128, 64]),
                                    vv[:, h * 64:(h + 1) * 64], start=True, stop=True), reads=[ident_f, vv], writes=[pv])
                        for pr in range(2):
                            kb.op("dve", lambda e, pr=pr: e.tensor_scalar(
                                t1[d][:, pr, :], sa[:, pr * 64:(pr + 1) * 64], tl_["B"][:, pr, col:col + 1], None, ALU.mult),
                                reads=[sa, tl_["B"]], writes=[t1[d]])
                            kb.op("dve", lambda e, pr=pr: e.scalar_tensor_tensor(
                                t2[d][:, pr, :], pv[:, pr * 64:(pr + 1) * 64], tl_["KD"][:, pr, col:col + 1], t1[d][:, pr, :], ALU.mult, ALU.add),
                                reads=[pv, tl_["KD"], t1[d]], writes=[t2[d]])
                            kb.op("dve", lambda e, pr=pr: e.scalar_tensor_tensor(
                                S_[:, pr, :], S_[:, pr, :], tl_["W"][:, pr, col:col + 1], t2[d][:, pr, :], ALU.mult, ALU.add),
                                reads=[S_, tl_["W"], t2[d]], writes=[S_])
                        for pr in range(2):
                            for hp in range(2):
                                rows = slice(64 * hp, 64 * hp + 64)
                                h = 2 * pr + hp
                                kb.op("pe", lambda e, pr=pr, rows=rows, h=h: e.matmul(
                                    py[0:64, h * 128 + col:h * 128 + col + 1], S_[rows, pr, :], tl_["RT"][rows, pr, col:col + 1],
                                    start=True, stop=True), reads=[S_, tl_["RT"]], writes=[py])
                for d in range(2):
                    c0 = cidx[d] * 128
                    kb.op("act", lambda e, d=d: e.copy(ysb[d][:, :], psY[d][0:64, :]), reads=[psY[d]], writes=[ysb[d]])
                    dst = RS["YF" if d == 0 else "YB"]
                    kb.dma("sp", dst.t.rearrange("(h v) t -> v h t", v=64)[:, :, c0:c0 + 128],
                           ysb[d][:, :].rearrange("v (h t) -> v h t", h=4), reads=[ysb[d]], writes=[Buf()])


    def phase_rwkv_chunked(l):
        CH = 64
        NCH = T // CH
        with kb.scope():
            def ldc(nm, shape):
                t = kb.sb("k" + nm, shape)
                kb.dma("sp", t[:], CT[nm].t, reads=[CT[nm]], writes=[t])
                return t
            Ms = ldc("rw_ms", [128, 2, 64]); MTs = ldc("rw_mts", [128, 2, 64]); MTi = ldc("rw_mti", [128, 2, 64])
            id2 = ldc("rw_id2", [128, 64])
            ones = kb.sb("rones", [128, 64])
            kb.op("pool", lambda e: e.memset(ones[:], 1.0), writes=[ones])
            ST = kb.sb("cST", [128, 4, 64])
            kb.op("pool", lambda e: e.memset(ST[:], 0.0), writes=[ST])
            names = ("AL", "W", "B", "KD", "RT")
            def t4(nm, n=2, w=64):
                return [kb.sb(f"{nm}{i}", [128, 4, w]) for i in range(n)]
            IN = {n: t4("ci" + n) for n in names}
            VTK = t4("cVTK")
            CS = t4("cCS", 1)[0]; TOT = kb.sb("cTOT", [128, 4]); TMP = t4("cTMP", 1)[0]
            Epos = t4("cEp", 1)[0]; Eneg = t4("cEn", 1)[0]; Eprev = t4("cEv", 1)[0]; Etot = t4("cEt", 1)[0]; Wtot = kb.sb("cWt", [128, 4])
            Ab = t4("cAb", 1)[0]; Bb = t4("cBb", 1)[0]; Kb = t4("cKb", 1)[0]; Rb = t4("cRb", 1)[0]; Bt = t4("cBt", 1)[0]; Kt = t4("cKt", 1)[0]
            Q = t4("cQ"); P = t4("cP"); ArbT = t4("cArbT", 1)[0]; AkvT = t4("cAkvT", 1)[0]; ArkT = t4("cArkT", 1)[0]
            X = t4("cX", 2, 128); Btok = t4("cBtok", 1)[0]; Ktok = t4("cKtok", 1)[0]
            RAT = t4("cRAT", 1)[0]; McT = t4("cMcT", 1)[0]; NcS = t4("cNcS", 1)[0]; DG = t4("cDG", 1)[0]
            ysb = [kb.sb(f"cysb{d}", [64, 256]) for d in range(2)]
            border = [3, 2, 1, 0] + list(range(NCH - 1, 3, -1))
            DP = [(d, pr) for d in range(2) for pr in range(2)]
            HP = [slice(0, 64), slice(64, 128)]

            def mm_all(ps, col_fn, lhs_fn, rhs_fn, reads, start=True, stop=True, w=None):
                for dp in range(4):
                    for hp in range(2):
                        r = HP[hp]
                        c0, c1 = col_fn(dp)
                        kb.op("pe", lambda e, dp=dp, r=r, c0=c0, c1=c1: e.matmul(ps[r, c0:c1], lhs_fn(dp, r), rhs_fn(dp, r), start=start, stop=stop),
                              reads=reads, writes=[ps])

            for ci in range(NCH):
                cidx = [ci, border[ci]]
                i2 = ci % 2
                for d in range(2):
                    c0 = cidx[d] * CH
                    for n in names:
                        src = RS[n if n in ("AL", "RT") else f"{n}{d}"]
                        kb.dma("sp" if d == 0 else "pool", IN[n][i2][:, 2 * d:2 * d + 2, :],
                               src.t.rearrange("(pr q) t -> q pr t", q=128)[:, :, c0:c0 + CH], reads=[src], writes=[IN[n][i2]])
                    for hp in range(2):
                        kb.dma("sp" if d == 0 else "pool", VTK[i2][HP[hp], 2 * d:2 * d + 2, :],
                               RS["VTOK"][c0:c0 + CH, :].rearrange("t (pr hp v) -> t pr hp v", pr=2, hp=2)[:, :, hp, :],
                               reads=[RS["VTOK"]], writes=[VTK[i2]])
                al, lw, be, kd, rt, vt = IN["AL"][i2], IN["W"][i2], IN["B"][i2], IN["KD"][i2], IN["RT"][i2], VTK[i2]
                if RW_STAGE <= 1:
                    continue
                for dp in range(4):
                    kb.op("dve", lambda e, dp=dp: e.tensor_tensor_scan(CS[:, dp, :], ones[:, :], lw[:, dp, :], 0.0, ALU.mult, ALU.add),
                          reads=[ones, lw], writes=[CS])
                kb.op("dve", lambda e: e.tensor_copy(TOT[:, :], CS[:, :, CH - 1]), reads=[CS], writes=[TOT])
                kb.op("dve", lambda e: e.tensor_tensor(CS[:, 2:4, :], lw[:, 2:4, :], CS[:, 2:4, :], ALU.subtract), reads=[lw, CS], writes=[CS])
                kb.op("dve", lambda e: e.tensor_tensor(CS[:, 2:4, :], CS[:, 2:4, :], TOT[:, 2:4].unsqueeze(2).broadcast_to([128, 2, CH]), ALU.add),
                      reads=[CS, TOT], writes=[CS])
                kb.op("act", lambda e: e.activation(Epos[:], CS[:], AF.Exp), reads=[CS], writes=[Epos])
                kb.op("act", lambda e: e.activation(Eneg[:], CS[:], AF.Exp, scale=-1.0), reads=[CS], writes=[Eneg])
                kb.op("pool", lambda e: e.tensor_tensor(TMP[:], CS[:], lw[:], ALU.subtract), reads=[CS, lw], writes=[TMP])
                kb.op("act", lambda e: e.activation(Eprev[:], TMP[:], AF.Exp), reads=[TMP], writes=[Eprev])
                kb.op("dve", lambda e: e.tensor_tensor(Etot[:], TOT[:, :].unsqueeze(2).broadcast_to([128, 4, CH]), CS[:], ALU.subtract),
                      reads=[TOT, CS], writes=[Etot])
                kb.op("act", lambda e: e.activation(Etot[:], Etot[:], AF.Exp), reads=[Etot], writes=[Etot])
                kb.op("act", lambda e: e.activation(Wtot[:], TOT[:], AF.Exp), reads=[TOT], writes=[Wtot])
                kb.op("dve", lambda e: e.tensor_tensor(Ab[:], al[:], Eprev[:], ALU.mult), reads=[al, Eprev], writes=[Ab])
                kb.op("pool", lambda e: e.tensor_tensor(Bb[:], be[:], Eneg[:], ALU.mult), reads=[be, Eneg], writes=[Bb])
                kb.op("dve", lambda e: e.tensor_tensor(Kb[:], kd[:], Eneg[:], ALU.mult), reads=[kd, Eneg], writes=[Kb])
                kb.op("pool", lambda e: e.tensor_tensor(Rb[:], rt[:], Epos[:], ALU.mult), reads=[rt, Epos], writes=[Rb])
                kb.op("dve", lambda e: e.tensor_tensor(Bt[:], be[:], Etot[:], ALU.mult), reads=[be, Etot], writes=[Bt])
                kb.op("pool", lambda e: e.tensor_tensor(Kt[:], kd[:], Etot[:], ALU.mult), reads=[kd, Etot], writes=[Kt])
                if RW_STAGE <= 2:
                    continue
                PA, PB, PC, PT1, PD, PX, PPQ, PE_ = PS
                mm_all(PA, lambda dp: (dp * 128, dp * 128 + 64), lambda dp, r: Bb[r, dp, :], lambda dp, r: Ab[r, dp, :], [Bb, Ab])
                mm_all(PA, lambda dp: (dp * 128 + 64, dp * 128 + 128), lambda dp, r: Bb[r, dp, :], lambda dp, r: Rb[r, dp, :], [Bb, Rb])
                mm_all(PB, lambda dp: (dp * 128, dp * 128 + 64), lambda dp, r: Kb[r, dp, :], lambda dp, r: Ab[r, dp, :], [Kb, Ab])
                mm_all(PB, lambda dp: (dp * 128 + 64, dp * 128 + 128), lambda dp, r: Kb[r, dp, :], lambda dp, r: Rb[r, dp, :], [Kb, Rb])
                mm_all(PC, lambda dp: (dp * 64, dp * 64 + 64), lambda dp, r: Ab[r, dp, :], lambda dp, r: Bb[r, dp, :], [Ab, Bb])
                q0, p0 = Q[0], P[0]
                pav = PA[:, :].rearrange("p (dp x) -> p dp x", dp=4)
                pbv = PB[:, :].rearrange("p (dp x) -> p dp x", dp=4)
                def mk(m):
                    return m[:, :, :].unsqueeze(2).broadcast_to([128, 2, 2, 64])
                def v4(ap):
                    return ap.rearrange("p (d pr) x -> p d pr x", d=2)
                kb.op("dve", lambda e: e.tensor_tensor(v4(q0[:]), v4(pav[:, :, 0:64]), mk(MTs), ALU.mult), reads=[PA, MTs], writes=[q0])
                kb.op("dve", lambda e: e.tensor_tensor(v4(ArbT[:]), v4(pav[:, :, 64:128]), mk(MTi), ALU.mult), reads=[PA, MTi], writes=[ArbT])
                kb.op("dve", lambda e: e.tensor_tensor(v4(AkvT[:]), v4(pbv[:, :, 0:64]), mk(MTs), ALU.mult), reads=[PB, MTs], writes=[AkvT])
                kb.op("dve", lambda e: e.tensor_tensor(v4(ArkT[:]), v4(pbv[:, :, 64:128]), mk(MTi), ALU.mult), reads=[PB, MTi], writes=[ArkT])
                kb.op("dve", lambda e: e.tensor_tensor(v4(p0[:]), v4(PC[:, 0:256].rearrange("p (dp x) -> p dp x", dp=4)), mk(Ms), ALU.mult),
                      reads=[PC, Ms], writes=[p0])
                if RW_STAGE <= 3:
                    continue
                def idb(r):
                    return ident_f[r, r.start:r.start + 64]
                mm_all(PT1, lambda dp: (dp * 128, dp * 128 + 64), lambda dp, r: Ab[r, dp, :], lambda dp, r: idb(r), [Ab, ident_f])
                mm_all(PT1, lambda dp: (dp * 128 + 64, dp * 128 + 128), lambda dp, r: Bt[r, dp, :], lambda dp, r: idb(r), [Bt, ident_f])
                mm_all(PC, lambda dp: (256 + dp * 64, 256 + dp * 64 + 64), lambda dp, r: Kt[r, dp, :], lambda dp, r: idb(r), [Kt, ident_f])
                x0 = X[0]
                pt1v = PT1[:, :].rearrange("p (dp x) -> p dp x", dp=4)
                kb.op("act", lambda e: e.copy(x0[:, :, 0:64], pt1v[:, :, 0:64]), reads=[PT1], writes=[x0])
                kb.op("act", lambda e: e.copy(Btok[:], pt1v[:, :, 64:128]), reads=[PT1], writes=[Btok])
                kb.op("act", lambda e: e.copy(Ktok[:], PC[:, 256:512].rearrange("p (dp x) -> p dp x", dp=4)), reads=[PC], writes=[Ktok])
                if RW_STAGE <= 4:
                    continue
                mm_all(PD, lambda dp: (dp * 64, dp * 64 + 64), lambda dp, r: AkvT[r, dp, :], lambda dp, r: vt[r, dp, :], [AkvT, vt])
                kb.op("act", lambda e: e.copy(x0[:, :, 64:128], PD[:, 0:256].rearrange("p (dp x) -> p dp x", dp=4)), reads=[PD], writes=[x0])
                if RW_STAGE <= 5:
                    continue
                qc, pc, xc = Q[0], P[0], X[0]
                for j in range(6):
                    qn, pn, xn = Q[(j + 1) % 2], P[(j + 1) % 2], X[(j + 1) % 2]
                    mm_all(PX, lambda dp: (dp * 128, dp * 128 + 128), lambda dp, r: qc[r, dp, :], lambda dp, r: xc[r, dp, :], [qc, xc])
                    kb.op("dve", lambda e, xn=xn, xc=xc: e.tensor_tensor(xn[:], xc[:], PX[:, :].rearrange("p (dp x) -> p dp x", dp=4), ALU.add),
                          reads=[xc, PX], writes=[xn])
                    if j < 5:
                        mm_all(PPQ, lambda dp: (dp * 64, dp * 64 + 64), lambda dp, r: qc[r, dp, :], lambda dp, r: pc[r, dp, :], [qc, pc])
                        mm_all(PPQ, lambda dp: (256 + dp * 64, 256 + dp * 64 + 64), lambda dp, r: pc[r, dp, :], lambda dp, r: qc[r, dp, :], [qc, pc])
                        kb.op("act", lambda e, pn=pn: e.copy(pn[:], PPQ[:, 0:256].rearrange("p (dp x) -> p dp x", dp=4)), reads=[PPQ], writes=[pn])
                        kb.op("act", lambda e, qn=qn: e.copy(qn[:], PPQ[:, 256:512].rearrange("p (dp x) -> p dp x", dp=4)), reads=[PPQ], writes=[qn])
                    qc, pc, xc = qn, pn, xn
                if RW_STAGE <= 6:
                    continue
                mm_all(PD, lambda dp: (256 + dp * 64, 256 + dp * 64 + 64), lambda dp, r: xc[r, dp, 0:64], lambda dp, r: ArbT[r, dp, :], [xc, ArbT])
                kb.op("dve", lambda e: e.tensor_tensor(RAT[:], Rb[:], PD[:, 256:512].rearrange("p (dp x) -> p dp x", dp=4), ALU.add),
                      reads=[Rb, PD], writes=[RAT])
                mm_all(PE_, lambda dp: (dp * 64, dp * 64 + 64), lambda dp, r: xc[r, dp, 0:64], lambda dp, r: Btok[r, dp, :], [xc, Btok])
                kb.op("pool", lambda e: e.tensor_tensor(DG[:], id2[:, :].unsqueeze(1).broadcast_to([128, 4, 64]),
                                                        Wtot[:, :].unsqueeze(2).broadcast_to([128, 4, 64]), ALU.mult), reads=[id2, Wtot], writes=[DG])
                kb.op("dve", lambda e: e.tensor_tensor(McT[:], DG[:], PE_[:, 0:256].rearrange("p (dp x) -> p dp x", dp=4), ALU.add),
                      reads=[DG, PE_], writes=[McT])
                for dp in range(4):
                    for hp in range(2):
                        r = HP[hp]
                        c0 = 256 + dp * 64
                        kb.op("pe", lambda e, dp=dp, r=r, c0=c0: e.matmul(PE_[r, c0:c0 + 64], Btok[r, dp, :], xc[r, dp, 64:128], start=True, stop=False),
                              reads=[Btok, xc], writes=[PE_])
                        kb.op("pe", lambda e, dp=dp, r=r, c0=c0: e.matmul(PE_[r, c0:c0 + 64], Ktok[r, dp, :], vt[r, dp, :], start=False, stop=True),
                              reads=[Ktok, vt], writes=[PE_])
                kb.op("act", lambda e: e.copy(NcS[:], PE_[:, 256:512].rearrange("p (dp x) -> p dp x", dp=4)), reads=[PE_], writes=[NcS])
                if RW_STAGE <= 7:
                    continue
                PYs = [PA, PT1]
                for dp in range(4):
                    for hp in range(2):
                        r = HP[hp]
                        PY = PYs[hp]
                        c0 = dp * 64
                        kb.op("pe", lambda e, dp=dp, r=r, c0=c0, PY=PY: e.matmul(PY[0:64, c0:c0 + 64], ST[r, dp, :], RAT[r, dp, :], start=True, stop=False),
                              reads=[ST, RAT], writes=[PY])
                        kb.op("pe", lambda e, dp=dp, r=r, c0=c0, PY=PY: e.matmul(PY[0:64, c0:c0 + 64], xc[r, dp, 64:128], ArbT[r, dp, :], start=False, stop=False),
                              reads=[xc, ArbT], writes=[PY])
                        kb.op("pe", lambda e, dp=dp, r=r, c0=c0, PY=PY: e.matmul(PY[0:64, c0:c0 + 64], vt[r, dp, :], ArkT[r, dp, :], start=False, stop=True),
                              reads=[vt, ArkT], writes=[PY])
                for d in range(2):
                    c0 = cidx[d] * CH
                    yv = ysb[d][:, :].rearrange("v (pr hp t) -> v pr hp t", pr=2, hp=2)
                    for hp in range(2):
                        kb.op("act", lambda e, d=d, hp=hp, yv=yv: e.copy(
                            yv[:, :, hp, :], PYs[hp][0:64, d * 128:(d + 1) * 128].rearrange("v (pr t) -> v pr t", pr=2)), reads=[PYs[hp]], writes=[ysb[d]])
                    dst = RS["YF" if d == 0 else "YB"]
                    kb.dma("sp", dst.t.rearrange("(h v) t -> v h t", v=64)[:, :, c0:c0 + CH],
                           ysb[d][:, :].rearrange("v (h t) -> v h t", h=4), reads=[ysb[d]], writes=[Buf()])
                if RW_STAGE <= 8:
                    continue
                PSS = PB
                mm_all(PSS, lambda dp: (dp * 64, dp * 64 + 64), lambda dp, r: McT[r, dp, :], lambda dp, r: ST[r, dp, :], [McT, ST])
                kb.op("dve", lambda e: e.tensor_tensor(ST[:], NcS[:], PSS[:, 0:256].rearrange("p (dp x) -> p dp x", dp=4), ALU.add),
                      reads=[NcS, PSS], writes=[ST])


    def phase_rwkv_chunked3(l):
        CH = 64
        NCH = T // CH
        with kb.scope():
            def ldc(nm, shape):
                t = kb.sb("k" + nm, shape)
                kb.dma("sp", t[:], CT[nm].t, reads=[CT[nm]], writes=[t])
                return t
            Ms = ldc("rw_ms", [128, 2, 64]); MTs = ldc("rw_mts", [128, 2, 64]); MTi = ldc("rw_mti", [128, 2, 64])
            id2 = ldc("rw_id2", [128, 64])
            ones = kb.sb("rones", [128, 64])
            kb.op("pool", lambda e: e.memset(ones[:], 1.0), writes=[ones])
            ST = kb.sb("cST", [128, 4, 64])
            kb.op("pool", lambda e: e.memset(ST[:], 0.0), writes=[ST])
            names = ("AL", "W", "B", "KD", "RT")
            import types
            def alloc_set(si):
                S = types.SimpleNamespace()
                def t4(nm, n=2, w=64):
                    return [kb.sb(f"{nm}s{si}_{i}", [128, 4, w]) for i in range(n)]
                S.IN = {n: t4("ci" + n, 1)[0] for n in names}
                S.VTK = t4("cVTK", 1)[0]
                S.CS = t4("cCS", 1)[0]; S.TOT = kb.sb(f"cTOT{si}", [128, 4]); S.TMP = t4("cTMP", 1)[0]
                S.Epos = t4("cEp", 1)[0]; S.Eneg = t4("cEn", 1)[0]; S.Eprev = t4("cEv", 1)[0]; S.Etot = t4("cEt", 1)[0]; S.Wtot = kb.sb(f"cWt{si}", [128, 4])
                S.Ab = t4("cAb", 1)[0]; S.Bb = t4("cBb", 1)[0]; S.Kb = t4("cKb", 1)[0]; S.Rb = t4("cRb", 1)[0]; S.Bt = t4("cBt", 1)[0]; S.Kt = t4("cKt", 1)[0]
                S.Q = t4("cQ"); S.P = t4("cP"); S.ArbT = t4("cArbT", 1)[0]; S.AkvT = t4("cAkvT", 1)[0]; S.ArkT = t4("cArkT", 1)[0]
                S.X = t4("cX", 2, 128); S.Btok = t4("cBtok", 1)[0]; S.Ktok = t4("cKtok", 1)[0]
                S.RAT = t4("cRAT", 1)[0]; S.McT = t4("cMcT", 1)[0]; S.NcS = t4("cNcS", 1)[0]; S.DG = t4("cDG", 1)[0]
                S.ysb = [kb.sb(f"cysb{si}_{d}", [64, 256]) for d in range(2)]
                S.banks = PS[4 * si:4 * si + 4]
                return S
            SETS = [alloc_set(0), alloc_set(1)]
            border = [3, 2, 1, 0] + list(range(NCH - 1, 3, -1))
            DP = [(d, pr) for d in range(2) for pr in range(2)]
            HP = [slice(0, 64), slice(64, 128)]

            def mm_all(ps, col_fn, lhs_fn, rhs_fn, reads, start=True, stop=True, w=None):
                for dp in range(4):
                    for hp in range(2):
                        r = HP[hp]
                        c0, c1 = col_fn(dp)
                        kb.op("pe", lambda e, dp=dp, r=r, c0=c0, c1=c1: e.matmul(ps[r, c0:c1], lhs_fn(dp, r), rhs_fn(dp, r), start=start, stop=stop),
                              reads=reads, writes=[ps])

            def chunk_gen(ci, S):
                cidx = [ci, border[ci]]
                IN, VTK = S.IN, S.VTK
                for d in range(2):
                    c0 = cidx[d] * CH
                    for n in names:
                        src = RS[n if n in ("AL", "RT") else f"{n}{d}"]
                        kb.dma("sp" if d == 0 else "pool", IN[n][:, 2 * d:2 * d + 2, :],
                               src.t.rearrange("(pr q) t -> q pr t", q=128)[:, :, c0:c0 + CH], reads=[src], writes=[IN[n]])
                    for hp in range(2):
                        kb.dma("sp" if d == 0 else "pool", VTK[HP[hp], 2 * d:2 * d + 2, :],
                               RS["VTOK"][c0:c0 + CH, :].rearrange("t (pr hp v) -> t pr hp v", pr=2, hp=2)[:, :, hp, :],
                               reads=[RS["VTOK"]], writes=[VTK])
                al, lw, be, kd, rt, vt = IN["AL"], IN["W"], IN["B"], IN["KD"], IN["RT"], VTK
                CS, TOT, TMP, Epos, Eneg, Eprev, Etot, Wtot = S.CS, S.TOT, S.TMP, S.Epos, S.Eneg, S.Eprev, S.Etot, S.Wtot
                Ab, Bb, Kb, Rb, Bt, Kt, Q, P, ArbT, AkvT, ArkT = S.Ab, S.Bb, S.Kb, S.Rb, S.Bt, S.Kt, S.Q, S.P, S.ArbT, S.AkvT, S.ArkT
                X, Btok, Ktok, RAT, McT, NcS, DG, ysb = S.X, S.Btok, S.Ktok, S.RAT, S.McT, S.NcS, S.DG, S.ysb
                yield
                for dp in range(4):
                    kb.op("dve", lambda e, dp=dp: e.tensor_tensor_scan(CS[:, dp, :], ones[:, :], lw[:, dp, :], 0.0, ALU.mult, ALU.add),
                          reads=[ones, lw], writes=[CS])
                kb.op("dve", lambda e: e.tensor_copy(TOT[:, :], CS[:, :, CH - 1]), reads=[CS], writes=[TOT])
                kb.op("dve", lambda e: e.tensor_tensor(CS[:, 2:4, :], lw[:, 2:4, :], CS[:, 2:4, :], ALU.subtract), reads=[lw, CS], writes=[CS])
                kb.op("dve", lambda e: e.tensor_tensor(CS[:, 2:4, :], CS[:, 2:4, :], TOT[:, 2:4].unsqueeze(2).broadcast_to([128, 2, CH]), ALU.add),
                      reads=[CS, TOT], writes=[CS])
                kb.op("act", lambda e: e.activation(Epos[:], CS[:], AF.Exp), reads=[CS], writes=[Epos])
                kb.op("act", lambda e: e.activation(Eneg[:], CS[:], AF.Exp, scale=-1.0), reads=[CS], writes=[Eneg])
                kb.op("pool", lambda e: e.tensor_tensor(TMP[:], CS[:], lw[:], ALU.subtract), reads=[CS, lw], writes=[TMP])
                kb.op("act", lambda e: e.activation(Eprev[:], TMP[:], AF.Exp), reads=[TMP], writes=[Eprev])
                kb.op("dve", lambda e: e.tensor_tensor(Etot[:], TOT[:, :].unsqueeze(2).broadcast_to([128, 4, CH]), CS[:], ALU.subtract),
                      reads=[TOT, CS], writes=[Etot])
                kb.op("act", lambda e: e.activation(Etot[:], Etot[:], AF.Exp), reads=[Etot], writes=[Etot])
                kb.op("act", lambda e: e.activation(Wtot[:], TOT[:], AF.Exp), reads=[TOT], writes=[Wtot])
                kb.op("dve", lambda e: e.tensor_tensor(Ab[:], al[:], Eprev[:], ALU.mult), reads=[al, Eprev], writes=[Ab])
                kb.op("pool", lambda e: e.tensor_tensor(Bb[:], be[:], Eneg[:], ALU.mult), reads=[be, Eneg], writes=[Bb])
                kb.op("dve", lambda e: e.tensor_tensor(Kb[:], kd[:], Eneg[:], ALU.mult), reads=[kd, Eneg], writes=[Kb])
                kb.op("pool", lambda e: e.tensor_tensor(Rb[:], rt[:], Epos[:], ALU.mult), reads=[rt, Epos], writes=[Rb])
                kb.op("dve", lambda e: e.tensor_tensor(Bt[:], be[:], Etot[:], ALU.mult), reads=[be, Etot], writes=[Bt])
                kb.op("pool", lambda e: e.tensor_tensor(Kt[:], kd[:], Etot[:], ALU.mult), reads=[kd, Etot], writes=[Kt])
                yield
                PA, PB, PC, PT1 = S.banks
                PD, PX, PPQ, PE_ = PA, PB, PC, PT1
                mm_all(PA, lambda dp: (dp * 128, dp * 128 + 64), lambda dp, r: Bb[r, dp, :], lambda dp, r: Ab[r, dp, :], [Bb, Ab])
                mm_all(PA, lambda dp: (dp * 128 + 64, dp * 128 + 128), lambda dp, r: Bb[r, dp, :], lambda dp, r: Rb[r, dp, :], [Bb, Rb])
                mm_all(PB, lambda dp: (dp * 128, dp * 128 + 64), lambda dp, r: Kb[r, dp, :], lambda dp, r: Ab[r, dp, :], [Kb, Ab])
                mm_all(PB, lambda dp: (dp * 128 + 64, dp * 128 + 128), lambda dp, r: Kb[r, dp, :], lambda dp, r: Rb[r, dp, :], [Kb, Rb])
                mm_all(PC, lambda dp: (dp * 64, dp * 64 + 64), lambda dp, r: Ab[r, dp, :], lambda dp, r: Bb[r, dp, :], [Ab, Bb])
                q0, p0 = Q[0], P[0]
                pav = PA[:, :].rearrange("p (dp x) -> p dp x", dp=4)
                pbv = PB[:, :].rearrange("p (dp x) -> p dp x", dp=4)
                def mk(m):
                    return m[:, :, :].unsqueeze(2).broadcast_to([128, 2, 2, 64])
                def v4(ap):
                    return ap.rearrange("p (d pr) x -> p d pr x", d=2)
                kb.op("dve", lambda e: e.tensor_tensor(v4(q0[:]), v4(pav[:, :, 0:64]), mk(MTs), ALU.mult), reads=[PA, MTs], writes=[q0])
                kb.op("dve", lambda e: e.tensor_tensor(v4(ArbT[:]), v4(pav[:, :, 64:128]), mk(MTi), ALU.mult), reads=[PA, MTi], writes=[ArbT])
                kb.op("dve", lambda e: e.tensor_tensor(v4(AkvT[:]), v4(pbv[:, :, 0:64]), mk(MTs), ALU.mult), reads=[PB, MTs], writes=[AkvT])
                kb.op("dve", lambda e: e.tensor_tensor(v4(ArkT[:]), v4(pbv[:, :, 64:128]), mk(MTi), ALU.mult), reads=[PB, MTi], writes=[ArkT])
                kb.op("dve", lambda e: e.tensor_tensor(v4(p0[:]), v4(PC[:, 0:256].rearrange("p (dp x) -> p dp x", dp=4)), mk(Ms), ALU.mult),
                      reads=[PC, Ms], writes=[p0])
                yield
                def idb(r):
                    return ident_f[r, r.start:r.start + 64]
                mm_all(PT1, lambda dp: (dp * 128, dp * 128 + 64), lambda dp, r: Ab[r, dp, :], lambda dp, r: idb(r), [Ab, ident_f])
                mm_all(PT1, lambda dp: (dp * 128 + 64, dp * 128 + 128), lambda dp, r: Bt[r, dp, :], lambda dp, r: idb(r), [Bt, ident_f])
                mm_all(PC, lambda dp: (256 + dp * 64, 256 + dp * 64 + 64), lambda dp, r: Kt[r, dp, :], lambda dp, r: idb(r), [Kt, ident_f])
                x0 = X[0]
                pt1v = PT1[:, :].rearrange("p (dp x) -> p dp x", dp=4)
                kb.op("act", lambda e: e.copy(x0[:, :, 0:64], pt1v[:, :, 0:64]), reads=[PT1], writes=[x0])
                kb.op("act", lambda e: e.copy(Btok[:], pt1v[:, :, 64:128]), reads=[PT1], writes=[Btok])
                kb.op("act", lambda e: e.copy(Ktok[:], PC[:, 256:512].rearrange("p (dp x) -> p dp x", dp=4)), reads=[PC], writes=[Ktok])
                yield
                mm_all(PD, lambda dp: (dp * 64, dp * 64 + 64), lambda dp, r: AkvT[r, dp, :], lambda dp, r: vt[r, dp, :], [AkvT, vt])
                kb.op("act", lambda e: e.copy(x0[:, :, 64:128], PD[:, 0:256].rearrange("p (dp x) -> p dp x", dp=4)), reads=[PD], writes=[x0])
                yield
                qc, pc, xc = Q[0], P[0], X[0]
                for j in range(6):
                    qn, pn, xn = Q[(j + 1) % 2], P[(j + 1) % 2], X[(j + 1) % 2]
                    mm_all(PX, lambda dp: (dp * 128, dp * 128 + 128), lambda dp, r: qc[r, dp, :], lambda dp, r: xc[r, dp, :], [qc, xc])
                    kb.op("dve", lambda e, xn=xn, xc=xc: e.tensor_tensor(xn[:], xc[:], PX[:, :].rearrange("p (dp x) -> p dp x", dp=4), ALU.add),
                          reads=[xc, PX], writes=[xn])
                    if j < 5:
                        mm_all(PPQ, lambda dp: (dp * 64, dp * 64 + 64), lambda dp, r: qc[r, dp, :], lambda dp, r: pc[r, dp, :], [qc, pc])
                        mm_all(PPQ, lambda dp: (256 + dp * 64, 256 + dp * 64 + 64), lambda dp, r: pc[r, dp, :], lambda dp, r: qc[r, dp, :], [qc, pc])
                        kb.op("act", lambda e, pn=pn: e.copy(pn[:], PPQ[:, 0:256].rearrange("p (dp x) -> p dp x", dp=4)), reads=[PPQ], writes=[pn])
                        kb.op("act", lambda e, qn=qn: e.copy(qn[:], PPQ[:, 256:512].rearrange("p (dp x) -> p dp x", dp=4)), reads=[PPQ], writes=[qn])
                    qc, pc, xc = qn, pn, xn
                    yield
                yield
                mm_all(PD, lambda dp: (256 + dp * 64, 256 + dp * 64 + 64), lambda dp, r: xc[r, dp, 0:64], lambda dp, r: ArbT[r, dp, :], [xc, ArbT])
                kb.op("dve", lambda e: e.tensor_tensor(RAT[:], Rb[:], PD[:, 256:512].rearrange("p (dp x) -> p dp x", dp=4), ALU.add),
                      reads=[Rb, PD], writes=[RAT])
                mm_all(PE_, lambda dp: (dp * 64, dp * 64 + 64), lambda dp, r: xc[r, dp, 0:64], lambda dp, r: Btok[r, dp, :], [xc, Btok])
                kb.op("pool", lambda e: e.tensor_tensor(DG[:], id2[:, :].unsqueeze(1).broadcast_to([128, 4, 64]),
                                                        Wtot[:, :].unsqueeze(2).broadcast_to([128, 4, 64]), ALU.mult), reads=[id2, Wtot], writes=[DG])
                kb.op("dve", lambda e: e.tensor_tensor(McT[:], DG[:], PE_[:, 0:256].rearrange("p (dp x) -> p dp x", dp=4), ALU.add),
                      reads=[DG, PE_], writes=[McT])
                for dp in range(4):
                    for hp in range(2):
                        r = HP[hp]
                        c0 = 256 + dp * 64
                        kb.op("pe", lambda e, dp=dp, r=r, c0=c0: e.matmul(PE_[r, c0:c0 + 64], Btok[r, dp, :], xc[r, dp, 64:128], start=True, stop=False),
                              reads=[Btok, xc], writes=[PE_])
                        kb.op("pe", lambda e, dp=dp, r=r, c0=c0: e.matmul(PE_[r, c0:c0 + 64], Ktok[r, dp, :], vt[r, dp, :], start=False, stop=True),
                              reads=[Ktok, vt], writes=[PE_])
                kb.op("act", lambda e: e.copy(NcS[:], PE_[:, 256:512].rearrange("p (dp x) -> p dp x", dp=4)), reads=[PE_], writes=[NcS])
                yield
                PYs = [PA, PB]
                for dp in range(4):
                    for hp in range(2):
                        r = HP[hp]
                        PY = PYs[hp]
                        c0 = dp * 64
                        kb.op("pe", lambda e, dp=dp, r=r, c0=c0, PY=PY: e.matmul(PY[0:64, c0:c0 + 64], ST[r, dp, :], RAT[r, dp, :], start=True, stop=False),
                              reads=[ST, RAT], writes=[PY])
                        kb.op("pe", lambda e, dp=dp, r=r, c0=c0, PY=PY: e.matmul(PY[0:64, c0:c0 + 64], xc[r, dp, 64:128], ArbT[r, dp, :], start=False, stop=False),
                              reads=[xc, ArbT], writes=[PY])
                        kb.op("pe", lambda e, dp=dp, r=r, c0=c0, PY=PY: e.matmul(PY[0:64, c0:c0 + 64], vt[r, dp, :], ArkT[r, dp, :], start=False, stop=True),
                              reads=[vt, ArkT], writes=[PY])
                for d in range(2):
                    c0 = cidx[d] * CH
                    yv = ysb[d][:, :].rearrange("v (pr hp t) -> v pr hp t", pr=2, hp=2)
                    for hp in range(2):
                        kb.op("act", lambda e, d=d, hp=hp, yv=yv: e.copy(
                            yv[:, :, hp, :], PYs[hp][0:64, d * 128:(d + 1) * 128].rearrange("v (pr t) -> v pr t", pr=2)), reads=[PYs[hp]], writes=[ysb[d]])
                    dst = RS["YF" if d == 0 else "YB"]
                    kb.dma("sp", dst.t.rearrange("(h v) t -> v h t", v=64)[:, :, c0:c0 + CH],
                           ysb[d][:, :].rearrange("v (h t) -> v h t", h=4), reads=[ysb[d]], writes=[Buf()])
                PSS = PT1
                mm_all(PSS, lambda dp: (dp * 64, dp * 64 + 64), lambda dp, r: McT[r, dp, :], lambda dp, r: ST[r, dp, :], [McT, ST])
                kb.op("dve", lambda e: e.tensor_tensor(ST[:], NcS[:], PSS[:, 0:256].rearrange("p (dp x) -> p dp x", dp=4), ALU.add),
                      reads=[NcS, PSS], writes=[ST])


            def lockstep(gens):
                gens = list(gens)
                while gens:
                    nxt = []
                    for g_ in gens:
                        try:
                            next(g_)
                            nxt.append(g_)
                        except StopIteration:
                            pass
                    gens = nxt
            for ci in range(0, NCH, 2):
                lockstep([chunk_gen(ci, SETS[0]), chunk_gen(ci + 1, SETS[1])])

    def phase_rwkv_chunked2(l):
        CH = 64
        NCH = T // CH
        with kb.scope():
            def ldc(nm, shape):
                t = kb.sb("k" + nm, shape)
                kb.dma("sp", t[:], CT[nm].t, reads=[CT[nm]], writes=[t])
                return t
            MsB = ldc("rw_msb", [128, 2, 128]); MTsB = ldc("rw_mtsb", [128, 2, 128]); MTi = ldc("rw_mti", [128, 2, 64])
            identr = kb.sb("cidr", [128, 128], F32R)
            kb.op("dve", lambda e: e.tensor_copy(identr[:], ident_f[:]), reads=[ident_f], writes=[identr])
            ones = kb.sb("rones", [128, 64])
            kb.op("pool", lambda e: e.memset(ones[:], 1.0), writes=[ones])
            def bd(nm, n=1, dt=F32R):
                ts = [kb.sb(f"{nm}{i}", [128, 4, 128], dt) for i in range(n)]
                for t in ts:
                    kb.op("pool", lambda e, t=t: e.memset(t[:].bitcast(F32) if dt == F32R else t[:], 0.0), writes=[t])
                return ts
            def t4(nm, n=1, w=64, dt=F32):
                return [kb.sb(f"{nm}{i}", [128, 4, w], dt) for i in range(n)]
            f32 = lambda ap: ap.bitcast(F32)
            names = ("AL", "W", "B", "KD", "RT")
            IN = {n: t4("di" + n, 2) for n in names}
            VT = bd("dVT", 2, F32)
            VTr = bd("dVTr")[0]
            ST = bd("dST")[0]
            CS = t4("dCS")[0]; TOT = kb.sb("dTOT", [128, 4]); TMP = t4("dTMP")[0]
            Epos = t4("dEp")[0]; Eneg = t4("dEn")[0]; Eprev = t4("dEv")[0]; Etot = t4("dEt")[0]; Wtot = kb.sb("dWt", [128, 4])
            Ab = bd("dAb")[0]; Bb = bd("dBb")[0]; Kb = bd("dKb")[0]; Bt = bd("dBt")[0]; Kt = bd("dKt")[0]
            Rb = t4("dRb", 1, 64, F32R)[0]
            Q = bd("dQ", 2); P = bd("dP", 2); AkvT = bd("dAkvT")[0]
            ArbT = t4("dArbT", 1, 64, F32R)[0]; ArkT = t4("dArkT", 1, 64, F32R)[0]; RAT = t4("dRAT", 1, 64, F32R)[0]
            X = [kb.sb(f"dX{i}", [128, 4, 256], F32R) for i in range(2)]
            Btok = bd("dBtok")[0]; Ktok = bd("dKtok")[0]; McT = bd("dMcT")[0]
            NcS = bd("dNcS", 1, F32)[0]; DG = bd("dDG", 1, F32)[0]
            ysb = [kb.sb(f"dysb{d}", [128, 2, 64]) for d in range(2)]
            border = [3, 2, 1, 0] + list(range(NCH - 1, 3, -1))
            H0, H1 = slice(0, 64), slice(64, 128)
            B0, B1, B2, B3, B4, B5, B6, B7 = PS

            def mm4(ps, c0, w, lhs, rhs, reads, start=True, stop=True):
                for dp in range(4):
                    kb.op("pe", lambda e, dp=dp: e.matmul(ps[:, c0 + dp * w:c0 + (dp + 1) * w], lhs(dp), rhs(dp), start=start, stop=stop),
                          reads=reads, writes=[ps])

            def v4(ap):
                return ap.rearrange("p (d pr) x -> p d pr x", d=2)

            def mk(m, w):
                return m[:, :, :].unsqueeze(2).broadcast_to([128, 2, 2, w])

            def pv(ps, c0, w):
                return ps[:, c0:c0 + 4 * w].rearrange("p (dp x) -> p dp x", dp=4)

            for ci in range(NCH):
                cidx = [ci, border[ci]]
                i2 = ci % 2
                vt = VT[i2]
                for d in range(2):
                    c0 = cidx[d] * CH
                    q_ = "sp" if d == 0 else "pool"
                    for n in names:
                        src = RS[n if n in ("AL", "RT") else f"{n}{d}"]
                        kb.dma(q_, IN[n][i2][:, 2 * d:2 * d + 2, :],
                               src.t.rearrange("(pr q) t -> q pr t", q=128)[:, :, c0:c0 + CH], reads=[src], writes=[IN[n][i2]])
                    for hp in range(2):
                        kb.dma(q_, vt[hp * 64:(hp + 1) * 64, 2 * d:2 * d + 2, hp * 64:(hp + 1) * 64],
                               RS["VTOK"][c0:c0 + CH, :].rearrange("t (pr hp v) -> t pr hp v", pr=2, hp=2)[:, :, hp, :],
                               reads=[RS["VTOK"]], writes=[vt])
                al, lw, be, kd, rt = IN["AL"][i2], IN["W"][i2], IN["B"][i2], IN["KD"][i2], IN["RT"][i2]
                kb.op("act", lambda e: e.copy(VTr[:], vt[:]), reads=[vt], writes=[VTr])
                for dp in range(4):
                    kb.op("dve", lambda e, dp=dp: e.tensor_tensor_scan(CS[:, dp, :], ones[:, :], lw[:, dp, :], 0.0, ALU.mult, ALU.add),
                          reads=[ones, lw], writes=[CS])
                kb.op("dve", lambda e: e.tensor_copy(TOT[:, :], CS[:, :, CH - 1]), reads=[CS], writes=[TOT])
                kb.op("dve", lambda e: e.tensor_tensor(CS[:, 2:4, :], lw[:, 2:4, :], CS[:, 2:4, :], ALU.subtract), reads=[lw, CS], writes=[CS])
                kb.op("dve", lambda e: e.tensor_tensor(CS[:, 2:4, :], CS[:, 2:4, :], TOT[:, 2:4].unsqueeze(2).broadcast_to([128, 2, CH]), ALU.add),
                      reads=[CS, TOT], writes=[CS])
                kb.op("act", lambda e: e.activation(Epos[:], CS[:], AF.Exp), reads=[CS], writes=[Epos])
                kb.op("act", lambda e: e.activation(Eneg[:], CS[:], AF.Exp, scale=-1.0), reads=[CS], writes=[Eneg])
                kb.op("pool", lambda e: e.tensor_tensor(TMP[:], CS[:], lw[:], ALU.subtract), reads=[CS, lw], writes=[TMP])
                kb.op("act", lambda e: e.activation(Eprev[:], TMP[:], AF.Exp), reads=[TMP], writes=[Eprev])
                kb.op("pool", lambda e: e.tensor_tensor(Etot[:], TOT[:, :].unsqueeze(2).broadcast_to([128, 4, CH]), CS[:], ALU.subtract),
                      reads=[TOT, CS], writes=[Etot])
                kb.op("act", lambda e: e.activation(Etot[:], Etot[:], AF.Exp), reads=[Etot], writes=[Etot])
                kb.op("act", lambda e: e.activation(Wtot[:], TOT[:], AF.Exp), reads=[TOT], writes=[Wtot])
                for k_, (dst, a_, b_) in enumerate(((Ab, al, Eprev), (Bb, be, Eneg), (Kb, kd, Eneg), (Bt, be, Etot), (Kt, kd, Etot))):
                    for hi, r in enumerate((H0, H1)):
                        eng = "dve" if (k_ + hi) % 2 == 0 else "pool"
                        kb.op(eng, lambda e, dst=dst, a_=a_, b_=b_, r=r: e.tensor_tensor(dst[r, :, r.start:r.start + 64], a_[r, :, :], b_[r, :, :], ALU.mult),
                              reads=[a_, b_], writes=[dst])
                kb.op("pool", lambda e: e.tensor_tensor(Rb[:], rt[:], Epos[:], ALU.mult), reads=[rt, Epos], writes=[Rb])
                mm4(B0, 0, 128, lambda dp: Bb[:, dp, :], lambda dp: Ab[:, dp, :], [Bb, Ab])
                mm4(B1, 0, 128, lambda dp: Kb[:, dp, :], lambda dp: Ab[:, dp, :], [Kb, Ab])
                mm4(B2, 0, 128, lambda dp: Ab[:, dp, :], lambda dp: Bb[:, dp, :], [Ab, Bb])
                mm4(B3, 0, 64, lambda dp: Bb[:, dp, :], lambda dp: Rb[:, dp, :], [Bb, Rb])
                mm4(B3, 256, 64, lambda dp: Kb[:, dp, :], lambda dp: Rb[:, dp, :], [Kb, Rb])
                q0, p0, x0 = Q[0], P[0], X[0]
                kb.op("dve", lambda e: e.tensor_tensor(v4(q0[:]), v4(pv(B0, 0, 128)), mk(MTsB, 128), ALU.mult), reads=[B0, MTsB], writes=[q0])
                kb.op("dve", lambda e: e.tensor_tensor(v4(AkvT[:]), v4(pv(B1, 0, 128)), mk(MTsB, 128), ALU.mult), reads=[B1, MTsB], writes=[AkvT])
                kb.op("dve", lambda e: e.tensor_tensor(v4(p0[:]), v4(pv(B2, 0, 128)), mk(MsB, 128), ALU.mult), reads=[B2, MsB], writes=[p0])
                kb.op("dve", lambda e: e.tensor_tensor(v4(ArbT[:]), v4(pv(B3, 0, 64)), mk(MTi, 64), ALU.mult), reads=[B3, MTi], writes=[ArbT])
                kb.op("dve", lambda e: e.tensor_tensor(v4(ArkT[:]), v4(pv(B3, 256, 64)), mk(MTi, 64), ALU.mult), reads=[B3, MTi], writes=[ArkT])
                mm4(B4, 0, 128, lambda dp: Ab[:, dp, :], lambda dp: identr[:, :], [Ab, identr])
                mm4(B6, 0, 128, lambda dp: Bt[:, dp, :], lambda dp: identr[:, :], [Bt, identr])
                mm4(B7, 0, 128, lambda dp: Kt[:, dp, :], lambda dp: identr[:, :], [Kt, identr])
                mm4(B5, 0, 128, lambda dp: AkvT[:, dp, :], lambda dp: VTr[:, dp, :], [AkvT, VTr])
                kb.op("act", lambda e: e.copy(x0[:, :, 0:128], pv(B4, 0, 128)), reads=[B4], writes=[x0])
                kb.op("act", lambda e: e.copy(Btok[:], pv(B6, 0, 128)), reads=[B6], writes=[Btok])
                kb.op("act", lambda e: e.copy(Ktok[:], pv(B7, 0, 128)), reads=[B7], writes=[Ktok])
                kb.op("act", lambda e: e.copy(x0[:, :, 128:256], pv(B5, 0, 128)), reads=[B5], writes=[x0])
                qc, pc, xc = Q[0], P[0], X[0]
                for j in range(6):
                    qn, pn, xn = Q[(j + 1) % 2], P[(j + 1) % 2], X[(j + 1) % 2]
                    for hf, bank in ((0, B4), (1, B5)):
                        for dq in range(2):
                            dp = hf * 2 + dq
                            kb.op("pe", lambda e, dp=dp, dq=dq, bank=bank: e.matmul(bank[:, dq * 256:(dq + 1) * 256], qc[:, dp, :], xc[:, dp, :],
                                                                                    start=True, stop=True), reads=[qc, xc], writes=[bank])
                        kb.op("dve", lambda e, hf=hf, bank=bank, xn=xn, xc=xc: e.tensor_tensor(
                            xn[:, 2 * hf:2 * hf + 2, :], f32(xc[:, 2 * hf:2 * hf + 2, :]), bank[:, :].rearrange("p (dq x) -> p dq x", dq=2), ALU.add),
                            reads=[xc, bank], writes=[xn])
                    if j < 5:
                        mm4(B6, 0, 128, lambda dp: qc[:, dp, :], lambda dp: pc[:, dp, :], [qc, pc])
                        mm4(B7, 0, 128, lambda dp: pc[:, dp, :], lambda dp: qc[:, dp, :], [qc, pc])
                        kb.op("act", lambda e, pn=pn: e.copy(pn[:], pv(B6, 0, 128)), reads=[B6], writes=[pn])
                        kb.op("act", lambda e, qn=qn: e.copy(qn[:], pv(B7, 0, 128)), reads=[B7], writes=[qn])
                    qc, pc, xc = qn, pn, xn
                mm4(B3, 0, 64, lambda dp: xc[:, dp, 0:128], lambda dp: ArbT[:, dp, :], [xc, ArbT])
                kb.op("dve", lambda e: e.tensor_tensor(RAT[:], f32(Rb[:]), pv(B3, 0, 64), ALU.add), reads=[Rb, B3], writes=[RAT])
                mm4(B2, 0, 128, lambda dp: xc[:, dp, 0:128], lambda dp: Btok[:, dp, :], [xc, Btok])
                kb.op("pool", lambda e: e.tensor_tensor(DG[:], ident_f[:, :].unsqueeze(1).broadcast_to([128, 4, 128]),
                                                        Wtot[:, :].unsqueeze(2).broadcast_to([128, 4, 128]), ALU.mult), reads=[ident_f, Wtot], writes=[DG])
                kb.op("dve", lambda e: e.tensor_tensor(McT[:], DG[:], pv(B2, 0, 128), ALU.add), reads=[DG, B2], writes=[McT])
                for dp in range(4):
                    kb.op("pe", lambda e, dp=dp: e.matmul(B0[:, dp * 128:(dp + 1) * 128], Btok[:, dp, :], xc[:, dp, 128:256], start=True, stop=False),
                          reads=[Btok, xc], writes=[B0])
                    kb.op("pe", lambda e, dp=dp: e.matmul(B0[:, dp * 128:(dp + 1) * 128], Ktok[:, dp, :], VTr[:, dp, :], start=False, stop=True),
                          reads=[Ktok, VTr], writes=[B0])
                kb.op("act", lambda e: e.copy(NcS[:], pv(B0, 0, 128)), reads=[B0], writes=[NcS])
                for dp in range(4):
                    c0 = dp * 64
                    kb.op("pe", lambda e, dp=dp, c0=c0: e.matmul(B1[:, c0:c0 + 64], ST[:, dp, :], RAT[:, dp, :], start=True, stop=False),
                          reads=[ST, RAT], writes=[B1])
                    kb.op("pe", lambda e, dp=dp, c0=c0: e.matmul(B1[:, c0:c0 + 64], xc[:, dp, 128:256], ArbT[:, dp, :], start=False, stop=False),
                          reads=[xc, ArbT], writes=[B1])
                    kb.op("pe", lambda e, dp=dp, c0=c0: e.matmul(B1[:, c0:c0 + 64], VTr[:, dp, :], ArkT[:, dp, :], start=False, stop=True),
                          reads=[VTr, ArkT], writes=[B1])
                for d in range(2):
                    c0 = cidx[d] * CH
                    kb.op("act", lambda e, d=d: e.copy(ysb[d][:, :, :], B1[:, d * 128:(d + 1) * 128].rearrange("p (pr t) -> p pr t", pr=2)),
                          reads=[B1], writes=[ysb[d]])
                    dst = RS["YF" if d == 0 else "YB"]
                    kb.dma("sp", dst.t.rearrange("(pr q) t -> q pr t", q=128)[:, :, c0:c0 + CH], ysb[d][:, :, :], reads=[ysb[d]], writes=[Buf()])
                mm4(B6, 0, 128, lambda dp: McT[:, dp, :], lambda dp: ST[:, dp, :], [McT, ST])
                kb.op("dve", lambda e: e.tensor_tensor(ST[:], NcS[:], pv(B6, 0, 128), ALU.add), reads=[NcS, B6], writes=[ST])

    def phase_rwkv_out(l, with_ctx):
        with kb.scope():
            rk_ = colvec("rrk", W["rw_r_k"][l, :], W["rw_r_k"], [128, 2], "(j p) -> p j", p=128)
            lg_ = colvec("rlg", W["rw_ln_g"][l, :], W["rw_ln_g"], [128, 2], "(j p) -> p j", p=128)
            lb_ = colvec("rlb", W["rw_ln_b"][l, :], W["rw_ln_b"], [128, 2], "(j p) -> p j", p=128)
            nm = ("YF", "YB", "RT", "KD0", "KD1", "VT")
            tl = [{n: kb.sb(f"o{n}{i}", [128, 512]) for n in nm} for i in range(2)]
            sg = [kb.sb(f"osg{i}", [128, 512], BF16) for i in range(2)]
            ob = [kb.sb(f"oob{i}", [128, 512], BF16) for i in range(2)]
            wk = [[kb.sb(f"owk{k}{i}", [128, 512]) for k in range(3)] for i in range(2)]
            it = 0
            for pr in range(2):
                rows = slice(pr * 128, (pr + 1) * 128)
                for (t0, nt) in TCH:
                    if not with_ctx and t0 + nt <= C:
                        continue
                    t_, s_, o_, (a_, b_, c_) = tl[it % 2], sg[it % 2], ob[it % 2], wk[it % 2]
                    for k, n in enumerate(nm):
                        kb.dma("sp" if k % 2 == 0 else "pool", t_[n][:, 0:nt], RS[n][rows, t0:t0 + nt], reads=[RS[n]], writes=[t_[n]])
                    kb.dma("sp", s_[:, 0:nt], RS["SGT"][rows, t0:t0 + nt], reads=[RS["SGT"]], writes=[s_])
                    y = t_["YF"]
                    kb.op("dve", lambda e: e.tensor_tensor(y[:, 0:nt], y[:, 0:nt], t_["YB"][:, 0:nt], ALU.add), reads=[y, t_["YB"]], writes=[y])
                    p1, p2, p3 = PS[(3 * it) % 8], PS[(3 * it + 1) % 8], PS[(3 * it + 2) % 8]
                    kb.op("pe", lambda e: e.matmul(p1[:, 0:nt], blk64[:], y[:, 0:nt], start=True, stop=True), reads=[blk64, y], writes=[p1])
                    kb.op("dve", lambda e: e.scalar_tensor_tensor(a_[:, 0:nt], p1[:, 0:nt], -1.0 / 64, y[:, 0:nt], ALU.mult, ALU.add),
                          reads=[p1, y], writes=[a_])
                    kb.op("act", lambda e: e.activation(b_[:, 0:nt], a_[:, 0:nt], AF.Square), reads=[a_], writes=[b_])
                    kb.op("pe", lambda e: e.matmul(p2[:, 0:nt], blk64[:], b_[:, 0:nt], start=True, stop=True), reads=[blk64, b_], writes=[p2])
                    kb.op("dve", lambda e: e.tensor_scalar(b_[:, 0:nt], p2[:, 0:nt], 1.0 / 64, 64e-5, ALU.mult, ALU.add), reads=[p2], writes=[b_])
                    kb.op("act", lambda e: e.sqrt(b_[:, 0:nt], b_[:, 0:nt]), reads=[b_], writes=[b_])
                    kb.op("dve", lambda e: e.reciprocal(b_[:, 0:nt], b_[:, 0:nt]), reads=[b_], writes=[b_])
                    kb.op("dve", lambda e: e.tensor_tensor(a_[:, 0:nt], a_[:, 0:nt], b_[:, 0:nt], ALU.mult), reads=[a_, b_], writes=[a_])
                    kb.op("dve", lambda e: e.tensor_scalar(a_[:, 0:nt], a_[:, 0:nt], lg_[:, pr:pr + 1], lb_[:, pr:pr + 1], ALU.mult, ALU.add),
                          reads=[a_, lg_, lb_], writes=[a_])
                    kb.op("pool", lambda e: e.tensor_tensor(c_[:, 0:nt], t_["KD0"][:, 0:nt], t_["KD1"][:, 0:nt], ALU.add),
                          reads=[t_["KD0"], t_["KD1"]], writes=[c_])
                    kb.op("dve", lambda e: e.scalar_tensor_tensor(c_[:, 0:nt], t_["RT"][:, 0:nt], rk_[:, pr:pr + 1], c_[:, 0:nt], ALU.mult, ALU.mult),
                          reads=[t_["RT"], rk_, c_], writes=[c_])
                    kb.op("pe", lambda e: e.matmul(p3[:, 0:nt], blk64[:], c_[:, 0:nt], start=True, stop=True), reads=[blk64, c_], writes=[p3])
                    kb.op("dve", lambda e: e.tensor_tensor(c_[:, 0:nt], p3[:, 0:nt], t_["VT"][:, 0:nt], ALU.mult), reads=[p3, t_["VT"]], writes=[c_])
                    kb.op("dve", lambda e: e.tensor_tensor(a_[:, 0:nt], a_[:, 0:nt], c_[:, 0:nt], ALU.add), reads=[a_, c_], writes=[a_])
                    kb.op("pool", lambda e: e.tensor_tensor(o_[:, 0:nt], a_[:, 0:nt], s_[:, 0:nt], ALU.mult), reads=[a_, s_], writes=[o_])
                    kb.dma("sp", mixT[256 + pr * 128:256 + (pr + 1) * 128, t0:t0 + nt], o_[:, 0:nt], reads=[o_], writes=[Buf()])
                    it += 1


    SEGS = {"L": dict(Ls=L, A=32, cbw=32, off=C, ut="UTL"), "C": dict(Ls=C, A=2, cbw=64, off=0, ut="UTC")}

    def phase_hyena_prep(l, with_ctx):
        with kb.scope():
            stage = kb.sb("hstage", [128, 8, 128])
            wts = [kb.sb(f"hwt{i}", [128, 8, 128], BF16) for i in range(2)]
            cw = kb.sb("hcw", [128, 6, 3])
            for k in range(3):
                kb.dma("sp", cw[:, :, k], W["hy_conv"][l, k, :].rearrange("(j p) -> p j", p=128), reads=[W["hy_conv"]], writes=[cw], slow=True)
            ncw = kb.sb("hncw", [128, 6, 3])
            kb.op("dve", lambda e: e.tensor_scalar(ncw[:], cw[:], -1.0, None, ALU.mult), reads=[cw], writes=[ncw])
            Zraw = kb.sb("hZraw", [128, T + 2])
            Zout = kb.sb("hZout", [128, T])
            kb.op("pool", lambda e: e.memset(Zraw[:, 0:1], 0.0), writes=[Zraw])
            kb.op("pool", lambda e: e.memset(Zraw[:, T + 1:T + 2], 0.0), writes=[Zraw])
            ub = kb.sb("hub", [128, 32 * 128])
            tG = [kb.sb(f"htG{i}", [128, 512], BF16) for i in range(2)]
            for oi, jt in enumerate(range(8)):
                wt = wts[oi % 2]
                c0 = HY0 + jt * 128 if jt < 6 else HYG0 + (jt - 6) * 128
                load_w(l, wt, c0, 128, stage)
                if jt >= 6:
                    for ci, (t0, nt) in enumerate(TCH):
                        p = PS[ci % 4]
                        proj_fm(p, wt, 0, 128, t0, nt)
                        g = tG[ci % 2]
                        kb.op("act", lambda e, p=p, g=g, nt=nt: e.activation(g[:, 0:nt], p[:, 0:nt], AF.Silu), reads=[p], writes=[g])
                        kb.dma("sp", HS["SG"][(jt - 6) * 128:(jt - 5) * 128, t0:t0 + nt], g[:, 0:nt], reads=[g], writes=[Buf()])
                    continue
                conv_tile(l, wt, cw, ncw, jt, Zraw, Zout)
                arr, half = jt // 2, jt % 2
                for sn in (("L", "C") if with_ctx else ("L",)):
                    sg = SEGS[sn]
                    A, cbw, off = sg["A"], sg["cbw"], sg["off"]
                    G = 128 // A
                    ncg = 128 // G
                    ubv = ub[:, 0:A * 128].rearrange("p (g a c) -> p g a c", g=ncg, a=A)
                    for a in range(A):
                        p = PS[4 + (a // 4) % 4]
                        kb.op("pe", lambda e, p=p, a=a, A=A, off=off: e.transpose(
                            p[:, (a % 4) * 128:(a % 4 + 1) * 128], Zout[:, off + a:off + a + 127 * A + 1:A], ident_f[:]),
                            reads=[Zout, ident_f], writes=[p])
                        if a % 4 == 3 or a == A - 1:
                            a0 = (a // 4) * 4
                            na = a - a0 + 1
                            kb.op("act", lambda e, p=p, a0=a0, na=na, G=G: e.copy(
                                ubv[:, :, a0:a0 + na, :], p[:, 0:na * 128].rearrange("p (a g c) -> p g a c", a=na, c=G)), reads=[p], writes=[ub])
                    nb = 128 // cbw
                    bsz = A * cbw
                    for b in range(nb):
                        dst = HS[sg["ut"]][arr, half * nb + b, :, :]
                        kb.dma("sp" if b % 2 == 0 else "pool", dst, ub[:, b * bsz:(b + 1) * bsz], reads=[ub], writes=[Buf()])

    def cmul(dre, dim_, sre, sim, tre, tim, conj, srcb, tabb, dstb, tmp):
        t1, t2 = tmp
        sh = tuple(slice(None) for _ in range(1))
        kb.op("dve", lambda e: e.tensor_tensor(t1, sre, tre, ALU.mult), reads=srcb + tabb, writes=[dstb[2]])
        kb.op("dve", lambda e: e.tensor_tensor(t2, sim, tim, ALU.mult), reads=srcb + tabb, writes=[dstb[3]])
        kb.op("pool", lambda e: e.tensor_tensor(dre, t1, t2, ALU.add if conj else ALU.subtract), reads=[dstb[2], dstb[3]], writes=[dstb[0]])
        kb.op("dve", lambda e: e.tensor_tensor(t1, sim, tre, ALU.mult), reads=srcb + tabb + [dstb[0]], writes=[dstb[2]])
        kb.op("dve", lambda e: e.tensor_tensor(t2, sre, tim, ALU.mult), reads=srcb + tabb + [dstb[0]], writes=[dstb[3]])
        kb.op("pool", lambda e: e.tensor_tensor(dim_, t1, t2, ALU.subtract if conj else ALU.add), reads=[dstb[2], dstb[3]], writes=[dstb[1]])

    def phase_hyena_main(l, with_ctx):
        PI = math.pi
        with kb.scope():
            fw1 = kb.sb("hfw1", [33, 64])
            fw2 = kb.sb("hfw2", [64, 64])
            fw3 = kb.sb("hfw3", [64, 1024])
            kb.dma("sp", fw1[:], W["hy_fw1"][l, :, :], reads=[W["hy_fw1"]], writes=[fw1])
            kb.dma("sp", fw2[:], W["hy_fw2"][l, :, :], reads=[W["hy_fw2"]], writes=[fw2])
            kb.dma("sp", fw3[:], W["hy_fw3"][l, :, :], reads=[W["hy_fw3"]], writes=[fw3])
            fb1 = colvec("hfb1", W["hy_fb1"][l, :], W["hy_fb1"], [64, 1], "(d o) -> d o", o=1)
            fb2 = colvec("hfb2", W["hy_fb2"][l, :], W["hy_fb2"], [64, 1], "(d o) -> d o", o=1)
            frq = colvec("hfrq", W["hy_freq"][l, :], W["hy_freq"], [64, 1], "(d o) -> d o", o=1)
            brow = kb.sb("hbrow", [1, 512])
            kb.dma("sp", brow[:], W["hy_bias"][l, :, :].rearrange("o c -> (o c)").rearrange("(x n) -> x n", x=1), reads=[W["hy_bias"]], writes=[brow])
            for sn in (("L", "C") if with_ctx else ("L",)):
                sg = SEGS[sn]
                Ls, A, cbw, off = sg["Ls"], sg["A"], sg["cbw"], sg["off"]
                G = 128 // A
                N = 2 * Ls
                ngr = cbw // G
                nblk = 256 // cbw
                pre = f"hy{sn}_"
                with kb.scope():
                    def ld(nm, shape):
                        t = kb.sb("k" + nm, shape)
                        src = CT[pre + nm]
                        kb.dma("sp", t[:], src.t, reads=[src], writes=[t])
                        return t
                    def ldr(nm, shape):
                        tr = kb.sb("r" + nm, shape, F32R)
                        with kb.scope():
                            t32 = ld(nm, shape)
                            kb.op("dve", lambda e: e.tensor_copy(tr[:], t32[:]), reads=[t32], writes=[tr])
                        return tr
                    F256 = ldr("F256", [128, 2, 512]); TWC = ld("TWC", [128, 256]); TWS = ld("TWS", [128, 256])
                    Dre = ldr("Dre", [128, 128]); Dim = ldr("Dim", [128, 128]); nDim = ldr("nDim", [128, 128])
                    E1 = ldr("E1", [128, 256]); E2 = ldr("E2", [128, 256])
                    TW2C = ld("TW2C", [128, 2, 128]); TW2S = ld("TW2S", [128, 2, 128])
                    IC = ldr("IC", [128, 2, 128]); IS = ldr("IS", [128, 2, 128])
                    h2T = kb.sb("h2T", [64, N])
                    with kb.scope():
                        zT = kb.sb("zT", [33, N])
                        kb.dma("sp", zT[:], CT[pre + "zT"].t, reads=[CT[pre + "zT"]], writes=[zT])
                        h1T = kb.sb("h1T", [64, N])
                        arg = [kb.sb(f"harg{i}", [64, 512]) for i in range(2)]
                        wr = [kb.sb(f"hwr{i}", [64, 512]) for i in range(2)]
                        for (src, K_, wgt, bcol, dst) in ((zT, 33, fw1, fb1, h1T), (h1T, 64, fw2, fb2, h2T)):
                            for ci, n0 in enumerate(range(0, N, 512)):
                                p = PS[ci % 4]
                                ag = arg[ci % 2]
                                kb.op("pe", lambda e: e.matmul(p[0:64, :], wgt[0:K_, :], src[0:K_, n0:n0 + 512], start=True, stop=True),
                                      reads=[wgt, src], writes=[p])
                                kb.op("dve", lambda e: e.tensor_scalar(ag[:, :], p[0:64, :], bcol[:, 0:1], frq[:, 0:1], ALU.add, ALU.mult),
                                      reads=[p, bcol, frq], writes=[ag])
                                for _w in range(2):
                                    kb.op("dve", lambda e: e.tensor_scalar(wr[0][:, :], ag[:, :], PI, -2 * PI, ALU.is_gt, ALU.mult), reads=[ag], writes=[wr[0]])
                                    kb.op("dve", lambda e: e.tensor_scalar(wr[1][:, :], ag[:, :], -PI, 2 * PI, ALU.is_lt, ALU.mult), reads=[ag], writes=[wr[1]])
                                    kb.op("dve", lambda e: e.tensor_tensor(ag[:, :], ag[:, :], wr[0][:, :], ALU.add), reads=[ag, wr[0]], writes=[ag])
                                    kb.op("dve", lambda e: e.tensor_tensor(ag[:, :], ag[:, :], wr[1][:, :], ALU.add), reads=[ag, wr[1]], writes=[ag])
                                kb.op("act", lambda e: e.activation(dst[:, n0:n0 + 512], ag[:, :], AF.Sin), reads=[ag], writes=[dst])
                    KT = [kb.sb(f"KT{o}", [128, 2, ngr, A, G]) for o in range(2)]
                    KTr = [kb.sb(f"KTr{o}", [128, 2, ngr, A, G], F32R) for o in range(2)]
                    uvr = kb.sb("huvr", [128, ngr, A * G], F32R)
                    KS = [kb.sb(f"KS{o}", [128, ngr, 512]) for o in range(2)]
                    DECt = kb.sb("DECt", [128, 2, ngr, A, G])
                    part = kb.sb("hpart", [128, cbw])
                    rn = kb.sb("hrn", [128, cbw])
                    ex = kb.sb("hex", [1, cbw])
                    uv = kb.sb("huv", [128, ngr, A * G]); x1 = kb.sb("hx1", [128, ngr, A * G]); x2 = kb.sb("hx2", [128, ngr, A * G])
                    u2 = kb.sb("hu2", [128, ngr, A * G], F32R); res = kb.sb("hres", [128, A, cbw])
                    dts = (F32R, F32R, F32, F32)
                    BpS = [[kb.sb(f"hBp{b}{i}", [128, 256], dts[i]) for i in range(4)] for b in range(2)]
                    BpbS = [[Buf() for _ in range(4)] for b in range(2)]
                    YpS = [[kb.sb(f"hYp{b}{i}", [128, 256], dts[i]) for i in range(4)] for b in range(2)]
                    YpbS = [[Buf() for _ in range(4)] for b in range(2)]
                    GpS = [[kb.sb(f"hGp{b}{i}", [128, 2, 128], dts[i]) for i in range(4)] for b in range(2)]
                    GpbS = [[Buf() for _ in range(4)] for b in range(2)]
                    fctr = [0]
                    Fm = kb.sb("hFm", [cbw, Ls])
                    sgm = kb.sb("hsgm", [cbw, Ls], BF16)

                    def fwd_fft(lhs_chunks, lhs_bufs, psB, psX):
                        n = len(lhs_chunks)
                        fctr[0] += 1
                        Bp, Bpb = BpS[fctr[0] % 2], BpbS[fctr[0] % 2]
                        for i, (ap, hf) in enumerate(lhs_chunks):
                            kb.op("pe", lambda e, ap=ap, hf=hf, i=i: e.matmul(psB[:, :], ap, F256[:, hf, :], start=(i == 0), stop=(i == n - 1)),
                                  reads=lhs_bufs + [F256], writes=[psB])
                        yield
                        cmul(Bp[0][:, :], Bp[1][:, :], psB[:, 0:256], psB[:, 256:512], TWC[:, :], TWS[:, :], True,
                             [psB], [TWC, TWS], Bpb, (Bp[2][:, :], Bp[3][:, :]))
                        yield
                        kb.op("pe", lambda e: e.matmul(psX[:, 0:256], Dre[:, :], Bp[0][:, :], start=True, stop=False), reads=[Dre, Bpb[0]], writes=[psX])
                        kb.op("pe", lambda e: e.matmul(psX[:, 0:256], nDim[:, :], Bp[1][:, :], start=False, stop=True), reads=[nDim, Bpb[1]], writes=[psX])
                        kb.op("pe", lambda e: e.matmul(psX[:, 256:512], Dim[:, :], Bp[0][:, :], start=True, stop=False), reads=[Dim, Bpb[0]], writes=[psX])
                        kb.op("pe", lambda e: e.matmul(psX[:, 256:512], Dre[:, :], Bp[1][:, :], start=False, stop=True), reads=[Dre, Bpb[1]], writes=[psX])

                    def conv_group(src, src_b, g, o, mulv, mul_b, dst_ap, dst_b, it):
                        psB, psX, psG, psy = PS[it % 2], PS[2 + it % 2], PS[4 + it % 2], PS[6 + it % 2]
                        Yp, Ypb, Gp, Gpb = YpS[it % 2], YpbS[it % 2], GpS[it % 2], GpbS[it % 2]
                        yield from fwd_fft([(src[:, g, :], 0)], [src_b], psB, psX)
                        yield
                        cmul(Yp[0][:, :], Yp[1][:, :], psX[:, 0:256], psX[:, 256:512], KS[o][:, g, 0:256], KS[o][:, g, 256:512], False,
                             [psX], [KS[o]], Ypb, (Yp[2][:, :], Yp[3][:, :]))
                        yield
                        for chn in range(2):
                            fs = slice(chn * 128, (chn + 1) * 128)
                            kb.op("pe", lambda e, fs=fs, chn=chn: e.matmul(psG[:, chn * 256:(chn + 1) * 256], Yp[0][:, fs], E1[:, :], start=True, stop=False),
                                  reads=[Ypb[0], E1], writes=[psG])
                            kb.op("pe", lambda e, fs=fs, chn=chn: e.matmul(psG[:, chn * 256:(chn + 1) * 256], Yp[1][:, fs], E2[:, :], start=False, stop=True),
                                  reads=[Ypb[1], E2], writes=[psG])
                        yield
                        pg = psG[:, :].rearrange("p (ch ri c) -> p ch ri c", ch=2, ri=2)
                        cmul(Gp[0][:, :, :], Gp[1][:, :, :], pg[:, :, 0, :], pg[:, :, 1, :], TW2C[:, :, :], TW2S[:, :, :], False,
                             [psG], [TW2C, TW2S], Gpb, (Gp[2][:, :, :], Gp[3][:, :, :]))
                        yield
                        k = 0
                        for chn in range(2):
                            for (tab, gsrc, gb) in ((IC, Gp[0], Gpb[0]), (IS, Gp[1], Gpb[1])):
                                kb.op("pe", lambda e, chn=chn, tab=tab, gsrc=gsrc, k=k: e.matmul(
                                    psy[:, 0:128], tab[:, chn, :], gsrc[:, chn, :], start=(k == 0), stop=(k == 3)), reads=[tab, gb], writes=[psy])
                                k += 1
                        yield
                        kb.op("dve", lambda e: e.tensor_tensor(dst_ap, psy[:, 0:128].rearrange("p (c a) -> p a c", a=A),
                                                               mulv[:, g, :].rearrange("p (a c) -> p a c", c=G), ALU.mult),
                              reads=[psy, mul_b], writes=[dst_b])

                    def lockstep(gens):
                        gens = list(gens)
                        while gens:
                            nxt = []
                            for g_ in gens:
                                try:
                                    next(g_)
                                    nxt.append(g_)
                                except StopIteration:
                                    pass
                            gens = nxt

                    def spec_group(o, g, it):
                        psB, psX = PS[it % 2], PS[2 + it % 2]
                        yield from fwd_fft([(KTr[o][:, 0, g, :, :].rearrange("p a c -> p (a c)"), 0),
                                            (KTr[o][:, 1, g, :, :].rearrange("p a c -> p (a c)"), 1)], [KTr[o]], psB, psX)
                        yield
                        kb.op("act", lambda e: e.copy(KS[o][:, g, :], psX[:, :]), reads=[psX], writes=[KS[o]])

                    git = 0
                    for cb in range(nblk):
                        kb.dma("sp", DECt[:].rearrange("p h g a c -> p (h g a c)"), CT[pre + "DEC"][cb, :, :], reads=[CT[pre + "DEC"]], writes=[DECt])
                        for ai, tile_ in enumerate((uv, x1, x2)):
                            kb.dma("pool", tile_[:].rearrange("p g x -> p (g x)"), HS[sg["ut"]][ai, cb, :, :], reads=[HS[sg["ut"]]], writes=[tile_])
                        for o in range(2):
                            for hf in range(2):
                                col0 = o * 512 + hf * 256 + cb * cbw
                                npb = 512 // cbw
                                for a in range(A):
                                    p = PS[(a // npb) % 4]
                                    kb.op("pe", lambda e, p=p, a=a, hf=hf, col0=col0, npb=npb: e.matmul(
                                        p[:, (a % npb) * cbw:(a % npb + 1) * cbw], h2T[0:64, hf * 128 * A + a:hf * 128 * A + a + 127 * A + 1:A],
                                        fw3[0:64, col0:col0 + cbw], start=True, stop=True), reads=[h2T, fw3], writes=[p])
                                    if a % npb == npb - 1 or a == A - 1:
                                        a0 = (a // npb) * npb
                                        na = a - a0 + 1
                                        kb.op("dve", lambda e, p=p, a0=a0, na=na, hf=hf, o=o: e.tensor_tensor(
                                            KT[o][:, hf, :, a0:a0 + na, :], p[:, 0:na * cbw].rearrange("p (a g c) -> p g a c", a=na, c=G),
                                            DECt[:, hf, :, a0:a0 + na, :], ALU.mult), reads=[p, DECt], writes=[KT[o]])
                            kb.op("dve", lambda e, o=o: e.tensor_reduce(part[:, :].rearrange("p (g c) -> p g c", c=G),
                                                                        KT[o][:, :, :, :, :].rearrange("p h g a c -> p g c h a"), AX.XY, ALU.add,
                                                                        apply_absolute_value=True), reads=[KT[o]], writes=[part])
                            pe_ = PS[4]
                            kb.op("pe", lambda e, o=o: e.matmul(pe_[0:1, 0:cbw], h2T[0:64, 0:1], fw3[0:64, o * 512 + 256 + cb * cbw:o * 512 + 256 + (cb + 1) * cbw],
                                                                start=True, stop=True), reads=[h2T, fw3], writes=[pe_])
                            kb.op("act", lambda e: e.activation(ex[0:1, :], pe_[0:1, 0:cbw], AF.Abs), reads=[pe_], writes=[ex])
                            kb.op("dve", lambda e: e.tensor_tensor(part[0:1, :], part[0:1, :], ex[0:1, :], ALU.add), reads=[part, ex], writes=[part])
                            pt_ = PS[5]
                            kb.op("pe", lambda e: e.matmul(pt_[:, 0:cbw], ones_f[:, :], part[:, :], start=True, stop=True), reads=[ones_f, part], writes=[pt_])
                            kb.op("dve", lambda e: e.reciprocal(rn[:, :], pt_[:, 0:cbw]), reads=[pt_], writes=[rn])
                            for hf in range(2):
                                kb.op("dve", lambda e, o=o, hf=hf: e.tensor_tensor(
                                    KTr[o][:, hf, :, :, :], KT[o][:, hf, :, :, :],
                                    rn[:, :].rearrange("p (g c) -> p g c", c=G).unsqueeze(2).broadcast_to([128, ngr, A, G]), ALU.mult),
                                    reads=[KT[o], rn], writes=[KTr[o]])
                            kb.op("dve", lambda e, o=o: e.tensor_tensor(
                                KTr[o][0:1, 0, :, 0, :], KTr[o][0:1, 0, :, 0, :].bitcast(F32),
                                brow[0:1, o * 256 + cb * cbw:o * 256 + (cb + 1) * cbw].rearrange("p (g c) -> p g c", c=G), ALU.add),
                                reads=[KTr[o], brow], writes=[KTr[o]])
                            for g in range(0, ngr, 2):
                                gg = [g] + ([g + 1] if g + 1 < ngr else [])
                                lockstep([spec_group(o, g_, git + k_) for k_, g_ in enumerate(gg)])
                                git += len(gg)
                        kb.op("act", lambda e: e.copy(uvr[:], uv[:]), reads=[uv], writes=[uvr])
                        for g in range(0, ngr, 2):
                            gg = [g] + ([g + 1] if g + 1 < ngr else [])
                            lockstep([conv_group(uvr, uvr, g_, 0, x1, x1, u2[:, g_, :].rearrange("p (a c) -> p a c", c=G), u2, git + k_)
                                      for k_, g_ in enumerate(gg)])
                            git += len(gg)
                        for g in range(0, ngr, 2):
                            gg = [g] + ([g + 1] if g + 1 < ngr else [])
                            lockstep([conv_group(u2, u2, g_, 1, x2, x2, res[:, :, g_ * G:(g_ + 1) * G], res, git + k_)
                                      for k_, g_ in enumerate(gg)])
                            git += len(gg)
                        kb.dma("sp", sgm[:], HS["SG"][cb * cbw:(cb + 1) * cbw, off:off + Ls], reads=[HS["SG"]], writes=[sgm])
                        Fv = Fm[:, :].rearrange("c (p a) -> c p a", a=A)
                        for a in range(A):
                            p = PS[4 + (a // 4) % 4]
                            kb.op("pe", lambda e, p=p, a=a: e.transpose(p[0:cbw, (a % 4) * 128:(a % 4 + 1) * 128], res[:, a, :], ident_f[:]),
                                  reads=[res, ident_f], writes=[p])
                            if a % 4 == 3 or a == A - 1:
                                a0 = (a // 4) * 4
                                na = a - a0 + 1
                                kb.op("act", lambda e, p=p, a0=a0, na=na: e.copy(
                                    Fv[:, :, a0:a0 + na], p[0:cbw, 0:na * 128].rearrange("c (a p) -> c p a", p=128)), reads=[p], writes=[Fm])
                        kb.op("pool", lambda e: e.tensor_tensor(sgm[:, :], Fm[:, :], sgm[:, :], ALU.mult), reads=[Fm, sgm], writes=[sgm])
                        kb.dma("sp", mixT[cb * cbw:(cb + 1) * cbw, off:off + Ls], sgm[:, :], reads=[sgm], writes=[Buf()])

    dbgn = [n for n, _ in dbg]
    for l in range(depth):
        last = (l == DEPTH - 1)
        with kb.scope():
            hT = kb.sb("hT", [128, 8, T], BF16)
            G1 = kb.sb("G1", [128, 2, D])
            SH = kb.sb("SH", [128, 2, D])
            phase_mod(l)
            phase_norm(l)
            if "noattn" not in dbgn:
                if os.environ.get("ATTN_ONLY", "") != "dense":
                    phase_attn(l, False, not last)
                if os.environ.get("ATTN_ONLY", "") != "window":
                    phase_attn(l, True, not last)
            if "norw" not in dbgn:
                phase_rwkv_prep(l)
            if "nohy" not in dbgn:
                phase_hyena_prep(l, not last)
            if "hT" in dbgn:
                tmp = kb.sb("dbghT", [128, T])
                for j in range(8):
                    kb.op("dve", lambda e, j=j, tmp=tmp: e.tensor_copy(tmp[:], hT[:, j, :]), reads=[hT], writes=[tmp])
                    kb.dma("sp", dbg_t["hT"][:, j, :], tmp[:], reads=[tmp], writes=[dbg_t["hT"]])
        if "norw" not in dbgn:
            {0: phase_rwkv_chunked, 1: phase_rwkv_chunked2, 3: phase_rwkv_chunked3}[RW_V2](l)
            phase_rwkv_out(l, not last)
        if "nohy" not in dbgn:
            phase_hyena_main(l, not last)
        if "noout" not in dbgn:
            phase_out(l, last)
    for n, s_ in dbg:
        if n == "xres":
            with kb.scope():
                tx = kb.sb("dbgx", [128, D])
                for i in range(NT):
                    kb.dma("sp", tx[:], xres[i * 128:(i + 1) * 128, :], reads=[xres_b[i]], writes=[tx])
                    kb.dma("sp", dbg_t[n][i * 128:(i + 1) * 128, :], tx[:], reads=[tx], writes=[dbg_t[n]])
        if n == "mixT":
            with kb.scope():
                tmpb = kb.sb("dbgmb", [128, T], BF16)
                tmpf = kb.sb("dbgmf", [128, T])
                for j in range(8):
                    kb.dma("sp", tmpb[:], mixT[j * 128:(j + 1) * 128, :], reads=[mixT], writes=[tmpb])
                    kb.op("dve", lambda e, tmpb=tmpb, tmpf=tmpf: e.tensor_copy(tmpf[:], tmpb[:]), reads=[tmpb], writes=[tmpf])
                    kb.dma("sp", dbg_t[n][j * 128:(j + 1) * 128, :], tmpf[:], reads=[tmpf], writes=[dbg_t[n]])
    kb.finish()
    kb.es.close()
    return kb, cst


_PROG = {}


def kernel(**inputs):
    if "p" not in _PROG:
        _PROG["p"] = build()
    kb, cst = _PROG["p"]
    f = lambda a: np.ascontiguousarray(np.asarray(a, dtype=np.float32))
    shared = {}
    for n in inputs:
        if n in ("x", "c", "ctx", "c_ctx"):
            continue
        shared[n] = f(inputs[n])
    shared["c_ctx"] = f(inputs["c_ctx"])
    for n, a in cst.items():
        shared["k_" + n] = np.ascontiguousarray(a)
    x, c, ctx = f(inputs["x"]), f(inputs["c"]), f(inputs["ctx"])
    B = x.shape[0]
    in_maps = []
    for b in range(B):
        m = dict(shared)
        m["x"] = np.ascontiguousarray(x[b])
        m["c"] = np.ascontiguousarray(c[b])
        m["ctx"] = np.ascontiguousarray(ctx[b])
        in_maps.append(m)
    res = run_bass_kernel_spmd(kb.nc, in_maps, core_ids=list(range(B)))
    return np.stack([np.asarray(res.results[b]["out"], dtype=np.float32) for b in range(B)], axis=0)
```

```python
import contextlib
import math
import numpy as np
import ml_dtypes
import concourse.bass as bass
import concourse.mybir as mybir
from concourse.bass_utils import run_bass_kernel_spmd

F32 = mybir.dt.float32
BF16 = mybir.dt.bfloat16
F32R = mybir.dt.float32r
ALU = mybir.AluOpType
AF = mybir.ActivationFunctionType
AX = mybir.AxisListType

D = 1024
L = 4096
C = 256
T = L + C
NT = T // 128
DEPTH = 4
D_IN = 3712
HY0, HYG0, RW0, RWG0, WA0, WAG0, FA0, FAG0 = 0, 768, 1024, 1920, 2176, 2688, 2944, 3456
EPS = 1e-6
NSLOT = 24
import os
RW_STAGE = int(os.environ.get('RW_STAGE', '99'))
INLINE_WAIT = int(os.environ.get('INLINE_WAIT', '1'))
POOL_DMA_TO_SP = int(os.environ.get('POOL_DMA_TO_SP', '1'))
RW_V2 = int(os.environ.get('RW_V2', '3'))


class Buf:
    def __init__(self, name=""):
        self.name = name
        self.w = None
        self.r = {}

    def wdeps(self):
        return [self.w] if self.w is not None else []

    def rdeps(self):
        return list(self.r.values())

    def add_reader(self, tok):
        k = tok[:2]
        if k not in self.r or self.r[k][2] < tok[2]:
            self.r[k] = tok

    def set_writer(self, tok):
        self.w = tok
        self.r = {}


class Tile(Buf):
    def __init__(self, name, t):
        super().__init__(name)
        self.t = t

    def __getitem__(self, key):
        return self.t[key]


class KB:
    def __init__(self):
        self.nc = bass.Bass("TRN2", target_bir_lowering=False)
        nc = self.nc
        self.es = contextlib.ExitStack()
        self.eng = {"pe": nc.tensor, "act": nc.scalar, "dve": nc.vector, "pool": nc.gpsimd, "sp": nc.sync}
        self.sem = {}
        self.cnt = {}
        self.waited = {e: {} for e in self.eng}
        for e in self.eng:
            self.sem[e] = self.es.enter_context(nc.semaphore("s_" + e))
            self.cnt[e] = 0
        self.slots = {}
        self.slot_i = {}
        for q in ("sp", "act", "pool"):
            self.slots[q] = [[self.es.enter_context(nc.semaphore(f"d_{q}{i}")), 0] for i in range(NSLOT)]
            self.slot_i[q] = 0
        self.n_ins = 0

    def sb(self, name, shape, dt=F32):
        self.uid = getattr(self, "uid", 0) + 1
        name = f"{name}_{self.uid}"
        return Tile(name, self.es.enter_context(self.nc.sbuf_tensor(name, list(shape), dt)))

    def ps(self, name, shape, dt=F32):
        return Tile(name, self.es.enter_context(self.nc.psum_tensor(name, list(shape), dt)))

    def dram(self, name, shape, dt=F32, kind="Internal"):
        t = self.nc.dram_tensor(name, list(shape), dt, kind=kind)
        b = Tile(name, t.ap())
        return b

    def _tok_sem(self, tok):
        if tok[0] == "e":
            return ("e", tok[1]), self.sem[tok[1]], tok[2]
        return ("d", tok[1]), self.slots[tok[1][0]][tok[1][1]][0], tok[2]

    def _wait(self, e, toks, defer=False):
        need = {}
        for tok in toks:
            if tok is None:
                continue
            key, sem, val = self._tok_sem(tok)
            if tok[0] == "e" and tok[1] == e and e == "pe":
                continue
            if self.waited[e].get(key, 0) >= val:
                continue
            if key not in need or need[key][1] < val:
                need[key] = (sem, val)
        items = list(need.items())
        inline = None
        if defer and INLINE_WAIT and items:
            inline = items.pop()
        for key, (sem, val) in items:
            self.eng[e].wait_ge(sem, val)
            self.waited[e][key] = val
        return inline

    def op(self, e, fn, reads=(), writes=()):
        toks = []
        for b in reads:
            toks += b.wdeps()
        for b in writes:
            toks += b.wdeps() + b.rdeps()
        inline = self._wait(e, toks, defer=True)
        ins = fn(self.eng[e])
        if inline is not None:
            key, (sem, val) = inline
            ins._wait_ge(sem, val)
            self.waited[e][key] = val
        self.cnt[e] += 1
        ins.then_inc(self.sem[e], 1)
        tok = ("e", e, self.cnt[e])
        for b in reads:
            b.add_reader(tok)
        for b in writes:
            b.set_writer(tok)
        self.n_ins += 1
        return ins

    def dma(self, q, out, in_, reads=(), writes=(), slow=False):
        if q == "pool" and POOL_DMA_TO_SP:
            q = "sp"
        i = self.slot_i[q]
        self.slot_i[q] = (i + 1) % NSLOT
        slot = self.slots[q][i]
        toks = []
        if slot[1] > 0:
            toks.append(("d", (q, i), slot[1]))
        for b in reads:
            toks += b.wdeps()
        for b in writes:
            toks += b.wdeps() + b.rdeps()
        self._wait(q, toks)
        if slow:
            ins = self.eng[q].dma_start(out=out, in_=in_, allow_slow_non_contiguous=True)
        else:
            ins = self.eng[q].dma_start(out=out, in_=in_)
        ins.then_inc(slot[0], 16)
        slot[1] += 16
        tok = ("d", (q, i), slot[1])
        for b in reads:
            b.add_reader(tok)
        for b in writes:
            b.set_writer(tok)
        self.n_ins += 1
        return ins

    def barrier(self):
        toks = [("e", e, self.cnt[e]) for e in self.eng if self.cnt[e] > 0]
        for q in self.slots:
            for i, s in enumerate(self.slots[q]):
                if s[1] > 0:
                    toks.append(("d", (q, i), s[1]))
        for e in self.eng:
            self._wait(e, toks)

    def finish(self):
        self.barrier()

    @contextlib.contextmanager
    def scope(self):
        es = contextlib.ExitStack()
        old = self.es
        self.es = es
        try:
            yield
        finally:
            self.barrier()
            self.es = old
            es.close()


def host_consts():
    cst = {}
    cst["ident_bf"] = np.eye(128, dtype=np.float32).astype(ml_dtypes.bfloat16)
    cst["ident_f"] = np.eye(128, dtype=np.float32)
    blk = np.zeros((128, 128), np.float32)
    blk[:64, :64] = 1.0
    blk[64:, 64:] = 1.0
    cst["blk64"] = blk
    cst["ones_f"] = np.ones((128, 128), np.float32)
    t = np.arange(L)
    row = (t // 64).astype(np.float32)
    col = (t % 64).astype(np.float32)
    inv = (10000.0 ** (-np.arange(16, dtype=np.float32) / 16)).astype(np.float32)
    cosT = np.zeros((128, L), np.float32)
    sinT = np.zeros((128, L), np.float32)
    perm = np.zeros((128, 128), np.float32)
    for p in range(128):
        d = p % 64
        sec, half, f = d // 32, (d % 32) // 16, d % 16
        pos = row if sec == 0 else col
        ang = (pos * inv[f]).astype(np.float32)
        cosT[p] = np.cos(ang)
        sinT[p] = np.sin(ang)
        if half == 0:
            perm[p + 16, p] = -1.0
        else:
            perm[p - 16, p] = 1.0
    cst["rope_cos"] = cosT
    cst["rope_sin"] = sinT
    cst["rope_perm"] = perm
    i = np.arange(128)[:, None]
    j = np.arange(384)[None, :]
    cst["wmask"] = np.where((j >= i) & (j <= i + 256), 0.0, -1e30).astype(np.float32)
    ii = np.arange(64)
    ms = np.zeros((128, 2, 64), np.float32); mts = np.zeros((128, 2, 64), np.float32); mti = np.zeros((128, 2, 64), np.float32)
    for hp in range(2):
        rows = slice(hp * 64, hp * 64 + 64)
        ms[rows, 0, :] = (ii[None, :] < ii[:, None]); ms[rows, 1, :] = (ii[None, :] > ii[:, None])
        mts[rows, 0, :] = (ii[:, None] < ii[None, :]); mts[rows, 1, :] = (ii[:, None] > ii[None, :])
        mti[rows, 0, :] = (ii[:, None] <= ii[None, :]); mti[rows, 1, :] = (ii[:, None] >= ii[None, :])
    cst["rw_ms"] = ms; cst["rw_mts"] = mts; cst["rw_mti"] = mti
    cst["rw_msb"] = np.concatenate([ms, ms], 2); cst["rw_mtsb"] = np.concatenate([mts, mts], 2)
    cst["rw_id2"] = np.concatenate([np.eye(64, dtype=np.float32)] * 2, 0)
    cst.update(hy_consts(L, 32, 32, "L"))
    cst.update(hy_consts(C, 2, 64, "C"))
    return cst


def hy_consts(Ls, A, cbw, tag):
    G = 128 // A
    N = 2 * Ls
    out = {}
    p = np.arange(128)
    f1 = np.arange(256)
    F = np.zeros((128, 2, 512), np.float64)
    for h in range(2):
        pp = h * 128 + p
        ang = 2 * np.pi * ((pp[:, None] * f1[None, :]) % 256) / 256
        F[:, h, 0:256] = np.cos(ang)
        F[:, h, 256:512] = -np.sin(ang)
    out["F256"] = F
    a_of_row = np.arange(128) // G
    th = 2 * np.pi * ((a_of_row[:, None] * f1[None, :]) % N) / N
    out["TWC"] = np.cos(th)
    out["TWS"] = np.sin(th)
    Dre = np.zeros((128, 128)); Dim = np.zeros((128, 128))
    E1 = np.zeros((128, 256)); E2 = np.zeros((128, 256))
    for a in range(A):
        for c in range(G):
            for f2 in range(A):
                ph = 2 * np.pi * ((a * f2) % A) / A
                Dre[a * G + c, c * A + f2] = np.cos(ph)
                Dim[a * G + c, c * A + f2] = -np.sin(ph)
                E1[c * A + f2, c * A + a] = np.cos(ph)
                E1[c * A + f2, 128 + c * A + a] = np.sin(ph)
                E2[c * A + f2, c * A + a] = -np.sin(ph)
                E2[c * A + f2, 128 + c * A + a] = np.cos(ph)
    out["Dre"] = Dre; out["Dim"] = Dim; out["nDim"] = -Dim; out["E1"] = E1; out["E2"] = E2
    a_of_col = np.arange(128) % A
    TW2C = np.zeros((128, 2, 128)); TW2S = np.zeros((128, 2, 128))
    IC = np.zeros((128, 2, 128)); IS = np.zeros((128, 2, 128))
    for ch in range(2):
        ff = ch * 128 + np.arange(128)
        th2 = 2 * np.pi * ((ff[:, None] * a_of_col[None, :]) % N) / N
        TW2C[:, ch, :] = np.cos(th2) / N
        TW2S[:, ch, :] = np.sin(th2) / N
        ph = 2 * np.pi * ((ff[:, None] * p[None, :]) % 256) / 256
        IC[:, ch, :] = np.cos(ph)
        IS[:, ch, :] = -np.sin(ph)
    out["TW2C"] = TW2C; out["TW2S"] = TW2S; out["IC"] = IC; out["IS"] = IS
    tp = np.arange(N)
    pos = np.where(tp < Ls, tp, N - tp).astype(np.float64)
    tn = (pos / (Ls - 1)).astype(np.float32)
    w = ((2.0 * math.pi / Ls) * pos).astype(np.float32)
    fb = np.linspace(1e-4, 15.0, 16, dtype=np.float32)
    zT = np.zeros((33, N), np.float32)
    zT[0] = tn
    zT[1:17] = np.cos(fb[:, None] * w[None, :])
    zT[17:33] = np.sin(fb[:, None] * w[None, :])
    out["zT"] = zT
    deltas = np.abs(np.linspace(math.log(1e-2) / 1.5, math.log(1e-2) / 0.3, 256, dtype=np.float32))
    dec = np.exp(-tn[:, None] * deltas[None, :]).astype(np.float32)
    dec[Ls, :] = 0.0
    nblk = 256 // cbw
    ngr = cbw // G
    DEC = np.zeros((nblk, 128, 2, ngr, A, G), np.float32)
    for h in range(2):
        for a in range(A):
            tpp = A * (h * 128 + p) + a
            for b in range(nblk):
                DEC[b, :, h, :, a, :] = dec[tpp, b * cbw:(b + 1) * cbw].reshape(128, ngr, G)
    out["DEC"] = DEC.reshape(nblk, 128, 2 * ngr * A * G)
    return {f"hy{tag}_{k}": np.ascontiguousarray(v.astype(np.float32)) for k, v in out.items()}

CONST_SPECS = None


def build(depth=DEPTH, dbg=()):
    kb = KB()
    nc = kb.nc
    cst = host_consts()
    def inp(name, shape, dt=F32):
        return kb.dram(name, shape, dt, kind="ExternalInput")

    x_in = inp("x", [L, D])
    c_in = inp("c", [D])
    ctx_in = inp("ctx", [C, D])
    cctx_in = inp("c_ctx", [D])
    W = {}
    wspec = {
        "mod_w": [DEPTH, D, 3 * D], "mod_b": [DEPTH, 3 * D], "norm_g": [DEPTH, D], "w_in": [DEPTH, D, D_IN],
        "w_out": [DEPTH, D, D], "wa_sink": [DEPTH, 4], "fa_q_norm": [DEPTH, 64], "fa_k_norm": [DEPTH, 64],
        "final_g": [D],
        "rw_conv": [DEPTH, 3, 896], "rw_w0": [DEPTH, 2, 256], "rw_w_up": [DEPTH, 2, 64, 256], "rw_a0": [DEPTH, 2, 256],
        "rw_a_up": [DEPTH, 2, 64, 256], "rw_k_k": [DEPTH, 256], "rw_k_a": [DEPTH, 256], "rw_r_k": [DEPTH, 256],
        "rw_ln_g": [DEPTH, 256], "rw_ln_b": [DEPTH, 256],
        "hy_conv": [DEPTH, 3, 768], "hy_fw1": [DEPTH, 33, 64], "hy_fb1": [DEPTH, 64], "hy_freq": [DEPTH, 64],
        "hy_fw2": [DEPTH, 64, 64], "hy_fb2": [DEPTH, 64], "hy_fw3": [DEPTH, 64, 1024], "hy_bias": [DEPTH, 2, 256],
    }
    for n, s in wspec.items():
        W[n] = inp(n, s)
    CT = {}
    for n, a in cst.items():
        CT[n] = inp("k_" + n, list(a.shape), BF16 if a.dtype == ml_dtypes.bfloat16 else F32)
    out = kb.dram("out", [L, D], F32, kind="ExternalOutput")
    xres = kb.dram("xres", [T, D], F32)
    mixT = kb.dram("mixT", [D, T], BF16)
    RS = {}
    for n in ("RT", "VT", "AL", "W0", "W1", "B0", "B1", "KD0", "KD1", "YF", "YB"):
        RS[n] = kb.dram("rs_" + n, [256, T])
    RS["VTOK"] = kb.dram("rs_VTOK", [T, 256])
    RS["SGT"] = kb.dram("rs_SGT", [256, T], BF16)
    HS = {"SG": kb.dram("hs_SG", [256, T], BF16),
          "UTL": kb.dram("hs_UTL", [3, 8, 128, 32 * 32]), "UTC": kb.dram("hs_UTC", [3, 4, 128, 2 * 64])}
    dbg_t = {}
    for n, s in dbg:
        dbg_t[n] = kb.dram("dbg_" + n, s, F32, kind="ExternalOutput")

    ident_bf = kb.sb("ident_bf", [128, 128], BF16)
    ident_f = kb.sb("ident_f", [128, 128])
    blk64 = kb.sb("blk64", [128, 128])
    ones_f = kb.sb("ones_f", [128, 128])
    for tl, n in ((ident_bf, "ident_bf"), (ident_f, "ident_f"), (blk64, "blk64"), (ones_f, "ones_f")):
        kb.dma("sp", tl[:], CT[n][:, :], reads=[CT[n]], writes=[tl])
    hT = G1 = SH = None
    GT = kb.sb("GT", [128, 2, D])
    PS = [kb.ps(f"ps{i}", [128, 512]) for i in range(8)]

    xres_b = [Buf(f"xres{i}") for i in range(NT)]

    def x_src(l, i):
        if l == 0:
            if i < 2:
                return ctx_in[i * 128:(i + 1) * 128, :], ctx_in
            return x_in[(i - 2) * 128:(i - 1) * 128, :], x_in
        return xres[i * 128:(i + 1) * 128, :], xres_b[i]

    def phase_mod(l):
        with kb.scope():
            cc = kb.sb("cc", [128, 2, 8])
            sc = kb.sb("sc", [128, 2, 8])
            mw = [kb.sb(f"mw{i}", [128, 8, 512]) for i in range(2)]
            mb = kb.sb("mb", [128, 3 * D])
            ng = kb.sb("ng", [128, D])
            modr = kb.sb("modr", [128, 2, 3 * D])
            kb.dma("sp", cc[:, 0, :], c_in.t.rearrange("(j p) -> p j", p=128), reads=[c_in], writes=[cc], slow=True)
            kb.dma("sp", cc[:, 1, :], cctx_in.t.rearrange("(j p) -> p j", p=128), reads=[cctx_in], writes=[cc], slow=True)
            kb.dma("sp", mb[:], W["mod_b"][l, :].partition_broadcast(128), reads=[W["mod_b"]], writes=[mb])
            kb.dma("sp", ng[:], W["norm_g"][l, :].partition_broadcast(128), reads=[W["norm_g"]], writes=[ng])
            kb.op("act", lambda e: e.activation(sc[:], cc[:], AF.Silu), reads=[cc], writes=[sc])
            for n in range(6):
                m = mw[n % 2]
                kb.dma("sp" if n % 2 == 0 else "pool", m[:],
                       W["mod_w"][l, :, n * 512:(n + 1) * 512].rearrange("(j p) n -> p j n", p=128),
                       reads=[W["mod_w"]], writes=[m])
                for i in range(2):
                    p = PS[(2 * n + i) % 8]
                    for j in range(8):
                        kb.op("pe", lambda e, p=p, i=i, j=j, m=m: e.matmul(
                            p[:, :], sc[:, i, j:j + 1].broadcast_to([128, 128]), m[:, j, :],
                            start=(j == 0), stop=(j == 7)), reads=[sc, m], writes=[p])
                    kb.op("dve", lambda e, p=p, i=i, n=n: e.tensor_tensor(
                        modr[:, i, n * 512:(n + 1) * 512], p[:, :], mb[:, n * 512:(n + 1) * 512], ALU.add),
                        reads=[p, mb], writes=[modr])
            for i in range(2):
                kb.op("dve", lambda e, i=i: e.scalar_tensor_tensor(
                    G1[:, i, :], modr[:, i, D:2 * D], 1.0, ng[:], ALU.add, ALU.mult), reads=[modr, ng], writes=[G1])
                kb.op("act", lambda e, i=i: e.copy(SH[:, i, :], modr[:, i, 0:D]), reads=[modr], writes=[SH])
                kb.op("act", lambda e, i=i: e.copy(GT[:, i, :], modr[:, i, 2 * D:3 * D]), reads=[modr], writes=[GT])

    def phase_norm(l):
        with kb.scope():
            xt = [kb.sb(f"xt{i}", [128, D]) for i in range(3)]
            junk = kb.sb("junk", [128, D])
            hf = [kb.sb(f"hf{i}", [128, D]) for i in range(2)]
            hb = [kb.sb(f"hb{i}", [128, D], BF16) for i in range(2)]
            st = [kb.sb(f"st{i}", [128, 4]) for i in range(2)]
            for i in range(NT):
                x, s, h, hbt = xt[i % 3], st[i % 2], hf[i % 2], hb[i % 2]
                sel = 1 if i < 2 else 0
                src, srcb = x_src(l, i)
                kb.dma("sp" if i % 2 == 0 else "pool", x[:], src, reads=[srcb], writes=[x])
                kb.op("act", lambda e, x=x, s=s: e.activation(junk[:], x[:], AF.Square, accum_out=s[:, 0:1]),
                      reads=[x], writes=[junk, s])
                kb.op("dve", lambda e, s=s: e.tensor_scalar(s[:, 1:2], s[:, 0:1], 1.0 / D, EPS, ALU.mult, ALU.add),
                      reads=[s], writes=[s])
                kb.op("act", lambda e, s=s: e.sqrt(s[:, 2:3], s[:, 1:2]), reads=[s], writes=[s])
                kb.op("dve", lambda e, s=s: e.reciprocal(s[:, 3:4], s[:, 2:3]), reads=[s], writes=[s])
                kb.op("dve", lambda e, x=x, s=s, h=h, sel=sel: e.scalar_tensor_tensor(
                    h[:], x[:], s[:, 3:4], G1[:, sel, :], ALU.mult, ALU.mult), reads=[x, s, G1], writes=[h])
                kb.op("pool", lambda e, h=h, hbt=hbt, sel=sel: e.tensor_tensor(hbt[:], h[:], SH[:, sel, :], ALU.add),
                      reads=[h, SH], writes=[hbt])
                p = PS[i % 4]
                pv = p[:, :].bitcast(BF16)
                for j in range(8):
                    kb.op("pe", lambda e, j=j, pv=pv, hbt=hbt: e.transpose(
                        pv[:, j * 128:(j + 1) * 128], hbt[:, j * 128:(j + 1) * 128], ident_bf[:]),
                        reads=[hbt, ident_bf], writes=[p])
                kb.op("act", lambda e, pv=pv, i=i: e.copy(
                    hT[:, :, i * 128:(i + 1) * 128], pv.rearrange("p (j t) -> p j t", j=8)), reads=[p], writes=[hT])

    def load_w(l, dst, col0, ncols, stage, q="sp"):
        kb.dma(q, stage[:, :, 0:ncols], W["w_in"][l, :, col0:col0 + ncols].rearrange("(j p) n -> p j n", p=128),
               reads=[W["w_in"]], writes=[stage])
        kb.op("pool", lambda e: e.tensor_copy(dst[:, :, 0:ncols], stage[:, :, 0:ncols]), reads=[stage], writes=[dst])

    def proj_fm(p, wt, c0, nc_, t0, nt):
        for j in range(8):
            kb.op("pe", lambda e, j=j: e.matmul(p[0:nc_, 0:nt], wt[:, j, c0:c0 + nc_], hT[:, j, t0:t0 + nt],
                                                start=(j == 0), stop=(j == 7)), reads=[wt, hT], writes=[p])

    def proj_tm(p, wt, c0, nc_, i):
        for j in range(8):
            kb.op("pe", lambda e, j=j: e.matmul(p[:, 0:nc_], hT[:, j, i * 128:(i + 1) * 128], wt[:, j, c0:c0 + nc_],
                                                start=(j == 0), stop=(j == 7)), reads=[wt, hT], writes=[p])

    TCH = [(t0, min(512, T - t0)) for t0 in range(0, T, 512)]

    def qk_prep(l, es_tiles, wt, c0, dst, dst_j, gvec, norm, rope):
        raw, sq, rs, rot = es_tiles
        for ci, (t0, nt) in enumerate(TCH):
            p = PS[ci % 2]
            proj_fm(p, wt, c0, 128, t0, nt)
            if norm:
                kb.op("act", lambda e, p=p, nt=nt: e.activation(sq[:, 0:nt], p[:, 0:nt], AF.Square), reads=[p], writes=[sq])
                p2 = PS[2 + ci % 2]
                kb.op("pe", lambda e, p2=p2, nt=nt: e.matmul(p2[:, 0:nt], blk64[:], sq[:, 0:nt], start=True, stop=True),
                      reads=[blk64, sq], writes=[p2])
                kb.op("dve", lambda e, p2=p2, nt=nt: e.tensor_scalar(rs[:, 0:nt], p2[:, 0:nt], 1.0 / 64, EPS, ALU.mult, ALU.add),
                      reads=[p2], writes=[rs])
                kb.op("act", lambda e, nt=nt: e.sqrt(rs[:, 0:nt], rs[:, 0:nt]), reads=[rs], writes=[rs])
                kb.op("dve", lambda e, nt=nt: e.reciprocal(rs[:, 0:nt], rs[:, 0:nt]), reads=[rs], writes=[rs])
                kb.op("dve", lambda e, p=p, nt=nt: e.scalar_tensor_tensor(
                    raw[:, 0:nt], p[:, 0:nt], gvec[:, 0:1], rs[:, 0:nt], ALU.mult, ALU.mult), reads=[p, gvec, rs], writes=[raw])
            else:
                kb.op("act", lambda e, p=p, nt=nt: e.copy(raw[:, 0:nt], p[:, 0:nt]), reads=[p], writes=[raw])
            lat0 = 0
            if t0 < C:
                lat0 = C - t0
                kb.op("pool", lambda e, t0=t0, lat0=lat0: e.tensor_copy(dst[:, dst_j, t0:t0 + lat0], raw[:, 0:lat0]),
                      reads=[raw], writes=[dst])
            if not rope:
                if nt > lat0:
                    kb.op("pool", lambda e, t0=t0, lat0=lat0, nt=nt: e.tensor_copy(
                        dst[:, dst_j, t0 + lat0:t0 + nt], raw[:, lat0:nt]), reads=[raw], writes=[dst])
                continue
            p3 = PS[4 + ci % 2]
            n_l = nt - lat0
            lp = t0 + lat0 - C
            kb.op("pe", lambda e, p3=p3, lat0=lat0, nt=nt: e.matmul(p3[:, lat0:nt], rope_perm[:], raw[:, lat0:nt], start=True, stop=True),
                  reads=[rope_perm, raw], writes=[p3])
            kb.op("dve", lambda e, p3=p3, lat0=lat0, nt=nt, lp=lp, n_l=n_l: e.tensor_tensor(
                rot[:, lat0:nt], p3[:, lat0:nt], rope_sin[:, lp:lp + n_l], ALU.mult), reads=[p3, rope_sin], writes=[rot])
            kb.op("pool", lambda e, lat0=lat0, nt=nt, lp=lp, n_l=n_l: e.tensor_tensor(
                raw[:, lat0:nt], raw[:, lat0:nt], rope_cos[:, lp:lp + n_l], ALU.mult), reads=[raw, rope_cos], writes=[raw])
            kb.op("dve", lambda e, t0=t0, lat0=lat0, nt=nt: e.tensor_tensor(
                dst[:, dst_j, t0 + lat0:t0 + nt], raw[:, lat0:nt], rot[:, lat0:nt], ALU.add), reads=[raw, rot], writes=[dst])

    rope_cos = rope_sin = rope_perm = None

    def phase_attn(l, dense, with_ctx):
        nonlocal rope_cos, rope_sin, rope_perm
        base = FA0 if dense else WA0
        gbase = FAG0 if dense else WAG0
        mrow = 768 if dense else 512
        with kb.scope():
            wt = kb.sb("wt", [128, 8, 768], BF16)
            gq = kb.sb("gq", [128, 1])
            gk = kb.sb("gk", [128, 1])
            sink = kb.sb("sink", [128, 4])
            if dense:
                for hh in range(2):
                    kb.dma("sp", gq[hh * 64:(hh + 1) * 64, :], W["fa_q_norm"][l, :].rearrange("(d o) -> d o", o=1),
                           reads=[W["fa_q_norm"]], writes=[gq], slow=True)
                    kb.dma("sp", gk[hh * 64:(hh + 1) * 64, :], W["fa_k_norm"][l, :].rearrange("(d o) -> d o", o=1),
                           reads=[W["fa_k_norm"]], writes=[gk], slow=True)
            else:
                kb.dma("sp", sink[:], W["wa_sink"][l, :].partition_broadcast(128), reads=[W["wa_sink"]], writes=[sink])
            QT = kb.sb("QT", [128, 2, T], BF16)
            KT = kb.sb("KT", [128, 1, T], BF16)
            VW = 65 if dense else 64
            Vt = kb.sb("Vt", [128, NT, 2, VW], BF16)
            SG = None
            with kb.scope():
                rope_cos = kb.sb("rope_cos", [128, L])
                rope_sin = kb.sb("rope_sin", [128, L])
                rope_perm = kb.sb("rope_perm", [128, 128])
                kb.dma("sp", rope_cos[:], CT["rope_cos"][:, :], reads=[CT["rope_cos"]], writes=[rope_cos])
                kb.dma("pool", rope_sin[:], CT["rope_sin"][:, :], reads=[CT["rope_sin"]], writes=[rope_sin])
                kb.dma("sp", rope_perm[:], CT["rope_perm"][:, :], reads=[CT["rope_perm"]], writes=[rope_perm])
                stage = kb.sb("wstage", [128, 8, 256])
                w4 = W["w_in"][l, :, base:base + 256].rearrange("(j p) (h d) -> p j h d", p=128, d=64)
                st4 = stage[:, :, 0:256].rearrange("p j (h d) -> p j h d", d=64)
                for hi, h in enumerate((0, 2, 1, 3)):
                    kb.dma("sp", st4[:, :, hi, :], w4[:, :, h, :], reads=[W["w_in"]], writes=[stage])
                kb.op("pool", lambda e: e.tensor_copy(wt[:, :, 0:256], stage[:]), reads=[stage], writes=[wt])
                kb.dma("pool", stage[:], W["w_in"][l, :, base + 256:base + 512].rearrange("(j p) n -> p j n", p=128),
                       reads=[W["w_in"]], writes=[stage])
                kb.op("pool", lambda e: e.tensor_copy(wt[:, :, 256:512], stage[:]), reads=[stage], writes=[wt])
                kb.dma("sp", stage[:], W["w_in"][l, :, gbase:gbase + 256].rearrange("(j p) n -> p j n", p=128),
                       reads=[W["w_in"]], writes=[stage])
                kb.op("pool", lambda e: e.tensor_copy(wt[:, :, 512:768], stage[:]), reads=[stage], writes=[wt])
                tl = (kb.sb("qraw", [128, 512]), kb.sb("qsq", [128, 512]), kb.sb("qrs", [128, 512]), kb.sb("qrot", [128, 512]))
                qk_prep(l, tl, wt, 0, QT, 0, gq, dense, True)
                qk_prep(l, tl, wt, 128, QT, 1, gq, dense, True)
                qk_prep(l, tl, wt, 256, KT, 0, gk, dense, True)
            if dense:
                kb.op("pool", lambda e: e.memset(Vt[:, :, :, 64:65], 1.0), writes=[Vt])
            for i in range(NT):
                p = PS[i % 2]
                proj_tm(p, wt, 384, 128, i)
                kb.op("act", lambda e, p=p, i=i: e.copy(Vt[:, i, :, 0:64], p[:, 0:128].rearrange("p (k d) -> p k d", d=64)),
                      reads=[p], writes=[Vt])
            with kb.scope():
                if dense:
                    attn_dense(l, wt, QT, KT, Vt, mrow, with_ctx)
                else:
                    attn_window(l, wt, QT, KT, Vt, sink, mrow, with_ctx)

    def attn_dense(l, wt, QT, KT, Vt, mrow, with_ctx):
        pt = [kb.sb(f"pt{i}", [128, 512], BF16) for i in range(4)]
        osb = [kb.sb(f"osb{i}", [128, 512]) for i in range(2)]
        rc = [kb.sb(f"rc{i}", [128, 512]) for i in range(2)]
        ob = [kb.sb(f"ob{i}", [128, 512], BF16) for i in range(2)]
        it = 0
        sgt = [kb.sb(f"sgt{i}", [64, 512], BF16) for i in range(2)]
        chunks = []
        if with_ctx:
            chunks.append((0, C, 0, 2))
        for t0 in range(C, T, 512):
            chunks.append((t0, 512, 0, NT))
        for h in range(4):
            kv, pr = h // 2, h % 2
            ks = slice(64 * kv, 64 * kv + 64)
            for (t0, nt, kb0, kb1) in chunks:
                po = PS[4 + it % 2]
                sg = sgt[it % 2]
                pg = PS[6 + it % 2]
                for j in range(8):
                    kb.op("pe", lambda e, j=j: e.matmul(
                        pg[0:64, 0:nt], wt[:, j, 512 + 64 * h:576 + 64 * h], hT[:, j, t0:t0 + nt], start=(j == 0), stop=(j == 7)),
                        reads=[wt, hT], writes=[pg])
                kb.op("act", lambda e: e.activation(sg[0:64, 0:nt], pg[0:64, 0:nt], AF.Silu), reads=[pg], writes=[sg])
                def pv_(kbi):
                    ptt = pt[kbi % 4]
                    kb.op("pe", lambda e: e.matmul(
                        po[0:65, 0:nt], Vt[:, kbi, kv, 0:65], ptt[:, 0:nt], start=(kbi == kb0), stop=(kbi == kb1 - 1)),
                        reads=[Vt, ptt], writes=[po])
                LA = 2
                for kbi in range(kb0, kb1):
                    psS = PS[kbi % 4]
                    ptt = pt[kbi % 4]
                    kb.op("pe", lambda e, psS=psS, kbi=kbi: e.matmul(
                        psS[:, 0:nt], KT[ks, 0, kbi * 128:(kbi + 1) * 128], QT[ks, pr, t0:t0 + nt], start=True, stop=True),
                        reads=[KT, QT], writes=[psS])
                    kb.op("act", lambda e, psS=psS, ptt=ptt: e.activation(ptt[:, 0:nt], psS[:, 0:nt], AF.Exp, scale=0.125),
                          reads=[psS], writes=[ptt])
                    if kbi - LA >= kb0:
                        pv_(kbi - LA)
                for kbi in range(max(kb0, kb1 - LA), kb1):
                    pv_(kbi)
                o_s, r_c, o_b = osb[it % 2], rc[it % 2], ob[it % 2]
                kb.op("dve", lambda e: e.reciprocal(r_c[64:65, 0:nt], po[64:65, 0:nt]), reads=[po], writes=[r_c])
                kb.op("act", lambda e: e.copy(o_s[0:64, 0:nt], po[0:64, 0:nt]), reads=[po], writes=[o_s])
                pb = PS[6 + it % 2]
                kb.op("pe", lambda e: e.matmul(pb[0:64, 0:nt], ones_f[64:65, 0:64], r_c[64:65, 0:nt], start=True, stop=True),
                      reads=[ones_f, r_c], writes=[pb])
                kb.op("dve", lambda e: e.tensor_tensor(o_s[0:64, 0:nt], o_s[0:64, 0:nt], pb[0:64, 0:nt], ALU.mult),
                      reads=[o_s, pb], writes=[o_s])
                kb.op("pool", lambda e: e.tensor_tensor(o_b[0:64, 0:nt], o_s[0:64, 0:nt], sg[0:64, 0:nt], ALU.mult),
                      reads=[o_s, sg], writes=[o_b])
                kb.dma("pool", mixT[mrow + 64 * h:mrow + 64 * h + 64, t0:t0 + nt], o_b[0:64, 0:nt], reads=[o_b], writes=[Buf()])
                it += 1

    def attn_window(l, wt, QT, KT, Vt, sink, mrow, with_ctx):
        wmask = kb.sb("wmask", [128, 384])
        kb.dma("sp", wmask[:], CT["wmask"][:, :], reads=[CT["wmask"]], writes=[wmask])
        nsink = kb.sb("nsink", [128, 4])
        kb.op("dve", lambda e: e.tensor_scalar(nsink[:], sink[:], -1.0, None, ALU.mult), reads=[sink], writes=[nsink])
        S = [kb.sb(f"wS{i}", [128, 640]) for i in range(2)]
        P = [kb.sb(f"wP{i}", [128, 640]) for i in range(2)]
        Pn = [kb.sb(f"wPn{i}", [128, 640], BF16) for i in range(2)]
        PT = [kb.sb(f"wPT{i}", [128, 640], BF16) for i in range(2)]
        st = [kb.sb(f"wst{i}", [128, 8]) for i in range(2)]
        sgt = [kb.sb(f"wsg{i}", [64, 128], BF16) for i in range(2)]
        ob = [kb.sb(f"wob{i}", [64, 128], BF16) for i in range(2)]
        it = 0
        for i in range(0 if with_ctx else 2, NT):
            if i < 2:
                loc = []
            else:
                loc = list(range(max(2, i - 1), min(NT - 1, i + 1) + 1))
            nl = 128 * len(loc)
            m0 = 128 if (i >= 2 and i - 1 < 2) else 0
            nk = nl + C
            ktiles = loc + [0, 1]
            def unit(h, it):
                kv, pr = h // 2, h % 2
                ks = slice(64 * kv, 64 * kv + 64)
                s_, p_, pn_, pt_, st_, sg, o_b = S[it % 2], P[it % 2], Pn[it % 2], PT[it % 2], st[it % 2], sgt[it % 2], ob[it % 2]
                psA, psB, psT, psOG = PS[it % 2], PS[2 + it % 2], PS[4 + it % 2], PS[6 + it % 2]
                psO, psG = psOG, psOG
                q_ap = QT[ks, pr, i * 128:(i + 1) * 128]
                if nl:
                    k0 = loc[0] * 128
                    kb.op("pe", lambda e: e.matmul(psA[:, 0:nl], q_ap, KT[ks, 0, k0:k0 + nl], start=True, stop=True),
                          reads=[QT, KT], writes=[psA])
                    kb.op("dve", lambda e: e.tensor_tensor(s_[:, 0:nl], psA[:, 0:nl], wmask[:, m0:m0 + nl], ALU.add),
                          reads=[psA, wmask], writes=[s_])
                kb.op("pe", lambda e: e.matmul(psB[:, 0:C], q_ap, KT[ks, 0, 0:C], start=True, stop=True),
                      reads=[QT, KT], writes=[psB])
                kb.op("act", lambda e: e.copy(s_[:, nl:nk], psB[:, 0:C]), reads=[psB], writes=[s_])
                yield
                kb.op("dve", lambda e: e.reduce_max(st_[:, 0:1], s_[:, 0:nk], AX.X), reads=[s_], writes=[st_])
                kb.op("dve", lambda e: e.tensor_scalar(st_[:, 1:2], st_[:, 0:1], -0.125, nsink[:, h:h + 1], ALU.mult, ALU.min),
                      reads=[st_, nsink], writes=[st_])
                kb.op("act", lambda e: e.activation(p_[:, 0:nk], s_[:, 0:nk], AF.Exp, bias=st_[:, 1:2], scale=0.125,
                                                    accum_out=st_[:, 2:3]), reads=[s_, st_], writes=[p_, st_])
                kb.op("act", lambda e: e.activation(st_[:, 3:4], sink[:, h:h + 1], AF.Exp, bias=st_[:, 1:2], scale=1.0),
                      reads=[sink, st_], writes=[st_])
                kb.op("dve", lambda e: e.tensor_tensor(st_[:, 4:5], st_[:, 2:3], st_[:, 3:4], ALU.add), reads=[st_], writes=[st_])
                kb.op("dve", lambda e: e.reciprocal(st_[:, 5:6], st_[:, 4:5]), reads=[st_], writes=[st_])
                kb.op("dve", lambda e: e.tensor_scalar(pn_[:, 0:nk], p_[:, 0:nk], st_[:, 5:6], None, ALU.mult),
                      reads=[p_, st_], writes=[pn_])
                yield
                pv = psT[:, :].bitcast(BF16)
                nb = nk // 128
                for b in range(nb):
                    kb.op("pe", lambda e, b=b: e.transpose(pv[:, b * 128:(b + 1) * 128], pn_[:, b * 128:(b + 1) * 128], ident_bf[:]),
                          reads=[pn_, ident_bf], writes=[psT])
                yield
                kb.op("act", lambda e: e.copy(pt_[:, 0:nk], pv[:, 0:nk]), reads=[psT], writes=[pt_])
                yield
                for b in range(nb):
                    kb.op("pe", lambda e, b=b: e.matmul(psO[0:64, 0:128], Vt[:, ktiles[b], kv, 0:64], pt_[:, b * 128:(b + 1) * 128],
                                                        start=(b == 0), stop=(b == nb - 1)), reads=[Vt, pt_], writes=[psO])
                for j in range(8):
                    kb.op("pe", lambda e, j=j: e.matmul(
                        psG[0:64, 128:256], wt[:, j, 512 + 64 * h:576 + 64 * h], hT[:, j, i * 128:(i + 1) * 128],
                        start=(j == 0), stop=(j == 7)), reads=[wt, hT], writes=[psG])
                yield
                kb.op("act", lambda e: e.activation(sg[:, :], psG[0:64, 128:256], AF.Silu), reads=[psG], writes=[sg])
                kb.op("dve", lambda e: e.tensor_tensor(o_b[:, :], psO[0:64, 0:128], sg[:, :], ALU.mult), reads=[psO, sg], writes=[o_b])
                kb.dma("sp", mixT[mrow + 64 * h:mrow + 64 * h + 64, i * 128:(i + 1) * 128], o_b[:, :], reads=[o_b], writes=[Buf()])

            for h0 in (0, 2):
                gens = [unit(h0, it), unit(h0 + 1, it + 1)]
                it += 2
                while gens:
                    nxt = []
                    for g_ in gens:
                        try:
                            next(g_)
                            nxt.append(g_)
                        except StopIteration:
                            pass
                    gens = nxt

    def phase_out(l, last):
        with kb.scope():
            wo = kb.sb("wo", [128, 8, D], BF16)
            stage = kb.sb("wostage", [128, 8, 256])
            for q in range(4):
                kb.dma("sp", stage[:], W["w_out"][l, :, q * 256:(q + 1) * 256].rearrange("(j p) n -> p j n", p=128),
                       reads=[W["w_out"]], writes=[stage])
                kb.op("pool", lambda e, q=q: e.tensor_copy(wo[:, :, q * 256:(q + 1) * 256], stage[:]), reads=[stage], writes=[wo])
            fg = kb.sb("fg", [128, D])
            if last:
                kb.dma("sp", fg[:], W["final_g"][:].partition_broadcast(128), reads=[W["final_g"]], writes=[fg])
            mt = [kb.sb(f"mt{i}", [128, 8, 128], BF16) for i in range(2)]
            xt = [kb.sb(f"oxt{i}", [128, D]) for i in range(2)]
            xn = [kb.sb(f"oxn{i}", [128, D]) for i in range(2)]
            tmp = [kb.sb(f"otmp{i}", [128, 512]) for i in range(2)]
            st = [kb.sb(f"ost{i}", [128, 4]) for i in range(2)]
            junk = kb.sb("ojunk", [128, D])
            mixv = mixT.t.rearrange("(j p) t -> p j t", p=128)
            for it, i in enumerate(range(2 if last else 0, NT)):
                m, x, xo, s = mt[it % 2], xt[it % 2], xn[it % 2], st[it % 2]
                sel = 1 if i < 2 else 0
                kb.dma("sp", m[:], mixv[:, :, i * 128:(i + 1) * 128], reads=[mixT], writes=[m])
                src, srcb = x_src(l, i)
                kb.dma("pool", x[:], src, reads=[srcb], writes=[x])
                for hf in range(2):
                    p = PS[(2 * it + hf) % 8]
                    tp = tmp[hf]
                    for j in range(8):
                        kb.op("pe", lambda e, j=j, p=p, m=m, hf=hf: e.matmul(p[:, :], m[:, j, :], wo[:, j, hf * 512:(hf + 1) * 512],
                                                                     start=(j == 0), stop=(j == 7)), reads=[m, wo], writes=[p])
                    kb.op("dve", lambda e, p=p, tp=tp, hf=hf, sel=sel: e.tensor_tensor(
                        tp[:], p[:, :], GT[:, sel, hf * 512:(hf + 1) * 512], ALU.mult), reads=[p, GT], writes=[tp])
                    kb.op("pool", lambda e, tp=tp, hf=hf, x=x, xo=xo: e.tensor_tensor(
                        xo[:, hf * 512:(hf + 1) * 512], x[:, hf * 512:(hf + 1) * 512], tp[:], ALU.add), reads=[x, tp], writes=[xo])
                if not last:
                    kb.dma("sp", xres[i * 128:(i + 1) * 128, :], xo[:], reads=[xo], writes=[xres_b[i]])
                else:
                    kb.op("act", lambda e, xo=xo, s=s: e.activation(junk[:], xo[:], AF.Square, accum_out=s[:, 0:1]),
                          reads=[xo], writes=[junk, s])
                    kb.op("dve", lambda e, s=s: e.tensor_scalar(s[:, 1:2], s[:, 0:1], 1.0 / D, EPS, ALU.mult, ALU.add),
                          reads=[s], writes=[s])
                    kb.op("act", lambda e, s=s: e.sqrt(s[:, 2:3], s[:, 1:2]), reads=[s], writes=[s])
                    kb.op("dve", lambda e, s=s: e.reciprocal(s[:, 3:4], s[:, 2:3]), reads=[s], writes=[s])
                    kb.op("dve", lambda e, xo=xo, s=s, x=x: e.scalar_tensor_tensor(
                        x[:], xo[:], s[:, 3:4], fg[:], ALU.mult, ALU.mult), reads=[xo, s, fg], writes=[x])
                    kb.dma("sp", out[(i - 2) * 128:(i - 1) * 128, :], x[:], reads=[x], writes=[Buf()])


    def conv_tile(l, wt, cw, ncw, jt, Zraw, Zout):
        for ci, (t0, nt) in enumerate(TCH):
            p = PS[ci % 4]
            proj_fm(p, wt, 0, 128, t0, nt)
            kb.op("act", lambda e, p=p, t0=t0, nt=nt: e.copy(Zraw[:, 1 + t0:1 + t0 + nt], p[:, 0:nt]), reads=[p], writes=[Zraw])
        kb.op("dve", lambda e: e.tensor_scalar(Zout[:, :], Zraw[:, 1:T + 1], cw[:, jt, 1:2], None, ALU.mult), reads=[Zraw, cw], writes=[Zout])
        kb.op("dve", lambda e: e.scalar_tensor_tensor(Zout[:, :], Zraw[:, 0:T], cw[:, jt, 0:1], Zout[:, :], ALU.mult, ALU.add),
              reads=[Zraw, cw, Zout], writes=[Zout])
        kb.op("dve", lambda e: e.scalar_tensor_tensor(Zout[:, :], Zraw[:, 2:T + 2], cw[:, jt, 2:3], Zout[:, :], ALU.mult, ALU.add),
              reads=[Zraw, cw, Zout], writes=[Zout])
        kb.op("dve", lambda e: e.scalar_tensor_tensor(Zout[:, C - 1:C], Zraw[:, C + 1:C + 2], ncw[:, jt, 2:3], Zout[:, C - 1:C], ALU.mult, ALU.add),
              reads=[Zraw, ncw, Zout], writes=[Zout])
        kb.op("dve", lambda e: e.scalar_tensor_tensor(Zout[:, C:C + 1], Zraw[:, C:C + 1], ncw[:, jt, 0:1], Zout[:, C:C + 1], ALU.mult, ALU.add),
              reads=[Zraw, ncw, Zout], writes=[Zout])

    def colvec(name, src_ap, srcb, shape, rearr, **kw):
        t = kb.sb(name, shape)
        kb.dma("sp", t[:], src_ap.rearrange(rearr, **kw), reads=[srcb], writes=[t], slow=True)
        return t

    def phase_rwkv_prep(l):
        with kb.scope():
            stage = kb.sb("rstage", [128, 8, 128])
            wts = [kb.sb(f"rwt{i}", [128, 8, 128], BF16) for i in range(2)]
            cw = kb.sb("rcw", [128, 7, 3])
            for k in range(3):
                kb.dma("sp", cw[:, :, k], W["rw_conv"][l, k, :].rearrange("(j p) -> p j", p=128), reads=[W["rw_conv"]], writes=[cw], slow=True)
            ncw = kb.sb("rncw", [128, 7, 3])
            kb.op("dve", lambda e: e.tensor_scalar(ncw[:], cw[:], -1.0, None, ALU.mult), reads=[cw], writes=[ncw])
            kk_ = colvec("rkk", W["rw_k_k"][l, :], W["rw_k_k"], [128, 2], "(j p) -> p j", p=128)
            ka_ = colvec("rka", W["rw_k_a"][l, :], W["rw_k_a"], [128, 2], "(j p) -> p j", p=128)
            omka = kb.sb("romka", [128, 2])
            kb.op("dve", lambda e: e.tensor_scalar(omka[:], ka_[:], -1.0, 1.0, ALU.mult, ALU.add), reads=[ka_], writes=[omka])
            w0_ = kb.sb("rw0", [128, 2, 2])
            a0_ = kb.sb("ra0", [128, 2, 2])
            for d in range(2):
                kb.dma("sp", w0_[:, d, :], W["rw_w0"][l, d, :].rearrange("(j p) -> p j", p=128), reads=[W["rw_w0"]], writes=[w0_], slow=True)
                kb.dma("sp", a0_[:, d, :], W["rw_a0"][l, d, :].rearrange("(j p) -> p j", p=128), reads=[W["rw_a0"]], writes=[a0_], slow=True)
            wup = kb.sb("rwup", [128, 2, 256])
            kb.dma("sp", wup[0:64, :, :], W["rw_w_up"][l, :, :, :].rearrange("d k n -> k d n"), reads=[W["rw_w_up"]], writes=[wup])
            kb.dma("sp", wup[64:128, :, :], W["rw_a_up"][l, :, :, :].rearrange("d k n -> k d n"), reads=[W["rw_a_up"]], writes=[wup])
            Zraw = kb.sb("rZraw", [128, T + 2])
            Zout = kb.sb("rZout", [128, T])
            Z6 = kb.sb("rZ6", [128, T])
            kb.op("pool", lambda e: e.memset(Zraw[:, 0:1], 0.0), writes=[Zraw])
            kb.op("pool", lambda e: e.memset(Zraw[:, T + 1:T + 2], 0.0), writes=[Zraw])
            tA = [kb.sb(f"rtA{i}", [128, 512]) for i in range(2)]
            tB = [kb.sb(f"rtB{i}", [128, 512]) for i in range(2)]
            tC = [kb.sb(f"rtC{i}", [128, 512]) for i in range(2)]
            tD = [kb.sb(f"rtD{i}", [128, 512]) for i in range(2)]
            tE = [kb.sb(f"rtE{i}", [128, 512]) for i in range(2)]
            tG = [kb.sb(f"rtG{i}", [128, 512], BF16) for i in range(2)]
            vt_ = [kb.sb(f"rvt{i}", [128, 128]) for i in range(2)]
            order = [6, 0, 1, 4, 5, 2, 3, 7, 8]
            for oi, jt in enumerate(order):
                wt = wts[oi % 2]
                c0 = RW0 + jt * 128 if jt < 7 else RWG0 + (jt - 7) * 128
                load_w(l, wt, c0, 128, stage)
                if jt >= 7:
                    for ci, (t0, nt) in enumerate(TCH):
                        p = PS[ci % 4]
                        proj_fm(p, wt, 0, 128, t0, nt)
                        g = tG[ci % 2]
                        kb.op("act", lambda e, p=p, g=g, nt=nt: e.activation(g[:, 0:nt], p[:, 0:nt], AF.Silu), reads=[p], writes=[g])
                        kb.dma("sp", RS["SGT"][(jt - 7) * 128:(jt - 6) * 128, t0:t0 + nt], g[:, 0:nt], reads=[g], writes=[Buf()])
                    continue
                conv_tile(l, wt, cw, ncw, jt, Zraw, Z6 if jt == 6 else Zout)
                if jt == 6:
                    kb.op("act", lambda e: e.activation(Z6[0:64, :], Z6[0:64, :], AF.Tanh), reads=[Z6], writes=[Z6])
                elif jt in (0, 1):
                    kb.dma("sp", RS["RT"][jt * 128:(jt + 1) * 128, :], Zout[:, :], reads=[Zout], writes=[Buf()])
                elif jt in (4, 5):
                    kb.dma("sp", RS["VT"][(jt - 4) * 128:(jt - 3) * 128, :], Zout[:, :], reads=[Zout], writes=[Buf()])
                    for i in range(NT):
                        p = PS[4 + i % 2]
                        kb.op("pe", lambda e, p=p, i=i: e.transpose(p[:, 0:128], Zout[:, i * 128:(i + 1) * 128], ident_f[:]),
                              reads=[Zout, ident_f], writes=[p])
                        v = vt_[i % 2]
                        kb.op("act", lambda e, p=p, v=v: e.copy(v[:, :], p[:, 0:128]), reads=[p], writes=[v])
                        kb.dma("pool", RS["VTOK"][i * 128:(i + 1) * 128, (jt - 4) * 128:(jt - 3) * 128], v[:, :], reads=[v], writes=[Buf()])
                else:
                    pt = jt - 2
                    rows = slice(pt * 128, (pt + 1) * 128)
                    for ci, (t0, nt) in enumerate(TCH):
                        a_, b_, c_, d_, e_ = tA[ci % 2], tB[ci % 2], tC[ci % 2], tD[ci % 2], tE[ci % 2]
                        zc = Zout[:, t0:t0 + nt]
                        kb.op("dve", lambda e: e.tensor_scalar(a_[:, 0:nt], zc, kk_[:, pt:pt + 1], None, ALU.mult), reads=[Zout, kk_], writes=[a_])
                        kb.op("act", lambda e: e.activation(b_[:, 0:nt], a_[:, 0:nt], AF.Square), reads=[a_], writes=[b_])
                        p = PS[ci % 2]
                        kb.op("pe", lambda e: e.matmul(p[:, 0:nt], blk64[:], b_[:, 0:nt], start=True, stop=True), reads=[blk64, b_], writes=[p])
                        kb.op("act", lambda e: e.sqrt(b_[:, 0:nt], p[:, 0:nt]), reads=[p], writes=[b_])
                        kb.op("dve", lambda e: e.tensor_scalar(b_[:, 0:nt], b_[:, 0:nt], 1e-12, None, ALU.max), reads=[b_], writes=[b_])
                        kb.op("dve", lambda e: e.reciprocal(b_[:, 0:nt], b_[:, 0:nt]), reads=[b_], writes=[b_])
                        kb.op("dve", lambda e: e.scalar_tensor_tensor(a_[:, 0:nt], a_[:, 0:nt], -1.0, b_[:, 0:nt], ALU.mult, ALU.mult),
                              reads=[a_, b_], writes=[a_])
                        kb.dma("sp", RS["AL"][rows, t0:t0 + nt], a_[:, 0:nt], reads=[a_], writes=[Buf()])
                        for d in range(2):
                            pa = PS[2 + d]
                            kb.op("pe", lambda e: e.matmul(pa[:, 0:nt], wup[64:128, d, pt * 128:(pt + 1) * 128], Z6[64:128, t0:t0 + nt],
                                                           start=True, stop=True), reads=[wup, Z6], writes=[pa])
                            kb.op("act", lambda e: e.activation(c_[:, 0:nt], pa[:, 0:nt], AF.Sigmoid, bias=a0_[:, d, pt:pt + 1]),
                                  reads=[pa, a0_], writes=[c_])
                            kb.op("dve", lambda e: e.scalar_tensor_tensor(d_[:, 0:nt], c_[:, 0:nt], -1.0, a_[:, 0:nt], ALU.mult, ALU.mult),
                                  reads=[c_, a_], writes=[d_])
                            kb.dma("sp", RS[f"B{d}"][rows, t0:t0 + nt], d_[:, 0:nt], reads=[d_], writes=[Buf()])
                            kb.op("dve", lambda e: e.tensor_scalar(c_[:, 0:nt], c_[:, 0:nt], ka_[:, pt:pt + 1], omka[:, pt:pt + 1], ALU.mult, ALU.add),
                                  reads=[c_, ka_, omka], writes=[c_])
                            kb.op("dve", lambda e: e.tensor_tensor(e_[:, 0:nt], c_[:, 0:nt], zc, ALU.mult), reads=[c_, Zout], writes=[e_])
                            kb.dma("pool", RS[f"KD{d}"][rows, t0:t0 + nt], e_[:, 0:nt], reads=[e_], writes=[Buf()])
                            pw = PS[4 + d]
                            kb.op("pe", lambda e: e.matmul(pw[:, 0:nt], wup[0:64, d, pt * 128:(pt + 1) * 128], Z6[0:64, t0:t0 + nt],
                                                           start=True, stop=True), reads=[wup, Z6], writes=[pw])
                            kb.op("act", lambda e: e.activation(c_[:, 0:nt], pw[:, 0:nt], AF.Sigmoid, bias=w0_[:, d, pt:pt + 1]),
                                  reads=[pw, w0_], writes=[c_])
                            kb.op("dve", lambda e: e.tensor_scalar(d_[:, 0:nt], c_[:, 0:nt], -math.exp(-0.5), None, ALU.mult),
                                  reads=[c_], writes=[d_])
                            kb.dma("pool", RS[f"W{d}"][rows, t0:t0 + nt], d_[:, 0:nt], reads=[d_], writes=[Buf()])

    def phase_rwkv_scan(l):
        with kb.scope():
            ST = [kb.sb(f"ST{d}", [128, 2, 64]) for d in range(2)]
            for d in range(2):
                kb.op("pool", lambda e, d=d: e.memset(ST[d][:], 0.0), writes=[ST[d]])
            names = ("AL", "W", "B", "KD", "RT")
            ch = [[{n: kb.sb(f"c{n}{d}{i}", [128, 2, 128]) for n in names} for i in range(2)] for d in range(2)]
            vch = [[kb.sb(f"cV{d}{i}", [128, 256]) for i in range(2)] for d in range(2)]
            t1 = [kb.sb(f"st1{d}", [128, 2, 64]) for d in range(2)]
            t2 = [kb.sb(f"st2{d}", [128, 2, 64]) for d in range(2)]
            ysb = [kb.sb(f"ysb{d}", [64, 512]) for d in range(2)]
            psSA, psV, psY = [PS[0], PS[1]], [PS[2], PS[3]], [PS[4], PS[5]]
            border = [1, 0] + list(range(NT - 1, 1, -1))
            for ci in range(NT):
                cidx = [ci, border[ci]]
                cur = []
                for d in range(2):
                    c0 = cidx[d] * 128
                    tl_ = ch[d][ci % 2]
                    for n in names:
                        src = RS[n if n in ("AL", "RT") else f"{n}{d}"]
                        kb.dma("sp" if d == 0 else "pool", tl_[n][:],
                               src.t.rearrange("(pr q) t -> q pr t", q=128)[:, :, c0:c0 + 128], reads=[src], writes=[tl_[n]])
                    vv = vch[d][ci % 2]
                    kb.dma("sp" if d == 0 else "pool", vv[:], RS["VTOK"][c0:c0 + 128, :], reads=[RS["VTOK"]], writes=[vv])
                    cur.append((tl_, vv))
                for tl in range(128):
                    for d in range(2):
                        col = tl if d == 0 else 127 - tl
                        tl_, vv = cur[d]
                        S_, sa, pv, py = ST[d], psSA[d], psV[d], psY[d]
                        for pr in range(2):
                            for hp in range(2):
                                rows = slice(64 * hp, 64 * hp + 64)
                                kb.op("pe", lambda e, pr=pr, rows=rows: e.matmul(
                                    sa[rows, pr * 64:(pr + 1) * 64], tl_["AL"][rows, pr, col:col + 1].broadcast_to([64, 64]),
                                    S_[rows, pr, :], start=True, stop=True), reads=[tl_["AL"], S_], writes=[sa])
                        for pr in range(2):
                            for hp in range(2):
                                rows = slice(64 * hp, 64 * hp + 64)
                                h = 2 * pr + hp
                                kb.op("pe", lambda e, pr=pr, rows=rows, h=h: e.matmul(
                                    pv[rows, pr * 64:(pr + 1) * 64], ident_f[:, col:col + 1].broadcast_to([128, 64]),
                                    vv[:, h * 64:(h + 1) * 64], start=True, stop=True), reads=[ident_f, vv], writes=[pv])
                        for pr in range(2):
                            kb.op("dve", lambda e, pr=pr: e.tensor_scalar(
                                t1[d][:, pr, :], sa[:, pr * 64:(pr + 1) * 64], tl_["B"][:, pr, col:col + 1], None, ALU.mult),
                                reads=[sa, tl_["B"]], writes=[t1[d]])
                            kb.op("dve", lambda e, pr=pr: e.scalar_tensor_tensor(
                                t2[d][:, pr, :], pv[:, pr * 64:(pr + 1) * 64], tl_["KD"][:, pr, col:col + 1], t1[d][:, pr, :], ALU.mult, ALU.add),
                                reads=[pv, tl_["KD"], t1[d]], writes=[t2[d]])
                            kb.op("dve", lambda e, pr=pr: e.scalar_tensor_tensor(
                                S_[:, pr, :], S_[:, pr, :], tl_["W"][:, pr, col:col + 1], t2[d][:, pr, :], ALU.mult, ALU.add),
                                reads=[S_, tl_["W"], t2[d]], writes=[S_])
                        for pr in range(2):
                            for hp in range(2):
                                rows = slice(64 * hp, 64 * hp + 64)
                                h = 2 * pr + hp
                                kb.op("pe", lambda e, pr=pr, rows=rows, h=h: e.matmul(
                                    py[0:64, h * 128 + col:h * 128 + col + 1], S_[rows, pr, :], tl_["RT"][rows, pr, col:col + 1],
                                    start=True, stop=True), reads=[S_, tl_["RT"]], writes=[py])
                for d in range(2):
                    c0 = cidx[d] * 128
                    kb.op("act", lambda e, d=d: e.copy(ysb[d][:, :], psY[d][0:64, :]), reads=[psY[d]], writes=[ysb[d]])
                    dst = RS["YF" if d == 0 else "YB"]
                    kb.dma("sp", dst.t.rearrange("(h v) t -> v h t", v=64)[:, :, c0:c0 + 128],
                           ysb[d][:, :].rearrange("v (h t) -> v h t", h=4), reads=[ysb[d]], writes=[Buf()])


    def phase_rwkv_chunked(l):
        CH = 64
        NCH = T // CH
        with kb.scope():
            def ldc(nm, shape):
                t = kb.sb("k" + nm, shape)
                kb.dma("sp", t[:], CT[nm].t, reads=[CT[nm]], writes=[t])
                return t
            Ms = ldc("rw_ms", [128, 2, 64]); MTs = ldc("rw_mts", [128, 2, 64]); MTi = ldc("rw_mti", [128, 2, 64])
            id2 = ldc("rw_id2", [128, 64])
            ones = kb.sb("rones", [128, 64])
            kb.op("pool", lambda e: e.memset(ones[:], 1.0), writes=[ones])
            ST = kb.sb("cST", [128, 4, 64])
            kb.op("pool", lambda e: e.memset(ST[:], 0.0), writes=[ST])
            names = ("AL", "W", "B", "KD", "RT")
            def t4(nm, n=2, w=64):
                return [kb.sb(f"{nm}{i}", [128, 4, w]) for i in range(n)]
            IN = {n: t4("ci" + n) for n in names}
            VTK = t4("cVTK")
            CS = t4("cCS", 1)[0]; TOT = kb.sb("cTOT", [128, 4]); TMP = t4("cTMP", 1)[0]
            Epos = t4("cEp", 1)[0]; Eneg = t4("cEn", 1)[0]; Eprev = t4("cEv", 1)[0]; Etot = t4("cEt", 1)[0]; Wtot = kb.sb("cWt", [128, 4])
            Ab = t4("cAb", 1)[0]; Bb = t4("cBb", 1)[0]; Kb = t4("cKb", 1)[0]; Rb = t4("cRb", 1)[0]; Bt = t4("cBt", 1)[0]; Kt = t4("cKt", 1)[0]
            Q = t4("cQ"); P = t4("cP"); ArbT = t4("cArbT", 1)[0]; AkvT = t4("cAkvT", 1)[0]; ArkT = t4("cArkT", 1)[0]
            X = t4("cX", 2, 128); Btok = t4("cBtok", 1)[0]; Ktok = t4("cKtok", 1)[0]
            RAT = t4("cRAT", 1)[0]; McT = t4("cMcT", 1)[0]; NcS = t4("cNcS", 1)[0]; DG = t4("cDG", 1)[0]
            ysb = [kb.sb(f"cysb{d}", [64, 256]) for d in range(2)]
            border = [3, 2, 1, 0] + list(range(NCH - 1, 3, -1))
            DP = [(d, pr) for d in range(2) for pr in range(2)]
            HP = [slice(0, 64), slice(64, 128)]

            def mm_all(ps, col_fn, lhs_fn, rhs_fn, reads, start=True, stop=True, w=None):
                for dp in range(4):
                    for hp in range(2):
                        r = HP[hp]
                        c0, c1 = col_fn(dp)
                        kb.op("pe", lambda e, dp=dp, r=r, c0=c0, c1=c1: e.matmul(ps[r, c0:c1], lhs_fn(dp, r), rhs_fn(dp, r), start=start, stop=stop),
                              reads=reads, writes=[ps])

            for ci in range(NCH):
                cidx = [ci, border[ci]]
                i2 = ci % 2
                for d in range(2):
                    c0 = cidx[d] * CH
                    for n in names:
                        src = RS[n if n in ("AL", "RT") else f"{n}{d}"]
                        kb.dma("sp" if d == 0 else "pool", IN[n][i2][:, 2 * d:2 * d + 2, :],
                               src.t.rearrange("(pr q) t -> q pr t", q=128)[:, :, c0:c0 + CH], reads=[src], writes=[IN[n][i2]])
                    for hp in range(2):
                        kb.dma("sp" if d == 0 else "pool", VTK[i2][HP[hp], 2 * d:2 * d + 2, :],
                               RS["VTOK"][c0:c0 + CH, :].rearrange("t (pr hp v) -> t pr hp v", pr=2, hp=2)[:, :, hp, :],
                               reads=[RS["VTOK"]], writes=[VTK[i2]])
                al, lw, be, kd, rt, vt = IN["AL"][i2], IN["W"][i2], IN["B"][i2], IN["KD"][i2], IN["RT"][i2], VTK[i2]
                if RW_STAGE <= 1:
                    continue
                for dp in range(4):
                    kb.op("dve", lambda e, dp=dp: e.tensor_tensor_scan(CS[:, dp, :], ones[:, :], lw[:, dp, :], 0.0, ALU.mult, ALU.add),
                          reads=[ones, lw], writes=[CS])
                kb.op("dve", lambda e: e.tensor_copy(TOT[:, :], CS[:, :, CH - 1]), reads=[CS], writes=[TOT])
                kb.op("dve", lambda e: e.tensor_tensor(CS[:, 2:4, :], lw[:, 2:4, :], CS[:, 2:4, :], ALU.subtract), reads=[lw, CS], writes=[CS])
                kb.op("dve", lambda e: e.tensor_tensor(CS[:, 2:4, :], CS[:, 2:4, :], TOT[:, 2:4].unsqueeze(2).broadcast_to([128, 2, CH]), ALU.add),
                      reads=[CS, TOT], writes=[CS])
                kb.op("act", lambda e: e.activation(Epos[:], CS[:], AF.Exp), reads=[CS], writes=[Epos])
                kb.op("act", lambda e: e.activation(Eneg[:], CS[:], AF.Exp, scale=-1.0), reads=[CS], writes=[Eneg])
                kb.op("pool", lambda e: e.tensor_tensor(TMP[:], CS[:], lw[:], ALU.subtract), reads=[CS, lw], writes=[TMP])
                kb.op("act", lambda e: e.activation(Eprev[:], TMP[:], AF.Exp), reads=[TMP], writes=[Eprev])
                kb.op("dve", lambda e: e.tensor_tensor(Etot[:], TOT[:, :].unsqueeze(2).broadcast_to([128, 4, CH]), CS[:], ALU.subtract),
                      reads=[TOT, CS], writes=[Etot])
                kb.op("act", lambda e: e.activation(Etot[:], Etot[:], AF.Exp), reads=[Etot], writes=[Etot])
                kb.op("act", lambda e: e.activation(Wtot[:], TOT[:], AF.Exp), reads=[TOT], writes=[Wtot])
                kb.op("dve", lambda e: e.tensor_tensor(Ab[:], al[:], Eprev[:], ALU.mult), reads=[al, Eprev], writes=[Ab])
                kb.op("pool", lambda e: e.tensor_tensor(Bb[:], be[:], Eneg[:], ALU.mult), reads=[be, Eneg], writes=[Bb])
                kb.op("dve", lambda e: e.tensor_tensor(Kb[:], kd[:], Eneg[:], ALU.mult), reads=[kd, Eneg], writes=[Kb])
                kb.op("pool", lambda e: e.tensor_tensor(Rb[:], rt[:], Epos[:], ALU.mult), reads=[rt, Epos], writes=[Rb])
                kb.op("dve", lambda e: e.tensor_tensor(Bt[:], be[:], Etot[:], ALU.mult), reads=[be, Etot], writes=[Bt])
                kb.op("pool", lambda e: e.tensor_tensor(Kt[:], kd[:], Etot[:], ALU.mult), reads=[kd, Etot], writes=[Kt])
                if RW_STAGE <= 2:
                    continue
                PA, PB, PC, PT1, PD, PX, PPQ, PE_ = PS
                mm_all(PA, lambda dp: (dp * 128, dp * 128 + 64), lambda dp, r: Bb[r, dp, :], lambda dp, r: Ab[r, dp, :], [Bb, Ab])
                mm_all(PA, lambda dp: (dp * 128 + 64, dp * 128 + 128), lambda dp, r: Bb[r, dp, :], lambda dp, r: Rb[r, dp, :], [Bb, Rb])
                mm_all(PB, lambda dp: (dp * 128, dp * 128 + 64), lambda dp, r: Kb[r, dp, :], lambda dp, r: Ab[r, dp, :], [Kb, Ab])
                mm_all(PB, lambda dp: (dp * 128 + 64, dp * 128 + 128), lambda dp, r: Kb[r, dp, :], lambda dp, r: Rb[r, dp, :], [Kb, Rb])
                mm_all(PC, lambda dp: (dp * 64, dp * 64 + 64), lambda dp, r: Ab[r, dp, :], lambda dp, r: Bb[r, dp, :], [Ab, Bb])
                q0, p0 = Q[0], P[0]
                pav = PA[:, :].rearrange("p (dp x) -> p dp x", dp=4)
                pbv = PB[:, :].rearrange("p (dp x) -> p dp x", dp=4)
                def mk(m):
                    return m[:, :, :].unsqueeze(2).broadcast_to([128, 2, 2, 64])
                def v4(ap):
                    return ap.rearrange("p (d pr) x -> p d pr x", d=2)
                kb.op("dve", lambda e: e.tensor_tensor(v4(q0[:]), v4(pav[:, :, 0:64]), mk(MTs), ALU.mult), reads=[PA, MTs], writes=[q0])
                kb.op("dve", lambda e: e.tensor_tensor(v4(ArbT[:]), v4(pav[:, :, 64:128]), mk(MTi), ALU.mult), reads=[PA, MTi], writes=[ArbT])
                kb.op("dve", lambda e: e.tensor_tensor(v4(AkvT[:]), v4(pbv[:, :, 0:64]), mk(MTs), ALU.mult), reads=[PB, MTs], writes=[AkvT])
                kb.op("dve", lambda e: e.tensor_tensor(v4(ArkT[:]), v4(pbv[:, :, 64:128]), mk(MTi), ALU.mult), reads=[PB, MTi], writes=[ArkT])
                kb.op("dve", lambda e: e.tensor_tensor(v4(p0[:]), v4(PC[:, 0:256].rearrange("p (dp x) -> p dp x", dp=4)), mk(Ms), ALU.mult),
                      reads=[PC, Ms], writes=[p0])
                if RW_STAGE <= 3:
                    continue
                def idb(r):
                    return ident_f[r, r.start:r.start + 64]
                mm_all(PT1, lambda dp: (dp * 128, dp * 128 + 64), lambda dp, r: Ab[r, dp, :], lambda dp, r: idb(r), [Ab, ident_f])
                mm_all(PT1, lambda dp: (dp * 128 + 64, dp * 128 + 128), lambda dp, r: Bt[r, dp, :], lambda dp, r: idb(r), [Bt, ident_f])
                mm_all(PC, lambda dp: (256 + dp * 64, 256 + dp * 64 + 64), lambda dp, r: Kt[r, dp, :], lambda dp, r: idb(r), [Kt, ident_f])
                x0 = X[0]
                pt1v = PT1[:, :].rearrange("p (dp x) -> p dp x", dp=4)
                kb.op("act", lambda e: e.copy(x0[:, :, 0:64], pt1v[:, :, 0:64]), reads=[PT1], writes=[x0])
                kb.op("act", lambda e: e.copy(Btok[:], pt1v[:, :, 64:128]), reads=[PT1], writes=[Btok])
                kb.op("act", lambda e: e.copy(Ktok[:], PC[:, 256:512].rearrange("p (dp x) -> p dp x", dp=4)), reads=[PC], writes=[Ktok])
                if RW_STAGE <= 4:
                    continue
                mm_all(PD, lambda dp: (dp * 64, dp * 64 + 64), lambda dp, r: AkvT[r, dp, :], lambda dp, r: vt[r, dp, :], [AkvT, vt])
                kb.op("act", lambda e: e.copy(x0[:, :, 64:128], PD[:, 0:256].rearrange("p (dp x) -> p dp x", dp=4)), reads=[PD], writes=[x0])
                if RW_STAGE <= 5:
                    continue
                qc, pc, xc = Q[0], P[0], X[0]
                for j in range(6):
                    qn, pn, xn = Q[(j + 1) % 2], P[(j + 1) % 2], X[(j + 1) % 2]
                    mm_all(PX, lambda dp: (dp * 128, dp * 128 + 128), lambda dp, r: qc[r, dp, :], lambda dp, r: xc[r, dp, :], [qc, xc])
                    kb.op("dve", lambda e, xn=xn, xc=xc: e.tensor_tensor(xn[:], xc[:], PX[:, :].rearrange("p (dp x) -> p dp x", dp=4), ALU.add),
                          reads=[xc, PX], writes=[xn])
                    if j < 5:
                        mm_all(PPQ, lambda dp: (dp * 64, dp * 64 + 64), lambda dp, r: qc[r, dp, :], lambda dp, r: pc[r, dp, :], [qc, pc])
                        mm_all(PPQ, lambda dp: (256 + dp * 64, 256 + dp * 64 + 64), lambda dp, r: pc[r, dp, :], lambda dp, r: qc[r, dp, :], [qc, pc])
                        kb.op("act", lambda e, pn=pn: e.copy(pn[:], PPQ[:, 0:256].rearrange("p (dp x) -> p dp x", dp=4)), reads=[PPQ], writes=[pn])
                        kb.op("act", lambda e, qn=qn: e.copy(qn[:], PPQ[:, 256:512].rearrange("p (dp x) -> p dp x", dp=4)), reads=[PPQ], writes=[qn])
                    qc, pc, xc = qn, pn, xn
                if RW_STAGE <= 6:
                    continue
                mm_all(PD, lambda dp: (256 + dp * 64, 256 + dp * 64 + 64), lambda dp, r: xc[r, dp, 0:64], lambda dp, r: ArbT[r, dp, :], [xc, ArbT])
                kb.op("dve", lambda e: e.tensor_tensor(RAT[:], Rb[:], PD[:, 256:512].rearrange("p (dp x) -> p dp x", dp=4), ALU.add),
                      reads=[Rb, PD], writes=[RAT])
                mm_all(PE_, lambda dp: (dp * 64, dp * 64 + 64), lambda dp, r: xc[r, dp, 0:64], lambda dp, r: Btok[r, dp, :], [xc, Btok])
                kb.op("pool", lambda e: e.tensor_tensor(DG[:], id2[:, :].unsqueeze(1).broadcast_to([128, 4, 64]),
                                                        Wtot[:, :].unsqueeze(2).broadcast_to([128, 4, 64]), ALU.mult), reads=[id2, Wtot], writes=[DG])
                kb.op("dve", lambda e: e.tensor_tensor(McT[:], DG[:], PE_[:, 0:256].rearrange("p (dp x) -> p dp x", dp=4), ALU.add),
                      reads=[DG, PE_], writes=[McT])
                for dp in range(4):
                    for hp in range(2):
                        r = HP[hp]
                        c0 = 256 + dp * 64
                        kb.op("pe", lambda e, dp=dp, r=r, c0=c0: e.matmul(PE_[r, c0:c0 + 64], Btok[r, dp, :], xc[r, dp, 64:128], start=True, stop=False),
                              reads=[Btok, xc], writes=[PE_])
                        kb.op("pe", lambda e, dp=dp, r=r, c0=c0: e.matmul(PE_[r, c0:c0 + 64], Ktok[r, dp, :], vt[r, dp, :], start=False, stop=True),
                              reads=[Ktok, vt], writes=[PE_])
                kb.op("act", lambda e: e.copy(NcS[:], PE_[:, 256:512].rearrange("p (dp x) -> p dp x", dp=4)), reads=[PE_], writes=[NcS])
                if RW_STAGE <= 7:
                    continue
                PYs = [PA, PT1]
                for dp in range(4):
                    for hp in range(2):
                        r = HP[hp]
                        PY = PYs[hp]
                        c0 = dp * 64
                        kb.op("pe", lambda e, dp=dp, r=r, c0=c0, PY=PY: e.matmul(PY[0:64, c0:c0 + 64], ST[r, dp, :], RAT[r, dp, :], start=True, stop=False),
                              reads=[ST, RAT], writes=[PY])
                        kb.op("pe", lambda e, dp=dp, r=r, c0=c0, PY=PY: e.matmul(PY[0:64, c0:c0 + 64], xc[r, dp, 64:128], ArbT[r, dp, :], start=False, stop=False),
                              reads=[xc, ArbT], writes=[PY])
                        kb.op("pe", lambda e, dp=dp, r=r, c0=c0, PY=PY: e.matmul(PY[0:64, c0:c0 + 64], vt[r, dp, :], ArkT[r, dp, :], start=False, stop=True),
                              reads=[vt, ArkT], writes=[PY])
                for d in range(2):
                    c0 = cidx[d] * CH
                    yv = ysb[d][:, :].rearrange("v (pr hp t) -> v pr hp t", pr=2, hp=2)
                    for hp in range(2):
                        kb.op("act", lambda e, d=d, hp=hp, yv=yv: e.copy(
                            yv[:, :, hp, :], PYs[hp][0:64, d * 128:(d + 1) * 128].rearrange("v (pr t) -> v pr t", pr=2)), reads=[PYs[hp]], writes=[ysb[d]])
                    dst = RS["YF" if d == 0 else "YB"]
                    kb.dma("sp", dst.t.rearrange("(h v) t -> v h t", v=64)[:, :, c0:c0 + CH],
                           ysb[d][:, :].rearrange("v (h t) -> v h t", h=4), reads=[ysb[d]], writes=[Buf()])
                if RW_STAGE <= 8:
                    continue
                PSS = PB
                mm_all(PSS, lambda dp: (dp * 64, dp * 64 + 64), lambda dp, r: McT[r, dp, :], lambda dp, r: ST[r, dp, :], [McT, ST])
                kb.op("dve", lambda e: e.tensor_tensor(ST[:], NcS[:], PSS[:, 0:256].rearrange("p (dp x) -> p dp x", dp=4), ALU.add),
                      reads=[NcS, PSS], writes=[ST])


    def phase_rwkv_chunked3(l):
        CH = 64
        NCH = T // CH
        with kb.scope():
            def ldc(nm, shape):
                t = kb.sb("k" + nm, shape)
                kb.dma("sp", t[:], CT[nm].t, reads=[CT[nm]], writes=[t])
                return t
            Ms = ldc("rw_ms", [128, 2, 64]); MTs = ldc("rw_mts", [128, 2, 64]); MTi = ldc("rw_mti", [128, 2, 64])
            id2 = ldc("rw_id2", [128, 64])
            ones = kb.sb("rones", [128, 64])
            kb.op("pool", lambda e: e.memset(ones[:], 1.0), writes=[ones])
            ST = kb.sb("cST", [128, 4, 64])
            kb.op("pool", lambda e: e.memset(ST[:], 0.0), writes=[ST])
            names = ("AL", "W", "B", "KD", "RT")
            import types
            def alloc_set(si):
                S = types.SimpleNamespace()
                def t4(nm, n=2, w=64):
                    return [kb.sb(f"{nm}s{si}_{i}", [128, 4, w]) for i in range(n)]
                S.IN = {n: t4("ci" + n, 1)[0] for n in names}
                S.VTK = t4("cVTK", 1)[0]
                S.CS = t4("cCS", 1)[0]; S.TOT = kb.sb(f"cTOT{si}", [128, 4]); S.TMP = t4("cTMP", 1)[0]
                S.Epos = t4("cEp", 1)[0]; S.Eneg = t4("cEn", 1)[0]; S.Eprev = t4("cEv", 1)[0]; S.Etot = t4("cEt", 1)[0]; S.Wtot = kb.sb(f"cWt{si}", [128, 4])
                S.Ab = t4("cAb", 1)[0]; S.Bb = t4("cBb", 1)[0]; S.Kb = t4("cKb", 1)[0]; S.Rb = t4("cRb", 1)[0]; S.Bt = t4("cBt", 1)[0]; S.Kt = t4("cKt", 1)[0]
                S.Q = t4("cQ"); S.P = t4("cP"); S.ArbT = t4("cArbT", 1)[0]; S.AkvT = t4("cAkvT", 1)[0]; S.ArkT = t4("cArkT", 1)[0]
                S.X = t4("cX", 2, 128); S.Btok = t4("cBtok", 1)[0]; S.Ktok = t4("cKtok", 1)[0]
                S.RAT = t4("cRAT", 1)[0]; S.McT = t4("cMcT", 1)[0]; S.NcS = t4("cNcS", 1)[0]; S.DG = t4("cDG", 1)[0]
                S.ysb = [kb.sb(f"cysb{si}_{d}", [64, 256]) for d in range(2)]
                S.banks = PS[4 * si:4 * si + 4]
                return S
            SETS = [alloc_set(0), alloc_set(1)]
            border = [3, 2, 1, 0] + list(range(NCH - 1, 3, -1))
            DP = [(d, pr) for d in range(2) for pr in range(2)]
            HP = [slice(0, 64), slice(64, 128)]

            def mm_all(ps, col_fn, lhs_fn, rhs_fn, reads, start=True, stop=True, w=None):
                for dp in range(4):
                    for hp in range(2):
                        r = HP[hp]
                        c0, c1 = col_fn(dp)
                        kb.op("pe", lambda e, dp=dp, r=r, c0=c0, c1=c1: e.matmul(ps[r, c0:c1], lhs_fn(dp, r), rhs_fn(dp, r), start=start, stop=stop),
                              reads=reads, writes=[ps])

            def chunk_gen(ci, S):
                cidx = [ci, border[ci]]
                IN, VTK = S.IN, S.VTK
                for d in range(2):
                    c0 = cidx[d] * CH
                    for n in names:
                        src = RS[n if n in ("AL", "RT") else f"{n}{d}"]
                        kb.dma("sp" if d == 0 else "pool", IN[n][:, 2 * d:2 * d + 2, :],
                               src.t.rearrange("(pr q) t -> q pr t", q=128)[:, :, c0:c0 + CH], reads=[src], writes=[IN[n]])
                    for hp in range(2):
                        kb.dma("sp" if d == 0 else "pool", VTK[HP[hp], 2 * d:2 * d + 2, :],
                               RS["VTOK"][c0:c0 + CH, :].rearrange("t (pr hp v) -> t pr hp v", pr=2, hp=2)[:, :, hp, :],
                               reads=[RS["VTOK"]], writes=[VTK])
                al, lw, be, kd, rt, vt = IN["AL"], IN["W"], IN["B"], IN["KD"], IN["RT"], VTK
                CS, TOT, TMP, Epos, Eneg, Eprev, Etot, Wtot = S.CS, S.TOT, S.TMP, S.Epos, S.Eneg, S.Eprev, S.Etot, S.Wtot
                Ab, Bb, Kb, Rb, Bt, Kt, Q, P, ArbT, AkvT, ArkT = S.Ab, S.Bb, S.Kb, S.Rb, S.Bt, S.Kt, S.Q, S.P, S.ArbT, S.AkvT, S.ArkT
                X, Btok, Ktok, RAT, McT, NcS, DG, ysb = S.X, S.Btok, S.Ktok, S.RAT, S.McT, S.NcS, S.DG, S.ysb
                yield
                for dp in range(4):
                    kb.op("dve", lambda e, dp=dp: e.tensor_tensor_scan(CS[:, dp, :], ones[:, :], lw[:, dp, :], 0.0, ALU.mult, ALU.add),
                          reads=[ones, lw], writes=[CS])
                kb.op("dve", lambda e: e.tensor_copy(TOT[:, :], CS[:, :, CH - 1]), reads=[CS], writes=[TOT])
                kb.op("dve", lambda e: e.tensor_tensor(CS[:, 2:4, :], lw[:, 2:4, :], CS[:, 2:4, :], ALU.subtract), reads=[lw, CS], writes=[CS])
                kb.op("dve", lambda e: e.tensor_tensor(CS[:, 2:4, :], CS[:, 2:4, :], TOT[:, 2:4].unsqueeze(2).broadcast_to([128, 2, CH]), ALU.add),
                      reads=[CS, TOT], writes=[CS])
                kb.op("act", lambda e: e.activation(Epos[:], CS[:], AF.Exp), reads=[CS], writes=[Epos])
                kb.op("act", lambda e: e.activation(Eneg[:], CS[:], AF.Exp, scale=-1.0), reads=[CS], writes=[Eneg])
                kb.op("pool", lambda e: e.tensor_tensor(TMP[:], CS[:], lw[:], ALU.subtract), reads=[CS, lw], writes=[TMP])
                kb.op("act", lambda e: e.activation(Eprev[:], TMP[:], AF.Exp), reads=[TMP], writes=[Eprev])
                kb.op("dve", lambda e: e.tensor_tensor(Etot[:], TOT[:, :].unsqueeze(2).broadcast_to([128, 4, CH]), CS[:], ALU.subtract),
                      reads=[TOT, CS], writes=[Etot])
                kb.op("act", lambda e: e.activation(Etot[:], Etot[:], AF.Exp), reads=[Etot], writes=[Etot])
                kb.op("act", lambda e: e.activation(Wtot[:], TOT[:], AF.Exp), reads=[TOT], writes=[Wtot])
                kb.op("dve", lambda e: e.tensor_tensor(Ab[:], al[:], Eprev[:], ALU.mult), reads=[al, Eprev], writes=[Ab])
                kb.op("pool", lambda e: e.tensor_tensor(Bb[:], be[:], Eneg[:], ALU.mult), reads=[be, Eneg], writes=[Bb])
                kb.op("dve", lambda e: e.tensor_tensor(Kb[:], kd[:], Eneg[:], ALU.mult), reads=[kd, Eneg], writes=[Kb])
                kb.op("pool", lambda e: e.tensor_tensor(Rb[:], rt[:], Epos[:], ALU.mult), reads=[rt, Epos], writes=[Rb])
                kb.op("dve", lambda e: e.tensor_tensor(Bt[:], be[:], Etot[:], ALU.mult), reads=[be, Etot], writes=[Bt])
                kb.op("pool", lambda e: e.tensor_tensor(Kt[:], kd[:], Etot[:], ALU.mult), reads=[kd, Etot], writes=[Kt])
                yield
                PA, PB, PC, PT1 = S.banks
                PD, PX, PPQ, PE_ = PA, PB, PC, PT1
                mm_all(PA, lambda dp: (dp * 128, dp * 128 + 64), lambda dp, r: Bb[r, dp, :], lambda dp, r: Ab[r, dp, :], [Bb, Ab])
                mm_all(PA, lambda dp: (dp * 128 + 64, dp * 128 + 128), lambda dp, r: Bb[r, dp, :], lambda dp, r: Rb[r, dp, :], [Bb, Rb])
                mm_all(PB, lambda dp: (dp * 128, dp * 128 + 64), lambda dp, r: Kb[r, dp, :], lambda dp, r: Ab[r, dp, :], [Kb, Ab])
                mm_all(PB, lambda dp: (dp * 128 + 64, dp * 128 + 128), lambda dp, r: Kb[r, dp, :], lambda dp, r: Rb[r, dp, :], [Kb, Rb])
                mm_all(PC, lambda dp: (dp * 64, dp * 64 + 64), lambda dp, r: Ab[r, dp, :], lambda dp, r: Bb[r, dp, :], [Ab, Bb])
                q0, p0 = Q[0], P[0]
                pav = PA[:, :].rearrange("p (dp x) -> p dp x", dp=4)
                pbv = PB[:, :].rearrange("p (dp x) -> p dp x", dp=4)
                def mk(m):
                    return m[:, :, :].unsqueeze(2).broadcast_to([128, 2, 2, 64])
                def v4(ap):
                    return ap.rearrange("p (d pr) x -> p d pr x", d=2)
                kb.op("dve", lambda e: e.tensor_tensor(v4(q0[:]), v4(pav[:, :, 0:64]), mk(MTs), ALU.mult), reads=[PA, MTs], writes=[q0])
                kb.op("dve", lambda e: e.tensor_tensor(v4(ArbT[:]), v4(pav[:, :, 64:128]), mk(MTi), ALU.mult), reads=[PA, MTi], writes=[ArbT])
                kb.op("dve", lambda e: e.tensor_tensor(v4(AkvT[:]), v4(pbv[:, :, 0:64]), mk(MTs), ALU.mult), reads=[PB, MTs], writes=[AkvT])
                kb.op("dve", lambda e: e.tensor_tensor(v4(ArkT[:]), v4(pbv[:, :, 64:128]), mk(MTi), ALU.mult), reads=[PB, MTi], writes=[ArkT])
                kb.op("dve", lambda e: e.tensor_tensor(v4(p0[:]), v4(PC[:, 0:256].rearrange("p (dp x) -> p dp x", dp=4)), mk(Ms), ALU.mult),
                      reads=[PC, Ms], writes=[p0])
                yield
                def idb(r):
                    return ident_f[r, r.start:r.start + 64]
                mm_all(PT1, lambda dp: (dp * 128, dp * 128 + 64), lambda dp, r: Ab[r, dp, :], lambda dp, r: idb(r), [Ab, ident_f])
                mm_all(PT1, lambda dp: (dp * 128 + 64, dp * 128 + 128), lambda dp, r: Bt[r, dp, :], lambda dp, r: idb(r), [Bt, ident_f])
                mm_all(PC, lambda dp: (256 + dp * 64, 256 + dp * 64 + 64), lambda dp, r: Kt[r, dp, :], lambda dp, r: idb(r), [Kt, ident_f])
                x0 = X[0]
                pt1v = PT1[:, :].rearrange("p (dp x) -> p dp x", dp=4)
                kb.op("act", lambda e: e.copy(x0[:, :, 0:64], pt1v[:, :, 0:64]), reads=[PT1], writes=[x0])
                kb.op("act", lambda e: e.copy(Btok[:], pt1v[:, :, 64:128]), reads=[PT1], writes=[Btok])
                kb.op("act", lambda e: e.copy(Ktok[:], PC[:, 256:512].rearrange("p (dp x) -> p dp x", dp=4)), reads=[PC], writes=[Ktok])
                yield
                mm_all(PD, lambda dp: (dp * 64, dp * 64 + 64), lambda dp, r: AkvT[r, dp, :], lambda dp, r: vt[r, dp, :], [AkvT, vt])
                kb.op("act", lambda e: e.copy(x0[:, :, 64:128], PD[:, 0:256].rearrange("p (dp x) -> p dp x", dp=4)), reads=[PD], writes=[x0])
                yield
                qc, pc, xc = Q[0], P[0], X[0]
                for j in range(6):
                    qn, pn, xn = Q[(j + 1) % 2], P[(j + 1) % 2], X[(j + 1) % 2]
                    mm_all(PX, lambda dp: (dp * 128, dp * 128 + 128), lambda dp, r: qc[r, dp, :], lambda dp, r: xc[r, dp, :], [qc, xc])
                    kb.op("dve", lambda e, xn=xn, xc=xc: e.tensor_tensor(xn[:], xc[:], PX[:, :].rearrange("p (dp x) -> p dp x", dp=4), ALU.add),
                          reads=[xc, PX], writes=[xn])
                    if j < 5:
                        mm_all(PPQ, lambda dp: (dp * 64, dp * 64 + 64), lambda dp, r: qc[r, dp, :], lambda dp, r: pc[r, dp, :], [qc, pc])
                        mm_all(PPQ, lambda dp: (256 + dp * 64, 256 + dp * 64 + 64), lambda dp, r: pc[r, dp, :], lambda dp, r: qc[r, dp, :], [qc, pc])
                        kb.op("act", lambda e, pn=pn: e.copy(pn[:], PPQ[:, 0:256].rearrange("p (dp x) -> p dp x", dp=4)), reads=[PPQ], writes=[pn])
                        kb.op("act", lambda e, qn=qn: e.copy(qn[:], PPQ[:, 256:512].rearrange("p (dp x) -> p dp x", dp=4)), reads=[PPQ], writes=[qn])
                    qc, pc, xc = qn, pn, xn
                    yield
                yield
                mm_all(PD, lambda dp: (256 + dp * 64, 256 + dp * 64 + 64), lambda dp, r: xc[r, dp, 0:64], lambda dp, r: ArbT[r, dp, :], [xc, ArbT])
                kb.op("dve", lambda e: e.tensor_tensor(RAT[:], Rb[:], PD[:, 256:512].rearrange("p (dp x) -> p dp x", dp=4), ALU.add),
                      reads=[Rb, PD], writes=[RAT])
                mm_all(PE_, lambda dp: (dp * 64, dp * 64 + 64), lambda dp, r: xc[r, dp, 0:64], lambda dp, r: Btok[r, dp, :], [xc, Btok])
                kb.op("pool", lambda e: e.tensor_tensor(DG[:], id2[:, :].unsqueeze(1).broadcast_to([128, 4, 64]),
                                                        Wtot[:, :].unsqueeze(2).broadcast_to([128, 4, 64]), ALU.mult), reads=[id2, Wtot], writes=[DG])
                kb.op("dve", lambda e: e.tensor_tensor(McT[:], DG[:], PE_[:, 0:256].rearrange("p (dp x) -> p dp x", dp=4), ALU.add),
                      reads=[DG, PE_], writes=[McT])
                for dp in range(4):
                    for hp in range(2):
                        r = HP[hp]
                        c0 = 256 + dp * 64
                        kb.op("pe", lambda e, dp=dp, r=r, c0=c0: e.matmul(PE_[r, c0:c0 + 64], Btok[r, dp, :], xc[r, dp, 64:128], start=True, stop=False),
                              reads=[Btok, xc], writes=[PE_])
                        kb.op("pe", lambda e, dp=dp, r=r, c0=c0: e.matmul(PE_[r, c0:c0 + 64], Ktok[r, dp, :], vt[r, dp, :], start=False, stop=True),
                              reads=[Ktok, vt], writes=[PE_])
                kb.op("act", lambda e: e.copy(NcS[:], PE_[:, 256:512].rearrange("p (dp x) -> p dp x", dp=4)), reads=[PE_], writes=[NcS])
                yield
                PYs = [PA, PB]
                for dp in range(4):
                    for hp in range(2):
                        r = HP[hp]
                        PY = PYs[hp]
                        c0 = dp * 64
                        kb.op("pe", lambda e, dp=dp, r=r, c0=c0, PY=PY: e.matmul(PY[0:64, c0:c0 + 64], ST[r, dp, :], RAT[r, dp, :], start=True, stop=False),
                              reads=[ST, RAT], writes=[PY])
                        kb.op("pe", lambda e, dp=dp, r=r, c0=c0, PY=PY: e.matmul(PY[0:64, c0:c0 + 64], xc[r, dp, 64:128], ArbT[r, dp, :], start=False, stop=False),
                              reads=[xc, ArbT], writes=[PY])
                        kb.op("pe", lambda e, dp=dp, r=r, c0=c0, PY=PY: e.matmul(PY[0:64, c0:c0 + 64], vt[r, dp, :], ArkT[r, dp, :], start=False, stop=True),
                              reads=[vt, ArkT], writes=[PY])
                for d in range(2):
                    c0 = cidx[d] * CH
                    yv = ysb[d][:, :].rearrange("v (pr hp t) -> v pr hp t", pr=2, hp=2)
                    for hp in range(2):
                        kb.op("act", lambda e, d=d, hp=hp, yv=yv: e.copy(
                            yv[:, :, hp, :], PYs[hp][0:64, d * 128:(d + 1) * 128].rearrange("v (pr t) -> v pr t", pr=2)), reads=[PYs[hp]], writes=[ysb[d]])
                    dst = RS["YF" if d == 0 else "YB"]
                    kb.dma("sp", dst.t.rearrange("(h v) t -> v h t", v=64)[:, :, c0:c0 + CH],
                           ysb[d][:, :].rearrange("v (h t) -> v h t", h=4), reads=[ysb[d]], writes=[Buf()])
                PSS = PT1
                mm_all(PSS, lambda dp: (dp * 64, dp * 64 + 64), lambda dp, r: McT[r, dp, :], lambda dp, r: ST[r, dp, :], [McT, ST])
                kb.op("dve", lambda e: e.tensor_tensor(ST[:], NcS[:], PSS[:, 0:256].rearrange("p (dp x) -> p dp x", dp=4), ALU.add),
                      reads=[NcS, PSS], writes=[ST])


            def lockstep(gens):
                gens = list(gens)
                while gens:
                    nxt = []
                    for g_ in gens:
                        try:
                            next(g_)
                            nxt.append(g_)
                        except StopIteration:
                            pass
                    gens = nxt
            for ci in range(0, NCH, 2):
                lockstep([chunk_gen(ci, SETS[0]), chunk_gen(ci + 1, SETS[1])])

    def phase_rwkv_chunked2(l):
        CH = 64
        NCH = T // CH
        with kb.scope():
            def ldc(nm, shape):
                t = kb.sb("k" + nm, shape)
                kb.dma("sp", t[:], CT[nm].t, reads=[CT[nm]], writes=[t])
                return t
            MsB = ldc("rw_msb", [128, 2, 128]); MTsB = ldc("rw_mtsb", [128, 2, 128]); MTi = ldc("rw_mti", [128, 2, 64])
            identr = kb.sb("cidr", [128, 128], F32R)
            kb.op("dve", lambda e: e.tensor_copy(identr[:], ident_f[:]), reads=[ident_f], writes=[identr])
            ones = kb.sb("rones", [128, 64])
            kb.op("pool", lambda e: e.memset(ones[:], 1.0), writes=[ones])
            def bd(nm, n=1, dt=F32R):
                ts = [kb.sb(f"{nm}{i}", [128, 4, 128], dt) for i in range(n)]
                for t in ts:
                    kb.op("pool", lambda e, t=t: e.memset(t[:].bitcast(F32) if dt == F32R else t[:], 0.0), writes=[t])
                return ts
            def t4(nm, n=1, w=64, dt=F32):
                return [kb.sb(f"{nm}{i}", [128, 4, w], dt) for i in range(n)]
            f32 = lambda ap: ap.bitcast(F32)
            names = ("AL", "W", "B", "KD", "RT")
            IN = {n: t4("di" + n, 2) for n in names}
            VT = bd("dVT", 2, F32)
            VTr = bd("dVTr")[0]
            ST = bd("dST")[0]
            CS = t4("dCS")[0]; TOT = kb.sb("dTOT", [128, 4]); TMP = t4("dTMP")[0]
            Epos = t4("dEp")[0]; Eneg = t4("dEn")[0]; Eprev = t4("dEv")[0]; Etot = t4("dEt")[0]; Wtot = kb.sb("dWt", [128, 4])
            Ab = bd("dAb")[0]; Bb = bd("dBb")[0]; Kb = bd("dKb")[0]; Bt = bd("dBt")[0]; Kt = bd("dKt")[0]
            Rb = t4("dRb", 1, 64, F32R)[0]
            Q = bd("dQ", 2); P = bd("dP", 2); AkvT = bd("dAkvT")[0]
            ArbT = t4("dArbT", 1, 64, F32R)[0]; ArkT = t4("dArkT", 1, 64, F32R)[0]; RAT = t4("dRAT", 1, 64, F32R)[0]
            X = [kb.sb(f"dX{i}", [128, 4, 256], F32R) for i in range(2)]
            Btok = bd("dBtok")[0]; Ktok = bd("dKtok")[0]; McT = bd("dMcT")[0]
            NcS = bd("dNcS", 1, F32)[0]; DG = bd("dDG", 1, F32)[0]
            ysb = [kb.sb(f"dysb{d}", [128, 2, 64]) for d in range(2)]
            border = [3, 2, 1, 0] + list(range(NCH - 1, 3, -1))
            H0, H1 = slice(0, 64), slice(64, 128)
            B0, B1, B2, B3, B4, B5, B6, B7 = PS

            def mm4(ps, c0, w, lhs, rhs, reads, start=True, stop=True):
                for dp in range(4):
                    kb.op("pe", lambda e, dp=dp: e.matmul(ps[:, c0 + dp * w:c0 + (dp + 1) * w], lhs(dp), rhs(dp), start=start, stop=stop),
                          reads=reads, writes=[ps])

            def v4(ap):
                return ap.rearrange("p (d pr) x -> p d pr x", d=2)

            def mk(m, w):
                return m[:, :, :].unsqueeze(2).broadcast_to([128, 2, 2, w])

            def pv(ps, c0, w):
                return ps[:, c0:c0 + 4 * w].rearrange("p (dp x) -> p dp x", dp=4)

            for ci in range(NCH):
                cidx = [ci, border[ci]]
                i2 = ci % 2
                vt = VT[i2]
                for d in range(2):
                    c0 = cidx[d] * CH
                    q_ = "sp" if d == 0 else "pool"
                    for n in names:
                        src = RS[n if n in ("AL", "RT") else f"{n}{d}"]
                        kb.dma(q_, IN[n][i2][:, 2 * d:2 * d + 2, :],
                               src.t.rearrange("(pr q) t -> q pr t", q=128)[:, :, c0:c0 + CH], reads=[src], writes=[IN[n][i2]])
                    for hp in range(2):
                        kb.dma(q_, vt[hp * 64:(hp + 1) * 64, 2 * d:2 * d + 2, hp * 64:(hp + 1) * 64],
                               RS["VTOK"][c0:c0 + CH, :].rearrange("t (pr hp v) -> t pr hp v", pr=2, hp=2)[:, :, hp, :],
                               reads=[RS["VTOK"]], writes=[vt])
                al, lw, be, kd, rt = IN["AL"][i2], IN["W"][i2], IN["B"][i2], IN["KD"][i2], IN["RT"][i2]
                kb.op("act", lambda e: e.copy(VTr[:], vt[:]), reads=[vt], writes=[VTr])
                for dp in range(4):
                    kb.op("dve", lambda e, dp=dp: e.tensor_tensor_scan(CS[:, dp, :], ones[:, :], lw[:, dp, :], 0.0, ALU.mult, ALU.add),
                          reads=[ones, lw], writes=[CS])
                kb.op("dve", lambda e: e.tensor_copy(TOT[:, :], CS[:, :, CH - 1]), reads=[CS], writes=[TOT])
                kb.op("dve", lambda e: e.tensor_tensor(CS[:, 2:4, :], lw[:, 2:4, :], CS[:, 2:4, :], ALU.subtract), reads=[lw, CS], writes=[CS])
                kb.op("dve", lambda e: e.tensor_tensor(CS[:, 2:4, :], CS[:, 2:4, :], TOT[:, 2:4].unsqueeze(2).broadcast_to([128, 2, CH]), ALU.add),
                      reads=[CS, TOT], writes=[CS])
                kb.op("act", lambda e: e.activation(Epos[:], CS[:], AF.Exp), reads=[CS], writes=[Epos])
                kb.op("act", lambda e: e.activation(Eneg[:], CS[:], AF.Exp, scale=-1.0), reads=[CS], writes=[Eneg])
                kb.op("pool", lambda e: e.tensor_tensor(TMP[:], CS[:], lw[:], ALU.subtract), reads=[CS, lw], writes=[TMP])
                kb.op("act", lambda e: e.activation(Eprev[:], TMP[:], AF.Exp), reads=[TMP], writes=[Eprev])
                kb.op("pool", lambda e: e.tensor_tensor(Etot[:], TOT[:, :].unsqueeze(2).broadcast_to([128, 4, CH]), CS[:], ALU.subtract),
                      reads=[TOT, CS], writes=[Etot])
                kb.op("act", lambda e: e.activation(Etot[:], Etot[:], AF.Exp), reads=[Etot], writes=[Etot])
                kb.op("act", lambda e: e.activation(Wtot[:], TOT[:], AF.Exp), reads=[TOT], writes=[Wtot])
                for k_, (dst, a_, b_) in enumerate(((Ab, al, Eprev), (Bb, be, Eneg), (Kb, kd, Eneg), (Bt, be, Etot), (Kt, kd, Etot))):
                    for hi, r in enumerate((H0, H1)):
                        eng = "dve" if (k_ + hi) % 2 == 0 else "pool"
                        kb.op(eng, lambda e, dst=dst, a_=a_, b_=b_, r=r: e.tensor_tensor(dst[r, :, r.start:r.start + 64], a_[r, :, :], b_[r, :, :], ALU.mult),
                              reads=[a_, b_], writes=[dst])
                kb.op("pool", lambda e: e.tensor_tensor(Rb[:], rt[:], Epos[:], ALU.mult), reads=[rt, Epos], writes=[Rb])
                mm4(B0, 0, 128, lambda dp: Bb[:, dp, :], lambda dp: Ab[:, dp, :], [Bb, Ab])
                mm4(B1, 0, 128, lambda dp: Kb[:, dp, :], lambda dp: Ab[:, dp, :], [Kb, Ab])
                mm4(B2, 0, 128, lambda dp: Ab[:, dp, :], lambda dp: Bb[:, dp, :], [Ab, Bb])
                mm4(B3, 0, 64, lambda dp: Bb[:, dp, :], lambda dp: Rb[:, dp, :], [Bb, Rb])
                mm4(B3, 256, 64, lambda dp: Kb[:, dp, :], lambda dp: Rb[:, dp, :], [Kb, Rb])
                q0, p0, x0 = Q[0], P[0], X[0]
                kb.op("dve", lambda e: e.tensor_tensor(v4(q0[:]), v4(pv(B0, 0, 128)), mk(MTsB, 128), ALU.mult), reads=[B0, MTsB], writes=[q0])
                kb.op("dve", lambda e: e.tensor_tensor(v4(AkvT[:]), v4(pv(B1, 0, 128)), mk(MTsB, 128), ALU.mult), reads=[B1, MTsB], writes=[AkvT])
                kb.op("dve", lambda e: e.tensor_tensor(v4(p0[:]), v4(pv(B2, 0, 128)), mk(MsB, 128), ALU.mult), reads=[B2, MsB], writes=[p0])
                kb.op("dve", lambda e: e.tensor_tensor(v4(ArbT[:]), v4(pv(B3, 0, 64)), mk(MTi, 64), ALU.mult), reads=[B3, MTi], writes=[ArbT])
                kb.op("dve", lambda e: e.tensor_tensor(v4(ArkT[:]), v4(pv(B3, 256, 64)), mk(MTi, 64), ALU.mult), reads=[B3, MTi], writes=[ArkT])
                mm4(B4, 0, 128, lambda dp: Ab[:, dp, :], lambda dp: identr[:, :], [Ab, identr])
                mm4(B6, 0, 128, lambda dp: Bt[:, dp, :], lambda dp: identr[:, :], [Bt, identr])
                mm4(B7, 0, 128, lambda dp: Kt[:, dp, :], lambda dp: identr[:, :], [Kt, identr])
                mm4(B5, 0, 128, lambda dp: AkvT[:, dp, :], lambda dp: VTr[:, dp, :], [AkvT, VTr])
                kb.op("act", lambda e: e.copy(x0[:, :, 0:128], pv(B4, 0, 128)), reads=[B4], writes=[x0])
                kb.op("act", lambda e: e.copy(Btok[:], pv(B6, 0, 128)), reads=[B6], writes=[Btok])
                kb.op("act", lambda e: e.copy(Ktok[:], pv(B7, 0, 128)), reads=[B7], writes=[Ktok])
                kb.op("act", lambda e: e.copy(x0[:, :, 128:256], pv(B5, 0, 128)), reads=[B5], writes=[x0])
                qc, pc, xc = Q[0], P[0], X[0]
                for j in range(6):
                    qn, pn, xn = Q[(j + 1) % 2], P[(j + 1) % 2], X[(j + 1) % 2]
                    for hf, bank in ((0, B4), (1, B5)):
                        for dq in range(2):
                            dp = hf * 2 + dq
                            kb.op("pe", lambda e, dp=dp, dq=dq, bank=bank: e.matmul(bank[:, dq * 256:(dq + 1) * 256], qc[:, dp, :], xc[:, dp, :],
                                                                                    start=True, stop=True), reads=[qc, xc], writes=[bank])
                        kb.op("dve", lambda e, hf=hf, bank=bank, xn=xn, xc=xc: e.tensor_tensor(
                            xn[:, 2 * hf:2 * hf + 2, :], f32(xc[:, 2 * hf:2 * hf + 2, :]), bank[:, :].rearrange("p (dq x) -> p dq x", dq=2), ALU.add),
                            reads=[xc, bank], writes=[xn])
                    if j < 5:
                        mm4(B6, 0, 128, lambda dp: qc[:, dp, :], lambda dp: pc[:, dp, :], [qc, pc])
                        mm4(B7, 0, 128, lambda dp: pc[:, dp, :], lambda dp: qc[:, dp, :], [qc, pc])
                        kb.op("act", lambda e, pn=pn: e.copy(pn[:], pv(B6, 0, 128)), reads=[B6], writes=[pn])
                        kb.op("act", lambda e, qn=qn: e.copy(qn[:], pv(B7, 0, 128)), reads=[B7], writes=[qn])
                    qc, pc, xc = qn, pn, xn
                mm4(B3, 0, 64, lambda dp: xc[:, dp, 0:128], lambda dp: ArbT[:, dp, :], [xc, ArbT])
                kb.op("dve", lambda e: e.tensor_tensor(RAT[:], f32(Rb[:]), pv(B3, 0, 64), ALU.add), reads=[Rb, B3], writes=[RAT])
                mm4(B2, 0, 128, lambda dp: xc[:, dp, 0:128], lambda dp: Btok[:, dp, :], [xc, Btok])
                kb.op("pool", lambda e: e.tensor_tensor(DG[:], ident_f[:, :].unsqueeze(1).broadcast_to([128, 4, 128]),
                                                        Wtot[:, :].unsqueeze(2).broadcast_to([128, 4, 128]), ALU.mult), reads=[ident_f, Wtot], writes=[DG])
                kb.op("dve", lambda e: e.tensor_tensor(McT[:], DG[:], pv(B2, 0, 128), ALU.add), reads=[DG, B2], writes=[McT])
                for dp in range(4):
                    kb.op("pe", lambda e, dp=dp: e.matmul(B0[:, dp * 128:(dp + 1) * 128], Btok[:, dp, :], xc[:, dp, 128:256], start=True, stop=False),
                          reads=[Btok, xc], writes=[B0])
                    kb.op("pe", lambda e, dp=dp: e.matmul(B0[:, dp * 128:(dp + 1) * 128], Ktok[:, dp, :], VTr[:, dp, :], start=False, stop=True),
                          reads=[Ktok, VTr], writes=[B0])
                kb.op("act", lambda e: e.copy(NcS[:], pv(B0, 0, 128)), reads=[B0], writes=[NcS])
                for dp in range(4):
                    c0 = dp * 64
                    kb.op("pe", lambda e, dp=dp, c0=c0: e.matmul(B1[:, c0:c0 + 64], ST[:, dp, :], RAT[:, dp, :], start=True, stop=False),
                          reads=[ST, RAT], writes=[B1])
                    kb.op("pe", lambda e, dp=dp, c0=c0: e.matmul(B1[:, c0:c0 + 64], xc[:, dp, 128:256], ArbT[:, dp, :], start=False, stop=False),
                          reads=[xc, ArbT], writes=[B1])
                    kb.op("pe", lambda e, dp=dp, c0=c0: e.matmul(B1[:, c0:c0 + 64], VTr[:, dp, :], ArkT[:, dp, :], start=False, stop=True),
                          reads=[VTr, ArkT], writes=[B1])
                for d in range(2):
                    c0 = cidx[d] * CH
                    kb.op("act", lambda e, d=d: e.copy(ysb[d][:, :, :], B1[:, d * 128:(d + 1) * 128].rearrange("p (pr t) -> p pr t", pr=2)),
                          reads=[B1], writes=[ysb[d]])
                    dst = RS["YF" if d == 0 else "YB"]
                    kb.dma("sp", dst.t.rearrange("(pr q) t -> q pr t", q=128)[:, :, c0:c0 + CH], ysb[d][:, :, :], reads=[ysb[d]], writes=[Buf()])
                mm4(B6, 0, 128, lambda dp: McT[:, dp, :], lambda dp: ST[:, dp, :], [McT, ST])
                kb.op("dve", lambda e: e.tensor_tensor(ST[:], NcS[:], pv(B6, 0, 128), ALU.add), reads=[NcS, B6], writes=[ST])

    def phase_rwkv_out(l, with_ctx):
        with kb.scope():
            rk_ = colvec("rrk", W["rw_r_k"][l, :], W["rw_r_k"], [128, 2], "(j p) -> p j", p=128)
            lg_ = colvec("rlg", W["rw_ln_g"][l, :], W["rw_ln_g"], [128, 2], "(j p) -> p j", p=128)
            lb_ = colvec("rlb", W["rw_ln_b"][l, :], W["rw_ln_b"], [128, 2], "(j p) -> p j", p=128)
            nm = ("YF", "YB", "RT", "KD0", "KD1", "VT")
            tl = [{n: kb.sb(f"o{n}{i}", [128, 512]) for n in nm} for i in range(2)]
            sg = [kb.sb(f"osg{i}", [128, 512], BF16) for i in range(2)]
            ob = [kb.sb(f"oob{i}", [128, 512], BF16) for i in range(2)]
            wk = [[kb.sb(f"owk{k}{i}", [128, 512]) for k in range(3)] for i in range(2)]
            it = 0
            for pr in range(2):
                rows = slice(pr * 128, (pr + 1) * 128)
                for (t0, nt) in TCH:
                    if not with_ctx and t0 + nt <= C:
                        continue
                    t_, s_, o_, (a_, b_, c_) = tl[it % 2], sg[it % 2], ob[it % 2], wk[it % 2]
                    for k, n in enumerate(nm):
                        kb.dma("sp" if k % 2 == 0 else "pool", t_[n][:, 0:nt], RS[n][rows, t0:t0 + nt], reads=[RS[n]], writes=[t_[n]])
                    kb.dma("sp", s_[:, 0:nt], RS["SGT"][rows, t0:t0 + nt], reads=[RS["SGT"]], writes=[s_])
                    y = t_["YF"]
                    kb.op("dve", lambda e: e.tensor_tensor(y[:, 0:nt], y[:, 0:nt], t_["YB"][:, 0:nt], ALU.add), reads=[y, t_["YB"]], writes=[y])
                    p1, p2, p3 = PS[(3 * it) % 8], PS[(3 * it + 1) % 8], PS[(3 * it + 2) % 8]
                    kb.op("pe", lambda e: e.matmul(p1[:, 0:nt], blk64[:], y[:, 0:nt], start=True, stop=True), reads=[blk64, y], writes=[p1])
                    kb.op("dve", lambda e: e.scalar_tensor_tensor(a_[:, 0:nt], p1[:, 0:nt], -1.0 / 64, y[:, 0:nt], ALU.mult, ALU.add),
                          reads=[p1, y], writes=[a_])
                    kb.op("act", lambda e: e.activation(b_[:, 0:nt], a_[:, 0:nt], AF.Square), reads=[a_], writes=[b_])
                    kb.op("pe", lambda e: e.matmul(p2[:, 0:nt], blk64[:], b_[:, 0:nt], start=True, stop=True), reads=[blk64, b_], writes=[p2])
                    kb.op("dve", lambda e: e.tensor_scalar(b_[:, 0:nt], p2[:, 0:nt], 1.0 / 64, 64e-5, ALU.mult, ALU.add), reads=[p2], writes=[b_])
                    kb.op("act", lambda e: e.sqrt(b_[:, 0:nt], b_[:, 0:nt]), reads=[b_], writes=[b_])
                    kb.op("dve", lambda e: e.reciprocal(b_[:, 0:nt], b_[:, 0:nt]), reads=[b_], writes=[b_])
                    kb.op("dve", lambda e: e.tensor_tensor(a_[:, 0:nt], a_[:, 0:nt], b_[:, 0:nt], ALU.mult), reads=[a_, b_], writes=[a_])
                    kb.op("dve", lambda e: e.tensor_scalar(a_[:, 0:nt], a_[:, 0:nt], lg_[:, pr:pr + 1], lb_[:, pr:pr + 1], ALU.mult, ALU.add),
                          reads=[a_, lg_, lb_], writes=[a_])
                    kb.op("pool", lambda e: e.tensor_tensor(c_[:, 0:nt], t_["KD0"][:, 0:nt], t_["KD1"][:, 0:nt], ALU.add),
                          reads=[t_["KD0"], t_["KD1"]], writes=[c_])
                    kb.op("dve", lambda e: e.scalar_tensor_tensor(c_[:, 0:nt], t_["RT"][:, 0:nt], rk_[:, pr:pr + 1], c_[:, 0:nt], ALU.mult, ALU.mult),
                          reads=[t_["RT"], rk_, c_], writes=[c_])
                    kb.op("pe", lambda e: e.matmul(p3[:, 0:nt], blk64[:], c_[:, 0:nt], start=True, stop=True), reads=[blk64, c_], writes=[p3])
                    kb.op("dve", lambda e: e.tensor_tensor(c_[:, 0:nt], p3[:, 0:nt], t_["VT"][:, 0:nt], ALU.mult), reads=[p3, t_["VT"]], writes=[c_])
                    kb.op("dve", lambda e: e.tensor_tensor(a_[:, 0:nt], a_[:, 0:nt], c_[:, 0:nt], ALU.add), reads=[a_, c_], writes=[a_])
                    kb.op("pool", lambda e: e.tensor_tensor(o_[:, 0:nt], a_[:, 0:nt], s_[:, 0:nt], ALU.mult), reads=[a_, s_], writes=[o_])
                    kb.dma("sp", mixT[256 + pr * 128:256 + (pr + 1) * 128, t0:t0 + nt], o_[:, 0:nt], reads=[o_], writes=[Buf()])
                    it += 1


    SEGS = {"L": dict(Ls=L, A=32, cbw=32, off=C, ut="UTL"), "C": dict(Ls=C, A=2, cbw=64, off=0, ut="UTC")}

    def phase_hyena_prep(l, with_ctx):
        with kb.scope():
            stage = kb.sb("hstage", [128, 8, 128])
            wts = [kb.sb(f"hwt{i}", [128, 8, 128], BF16) for i in range(2)]
            cw = kb.sb("hcw", [128, 6, 3])
            for k in range(3):
                kb.dma("sp", cw[:, :, k], W["hy_conv"][l, k, :].rearrange("(j p) -> p j", p=128), reads=[W["hy_conv"]], writes=[cw], slow=True)
            ncw = kb.sb("hncw", [128, 6, 3])
            kb.op("dve", lambda e: e.tensor_scalar(ncw[:], cw[:], -1.0, None, ALU.mult), reads=[cw], writes=[ncw])
            Zraw = kb.sb("hZraw", [128, T + 2])
            Zout = kb.sb("hZout", [128, T])
            kb.op("pool", lambda e: e.memset(Zraw[:, 0:1], 0.0), writes=[Zraw])
            kb.op("pool", lambda e: e.memset(Zraw[:, T + 1:T + 2], 0.0), writes=[Zraw])
            ub = kb.sb("hub", [128, 32 * 128])
            tG = [kb.sb(f"htG{i}", [128, 512], BF16) for i in range(2)]
            for oi, jt in enumerate(range(8)):
                wt = wts[oi % 2]
                c0 = HY0 + jt * 128 if jt < 6 else HYG0 + (jt - 6) * 128
                load_w(l, wt, c0, 128, stage)
                if jt >= 6:
                    for ci, (t0, nt) in enumerate(TCH):
                        p = PS[ci % 4]
                        proj_fm(p, wt, 0, 128, t0, nt)
                        g = tG[ci % 2]
                        kb.op("act", lambda e, p=p, g=g, nt=nt: e.activation(g[:, 0:nt], p[:, 0:nt], AF.Silu), reads=[p], writes=[g])
                        kb.dma("sp", HS["SG"][(jt - 6) * 128:(jt - 5) * 128, t0:t0 + nt], g[:, 0:nt], reads=[g], writes=[Buf()])
                    continue
                conv_tile(l, wt, cw, ncw, jt, Zraw, Zout)
                arr, half = jt // 2, jt % 2
                for sn in (("L", "C") if with_ctx else ("L",)):
                    sg = SEGS[sn]
                    A, cbw, off = sg["A"], sg["cbw"], sg["off"]
                    G = 128 // A
                    ncg = 128 // G
                    ubv = ub[:, 0:A * 128].rearrange("p (g a c) -> p g a c", g=ncg, a=A)
                    for a in range(A):
                        p = PS[4 + (a // 4) % 4]
                        kb.op("pe", lambda e, p=p, a=a, A=A, off=off: e.transpose(
                            p[:, (a % 4) * 128:(a % 4 + 1) * 128], Zout[:, off + a:off + a + 127 * A + 1:A], ident_f[:]),
                            reads=[Zout, ident_f], writes=[p])
                        if a % 4 == 3 or a == A - 1:
                            a0 = (a // 4) * 4
                            na = a - a0 + 1
                            kb.op("act", lambda e, p=p, a0=a0, na=na, G=G: e.copy(
                                ubv[:, :, a0:a0 + na, :], p[:, 0:na * 128].rearrange("p (a g c) -> p g a c", a=na, c=G)), reads=[p], writes=[ub])
                    nb = 128 // cbw
                    bsz = A * cbw
                    for b in range(nb):
                        dst = HS[sg["ut"]][arr, half * nb + b, :, :]
                        kb.dma("sp" if b % 2 == 0 else "pool", dst, ub[:, b * bsz:(b + 1) * bsz], reads=[ub], writes=[Buf()])

    def cmul(dre, dim_, sre, sim, tre, tim, conj, srcb, tabb, dstb, tmp):
        t1, t2 = tmp
        sh = tuple(slice(None) for _ in range(1))
        kb.op("dve", lambda e: e.tensor_tensor(t1, sre, tre, ALU.mult), reads=srcb + tabb, writes=[dstb[2]])
        kb.op("dve", lambda e: e.tensor_tensor(t2, sim, tim, ALU.mult), reads=srcb + tabb, writes=[dstb[3]])
        kb.op("pool", lambda e: e.tensor_tensor(dre, t1, t2, ALU.add if conj else ALU.subtract), reads=[dstb[2], dstb[3]], writes=[dstb[0]])
        kb.op("dve", lambda e: e.tensor_tensor(t1, sim, tre, ALU.mult), reads=srcb + tabb + [dstb[0]], writes=[dstb[2]])
        kb.op("dve", lambda e: e.tensor_tensor(t2, sre, tim, ALU.mult), reads=srcb + tabb + [dstb[0]], writes=[dstb[3]])
        kb.op("pool", lambda e: e.tensor_tensor(dim_, t1, t2, ALU.subtract if conj else ALU.add), reads=[dstb[2], dstb[3]], writes=[dstb[1]])

    def phase_hyena_main(l, with_ctx):
        PI = math.pi
        with kb.scope():
            fw1 = kb.sb("hfw1", [33, 64])
            fw2 = kb.sb("hfw2", [64, 64])
            fw3 = kb.sb("hfw3", [64, 1024])
            kb.dma("sp", fw1[:], W["hy_fw1"][l, :, :], reads=[W["hy_fw1"]], writes=[fw1])
            kb.dma("sp", fw2[:], W["hy_fw2"][l, :, :], reads=[W["hy_fw2"]], writes=[fw2])
            kb.dma("sp", fw3[:], W["hy_fw3"][l, :, :], reads=[W["hy_fw3"]], writes=[fw3])
            fb1 = colvec("hfb1", W["hy_fb1"][l, :], W["hy_fb1"], [64, 1], "(d o) -> d o", o=1)
            fb2 = colvec("hfb2", W["hy_fb2"][l, :], W["hy_fb2"], [64, 1], "(d o) -> d o", o=1)
            frq = colvec("hfrq", W["hy_freq"][l, :], W["hy_freq"], [64, 1], "(d o) -> d o", o=1)
            brow = kb.sb("hbrow", [1, 512])
            kb.dma("sp", brow[:], W["hy_bias"][l, :, :].rearrange("o c -> (o c)").rearrange("(x n) -> x n", x=1), reads=[W["hy_bias"]], writes=[brow])
            for sn in (("L", "C") if with_ctx else ("L",)):
                sg = SEGS[sn]
                Ls, A, cbw, off = sg["Ls"], sg["A"], sg["cbw"], sg["off"]
                G = 128 // A
                N = 2 * Ls
                ngr = cbw // G
                nblk = 256 // cbw
                pre = f"hy{sn}_"
                with kb.scope():
                    def ld(nm, shape):
                        t = kb.sb("k" + nm, shape)
                        src = CT[pre + nm]
                        kb.dma("sp", t[:], src.t, reads=[src], writes=[t])
                        return t
                    def ldr(nm, shape):
                        tr = kb.sb("r" + nm, shape, F32R)
                        with kb.scope():
                            t32 = ld(nm, shape)
                            kb.op("dve", lambda e: e.tensor_copy(tr[:], t32[:]), reads=[t32], writes=[tr])
                        return tr
                    F256 = ldr("F256", [128, 2, 512]); TWC = ld("TWC", [128, 256]); TWS = ld("TWS", [128, 256])
                    Dre = ldr("Dre", [128, 128]); Dim = ldr("Dim", [128, 128]); nDim = ldr("nDim", [128, 128])
                    E1 = ldr("E1", [128, 256]); E2 = ldr("E2", [128, 256])
                    TW2C = ld("TW2C", [128, 2, 128]); TW2S = ld("TW2S", [128, 2, 128])
                    IC = ldr("IC", [128, 2, 128]); IS = ldr("IS", [128, 2, 128])
                    h2T = kb.sb("h2T", [64, N])
                    with kb.scope():
                        zT = kb.sb("zT", [33, N])
                        kb.dma("sp", zT[:], CT[pre + "zT"].t, reads=[CT[pre + "zT"]], writes=[zT])
                        h1T = kb.sb("h1T", [64, N])
                        arg = [kb.sb(f"harg{i}", [64, 512]) for i in range(2)]
                        wr = [kb.sb(f"hwr{i}", [64, 512]) for i in range(2)]
                        for (src, K_, wgt, bcol, dst) in ((zT, 33, fw1, fb1, h1T), (h1T, 64, fw2, fb2, h2T)):
                            for ci, n0 in enumerate(range(0, N, 512)):
                                p = PS[ci % 4]
                                ag = arg[ci % 2]
                                kb.op("pe", lambda e: e.matmul(p[0:64, :], wgt[0:K_, :], src[0:K_, n0:n0 + 512], start=True, stop=True),
                                      reads=[wgt, src], writes=[p])
                                kb.op("dve", lambda e: e.tensor_scalar(ag[:, :], p[0:64, :], bcol[:, 0:1], frq[:, 0:1], ALU.add, ALU.mult),
                                      reads=[p, bcol, frq], writes=[ag])
                                for _w in range(2):
                                    kb.op("dve", lambda e: e.tensor_scalar(wr[0][:, :], ag[:, :], PI, -2 * PI, ALU.is_gt, ALU.mult), reads=[ag], writes=[wr[0]])
                                    kb.op("dve", lambda e: e.tensor_scalar(wr[1][:, :], ag[:, :], -PI, 2 * PI, ALU.is_lt, ALU.mult), reads=[ag], writes=[wr[1]])
                                    kb.op("dve", lambda e: e.tensor_tensor(ag[:, :], ag[:, :], wr[0][:, :], ALU.add), reads=[ag, wr[0]], writes=[ag])
                                    kb.op("dve", lambda e: e.tensor_tensor(ag[:, :], ag[:, :], wr[1][:, :], ALU.add), reads=[ag, wr[1]], writes=[ag])
                                kb.op("act", lambda e: e.activation(dst[:, n0:n0 + 512], ag[:, :], AF.Sin), reads=[ag], writes=[dst])
                    KT = [kb.sb(f"KT{o}", [128, 2, ngr, A, G]) for o in range(2)]
                    KTr = [kb.sb(f"KTr{o}", [128, 2, ngr, A, G], F32R) for o in range(2)]
                    uvr = kb.sb("huvr", [128, ngr, A * G], F32R)
                    KS = [kb.sb(f"KS{o}", [128, ngr, 512]) for o in range(2)]
                    DECt = kb.sb("DECt", [128, 2, ngr, A, G])
                    part = kb.sb("hpart", [128, cbw])
                    rn = kb.sb("hrn", [128, cbw])
                    ex = kb.sb("hex", [1, cbw])
                    uv = kb.sb("huv", [128, ngr, A * G]); x1 = kb.sb("hx1", [128, ngr, A * G]); x2 = kb.sb("hx2", [128, ngr, A * G])
                    u2 = kb.sb("hu2", [128, ngr, A * G], F32R); res = kb.sb("hres", [128, A, cbw])
                    dts = (F32R, F32R, F32, F32)
                    BpS = [[kb.sb(f"hBp{b}{i}", [128, 256], dts[i]) for i in range(4)] for b in range(2)]
                    BpbS = [[Buf() for _ in range(4)] for b in range(2)]
                    YpS = [[kb.sb(f"hYp{b}{i}", [128, 256], dts[i]) for i in range(4)] for b in range(2)]
                    YpbS = [[Buf() for _ in range(4)] for b in range(2)]
                    GpS = [[kb.sb(f"hGp{b}{i}", [128, 2, 128], dts[i]) for i in range(4)] for b in range(2)]
                    GpbS = [[Buf() for _ in range(4)] for b in range(2)]
                    fctr = [0]
                    Fm = kb.sb("hFm", [cbw, Ls])
                    sgm = kb.sb("hsgm", [cbw, Ls], BF16)

                    def fwd_fft(lhs_chunks, lhs_bufs, psB, psX):
                        n = len(lhs_chunks)
                        fctr[0] += 1
                        Bp, Bpb = BpS[fctr[0] % 2], BpbS[fctr[0] % 2]
                        for i, (ap, hf) in enumerate(lhs_chunks):
                            kb.op("pe", lambda e, ap=ap, hf=hf, i=i: e.matmul(psB[:, :], ap, F256[:, hf, :], start=(i == 0), stop=(i == n - 1)),
                                  reads=lhs_bufs + [F256], writes=[psB])
                        yield
                        cmul(Bp[0][:, :], Bp[1][:, :], psB[:, 0:256], psB[:, 256:512], TWC[:, :], TWS[:, :], True,
                             [psB], [TWC, TWS], Bpb, (Bp[2][:, :], Bp[3][:, :]))
                        yield
                        kb.op("pe", lambda e: e.matmul(psX[:, 0:256], Dre[:, :], Bp[0][:, :], start=True, stop=False), reads=[Dre, Bpb[0]], writes=[psX])
                        kb.op("pe", lambda e: e.matmul(psX[:, 0:256], nDim[:, :], Bp[1][:, :], start=False, stop=True), reads=[nDim, Bpb[1]], writes=[psX])
                        kb.op("pe", lambda e: e.matmul(psX[:, 256:512], Dim[:, :], Bp[0][:, :], start=True, stop=False), reads=[Dim, Bpb[0]], writes=[psX])
                        kb.op("pe", lambda e: e.matmul(psX[:, 256:512], Dre[:, :], Bp[1][:, :], start=False, stop=True), reads=[Dre, Bpb[1]], writes=[psX])

                    def conv_group(src, src_b, g, o, mulv, mul_b, dst_ap, dst_b, it):
                        psB, psX, psG, psy = PS[it % 2], PS[2 + it % 2], PS[4 + it % 2], PS[6 + it % 2]
                        Yp, Ypb, Gp, Gpb = YpS[it % 2], YpbS[it % 2], GpS[it % 2], GpbS[it % 2]
                        yield from fwd_fft([(src[:, g, :], 0)], [src_b], psB, psX)
                        yield
                        cmul(Yp[0][:, :], Yp[1][:, :], psX[:, 0:256], psX[:, 256:512], KS[o][:, g, 0:256], KS[o][:, g, 256:512], False,
                             [psX], [KS[o]], Ypb, (Yp[2][:, :], Yp[3][:, :]))
                        yield
                        for chn in range(2):
                            fs = slice(chn * 128, (chn + 1) * 128)
                            kb.op("pe", lambda e, fs=fs, chn=chn: e.matmul(psG[:, chn * 256:(chn + 1) * 256], Yp[0][:, fs], E1[:, :], start=True, stop=False),
                                  reads=[Ypb[0], E1], writes=[psG])
                            kb.op("pe", lambda e, fs=fs, chn=chn: e.matmul(psG[:, chn * 256:(chn + 1) * 256], Yp[1][:, fs], E2[:, :], start=False, stop=True),
                                  reads=[Ypb[1], E2], writes=[psG])
                        yield
                        pg = psG[:, :].rearrange("p (ch ri c) -> p ch ri c", ch=2, ri=2)
                        cmul(Gp[0][:, :, :], Gp[1][:, :, :], pg[:, :, 0, :], pg[:, :, 1, :], TW2C[:, :, :], TW2S[:, :, :], False,
                             [psG], [TW2C, TW2S], Gpb, (Gp[2][:, :, :], Gp[3][:, :, :]))
                        yield
                        k = 0
                        for chn in range(2):
                            for (tab, gsrc, gb) in ((IC, Gp[0], Gpb[0]), (IS, Gp[1], Gpb[1])):
                                kb.op("pe", lambda e, chn=chn, tab=tab, gsrc=gsrc, k=k: e.matmul(
                                    psy[:, 0:128], tab[:, chn, :], gsrc[:, chn, :], start=(k == 0), stop=(k == 3)), reads=[tab, gb], writes=[psy])
                                k += 1
                        yield
                        kb.op("dve", lambda e: e.tensor_tensor(dst_ap, psy[:, 0:128].rearrange("p (c a) -> p a c", a=A),
                                                               mulv[:, g, :].rearrange("p (a c) -> p a c", c=G), ALU.mult),
                              reads=[psy, mul_b], writes=[dst_b])

                    def lockstep(gens):
                        gens = list(gens)
                        while gens:
                            nxt = []
                            for g_ in gens:
                                try:
                                    next(g_)
                                    nxt.append(g_)
                                except StopIteration:
                                    pass
                            gens = nxt

                    def spec_group(o, g, it):
                        psB, psX = PS[it % 2], PS[2 + it % 2]
                        yield from fwd_fft([(KTr[o][:, 0, g, :, :].rearrange("p a c -> p (a c)"), 0),
                                            (KTr[o][:, 1, g, :, :].rearrange("p a c -> p (a c)"), 1)], [KTr[o]], psB, psX)
                        yield
                        kb.op("act", lambda e: e.copy(KS[o][:, g, :], psX[:, :]), reads=[psX], writes=[KS[o]])

                    git = 0
                    for cb in range(nblk):
                        kb.dma("sp", DECt[:].rearrange("p h g a c -> p (h g a c)"), CT[pre + "DEC"][cb, :, :], reads=[CT[pre + "DEC"]], writes=[DECt])
                        for ai, tile_ in enumerate((uv, x1, x2)):
                            kb.dma("pool", tile_[:].rearrange("p g x -> p (g x)"), HS[sg["ut"]][ai, cb, :, :], reads=[HS[sg["ut"]]], writes=[tile_])
                        for o in range(2):
                            for hf in range(2):
                                col0 = o * 512 + hf * 256 + cb * cbw
                                npb = 512 // cbw
                                for a in range(A):
                                    p = PS[(a // npb) % 4]
                                    kb.op("pe", lambda e, p=p, a=a, hf=hf, col0=col0, npb=npb: e.matmul(
                                        p[:, (a % npb) * cbw:(a % npb + 1) * cbw], h2T[0:64, hf * 128 * A + a:hf * 128 * A + a + 127 * A + 1:A],
                                        fw3[0:64, col0:col0 + cbw], start=True, stop=True), reads=[h2T, fw3], writes=[p])
                                    if a % npb == npb - 1 or a == A - 1:
                                        a0 = (a // npb) * npb
                                        na = a - a0 + 1
                                        kb.op("dve", lambda e, p=p, a0=a0, na=na, hf=hf, o=o: e.tensor_tensor(
                                            KT[o][:, hf, :, a0:a0 + na, :], p[:, 0:na * cbw].rearrange("p (a g c) -> p g a c", a=na, c=G),
                                            DECt[:, hf, :, a0:a0 + na, :], ALU.mult), reads=[p, DECt], writes=[KT[o]])
                            kb.op("dve", lambda e, o=o: e.tensor_reduce(part[:, :].rearrange("p (g c) -> p g c", c=G),
                                                                        KT[o][:, :, :, :, :].rearrange("p h g a c -> p g c h a"), AX.XY, ALU.add,
                                                                        apply_absolute_value=True), reads=[KT[o]], writes=[part])
                            pe_ = PS[4]
                            kb.op("pe", lambda e, o=o: e.matmul(pe_[0:1, 0:cbw], h2T[0:64, 0:1], fw3[0:64, o * 512 + 256 + cb * cbw:o * 512 + 256 + (cb + 1) * cbw],
                                                                start=True, stop=True), reads=[h2T, fw3], writes=[pe_])
                            kb.op("act", lambda e: e.activation(ex[0:1, :], pe_[0:1, 0:cbw], AF.Abs), reads=[pe_], writes=[ex])
                            kb.op("dve", lambda e: e.tensor_tensor(part[0:1, :], part[0:1, :], ex[0:1, :], ALU.add), reads=[part, ex], writes=[part])
                            pt_ = PS[5]
                            kb.op("pe", lambda e: e.matmul(pt_[:, 0:cbw], ones_f[:, :], part[:, :], start=True, stop=True), reads=[ones_f, part], writes=[pt_])
                            kb.op("dve", lambda e: e.reciprocal(rn[:, :], pt_[:, 0:cbw]), reads=[pt_], writes=[rn])
                            for hf in range(2):
                                kb.op("dve", lambda e, o=o, hf=hf: e.tensor_tensor(
                                    KTr[o][:, hf, :, :, :], KT[o][:, hf, :, :, :],
                                    rn[:, :].rearrange("p (g c) -> p g c", c=G).unsqueeze(2).broadcast_to([128, ngr, A, G]), ALU.mult),
                                    reads=[KT[o], rn], writes=[KTr[o]])
                            kb.op("dve", lambda e, o=o: e.tensor_tensor(
                                KTr[o][0:1, 0, :, 0, :], KTr[o][0:1, 0, :, 0, :].bitcast(F32),
                                brow[0:1, o * 256 + cb * cbw:o * 256 + (cb + 1) * cbw].rearrange("p (g c) -> p g c", c=G), ALU.add),
                                reads=[KTr[o], brow], writes=[KTr[o]])
                            for g in range(0, ngr, 2):
                                gg = [g] + ([g + 1] if g + 1 < ngr else [])
                                lockstep([spec_group(o, g_, git + k_) for k_, g_ in enumerate(gg)])
                                git += len(gg)
                        kb.op("act", lambda e: e.copy(uvr[:], uv[:]), reads=[uv], writes=[uvr])
                        for g in range(0, ngr, 2):
                            gg = [g] + ([g + 1] if g + 1 < ngr else [])
                            lockstep([conv_group(uvr, uvr, g_, 0, x1, x1, u2[:, g_, :].rearrange("p (a c) -> p a c", c=G), u2, git + k_)
                                      for k_, g_ in enumerate(gg)])
                            git += len(gg)
                        for g in range(0, ngr, 2):
                            gg = [g] + ([g + 1] if g + 1 < ngr else [])
                            lockstep([conv_group(u2, u2, g_, 1, x2, x2, res[:, :, g_ * G:(g_ + 1) * G], res, git + k_)
                                      for k_, g_ in enumerate(gg)])
                            git += len(gg)
                        kb.dma("sp", sgm[:], HS["SG"][cb * cbw:(cb + 1) * cbw, off:off + Ls], reads=[HS["SG"]], writes=[sgm])
                        Fv = Fm[:, :].rearrange("c (p a) -> c p a", a=A)
                        for a in range(A):
                            p = PS[4 + (a // 4) % 4]
                            kb.op("pe", lambda e, p=p, a=a: e.transpose(p[0:cbw, (a % 4) * 128:(a % 4 + 1) * 128], res[:, a, :], ident_f[:]),
                                  reads=[res, ident_f], writes=[p])
                            if a % 4 == 3 or a == A - 1:
                                a0 = (a // 4) * 4
                                na = a - a0 + 1
                                kb.op("act", lambda e, p=p, a0=a0, na=na: e.copy(
                                    Fv[:, :, a0:a0 + na], p[0:cbw, 0:na * 128].rearrange("c (a p) -> c p a", p=128)), reads=[p], writes=[Fm])
                        kb.op("pool", lambda e: e.tensor_tensor(sgm[:, :], Fm[:, :], sgm[:, :], ALU.mult), reads=[Fm, sgm], writes=[sgm])
                        kb.dma("sp", mixT[cb * cbw:(cb + 1) * cbw, off:off + Ls], sgm[:, :], reads=[sgm], writes=[Buf()])

    dbgn = [n for n, _ in dbg]
    for l in range(depth):
        last = (l == DEPTH - 1)
        with kb.scope():
            hT = kb.sb("hT", [128, 8, T], BF16)
            G1 = kb.sb("G1", [128, 2, D])
            SH = kb.sb("SH", [128, 2, D])
            phase_mod(l)
            phase_norm(l)
            if "noattn" not in dbgn:
                if os.environ.get("ATTN_ONLY", "") != "dense":
                    phase_attn(l, False, not last)
                if os.environ.get("ATTN_ONLY", "") != "window":
                    phase_attn(l, True, not last)
            if "norw" not in dbgn:
                phase_rwkv_prep(l)
            if "nohy" not in dbgn:
                phase_hyena_prep(l, not last)
            if "hT" in dbgn:
                tmp = kb.sb("dbghT", [128, T])
                for j in range(8):
                    kb.op("dve", lambda e, j=j, tmp=tmp: e.tensor_copy(tmp[:], hT[:, j, :]), reads=[hT], writes=[tmp])
                    kb.dma("sp", dbg_t["hT"][:, j, :], tmp[:], reads=[tmp], writes=[dbg_t["hT"]])
        if "norw" not in dbgn:
            {0: phase_rwkv_chunked, 1: phase_rwkv_chunked2, 3: phase_rwkv_chunked3}[RW_V2](l)
            phase_rwkv_out(l, not last)
        if "nohy" not in dbgn:
            phase_hyena_main(l, not last)
        if "noout" not in dbgn:
            phase_out(l, last)
    for n, s_ in dbg:
        if n == "xres":
            with kb.scope():
                tx = kb.sb("dbgx", [128, D])
                for i in range(NT):
                    kb.dma("sp", tx[:], xres[i * 128:(i + 1) * 128, :], reads=[xres_b[i]], writes=[tx])
                    kb.dma("sp", dbg_t[n][i * 128:(i + 1) * 128, :], tx[:], reads=[tx], writes=[dbg_t[n]])
        if n == "mixT":
            with kb.scope():
                tmpb = kb.sb("dbgmb", [128, T], BF16)
                tmpf = kb.sb("dbgmf", [128, T])
                for j in range(8):
                    kb.dma("sp", tmpb[:], mixT[j * 128:(j + 1) * 128, :], reads=[mixT], writes=[tmpb])
                    kb.op("dve", lambda e, tmpb=tmpb, tmpf=tmpf: e.tensor_copy(tmpf[:], tmpb[:]), reads=[tmpb], writes=[tmpf])
                    kb.dma("sp", dbg_t[n][j * 128:(j + 1) * 128, :], tmpf[:], reads=[tmpf], writes=[dbg_t[n]])
    kb.finish()
    kb.es.close()
    return kb, cst


_PROG = {}


def kernel(**inputs):
    if "p" not in _PROG:
        _PROG["p"] = build()
    kb, cst = _PROG["p"]
    f = lambda a: np.ascontiguousarray(np.asarray(a, dtype=np.float32))
    shared = {}
    for n in inputs:
        if n in ("x", "c", "ctx", "c_ctx"):
            continue
        shared[n] = f(inputs[n])
    shared["c_ctx"] = f(inputs["c_ctx"])
    for n, a in cst.items():
        shared["k_" + n] = np.ascontiguousarray(a)
    x, c, ctx = f(inputs["x"]), f(inputs["c"]), f(inputs["ctx"])
    B = x.shape[0]
    in_maps = []
    for b in range(B):
        m = dict(shared)
        m["x"] = np.ascontiguousarray(x[b])
        m["c"] = np.ascontiguousarray(c[b])
        m["ctx"] = np.ascontiguousarray(ctx[b])
        in_maps.append(m)
    res = run_bass_kernel_spmd(kb.nc, in_maps, core_ids=list(range(B)))
    return np.stack([np.asarray(res.results[b]["out"], dtype=np.float32) for b in range(B)], axis=0)
```

```python
import contextlib
import math
import numpy as np
import ml_dtypes
import concourse.bass as bass
import concourse.mybir as mybir
from concourse.bass_utils import run_bass_kernel_spmd

F32 = mybir.dt.float32
BF16 = mybir.dt.bfloat16
F32R = mybir.dt.float32r
ALU = mybir.AluOpType
AF = mybir.ActivationFunctionType
AX = mybir.AxisListType

D = 1024
L = 4096
C = 256
T = L + C
NT = T // 128
DEPTH = 4
D_IN = 3712
HY0, HYG0, RW0, RWG0, WA0, WAG0, FA0, FAG0 = 0, 768, 1024, 1920, 2176, 2688, 2944, 3456
EPS = 1e-6
NSLOT = 24
import os
RW_STAGE = int(os.environ.get('RW_STAGE', '99'))
INLINE_WAIT = int(os.environ.get('INLINE_WAIT', '1'))
POOL_DMA_TO_SP = int(os.environ.get('POOL_DMA_TO_SP', '1'))
RW_V2 = int(os.environ.get('RW_V2', '3'))


class Buf:
    def __init__(self, name=""):
        self.name = name
        self.w = None
        self.r = {}

    def wdeps(self):
        return [self.w] if self.w is not None else []

    def rdeps(self):
        return list(self.r.values())

    def add_reader(self, tok):
        k = tok[:2]
        if k not in self.r or self.r[k][2] < tok[2]:
            self.r[k] = tok

    def set_writer(self, tok):
        self.w = tok
        self.r = {}


class Tile(Buf):
    def __init__(self, name, t):
        super().__init__(name)
        self.t = t

    def __getitem__(self, key):
        return self.t[key]


class KB:
    def __init__(self):
        self.nc = bass.Bass("TRN2", target_bir_lowering=False)
        nc = self.nc
        self.es = contextlib.ExitStack()
        self.eng = {"pe": nc.tensor, "act": nc.scalar, "dve": nc.vector, "pool": nc.gpsimd, "sp": nc.sync}
        self.sem = {}
        self.cnt = {}
        self.waited = {e: {} for e in self.eng}
        for e in self.eng:
            self.sem[e] = self.es.enter_context(nc.semaphore("s_" + e))
            self.cnt[e] = 0
        self.slots = {}
        self.slot_i = {}
        for q in ("sp", "act", "pool"):
            self.slots[q] = [[self.es.enter_context(nc.semaphore(f"d_{q}{i}")), 0] for i in range(NSLOT)]
            self.slot_i[q] = 0
        self.n_ins = 0

    def sb(self, name, shape, dt=F32):
        self.uid = getattr(self, "uid", 0) + 1
        name = f"{name}_{self.uid}"
        return Tile(name, self.es.enter_context(self.nc.sbuf_tensor(name, list(shape), dt)))

    def ps(self, name, shape, dt=F32):
        return Tile(name, self.es.enter_context(self.nc.psum_tensor(name, list(shape), dt)))

    def dram(self, name, shape, dt=F32, kind="Internal"):
        t = self.nc.dram_tensor(name, list(shape), dt, kind=kind)
        b = Tile(name, t.ap())
        return b

    def _tok_sem(self, tok):
        if tok[0] == "e":
            return ("e", tok[1]), self.sem[tok[1]], tok[2]
        return ("d", tok[1]), self.slots[tok[1][0]][tok[1][1]][0], tok[2]

    def _wait(self, e, toks, defer=False):
        need = {}
        for tok in toks:
            if tok is None:
                continue
            key, sem, val = self._tok_sem(tok)
            if tok[0] == "e" and tok[1] == e and e == "pe":
                continue
            if self.waited[e].get(key, 0) >= val:
                continue
            if key not in need or need[key][1] < val:
                need[key] = (sem, val)
        items = list(need.items())
        inline = None
        if defer and INLINE_WAIT and items:
            inline = items.pop()
        for key, (sem, val) in items:
            self.eng[e].wait_ge(sem, val)
            self.waited[e][key] = val
        return inline

    def op(self, e, fn, reads=(), writes=()):
        toks = []
        for b in reads:
            toks += b.wdeps()
        for b in writes:
            toks += b.wdeps() + b.rdeps()
        inline = self._wait(e, toks, defer=True)
        ins = fn(self.eng[e])
        if inline is not None:
            key, (sem, val) = inline
            ins._wait_ge(sem, val)
            self.waited[e][key] = val
        self.cnt[e] += 1
        ins.then_inc(self.sem[e], 1)
        tok = ("e", e, self.cnt[e])
        for b in reads:
            b.add_reader(tok)
        for b in writes:
            b.set_writer(tok)
        self.n_ins += 1
        return ins

    def dma(self, q, out, in_, reads=(), writes=(), slow=False):
        if q == "pool" and POOL_DMA_TO_SP:
            q = "sp"
        i = self.slot_i[q]
        self.slot_i[q] = (i + 1) % NSLOT
        slot = self.slots[q][i]
        toks = []
        if slot[1] > 0:
            toks.append(("d", (q, i), slot[1]))
        for b in reads:
            toks += b.wdeps()
        for b in writes:
            toks += b.wdeps() + b.rdeps()
        self._wait(q, toks)
        if slow:
            ins = self.eng[q].dma_start(out=out, in_=in_, allow_slow_non_contiguous=True)
        else:
            ins = self.eng[q].dma_start(out=out, in_=in_)
        ins.then_inc(slot[0], 16)
        slot[1] += 16
        tok = ("d", (q, i), slot[1])
        for b in reads:
            b.add_reader(tok)
        for b in writes:
            b.set_writer(tok)
        self.n_ins += 1
        return ins

    def barrier(self):
        toks = [("e", e, self.cnt[e]) for e in self.eng if self.cnt[e] > 0]
        for q in self.slots:
            for i, s in enumerate(self.slots[q]):
                if s[1] > 0:
                    toks.append(("d", (q, i), s[1]))
        for e in self.eng:
            self._wait(e, toks)

    def finish(self):
        self.barrier()

    @contextlib.contextmanager
    def scope(self):
        es = contextlib.ExitStack()
        old = self.es
        self.es = es
        try:
            yield
        finally:
            self.barrier()
            self.es = old
            es.close()


def host_consts():
    cst = {}
    cst["ident_bf"] = np.eye(128, dtype=np.float32).astype(ml_dtypes.bfloat16)
    cst["ident_f"] = np.eye(128, dtype=np.float32)
    blk = np.zeros((128, 128), np.float32)
    blk[:64, :64] = 1.0
    blk[64:, 64:] = 1.0
    cst["blk64"] = blk
    cst["ones_f"] = np.ones((128, 128), np.float32)
    t = np.arange(L)
    row = (t // 64).astype(np.float32)
    col = (t % 64).astype(np.float32)
    inv = (10000.0 ** (-np.arange(16, dtype=np.float32) / 16)).astype(np.float32)
    cosT = np.zeros((128, L), np.float32)
    sinT = np.zeros((128, L), np.float32)
    perm = np.zeros((128, 128), np.float32)
    for p in range(128):
        d = p % 64
        sec, half, f = d // 32, (d % 32) // 16, d % 16
        pos = row if sec == 0 else col
        ang = (pos * inv[f]).astype(np.float32)
        cosT[p] = np.cos(ang)
        sinT[p] = np.sin(ang)
        if half == 0:
            perm[p + 16, p] = -1.0
        else:
            perm[p - 16, p] = 1.0
    cst["rope_cos"] = cosT
    cst["rope_sin"] = sinT
    cst["rope_perm"] = perm
    i = np.arange(128)[:, None]
    j = np.arange(384)[None, :]
    cst["wmask"] = np.where((j >= i) & (j <= i + 256), 0.0, -1e30).astype(np.float32)
    ii = np.arange(64)
    ms = np.zeros((128, 2, 64), np.float32); mts = np.zeros((128, 2, 64), np.float32); mti = np.zeros((128, 2, 64), np.float32)
    for hp in range(2):
        rows = slice(hp * 64, hp * 64 + 64)
        ms[rows, 0, :] = (ii[None, :] < ii[:, None]); ms[rows, 1, :] = (ii[None, :] > ii[:, None])
        mts[rows, 0, :] = (ii[:, None] < ii[None, :]); mts[rows, 1, :] = (ii[:, None] > ii[None, :])
        mti[rows, 0, :] = (ii[:, None] <= ii[None, :]); mti[rows, 1, :] = (ii[:, None] >= ii[None, :])
    cst["rw_ms"] = ms; cst["rw_mts"] = mts; cst["rw_mti"] = mti
    cst["rw_msb"] = np.concatenate([ms, ms], 2); cst["rw_mtsb"] = np.concatenate([mts, mts], 2)
    cst["rw_id2"] = np.concatenate([np.eye(64, dtype=np.float32)] * 2, 0)
    cst.update(hy_consts(L, 32, 32, "L"))
    cst.update(hy_consts(C, 2, 64, "C"))
    return cst


def hy_consts(Ls, A, cbw, tag):
    G = 128 // A
    N = 2 * Ls
    out = {}
    p = np.arange(128)
    f1 = np.arange(256)
    F = np.zeros((128, 2, 512), np.float64)
    for h in range(2):
        pp = h * 128 + p
        ang = 2 * np.pi * ((pp[:, None] * f1[None, :]) % 256) / 256
        F[:, h, 0:256] = np.cos(ang)
        F[:, h, 256:512] = -np.sin(ang)
    out["F256"] = F
    a_of_row = np.arange(128) // G
    th = 2 * np.pi * ((a_of_row[:, None] * f1[None, :]) % N) / N
    out["TWC"] = np.cos(th)
    out["TWS"] = np.sin(th)
    Dre = np.zeros((128, 128)); Dim = np.zeros((128, 128))
    E1 = np.zeros((128, 256)); E2 = np.zeros((128, 256))
    for a in range(A):
        for c in range(G):
            for f2 in range(A):
                ph = 2 * np.pi * ((a * f2) % A) / A
                Dre[a * G + c, c * A + f2] = np.cos(ph)
                Dim[a * G + c, c * A + f2] = -np.sin(ph)
                E1[c * A + f2, c * A + a] = np.cos(ph)
                E1[c * A + f2, 128 + c * A + a] = np.sin(ph)
                E2[c * A + f2, c * A + a] = -np.sin(ph)
                E2[c * A + f2, 128 + c * A + a] = np.cos(ph)
    out["Dre"] = Dre; out["Dim"] = Dim; out["nDim"] = -Dim; out["E1"] = E1; out["E2"] = E2
    a_of_col = np.arange(128) % A
    TW2C = np.zeros((128, 2, 128)); TW2S = np.zeros((128, 2, 128))
    IC = np.zeros((128, 2, 128)); IS = np.zeros((128, 2, 128))
    for ch in range(2):
        ff = ch * 128 + np.arange(128)
        th2 = 2 * np.pi * ((ff[:, None] * a_of_col[None, :]) % N) / N
        TW2C[:, ch, :] = np.cos(th2) / N
        TW2S[:, ch, :] = np.sin(th2) / N
        ph = 2 * np.pi * ((ff[:, None] * p[None, :]) % 256) / 256
        IC[:, ch, :] = np.cos(ph)
        IS[:, ch, :] = -np.sin(ph)
    out["TW2C"] = TW2C; out["TW2S"] = TW2S; out["IC"] = IC; out["IS"] = IS
    tp = np.arange(N)
    pos = np.where(tp < Ls, tp, N - tp).astype(np.float64)
    tn = (pos / (Ls - 1)).astype(np.float32)
    w = ((2.0 * math.pi / Ls) * pos).astype(np.float32)
    fb = np.linspace(1e-4, 15.0, 16, dtype=np.float32)
    zT = np.zeros((33, N), np.float32)
    zT[0] = tn
    zT[1:17] = np.cos(fb[:, None] * w[None, :])
    zT[17:33] = np.sin(fb[:, None] * w[None, :])
    out["zT"] = zT
    deltas = np.abs(np.linspace(math.log(1e-2) / 1.5, math.log(1e-2) / 0.3, 256, dtype=np.float32))
    dec = np.exp(-tn[:, None] * deltas[None, :]).astype(np.float32)
    dec[Ls, :] = 0.0
    nblk = 256 // cbw
    ngr = cbw // G
    DEC = np.zeros((nblk, 128, 2, ngr, A, G), np.float32)
    for h in range(2):
        for a in range(A):
            tpp = A * (h * 128 + p) + a
            for b in range(nblk):
                DEC[b, :, h, :, a, :] = dec[tpp, b * cbw:(b + 1) * cbw].reshape(128, ngr, G)
    out["DEC"] = DEC.reshape(nblk, 128, 2 * ngr * A * G)
    return {f"hy{tag}_{k}": np.ascontiguousarray(v.astype(np.float32)) for k, v in out.items()}

CONST_SPECS = None


def build(depth=DEPTH, dbg=()):
    kb = KB()
    nc = kb.nc
    cst = host_consts()
    def inp(name, shape, dt=F32):
        return kb.dram(name, shape, dt, kind="ExternalInput")

    x_in = inp("x", [L, D])
    c_in = inp("c", [D])
    ctx_in = inp("ctx", [C, D])
    cctx_in = inp("c_ctx", [D])
    W = {}
    wspec = {
        "mod_w": [DEPTH, D, 3 * D], "mod_b": [DEPTH, 3 * D], "norm_g": [DEPTH, D], "w_in": [DEPTH, D, D_IN],
        "w_out": [DEPTH, D, D], "wa_sink": [DEPTH, 4], "fa_q_norm": [DEPTH, 64], "fa_k_norm": [DEPTH, 64],
        "final_g": [D],
        "rw_conv": [DEPTH, 3, 896], "rw_w0": [DEPTH, 2, 256], "rw_w_up": [DEPTH, 2, 64, 256], "rw_a0": [DEPTH, 2, 256],
        "rw_a_up": [DEPTH, 2, 64, 256], "rw_k_k": [DEPTH, 256], "rw_k_a": [DEPTH, 256], "rw_r_k": [DEPTH, 256],
        "rw_ln_g": [DEPTH, 256], "rw_ln_b": [DEPTH, 256],
        "hy_conv": [DEPTH, 3, 768], "hy_fw1": [DEPTH, 33, 64], "hy_fb1": [DEPTH, 64], "hy_freq": [DEPTH, 64],
        "hy_fw2": [DEPTH, 64, 64], "hy_fb2": [DEPTH, 64], "hy_fw3": [DEPTH, 64, 1024], "hy_bias": [DEPTH, 2, 256],
    }
    for n, s in wspec.items():
        W[n] = inp(n, s)
    CT = {}
    for n, a in cst.items():
        CT[n] = inp("k_" + n, list(a.shape), BF16 if a.dtype == ml_dtypes.bfloat16 else F32)
    out = kb.dram("out", [L, D], F32, kind="ExternalOutput")
    xres = kb.dram("xres", [T, D], F32)
    mixT = kb.dram("mixT", [D, T], BF16)
    RS = {}
    for n in ("RT", "VT", "AL", "W0", "W1", "B0", "B1", "KD0", "KD1", "YF", "YB"):
        RS[n] = kb.dram("rs_" + n, [256, T])
    RS["VTOK"] = kb.dram("rs_VTOK", [T, 256])
    RS["SGT"] = kb.dram("rs_SGT", [256, T], BF16)
    HS = {"SG": kb.dram("hs_SG", [256, T], BF16),
          "UTL": kb.dram("hs_UTL", [3, 8, 128, 32 * 32]), "UTC": kb.dram("hs_UTC", [3, 4, 128, 2 * 64])}
    dbg_t = {}
    for n, s in dbg:
        dbg_t[n] = kb.dram("dbg_" + n, s, F32, kind="ExternalOutput")

    ident_bf = kb.sb("ident_bf", [128, 128], BF16)
    ident_f = kb.sb("ident_f", [128, 128])
    blk64 = kb.sb("blk64", [128, 128])
    ones_f = kb.sb("ones_f", [128, 128])
    for tl, n in ((ident_bf, "ident_bf"), (ident_f, "ident_f"), (blk64, "blk64"), (ones_f, "ones_f")):
        kb.dma("sp", tl[:], CT[n][:, :], reads=[CT[n]], writes=[tl])
    hT = G1 = SH = None
    GT = kb.sb("GT", [128, 2, D])
    PS = [kb.ps(f"ps{i}", [128, 512]) for i in range(8)]

    xres_b = [Buf(f"xres{i}") for i in range(NT)]

    def x_src(l, i):
        if l == 0:
            if i < 2:
                return ctx_in[i * 128:(i + 1) * 128, :], ctx_in
            return x_in[(i - 2) * 128:(i - 1) * 128, :], x_in
        return xres[i * 128:(i + 1) * 128, :], xres_b[i]

    def phase_mod(l):
        with kb.scope():
            cc = kb.sb("cc", [128, 2, 8])
            sc = kb.sb("sc", [128, 2, 8])
            mw = [kb.sb(f"mw{i}", [128, 8, 512]) for i in range(2)]
            mb = kb.sb("mb", [128, 3 * D])
            ng = kb.sb("ng", [128, D])
            modr = kb.sb("modr", [128, 2, 3 * D])
            kb.dma("sp", cc[:, 0, :], c_in.t.rearrange("(j p) -> p j", p=128), reads=[c_in], writes=[cc], slow=True)
            kb.dma("sp", cc[:, 1, :], cctx_in.t.rearrange("(j p) -> p j", p=128), reads=[cctx_in], writes=[cc], slow=True)
            kb.dma("sp", mb[:], W["mod_b"][l, :].partition_broadcast(128), reads=[W["mod_b"]], writes=[mb])
            kb.dma("sp", ng[:], W["norm_g"][l, :].partition_broadcast(128), reads=[W["norm_g"]], writes=[ng])
            kb.op("act", lambda e: e.activation(sc[:], cc[:], AF.Silu), reads=[cc], writes=[sc])
            for n in range(6):
                m = mw[n % 2]
                kb.dma("sp" if n % 2 == 0 else "pool", m[:],
                       W["mod_w"][l, :, n * 512:(n + 1) * 512].rearrange("(j p) n -> p j n", p=128),
                       reads=[W["mod_w"]], writes=[m])
                for i in range(2):
                    p = PS[(2 * n + i) % 8]
                    for j in range(8):
                        kb.op("pe", lambda e, p=p, i=i, j=j, m=m: e.matmul(
                            p[:, :], sc[:, i, j:j + 1].broadcast_to([128, 128]), m[:, j, :],
                            start=(j == 0), stop=(j == 7)), reads=[sc, m], writes=[p])
                    kb.op("dve", lambda e, p=p, i=i, n=n: e.tensor_tensor(
                        modr[:, i, n * 512:(n + 1) * 512], p[:, :], mb[:, n * 512:(n + 1) * 512], ALU.add),
                        reads=[p, mb], writes=[modr])
            for i in range(2):
                kb.op("dve", lambda e, i=i: e.scalar_tensor_tensor(
                    G1[:, i, :], modr[:, i, D:2 * D], 1.0, ng[:], ALU.add, ALU.mult), reads=[modr, ng], writes=[G1])
                kb.op("act", lambda e, i=i: e.copy(SH[:, i, :], modr[:, i, 0:D]), reads=[modr], writes=[SH])
                kb.op("act", lambda e, i=i: e.copy(GT[:, i, :], modr[:, i, 2 * D:3 * D]), reads=[modr], writes=[GT])

    def phase_norm(l):
        with kb.scope():
            xt = [kb.sb(f"xt{i}", [128, D]) for i in range(3)]
            junk = kb.sb("junk", [128, D])
            hf = [kb.sb(f"hf{i}", [128, D]) for i in range(2)]
            hb = [kb.sb(f"hb{i}", [128, D], BF16) for i in range(2)]
            st = [kb.sb(f"st{i}", [128, 4]) for i in range(2)]
            for i in range(NT):
                x, s, h, hbt = xt[i % 3], st[i % 2], hf[i % 2], hb[i % 2]
                sel = 1 if i < 2 else 0
                src, srcb = x_src(l, i)
                kb.dma("sp" if i % 2 == 0 else "pool", x[:], src, reads=[srcb], writes=[x])
                kb.op("act", lambda e, x=x, s=s: e.activation(junk[:], x[:], AF.Square, accum_out=s[:, 0:1]),
                      reads=[x], writes=[junk, s])
                kb.op("dve", lambda e, s=s: e.tensor_scalar(s[:, 1:2], s[:, 0:1], 1.0 / D, EPS, ALU.mult, ALU.add),
                      reads=[s], writes=[s])
                kb.op("act", lambda e, s=s: e.sqrt(s[:, 2:3], s[:, 1:2]), reads=[s], writes=[s])
                kb.op("dve", lambda e, s=s: e.reciprocal(s[:, 3:4], s[:, 2:3]), reads=[s], writes=[s])
                kb.op("dve", lambda e, x=x, s=s, h=h, sel=sel: e.scalar_tensor_tensor(
                    h[:], x[:], s[:, 3:4], G1[:, sel, :], ALU.mult, ALU.mult), reads=[x, s, G1], writes=[h])
                kb.op("pool", lambda e, h=h, hbt=hbt, sel=sel: e.tensor_tensor(hbt[:], h[:], SH[:, sel, :], ALU.add),
                      reads=[h, SH], writes=[hbt])
                p = PS[i % 4]
                pv = p[:, :].bitcast(BF16)
                for j in range(8):
                    kb.op("pe", lambda e, j=j, pv=pv, hbt=hbt: e.transpose(
                        pv[:, j * 128:(j + 1) * 128], hbt[:, j * 128:(j + 1) * 128], ident_bf[:]),
                        reads=[hbt, ident_bf], writes=[p])
                kb.op("act", lambda e, pv=pv, i=i: e.copy(
                    hT[:, :, i * 128:(i + 1) * 128], pv.rearrange("p (j t) -> p j t", j=8)), reads=[p], writes=[hT])

    def load_w(l, dst, col0, ncols, stage, q="sp"):
        kb.dma(q, stage[:, :, 0:ncols], W["w_in"][l, :, col0:col0 + ncols].rearrange("(j p) n -> p j n", p=128),
               reads=[W["w_in"]], writes=[stage])
        kb.op("pool", lambda e: e.tensor_copy(dst[:, :, 0:ncols], stage[:, :, 0:ncols]), reads=[stage], writes=[dst])

    def proj_fm(p, wt, c0, nc_, t0, nt):
        for j in range(8):
            kb.op("pe", lambda e, j=j: e.matmul(p[0:nc_, 0:nt], wt[:, j, c0:c0 + nc_], hT[:, j, t0:t0 + nt],
                                                start=(j == 0), stop=(j == 7)), reads=[wt, hT], writes=[p])

    def proj_tm(p, wt, c0, nc_, i):
        for j in range(8):
            kb.op("pe", lambda e, j=j: e.matmul(p[:, 0:nc_], hT[:, j, i * 128:(i + 1) * 128], wt[:, j, c0:c0 + nc_],
                                                start=(j == 0), stop=(j == 7)), reads=[wt, hT], writes=[p])

    TCH = [(t0, min(512, T - t0)) for t0 in range(0, T, 512)]

    def qk_prep(l, es_tiles, wt, c0, dst, dst_j, gvec, norm, rope):
        raw, sq, rs, rot = es_tiles
        for ci, (t0, nt) in enumerate(TCH):
            p = PS[ci % 2]
            proj_fm(p, wt, c0, 128, t0, nt)
            if norm:
                kb.op("act", lambda e, p=p, nt=nt: e.activation(sq[:, 0:nt], p[:, 0:nt], AF.Square), reads=[p], writes=[sq])
                p2 = PS[2 + ci % 2]
                kb.op("pe", lambda e, p2=p2, nt=nt: e.matmul(p2[:, 0:nt], blk64[:], sq[:, 0:nt], start=True, stop=True),
                      reads=[blk64, sq], writes=[p2])
                kb.op("dve", lambda e, p2=p2, nt=nt: e.tensor_scalar(rs[:, 0:nt], p2[:, 0:nt], 1.0 / 64, EPS, ALU.mult, ALU.add),
                      reads=[p2], writes=[rs])
                kb.op("act", lambda e, nt=nt: e.sqrt(rs[:, 0:nt], rs[:, 0:nt]), reads=[rs], writes=[rs])
                kb.op("dve", lambda e, nt=nt: e.reciprocal(rs[:, 0:nt], rs[:, 0:nt]), reads=[rs], writes=[rs])
                kb.op("dve", lambda e, p=p, nt=nt: e.scalar_tensor_tensor(
                    raw[:, 0:nt], p[:, 0:nt], gvec[:, 0:1], rs[:, 0:nt], ALU.mult, ALU.mult), reads=[p, gvec, rs], writes=[raw])
            else:
                kb.op("act", lambda e, p=p, nt=nt: e.copy(raw[:, 0:nt], p[:, 0:nt]), reads=[p], writes=[raw])
            lat0 = 0
            if t0 < C:
                lat0 = C - t0
                kb.op("pool", lambda e, t0=t0, lat0=lat0: e.tensor_copy(dst[:, dst_j, t0:t0 + lat0], raw[:, 0:lat0]),
                      reads=[raw], writes=[dst])
            if not rope:
                if nt > lat0:
                    kb.op("pool", lambda e, t0=t0, lat0=lat0, nt=nt: e.tensor_copy(
                        dst[:, dst_j, t0 + lat0:t0 + nt], raw[:, lat0:nt]), reads=[raw], writes=[dst])
                continue
            p3 = PS[4 + ci % 2]
            n_l = nt - lat0
            lp = t0 + lat0 - C
            kb.op("pe", lambda e, p3=p3, lat0=lat0, nt=nt: e.matmul(p3[:, lat0:nt], rope_perm[:], raw[:, lat0:nt], start=True, stop=True),
                  reads=[rope_perm, raw], writes=[p3])
            kb.op("dve", lambda e, p3=p3, lat0=lat0, nt=nt, lp=lp, n_l=n_l: e.tensor_tensor(
                rot[:, lat0:nt], p3[:, lat0:nt], rope_sin[:, lp:lp + n_l], ALU.mult), reads=[p3, rope_sin], writes=[rot])
            kb.op("pool", lambda e, lat0=lat0, nt=nt, lp=lp, n_l=n_l: e.tensor_tensor(
                raw[:, lat0:nt], raw[:, lat0:nt], rope_cos[:, lp:lp + n_l], ALU.mult), reads=[raw, rope_cos], writes=[raw])
            kb.op("dve", lambda e, t0=t0, lat0=lat0, nt=nt: e.tensor_tensor(
                dst[:, dst_j, t0 + lat0:t0 + nt], raw[:, lat0:nt], rot[:, lat0:nt], ALU.add), reads=[raw, rot], writes=[dst])

    rope_cos = rope_sin = rope_perm = None

    def phase_attn(l, dense, with_ctx):
        nonlocal rope_cos, rope_sin, rope_perm
        base = FA0 if dense else WA0
        gbase = FAG0 if dense else WAG0
        mrow = 768 if dense else 512
        with kb.scope():
            wt = kb.sb("wt", [128, 8, 768], BF16)
            gq = kb.sb("gq", [128, 1])
            gk = kb.sb("gk", [128, 1])
            sink = kb.sb("sink", [128, 4])
            if dense:
                for hh in range(2):
                    kb.dma("sp", gq[hh * 64:(hh + 1) * 64, :], W["fa_q_norm"][l, :].rearrange("(d o) -> d o", o=1),
                           reads=[W["fa_q_norm"]], writes=[gq], slow=True)
                    kb.dma("sp", gk[hh * 64:(hh + 1) * 64, :], W["fa_k_norm"][l, :].rearrange("(d o) -> d o", o=1),
                           reads=[W["fa_k_norm"]], writes=[gk], slow=True)
            else:
                kb.dma("sp", sink[:], W["wa_sink"][l, :].partition_broadcast(128), reads=[W["wa_sink"]], writes=[sink])
            QT = kb.sb("QT", [128, 2, T], BF16)
            KT = kb.sb("KT", [128, 1, T], BF16)
            VW = 65 if dense else 64
            Vt = kb.sb("Vt", [128, NT, 2, VW], BF16)
            SG = None
            with kb.scope():
                rope_cos = kb.sb("rope_cos", [128, L])
                rope_sin = kb.sb("rope_sin", [128, L])
                rope_perm = kb.sb("rope_perm", [128, 128])
                kb.dma("sp", rope_cos[:], CT["rope_cos"][:, :], reads=[CT["rope_cos"]], writes=[rope_cos])
                kb.dma("pool", rope_sin[:], CT["rope_sin"][:, :], reads=[CT["rope_sin"]], writes=[rope_sin])
                kb.dma("sp", rope_perm[:], CT["rope_perm"][:, :], reads=[CT["rope_perm"]], writes=[rope_perm])
                stage = kb.sb("wstage", [128, 8, 256])
                w4 = W["w_in"][l, :, base:base + 256].rearrange("(j p) (h d) -> p j h d", p=128, d=64)
                st4 = stage[:, :, 0:256].rearrange("p j (h d) -> p j h d", d=64)
                for hi, h in enumerate((0, 2, 1, 3)):
                    kb.dma("sp", st4[:, :, hi, :], w4[:, :, h, :], reads=[W["w_in"]], writes=[stage])
                kb.op("pool", lambda e: e.tensor_copy(wt[:, :, 0:256], stage[:]), reads=[stage], writes=[wt])
                kb.dma("pool", stage[:], W["w_in"][l, :, base + 256:base + 512].rearrange("(j p) n -> p j n", p=128),
                       reads=[W["w_in"]], writes=[stage])
                kb.op("pool", lambda e: e.tensor_copy(wt[:, :, 256:512], stage[:]), reads=[stage], writes=[wt])
                kb.dma("sp", stage[:], W["w_in"][l, :, gbase:gbase + 256].rearrange("(j p) n -> p j n", p=128),
                       reads=[W["w_in"]], writes=[stage])
                kb.op("pool", lambda e: e.tensor_copy(wt[:, :, 512:768], stage[:]), reads=[stage], writes=[wt])
                tl = (kb.sb("qraw", [128, 512]), kb.sb("qsq", [128, 512]), kb.sb("qrs", [128, 512]), kb.sb("qrot", [128, 512]))
                qk_prep(l, tl, wt, 0, QT, 0, gq, dense, True)
                qk_prep(l, tl, wt, 128, QT, 1, gq, dense, True)
                qk_prep(l, tl, wt, 256, KT, 0, gk, dense, True)
            if dense:
                kb.op("pool", lambda e: e.memset(Vt[:, :, :, 64:65], 1.0), writes=[Vt])
            for i in range(NT):
                p = PS[i % 2]
                proj_tm(p, wt, 384, 128, i)
                kb.op("act", lambda e, p=p, i=i: e.copy(Vt[:, i, :, 0:64], p[:, 0:128].rearrange("p (k d) -> p k d", d=64)),
                      reads=[p], writes=[Vt])
            with kb.scope():
                if dense:
                    attn_dense(l, wt, QT, KT, Vt, mrow, with_ctx)
                else:
                    attn_window(l, wt, QT, KT, Vt, sink, mrow, with_ctx)

    def attn_dense(l, wt, QT, KT, Vt, mrow, with_ctx):
        pt = [kb.sb(f"pt{i}", [128, 512], BF16) for i in range(4)]
        osb = [kb.sb(f"osb{i}", [128, 512]) for i in range(2)]
        rc = [kb.sb(f"rc{i}", [128, 512]) for i in range(2)]
        ob = [kb.sb(f"ob{i}", [128, 512], BF16) for i in range(2)]
        it = 0
        sgt = [kb.sb(f"sgt{i}", [64, 512], BF16) for i in range(2)]
        chunks = []
        if with_ctx:
            chunks.append((0, C, 0, 2))
        for t0 in range(C, T, 512):
            chunks.append((t0, 512, 0, NT))
        for h in range(4):
            kv, pr = h // 2, h % 2
            ks = slice(64 * kv, 64 * kv + 64)
            for (t0, nt, kb0, kb1) in chunks:
                po = PS[4 + it % 2]
                sg = sgt[it % 2]
                pg = PS[6 + it % 2]
                for j in range(8):
                    kb.op("pe", lambda e, j=j: e.matmul(
                        pg[0:64, 0:nt], wt[:, j, 512 + 64 * h:576 + 64 * h], hT[:, j, t0:t0 + nt], start=(j == 0), stop=(j == 7)),
                        reads=[wt, hT], writes=[pg])
                kb.op("act", lambda e: e.activation(sg[0:64, 0:nt], pg[0:64, 0:nt], AF.Silu), reads=[pg], writes=[sg])
                def pv_(kbi):
                    ptt = pt[kbi % 4]
                    kb.op("pe", lambda e: e.matmul(
                        po[0:65, 0:nt], Vt[:, kbi, kv, 0:65], ptt[:, 0:nt], start=(kbi == kb0), stop=(kbi == kb1 - 1)),
                        reads=[Vt, ptt], writes=[po])
                LA = 2
                for kbi in range(kb0, kb1):
                    psS = PS[kbi % 4]
                    ptt = pt[kbi % 4]
                    kb.op("pe", lambda e, psS=psS, kbi=kbi: e.matmul(
                        psS[:, 0:nt], KT[ks, 0, kbi * 128:(kbi + 1) * 128], QT[ks, pr, t0:t0 + nt], start=True, stop=True),
                        reads=[KT, QT], writes=[psS])
                    kb.op("act", lambda e, psS=psS, ptt=ptt: e.activation(ptt[:, 0:nt], psS[:, 0:nt], AF.Exp, scale=0.125),
                          reads=[psS], writes=[ptt])
                    if kbi - LA >= kb0:
                        pv_(kbi - LA)
                for kbi in range(max(kb0, kb1 - LA), kb1):
                    pv_(kbi)
                o_s, r_c, o_b = osb[it % 2], rc[it % 2], ob[it % 2]
                kb.op("dve", lambda e: e.reciprocal(r_c[64:65, 0:nt], po[64:65, 0:nt]), reads=[po], writes=[r_c])
                kb.op("act", lambda e: e.copy(o_s[0:64, 0:nt], po[0:64, 0:nt]), reads=[po], writes=[o_s])
                pb = PS[6 + it % 2]
                kb.op("pe", lambda e: e.matmul(pb[0:64, 0:nt], ones_f[64:65, 0:64], r_c[64:65, 0:nt], start=True, stop=True),
                      reads=[ones_f, r_c], writes=[pb])
                kb.op("dve", lambda e: e.tensor_tensor(o_s[0:64, 0:nt], o_s[0:64, 0:nt], pb[0:64, 0:nt], ALU.mult),
                      reads=[o_s, pb], writes=[o_s])
                kb.op("pool", lambda e: e.tensor_tensor(o_b[0:64, 0:nt], o_s[0:64, 0:nt], sg[0:64, 0:nt], ALU.mult),
                      reads=[o_s, sg], writes=[o_b])
                kb.dma("pool", mixT[mrow + 64 * h:mrow + 64 * h + 64, t0:t0 + nt], o_b[0:64, 0:nt], reads=[o_b], writes=[Buf()])
                it += 1

    def attn_window(l, wt, QT, KT, Vt, sink, mrow, with_ctx):
        wmask = kb.sb("wmask", [128, 384])
        kb.dma("sp", wmask[:], CT["wmask"][:, :], reads=[CT["wmask"]], writes=[wmask])
        nsink = kb.sb("nsink", [128, 4])
        kb.op("dve", lambda e: e.tensor_scalar(nsink[:], sink[:], -1.0, None, ALU.mult), reads=[sink], writes=[nsink])
        S = [kb.sb(f"wS{i}", [128, 640]) for i in range(4)]
        P = [kb.sb(f"wP{i}", [128, 640]) for i in range(4)]
        Pn = [kb.sb(f"wPn{i}", [128, 640], BF16) for i in range(4)]
        PT = [kb.sb(f"wPT{i}", [128, 640], BF16) for i in range(4)]
        st = [kb.sb(f"wst{i}", [128, 8]) for i in range(4)]
        sgt = [kb.sb(f"wsg{i}", [64, 128], BF16) for i in range(4)]
        ob = [kb.sb(f"wob{i}", [64, 128], BF16) for i in range(4)]
        it = 0
        for i in range(0 if with_ctx else 2, NT):
            if i < 2:
                loc = []
            else:
                loc = list(range(max(2, i - 1), min(NT - 1, i + 1) + 1))
            nl = 128 * len(loc)
            m0 = 128 if (i >= 2 and i - 1 < 2) else 0
            nk = nl + C
            ktiles = loc + [0, 1]
            def unit(h, it):
                kv, pr = h // 2, h % 2
                ks = slice(64 * kv, 64 * kv + 64)
                s_, p_, pn_, pt_, st_, sg, o_b = S[it % 4], P[it % 4], Pn[it % 4], PT[it % 4], st[it % 4], sgt[it % 4], ob[it % 4]
                psA, psB = PS[2 * (it % 4)], PS[2 * (it % 4) + 1]
                psT, psOG = psA, psB
                psO, psG = psOG, psOG
                q_ap = QT[ks, pr, i * 128:(i + 1) * 128]
                if nl:
                    k0 = loc[0] * 128
                    kb.op("pe", lambda e: e.matmul(psA[:, 0:nl], q_ap, KT[ks, 0, k0:k0 + nl], start=True, stop=True),
                          reads=[QT, KT], writes=[psA])
                    kb.op("dve", lambda e: e.tensor_tensor(s_[:, 0:nl], psA[:, 0:nl], wmask[:, m0:m0 + nl], ALU.add),
                          reads=[psA, wmask], writes=[s_])
                kb.op("pe", lambda e: e.matmul(psB[:, 0:C], q_ap, KT[ks, 0, 0:C], start=True, stop=True),
                      reads=[QT, KT], writes=[psB])
                kb.op("act", lambda e: e.copy(s_[:, nl:nk], psB[:, 0:C]), reads=[psB], writes=[s_])
                yield
                kb.op("dve", lambda e: e.reduce_max(st_[:, 0:1], s_[:, 0:nk], AX.X), reads=[s_], writes=[st_])
                kb.op("dve", lambda e: e.tensor_scalar(st_[:, 1:2], st_[:, 0:1], -0.125, nsink[:, h:h + 1], ALU.mult, ALU.min),
                      reads=[st_, nsink], writes=[st_])
                kb.op("act", lambda e: e.activation(p_[:, 0:nk], s_[:, 0:nk], AF.Exp, bias=st_[:, 1:2], scale=0.125,
                                                    accum_out=st_[:, 2:3]), reads=[s_, st_], writes=[p_, st_])
                kb.op("act", lambda e: e.activation(st_[:, 3:4], sink[:, h:h + 1], AF.Exp, bias=st_[:, 1:2], scale=1.0),
                      reads=[sink, st_], writes=[st_])
                kb.op("dve", lambda e: e.tensor_tensor(st_[:, 4:5], st_[:, 2:3], st_[:, 3:4], ALU.add), reads=[st_], writes=[st_])
                kb.op("dve", lambda e: e.reciprocal(st_[:, 5:6], st_[:, 4:5]), reads=[st_], writes=[st_])
                kb.op("dve", lambda e: e.tensor_scalar(pn_[:, 0:nk], p_[:, 0:nk], st_[:, 5:6], None, ALU.mult),
                      reads=[p_, st_], writes=[pn_])
                yield
                pv = psT[:, :].bitcast(BF16)
                nb = nk // 128
                for b in range(nb):
                    kb.op("pe", lambda e, b=b: e.transpose(pv[:, b * 128:(b + 1) * 128], pn_[:, b * 128:(b + 1) * 128], ident_bf[:]),
                          reads=[pn_, ident_bf], writes=[psT])
                yield
                kb.op("act", lambda e: e.copy(pt_[:, 0:nk], pv[:, 0:nk]), reads=[psT], writes=[pt_])
                yield
                for b in range(nb):
                    kb.op("pe", lambda e, b=b: e.matmul(psO[0:64, 0:128], Vt[:, ktiles[b], kv, 0:64], pt_[:, b * 128:(b + 1) * 128],
                                                        start=(b == 0), stop=(b == nb - 1)), reads=[Vt, pt_], writes=[psO])
                for j in range(8):
                    kb.op("pe", lambda e, j=j: e.matmul(
                        psG[0:64, 128:256], wt[:, j, 512 + 64 * h:576 + 64 * h], hT[:, j, i * 128:(i + 1) * 128],
                        start=(j == 0), stop=(j == 7)), reads=[wt, hT], writes=[psG])
                yield
                kb.op("act", lambda e: e.activation(sg[:, :], psG[0:64, 128:256], AF.Silu), reads=[psG], writes=[sg])
                kb.op("dve", lambda e: e.tensor_tensor(o_b[:, :], psO[0:64, 0:128], sg[:, :], ALU.mult), reads=[psO, sg], writes=[o_b])
                kb.dma("sp", mixT[mrow + 64 * h:mrow + 64 * h + 64, i * 128:(i + 1) * 128], o_b[:, :], reads=[o_b], writes=[Buf()])

            for h0 in (0,):
                gens = [unit(h_, it + h_) for h_ in range(4)]
                it += 4
                while gens:
                    nxt = []
                    for g_ in gens:
                        try:
                            next(g_)
                            nxt.append(g_)
                        except StopIteration:
                            pass
                    gens = nxt

    def phase_out(l, last):
        with kb.scope():
            wo = kb.sb("wo", [128, 8, D], BF16)
            stage = kb.sb("wostage", [128, 8, 256])
            for q in range(4):
                kb.dma("sp", stage[:], W["w_out"][l, :, q * 256:(q + 1) * 256].rearrange("(j p) n -> p j n", p=128),
                       reads=[W["w_out"]], writes=[stage])
                kb.op("pool", lambda e, q=q: e.tensor_copy(wo[:, :, q * 256:(q + 1) * 256], stage[:]), reads=[stage], writes=[wo])
            fg = kb.sb("fg", [128, D])
            if last:
                kb.dma("sp", fg[:], W["final_g"][:].partition_broadcast(128), reads=[W["final_g"]], writes=[fg])
            mt = [kb.sb(f"mt{i}", [128, 8, 128], BF16) for i in range(2)]
            xt = [kb.sb(f"oxt{i}", [128, D]) for i in range(2)]
            xn = [kb.sb(f"oxn{i}", [128, D]) for i in range(2)]
            tmp = [kb.sb(f"otmp{i}", [128, 512]) for i in range(2)]
            st = [kb.sb(f"ost{i}", [128, 4]) for i in range(2)]
            junk = kb.sb("ojunk", [128, D])
            mixv = mixT.t.rearrange("(j p) t -> p j t", p=128)
            for it, i in enumerate(range(2 if last else 0, NT)):
                m, x, xo, s = mt[it % 2], xt[it % 2], xn[it % 2], st[it % 2]
                sel = 1 if i < 2 else 0
                kb.dma("sp", m[:], mixv[:, :, i * 128:(i + 1) * 128], reads=[mixT], writes=[m])
                src, srcb = x_src(l, i)
                kb.dma("pool", x[:], src, reads=[srcb], writes=[x])
                for hf in range(2):
                    p = PS[(2 * it + hf) % 8]
                    tp = tmp[hf]
                    for j in range(8):
                        kb.op("pe", lambda e, j=j, p=p, m=m, hf=hf: e.matmul(p[:, :], m[:, j, :], wo[:, j, hf * 512:(hf + 1) * 512],
                                                                     start=(j == 0), stop=(j == 7)), reads=[m, wo], writes=[p])
                    kb.op("dve", lambda e, p=p, tp=tp, hf=hf, sel=sel: e.tensor_tensor(
                        tp[:], p[:, :], GT[:, sel, hf * 512:(hf + 1) * 512], ALU.mult), reads=[p, GT], writes=[tp])
                    kb.op("pool", lambda e, tp=tp, hf=hf, x=x, xo=xo: e.tensor_tensor(
                        xo[:, hf * 512:(hf + 1) * 512], x[:, hf * 512:(hf + 1) * 512], tp[:], ALU.add), reads=[x, tp], writes=[xo])
                if not last:
                    kb.dma("sp", xres[i * 128:(i + 1) * 128, :], xo[:], reads=[xo], writes=[xres_b[i]])
                else:
                    kb.op("act", lambda e, xo=xo, s=s: e.activation(junk[:], xo[:], AF.Square, accum_out=s[:, 0:1]),
                          reads=[xo], writes=[junk, s])
                    kb.op("dve", lambda e, s=s: e.tensor_scalar(s[:, 1:2], s[:, 0:1], 1.0 / D, EPS, ALU.mult, ALU.add),
                          reads=[s], writes=[s])
                    kb.op("act", lambda e, s=s: e.sqrt(s[:, 2:3], s[:, 1:2]), reads=[s], writes=[s])
                    kb.op("dve", lambda e, s=s: e.reciprocal(s[:, 3:4], s[:, 2:3]), reads=[s], writes=[s])
                    kb.op("dve", lambda e, xo=xo, s=s, x=x: e.scalar_tensor_tensor(
                        x[:], xo[:], s[:, 3:4], fg[:], ALU.mult, ALU.mult), reads=[xo, s, fg], writes=[x])
                    kb.dma("sp", out[(i - 2) * 128:(i - 1) * 128, :], x[:], reads=[x], writes=[Buf()])


    def conv_tile(l, wt, cw, ncw, jt, Zraw, Zout):
        for ci, (t0, nt) in enumerate(TCH):
            p = PS[ci % 4]
            proj_fm(p, wt, 0, 128, t0, nt)
            kb.op("act", lambda e, p=p, t0=t0, nt=nt: e.copy(Zraw[:, 1 + t0:1 + t0 + nt], p[:, 0:nt]), reads=[p], writes=[Zraw])
        kb.op("dve", lambda e: e.tensor_scalar(Zout[:, :], Zraw[:, 1:T + 1], cw[:, jt, 1:2], None, ALU.mult), reads=[Zraw, cw], writes=[Zout])
        kb.op("dve", lambda e: e.scalar_tensor_tensor(Zout[:, :], Zraw[:, 0:T], cw[:, jt, 0:1], Zout[:, :], ALU.mult, ALU.add),
              reads=[Zraw, cw, Zout], writes=[Zout])
        kb.op("dve", lambda e: e.scalar_tensor_tensor(Zout[:, :], Zraw[:, 2:T + 2], cw[:, jt, 2:3], Zout[:, :], ALU.mult, ALU.add),
              reads=[Zraw, cw, Zout], writes=[Zout])
        kb.op("dve", lambda e: e.scalar_tensor_tensor(Zout[:, C - 1:C], Zraw[:, C + 1:C + 2], ncw[:, jt, 2:3], Zout[:, C - 1:C], ALU.mult, ALU.add),
              reads=[Zraw, ncw, Zout], writes=[Zout])
        kb.op("dve", lambda e: e.scalar_tensor_tensor(Zout[:, C:C + 1], Zraw[:, C:C + 1], ncw[:, jt, 0:1], Zout[:, C:C + 1], ALU.mult, ALU.add),
              reads=[Zraw, ncw, Zout], writes=[Zout])

    def colvec(name, src_ap, srcb, shape, rearr, **kw):
        t = kb.sb(name, shape)
        kb.dma("sp", t[:], src_ap.rearrange(rearr, **kw), reads=[srcb], writes=[t], slow=True)
        return t

    def phase_rwkv_prep(l):
        with kb.scope():
            stage = kb.sb("rstage", [128, 8, 128])
            wts = [kb.sb(f"rwt{i}", [128, 8, 128], BF16) for i in range(2)]
            cw = kb.sb("rcw", [128, 7, 3])
            for k in range(3):
                kb.dma("sp", cw[:, :, k], W["rw_conv"][l, k, :].rearrange("(j p) -> p j", p=128), reads=[W["rw_conv"]], writes=[cw], slow=True)
            ncw = kb.sb("rncw", [128, 7, 3])
            kb.op("dve", lambda e: e.tensor_scalar(ncw[:], cw[:], -1.0, None, ALU.mult), reads=[cw], writes=[ncw])
            kk_ = colvec("rkk", W["rw_k_k"][l, :], W["rw_k_k"], [128, 2], "(j p) -> p j", p=128)
            ka_ = colvec("rka", W["rw_k_a"][l, :], W["rw_k_a"], [128, 2], "(j p) -> p j", p=128)
            omka = kb.sb("romka", [128, 2])
            kb.op("dve", lambda e: e.tensor_scalar(omka[:], ka_[:], -1.0, 1.0, ALU.mult, ALU.add), reads=[ka_], writes=[omka])
            w0_ = kb.sb("rw0", [128, 2, 2])
            a0_ = kb.sb("ra0", [128, 2, 2])
            for d in range(2):
                kb.dma("sp", w0_[:, d, :], W["rw_w0"][l, d, :].rearrange("(j p) -> p j", p=128), reads=[W["rw_w0"]], writes=[w0_], slow=True)
                kb.dma("sp", a0_[:, d, :], W["rw_a0"][l, d, :].rearrange("(j p) -> p j", p=128), reads=[W["rw_a0"]], writes=[a0_], slow=True)
            wup = kb.sb("rwup", [128, 2, 256])
            kb.dma("sp", wup[0:64, :, :], W["rw_w_up"][l, :, :, :].rearrange("d k n -> k d n"), reads=[W["rw_w_up"]], writes=[wup])
            kb.dma("sp", wup[64:128, :, :], W["rw_a_up"][l, :, :, :].rearrange("d k n -> k d n"), reads=[W["rw_a_up"]], writes=[wup])
            Zraw = kb.sb("rZraw", [128, T + 2])
            Zout = kb.sb("rZout", [128, T])
            Z6 = kb.sb("rZ6", [128, T])
            kb.op("pool", lambda e: e.memset(Zraw[:, 0:1], 0.0), writes=[Zraw])
            kb.op("pool", lambda e: e.memset(Zraw[:, T + 1:T + 2], 0.0), writes=[Zraw])
            tA = [kb.sb(f"rtA{i}", [128, 512]) for i in range(2)]
            tB = [kb.sb(f"rtB{i}", [128, 512]) for i in range(2)]
            tC = [kb.sb(f"rtC{i}", [128, 512]) for i in range(2)]
            tD = [kb.sb(f"rtD{i}", [128, 512]) for i in range(2)]
            tE = [kb.sb(f"rtE{i}", [128, 512]) for i in range(2)]
            tG = [kb.sb(f"rtG{i}", [128, 512], BF16) for i in range(2)]
            vt_ = [kb.sb(f"rvt{i}", [128, 128]) for i in range(2)]
            order = [6, 0, 1, 4, 5, 2, 3, 7, 8]
            for oi, jt in enumerate(order):
                wt = wts[oi % 2]
                c0 = RW0 + jt * 128 if jt < 7 else RWG0 + (jt - 7) * 128
                load_w(l, wt, c0, 128, stage)
                if jt >= 7:
                    for ci, (t0, nt) in enumerate(TCH):
                        p = PS[ci % 4]
                        proj_fm(p, wt, 0, 128, t0, nt)
                        g = tG[ci % 2]
                        kb.op("act", lambda e, p=p, g=g, nt=nt: e.activation(g[:, 0:nt], p[:, 0:nt], AF.Silu), reads=[p], writes=[g])
                        kb.dma("sp", RS["SGT"][(jt - 7) * 128:(jt - 6) * 128, t0:t0 + nt], g[:, 0:nt], reads=[g], writes=[Buf()])
                    continue
                conv_tile(l, wt, cw, ncw, jt, Zraw, Z6 if jt == 6 else Zout)
                if jt == 6:
                    kb.op("act", lambda e: e.activation(Z6[0:64, :], Z6[0:64, :], AF.Tanh), reads=[Z6], writes=[Z6])
                elif jt in (0, 1):
                    kb.dma("sp", RS["RT"][jt * 128:(jt + 1) * 128, :], Zout[:, :], reads=[Zout], writes=[Buf()])
                elif jt in (4, 5):
                    kb.dma("sp", RS["VT"][(jt - 4) * 128:(jt - 3) * 128, :], Zout[:, :], reads=[Zout], writes=[Buf()])
                    for i in range(NT):
                        p = PS[4 + i % 2]
                        kb.op("pe", lambda e, p=p, i=i: e.transpose(p[:, 0:128], Zout[:, i * 128:(i + 1) * 128], ident_f[:]),
                              reads=[Zout, ident_f], writes=[p])
                        v = vt_[i % 2]
                        kb.op("act", lambda e, p=p, v=v: e.copy(v[:, :], p[:, 0:128]), reads=[p], writes=[v])
                        kb.dma("pool", RS["VTOK"][i * 128:(i + 1) * 128, (jt - 4) * 128:(jt - 3) * 128], v[:, :], reads=[v], writes=[Buf()])
                else:
                    pt = jt - 2
                    rows = slice(pt * 128, (pt + 1) * 128)
                    for ci, (t0, nt) in enumerate(TCH):
                        a_, b_, c_, d_, e_ = tA[ci % 2], tB[ci % 2], tC[ci % 2], tD[ci % 2], tE[ci % 2]
                        zc = Zout[:, t0:t0 + nt]
                        kb.op("dve", lambda e: e.tensor_scalar(a_[:, 0:nt], zc, kk_[:, pt:pt + 1], None, ALU.mult), reads=[Zout, kk_], writes=[a_])
                        kb.op("act", lambda e: e.activation(b_[:, 0:nt], a_[:, 0:nt], AF.Square), reads=[a_], writes=[b_])
                        p = PS[ci % 2]
                        kb.op("pe", lambda e: e.matmul(p[:, 0:nt], blk64[:], b_[:, 0:nt], start=True, stop=True), reads=[blk64, b_], writes=[p])
                        kb.op("act", lambda e: e.sqrt(b_[:, 0:nt], p[:, 0:nt]), reads=[p], writes=[b_])
                        kb.op("dve", lambda e: e.tensor_scalar(b_[:, 0:nt], b_[:, 0:nt], 1e-12, None, ALU.max), reads=[b_], writes=[b_])
                        kb.op("dve", lambda e: e.reciprocal(b_[:, 0:nt], b_[:, 0:nt]), reads=[b_], writes=[b_])
                        kb.op("dve", lambda e: e.scalar_tensor_tensor(a_[:, 0:nt], a_[:, 0:nt], -1.0, b_[:, 0:nt], ALU.mult, ALU.mult),
                              reads=[a_, b_], writes=[a_])
                        kb.dma("sp", RS["AL"][rows, t0:t0 + nt], a_[:, 0:nt], reads=[a_], writes=[Buf()])
                        for d in range(2):
                            pa = PS[2 + d]
                            kb.op("pe", lambda e: e.matmul(pa[:, 0:nt], wup[64:128, d, pt * 128:(pt + 1) * 128], Z6[64:128, t0:t0 + nt],
                                                           start=True, stop=True), reads=[wup, Z6], writes=[pa])
                            kb.op("act", lambda e: e.activation(c_[:, 0:nt], pa[:, 0:nt], AF.Sigmoid, bias=a0_[:, d, pt:pt + 1]),
                                  reads=[pa, a0_], writes=[c_])
                            kb.op("dve", lambda e: e.scalar_tensor_tensor(d_[:, 0:nt], c_[:, 0:nt], -1.0, a_[:, 0:nt], ALU.mult, ALU.mult),
                                  reads=[c_, a_], writes=[d_])
                            kb.dma("sp", RS[f"B{d}"][rows, t0:t0 + nt], d_[:, 0:nt], reads=[d_], writes=[Buf()])
                            kb.op("dve", lambda e: e.tensor_scalar(c_[:, 0:nt], c_[:, 0:nt], ka_[:, pt:pt + 1], omka[:, pt:pt + 1], ALU.mult, ALU.add),
                                  reads=[c_, ka_, omka], writes=[c_])
                            kb.op("dve", lambda e: e.tensor_tensor(e_[:, 0:nt], c_[:, 0:nt], zc, ALU.mult), reads=[c_, Zout], writes=[e_])
                            kb.dma("pool", RS[f"KD{d}"][rows, t0:t0 + nt], e_[:, 0:nt], reads=[e_], writes=[Buf()])
                            pw = PS[4 + d]
                            kb.op("pe", lambda e: e.matmul(pw[:, 0:nt], wup[0:64, d, pt * 128:(pt + 1) * 128], Z6[0:64, t0:t0 + nt],
                                                           start=True, stop=True), reads=[wup, Z6], writes=[pw])
                            kb.op("act", lambda e: e.activation(c_[:, 0:nt], pw[:, 0:nt], AF.Sigmoid, bias=w0_[:, d, pt:pt + 1]),
                                  reads=[pw, w0_], writes=[c_])
                            kb.op("dve", lambda e: e.tensor_scalar(d_[:, 0:nt], c_[:, 0:nt], -math.exp(-0.5), None, ALU.mult),
                                  reads=[c_], writes=[d_])
                            kb.dma("pool", RS[f"W{d}"][rows, t0:t0 + nt], d_[:, 0:nt], reads=[d_], writes=[Buf()])

    def phase_rwkv_scan(l):
        with kb.scope():
            ST = [kb.sb(f"ST{d}", [128, 2, 64]) for d in range(2)]
            for d in range(2):
                kb.op("pool", lambda e, d=d: e.memset(ST[d][:], 0.0), writes=[ST[d]])
            names = ("AL", "W", "B", "KD", "RT")
            ch = [[{n: kb.sb(f"c{n}{d}{i}", [128, 2, 128]) for n in names} for i in range(2)] for d in range(2)]
            vch = [[kb.sb(f"cV{d}{i}", [128, 256]) for i in range(2)] for d in range(2)]
            t1 = [kb.sb(f"st1{d}", [128, 2, 64]) for d in range(2)]
            t2 = [kb.sb(f"st2{d}", [128, 2, 64]) for d in range(2)]
            ysb = [kb.sb(f"ysb{d}", [64, 512]) for d in range(2)]
            psSA, psV, psY = [PS[0], PS[1]], [PS[2], PS[3]], [PS[4], PS[5]]
            border = [1, 0] + list(range(NT - 1, 1, -1))
            for ci in range(NT):
                cidx = [ci, border[ci]]
                cur = []
                for d in range(2):
                    c0 = cidx[d] * 128
                    tl_ = ch[d][ci % 2]
                    for n in names:
                        src = RS[n if n in ("AL", "RT") else f"{n}{d}"]
                        kb.dma("sp" if d == 0 else "pool", tl_[n][:],
                               src.t.rearrange("(pr q) t -> q pr t", q=128)[:, :, c0:c0 + 128], reads=[src], writes=[tl_[n]])
                    vv = vch[d][ci % 2]
                    kb.dma("sp" if d == 0 else "pool", vv[:], RS["VTOK"][c0:c0 + 128, :], reads=[RS["VTOK"]], writes=[vv])
                    cur.append((tl_, vv))
                for tl in range(128):
                    for d in range(2):
                        col = tl if d == 0 else 127 - tl
                        tl_, vv = cur[d]
                        S_, sa, pv, py = ST[d], psSA[d], psV[d], psY[d]
                        for pr in range(2):
                            for hp in range(2):
                                rows = slice(64 * hp, 64 * hp + 64)
                                kb.op("pe", lambda e, pr=pr, rows=rows: e.matmul(
                                    sa[rows, pr * 64:(pr + 1) * 64], tl_["AL"][rows, pr, col:col + 1].broadcast_to([64, 64]),
                                    S_[rows, pr, :], start=True, stop=True), reads=[tl_["AL"], S_], writes=[sa])
                        for pr in range(2):
                            for hp in range(2):
                                rows = slice(64 * hp, 64 * hp + 64)
                                h = 2 * pr + hp
                                kb.op("pe", lambda e, pr=pr, rows=rows, h=h: e.matmul(
                                    pv[rows, pr * 64:(pr + 1) * 64], ident_f[:, col:col + 1].broadcast_to([128, 64]),
                                    vv[:, h * 64:(h + 1) * 64], start=True, stop=True), reads=[ident_f, vv], writes=[pv])
                        for pr in range(2):
                            kb.op("dve", lambda e, pr=pr: e.tensor_scalar(
                                t1[d][:, pr, :], sa[:, pr * 64:(pr + 1) * 64], tl_["B"][:, pr, col:col + 1], None, ALU.mult),
                                reads=[sa, tl_["B"]], writes=[t1[d]])
                            kb.op("dve", lambda e, pr=pr: e.scalar_tensor_tensor(
                                t2[d][:, pr, :], pv[:, pr * 64:(pr + 1) * 64], tl_["KD"][:, pr, col:col + 1], t1[d][:, pr, :], ALU.mult, ALU.add),
                                reads=[pv, tl_["KD"], t1[d]], writes=[t2[d]])
                            kb.op("dve", lambda e, pr=pr: e.scalar_tensor_tensor(
                                S_[:, pr, :], S_[:, pr, :], tl_["W"][:, pr, col:col + 1], t2[d][:, pr, :], ALU.mult, ALU.add),
                                reads=[S_, tl_["W"], t2[d]], writes=[S_])
                        for pr in range(2):
                            for hp in range(2):
                                rows = slice(64 * hp, 64 * hp + 64)
                                h = 2 * pr + hp
                                kb.op("pe", lambda e, pr=pr, rows=rows, h=h: e.matmul(
                                    py[0:64, h * 128 + col:h * 128 + col + 1], S_[rows, pr, :], tl_["RT"][rows, pr, col:col + 1],
                                    start=True, stop=True), reads=[S_, tl_["RT"]], writes=[py])
                for d in range(2):
                    c0 = cidx[d] * 128
                    kb.op("act", lambda e, d=d: e.copy(ysb[d][:, :], psY[d][0:64, :]), reads=[psY[d]], writes=[ysb[d]])
                    dst = RS["YF" if d == 0 else "YB"]
                    kb.dma("sp", dst.t.rearrange("(h v) t -> v h t", v=64)[:, :, c0:c0 + 128],
                           ysb[d][:, :].rearrange("v (h t) -> v h t", h=4), reads=[ysb[d]], writes=[Buf()])


    def phase_rwkv_chunked(l):
        CH = 64
        NCH = T // CH
        with kb.scope():
            def ldc(nm, shape):
                t = kb.sb("k" + nm, shape)
                kb.dma("sp", t[:], CT[nm].t, reads=[CT[nm]], writes=[t])
                return t
            Ms = ldc("rw_ms", [128, 2, 64]); MTs = ldc("rw_mts", [128, 2, 64]); MTi = ldc("rw_mti", [128, 2, 64])
            id2 = ldc("rw_id2", [128, 64])
            ones = kb.sb("rones", [128, 64])
            kb.op("pool", lambda e: e.memset(ones[:], 1.0), writes=[ones])
            ST = kb.sb("cST", [128, 4, 64])
            kb.op("pool", lambda e: e.memset(ST[:], 0.0), writes=[ST])
            names = ("AL", "W", "B", "KD", "RT")
            def t4(nm, n=2, w=64):
                return [kb.sb(f"{nm}{i}", [128, 4, w]) for i in range(n)]
            IN = {n: t4("ci" + n) for n in names}
            VTK = t4("cVTK")
            CS = t4("cCS", 1)[0]; TOT = kb.sb("cTOT", [128, 4]); TMP = t4("cTMP", 1)[0]
            Epos = t4("cEp", 1)[0]; Eneg = t4("cEn", 1)[0]; Eprev = t4("cEv", 1)[0]; Etot = t4("cEt", 1)[0]; Wtot = kb.sb("cWt", [128, 4])
            Ab = t4("cAb", 1)[0]; Bb = t4("cBb", 1)[0]; Kb = t4("cKb", 1)[0]; Rb = t4("cRb", 1)[0]; Bt = t4("cBt", 1)[0]; Kt = t4("cKt", 1)[0]
            Q = t4("cQ"); P = t4("cP"); ArbT = t4("cArbT", 1)[0]; AkvT = t4("cAkvT", 1)[0]; ArkT = t4("cArkT", 1)[0]
            X = t4("cX", 2, 128); Btok = t4("cBtok", 1)[0]; Ktok = t4("cKtok", 1)[0]
            RAT = t4("cRAT", 1)[0]; McT = t4("cMcT", 1)[0]; NcS = t4("cNcS", 1)[0]; DG = t4("cDG", 1)[0]
            ysb = [kb.sb(f"cysb{d}", [64, 256]) for d in range(2)]
            border = [3, 2, 1, 0] + list(range(NCH - 1, 3, -1))
            DP = [(d, pr) for d in range(2) for pr in range(2)]
            HP = [slice(0, 64), slice(64, 128)]

            def mm_all(ps, col_fn, lhs_fn, rhs_fn, reads, start=True, stop=True, w=None):
                for dp in range(4):
                    for hp in range(2):
                        r = HP[hp]
                        c0, c1 = col_fn(dp)
                        kb.op("pe", lambda e, dp=dp, r=r, c0=c0, c1=c1: e.matmul(ps[r, c0:c1], lhs_fn(dp, r), rhs_fn(dp, r), start=start, stop=stop),
                              reads=reads, writes=[ps])

            for ci in range(NCH):
                cidx = [ci, border[ci]]
                i2 = ci % 2
                for d in range(2):
                    c0 = cidx[d] * CH
                    for n in names:
                        src = RS[n if n in ("AL", "RT") else f"{n}{d}"]
                        kb.dma("sp" if d == 0 else "pool", IN[n][i2][:, 2 * d:2 * d + 2, :],
                               src.t.rearrange("(pr q) t -> q pr t", q=128)[:, :, c0:c0 + CH], reads=[src], writes=[IN[n][i2]])
                    for hp in range(2):
                        kb.dma("sp" if d == 0 else "pool", VTK[i2][HP[hp], 2 * d:2 * d + 2, :],
                               RS["VTOK"][c0:c0 + CH, :].rearrange("t (pr hp v) -> t pr hp v", pr=2, hp=2)[:, :, hp, :],
                               reads=[RS["VTOK"]], writes=[VTK[i2]])
                al, lw, be, kd, rt, vt = IN["AL"][i2], IN["W"][i2], IN["B"][i2], IN["KD"][i2], IN["RT"][i2], VTK[i2]
                if RW_STAGE <= 1:
                    continue
                for dp in range(4):
                    kb.op("dve", lambda e, dp=dp: e.tensor_tensor_scan(CS[:, dp, :], ones[:, :], lw[:, dp, :], 0.0, ALU.mult, ALU.add),
                          reads=[ones, lw], writes=[CS])
                kb.op("dve", lambda e: e.tensor_copy(TOT[:, :], CS[:, :, CH - 1]), reads=[CS], writes=[TOT])
                kb.op("dve", lambda e: e.tensor_tensor(CS[:, 2:4, :], lw[:, 2:4, :], CS[:, 2:4, :], ALU.subtract), reads=[lw, CS], writes=[CS])
                kb.op("dve", lambda e: e.tensor_tensor(CS[:, 2:4, :], CS[:, 2:4, :], TOT[:, 2:4].unsqueeze(2).broadcast_to([128, 2, CH]), ALU.add),
                      reads=[CS, TOT], writes=[CS])
                kb.op("act", lambda e: e.activation(Epos[:], CS[:], AF.Exp), reads=[CS], writes=[Epos])
                kb.op("act", lambda e: e.activation(Eneg[:], CS[:], AF.Exp, scale=-1.0), reads=[CS], writes=[Eneg])
                kb.op("pool", lambda e: e.tensor_tensor(TMP[:], CS[:], lw[:], ALU.subtract), reads=[CS, lw], writes=[TMP])
                kb.op("act", lambda e: e.activation(Eprev[:], TMP[:], AF.Exp), reads=[TMP], writes=[Eprev])
                kb.op("dve", lambda e: e.tensor_tensor(Etot[:], TOT[:, :].unsqueeze(2).broadcast_to([128, 4, CH]), CS[:], ALU.subtract),
                      reads=[TOT, CS], writes=[Etot])
                kb.op("act", lambda e: e.activation(Etot[:], Etot[:], AF.Exp), reads=[Etot], writes=[Etot])
                kb.op("act", lambda e: e.activation(Wtot[:], TOT[:], AF.Exp), reads=[TOT], writes=[Wtot])
                kb.op("dve", lambda e: e.tensor_tensor(Ab[:], al[:], Eprev[:], ALU.mult), reads=[al, Eprev], writes=[Ab])
                kb.op("pool", lambda e: e.tensor_tensor(Bb[:], be[:], Eneg[:], ALU.mult), reads=[be, Eneg], writes=[Bb])
                kb.op("dve", lambda e: e.tensor_tensor(Kb[:], kd[:], Eneg[:], ALU.mult), reads=[kd, Eneg], writes=[Kb])
                kb.op("pool", lambda e: e.tensor_tensor(Rb[:], rt[:], Epos[:], ALU.mult), reads=[rt, Epos], writes=[Rb])
                kb.op("dve", lambda e: e.tensor_tensor(Bt[:], be[:], Etot[:], ALU.mult), reads=[be, Etot], writes=[Bt])
                kb.op("pool", lambda e: e.tensor_tensor(Kt[:], kd[:], Etot[:], ALU.mult), reads=[kd, Etot], writes=[Kt])
                if RW_STAGE <= 2:
                    continue
                PA, PB, PC, PT1, PD, PX, PPQ, PE_ = PS
                mm_all(PA, lambda dp: (dp * 128, dp * 128 + 64), lambda dp, r: Bb[r, dp, :], lambda dp, r: Ab[r, dp, :], [Bb, Ab])
                mm_all(PA, lambda dp: (dp * 128 + 64, dp * 128 + 128), lambda dp, r: Bb[r, dp, :], lambda dp, r: Rb[r, dp, :], [Bb, Rb])
                mm_all(PB, lambda dp: (dp * 128, dp * 128 + 64), lambda dp, r: Kb[r, dp, :], lambda dp, r: Ab[r, dp, :], [Kb, Ab])
                mm_all(PB, lambda dp: (dp * 128 + 64, dp * 128 + 128), lambda dp, r: Kb[r, dp, :], lambda dp, r: Rb[r, dp, :], [Kb, Rb])
                mm_all(PC, lambda dp: (dp * 64, dp * 64 + 64), lambda dp, r: Ab[r, dp, :], lambda dp, r: Bb[r, dp, :], [Ab, Bb])
                q0, p0 = Q[0], P[0]
                pav = PA[:, :].rearrange("p (dp x) -> p dp x", dp=4)
                pbv = PB[:, :].rearrange("p (dp x) -> p dp x", dp=4)
                def mk(m):
                    return m[:, :, :].unsqueeze(2).broadcast_to([128, 2, 2, 64])
                def v4(ap):
                    return ap.rearrange("p (d pr) x -> p d pr x", d=2)
                kb.op("dve", lambda e: e.tensor_tensor(v4(q0[:]), v4(pav[:, :, 0:64]), mk(MTs), ALU.mult), reads=[PA, MTs], writes=[q0])
                kb.op("dve", lambda e: e.tensor_tensor(v4(ArbT[:]), v4(pav[:, :, 64:128]), mk(MTi), ALU.mult), reads=[PA, MTi], writes=[ArbT])
                kb.op("dve", lambda e: e.tensor_tensor(v4(AkvT[:]), v4(pbv[:, :, 0:64]), mk(MTs), ALU.mult), reads=[PB, MTs], writes=[AkvT])
                kb.op("dve", lambda e: e.tensor_tensor(v4(ArkT[:]), v4(pbv[:, :, 64:128]), mk(MTi), ALU.mult), reads=[PB, MTi], writes=[ArkT])
                kb.op("dve", lambda e: e.tensor_tensor(v4(p0[:]), v4(PC[:, 0:256].rearrange("p (dp x) -> p dp x", dp=4)), mk(Ms), ALU.mult),
                      reads=[PC, Ms], writes=[p0])
                if RW_STAGE <= 3:
                    continue
                def idb(r):
                    return ident_f[r, r.start:r.start + 64]
                mm_all(PT1, lambda dp: (dp * 128, dp * 128 + 64), lambda dp, r: Ab[r, dp, :], lambda dp, r: idb(r), [Ab, ident_f])
                mm_all(PT1, lambda dp: (dp * 128 + 64, dp * 128 + 128), lambda dp, r: Bt[r, dp, :], lambda dp, r: idb(r), [Bt, ident_f])
                mm_all(PC, lambda dp: (256 + dp * 64, 256 + dp * 64 + 64), lambda dp, r: Kt[r, dp, :], lambda dp, r: idb(r), [Kt, ident_f])
                x0 = X[0]
                pt1v = PT1[:, :].rearrange("p (dp x) -> p dp x", dp=4)
                kb.op("act", lambda e: e.copy(x0[:, :, 0:64], pt1v[:, :, 0:64]), reads=[PT1], writes=[x0])
                kb.op("act", lambda e: e.copy(Btok[:], pt1v[:, :, 64:128]), reads=[PT1], writes=[Btok])
                kb.op("act", lambda e: e.copy(Ktok[:], PC[:, 256:512].rearrange("p (dp x) -> p dp x", dp=4)), reads=[PC], writes=[Ktok])
                if RW_STAGE <= 4:
                    continue
                mm_all(PD, lambda dp: (dp * 64, dp * 64 + 64), lambda dp, r: AkvT[r, dp, :], lambda dp, r: vt[r, dp, :], [AkvT, vt])
                kb.op("act", lambda e: e.copy(x0[:, :, 64:128], PD[:, 0:256].rearrange("p (dp x) -> p dp x", dp=4)), reads=[PD], writes=[x0])
                if RW_STAGE <= 5:
                    continue
                qc, pc, xc = Q[0], P[0], X[0]
                for j in range(6):
                    qn, pn, xn = Q[(j + 1) % 2], P[(j + 1) % 2], X[(j + 1) % 2]
                    mm_all(PX, lambda dp: (dp * 128, dp * 128 + 128), lambda dp, r: qc[r, dp, :], lambda dp, r: xc[r, dp, :], [qc, xc])
                    kb.op("dve", lambda e, xn=xn, xc=xc: e.tensor_tensor(xn[:], xc[:], PX[:, :].rearrange("p (dp x) -> p dp x", dp=4), ALU.add),
                          reads=[xc, PX], writes=[xn])
                    if j < 5:
                        mm_all(PPQ, lambda dp: (dp * 64, dp * 64 + 64), lambda dp, r: qc[r, dp, :], lambda dp, r: pc[r, dp, :], [qc, pc])
                        mm_all(PPQ, lambda dp: (256 + dp * 64, 256 + dp * 64 + 64), lambda dp, r: pc[r, dp, :], lambda dp, r: qc[r, dp, :], [qc, pc])
                        kb.op("act", lambda e, pn=pn: e.copy(pn[:], PPQ[:, 0:256].rearrange("p (dp x) -> p dp x", dp=4)), reads=[PPQ], writes=[pn])
                        kb.op("act", lambda e, qn=qn: e.copy(qn[:], PPQ[:, 256:512].rearrange("p (dp x) -> p dp x", dp=4)), reads=[PPQ], writes=[qn])
                    qc, pc, xc = qn, pn, xn
                if RW_STAGE <= 6:
                    continue
                mm_all(PD, lambda dp: (256 + dp * 64, 256 + dp * 64 + 64), lambda dp, r: xc[r, dp, 0:64], lambda dp, r: ArbT[r, dp, :], [xc, ArbT])
                kb.op("dve", lambda e: e.tensor_tensor(RAT[:], Rb[:], PD[:, 256:512].rearrange("p (dp x) -> p dp x", dp=4), ALU.add),
                      reads=[Rb, PD], writes=[RAT])
                mm_all(PE_, lambda dp: (dp * 64, dp * 64 + 64), lambda dp, r: xc[r, dp, 0:64], lambda dp, r: Btok[r, dp, :], [xc, Btok])
                kb.op("pool", lambda e: e.tensor_tensor(DG[:], id2[:, :].unsqueeze(1).broadcast_to([128, 4, 64]),
                                                        Wtot[:, :].unsqueeze(2).broadcast_to([128, 4, 64]), ALU.mult), reads=[id2, Wtot], writes=[DG])
                kb.op("dve", lambda e: e.tensor_tensor(McT[:], DG[:], PE_[:, 0:256].rearrange("p (dp x) -> p dp x", dp=4), ALU.add),
                      reads=[DG, PE_], writes=[McT])
                for dp in range(4):
                    for hp in range(2):
                        r = HP[hp]
                        c0 = 256 + dp * 64
                        kb.op("pe", lambda e, dp=dp, r=r, c0=c0: e.matmul(PE_[r, c0:c0 + 64], Btok[r, dp, :], xc[r, dp, 64:128], start=True, stop=False),
                              reads=[Btok, xc], writes=[PE_])
                        kb.op("pe", lambda e, dp=dp, r=r, c0=c0: e.matmul(PE_[r, c0:c0 + 64], Ktok[r, dp, :], vt[r, dp, :], start=False, stop=True),
                              reads=[Ktok, vt], writes=[PE_])
                kb.op("act", lambda e: e.copy(NcS[:], PE_[:, 256:512].rearrange("p (dp x) -> p dp x", dp=4)), reads=[PE_], writes=[NcS])
                if RW_STAGE <= 7:
                    continue
                PYs = [PA, PT1]
                for dp in range(4):
                    for hp in range(2):
                        r = HP[hp]
                        PY = PYs[hp]
                        c0 = dp * 64
                        kb.op("pe", lambda e, dp=dp, r=r, c0=c0, PY=PY: e.matmul(PY[0:64, c0:c0 + 64], ST[r, dp, :], RAT[r, dp, :], start=True, stop=False),
                              reads=[ST, RAT], writes=[PY])
                        kb.op("pe", lambda e, dp=dp, r=r, c0=c0, PY=PY: e.matmul(PY[0:64, c0:c0 + 64], xc[r, dp, 64:128], ArbT[r, dp, :], start=False, stop=False),
                              reads=[xc, ArbT], writes=[PY])
                        kb.op("pe", lambda e, dp=dp, r=r, c0=c0, PY=PY: e.matmul(PY[0:64, c0:c0 + 64], vt[r, dp, :], ArkT[r, dp, :], start=False, stop=True),
                              reads=[vt, ArkT], writes=[PY])
                for d in range(2):
                    c0 = cidx[d] * CH
                    yv = ysb[d][:, :].rearrange("v (pr hp t) -> v pr hp t", pr=2, hp=2)
                    for hp in range(2):
                        kb.op("act", lambda e, d=d, hp=hp, yv=yv: e.copy(
                            yv[:, :, hp, :], PYs[hp][0:64, d * 128:(d + 1) * 128].rearrange("v (pr t) -> v pr t", pr=2)), reads=[PYs[hp]], writes=[ysb[d]])
                    dst = RS["YF" if d == 0 else "YB"]
                    kb.dma("sp", dst.t.rearrange("(h v) t -> v h t", v=64)[:, :, c0:c0 + CH],
                           ysb[d][:, :].rearrange("v (h t) -> v h t", h=4), reads=[ysb[d]], writes=[Buf()])
                if RW_STAGE <= 8:
                    continue
                PSS = PB
                mm_all(PSS, lambda dp: (dp * 64, dp * 64 + 64), lambda dp, r: McT[r, dp, :], lambda dp, r: ST[r, dp, :], [McT, ST])
                kb.op("dve", lambda e: e.tensor_tensor(ST[:], NcS[:], PSS[:, 0:256].rearrange("p (dp x) -> p dp x", dp=4), ALU.add),
                      reads=[NcS, PSS], writes=[ST])


    def phase_rwkv_chunked3(l):
        CH = 64
        NCH = T // CH
        with kb.scope():
            def ldc(nm, shape):
                t = kb.sb("k" + nm, shape)
                kb.dma("sp", t[:], CT[nm].t, reads=[CT[nm]], writes=[t])
                return t
            Ms = ldc("rw_ms", [128, 2, 64]); MTs = ldc("rw_mts", [128, 2, 64]); MTi = ldc("rw_mti", [128, 2, 64])
            id2 = ldc("rw_id2", [128, 64])
            ones = kb.sb("rones", [128, 64])
            kb.op("pool", lambda e: e.memset(ones[:], 1.0), writes=[ones])
            ST = kb.sb("cST", [128, 4, 64])
            kb.op("pool", lambda e: e.memset(ST[:], 0.0), writes=[ST])
            names = ("AL", "W", "B", "KD", "RT")
            import types
            def alloc_set(si):
                S = types.SimpleNamespace()
                def t4(nm, n=2, w=64):
                    return [kb.sb(f"{nm}s{si}_{i}", [128, 4, w]) for i in range(n)]
                S.IN = {n: t4("ci" + n, 1)[0] for n in names}
                S.VTK = t4("cVTK", 1)[0]
                S.CS = t4("cCS", 1)[0]; S.TOT = kb.sb(f"cTOT{si}", [128, 4]); S.TMP = t4("cTMP", 1)[0]
                S.Epos = t4("cEp", 1)[0]; S.Eneg = t4("cEn", 1)[0]; S.Eprev = t4("cEv", 1)[0]; S.Etot = t4("cEt", 1)[0]; S.Wtot = kb.sb(f"cWt{si}", [128, 4])
                S.Ab = t4("cAb", 1)[0]; S.Bb = t4("cBb", 1)[0]; S.Kb = t4("cKb", 1)[0]; S.Rb = t4("cRb", 1)[0]; S.Bt = t4("cBt", 1)[0]; S.Kt = t4("cKt", 1)[0]
                S.Q = t4("cQ"); S.P = t4("cP"); S.ArbT = t4("cArbT", 1)[0]; S.AkvT = t4("cAkvT", 1)[0]; S.ArkT = t4("cArkT", 1)[0]
                S.X = t4("cX", 2, 128); S.Btok = t4("cBtok", 1)[0]; S.Ktok = t4("cKtok", 1)[0]
                S.RAT = t4("cRAT", 1)[0]; S.McT = t4("cMcT", 1)[0]; S.NcS = t4("cNcS", 1)[0]; S.DG = t4("cDG", 1)[0]
                S.ysb = [kb.sb(f"cysb{si}_{d}", [64, 256]) for d in range(2)]
                S.banks = PS[4 * si:4 * si + 4]
                return S
            SETS = [alloc_set(0), alloc_set(1)]
            border = [3, 2, 1, 0] + list(range(NCH - 1, 3, -1))
            DP = [(d, pr) for d in range(2) for pr in range(2)]
            HP = [slice(0, 64), slice(64, 128)]

            def mm_all(ps, col_fn, lhs_fn, rhs_fn, reads, start=True, stop=True, w=None):
                for dp in range(4):
                    for hp in range(2):
                        r = HP[hp]
                        c0, c1 = col_fn(dp)
                        kb.op("pe", lambda e, dp=dp, r=r, c0=c0, c1=c1: e.matmul(ps[r, c0:c1], lhs_fn(dp, r), rhs_fn(dp, r), start=start, stop=stop),
                              reads=reads, writes=[ps])

            def chunk_gen(ci, S):
                cidx = [ci, border[ci]]
                IN, VTK = S.IN, S.VTK
                for d in range(2):
                    c0 = cidx[d] * CH
                    for n in names:
                        src = RS[n if n in ("AL", "RT") else f"{n}{d}"]
                        kb.dma("sp" if d == 0 else "pool", IN[n][:, 2 * d:2 * d + 2, :],
                               src.t.rearrange("(pr q) t -> q pr t", q=128)[:, :, c0:c0 + CH], reads=[src], writes=[IN[n]])
                    for hp in range(2):
                        kb.dma("sp" if d == 0 else "pool", VTK[HP[hp], 2 * d:2 * d + 2, :],
                               RS["VTOK"][c0:c0 + CH, :].rearrange("t (pr hp v) -> t pr hp v", pr=2, hp=2)[:, :, hp, :],
                               reads=[RS["VTOK"]], writes=[VTK])
                al, lw, be, kd, rt, vt = IN["AL"], IN["W"], IN["B"], IN["KD"], IN["RT"], VTK
                CS, TOT, TMP, Epos, Eneg, Eprev, Etot, Wtot = S.CS, S.TOT, S.TMP, S.Epos, S.Eneg, S.Eprev, S.Etot, S.Wtot
                Ab, Bb, Kb, Rb, Bt, Kt, Q, P, ArbT, AkvT, ArkT = S.Ab, S.Bb, S.Kb, S.Rb, S.Bt, S.Kt, S.Q, S.P, S.ArbT, S.AkvT, S.ArkT
                X, Btok, Ktok, RAT, McT, NcS, DG, ysb = S.X, S.Btok, S.Ktok, S.RAT, S.McT, S.NcS, S.DG, S.ysb
                yield
                for dp in range(4):
                    kb.op("dve", lambda e, dp=dp: e.tensor_tensor_scan(CS[:, dp, :], ones[:, :], lw[:, dp, :], 0.0, ALU.mult, ALU.add),
                          reads=[ones, lw], writes=[CS])
                kb.op("dve", lambda e: e.tensor_copy(TOT[:, :], CS[:, :, CH - 1]), reads=[CS], writes=[TOT])
                kb.op("dve", lambda e: e.tensor_tensor(CS[:, 2:4, :], lw[:, 2:4, :], CS[:, 2:4, :], ALU.subtract), reads=[lw, CS], writes=[CS])
                kb.op("dve", lambda e: e.tensor_tensor(CS[:, 2:4, :], CS[:, 2:4, :], TOT[:, 2:4].unsqueeze(2).broadcast_to([128, 2, CH]), ALU.add),
                      reads=[CS, TOT], writes=[CS])
                kb.op("act", lambda e: e.activation(Epos[:], CS[:], AF.Exp), reads=[CS], writes=[Epos])
                kb.op("act", lambda e: e.activation(Eneg[:], CS[:], AF.Exp, scale=-1.0), reads=[CS], writes=[Eneg])
                kb.op("pool", lambda e: e.tensor_tensor(TMP[:], CS[:], lw[:], ALU.subtract), reads=[CS, lw], writes=[TMP])
                kb.op("act", lambda e: e.activation(Eprev[:], TMP[:], AF.Exp), reads=[TMP], writes=[Eprev])
                kb.op("dve", lambda e: e.tensor_tensor(Etot[:], TOT[:, :].unsqueeze(2).broadcast_to([128, 4, CH]), CS[:], ALU.subtract),
                      reads=[TOT, CS], writes=[Etot])
                kb.op("act", lambda e: e.activation(Etot[:], Etot[:], AF.Exp), reads=[Etot], writes=[Etot])
                kb.op("act", lambda e: e.activation(Wtot[:], TOT[:], AF.Exp), reads=[TOT], writes=[Wtot])
                kb.op("dve", lambda e: e.tensor_tensor(Ab[:], al[:], Eprev[:], ALU.mult), reads=[al, Eprev], writes=[Ab])
                kb.op("pool", lambda e: e.tensor_tensor(Bb[:], be[:], Eneg[:], ALU.mult), reads=[be, Eneg], writes=[Bb])
                kb.op("dve", lambda e: e.tensor_tensor(Kb[:], kd[:], Eneg[:], ALU.mult), reads=[kd, Eneg], writes=[Kb])
                kb.op("pool", lambda e: e.tensor_tensor(Rb[:], rt[:], Epos[:], ALU.mult), reads=[rt, Epos], writes=[Rb])
                kb.op("dve", lambda e: e.tensor_tensor(Bt[:], be[:], Etot[:], ALU.mult), reads=[be, Etot], writes=[Bt])
                kb.op("pool", lambda e: e.tensor_tensor(Kt[:], kd[:], Etot[:], ALU.mult), reads=[kd, Etot], writes=[Kt])
                yield
                PA, PB, PC, PT1 = S.banks
                PD, PX, PPQ, PE_ = PA, PB, PC, PT1
                mm_all(PA, lambda dp: (dp * 128, dp * 128 + 64), lambda dp, r: Bb[r, dp, :], lambda dp, r: Ab[r, dp, :], [Bb, Ab])
                mm_all(PA, lambda dp: (dp * 128 + 64, dp * 128 + 128), lambda dp, r: Bb[r, dp, :], lambda dp, r: Rb[r, dp, :], [Bb, Rb])
                mm_all(PB, lambda dp: (dp * 128, dp * 128 + 64), lambda dp, r: Kb[r, dp, :], lambda dp, r: Ab[r, dp, :], [Kb, Ab])
                mm_all(PB, lambda dp: (dp * 128 + 64, dp * 128 + 128), lambda dp, r: Kb[r, dp, :], lambda dp, r: Rb[r, dp, :], [Kb, Rb])
                mm_all(PC, lambda dp: (dp * 64, dp * 64 + 64), lambda dp, r: Ab[r, dp, :], lambda dp, r: Bb[r, dp, :], [Ab, Bb])
                q0, p0 = Q[0], P[0]
                pav = PA[:, :].rearrange("p (dp x) -> p dp x", dp=4)
                pbv = PB[:, :].rearrange("p (dp x) -> p dp x", dp=4)
                def mk(m):
                    return m[:, :, :].unsqueeze(2).broadcast_to([128, 2, 2, 64])
                def v4(ap):
                    return ap.rearrange("p (d pr) x -> p d pr x", d=2)
                kb.op("dve", lambda e: e.tensor_tensor(v4(q0[:]), v4(pav[:, :, 0:64]), mk(MTs), ALU.mult), reads=[PA, MTs], writes=[q0])
                kb.op("dve", lambda e: e.tensor_tensor(v4(ArbT[:]), v4(pav[:, :, 64:128]), mk(MTi), ALU.mult), reads=[PA, MTi], writes=[ArbT])
                kb.op("dve", lambda e: e.tensor_tensor(v4(AkvT[:]), v4(pbv[:, :, 0:64]), mk(MTs), ALU.mult), reads=[PB, MTs], writes=[AkvT])
                kb.op("dve", lambda e: e.tensor_tensor(v4(ArkT[:]), v4(pbv[:, :, 64:128]), mk(MTi), ALU.mult), reads=[PB, MTi], writes=[ArkT])
                kb.op("dve", lambda e: e.tensor_tensor(v4(p0[:]), v4(PC[:, 0:256].rearrange("p (dp x) -> p dp x", dp=4)), mk(Ms), ALU.mult),
                      reads=[PC, Ms], writes=[p0])
                yield
                def idb(r):
                    return ident_f[r, r.start:r.start + 64]
                mm_all(PT1, lambda dp: (dp * 128, dp * 128 + 64), lambda dp, r: Ab[r, dp, :], lambda dp, r: idb(r), [Ab, ident_f])
                mm_all(PT1, lambda dp: (dp * 128 + 64, dp * 128 + 128), lambda dp, r: Bt[r, dp, :], lambda dp, r: idb(r), [Bt, ident_f])
                mm_all(PC, lambda dp: (256 + dp * 64, 256 + dp * 64 + 64), lambda dp, r: Kt[r, dp, :], lambda dp, r: idb(r), [Kt, ident_f])
                x0 = X[0]
                pt1v = PT1[:, :].rearrange("p (dp x) -> p dp x", dp=4)
                kb.op("act", lambda e: e.copy(x0[:, :, 0:64], pt1v[:, :, 0:64]), reads=[PT1], writes=[x0])
                kb.op("act", lambda e: e.copy(Btok[:], pt1v[:, :, 64:128]), reads=[PT1], writes=[Btok])
                kb.op("act", lambda e: e.copy(Ktok[:], PC[:, 256:512].rearrange("p (dp x) -> p dp x", dp=4)), reads=[PC], writes=[Ktok])
                yield
                mm_all(PD, lambda dp: (dp * 64, dp * 64 + 64), lambda dp, r: AkvT[r, dp, :], lambda dp, r: vt[r, dp, :], [AkvT, vt])
                kb.op("act", lambda e: e.copy(x0[:, :, 64:128], PD[:, 0:256].rearrange("p (dp x) -> p dp x", dp=4)), reads=[PD], writes=[x0])
                yield
                qc, pc, xc = Q[0], P[0], X[0]
                for j in range(6):
                    qn, pn, xn = Q[(j + 1) % 2], P[(j + 1) % 2], X[(j + 1) % 2]
                    mm_all(PX, lambda dp: (dp * 128, dp * 128 + 128), lambda dp, r: qc[r, dp, :], lambda dp, r: xc[r, dp, :], [qc, xc])
                    kb.op("dve", lambda e, xn=xn, xc=xc: e.tensor_tensor(xn[:], xc[:], PX[:, :].rearrange("p (dp x) -> p dp x", dp=4), ALU.add),
                          reads=[xc, PX], writes=[xn])
                    if j < 5:
                        mm_all(PPQ, lambda dp: (dp * 64, dp * 64 + 64), lambda dp, r: qc[r, dp, :], lambda dp, r: pc[r, dp, :], [qc, pc])
                        mm_all(PPQ, lambda dp: (256 + dp * 64, 256 + dp * 64 + 64), lambda dp, r: pc[r, dp, :], lambda dp, r: qc[r, dp, :], [qc, pc])
                        kb.op("act", lambda e, pn=pn: e.copy(pn[:], PPQ[:, 0:256].rearrange("p (dp x) -> p dp x", dp=4)), reads=[PPQ], writes=[pn])
                        kb.op("act", lambda e, qn=qn: e.copy(qn[:], PPQ[:, 256:512].rearrange("p (dp x) -> p dp x", dp=4)), reads=[PPQ], writes=[qn])
                    qc, pc, xc = qn, pn, xn
                    yield
                yield
                mm_all(PD, lambda dp: (256 + dp * 64, 256 + dp * 64 + 64), lambda dp, r: xc[r, dp, 0:64], lambda dp, r: ArbT[r, dp, :], [xc, ArbT])
                kb.op("dve", lambda e: e.tensor_tensor(RAT[:], Rb[:], PD[:, 256:512].rearrange("p (dp x) -> p dp x", dp=4), ALU.add),
                      reads=[Rb, PD], writes=[RAT])
                mm_all(PE_, lambda dp: (dp * 64, dp * 64 + 64), lambda dp, r: xc[r, dp, 0:64], lambda dp, r: Btok[r, dp, :], [xc, Btok])
                kb.op("pool", lambda e: e.tensor_tensor(DG[:], id2[:, :].unsqueeze(1).broadcast_to([128, 4, 64]),
                                                        Wtot[:, :].unsqueeze(2).broadcast_to([128, 4, 64]), ALU.mult), reads=[id2, Wtot], writes=[DG])
                kb.op("dve", lambda e: e.tensor_tensor(McT[:], DG[:], PE_[:, 0:256].rearrange("p (dp x) -> p dp x", dp=4), ALU.add),
                      reads=[DG, PE_], writes=[McT])
                for dp in range(4):
                    for hp in range(2):
                        r = HP[hp]
                        c0 = 256 + dp * 64
                        kb.op("pe", lambda e, dp=dp, r=r, c0=c0: e.matmul(PE_[r, c0:c0 + 64], Btok[r, dp, :], xc[r, dp, 64:128], start=True, stop=False),
                              reads=[Btok, xc], writes=[PE_])
                        kb.op("pe", lambda e, dp=dp, r=r, c0=c0: e.matmul(PE_[r, c0:c0 + 64], Ktok[r, dp, :], vt[r, dp, :], start=False, stop=True),
                              reads=[Ktok, vt], writes=[PE_])
                kb.op("act", lambda e: e.copy(NcS[:], PE_[:, 256:512].rearrange("p (dp x) -> p dp x", dp=4)), reads=[PE_], writes=[NcS])
                yield
                PYs = [PA, PB]
                for dp in range(4):
                    for hp in range(2):
                        r = HP[hp]
                        PY = PYs[hp]
                        c0 = dp * 64
                        kb.op("pe", lambda e, dp=dp, r=r, c0=c0, PY=PY: e.matmul(PY[0:64, c0:c0 + 64], ST[r, dp, :], RAT[r, dp, :], start=True, stop=False),
                              reads=[ST, RAT], writes=[PY])
                        kb.op("pe", lambda e, dp=dp, r=r, c0=c0, PY=PY: e.matmul(PY[0:64, c0:c0 + 64], xc[r, dp, 64:128], ArbT[r, dp, :], start=False, stop=False),
                              reads=[xc, ArbT], writes=[PY])
                        kb.op("pe", lambda e, dp=dp, r=r, c0=c0, PY=PY: e.matmul(PY[0:64, c0:c0 + 64], vt[r, dp, :], ArkT[r, dp, :], start=False, stop=True),
                              reads=[vt, ArkT], writes=[PY])
                for d in range(2):
                    c0 = cidx[d] * CH
                    yv = ysb[d][:, :].rearrange("v (pr hp t) -> v pr hp t", pr=2, hp=2)
                    for hp in range(2):
                        kb.op("act", lambda e, d=d, hp=hp, yv=yv: e.copy(
                            yv[:, :, hp, :], PYs[hp][0:64, d * 128:(d + 1) * 128].rearrange("v (pr t) -> v pr t", pr=2)), reads=[PYs[hp]], writes=[ysb[d]])
                    dst = RS["YF" if d == 0 else "YB"]
                    kb.dma("sp", dst.t.rearrange("(h v) t -> v h t", v=64)[:, :, c0:c0 + CH],
                           ysb[d][:, :].rearrange("v (h t) -> v h t", h=4), reads=[ysb[d]], writes=[Buf()])
                PSS = PT1
                mm_all(PSS, lambda dp: (dp * 64, dp * 64 + 64), lambda dp, r: McT[r, dp, :], lambda dp, r: ST[r, dp, :], [McT, ST])
                kb.op("dve", lambda e: e.tensor_tensor(ST[:], NcS[:], PSS[:, 0:256].rearrange("p (dp x) -> p dp x", dp=4), ALU.add),
                      reads=[NcS, PSS], writes=[ST])


            def lockstep(gens):
                gens = list(gens)
                while gens:
                    nxt = []
                    for g_ in gens:
                        try:
                            next(g_)
                            nxt.append(g_)
                        except StopIteration:
                            pass
                    gens = nxt
            for ci in range(0, NCH, 2):
                lockstep([chunk_gen(ci, SETS[0]), chunk_gen(ci + 1, SETS[1])])

    def phase_rwkv_chunked2(l):
        CH = 64
        NCH = T // CH
        with kb.scope():
            def ldc(nm, shape):
                t = kb.sb("k" + nm, shape)
                kb.dma("sp", t[:], CT[nm].t, reads=[CT[nm]], writes=[t])
                return t
            MsB = ldc("rw_msb", [128, 2, 128]); MTsB = ldc("rw_mtsb", [128, 2, 128]); MTi = ldc("rw_mti", [128, 2, 64])
            identr = kb.sb("cidr", [128, 128], F32R)
            kb.op("dve", lambda e: e.tensor_copy(identr[:], ident_f[:]), reads=[ident_f], writes=[identr])
            ones = kb.sb("rones", [128, 64])
            kb.op("pool", lambda e: e.memset(ones[:], 1.0), writes=[ones])
            def bd(nm, n=1, dt=F32R):
                ts = [kb.sb(f"{nm}{i}", [128, 4, 128], dt) for i in range(n)]
                for t in ts:
                    kb.op("pool", lambda e, t=t: e.memset(t[:].bitcast(F32) if dt == F32R else t[:], 0.0), writes=[t])
                return ts
            def t4(nm, n=1, w=64, dt=F32):
                return [kb.sb(f"{nm}{i}", [128, 4, w], dt) for i in range(n)]
            f32 = lambda ap: ap.bitcast(F32)
            names = ("AL", "W", "B", "KD", "RT")
            IN = {n: t4("di" + n, 2) for n in names}
            VT = bd("dVT", 2, F32)
            VTr = bd("dVTr")[0]
            ST = bd("dST")[0]
            CS = t4("dCS")[0]; TOT = kb.sb("dTOT", [128, 4]); TMP = t4("dTMP")[0]
            Epos = t4("dEp")[0]; Eneg = t4("dEn")[0]; Eprev = t4("dEv")[0]; Etot = t4("dEt")[0]; Wtot = kb.sb("dWt", [128, 4])
            Ab = bd("dAb")[0]; Bb = bd("dBb")[0]; Kb = bd("dKb")[0]; Bt = bd("dBt")[0]; Kt = bd("dKt")[0]
            Rb = t4("dRb", 1, 64, F32R)[0]
            Q = bd("dQ", 2); P = bd("dP", 2); AkvT = bd("dAkvT")[0]
            ArbT = t4("dArbT", 1, 64, F32R)[0]; ArkT = t4("dArkT", 1, 64, F32R)[0]; RAT = t4("dRAT", 1, 64, F32R)[0]
            X = [kb.sb(f"dX{i}", [128, 4, 256], F32R) for i in range(2)]
            Btok = bd("dBtok")[0]; Ktok = bd("dKtok")[0]; McT = bd("dMcT")[0]
            NcS = bd("dNcS", 1, F32)[0]; DG = bd("dDG", 1, F32)[0]
            ysb = [kb.sb(f"dysb{d}", [128, 2, 64]) for d in range(2)]
            border = [3, 2, 1, 0] + list(range(NCH - 1, 3, -1))
            H0, H1 = slice(0, 64), slice(64, 128)
            B0, B1, B2, B3, B4, B5, B6, B7 = PS

            def mm4(ps, c0, w, lhs, rhs, reads, start=True, stop=True):
                for dp in range(4):
                    kb.op("pe", lambda e, dp=dp: e.matmul(ps[:, c0 + dp * w:c0 + (dp + 1) * w], lhs(dp), rhs(dp), start=start, stop=stop),
                          reads=reads, writes=[ps])

            def v4(ap):
                return ap.rearrange("p (d pr) x -> p d pr x", d=2)

            def mk(m, w):
                return m[:, :, :].unsqueeze(2).broadcast_to([128, 2, 2, w])

            def pv(ps, c0, w):
                return ps[:, c0:c0 + 4 * w].rearrange("p (dp x) -> p dp x", dp=4)

            for ci in range(NCH):
                cidx = [ci, border[ci]]
                i2 = ci % 2
                vt = VT[i2]
                for d in range(2):
                    c0 = cidx[d] * CH
                    q_ = "sp" if d == 0 else "pool"
                    for n in names:
                        src = RS[n if n in ("AL", "RT") else f"{n}{d}"]
                        kb.dma(q_, IN[n][i2][:, 2 * d:2 * d + 2, :],
                               src.t.rearrange("(pr q) t -> q pr t", q=128)[:, :, c0:c0 + CH], reads=[src], writes=[IN[n][i2]])
                    for hp in range(2):
                        kb.dma(q_, vt[hp * 64:(hp + 1) * 64, 2 * d:2 * d + 2, hp * 64:(hp + 1) * 64],
                               RS["VTOK"][c0:c0 + CH, :].rearrange("t (pr hp v) -> t pr hp v", pr=2, hp=2)[:, :, hp, :],
                               reads=[RS["VTOK"]], writes=[vt])
                al, lw, be, kd, rt = IN["AL"][i2], IN["W"][i2], IN["B"][i2], IN["KD"][i2], IN["RT"][i2]
                kb.op("act", lambda e: e.copy(VTr[:], vt[:]), reads=[vt], writes=[VTr])
                for dp in range(4):
                    kb.op("dve", lambda e, dp=dp: e.tensor_tensor_scan(CS[:, dp, :], ones[:, :], lw[:, dp, :], 0.0, ALU.mult, ALU.add),
                          reads=[ones, lw], writes=[CS])
                kb.op("dve", lambda e: e.tensor_copy(TOT[:, :], CS[:, :, CH - 1]), reads=[CS], writes=[TOT])
                kb.op("dve", lambda e: e.tensor_tensor(CS[:, 2:4, :], lw[:, 2:4, :], CS[:, 2:4, :], ALU.subtract), reads=[lw, CS], writes=[CS])
                kb.op("dve", lambda e: e.tensor_tensor(CS[:, 2:4, :], CS[:, 2:4, :], TOT[:, 2:4].unsqueeze(2).broadcast_to([128, 2, CH]), ALU.add),
                      reads=[CS, TOT], writes=[CS])
                kb.op("act", lambda e: e.activation(Epos[:], CS[:], AF.Exp), reads=[CS], writes=[Epos])
                kb.op("act", lambda e: e.activation(Eneg[:], CS[:], AF.Exp, scale=-1.0), reads=[CS], writes=[Eneg])
                kb.op("pool", lambda e: e.tensor_tensor(TMP[:], CS[:], lw[:], ALU.subtract), reads=[CS, lw], writes=[TMP])
                kb.op("act", lambda e: e.activation(Eprev[:], TMP[:], AF.Exp), reads=[TMP], writes=[Eprev])
                kb.op("pool", lambda e: e.tensor_tensor(Etot[:], TOT[:, :].unsqueeze(2).broadcast_to([128, 4, CH]), CS[:], ALU.subtract),
                      reads=[TOT, CS], writes=[Etot])
                kb.op("act", lambda e: e.activation(Etot[:], Etot[:], AF.Exp), reads=[Etot], writes=[Etot])
                kb.op("act", lambda e: e.activation(Wtot[:], TOT[:], AF.Exp), reads=[TOT], writes=[Wtot])
                for k_, (dst, a_, b_) in enumerate(((Ab, al, Eprev), (Bb, be, Eneg), (Kb, kd, Eneg), (Bt, be, Etot), (Kt, kd, Etot))):
                    for hi, r in enumerate((H0, H1)):
                        eng = "dve" if (k_ + hi) % 2 == 0 else "pool"
                        kb.op(eng, lambda e, dst=dst, a_=a_, b_=b_, r=r: e.tensor_tensor(dst[r, :, r.start:r.start + 64], a_[r, :, :], b_[r, :, :], ALU.mult),
                              reads=[a_, b_], writes=[dst])
                kb.op("pool", lambda e: e.tensor_tensor(Rb[:], rt[:], Epos[:], ALU.mult), reads=[rt, Epos], writes=[Rb])
                mm4(B0, 0, 128, lambda dp: Bb[:, dp, :], lambda dp: Ab[:, dp, :], [Bb, Ab])
                mm4(B1, 0, 128, lambda dp: Kb[:, dp, :], lambda dp: Ab[:, dp, :], [Kb, Ab])
                mm4(B2, 0, 128, lambda dp: Ab[:, dp, :], lambda dp: Bb[:, dp, :], [Ab, Bb])
                mm4(B3, 0, 64, lambda dp: Bb[:, dp, :], lambda dp: Rb[:, dp, :], [Bb, Rb])
                mm4(B3, 256, 64, lambda dp: Kb[:, dp, :], lambda dp: Rb[:, dp, :], [Kb, Rb])
                q0, p0, x0 = Q[0], P[0], X[0]
                kb.op("dve", lambda e: e.tensor_tensor(v4(q0[:]), v4(pv(B0, 0, 128)), mk(MTsB, 128), ALU.mult), reads=[B0, MTsB], writes=[q0])
                kb.op("dve", lambda e: e.tensor_tensor(v4(AkvT[:]), v4(pv(B1, 0, 128)), mk(MTsB, 128), ALU.mult), reads=[B1, MTsB], writes=[AkvT])
                kb.op("dve", lambda e: e.tensor_tensor(v4(p0[:]), v4(pv(B2, 0, 128)), mk(MsB, 128), ALU.mult), reads=[B2, MsB], writes=[p0])
                kb.op("dve", lambda e: e.tensor_tensor(v4(ArbT[:]), v4(pv(B3, 0, 64)), mk(MTi, 64), ALU.mult), reads=[B3, MTi], writes=[ArbT])
                kb.op("dve", lambda e: e.tensor_tensor(v4(ArkT[:]), v4(pv(B3, 256, 64)), mk(MTi, 64), ALU.mult), reads=[B3, MTi], writes=[ArkT])
                mm4(B4, 0, 128, lambda dp: Ab[:, dp, :], lambda dp: identr[:, :], [Ab, identr])
                mm4(B6, 0, 128, lambda dp: Bt[:, dp, :], lambda dp: identr[:, :], [Bt, identr])
                mm4(B7, 0, 128, lambda dp: Kt[:, dp, :], lambda dp: identr[:, :], [Kt, identr])
                mm4(B5, 0, 128, lambda dp: AkvT[:, dp, :], lambda dp: VTr[:, dp, :], [AkvT, VTr])
                kb.op("act", lambda e: e.copy(x0[:, :, 0:128], pv(B4, 0, 128)), reads=[B4], writes=[x0])
                kb.op("act", lambda e: e.copy(Btok[:], pv(B6, 0, 128)), reads=[B6], writes=[Btok])
                kb.op("act", lambda e: e.copy(Ktok[:], pv(B7, 0, 128)), reads=[B7], writes=[Ktok])
                kb.op("act", lambda e: e.copy(x0[:, :, 128:256], pv(B5, 0, 128)), reads=[B5], writes=[x0])
                qc, pc, xc = Q[0], P[0], X[0]
                for j in range(6):
                    qn, pn, xn = Q[(j + 1) % 2], P[(j + 1) % 2], X[(j + 1) % 2]
                    for hf, bank in ((0, B4), (1, B5)):
                        for dq in range(2):
                            dp = hf * 2 + dq
                            kb.op("pe", lambda e, dp=dp, dq=dq, bank=bank: e.matmul(bank[:, dq * 256:(dq + 1) * 256], qc[:, dp, :], xc[:, dp, :],
                                                                                    start=True, stop=True), reads=[qc, xc], writes=[bank])
                        kb.op("dve", lambda e, hf=hf, bank=bank, xn=xn, xc=xc: e.tensor_tensor(
                            xn[:, 2 * hf:2 * hf + 2, :], f32(xc[:, 2 * hf:2 * hf + 2, :]), bank[:, :].rearrange("p (dq x) -> p dq x", dq=2), ALU.add),
                            reads=[xc, bank], writes=[xn])
                    if j < 5:
                        mm4(B6, 0, 128, lambda dp: qc[:, dp, :], lambda dp: pc[:, dp, :], [qc, pc])
                        mm4(B7, 0, 128, lambda dp: pc[:, dp, :], lambda dp: qc[:, dp, :], [qc, pc])
                        kb.op("act", lambda e, pn=pn: e.copy(pn[:], pv(B6, 0, 128)), reads=[B6], writes=[pn])
                        kb.op("act", lambda e, qn=qn: e.copy(qn[:], pv(B7, 0, 128)), reads=[B7], writes=[qn])
                    qc, pc, xc = qn, pn, xn
                mm4(B3, 0, 64, lambda dp: xc[:, dp, 0:128], lambda dp: ArbT[:, dp, :], [xc, ArbT])
                kb.op("dve", lambda e: e.tensor_tensor(RAT[:], f32(Rb[:]), pv(B3, 0, 64), ALU.add), reads=[Rb, B3], writes=[RAT])
                mm4(B2, 0, 128, lambda dp: xc[:, dp, 0:128], lambda dp: Btok[:, dp, :], [xc, Btok])
                kb.op("pool", lambda e: e.tensor_tensor(DG[:], ident_f[:, :].unsqueeze(1).broadcast_to([128, 4, 128]),
                                                        Wtot[:, :].unsqueeze(2).broadcast_to([128, 4, 128]), ALU.mult), reads=[ident_f, Wtot], writes=[DG])
                kb.op("dve", lambda e: e.tensor_tensor(McT[:], DG[:], pv(B2, 0, 128), ALU.add), reads=[DG, B2], writes=[McT])
                for dp in range(4):
                    kb.op("pe", lambda e, dp=dp: e.matmul(B0[:, dp * 128:(dp + 1) * 128], Btok[:, dp, :], xc[:, dp, 128:256], start=True, stop=False),
                          reads=[Btok, xc], writes=[B0])
                    kb.op("pe", lambda e, dp=dp: e.matmul(B0[:, dp * 128:(dp + 1) * 128], Ktok[:, dp, :], VTr[:, dp, :], start=False, stop=True),
                          reads=[Ktok, VTr], writes=[B0])
                kb.op("act", lambda e: e.copy(NcS[:], pv(B0, 0, 128)), reads=[B0], writes=[NcS])
                for dp in range(4):
                    c0 = dp * 64
                    kb.op("pe", lambda e, dp=dp, c0=c0: e.matmul(B1[:, c0:c0 + 64], ST[:, dp, :], RAT[:, dp, :], start=True, stop=False),
                          reads=[ST, RAT], writes=[B1])
                    kb.op("pe", lambda e, dp=dp, c0=c0: e.matmul(B1[:, c0:c0 + 64], xc[:, dp, 128:256], ArbT[:, dp, :], start=False, stop=False),
                          reads=[xc, ArbT], writes=[B1])
                    kb.op("pe", lambda e, dp=dp, c0=c0: e.matmul(B1[:, c0:c0 + 64], VTr[:, dp, :], ArkT[:, dp, :], start=False, stop=True),
                          reads=[VTr, ArkT], writes=[B1])
                for d in range(2):
                    c0 = cidx[d] * CH
                    kb.op("act", lambda e, d=d: e.copy(ysb[d][:, :, :], B1[:, d * 128:(d + 1) * 128].rearrange("p (pr t) -> p pr t", pr=2)),
                          reads=[B1], writes=[ysb[d]])
                    dst = RS["YF" if d == 0 else "YB"]
                    kb.dma("sp", dst.t.rearrange("(pr q) t -> q pr t", q=128)[:, :, c0:c0 + CH], ysb[d][:, :, :], reads=[ysb[d]], writes=[Buf()])
                mm4(B6, 0, 128, lambda dp: McT[:, dp, :], lambda dp: ST[:, dp, :], [McT, ST])
                kb.op("dve", lambda e: e.tensor_tensor(ST[:], NcS[:], pv(B6, 0, 128), ALU.add), reads=[NcS, B6], writes=[ST])

    def phase_rwkv_out(l, with_ctx):
        with kb.scope():
            rk_ = colvec("rrk", W["rw_r_k"][l, :], W["rw_r_k"], [128, 2], "(j p) -> p j", p=128)
            lg_ = colvec("rlg", W["rw_ln_g"][l, :], W["rw_ln_g"], [128, 2], "(j p) -> p j", p=128)
            lb_ = colvec("rlb", W["rw_ln_b"][l, :], W["rw_ln_b"], [128, 2], "(j p) -> p j", p=128)
            nm = ("YF", "YB", "RT", "KD0", "KD1", "VT")
            tl = [{n: kb.sb(f"o{n}{i}", [128, 512]) for n in nm} for i in range(2)]
            sg = [kb.sb(f"osg{i}", [128, 512], BF16) for i in range(2)]
            ob = [kb.sb(f"oob{i}", [128, 512], BF16) for i in range(2)]
            wk = [[kb.sb(f"owk{k}{i}", [128, 512]) for k in range(3)] for i in range(2)]
            it = 0
            for pr in range(2):
                rows = slice(pr * 128, (pr + 1) * 128)
                for (t0, nt) in TCH:
                    if not with_ctx and t0 + nt <= C:
                        continue
                    t_, s_, o_, (a_, b_, c_) = tl[it % 2], sg[it % 2], ob[it % 2], wk[it % 2]
                    for k, n in enumerate(nm):
                        kb.dma("sp" if k % 2 == 0 else "pool", t_[n][:, 0:nt], RS[n][rows, t0:t0 + nt], reads=[RS[n]], writes=[t_[n]])
                    kb.dma("sp", s_[:, 0:nt], RS["SGT"][rows, t0:t0 + nt], reads=[RS["SGT"]], writes=[s_])
                    y = t_["YF"]
                    kb.op("dve", lambda e: e.tensor_tensor(y[:, 0:nt], y[:, 0:nt], t_["YB"][:, 0:nt], ALU.add), reads=[y, t_["YB"]], writes=[y])
                    p1, p2, p3 = PS[(3 * it) % 8], PS[(3 * it + 1) % 8], PS[(3 * it + 2) % 8]
                    kb.op("pe", lambda e: e.matmul(p1[:, 0:nt], blk64[:], y[:, 0:nt], start=True, stop=True), reads=[blk64, y], writes=[p1])
                    kb.op("dve", lambda e: e.scalar_tensor_tensor(a_[:, 0:nt], p1[:, 0:nt], -1.0 / 64, y[:, 0:nt], ALU.mult, ALU.add),
                          reads=[p1, y], writes=[a_])
                    kb.op("act", lambda e: e.activation(b_[:, 0:nt], a_[:, 0:nt], AF.Square), reads=[a_], writes=[b_])
                    kb.op("pe", lambda e: e.matmul(p2[:, 0:nt], blk64[:], b_[:, 0:nt], start=True, stop=True), reads=[blk64, b_], writes=[p2])
                    kb.op("dve", lambda e: e.tensor_scalar(b_[:, 0:nt], p2[:, 0:nt], 1.0 / 64, 64e-5, ALU.mult, ALU.add), reads=[p2], writes=[b_])
                    kb.op("act", lambda e: e.sqrt(b_[:, 0:nt], b_[:, 0:nt]), reads=[b_], writes=[b_])
                    kb.op("dve", lambda e: e.reciprocal(b_[:, 0:nt], b_[:, 0:nt]), reads=[b_], writes=[b_])
                    kb.op("dve", lambda e: e.tensor_tensor(a_[:, 0:nt], a_[:, 0:nt], b_[:, 0:nt], ALU.mult), reads=[a_, b_], writes=[a_])
                    kb.op("dve", lambda e: e.tensor_scalar(a_[:, 0:nt], a_[:, 0:nt], lg_[:, pr:pr + 1], lb_[:, pr:pr + 1], ALU.mult, ALU.add),
                          reads=[a_, lg_, lb_], writes=[a_])
                    kb.op("pool", lambda e: e.tensor_tensor(c_[:, 0:nt], t_["KD0"][:, 0:nt], t_["KD1"][:, 0:nt], ALU.add),
                          reads=[t_["KD0"], t_["KD1"]], writes=[c_])
                    kb.op("dve", lambda e: e.scalar_tensor_tensor(c_[:, 0:nt], t_["RT"][:, 0:nt], rk_[:, pr:pr + 1], c_[:, 0:nt], ALU.mult, ALU.mult),
                          reads=[t_["RT"], rk_, c_], writes=[c_])
                    kb.op("pe", lambda e: e.matmul(p3[:, 0:nt], blk64[:], c_[:, 0:nt], start=True, stop=True), reads=[blk64, c_], writes=[p3])
                    kb.op("dve", lambda e: e.tensor_tensor(c_[:, 0:nt], p3[:, 0:nt], t_["VT"][:, 0:nt], ALU.mult), reads=[p3, t_["VT"]], writes=[c_])
                    kb.op("dve", lambda e: e.tensor_tensor(a_[:, 0:nt], a_[:, 0:nt], c_[:, 0:nt], ALU.add), reads=[a_, c_], writes=[a_])
                    kb.op("pool", lambda e: e.tensor_tensor(o_[:, 0:nt], a_[:, 0:nt], s_[:, 0:nt], ALU.mult), reads=[a_, s_], writes=[o_])
                    kb.dma("sp", mixT[256 + pr * 128:256 + (pr + 1) * 128, t0:t0 + nt], o_[:, 0:nt], reads=[o_], writes=[Buf()])
                    it += 1


    SEGS = {"L": dict(Ls=L, A=32, cbw=32, off=C, ut="UTL"), "C": dict(Ls=C, A=2, cbw=64, off=0, ut="UTC")}

    def phase_hyena_prep(l, with_ctx):
        with kb.scope():
            stage = kb.sb("hstage", [128, 8, 128])
            wts = [kb.sb(f"hwt{i}", [128, 8, 128], BF16) for i in range(2)]
            cw = kb.sb("hcw", [128, 6, 3])
            for k in range(3):
                kb.dma("sp", cw[:, :, k], W["hy_conv"][l, k, :].rearrange("(j p) -> p j", p=128), reads=[W["hy_conv"]], writes=[cw], slow=True)
            ncw = kb.sb("hncw", [128, 6, 3])
            kb.op("dve", lambda e: e.tensor_scalar(ncw[:], cw[:], -1.0, None, ALU.mult), reads=[cw], writes=[ncw])
            Zraw = kb.sb("hZraw", [128, T + 2])
            Zout = kb.sb("hZout", [128, T])
            kb.op("pool", lambda e: e.memset(Zraw[:, 0:1], 0.0), writes=[Zraw])
            kb.op("pool", lambda e: e.memset(Zraw[:, T + 1:T + 2], 0.0), writes=[Zraw])
            ub = kb.sb("hub", [128, 32 * 128])
            tG = [kb.sb(f"htG{i}", [128, 512], BF16) for i in range(2)]
            for oi, jt in enumerate(range(8)):
                wt = wts[oi % 2]
                c0 = HY0 + jt * 128 if jt < 6 else HYG0 + (jt - 6) * 128
                load_w(l, wt, c0, 128, stage)
                if jt >= 6:
                    for ci, (t0, nt) in enumerate(TCH):
                        p = PS[ci % 4]
                        proj_fm(p, wt, 0, 128, t0, nt)
                        g = tG[ci % 2]
                        kb.op("act", lambda e, p=p, g=g, nt=nt: e.activation(g[:, 0:nt], p[:, 0:nt], AF.Silu), reads=[p], writes=[g])
                        kb.dma("sp", HS["SG"][(jt - 6) * 128:(jt - 5) * 128, t0:t0 + nt], g[:, 0:nt], reads=[g], writes=[Buf()])
                    continue
                conv_tile(l, wt, cw, ncw, jt, Zraw, Zout)
                arr, half = jt // 2, jt % 2
                for sn in (("L", "C") if with_ctx else ("L",)):
                    sg = SEGS[sn]
                    A, cbw, off = sg["A"], sg["cbw"], sg["off"]
                    G = 128 // A
                    ncg = 128 // G
                    ubv = ub[:, 0:A * 128].rearrange("p (g a c) -> p g a c", g=ncg, a=A)
                    for a in range(A):
                        p = PS[4 + (a // 4) % 4]
                        kb.op("pe", lambda e, p=p, a=a, A=A, off=off: e.transpose(
                            p[:, (a % 4) * 128:(a % 4 + 1) * 128], Zout[:, off + a:off + a + 127 * A + 1:A], ident_f[:]),
                            reads=[Zout, ident_f], writes=[p])
                        if a % 4 == 3 or a == A - 1:
                            a0 = (a // 4) * 4
                            na = a - a0 + 1
                            kb.op("act", lambda e, p=p, a0=a0, na=na, G=G: e.copy(
                                ubv[:, :, a0:a0 + na, :], p[:, 0:na * 128].rearrange("p (a g c) -> p g a c", a=na, c=G)), reads=[p], writes=[ub])
                    nb = 128 // cbw
                    bsz = A * cbw
                    for b in range(nb):
                        dst = HS[sg["ut"]][arr, half * nb + b, :, :]
                        kb.dma("sp" if b % 2 == 0 else "pool", dst, ub[:, b * bsz:(b + 1) * bsz], reads=[ub], writes=[Buf()])

    def cmul(dre, dim_, sre, sim, tre, tim, conj, srcb, tabb, dstb, tmp):
        t1, t2 = tmp
        sh = tuple(slice(None) for _ in range(1))
        kb.op("dve", lambda e: e.tensor_tensor(t1, sre, tre, ALU.mult), reads=srcb + tabb, writes=[dstb[2]])
        kb.op("dve", lambda e: e.tensor_tensor(t2, sim, tim, ALU.mult), reads=srcb + tabb, writes=[dstb[3]])
        kb.op("pool", lambda e: e.tensor_tensor(dre, t1, t2, ALU.add if conj else ALU.subtract), reads=[dstb[2], dstb[3]], writes=[dstb[0]])
        kb.op("dve", lambda e: e.tensor_tensor(t1, sim, tre, ALU.mult), reads=srcb + tabb + [dstb[0]], writes=[dstb[2]])
        kb.op("dve", lambda e: e.tensor_tensor(t2, sre, tim, ALU.mult), reads=srcb + tabb + [dstb[0]], writes=[dstb[3]])
        kb.op("pool", lambda e: e.tensor_tensor(dim_, t1, t2, ALU.subtract if conj else ALU.add), reads=[dstb[2], dstb[3]], writes=[dstb[1]])

    def phase_hyena_main(l, with_ctx):
        PI = math.pi
        with kb.scope():
            fw1 = kb.sb("hfw1", [33, 64])
            fw2 = kb.sb("hfw2", [64, 64])
            fw3 = kb.sb("hfw3", [64, 1024])
            kb.dma("sp", fw1[:], W["hy_fw1"][l, :, :], reads=[W["hy_fw1"]], writes=[fw1])
            kb.dma("sp", fw2[:], W["hy_fw2"][l, :, :], reads=[W["hy_fw2"]], writes=[fw2])
            kb.dma("sp", fw3[:], W["hy_fw3"][l, :, :], reads=[W["hy_fw3"]], writes=[fw3])
            fb1 = colvec("hfb1", W["hy_fb1"][l, :], W["hy_fb1"], [64, 1], "(d o) -> d o", o=1)
            fb2 = colvec("hfb2", W["hy_fb2"][l, :], W["hy_fb2"], [64, 1], "(d o) -> d o", o=1)
            frq = colvec("hfrq", W["hy_freq"][l, :], W["hy_freq"], [64, 1], "(d o) -> d o", o=1)
            brow = kb.sb("hbrow", [1, 512])
            kb.dma("sp", brow[:], W["hy_bias"][l, :, :].rearrange("o c -> (o c)").rearrange("(x n) -> x n", x=1), reads=[W["hy_bias"]], writes=[brow])
            for sn in (("L", "C") if with_ctx else ("L",)):
                sg = SEGS[sn]
                Ls, A, cbw, off = sg["Ls"], sg["A"], sg["cbw"], sg["off"]
                G = 128 // A
                N = 2 * Ls
                ngr = cbw // G
                nblk = 256 // cbw
                pre = f"hy{sn}_"
                with kb.scope():
                    def ld(nm, shape):
                        t = kb.sb("k" + nm, shape)
                        src = CT[pre + nm]
                        kb.dma("sp", t[:], src.t, reads=[src], writes=[t])
                        return t
                    def ldr(nm, shape):
                        tr = kb.sb("r" + nm, shape, F32R)
                        with kb.scope():
                            t32 = ld(nm, shape)
                            kb.op("dve", lambda e: e.tensor_copy(tr[:], t32[:]), reads=[t32], writes=[tr])
                        return tr
                    F256 = ldr("F256", [128, 2, 512]); TWC = ld("TWC", [128, 256]); TWS = ld("TWS", [128, 256])
                    Dre = ldr("Dre", [128, 128]); Dim = ldr("Dim", [128, 128]); nDim = ldr("nDim", [128, 128])
                    E1 = ldr("E1", [128, 256]); E2 = ldr("E2", [128, 256])
                    TW2C = ld("TW2C", [128, 2, 128]); TW2S = ld("TW2S", [128, 2, 128])
                    IC = ldr("IC", [128, 2, 128]); IS = ldr("IS", [128, 2, 128])
                    h2T = kb.sb("h2T", [64, N])
                    with kb.scope():
                        zT = kb.sb("zT", [33, N])
                        kb.dma("sp", zT[:], CT[pre + "zT"].t, reads=[CT[pre + "zT"]], writes=[zT])
                        h1T = kb.sb("h1T", [64, N])
                        arg = [kb.sb(f"harg{i}", [64, 512]) for i in range(2)]
                        wr = [kb.sb(f"hwr{i}", [64, 512]) for i in range(2)]
                        for (src, K_, wgt, bcol, dst) in ((zT, 33, fw1, fb1, h1T), (h1T, 64, fw2, fb2, h2T)):
                            for ci, n0 in enumerate(range(0, N, 512)):
                                p = PS[ci % 4]
                                ag = arg[ci % 2]
                                kb.op("pe", lambda e: e.matmul(p[0:64, :], wgt[0:K_, :], src[0:K_, n0:n0 + 512], start=True, stop=True),
                                      reads=[wgt, src], writes=[p])
                                kb.op("dve", lambda e: e.tensor_scalar(ag[:, :], p[0:64, :], bcol[:, 0:1], frq[:, 0:1], ALU.add, ALU.mult),
                                      reads=[p, bcol, frq], writes=[ag])
                                for _w in range(2):
                                    kb.op("dve", lambda e: e.tensor_scalar(wr[0][:, :], ag[:, :], PI, -2 * PI, ALU.is_gt, ALU.mult), reads=[ag], writes=[wr[0]])
                                    kb.op("dve", lambda e: e.tensor_scalar(wr[1][:, :], ag[:, :], -PI, 2 * PI, ALU.is_lt, ALU.mult), reads=[ag], writes=[wr[1]])
                                    kb.op("dve", lambda e: e.tensor_tensor(ag[:, :], ag[:, :], wr[0][:, :], ALU.add), reads=[ag, wr[0]], writes=[ag])
                                    kb.op("dve", lambda e: e.tensor_tensor(ag[:, :], ag[:, :], wr[1][:, :], ALU.add), reads=[ag, wr[1]], writes=[ag])
                                kb.op("act", lambda e: e.activation(dst[:, n0:n0 + 512], ag[:, :], AF.Sin), reads=[ag], writes=[dst])
                    KT1 = kb.sb("KT", [128, 2, ngr, A, G])
                    KTr1 = kb.sb("KTr", [128, 2, ngr, A, G], F32R)
                    KT = [KT1, KT1]
                    KTr = [KTr1, KTr1]
                    uvr = kb.sb("huvr", [128, ngr, A * G], F32R)
                    KS = [kb.sb(f"KS{o}", [128, ngr, 512]) for o in range(2)]
                    DECt = kb.sb("DECt", [128, 2, ngr, A, G])
                    part = kb.sb("hpart", [128, cbw])
                    rn = kb.sb("hrn", [128, cbw])
                    ex = kb.sb("hex", [1, cbw])
                    uv = kb.sb("huv", [128, ngr, A * G]); x1 = kb.sb("hx1", [128, ngr, A * G]); x2 = kb.sb("hx2", [128, ngr, A * G])
                    u2 = kb.sb("hu2", [128, ngr, A * G], F32R); res = kb.sb("hres", [128, A, cbw])
                    dts = (F32R, F32R, F32, F32)
                    NL = 4 if ngr >= 4 else 2
                    BpS = [[kb.sb(f"hBp{b}{i}", [128, 256], dts[i]) for i in range(4)] for b in range(NL)]
                    BpbS = [[Buf() for _ in range(4)] for b in range(NL)]
                    YpS = [[kb.sb(f"hYp{b}{i}", [128, 256], dts[i]) for i in range(4)] for b in range(NL)]
                    YpbS = [[Buf() for _ in range(4)] for b in range(NL)]
                    GpS = [[kb.sb(f"hGp{b}{i}", [128, 2, 128], dts[i]) for i in range(4)] for b in range(NL)]
                    GpbS = [[Buf() for _ in range(4)] for b in range(NL)]
                    fctr = [0]
                    sgm = kb.sb("hsgm", [cbw, Ls], BF16)

                    def fwd_fft(lhs_chunks, lhs_bufs, psB, psX):
                        n = len(lhs_chunks)
                        fctr[0] += 1
                        Bp, Bpb = BpS[fctr[0] % NL], BpbS[fctr[0] % NL]
                        for i, (ap, hf) in enumerate(lhs_chunks):
                            kb.op("pe", lambda e, ap=ap, hf=hf, i=i: e.matmul(psB[:, :], ap, F256[:, hf, :], start=(i == 0), stop=(i == n - 1)),
                                  reads=lhs_bufs + [F256], writes=[psB])
                        yield
                        cmul(Bp[0][:, :], Bp[1][:, :], psB[:, 0:256], psB[:, 256:512], TWC[:, :], TWS[:, :], True,
                             [psB], [TWC, TWS], Bpb, (Bp[2][:, :], Bp[3][:, :]))
                        yield
                        kb.op("pe", lambda e: e.matmul(psX[:, 0:256], Dre[:, :], Bp[0][:, :], start=True, stop=False), reads=[Dre, Bpb[0]], writes=[psX])
                        kb.op("pe", lambda e: e.matmul(psX[:, 0:256], nDim[:, :], Bp[1][:, :], start=False, stop=True), reads=[nDim, Bpb[1]], writes=[psX])
                        kb.op("pe", lambda e: e.matmul(psX[:, 256:512], Dim[:, :], Bp[0][:, :], start=True, stop=False), reads=[Dim, Bpb[0]], writes=[psX])
                        kb.op("pe", lambda e: e.matmul(psX[:, 256:512], Dre[:, :], Bp[1][:, :], start=False, stop=True), reads=[Dre, Bpb[1]], writes=[psX])

                    def conv_group(src, src_b, g, o, mulv, mul_b, dst_ap, dst_b, it):
                        ln = it % NL
                        if NL == 4:
                            psB, psX, psG, psy = PS[2 * ln], PS[2 * ln + 1], PS[2 * ln], PS[2 * ln + 1]
                        else:
                            psB, psX, psG, psy = PS[it % 2], PS[2 + it % 2], PS[4 + it % 2], PS[6 + it % 2]
                        Yp, Ypb, Gp, Gpb = YpS[ln], YpbS[ln], GpS[ln], GpbS[ln]
                        yield from fwd_fft([(src[:, g, :], 0)], [src_b], psB, psX)
                        yield
                        cmul(Yp[0][:, :], Yp[1][:, :], psX[:, 0:256], psX[:, 256:512], KS[o][:, g, 0:256], KS[o][:, g, 256:512], False,
                             [psX], [KS[o]], Ypb, (Yp[2][:, :], Yp[3][:, :]))
                        yield
                        for chn in range(2):
                            fs = slice(chn * 128, (chn + 1) * 128)
                            kb.op("pe", lambda e, fs=fs, chn=chn: e.matmul(psG[:, chn * 256:(chn + 1) * 256], Yp[0][:, fs], E1[:, :], start=True, stop=False),
                                  reads=[Ypb[0], E1], writes=[psG])
                            kb.op("pe", lambda e, fs=fs, chn=chn: e.matmul(psG[:, chn * 256:(chn + 1) * 256], Yp[1][:, fs], E2[:, :], start=False, stop=True),
                                  reads=[Ypb[1], E2], writes=[psG])
                        yield
                        pg = psG[:, :].rearrange("p (ch ri c) -> p ch ri c", ch=2, ri=2)
                        cmul(Gp[0][:, :, :], Gp[1][:, :, :], pg[:, :, 0, :], pg[:, :, 1, :], TW2C[:, :, :], TW2S[:, :, :], False,
                             [psG], [TW2C, TW2S], Gpb, (Gp[2][:, :, :], Gp[3][:, :, :]))
                        yield
                        k = 0
                        for chn in range(2):
                            for (tab, gsrc, gb) in ((IC, Gp[0], Gpb[0]), (IS, Gp[1], Gpb[1])):
                                kb.op("pe", lambda e, chn=chn, tab=tab, gsrc=gsrc, k=k: e.matmul(
                                    psy[:, 0:128], tab[:, chn, :], gsrc[:, chn, :], start=(k == 0), stop=(k == 3)), reads=[tab, gb], writes=[psy])
                                k += 1
                        yield
                        kb.op("dve", lambda e: e.tensor_tensor(dst_ap, psy[:, 0:128].rearrange("p (c a) -> p a c", a=A),
                                                               mulv[:, g, :].rearrange("p (a c) -> p a c", c=G), ALU.mult),
                              reads=[psy, mul_b], writes=[dst_b])

                    def lockstep(gens):
                        gens = list(gens)
                        while gens:
                            nxt = []
                            for g_ in gens:
                                try:
                                    next(g_)
                                    nxt.append(g_)
                                except StopIteration:
                                    pass
                            gens = nxt

                    def spec_group(o, g, it):
                        if NL == 4:
                            psB, psX = PS[2 * (it % 4)], PS[2 * (it % 4) + 1]
                        else:
                            psB, psX = PS[it % 2], PS[2 + it % 2]
                        yield from fwd_fft([(KTr[o][:, 0, g, :, :].rearrange("p a c -> p (a c)"), 0),
                                            (KTr[o][:, 1, g, :, :].rearrange("p a c -> p (a c)"), 1)], [KTr[o]], psB, psX)
                        yield
                        kb.op("act", lambda e: e.copy(KS[o][:, g, :], psX[:, :]), reads=[psX], writes=[KS[o]])

                    git = 0
                    for cb in range(nblk):
                        kb.dma("sp", DECt[:].rearrange("p h g a c -> p (h g a c)"), CT[pre + "DEC"][cb, :, :], reads=[CT[pre + "DEC"]], writes=[DECt])
                        for ai, tile_ in enumerate((uv, x1, x2)):
                            kb.dma("pool", tile_[:].rearrange("p g x -> p (g x)"), HS[sg["ut"]][ai, cb, :, :], reads=[HS[sg["ut"]]], writes=[tile_])
                        for o in range(2):
                            for hf in range(2):
                                col0 = o * 512 + hf * 256 + cb * cbw
                                npb = 512 // cbw
                                for a in range(A):
                                    p = PS[(a // npb) % 4]
                                    kb.op("pe", lambda e, p=p, a=a, hf=hf, col0=col0, npb=npb: e.matmul(
                                        p[:, (a % npb) * cbw:(a % npb + 1) * cbw], h2T[0:64, hf * 128 * A + a:hf * 128 * A + a + 127 * A + 1:A],
                                        fw3[0:64, col0:col0 + cbw], start=True, stop=True), reads=[h2T, fw3], writes=[p])
                                    if a % npb == npb - 1 or a == A - 1:
                                        a0 = (a // npb) * npb
                                        na = a - a0 + 1
                                        kb.op("dve", lambda e, p=p, a0=a0, na=na, hf=hf, o=o: e.tensor_tensor(
                                            KT[o][:, hf, :, a0:a0 + na, :], p[:, 0:na * cbw].rearrange("p (a g c) -> p g a c", a=na, c=G),
                                            DECt[:, hf, :, a0:a0 + na, :], ALU.mult), reads=[p, DECt], writes=[KT[o]])
                            kb.op("dve", lambda e, o=o: e.tensor_reduce(part[:, :].rearrange("p (g c) -> p g c", c=G),
                                                                        KT[o][:, :, :, :, :].rearrange("p h g a c -> p g c h a"), AX.XY, ALU.add,
                                                                        apply_absolute_value=True), reads=[KT[o]], writes=[part])
                            pe_ = PS[4]
                            kb.op("pe", lambda e, o=o: e.matmul(pe_[0:1, 0:cbw], h2T[0:64, 0:1], fw3[0:64, o * 512 + 256 + cb * cbw:o * 512 + 256 + (cb + 1) * cbw],
                                                                start=True, stop=True), reads=[h2T, fw3], writes=[pe_])
                            kb.op("act", lambda e: e.activation(ex[0:1, :], pe_[0:1, 0:cbw], AF.Abs), reads=[pe_], writes=[ex])
                            kb.op("dve", lambda e: e.tensor_tensor(part[0:1, :], part[0:1, :], ex[0:1, :], ALU.add), reads=[part, ex], writes=[part])
                            pt_ = PS[5]
                            kb.op("pe", lambda e: e.matmul(pt_[:, 0:cbw], ones_f[:, :], part[:, :], start=True, stop=True), reads=[ones_f, part], writes=[pt_])
                            kb.op("dve", lambda e: e.reciprocal(rn[:, :], pt_[:, 0:cbw]), reads=[pt_], writes=[rn])
                            for hf in range(2):
                                kb.op("dve", lambda e, o=o, hf=hf: e.tensor_tensor(
                                    KTr[o][:, hf, :, :, :], KT[o][:, hf, :, :, :],
                                    rn[:, :].rearrange("p (g c) -> p g c", c=G).unsqueeze(2).broadcast_to([128, ngr, A, G]), ALU.mult),
                                    reads=[KT[o], rn], writes=[KTr[o]])
                            kb.op("dve", lambda e, o=o: e.tensor_tensor(
                                KTr[o][0:1, 0, :, 0, :], KTr[o][0:1, 0, :, 0, :].bitcast(F32),
                                brow[0:1, o * 256 + cb * cbw:o * 256 + (cb + 1) * cbw].rearrange("p (g c) -> p g c", c=G), ALU.add),
                                reads=[KTr[o], brow], writes=[KTr[o]])
                            for g in range(0, ngr, NL):
                                gg = [g_ for g_ in range(g, min(ngr, g + NL))]
                                lockstep([spec_group(o, g_, git + k_) for k_, g_ in enumerate(gg)])
                                git += len(gg)
                        kb.op("act", lambda e: e.copy(uvr[:], uv[:]), reads=[uv], writes=[uvr])
                        for g in range(0, ngr, NL):
                            gg = [g_ for g_ in range(g, min(ngr, g + NL))]
                            lockstep([conv_group(uvr, uvr, g_, 0, x1, x1, u2[:, g_, :].rearrange("p (a c) -> p a c", c=G), u2, git + k_)
                                      for k_, g_ in enumerate(gg)])
                            git += len(gg)
                        for g in range(0, ngr, NL):
                            gg = [g_ for g_ in range(g, min(ngr, g + NL))]
                            lockstep([conv_group(u2, u2, g_, 1, x2, x2, res[:, :, g_ * G:(g_ + 1) * G], res, git + k_)
                                      for k_, g_ in enumerate(gg)])
                            git += len(gg)
                        kb.dma("sp", sgm[:], HS["SG"][cb * cbw:(cb + 1) * cbw, off:off + Ls], reads=[HS["SG"]], writes=[sgm])
                        Fv = sgm[:, :].rearrange("c (p a) -> c p a", a=A)
                        for a in range(A):
                            p = PS[4 + (a // 4) % 4]
                            kb.op("pe", lambda e, p=p, a=a: e.transpose(p[0:cbw, (a % 4) * 128:(a % 4 + 1) * 128], res[:, a, :], ident_f[:]),
                                  reads=[res, ident_f], writes=[p])
                            if a % 4 == 3 or a == A - 1:
                                a0 = (a // 4) * 4
                                na = a - a0 + 1
                                kb.op("dve", lambda e, p=p, a0=a0, na=na: e.tensor_tensor(
                                    Fv[:, :, a0:a0 + na], p[0:cbw, 0:na * 128].rearrange("c (a p) -> c p a", p=128), Fv[:, :, a0:a0 + na], ALU.mult),
                                    reads=[p, sgm], writes=[sgm])
                        kb.dma("sp", mixT[cb * cbw:(cb + 1) * cbw, off:off + Ls], sgm[:, :], reads=[sgm], writes=[Buf()])

    dbgn = [n for n, _ in dbg]
    for l in range(depth):
        last = (l == DEPTH - 1)
        with kb.scope():
            hT = kb.sb("hT", [128, 8, T], BF16)
            G1 = kb.sb("G1", [128, 2, D])
            SH = kb.sb("SH", [128, 2, D])
            phase_mod(l)
            phase_norm(l)
            if "noattn" not in dbgn:
                if os.environ.get("ATTN_ONLY", "") != "dense":
                    phase_attn(l, False, not last)
                if os.environ.get("ATTN_ONLY", "") != "window":
                    phase_attn(l, True, not last)
            if "norw" not in dbgn:
                phase_rwkv_prep(l)
            if "nohy" not in dbgn:
                phase_hyena_prep(l, not last)
            if "hT" in dbgn:
                tmp = kb.sb("dbghT", [128, T])
                for j in range(8):
                    kb.op("dve", lambda e, j=j, tmp=tmp: e.tensor_copy(tmp[:], hT[:, j, :]), reads=[hT], writes=[tmp])
                    kb.dma("sp", dbg_t["hT"][:, j, :], tmp[:], reads=[tmp], writes=[dbg_t["hT"]])
        if "norw" not in dbgn:
            {0: phase_rwkv_chunked, 1: phase_rwkv_chunked2, 3: phase_rwkv_chunked3}[RW_V2](l)
            phase_rwkv_out(l, not last)
        if "nohy" not in dbgn:
            phase_hyena_main(l, not last)
        if "noout" not in dbgn:
            phase_out(l, last)
    for n, s_ in dbg:
        if n == "xres":
            with kb.scope():
                tx = kb.sb("dbgx", [128, D])
                for i in range(NT):
                    kb.dma("sp", tx[:], xres[i * 128:(i + 1) * 128, :], reads=[xres_b[i]], writes=[tx])
                    kb.dma("sp", dbg_t[n][i * 128:(i + 1) * 128, :], tx[:], reads=[tx], writes=[dbg_t[n]])
        if n == "mixT":
            with kb.scope():
                tmpb = kb.sb("dbgmb", [128, T], BF16)
                tmpf = kb.sb("dbgmf", [128, T])
                for j in range(8):
                    kb.dma("sp", tmpb[:], mixT[j * 128:(j + 1) * 128, :], reads=[mixT], writes=[tmpb])
                    kb.op("dve", lambda e, tmpb=tmpb, tmpf=tmpf: e.tensor_copy(tmpf[:], tmpb[:]), reads=[tmpb], writes=[tmpf])
                    kb.dma("sp", dbg_t[n][j * 128:(j + 1) * 128, :], tmpf[:], reads=[tmpf], writes=[dbg_t[n]])
    kb.finish()
    kb.es.close()
    return kb, cst


_PROG = {}


def kernel(**inputs):
    if "p" not in _PROG:
        _PROG["p"] = build()
    kb, cst = _PROG["p"]
    f = lambda a: np.ascontiguousarray(np.asarray(a, dtype=np.float32))
    shared = {}
    for n in inputs:
        if n in ("x", "c", "ctx", "c_ctx"):
            continue
        shared[n] = f(inputs[n])
    shared["c_ctx"] = f(inputs["c_ctx"])
    for n, a in cst.items():
        shared["k_" + n] = np.ascontiguousarray(a)
    x, c, ctx = f(inputs["x"]), f(inputs["c"]), f(inputs["ctx"])
    B = x.shape[0]
    in_maps = []
    for b in range(B):
        m = dict(shared)
        m["x"] = np.ascontiguousarray(x[b])
        m["c"] = np.ascontiguousarray(c[b])
        m["ctx"] = np.ascontiguousarray(ctx[b])
        in_maps.append(m)
    res = run_bass_kernel_spmd(kb.nc, in_maps, core_ids=list(range(B)))
    return np.stack([np.asarray(res.results[b]["out"], dtype=np.float32) for b in range(B)], axis=0)
```

```python
import contextlib
import math
import numpy as np
import ml_dtypes
import concourse.bass as bass
import concourse.mybir as mybir
from concourse.bass_utils import run_bass_kernel_spmd

F32 = mybir.dt.float32
BF16 = mybir.dt.bfloat16
F32R = mybir.dt.float32r
ALU = mybir.AluOpType
AF = mybir.ActivationFunctionType
AX = mybir.AxisListType

D = 1024
L = 4096
C = 256
T = L + C
NT = T // 128
DEPTH = 4
D_IN = 3712
HY0, HYG0, RW0, RWG0, WA0, WAG0, FA0, FAG0 = 0, 768, 1024, 1920, 2176, 2688, 2944, 3456
EPS = 1e-6
NSLOT = 24
import os
RW_STAGE = int(os.environ.get('RW_STAGE', '99'))
INLINE_WAIT = int(os.environ.get('INLINE_WAIT', '1'))
POOL_DMA_TO_SP = int(os.environ.get('POOL_DMA_TO_SP', '1'))
RW_V2 = int(os.environ.get('RW_V2', '3'))


class Buf:
    def __init__(self, name=""):
        self.name = name
        self.w = None
        self.r = {}

    def wdeps(self):
        return [self.w] if self.w is not None else []

    def rdeps(self):
        return list(self.r.values())

    def add_reader(self, tok):
        k = tok[:2]
        if k not in self.r or self.r[k][2] < tok[2]:
            self.r[k] = tok

    def set_writer(self, tok):
        self.w = tok
        self.r = {}


class Tile(Buf):
    def __init__(self, name, t):
        super().__init__(name)
        self.t = t

    def __getitem__(self, key):
        return self.t[key]


class KB:
    def __init__(self):
        self.nc = bass.Bass("TRN2", target_bir_lowering=False)
        nc = self.nc
        self.es = contextlib.ExitStack()
        self.eng = {"pe": nc.tensor, "act": nc.scalar, "dve": nc.vector, "pool": nc.gpsimd, "sp": nc.sync}
        self.sem = {}
        self.cnt = {}
        self.waited = {e: {} for e in self.eng}
        for e in self.eng:
            self.sem[e] = self.es.enter_context(nc.semaphore("s_" + e))
            self.cnt[e] = 0
        self.slots = {}
        self.slot_i = {}
        for q in ("sp", "act", "pool"):
            self.slots[q] = [[self.es.enter_context(nc.semaphore(f"d_{q}{i}")), 0] for i in range(NSLOT)]
            self.slot_i[q] = 0
        self.n_ins = 0

    def sb(self, name, shape, dt=F32):
        self.uid = getattr(self, "uid", 0) + 1
        name = f"{name}_{self.uid}"
        return Tile(name, self.es.enter_context(self.nc.sbuf_tensor(name, list(shape), dt)))

    def ps(self, name, shape, dt=F32):
        return Tile(name, self.es.enter_context(self.nc.psum_tensor(name, list(shape), dt)))

    def dram(self, name, shape, dt=F32, kind="Internal"):
        t = self.nc.dram_tensor(name, list(shape), dt, kind=kind)
        b = Tile(name, t.ap())
        return b

    def _tok_sem(self, tok):
        if tok[0] == "e":
            return ("e", tok[1]), self.sem[tok[1]], tok[2]
        return ("d", tok[1]), self.slots[tok[1][0]][tok[1][1]][0], tok[2]

    def _wait(self, e, toks, defer=False):
        need = {}
        for tok in toks:
            if tok is None:
                continue
            key, sem, val = self._tok_sem(tok)
            if tok[0] == "e" and tok[1] == e and e == "pe":
                continue
            if self.waited[e].get(key, 0) >= val:
                continue
            if key not in need or need[key][1] < val:
                need[key] = (sem, val)
        items = list(need.items())
        inline = None
        if defer and INLINE_WAIT and items:
            inline = items.pop()
        for key, (sem, val) in items:
            self.eng[e].wait_ge(sem, val)
            self.waited[e][key] = val
        return inline

    def op(self, e, fn, reads=(), writes=()):
        toks = []
        for b in reads:
            toks += b.wdeps()
        for b in writes:
            toks += b.wdeps() + b.rdeps()
        inline = self._wait(e, toks, defer=True)
        ins = fn(self.eng[e])
        if inline is not None:
            key, (sem, val) = inline
            ins._wait_ge(sem, val)
            self.waited[e][key] = val
        self.cnt[e] += 1
        ins.then_inc(self.sem[e], 1)
        tok = ("e", e, self.cnt[e])
        for b in reads:
            b.add_reader(tok)
        for b in writes:
            b.set_writer(tok)
        self.n_ins += 1
        return ins

    def dma(self, q, out, in_, reads=(), writes=(), slow=False):
        if q == "pool" and POOL_DMA_TO_SP:
            q = "sp"
        i = self.slot_i[q]
        self.slot_i[q] = (i + 1) % NSLOT
        slot = self.slots[q][i]
        toks = []
        if slot[1] > 0:
            toks.append(("d", (q, i), slot[1]))
        for b in reads:
            toks += b.wdeps()
        for b in writes:
            toks += b.wdeps() + b.rdeps()
        self._wait(q, toks)
        if slow:
            ins = self.eng[q].dma_start(out=out, in_=in_, allow_slow_non_contiguous=True)
        else:
            ins = self.eng[q].dma_start(out=out, in_=in_)
        ins.then_inc(slot[0], 16)
        slot[1] += 16
        tok = ("d", (q, i), slot[1])
        for b in reads:
            b.add_reader(tok)
        for b in writes:
            b.set_writer(tok)
        self.n_ins += 1
        return ins

    def barrier(self):
        toks = [("e", e, self.cnt[e]) for e in self.eng if self.cnt[e] > 0]
        for q in self.slots:
            for i, s in enumerate(self.slots[q]):
                if s[1] > 0:
                    toks.append(("d", (q, i), s[1]))
        for e in self.eng:
            self._wait(e, toks)

    def finish(self):
        self.barrier()

    @contextlib.contextmanager
    def scope(self):
        es = contextlib.ExitStack()
        old = self.es
        self.es = es
        try:
            yield
        finally:
            self.barrier()
            self.es = old
            es.close()


def host_consts():
    cst = {}
    cst["ident_bf"] = np.eye(128, dtype=np.float32).astype(ml_dtypes.bfloat16)
    cst["ident_f"] = np.eye(128, dtype=np.float32)
    blk = np.zeros((128, 128), np.float32)
    blk[:64, :64] = 1.0
    blk[64:, 64:] = 1.0
    cst["blk64"] = blk
    cst["ones_f"] = np.ones((128, 128), np.float32)
    t = np.arange(L)
    row = (t // 64).astype(np.float32)
    col = (t % 64).astype(np.float32)
    inv = (10000.0 ** (-np.arange(16, dtype=np.float32) / 16)).astype(np.float32)
    cosT = np.zeros((128, L), np.float32)
    sinT = np.zeros((128, L), np.float32)
    perm = np.zeros((128, 128), np.float32)
    for p in range(128):
        d = p % 64
        sec, half, f = d // 32, (d % 32) // 16, d % 16
        pos = row if sec == 0 else col
        ang = (pos * inv[f]).astype(np.float32)
        cosT[p] = np.cos(ang)
        sinT[p] = np.sin(ang)
        if half == 0:
            perm[p + 16, p] = -1.0
        else:
            perm[p - 16, p] = 1.0
    cst["rope_cos"] = cosT
    cst["rope_sin"] = sinT
    cst["rope_perm"] = perm
    i = np.arange(128)[:, None]
    j = np.arange(384)[None, :]
    cst["wmask"] = np.where((j >= i) & (j <= i + 256), 0.0, -1e30).astype(np.float32)
    ii = np.arange(64)
    ms = np.zeros((128, 2, 64), np.float32); mts = np.zeros((128, 2, 64), np.float32); mti = np.zeros((128, 2, 64), np.float32)
    for hp in range(2):
        rows = slice(hp * 64, hp * 64 + 64)
        ms[rows, 0, :] = (ii[None, :] < ii[:, None]); ms[rows, 1, :] = (ii[None, :] > ii[:, None])
        mts[rows, 0, :] = (ii[:, None] < ii[None, :]); mts[rows, 1, :] = (ii[:, None] > ii[None, :])
        mti[rows, 0, :] = (ii[:, None] <= ii[None, :]); mti[rows, 1, :] = (ii[:, None] >= ii[None, :])
    cst["rw_ms"] = ms; cst["rw_mts"] = mts; cst["rw_mti"] = mti
    cst["rw_msb"] = np.concatenate([ms, ms], 2); cst["rw_mtsb"] = np.concatenate([mts, mts], 2)
    cst["rw_id2"] = np.concatenate([np.eye(64, dtype=np.float32)] * 2, 0)
    cst.update(hy_consts(L, 32, 32, "L"))
    cst.update(hy_consts(C, 2, 64, "C"))
    return cst


def hy_consts(Ls, A, cbw, tag):
    G = 128 // A
    N = 2 * Ls
    out = {}
    p = np.arange(128)
    f1 = np.arange(256)
    F = np.zeros((128, 2, 512), np.float64)
    for h in range(2):
        pp = h * 128 + p
        ang = 2 * np.pi * ((pp[:, None] * f1[None, :]) % 256) / 256
        F[:, h, 0:256] = np.cos(ang)
        F[:, h, 256:512] = -np.sin(ang)
    out["F256"] = F
    a_of_row = np.arange(128) // G
    th = 2 * np.pi * ((a_of_row[:, None] * f1[None, :]) % N) / N
    out["TWC"] = np.cos(th)
    out["TWS"] = np.sin(th)
    Dre = np.zeros((128, 128)); Dim = np.zeros((128, 128))
    E1 = np.zeros((128, 256)); E2 = np.zeros((128, 256))
    for a in range(A):
        for c in range(G):
            for f2 in range(A):
                ph = 2 * np.pi * ((a * f2) % A) / A
                Dre[a * G + c, c * A + f2] = np.cos(ph)
                Dim[a * G + c, c * A + f2] = -np.sin(ph)
                E1[c * A + f2, c * A + a] = np.cos(ph)
                E1[c * A + f2, 128 + c * A + a] = np.sin(ph)
                E2[c * A + f2, c * A + a] = -np.sin(ph)
                E2[c * A + f2, 128 + c * A + a] = np.cos(ph)
    out["Dre"] = Dre; out["Dim"] = Dim; out["nDim"] = -Dim; out["E1"] = E1; out["E2"] = E2
    a_of_col = np.arange(128) % A
    TW2C = np.zeros((128, 2, 128)); TW2S = np.zeros((128, 2, 128))
    IC = np.zeros((128, 2, 128)); IS = np.zeros((128, 2, 128))
    for ch in range(2):
        ff = ch * 128 + np.arange(128)
        th2 = 2 * np.pi * ((ff[:, None] * a_of_col[None, :]) % N) / N
        TW2C[:, ch, :] = np.cos(th2) / N
        TW2S[:, ch, :] = np.sin(th2) / N
        ph = 2 * np.pi * ((ff[:, None] * p[None, :]) % 256) / 256
        IC[:, ch, :] = np.cos(ph)
        IS[:, ch, :] = -np.sin(ph)
    out["TW2C"] = TW2C; out["TW2S"] = TW2S; out["IC"] = IC; out["IS"] = IS
    tp = np.arange(N)
    pos = np.where(tp < Ls, tp, N - tp).astype(np.float64)
    tn = (pos / (Ls - 1)).astype(np.float32)
    w = ((2.0 * math.pi / Ls) * pos).astype(np.float32)
    fb = np.linspace(1e-4, 15.0, 16, dtype=np.float32)
    zT = np.zeros((33, N), np.float32)
    zT[0] = tn
    zT[1:17] = np.cos(fb[:, None] * w[None, :])
    zT[17:33] = np.sin(fb[:, None] * w[None, :])
    out["zT"] = zT
    deltas = np.abs(np.linspace(math.log(1e-2) / 1.5, math.log(1e-2) / 0.3, 256, dtype=np.float32))
    dec = np.exp(-tn[:, None] * deltas[None, :]).astype(np.float32)
    dec[Ls, :] = 0.0
    nblk = 256 // cbw
    ngr = cbw // G
    DEC = np.zeros((nblk, 128, 2, ngr, A, G), np.float32)
    for h in range(2):
        for a in range(A):
            tpp = A * (h * 128 + p) + a
            for b in range(nblk):
                DEC[b, :, h, :, a, :] = dec[tpp, b * cbw:(b + 1) * cbw].reshape(128, ngr, G)
    out["DEC"] = DEC.reshape(nblk, 128, 2 * ngr * A * G)
    return {f"hy{tag}_{k}": np.ascontiguousarray(v.astype(np.float32)) for k, v in out.items()}

CONST_SPECS = None


def build(depth=DEPTH, dbg=()):
    kb = KB()
    nc = kb.nc
    cst = host_consts()
    def inp(name, shape, dt=F32):
        return kb.dram(name, shape, dt, kind="ExternalInput")

    x_in = inp("x", [L, D])
    c_in = inp("c", [D])
    ctx_in = inp("ctx", [C, D])
    cctx_in = inp("c_ctx", [D])
    W = {}
    wspec = {
        "mod_w": [DEPTH, D, 3 * D], "mod_b": [DEPTH, 3 * D], "norm_g": [DEPTH, D], "w_in": [DEPTH, D, D_IN],
        "w_out": [DEPTH, D, D], "wa_sink": [DEPTH, 4], "fa_q_norm": [DEPTH, 64], "fa_k_norm": [DEPTH, 64],
        "final_g": [D],
        "rw_conv": [DEPTH, 3, 896], "rw_w0": [DEPTH, 2, 256], "rw_w_up": [DEPTH, 2, 64, 256], "rw_a0": [DEPTH, 2, 256],
        "rw_a_up": [DEPTH, 2, 64, 256], "rw_k_k": [DEPTH, 256], "rw_k_a": [DEPTH, 256], "rw_r_k": [DEPTH, 256],
        "rw_ln_g": [DEPTH, 256], "rw_ln_b": [DEPTH, 256],
        "hy_conv": [DEPTH, 3, 768], "hy_fw1": [DEPTH, 33, 64], "hy_fb1": [DEPTH, 64], "hy_freq": [DEPTH, 64],
        "hy_fw2": [DEPTH, 64, 64], "hy_fb2": [DEPTH, 64], "hy_fw3": [DEPTH, 64, 1024], "hy_bias": [DEPTH, 2, 256],
    }
    for n, s in wspec.items():
        W[n] = inp(n, s)
    CT = {}
    for n, a in cst.items():
        CT[n] = inp("k_" + n, list(a.shape), BF16 if a.dtype == ml_dtypes.bfloat16 else F32)
    out = kb.dram("out", [L, D], F32, kind="ExternalOutput")
    xres = kb.dram("xres", [T, D], F32)
    mixT = kb.dram("mixT", [D, T], BF16)
    RS = {}
    for n in ("RT", "VT", "AL", "W0", "W1", "B0", "B1", "KD0", "KD1", "YF", "YB"):
        RS[n] = kb.dram("rs_" + n, [256, T])
    RS["VTOK"] = kb.dram("rs_VTOK", [T, 256])
    RS["SGT"] = kb.dram("rs_SGT", [256, T], BF16)
    HS = {"SG": kb.dram("hs_SG", [256, T], BF16),
          "UTL": kb.dram("hs_UTL", [3, 8, 128, 32 * 32]), "UTC": kb.dram("hs_UTC", [3, 4, 128, 2 * 64])}
    dbg_t = {}
    for n, s in dbg:
        dbg_t[n] = kb.dram("dbg_" + n, s, F32, kind="ExternalOutput")

    ident_bf = kb.sb("ident_bf", [128, 128], BF16)
    ident_f = kb.sb("ident_f", [128, 128])
    blk64 = kb.sb("blk64", [128, 128])
    ones_f = kb.sb("ones_f", [128, 128])
    for tl, n in ((ident_bf, "ident_bf"), (ident_f, "ident_f"), (blk64, "blk64"), (ones_f, "ones_f")):
        kb.dma("sp", tl[:], CT[n][:, :], reads=[CT[n]], writes=[tl])
    hT = G1 = SH = None
    GT = kb.sb("GT", [128, 2, D])
    PS = [kb.ps(f"ps{i}", [128, 512]) for i in range(8)]

    xres_b = [Buf(f"xres{i}") for i in range(NT)]

    def x_src(l, i):
        if l == 0:
            if i < 2:
                return ctx_in[i * 128:(i + 1) * 128, :], ctx_in
            return x_in[(i - 2) * 128:(i - 1) * 128, :], x_in
        return xres[i * 128:(i + 1) * 128, :], xres_b[i]

    def phase_mod(l):
        with kb.scope():
            cc = kb.sb("cc", [128, 2, 8])
            sc = kb.sb("sc", [128, 2, 8])
            mw = [kb.sb(f"mw{i}", [128, 8, 512]) for i in range(2)]
            mb = kb.sb("mb", [128, 3 * D])
            ng = kb.sb("ng", [128, D])
            modr = kb.sb("modr", [128, 2, 3 * D])
            kb.dma("sp", cc[:, 0, :], c_in.t.rearrange("(j p) -> p j", p=128), reads=[c_in], writes=[cc], slow=True)
            kb.dma("sp", cc[:, 1, :], cctx_in.t.rearrange("(j p) -> p j", p=128), reads=[cctx_in], writes=[cc], slow=True)
            kb.dma("sp", mb[:], W["mod_b"][l, :].partition_broadcast(128), reads=[W["mod_b"]], writes=[mb])
            kb.dma("sp", ng[:], W["norm_g"][l, :].partition_broadcast(128), reads=[W["norm_g"]], writes=[ng])
            kb.op("act", lambda e: e.activation(sc[:], cc[:], AF.Silu), reads=[cc], writes=[sc])
            for n in range(6):
                m = mw[n % 2]
                kb.dma("sp" if n % 2 == 0 else "pool", m[:],
                       W["mod_w"][l, :, n * 512:(n + 1) * 512].rearrange("(j p) n -> p j n", p=128),
                       reads=[W["mod_w"]], writes=[m])
                for i in range(2):
                    p = PS[(2 * n + i) % 8]
                    for j in range(8):
                        kb.op("pe", lambda e, p=p, i=i, j=j, m=m: e.matmul(
                            p[:, :], sc[:, i, j:j + 1].broadcast_to([128, 128]), m[:, j, :],
                            start=(j == 0), stop=(j == 7)), reads=[sc, m], writes=[p])
                    kb.op("dve", lambda e, p=p, i=i, n=n: e.tensor_tensor(
                        modr[:, i, n * 512:(n + 1) * 512], p[:, :], mb[:, n * 512:(n + 1) * 512], ALU.add),
                        reads=[p, mb], writes=[modr])
            for i in range(2):
                kb.op("dve", lambda e, i=i: e.scalar_tensor_tensor(
                    G1[:, i, :], modr[:, i, D:2 * D], 1.0, ng[:], ALU.add, ALU.mult), reads=[modr, ng], writes=[G1])
                kb.op("act", lambda e, i=i: e.copy(SH[:, i, :], modr[:, i, 0:D]), reads=[modr], writes=[SH])
                kb.op("act", lambda e, i=i: e.copy(GT[:, i, :], modr[:, i, 2 * D:3 * D]), reads=[modr], writes=[GT])

    def phase_norm(l):
        with kb.scope():
            xt = [kb.sb(f"xt{i}", [128, D]) for i in range(3)]
            junk = kb.sb("junk", [128, D])
            hf = [kb.sb(f"hf{i}", [128, D]) for i in range(2)]
            hb = [kb.sb(f"hb{i}", [128, D], BF16) for i in range(2)]
            st = [kb.sb(f"st{i}", [128, 4]) for i in range(2)]
            for i in range(NT):
                x, s, h, hbt = xt[i % 3], st[i % 2], hf[i % 2], hb[i % 2]
                sel = 1 if i < 2 else 0
                src, srcb = x_src(l, i)
                kb.dma("sp" if i % 2 == 0 else "pool", x[:], src, reads=[srcb], writes=[x])
                kb.op("act", lambda e, x=x, s=s: e.activation(junk[:], x[:], AF.Square, accum_out=s[:, 0:1]),
                      reads=[x], writes=[junk, s])
                kb.op("dve", lambda e, s=s: e.tensor_scalar(s[:, 1:2], s[:, 0:1], 1.0 / D, EPS, ALU.mult, ALU.add),
                      reads=[s], writes=[s])
                kb.op("act", lambda e, s=s: e.sqrt(s[:, 2:3], s[:, 1:2]), reads=[s], writes=[s])
                kb.op("dve", lambda e, s=s: e.reciprocal(s[:, 3:4], s[:, 2:3]), reads=[s], writes=[s])
                kb.op("dve", lambda e, x=x, s=s, h=h, sel=sel: e.scalar_tensor_tensor(
                    h[:], x[:], s[:, 3:4], G1[:, sel, :], ALU.mult, ALU.mult), reads=[x, s, G1], writes=[h])
                kb.op("pool", lambda e, h=h, hbt=hbt, sel=sel: e.tensor_tensor(hbt[:], h[:], SH[:, sel, :], ALU.add),
                      reads=[h, SH], writes=[hbt])
                p = PS[i % 4]
                pv = p[:, :].bitcast(BF16)
                for j in range(8):
                    kb.op("pe", lambda e, j=j, pv=pv, hbt=hbt: e.transpose(
                        pv[:, j * 128:(j + 1) * 128], hbt[:, j * 128:(j + 1) * 128], ident_bf[:]),
                        reads=[hbt, ident_bf], writes=[p])
                kb.op("act", lambda e, pv=pv, i=i: e.copy(
                    hT[:, :, i * 128:(i + 1) * 128], pv.rearrange("p (j t) -> p j t", j=8)), reads=[p], writes=[hT])

    def load_w(l, dst, col0, ncols, stage, q="sp"):
        kb.dma(q, stage[:, :, 0:ncols], W["w_in"][l, :, col0:col0 + ncols].rearrange("(j p) n -> p j n", p=128),
               reads=[W["w_in"]], writes=[stage])
        kb.op("pool", lambda e: e.tensor_copy(dst[:, :, 0:ncols], stage[:, :, 0:ncols]), reads=[stage], writes=[dst])

    def proj_fm(p, wt, c0, nc_, t0, nt):
        for j in range(8):
            kb.op("pe", lambda e, j=j: e.matmul(p[0:nc_, 0:nt], wt[:, j, c0:c0 + nc_], hT[:, j, t0:t0 + nt],
                                                start=(j == 0), stop=(j == 7)), reads=[wt, hT], writes=[p])

    def proj_tm(p, wt, c0, nc_, i):
        for j in range(8):
            kb.op("pe", lambda e, j=j: e.matmul(p[:, 0:nc_], hT[:, j, i * 128:(i + 1) * 128], wt[:, j, c0:c0 + nc_],
                                                start=(j == 0), stop=(j == 7)), reads=[wt, hT], writes=[p])

    TCH = [(t0, min(512, T - t0)) for t0 in range(0, T, 512)]

    def qk_prep(l, es_tiles, wt, c0, dst, dst_j, gvec, norm, rope):
        raw, sq, rs, rot = es_tiles
        for ci, (t0, nt) in enumerate(TCH):
            p = PS[ci % 2]
            proj_fm(p, wt, c0, 128, t0, nt)
            if norm:
                kb.op("act", lambda e, p=p, nt=nt: e.activation(sq[:, 0:nt], p[:, 0:nt], AF.Square), reads=[p], writes=[sq])
                p2 = PS[2 + ci % 2]
                kb.op("pe", lambda e, p2=p2, nt=nt: e.matmul(p2[:, 0:nt], blk64[:], sq[:, 0:nt], start=True, stop=True),
                      reads=[blk64, sq], writes=[p2])
                kb.op("dve", lambda e, p2=p2, nt=nt: e.tensor_scalar(rs[:, 0:nt], p2[:, 0:nt], 1.0 / 64, EPS, ALU.mult, ALU.add),
                      reads=[p2], writes=[rs])
                kb.op("act", lambda e, nt=nt: e.sqrt(rs[:, 0:nt], rs[:, 0:nt]), reads=[rs], writes=[rs])
                kb.op("dve", lambda e, nt=nt: e.reciprocal(rs[:, 0:nt], rs[:, 0:nt]), reads=[rs], writes=[rs])
                kb.op("dve", lambda e, p=p, nt=nt: e.scalar_tensor_tensor(
                    raw[:, 0:nt], p[:, 0:nt], gvec[:, 0:1], rs[:, 0:nt], ALU.mult, ALU.mult), reads=[p, gvec, rs], writes=[raw])
            else:
                kb.op("act", lambda e, p=p, nt=nt: e.copy(raw[:, 0:nt], p[:, 0:nt]), reads=[p], writes=[raw])
            lat0 = 0
            if t0 < C:
                lat0 = C - t0
                kb.op("pool", lambda e, t0=t0, lat0=lat0: e.tensor_copy(dst[:, dst_j, t0:t0 + lat0], raw[:, 0:lat0]),
                      reads=[raw], writes=[dst])
            if not rope:
                if nt > lat0:
                    kb.op("pool", lambda e, t0=t0, lat0=lat0, nt=nt: e.tensor_copy(
                        dst[:, dst_j, t0 + lat0:t0 + nt], raw[:, lat0:nt]), reads=[raw], writes=[dst])
                continue
            p3 = PS[4 + ci % 2]
            n_l = nt - lat0
            lp = t0 + lat0 - C
            kb.op("pe", lambda e, p3=p3, lat0=lat0, nt=nt: e.matmul(p3[:, lat0:nt], rope_perm[:], raw[:, lat0:nt], start=True, stop=True),
                  reads=[rope_perm, raw], writes=[p3])
            kb.op("dve", lambda e, p3=p3, lat0=lat0, nt=nt, lp=lp, n_l=n_l: e.tensor_tensor(
                rot[:, lat0:nt], p3[:, lat0:nt], rope_sin[:, lp:lp + n_l], ALU.mult), reads=[p3, rope_sin], writes=[rot])
            kb.op("pool", lambda e, lat0=lat0, nt=nt, lp=lp, n_l=n_l: e.tensor_tensor(
                raw[:, lat0:nt], raw[:, lat0:nt], rope_cos[:, lp:lp + n_l], ALU.mult), reads=[raw, rope_cos], writes=[raw])
            kb.op("dve", lambda e, t0=t0, lat0=lat0, nt=nt: e.tensor_tensor(
                dst[:, dst_j, t0 + lat0:t0 + nt], raw[:, lat0:nt], rot[:, lat0:nt], ALU.add), reads=[raw, rot], writes=[dst])

    rope_cos = rope_sin = rope_perm = None

    def phase_attn(l, dense, with_ctx):
        nonlocal rope_cos, rope_sin, rope_perm
        base = FA0 if dense else WA0
        gbase = FAG0 if dense else WAG0
        mrow = 768 if dense else 512
        with kb.scope():
            wt = kb.sb("wt", [128, 8, 768], BF16)
            gq = kb.sb("gq", [128, 1])
            gk = kb.sb("gk", [128, 1])
            sink = kb.sb("sink", [128, 4])
            if dense:
                for hh in range(2):
                    kb.dma("sp", gq[hh * 64:(hh + 1) * 64, :], W["fa_q_norm"][l, :].rearrange("(d o) -> d o", o=1),
                           reads=[W["fa_q_norm"]], writes=[gq], slow=True)
                    kb.dma("sp", gk[hh * 64:(hh + 1) * 64, :], W["fa_k_norm"][l, :].rearrange("(d o) -> d o", o=1),
                           reads=[W["fa_k_norm"]], writes=[gk], slow=True)
            else:
                kb.dma("sp", sink[:], W["wa_sink"][l, :].partition_broadcast(128), reads=[W["wa_sink"]], writes=[sink])
            QT = kb.sb("QT", [128, 2, T], BF16)
            KT = kb.sb("KT", [128, 1, T], BF16)
            VW = 65 if dense else 64
            Vt = kb.sb("Vt", [128, NT, 2, VW], BF16)
            SG = None
            with kb.scope():
                rope_cos = kb.sb("rope_cos", [128, L])
                rope_sin = kb.sb("rope_sin", [128, L])
                rope_perm = kb.sb("rope_perm", [128, 128])
                kb.dma("sp", rope_cos[:], CT["rope_cos"][:, :], reads=[CT["rope_cos"]], writes=[rope_cos])
                kb.dma("pool", rope_sin[:], CT["rope_sin"][:, :], reads=[CT["rope_sin"]], writes=[rope_sin])
                kb.dma("sp", rope_perm[:], CT["rope_perm"][:, :], reads=[CT["rope_perm"]], writes=[rope_perm])
                stage = kb.sb("wstage", [128, 8, 256])
                w4 = W["w_in"][l, :, base:base + 256].rearrange("(j p) (h d) -> p j h d", p=128, d=64)
                st4 = stage[:, :, 0:256].rearrange("p j (h d) -> p j h d", d=64)
                for hi, h in enumerate((0, 2, 1, 3)):
                    kb.dma("sp", st4[:, :, hi, :], w4[:, :, h, :], reads=[W["w_in"]], writes=[stage])
                kb.op("pool", lambda e: e.tensor_copy(wt[:, :, 0:256], stage[:]), reads=[stage], writes=[wt])
                kb.dma("pool", stage[:], W["w_in"][l, :, base + 256:base + 512].rearrange("(j p) n -> p j n", p=128),
                       reads=[W["w_in"]], writes=[stage])
                kb.op("pool", lambda e: e.tensor_copy(wt[:, :, 256:512], stage[:]), reads=[stage], writes=[wt])
                kb.dma("sp", stage[:], W["w_in"][l, :, gbase:gbase + 256].rearrange("(j p) n -> p j n", p=128),
                       reads=[W["w_in"]], writes=[stage])
                kb.op("pool", lambda e: e.tensor_copy(wt[:, :, 512:768], stage[:]), reads=[stage], writes=[wt])
                tl = (kb.sb("qraw", [128, 512]), kb.sb("qsq", [128, 512]), kb.sb("qrs", [128, 512]), kb.sb("qrot", [128, 512]))
                qk_prep(l, tl, wt, 0, QT, 0, gq, dense, True)
                qk_prep(l, tl, wt, 128, QT, 1, gq, dense, True)
                qk_prep(l, tl, wt, 256, KT, 0, gk, dense, True)
            if dense:
                kb.op("pool", lambda e: e.memset(Vt[:, :, :, 64:65], 1.0), writes=[Vt])
            for i in range(NT):
                p = PS[i % 2]
                proj_tm(p, wt, 384, 128, i)
                kb.op("act", lambda e, p=p, i=i: e.copy(Vt[:, i, :, 0:64], p[:, 0:128].rearrange("p (k d) -> p k d", d=64)),
                      reads=[p], writes=[Vt])
            with kb.scope():
                if dense:
                    attn_dense(l, wt, QT, KT, Vt, mrow, with_ctx)
                else:
                    attn_window(l, wt, QT, KT, Vt, sink, mrow, with_ctx)

    def attn_dense(l, wt, QT, KT, Vt, mrow, with_ctx):
        pt = [kb.sb(f"pt{i}", [128, 512], BF16) for i in range(8)]
        osb = [kb.sb(f"osb{i}", [128, 512]) for i in range(2)]
        rc = [kb.sb(f"rc{i}", [128, 512]) for i in range(2)]
        ob = [kb.sb(f"ob{i}", [128, 512], BF16) for i in range(2)]
        sgt = [kb.sb(f"sgt{i}", [64, 512], BF16) for i in range(2)]
        chunks = []
        if with_ctx:
            chunks.append((0, C, 0, 2))
        for t0 in range(C, T, 512):
            chunks.append((t0, 512, 0, NT))
        for pr in range(2):
            heads_ = (pr, pr + 2)
            for (t0, nt, kb0, kb1) in chunks:
                po = [PS[4], PS[5]]
                pg = [PS[6], PS[7]]
                for s_, h in enumerate(heads_):
                    for j in range(8):
                        kb.op("pe", lambda e, j=j, s_=s_, h=h: e.matmul(
                            pg[s_][0:64, 0:nt], wt[:, j, 512 + 64 * h:576 + 64 * h], hT[:, j, t0:t0 + nt], start=(j == 0), stop=(j == 7)),
                            reads=[wt, hT], writes=[pg[s_]])
                    kb.op("act", lambda e, s_=s_: e.activation(sgt[s_][0:64, 0:nt], pg[s_][0:64, 0:nt], AF.Silu), reads=[pg[s_]], writes=[sgt[s_]])

                def pv_(kbi):
                    for s_ in range(2):
                        ptt = pt[(2 * kbi + s_) % 8]
                        kb.op("pe", lambda e, s_=s_, ptt=ptt: e.matmul(
                            po[s_][0:65, 0:nt], Vt[:, kbi, s_, 0:65], ptt[:, 0:nt], start=(kbi == kb0), stop=(kbi == kb1 - 1)),
                            reads=[Vt, ptt], writes=[po[s_]])
                LA = 1
                for kbi in range(kb0, kb1):
                    for s_ in range(2):
                        ks = slice(64 * s_, 64 * s_ + 64)
                        psS = PS[(2 * kbi + s_) % 4]
                        kb.op("pe", lambda e, psS=psS, ks=ks: e.matmul(
                            psS[:, 0:nt], KT[ks, 0, kbi * 128:(kbi + 1) * 128], QT[ks, pr, t0:t0 + nt], start=True, stop=True),
                            reads=[KT, QT], writes=[psS])
                    for s_ in range(2):
                        psS = PS[(2 * kbi + s_) % 4]
                        ptt = pt[(2 * kbi + s_) % 8]
                        kb.op("act", lambda e, psS=psS, ptt=ptt: e.activation(ptt[:, 0:nt], psS[:, 0:nt], AF.Exp, scale=0.125),
                              reads=[psS], writes=[ptt])
                    if kbi - LA >= kb0:
                        pv_(kbi - LA)
                for kbi in range(max(kb0, kb1 - LA), kb1):
                    pv_(kbi)
                for s_, h in enumerate(heads_):
                    o_s, r_c, o_b, sg = osb[s_], rc[s_], ob[s_], sgt[s_]
                    pb = pg[s_]
                    kb.op("dve", lambda e, s_=s_, r_c=r_c: e.reciprocal(r_c[64:65, 0:nt], po[s_][64:65, 0:nt]), reads=[po[s_]], writes=[r_c])
                    kb.op("act", lambda e, s_=s_, o_s=o_s: e.copy(o_s[0:64, 0:nt], po[s_][0:64, 0:nt]), reads=[po[s_]], writes=[o_s])
                    kb.op("pe", lambda e, pb=pb, r_c=r_c: e.matmul(pb[0:64, 0:nt], ones_f[64:65, 0:64], r_c[64:65, 0:nt], start=True, stop=True),
                          reads=[ones_f, r_c], writes=[pb])
                    kb.op("dve", lambda e, o_s=o_s, pb=pb: e.tensor_tensor(o_s[0:64, 0:nt], o_s[0:64, 0:nt], pb[0:64, 0:nt], ALU.mult),
                          reads=[o_s, pb], writes=[o_s])
                    kb.op("pool", lambda e, o_s=o_s, o_b=o_b, sg=sg: e.tensor_tensor(o_b[0:64, 0:nt], o_s[0:64, 0:nt], sg[0:64, 0:nt], ALU.mult),
                          reads=[o_s, sg], writes=[o_b])
                    kb.dma("sp", mixT[mrow + 64 * h:mrow + 64 * h + 64, t0:t0 + nt], o_b[0:64, 0:nt], reads=[o_b], writes=[Buf()])

    def attn_window(l, wt, QT, KT, Vt, sink, mrow, with_ctx):
        wmask = kb.sb("wmask", [128, 384])
        kb.dma("sp", wmask[:], CT["wmask"][:, :], reads=[CT["wmask"]], writes=[wmask])
        nsink = kb.sb("nsink", [128, 4])
        kb.op("dve", lambda e: e.tensor_scalar(nsink[:], sink[:], -1.0, None, ALU.mult), reads=[sink], writes=[nsink])
        S = [kb.sb(f"wS{i}", [128, 640]) for i in range(4)]
        P = [kb.sb(f"wP{i}", [128, 640]) for i in range(4)]
        Pn = [kb.sb(f"wPn{i}", [128, 640], BF16) for i in range(4)]
        PT = [kb.sb(f"wPT{i}", [128, 640], BF16) for i in range(4)]
        st = [kb.sb(f"wst{i}", [128, 8]) for i in range(4)]
        sgt = [kb.sb(f"wsg{i}", [64, 128], BF16) for i in range(4)]
        ob = [kb.sb(f"wob{i}", [64, 128], BF16) for i in range(4)]
        it = 0
        for i in range(0 if with_ctx else 2, NT):
            if i < 2:
                loc = []
            else:
                loc = list(range(max(2, i - 1), min(NT - 1, i + 1) + 1))
            nl = 128 * len(loc)
            m0 = 128 if (i >= 2 and i - 1 < 2) else 0
            nk = nl + C
            ktiles = loc + [0, 1]
            def unit(h, it):
                kv, pr = h // 2, h % 2
                ks = slice(64 * kv, 64 * kv + 64)
                s_, p_, pn_, pt_, st_, sg, o_b = S[it % 4], P[it % 4], Pn[it % 4], PT[it % 4], st[it % 4], sgt[it % 4], ob[it % 4]
                psA, psB = PS[2 * (it % 4)], PS[2 * (it % 4) + 1]
                psT, psOG = psA, psB
                psO, psG = psOG, psOG
                q_ap = QT[ks, pr, i * 128:(i + 1) * 128]
                if nl:
                    k0 = loc[0] * 128
                    kb.op("pe", lambda e: e.matmul(psA[:, 0:nl], q_ap, KT[ks, 0, k0:k0 + nl], start=True, stop=True),
                          reads=[QT, KT], writes=[psA])
                    kb.op("dve", lambda e: e.tensor_tensor(s_[:, 0:nl], psA[:, 0:nl], wmask[:, m0:m0 + nl], ALU.add),
                          reads=[psA, wmask], writes=[s_])
                kb.op("pe", lambda e: e.matmul(psB[:, 0:C], q_ap, KT[ks, 0, 0:C], start=True, stop=True),
                      reads=[QT, KT], writes=[psB])
                kb.op("act", lambda e: e.copy(s_[:, nl:nk], psB[:, 0:C]), reads=[psB], writes=[s_])
                yield
                kb.op("dve", lambda e: e.reduce_max(st_[:, 0:1], s_[:, 0:nk], AX.X), reads=[s_], writes=[st_])
                kb.op("dve", lambda e: e.tensor_scalar(st_[:, 1:2], st_[:, 0:1], -0.125, nsink[:, h:h + 1], ALU.mult, ALU.min),
                      reads=[st_, nsink], writes=[st_])
                kb.op("act", lambda e: e.activation(p_[:, 0:nk], s_[:, 0:nk], AF.Exp, bias=st_[:, 1:2], scale=0.125,
                                                    accum_out=st_[:, 2:3]), reads=[s_, st_], writes=[p_, st_])
                kb.op("act", lambda e: e.activation(st_[:, 3:4], sink[:, h:h + 1], AF.Exp, bias=st_[:, 1:2], scale=1.0),
                      reads=[sink, st_], writes=[st_])
                kb.op("dve", lambda e: e.tensor_tensor(st_[:, 4:5], st_[:, 2:3], st_[:, 3:4], ALU.add), reads=[st_], writes=[st_])
                kb.op("dve", lambda e: e.reciprocal(st_[:, 5:6], st_[:, 4:5]), reads=[st_], writes=[st_])
                kb.op("dve", lambda e: e.tensor_scalar(pn_[:, 0:nk], p_[:, 0:nk], st_[:, 5:6], None, ALU.mult),
                      reads=[p_, st_], writes=[pn_])
                yield
                pv = psT[:, :].bitcast(BF16)
                nb = nk // 128
                for b in range(nb):
                    kb.op("pe", lambda e, b=b: e.transpose(pv[:, b * 128:(b + 1) * 128], pn_[:, b * 128:(b + 1) * 128], ident_bf[:]),
                          reads=[pn_, ident_bf], writes=[psT])
                yield
                kb.op("act", lambda e: e.copy(pt_[:, 0:nk], pv[:, 0:nk]), reads=[psT], writes=[pt_])
                yield
                for b in range(nb):
                    kb.op("pe", lambda e, b=b: e.matmul(psO[0:64, 0:128], Vt[:, ktiles[b], kv, 0:64], pt_[:, b * 128:(b + 1) * 128],
                                                        start=(b == 0), stop=(b == nb - 1)), reads=[Vt, pt_], writes=[psO])
                for j in range(8):
                    kb.op("pe", lambda e, j=j: e.matmul(
                        psG[0:64, 128:256], wt[:, j, 512 + 64 * h:576 + 64 * h], hT[:, j, i * 128:(i + 1) * 128],
                        start=(j == 0), stop=(j == 7)), reads=[wt, hT], writes=[psG])
                yield
                kb.op("act", lambda e: e.activation(sg[:, :], psG[0:64, 128:256], AF.Silu), reads=[psG], writes=[sg])
                kb.op("dve", lambda e: e.tensor_tensor(o_b[:, :], psO[0:64, 0:128], sg[:, :], ALU.mult), reads=[psO, sg], writes=[o_b])
                kb.dma("sp", mixT[mrow + 64 * h:mrow + 64 * h + 64, i * 128:(i + 1) * 128], o_b[:, :], reads=[o_b], writes=[Buf()])

            for h0 in (0,):
                gens = [unit(h_, it + k_) for k_, h_ in enumerate((0, 2, 1, 3))]
                it += 4
                while gens:
                    nxt = []
                    for g_ in gens:
                        try:
                            next(g_)
                            nxt.append(g_)
                        except StopIteration:
                            pass
                    gens = nxt

    def phase_out(l, last):
        with kb.scope():
            wo = kb.sb("wo", [128, 8, D], BF16)
            stage = kb.sb("wostage", [128, 8, 256])
            for q in range(4):
                kb.dma("sp", stage[:], W["w_out"][l, :, q * 256:(q + 1) * 256].rearrange("(j p) n -> p j n", p=128),
                       reads=[W["w_out"]], writes=[stage])
                kb.op("pool", lambda e, q=q: e.tensor_copy(wo[:, :, q * 256:(q + 1) * 256], stage[:]), reads=[stage], writes=[wo])
            fg = kb.sb("fg", [128, D])
            if last:
                kb.dma("sp", fg[:], W["final_g"][:].partition_broadcast(128), reads=[W["final_g"]], writes=[fg])
            mt = [kb.sb(f"mt{i}", [128, 8, 128], BF16) for i in range(2)]
            xt = [kb.sb(f"oxt{i}", [128, D]) for i in range(2)]
            xn = [kb.sb(f"oxn{i}", [128, D]) for i in range(2)]
            tmp = [kb.sb(f"otmp{i}", [128, 512]) for i in range(2)]
            st = [kb.sb(f"ost{i}", [128, 4]) for i in range(2)]
            junk = kb.sb("ojunk", [128, D])
            mixv = mixT.t.rearrange("(j p) t -> p j t", p=128)
            for it, i in enumerate(range(2 if last else 0, NT)):
                m, x, xo, s = mt[it % 2], xt[it % 2], xn[it % 2], st[it % 2]
                sel = 1 if i < 2 else 0
                kb.dma("sp", m[:], mixv[:, :, i * 128:(i + 1) * 128], reads=[mixT], writes=[m])
                src, srcb = x_src(l, i)
                kb.dma("pool", x[:], src, reads=[srcb], writes=[x])
                for hf in range(2):
                    p = PS[(2 * it + hf) % 8]
                    tp = tmp[hf]
                    for j in range(8):
                        kb.op("pe", lambda e, j=j, p=p, m=m, hf=hf: e.matmul(p[:, :], m[:, j, :], wo[:, j, hf * 512:(hf + 1) * 512],
                                                                     start=(j == 0), stop=(j == 7)), reads=[m, wo], writes=[p])
                    kb.op("dve", lambda e, p=p, tp=tp, hf=hf, sel=sel: e.tensor_tensor(
                        tp[:], p[:, :], GT[:, sel, hf * 512:(hf + 1) * 512], ALU.mult), reads=[p, GT], writes=[tp])
                    kb.op("pool", lambda e, tp=tp, hf=hf, x=x, xo=xo: e.tensor_tensor(
                        xo[:, hf * 512:(hf + 1) * 512], x[:, hf * 512:(hf + 1) * 512], tp[:], ALU.add), reads=[x, tp], writes=[xo])
                if not last:
                    kb.dma("sp", xres[i * 128:(i + 1) * 128, :], xo[:], reads=[xo], writes=[xres_b[i]])
                else:
                    kb.op("act", lambda e, xo=xo, s=s: e.activation(junk[:], xo[:], AF.Square, accum_out=s[:, 0:1]),
                          reads=[xo], writes=[junk, s])
                    kb.op("dve", lambda e, s=s: e.tensor_scalar(s[:, 1:2], s[:, 0:1], 1.0 / D, EPS, ALU.mult, ALU.add),
                          reads=[s], writes=[s])
                    kb.op("act", lambda e, s=s: e.sqrt(s[:, 2:3], s[:, 1:2]), reads=[s], writes=[s])
                    kb.op("dve", lambda e, s=s: e.reciprocal(s[:, 3:4], s[:, 2:3]), reads=[s], writes=[s])
                    kb.op("dve", lambda e, xo=xo, s=s, x=x: e.scalar_tensor_tensor(
                        x[:], xo[:], s[:, 3:4], fg[:], ALU.mult, ALU.mult), reads=[xo, s, fg], writes=[x])
                    kb.dma("sp", out[(i - 2) * 128:(i - 1) * 128, :], x[:], reads=[x], writes=[Buf()])


    def conv_tile(l, wt, cw, ncw, jt, Zraw, Zout):
        for ci, (t0, nt) in enumerate(TCH):
            p = PS[ci % 4]
            proj_fm(p, wt, 0, 128, t0, nt)
            kb.op("act", lambda e, p=p, t0=t0, nt=nt: e.copy(Zraw[:, 1 + t0:1 + t0 + nt], p[:, 0:nt]), reads=[p], writes=[Zraw])
        kb.op("dve", lambda e: e.tensor_scalar(Zout[:, :], Zraw[:, 1:T + 1], cw[:, jt, 1:2], None, ALU.mult), reads=[Zraw, cw], writes=[Zout])
        kb.op("dve", lambda e: e.scalar_tensor_tensor(Zout[:, :], Zraw[:, 0:T], cw[:, jt, 0:1], Zout[:, :], ALU.mult, ALU.add),
              reads=[Zraw, cw, Zout], writes=[Zout])
        kb.op("dve", lambda e: e.scalar_tensor_tensor(Zout[:, :], Zraw[:, 2:T + 2], cw[:, jt, 2:3], Zout[:, :], ALU.mult, ALU.add),
              reads=[Zraw, cw, Zout], writes=[Zout])
        kb.op("dve", lambda e: e.scalar_tensor_tensor(Zout[:, C - 1:C], Zraw[:, C + 1:C + 2], ncw[:, jt, 2:3], Zout[:, C - 1:C], ALU.mult, ALU.add),
              reads=[Zraw, ncw, Zout], writes=[Zout])
        kb.op("dve", lambda e: e.scalar_tensor_tensor(Zout[:, C:C + 1], Zraw[:, C:C + 1], ncw[:, jt, 0:1], Zout[:, C:C + 1], ALU.mult, ALU.add),
              reads=[Zraw, ncw, Zout], writes=[Zout])

    def colvec(name, src_ap, srcb, shape, rearr, **kw):
        t = kb.sb(name, shape)
        kb.dma("sp", t[:], src_ap.rearrange(rearr, **kw), reads=[srcb], writes=[t], slow=True)
        return t

    def phase_rwkv_prep(l):
        with kb.scope():
            stage = kb.sb("rstage", [128, 8, 128])
            wts = [kb.sb(f"rwt{i}", [128, 8, 128], BF16) for i in range(2)]
            cw = kb.sb("rcw", [128, 7, 3])
            for k in range(3):
                kb.dma("sp", cw[:, :, k], W["rw_conv"][l, k, :].rearrange("(j p) -> p j", p=128), reads=[W["rw_conv"]], writes=[cw], slow=True)
            ncw = kb.sb("rncw", [128, 7, 3])
            kb.op("dve", lambda e: e.tensor_scalar(ncw[:], cw[:], -1.0, None, ALU.mult), reads=[cw], writes=[ncw])
            kk_ = colvec("rkk", W["rw_k_k"][l, :], W["rw_k_k"], [128, 2], "(j p) -> p j", p=128)
            ka_ = colvec("rka", W["rw_k_a"][l, :], W["rw_k_a"], [128, 2], "(j p) -> p j", p=128)
            omka = kb.sb("romka", [128, 2])
            kb.op("dve", lambda e: e.tensor_scalar(omka[:], ka_[:], -1.0, 1.0, ALU.mult, ALU.add), reads=[ka_], writes=[omka])
            w0_ = kb.sb("rw0", [128, 2, 2])
            a0_ = kb.sb("ra0", [128, 2, 2])
            for d in range(2):
                kb.dma("sp", w0_[:, d, :], W["rw_w0"][l, d, :].rearrange("(j p) -> p j", p=128), reads=[W["rw_w0"]], writes=[w0_], slow=True)
                kb.dma("sp", a0_[:, d, :], W["rw_a0"][l, d, :].rearrange("(j p) -> p j", p=128), reads=[W["rw_a0"]], writes=[a0_], slow=True)
            wup = kb.sb("rwup", [128, 2, 256])
            kb.dma("sp", wup[0:64, :, :], W["rw_w_up"][l, :, :, :].rearrange("d k n -> k d n"), reads=[W["rw_w_up"]], writes=[wup])
            kb.dma("sp", wup[64:128, :, :], W["rw_a_up"][l, :, :, :].rearrange("d k n -> k d n"), reads=[W["rw_a_up"]], writes=[wup])
            Zraw = kb.sb("rZraw", [128, T + 2])
            Zout = kb.sb("rZout", [128, T])
            Z6 = kb.sb("rZ6", [128, T])
            kb.op("pool", lambda e: e.memset(Zraw[:, 0:1], 0.0), writes=[Zraw])
            kb.op("pool", lambda e: e.memset(Zraw[:, T + 1:T + 2], 0.0), writes=[Zraw])
            tA = [kb.sb(f"rtA{i}", [128, 512]) for i in range(2)]
            tB = [kb.sb(f"rtB{i}", [128, 512]) for i in range(2)]
            tC = [kb.sb(f"rtC{i}", [128, 512]) for i in range(2)]
            tD = [kb.sb(f"rtD{i}", [128, 512]) for i in range(2)]
            tE = [kb.sb(f"rtE{i}", [128, 512]) for i in range(2)]
            tG = [kb.sb(f"rtG{i}", [128, 512], BF16) for i in range(2)]
            vt_ = [kb.sb(f"rvt{i}", [128, 128]) for i in range(2)]
            order = [6, 0, 1, 4, 5, 2, 3, 7, 8]
            for oi, jt in enumerate(order):
                wt = wts[oi % 2]
                c0 = RW0 + jt * 128 if jt < 7 else RWG0 + (jt - 7) * 128
                load_w(l, wt, c0, 128, stage)
                if jt >= 7:
                    for ci, (t0, nt) in enumerate(TCH):
                        p = PS[ci % 4]
                        proj_fm(p, wt, 0, 128, t0, nt)
                        g = tG[ci % 2]
                        kb.op("act", lambda e, p=p, g=g, nt=nt: e.activation(g[:, 0:nt], p[:, 0:nt], AF.Silu), reads=[p], writes=[g])
                        kb.dma("sp", RS["SGT"][(jt - 7) * 128:(jt - 6) * 128, t0:t0 + nt], g[:, 0:nt], reads=[g], writes=[Buf()])
                    continue
                conv_tile(l, wt, cw, ncw, jt, Zraw, Z6 if jt == 6 else Zout)
                if jt == 6:
                    kb.op("act", lambda e: e.activation(Z6[0:64, :], Z6[0:64, :], AF.Tanh), reads=[Z6], writes=[Z6])
                elif jt in (0, 1):
                    kb.dma("sp", RS["RT"][jt * 128:(jt + 1) * 128, :], Zout[:, :], reads=[Zout], writes=[Buf()])
                elif jt in (4, 5):
                    kb.dma("sp", RS["VT"][(jt - 4) * 128:(jt - 3) * 128, :], Zout[:, :], reads=[Zout], writes=[Buf()])
                    for i in range(NT):
                        p = PS[4 + i % 2]
                        kb.op("pe", lambda e, p=p, i=i: e.transpose(p[:, 0:128], Zout[:, i * 128:(i + 1) * 128], ident_f[:]),
                              reads=[Zout, ident_f], writes=[p])
                        v = vt_[i % 2]
                        kb.op("act", lambda e, p=p, v=v: e.copy(v[:, :], p[:, 0:128]), reads=[p], writes=[v])
                        kb.dma("pool", RS["VTOK"][i * 128:(i + 1) * 128, (jt - 4) * 128:(jt - 3) * 128], v[:, :], reads=[v], writes=[Buf()])
                else:
                    pt = jt - 2
                    rows = slice(pt * 128, (pt + 1) * 128)
                    for ci, (t0, nt) in enumerate(TCH):
                        a_, b_, c_, d_, e_ = tA[ci % 2], tB[ci % 2], tC[ci % 2], tD[ci % 2], tE[ci % 2]
                        zc = Zout[:, t0:t0 + nt]
                        kb.op("dve", lambda e: e.tensor_scalar(a_[:, 0:nt], zc, kk_[:, pt:pt + 1], None, ALU.mult), reads=[Zout, kk_], writes=[a_])
                        kb.op("act", lambda e: e.activation(b_[:, 0:nt], a_[:, 0:nt], AF.Square), reads=[a_], writes=[b_])
                        p = PS[ci % 2]
                        kb.op("pe", lambda e: e.matmul(p[:, 0:nt], blk64[:], b_[:, 0:nt], start=True, stop=True), reads=[blk64, b_], writes=[p])
                        kb.op("act", lambda e: e.sqrt(b_[:, 0:nt], p[:, 0:nt]), reads=[p], writes=[b_])
                        kb.op("dve", lambda e: e.tensor_scalar(b_[:, 0:nt], b_[:, 0:nt], 1e-12, None, ALU.max), reads=[b_], writes=[b_])
                        kb.op("dve", lambda e: e.reciprocal(b_[:, 0:nt], b_[:, 0:nt]), reads=[b_], writes=[b_])
                        kb.op("dve", lambda e: e.scalar_tensor_tensor(a_[:, 0:nt], a_[:, 0:nt], -1.0, b_[:, 0:nt], ALU.mult, ALU.mult),
                              reads=[a_, b_], writes=[a_])
                        kb.dma("sp", RS["AL"][rows, t0:t0 + nt], a_[:, 0:nt], reads=[a_], writes=[Buf()])
                        for d in range(2):
                            pa = PS[2 + d]
                            kb.op("pe", lambda e: e.matmul(pa[:, 0:nt], wup[64:128, d, pt * 128:(pt + 1) * 128], Z6[64:128, t0:t0 + nt],
                                                           start=True, stop=True), reads=[wup, Z6], writes=[pa])
                            kb.op("act", lambda e: e.activation(c_[:, 0:nt], pa[:, 0:nt], AF.Sigmoid, bias=a0_[:, d, pt:pt + 1]),
                                  reads=[pa, a0_], writes=[c_])
                            kb.op("dve", lambda e: e.scalar_tensor_tensor(d_[:, 0:nt], c_[:, 0:nt], -1.0, a_[:, 0:nt], ALU.mult, ALU.mult),
                                  reads=[c_, a_], writes=[d_])
                            kb.dma("sp", RS[f"B{d}"][rows, t0:t0 + nt], d_[:, 0:nt], reads=[d_], writes=[Buf()])
                            kb.op("dve", lambda e: e.tensor_scalar(c_[:, 0:nt], c_[:, 0:nt], ka_[:, pt:pt + 1], omka[:, pt:pt + 1], ALU.mult, ALU.add),
                                  reads=[c_, ka_, omka], writes=[c_])
                            kb.op("dve", lambda e: e.tensor_tensor(e_[:, 0:nt], c_[:, 0:nt], zc, ALU.mult), reads=[c_, Zout], writes=[e_])
                            kb.dma("pool", RS[f"KD{d}"][rows, t0:t0 + nt], e_[:, 0:nt], reads=[e_], writes=[Buf()])
                            pw = PS[4 + d]
                            kb.op("pe", lambda e: e.matmul(pw[:, 0:nt], wup[0:64, d, pt * 128:(pt + 1) * 128], Z6[0:64, t0:t0 + nt],
                                                           start=True, stop=True), reads=[wup, Z6], writes=[pw])
                            kb.op("act", lambda e: e.activation(c_[:, 0:nt], pw[:, 0:nt], AF.Sigmoid, bias=w0_[:, d, pt:pt + 1]),
                                  reads=[pw, w0_], writes=[c_])
                            kb.op("dve", lambda e: e.tensor_scalar(d_[:, 0:nt], c_[:, 0:nt], -math.exp(-0.5), None, ALU.mult),
                                  reads=[c_], writes=[d_])
                            kb.dma("pool", RS[f"W{d}"][rows, t0:t0 + nt], d_[:, 0:nt], reads=[d_], writes=[Buf()])

    def phase_rwkv_scan(l):
        with kb.scope():
            ST = [kb.sb(f"ST{d}", [128, 2, 64]) for d in range(2)]
            for d in range(2):
                kb.op("pool", lambda e, d=d: e.memset(ST[d][:], 0.0), writes=[ST[d]])
            names = ("AL", "W", "B", "KD", "RT")
            ch = [[{n: kb.sb(f"c{n}{d}{i}", [128, 2, 128]) for n in names} for i in range(2)] for d in range(2)]
            vch = [[kb.sb(f"cV{d}{i}", [128, 256]) for i in range(2)] for d in range(2)]
            t1 = [kb.sb(f"st1{d}", [128, 2, 64]) for d in range(2)]
            t2 = [kb.sb(f"st2{d}", [128, 2, 64]) for d in range(2)]
            ysb = [kb.sb(f"ysb{d}", [64, 512]) for d in range(2)]
            psSA, psV, psY = [PS[0], PS[1]], [PS[2], PS[3]], [PS[4], PS[5]]
            border = [1, 0] + list(range(NT - 1, 1, -1))
            for ci in range(NT):
                cidx = [ci, border[ci]]
                cur = []
                for d in range(2):
                    c0 = cidx[d] * 128
                    tl_ = ch[d][ci % 2]
                    for n in names:
                        src = RS[n if n in ("AL", "RT") else f"{n}{d}"]
                        kb.dma("sp" if d == 0 else "pool", tl_[n][:],
                               src.t.rearrange("(pr q) t -> q pr t", q=128)[:, :, c0:c0 + 128], reads=[src], writes=[tl_[n]])
                    vv = vch[d][ci % 2]
                    kb.dma("sp" if d == 0 else "pool", vv[:], RS["VTOK"][c0:c0 + 128, :], reads=[RS["VTOK"]], writes=[vv])
                    cur.append((tl_, vv))
                for tl in range(128):
                    for d in range(2):
                        col = tl if d == 0 else 127 - tl
                        tl_, vv = cur[d]
                        S_, sa, pv, py = ST[d], psSA[d], psV[d], psY[d]
                        for pr in range(2):
                            for hp in range(2):
                                rows = slice(64 * hp, 64 * hp + 64)
                                kb.op("pe", lambda e, pr=pr, rows=rows: e.matmul(
                                    sa[rows, pr * 64:(pr + 1) * 64], tl_["AL"][rows, pr, col:col + 1].broadcast_to([64, 64]),
                                    S_[rows, pr, :], start=True, stop=True), reads=[tl_["AL"], S_], writes=[sa])
                        for pr in range(2):
                            for hp in range(2):
                                rows = slice(64 * hp, 64 * hp + 64)
                                h = 2 * pr + hp
                                kb.op("pe", lambda e, pr=pr, rows=rows, h=h: e.matmul(
                                    pv[rows, pr * 64:(pr + 1) * 64], ident_f[:, col:col + 1].broadcast_to([128, 64]),
                                    vv[:, h * 64:(h + 1) * 64], start=True, stop=True), reads=[ident_f, vv], writes=[pv])
                        for pr in range(2):
                            kb.op("dve", lambda e, pr=pr: e.tensor_scalar(
                                t1[d][:, pr, :], sa[:, pr * 64:(pr + 1) * 64], tl_["B"][:, pr, col:col + 1], None, ALU.mult),
                                reads=[sa, tl_["B"]], writes=[t1[d]])
                            kb.op("dve", lambda e, pr=pr: e.scalar_tensor_tensor(
                                t2[d][:, pr, :], pv[:, pr * 64:(pr + 1) * 64], tl_["KD"][:, pr, col:col + 1], t1[d][:, pr, :], ALU.mult, ALU.add),
                                reads=[pv, tl_["KD"], t1[d]], writes=[t2[d]])
                            kb.op("dve", lambda e, pr=pr: e.scalar_tensor_tensor(
                                S_[:, pr, :], S_[:, pr, :], tl_["W"][:, pr, col:col + 1], t2[d][:, pr, :], ALU.mult, ALU.add),
                                reads=[S_, tl_["W"], t2[d]], writes=[S_])
                        for pr in range(2):
                            for hp in range(2):
                                rows = slice(64 * hp, 64 * hp + 64)
                                h = 2 * pr + hp
                                kb.op("pe", lambda e, pr=pr, rows=rows, h=h: e.matmul(
                                    py[0:64, h * 128 + col:h * 128 + col + 1], S_[rows, pr, :], tl_["RT"][rows, pr, col:col + 1],
                                    start=True, stop=True), reads=[S_, tl_["RT"]], writes=[py])
                for d in range(2):
                    c0 = cidx[d] * 128
                    kb.op("act", lambda e, d=d: e.copy(ysb[d][:, :], psY[d][0:64, :]), reads=[psY[d]], writes=[ysb[d]])
                    dst = RS["YF" if d == 0 else "YB"]
                    kb.dma("sp", dst.t.rearrange("(h v) t -> v h t", v=64)[:, :, c0:c0 + 128],
                           ysb[d][:, :].rearrange("v (h t) -> v h t", h=4), reads=[ysb[d]], writes=[Buf()])


    def phase_rwkv_chunked(l):
        CH = 64
        NCH = T // CH
        with kb.scope():
            def ldc(nm, shape):
                t = kb.sb("k" + nm, shape)
                kb.dma("sp", t[:], CT[nm].t, reads=[CT[nm]], writes=[t])
                return t
            Ms = ldc("rw_ms", [128, 2, 64]); MTs = ldc("rw_mts", [128, 2, 64]); MTi = ldc("rw_mti", [128, 2, 64])
            id2 = ldc("rw_id2", [128, 64])
            ones = kb.sb("rones", [128, 64])
            kb.op("pool", lambda e: e.memset(ones[:], 1.0), writes=[ones])
            ST = kb.sb("cST", [128, 4, 64])
            kb.op("pool", lambda e: e.memset(ST[:], 0.0), writes=[ST])
            names = ("AL", "W", "B", "KD", "RT")
            def t4(nm, n=2, w=64):
                return [kb.sb(f"{nm}{i}", [128, 4, w]) for i in range(n)]
            IN = {n: t4("ci" + n) for n in names}
            VTK = t4("cVTK")
            CS = t4("cCS", 1)[0]; TOT = kb.sb("cTOT", [128, 4]); TMP = t4("cTMP", 1)[0]
            Epos = t4("cEp", 1)[0]; Eneg = t4("cEn", 1)[0]; Eprev = t4("cEv", 1)[0]; Etot = t4("cEt", 1)[0]; Wtot = kb.sb("cWt", [128, 4])
            Ab = t4("cAb", 1)[0]; Bb = t4("cBb", 1)[0]; Kb = t4("cKb", 1)[0]; Rb = t4("cRb", 1)[0]; Bt = t4("cBt", 1)[0]; Kt = t4("cKt", 1)[0]
            Q = t4("cQ"); P = t4("cP"); ArbT = t4("cArbT", 1)[0]; AkvT = t4("cAkvT", 1)[0]; ArkT = t4("cArkT", 1)[0]
            X = t4("cX", 2, 128); Btok = t4("cBtok", 1)[0]; Ktok = t4("cKtok", 1)[0]
            RAT = t4("cRAT", 1)[0]; McT = t4("cMcT", 1)[0]; NcS = t4("cNcS", 1)[0]; DG = t4("cDG", 1)[0]
            ysb = [kb.sb(f"cysb{d}", [64, 256]) for d in range(2)]
            border = [3, 2, 1, 0] + list(range(NCH - 1, 3, -1))
            DP = [(d, pr) for d in range(2) for pr in range(2)]
            HP = [slice(0, 64), slice(64, 128)]

            def mm_all(ps, col_fn, lhs_fn, rhs_fn, reads, start=True, stop=True, w=None):
                for dp in range(4):
                    for hp in range(2):
                        r = HP[hp]
                        c0, c1 = col_fn(dp)
                        kb.op("pe", lambda e, dp=dp, r=r, c0=c0, c1=c1: e.matmul(ps[r, c0:c1], lhs_fn(dp, r), rhs_fn(dp, r), start=start, stop=stop),
                              reads=reads, writes=[ps])

            for ci in range(NCH):
                cidx = [ci, border[ci]]
                i2 = ci % 2
                for d in range(2):
                    c0 = cidx[d] * CH
                    for n in names:
                        src = RS[n if n in ("AL", "RT") else f"{n}{d}"]
                        kb.dma("sp" if d == 0 else "pool", IN[n][i2][:, 2 * d:2 * d + 2, :],
                               src.t.rearrange("(pr q) t -> q pr t", q=128)[:, :, c0:c0 + CH], reads=[src], writes=[IN[n][i2]])
                    for hp in range(2):
                        kb.dma("sp" if d == 0 else "pool", VTK[i2][HP[hp], 2 * d:2 * d + 2, :],
                               RS["VTOK"][c0:c0 + CH, :].rearrange("t (pr hp v) -> t pr hp v", pr=2, hp=2)[:, :, hp, :],
                               reads=[RS["VTOK"]], writes=[VTK[i2]])
                al, lw, be, kd, rt, vt = IN["AL"][i2], IN["W"][i2], IN["B"][i2], IN["KD"][i2], IN["RT"][i2], VTK[i2]
                if RW_STAGE <= 1:
                    continue
                for dp in range(4):
                    kb.op("dve", lambda e, dp=dp: e.tensor_tensor_scan(CS[:, dp, :], ones[:, :], lw[:, dp, :], 0.0, ALU.mult, ALU.add),
                          reads=[ones, lw], writes=[CS])
                kb.op("dve", lambda e: e.tensor_copy(TOT[:, :], CS[:, :, CH - 1]), reads=[CS], writes=[TOT])
                kb.op("dve", lambda e: e.tensor_tensor(CS[:, 2:4, :], lw[:, 2:4, :], CS[:, 2:4, :], ALU.subtract), reads=[lw, CS], writes=[CS])
                kb.op("dve", lambda e: e.tensor_tensor(CS[:, 2:4, :], CS[:, 2:4, :], TOT[:, 2:4].unsqueeze(2).broadcast_to([128, 2, CH]), ALU.add),
                      reads=[CS, TOT], writes=[CS])
                kb.op("act", lambda e: e.activation(Epos[:], CS[:], AF.Exp), reads=[CS], writes=[Epos])
                kb.op("act", lambda e: e.activation(Eneg[:], CS[:], AF.Exp, scale=-1.0), reads=[CS], writes=[Eneg])
                kb.op("pool", lambda e: e.tensor_tensor(TMP[:], CS[:], lw[:], ALU.subtract), reads=[CS, lw], writes=[TMP])
                kb.op("act", lambda e: e.activation(Eprev[:], TMP[:], AF.Exp), reads=[TMP], writes=[Eprev])
                kb.op("dve", lambda e: e.tensor_tensor(Etot[:], TOT[:, :].unsqueeze(2).broadcast_to([128, 4, CH]), CS[:], ALU.subtract),
                      reads=[TOT, CS], writes=[Etot])
                kb.op("act", lambda e: e.activation(Etot[:], Etot[:], AF.Exp), reads=[Etot], writes=[Etot])
                kb.op("act", lambda e: e.activation(Wtot[:], TOT[:], AF.Exp), reads=[TOT], writes=[Wtot])
                kb.op("dve", lambda e: e.tensor_tensor(Ab[:], al[:], Eprev[:], ALU.mult), reads=[al, Eprev], writes=[Ab])
                kb.op("pool", lambda e: e.tensor_tensor(Bb[:], be[:], Eneg[:], ALU.mult), reads=[be, Eneg], writes=[Bb])
                kb.op("dve", lambda e: e.tensor_tensor(Kb[:], kd[:], Eneg[:], ALU.mult), reads=[kd, Eneg], writes=[Kb])
                kb.op("pool", lambda e: e.tensor_tensor(Rb[:], rt[:], Epos[:], ALU.mult), reads=[rt, Epos], writes=[Rb])
                kb.op("dve", lambda e: e.tensor_tensor(Bt[:], be[:], Etot[:], ALU.mult), reads=[be, Etot], writes=[Bt])
                kb.op("pool", lambda e: e.tensor_tensor(Kt[:], kd[:], Etot[:], ALU.mult), reads=[kd, Etot], writes=[Kt])
                if RW_STAGE <= 2:
                    continue
                PA, PB, PC, PT1, PD, PX, PPQ, PE_ = PS
                mm_all(PA, lambda dp: (dp * 128, dp * 128 + 64), lambda dp, r: Bb[r, dp, :], lambda dp, r: Ab[r, dp, :], [Bb, Ab])
                mm_all(PA, lambda dp: (dp * 128 + 64, dp * 128 + 128), lambda dp, r: Bb[r, dp, :], lambda dp, r: Rb[r, dp, :], [Bb, Rb])
                mm_all(PB, lambda dp: (dp * 128, dp * 128 + 64), lambda dp, r: Kb[r, dp, :], lambda dp, r: Ab[r, dp, :], [Kb, Ab])
                mm_all(PB, lambda dp: (dp * 128 + 64, dp * 128 + 128), lambda dp, r: Kb[r, dp, :], lambda dp, r: Rb[r, dp, :], [Kb, Rb])
                mm_all(PC, lambda dp: (dp * 64, dp * 64 + 64), lambda dp, r: Ab[r, dp, :], lambda dp, r: Bb[r, dp, :], [Ab, Bb])
                q0, p0 = Q[0], P[0]
                pav = PA[:, :].rearrange("p (dp x) -> p dp x", dp=4)
                pbv = PB[:, :].rearrange("p (dp x) -> p dp x", dp=4)
                def mk(m):
                    return m[:, :, :].unsqueeze(2).broadcast_to([128, 2, 2, 64])
                def v4(ap):
                    return ap.rearrange("p (d pr) x -> p d pr x", d=2)
                kb.op("dve", lambda e: e.tensor_tensor(v4(q0[:]), v4(pav[:, :, 0:64]), mk(MTs), ALU.mult), reads=[PA, MTs], writes=[q0])
                kb.op("dve", lambda e: e.tensor_tensor(v4(ArbT[:]), v4(pav[:, :, 64:128]), mk(MTi), ALU.mult), reads=[PA, MTi], writes=[ArbT])
                kb.op("dve", lambda e: e.tensor_tensor(v4(AkvT[:]), v4(pbv[:, :, 0:64]), mk(MTs), ALU.mult), reads=[PB, MTs], writes=[AkvT])
                kb.op("dve", lambda e: e.tensor_tensor(v4(ArkT[:]), v4(pbv[:, :, 64:128]), mk(MTi), ALU.mult), reads=[PB, MTi], writes=[ArkT])
                kb.op("dve", lambda e: e.tensor_tensor(v4(p0[:]), v4(PC[:, 0:256].rearrange("p (dp x) -> p dp x", dp=4)), mk(Ms), ALU.mult),
                      reads=[PC, Ms], writes=[p0])
                if RW_STAGE <= 3:
                    continue
                def idb(r):
                    return ident_f[r, r.start:r.start + 64]
                mm_all(PT1, lambda dp: (dp * 128, dp * 128 + 64), lambda dp, r: Ab[r, dp, :], lambda dp, r: idb(r), [Ab, ident_f])
                mm_all(PT1, lambda dp: (dp * 128 + 64, dp * 128 + 128), lambda dp, r: Bt[r, dp, :], lambda dp, r: idb(r), [Bt, ident_f])
                mm_all(PC, lambda dp: (256 + dp * 64, 256 + dp * 64 + 64), lambda dp, r: Kt[r, dp, :], lambda dp, r: idb(r), [Kt, ident_f])
                x0 = X[0]
                pt1v = PT1[:, :].rearrange("p (dp x) -> p dp x", dp=4)
                kb.op("act", lambda e: e.copy(x0[:, :, 0:64], pt1v[:, :, 0:64]), reads=[PT1], writes=[x0])
                kb.op("act", lambda e: e.copy(Btok[:], pt1v[:, :, 64:128]), reads=[PT1], writes=[Btok])
                kb.op("act", lambda e: e.copy(Ktok[:], PC[:, 256:512].rearrange("p (dp x) -> p dp x", dp=4)), reads=[PC], writes=[Ktok])
                if RW_STAGE <= 4:
                    continue
                mm_all(PD, lambda dp: (dp * 64, dp * 64 + 64), lambda dp, r: AkvT[r, dp, :], lambda dp, r: vt[r, dp, :], [AkvT, vt])
                kb.op("act", lambda e: e.copy(x0[:, :, 64:128], PD[:, 0:256].rearrange("p (dp x) -> p dp x", dp=4)), reads=[PD], writes=[x0])
                if RW_STAGE <= 5:
                    continue
                qc, pc, xc = Q[0], P[0], X[0]
                for j in range(6):
                    qn, pn, xn = Q[(j + 1) % 2], P[(j + 1) % 2], X[(j + 1) % 2]
                    mm_all(PX, lambda dp: (dp * 128, dp * 128 + 128), lambda dp, r: qc[r, dp, :], lambda dp, r: xc[r, dp, :], [qc, xc])
                    kb.op("dve", lambda e, xn=xn, xc=xc: e.tensor_tensor(xn[:], xc[:], PX[:, :].rearrange("p (dp x) -> p dp x", dp=4), ALU.add),
                          reads=[xc, PX], writes=[xn])
                    if j < 5:
                        mm_all(PPQ, lambda dp: (dp * 64, dp * 64 + 64), lambda dp, r: qc[r, dp, :], lambda dp, r: pc[r, dp, :], [qc, pc])
                        mm_all(PPQ, lambda dp: (256 + dp * 64, 256 + dp * 64 + 64), lambda dp, r: pc[r, dp, :], lambda dp, r: qc[r, dp, :], [qc, pc])
                        kb.op("act", lambda e, pn=pn: e.copy(pn[:], PPQ[:, 0:256].rearrange("p (dp x) -> p dp x", dp=4)), reads=[PPQ], writes=[pn])
                        kb.op("act", lambda e, qn=qn: e.copy(qn[:], PPQ[:, 256:512].rearrange("p (dp x) -> p dp x", dp=4)), reads=[PPQ], writes=[qn])
                    qc, pc, xc = qn, pn, xn
                if RW_STAGE <= 6:
                    continue
                mm_all(PD, lambda dp: (256 + dp * 64, 256 + dp * 64 + 64), lambda dp, r: xc[r, dp, 0:64], lambda dp, r: ArbT[r, dp, :], [xc, ArbT])
                kb.op("dve", lambda e: e.tensor_tensor(RAT[:], Rb[:], PD[:, 256:512].rearrange("p (dp x) -> p dp x", dp=4), ALU.add),
                      reads=[Rb, PD], writes=[RAT])
                mm_all(PE_, lambda dp: (dp * 64, dp * 64 + 64), lambda dp, r: xc[r, dp, 0:64], lambda dp, r: Btok[r, dp, :], [xc, Btok])
                kb.op("pool", lambda e: e.tensor_tensor(DG[:], id2[:, :].unsqueeze(1).broadcast_to([128, 4, 64]),
                                                        Wtot[:, :].unsqueeze(2).broadcast_to([128, 4, 64]), ALU.mult), reads=[id2, Wtot], writes=[DG])
                kb.op("dve", lambda e: e.tensor_tensor(McT[:], DG[:], PE_[:, 0:256].rearrange("p (dp x) -> p dp x", dp=4), ALU.add),
                      reads=[DG, PE_], writes=[McT])
                for dp in range(4):
                    for hp in range(2):
                        r = HP[hp]
                        c0 = 256 + dp * 64
                        kb.op("pe", lambda e, dp=dp, r=r, c0=c0: e.matmul(PE_[r, c0:c0 + 64], Btok[r, dp, :], xc[r, dp, 64:128], start=True, stop=False),
                              reads=[Btok, xc], writes=[PE_])
                        kb.op("pe", lambda e, dp=dp, r=r, c0=c0: e.matmul(PE_[r, c0:c0 + 64], Ktok[r, dp, :], vt[r, dp, :], start=False, stop=True),
                              reads=[Ktok, vt], writes=[PE_])
                kb.op("act", lambda e: e.copy(NcS[:], PE_[:, 256:512].rearrange("p (dp x) -> p dp x", dp=4)), reads=[PE_], writes=[NcS])
                if RW_STAGE <= 7:
                    continue
                PYs = [PA, PT1]
                for dp in range(4):
                    for hp in range(2):
                        r = HP[hp]
                        PY = PYs[hp]
                        c0 = dp * 64
                        kb.op("pe", lambda e, dp=dp, r=r, c0=c0, PY=PY: e.matmul(PY[0:64, c0:c0 + 64], ST[r, dp, :], RAT[r, dp, :], start=True, stop=False),
                              reads=[ST, RAT], writes=[PY])
                        kb.op("pe", lambda e, dp=dp, r=r, c0=c0, PY=PY: e.matmul(PY[0:64, c0:c0 + 64], xc[r, dp, 64:128], ArbT[r, dp, :], start=False, stop=False),
                              reads=[xc, ArbT], writes=[PY])
                        kb.op("pe", lambda e, dp=dp, r=r, c0=c0, PY=PY: e.matmul(PY[0:64, c0:c0 + 64], vt[r, dp, :], ArkT[r, dp, :], start=False, stop=True),
                              reads=[vt, ArkT], writes=[PY])
                for d in range(2):
                    c0 = cidx[d] * CH
                    yv = ysb[d][:, :].rearrange("v (pr hp t) -> v pr hp t", pr=2, hp=2)
                    for hp in range(2):
                        kb.op("act", lambda e, d=d, hp=hp, yv=yv: e.copy(
                            yv[:, :, hp, :], PYs[hp][0:64, d * 128:(d + 1) * 128].rearrange("v (pr t) -> v pr t", pr=2)), reads=[PYs[hp]], writes=[ysb[d]])
                    dst = RS["YF" if d == 0 else "YB"]
                    kb.dma("sp", dst.t.rearrange("(h v) t -> v h t", v=64)[:, :, c0:c0 + CH],
                           ysb[d][:, :].rearrange("v (h t) -> v h t", h=4), reads=[ysb[d]], writes=[Buf()])
                if RW_STAGE <= 8:
                    continue
                PSS = PB
                mm_all(PSS, lambda dp: (dp * 64, dp * 64 + 64), lambda dp, r: McT[r, dp, :], lambda dp, r: ST[r, dp, :], [McT, ST])
                kb.op("dve", lambda e: e.tensor_tensor(ST[:], NcS[:], PSS[:, 0:256].rearrange("p (dp x) -> p dp x", dp=4), ALU.add),
                      reads=[NcS, PSS], writes=[ST])


    def phase_rwkv_chunked3(l):
        CH = 64
        NCH = T // CH
        with kb.scope():
            def ldc(nm, shape):
                t = kb.sb("k" + nm, shape)
                kb.dma("sp", t[:], CT[nm].t, reads=[CT[nm]], writes=[t])
                return t
            Ms = ldc("rw_ms", [128, 2, 64]); MTs = ldc("rw_mts", [128, 2, 64]); MTi = ldc("rw_mti", [128, 2, 64])
            id2 = ldc("rw_id2", [128, 64])
            ones = kb.sb("rones", [128, 64])
            kb.op("pool", lambda e: e.memset(ones[:], 1.0), writes=[ones])
            ST = kb.sb("cST", [128, 4, 64])
            kb.op("pool", lambda e: e.memset(ST[:], 0.0), writes=[ST])
            names = ("AL", "W", "B", "KD", "RT")
            import types
            def alloc_set(si):
                S = types.SimpleNamespace()
                def t4(nm, n=2, w=64):
                    return [kb.sb(f"{nm}s{si}_{i}", [128, 4, w]) for i in range(n)]
                S.IN = {n: t4("ci" + n, 1)[0] for n in names}
                S.VTK = t4("cVTK", 1)[0]
                S.CS = t4("cCS", 1)[0]; S.TOT = kb.sb(f"cTOT{si}", [128, 4]); S.TMP = t4("cTMP", 1)[0]
                S.Epos = t4("cEp", 1)[0]; S.Eneg = t4("cEn", 1)[0]; S.Eprev = t4("cEv", 1)[0]; S.Etot = t4("cEt", 1)[0]; S.Wtot = kb.sb(f"cWt{si}", [128, 4])
                S.Ab = t4("cAb", 1)[0]; S.Bb = t4("cBb", 1)[0]; S.Kb = t4("cKb", 1)[0]; S.Rb = t4("cRb", 1)[0]; S.Bt = t4("cBt", 1)[0]; S.Kt = t4("cKt", 1)[0]
                S.Q = t4("cQ"); S.P = t4("cP"); S.ArbT = t4("cArbT", 1)[0]; S.AkvT = t4("cAkvT", 1)[0]; S.ArkT = t4("cArkT", 1)[0]
                S.X = t4("cX", 2, 128); S.Btok = t4("cBtok", 1)[0]; S.Ktok = t4("cKtok", 1)[0]
                S.RAT = t4("cRAT", 1)[0]; S.McT = t4("cMcT", 1)[0]; S.NcS = t4("cNcS", 1)[0]; S.DG = t4("cDG", 1)[0]
                S.ysb = [kb.sb(f"cysb{si}_{d}", [64, 256]) for d in range(2)]
                S.banks = PS[4 * si:4 * si + 4]
                return S
            SETS = [alloc_set(0), alloc_set(1)]
            border = [3, 2, 1, 0] + list(range(NCH - 1, 3, -1))
            DP = [(d, pr) for d in range(2) for pr in range(2)]
            HP = [slice(0, 64), slice(64, 128)]

            def mm_all(ps, col_fn, lhs_fn, rhs_fn, reads, start=True, stop=True, w=None):
                for dp in range(4):
                    for hp in range(2):
                        r = HP[hp]
                        c0, c1 = col_fn(dp)
                        kb.op("pe", lambda e, dp=dp, r=r, c0=c0, c1=c1: e.matmul(ps[r, c0:c1], lhs_fn(dp, r), rhs_fn(dp, r), start=start, stop=stop),
                              reads=reads, writes=[ps])

            def chunk_gen(ci, S):
                cidx = [ci, border[ci]]
                IN, VTK = S.IN, S.VTK
                for d in range(2):
                    c0 = cidx[d] * CH
                    for n in names:
                        src = RS[n if n in ("AL", "RT") else f"{n}{d}"]
                        kb.dma("sp" if d == 0 else "pool", IN[n][:, 2 * d:2 * d + 2, :],
                               src.t.rearrange("(pr q) t -> q pr t", q=128)[:, :, c0:c0 + CH], reads=[src], writes=[IN[n]])
                    for hp in range(2):
                        kb.dma("sp" if d == 0 else "pool", VTK[HP[hp], 2 * d:2 * d + 2, :],
                               RS["VTOK"][c0:c0 + CH, :].rearrange("t (pr hp v) -> t pr hp v", pr=2, hp=2)[:, :, hp, :],
                               reads=[RS["VTOK"]], writes=[VTK])
                al, lw, be, kd, rt, vt = IN["AL"], IN["W"], IN["B"], IN["KD"], IN["RT"], VTK
                CS, TOT, TMP, Epos, Eneg, Eprev, Etot, Wtot = S.CS, S.TOT, S.TMP, S.Epos, S.Eneg, S.Eprev, S.Etot, S.Wtot
                Ab, Bb, Kb, Rb, Bt, Kt, Q, P, ArbT, AkvT, ArkT = S.Ab, S.Bb, S.Kb, S.Rb, S.Bt, S.Kt, S.Q, S.P, S.ArbT, S.AkvT, S.ArkT
                X, Btok, Ktok, RAT, McT, NcS, DG, ysb = S.X, S.Btok, S.Ktok, S.RAT, S.McT, S.NcS, S.DG, S.ysb
                yield
                for dp in range(4):
                    kb.op("dve", lambda e, dp=dp: e.tensor_tensor_scan(CS[:, dp, :], ones[:, :], lw[:, dp, :], 0.0, ALU.mult, ALU.add),
                          reads=[ones, lw], writes=[CS])
                kb.op("dve", lambda e: e.tensor_copy(TOT[:, :], CS[:, :, CH - 1]), reads=[CS], writes=[TOT])
                kb.op("dve", lambda e: e.tensor_tensor(CS[:, 2:4, :], lw[:, 2:4, :], CS[:, 2:4, :], ALU.subtract), reads=[lw, CS], writes=[CS])
                kb.op("dve", lambda e: e.tensor_tensor(CS[:, 2:4, :], CS[:, 2:4, :], TOT[:, 2:4].unsqueeze(2).broadcast_to([128, 2, CH]), ALU.add),
                      reads=[CS, TOT], writes=[CS])
                kb.op("act", lambda e: e.activation(Epos[:], CS[:], AF.Exp), reads=[CS], writes=[Epos])
                kb.op("act", lambda e: e.activation(Eneg[:], CS[:], AF.Exp, scale=-1.0), reads=[CS], writes=[Eneg])
                kb.op("pool", lambda e: e.tensor_tensor(TMP[:], CS[:], lw[:], ALU.subtract), reads=[CS, lw], writes=[TMP])
                kb.op("act", lambda e: e.activation(Eprev[:], TMP[:], AF.Exp), reads=[TMP], writes=[Eprev])
                kb.op("dve", lambda e: e.tensor_tensor(Etot[:], TOT[:, :].unsqueeze(2).broadcast_to([128, 4, CH]), CS[:], ALU.subtract),
                      reads=[TOT, CS], writes=[Etot])
                kb.op("act", lambda e: e.activation(Etot[:], Etot[:], AF.Exp), reads=[Etot], writes=[Etot])
                kb.op("act", lambda e: e.activation(Wtot[:], TOT[:], AF.Exp), reads=[TOT], writes=[Wtot])
                kb.op("dve", lambda e: e.tensor_tensor(Ab[:], al[:], Eprev[:], ALU.mult), reads=[al, Eprev], writes=[Ab])
                kb.op("pool", lambda e: e.tensor_tensor(Bb[:], be[:], Eneg[:], ALU.mult), reads=[be, Eneg], writes=[Bb])
                kb.op("dve", lambda e: e.tensor_tensor(Kb[:], kd[:], Eneg[:], ALU.mult), reads=[kd, Eneg], writes=[Kb])
                kb.op("pool", lambda e: e.tensor_tensor(Rb[:], rt[:], Epos[:], ALU.mult), reads=[rt, Epos], writes=[Rb])
                kb.op("dve", lambda e: e.tensor_tensor(Bt[:], be[:], Etot[:], ALU.mult), reads=[be, Etot], writes=[Bt])
                kb.op("pool", lambda e: e.tensor_tensor(Kt[:], kd[:], Etot[:], ALU.mult), reads=[kd, Etot], writes=[Kt])
                yield
                PA, PB, PC, PT1 = S.banks
                PD, PX, PPQ, PE_ = PA, PB, PC, PT1
                mm_all(PA, lambda dp: (dp * 128, dp * 128 + 64), lambda dp, r: Bb[r, dp, :], lambda dp, r: Ab[r, dp, :], [Bb, Ab])
                mm_all(PA, lambda dp: (dp * 128 + 64, dp * 128 + 128), lambda dp, r: Bb[r, dp, :], lambda dp, r: Rb[r, dp, :], [Bb, Rb])
                mm_all(PB, lambda dp: (dp * 128, dp * 128 + 64), lambda dp, r: Kb[r, dp, :], lambda dp, r: Ab[r, dp, :], [Kb, Ab])
                mm_all(PB, lambda dp: (dp * 128 + 64, dp * 128 + 128), lambda dp, r: Kb[r, dp, :], lambda dp, r: Rb[r, dp, :], [Kb, Rb])
                mm_all(PC, lambda dp: (dp * 64, dp * 64 + 64), lambda dp, r: Ab[r, dp, :], lambda dp, r: Bb[r, dp, :], [Ab, Bb])
                q0, p0 = Q[0], P[0]
                pav = PA[:, :].rearrange("p (dp x) -> p dp x", dp=4)
                pbv = PB[:, :].rearrange("p (dp x) -> p dp x", dp=4)
                def mk(m):
                    return m[:, :, :].unsqueeze(2).broadcast_to([128, 2, 2, 64])
                def v4(ap):
                    return ap.rearrange("p (d pr) x -> p d pr x", d=2)
                kb.op("dve", lambda e: e.tensor_tensor(v4(q0[:]), v4(pav[:, :, 0:64]), mk(MTs), ALU.mult), reads=[PA, MTs], writes=[q0])
                kb.op("dve", lambda e: e.tensor_tensor(v4(ArbT[:]), v4(pav[:, :, 64:128]), mk(MTi), ALU.mult), reads=[PA, MTi], writes=[ArbT])
                kb.op("dve", lambda e: e.tensor_tensor(v4(AkvT[:]), v4(pbv[:, :, 0:64]), mk(MTs), ALU.mult), reads=[PB, MTs], writes=[AkvT])
                kb.op("dve", lambda e: e.tensor_tensor(v4(ArkT[:]), v4(pbv[:, :, 64:128]), mk(MTi), ALU.mult), reads=[PB, MTi], writes=[ArkT])
                kb.op("dve", lambda e: e.tensor_tensor(v4(p0[:]), v4(PC[:, 0:256].rearrange("p (dp x) -> p dp x", dp=4)), mk(Ms), ALU.mult),
                      reads=[PC, Ms], writes=[p0])
                yield
                def idb(r):
                    return ident_f[r, r.start:r.start + 64]
                mm_all(PT1, lambda dp: (dp * 128, dp * 128 + 64), lambda dp, r: Ab[r, dp, :], lambda dp, r: idb(r), [Ab, ident_f])
                mm_all(PT1, lambda dp: (dp * 128 + 64, dp * 128 + 128), lambda dp, r: Bt[r, dp, :], lambda dp, r: idb(r), [Bt, ident_f])
                mm_all(PC, lambda dp: (256 + dp * 64, 256 + dp * 64 + 64), lambda dp, r: Kt[r, dp, :], lambda dp, r: idb(r), [Kt, ident_f])
                x0 = X[0]
                pt1v = PT1[:, :].rearrange("p (dp x) -> p dp x", dp=4)
                kb.op("act", lambda e: e.copy(x0[:, :, 0:64], pt1v[:, :, 0:64]), reads=[PT1], writes=[x0])
                kb.op("act", lambda e: e.copy(Btok[:], pt1v[:, :, 64:128]), reads=[PT1], writes=[Btok])
                kb.op("act", lambda e: e.copy(Ktok[:], PC[:, 256:512].rearrange("p (dp x) -> p dp x", dp=4)), reads=[PC], writes=[Ktok])
                yield
                mm_all(PD, lambda dp: (dp * 64, dp * 64 + 64), lambda dp, r: AkvT[r, dp, :], lambda dp, r: vt[r, dp, :], [AkvT, vt])
                kb.op("act", lambda e: e.copy(x0[:, :, 64:128], PD[:, 0:256].rearrange("p (dp x) -> p dp x", dp=4)), reads=[PD], writes=[x0])
                yield
                qc, pc, xc = Q[0], P[0], X[0]
                for j in range(6):
                    qn, pn, xn = Q[(j + 1) % 2], P[(j + 1) % 2], X[(j + 1) % 2]
                    mm_all(PX, lambda dp: (dp * 128, dp * 128 + 128), lambda dp, r: qc[r, dp, :], lambda dp, r: xc[r, dp, :], [qc, xc])
                    kb.op("dve", lambda e, xn=xn, xc=xc: e.tensor_tensor(xn[:], xc[:], PX[:, :].rearrange("p (dp x) -> p dp x", dp=4), ALU.add),
                          reads=[xc, PX], writes=[xn])
                    if j < 5:
                        mm_all(PPQ, lambda dp: (dp * 64, dp * 64 + 64), lambda dp, r: qc[r, dp, :], lambda dp, r: pc[r, dp, :], [qc, pc])
                        mm_all(PPQ, lambda dp: (256 + dp * 64, 256 + dp * 64 + 64), lambda dp, r: pc[r, dp, :], lambda dp, r: qc[r, dp, :], [qc, pc])
                        kb.op("act", lambda e, pn=pn: e.copy(pn[:], PPQ[:, 0:256].rearrange("p (dp x) -> p dp x", dp=4)), reads=[PPQ], writes=[pn])
                        kb.op("act", lambda e, qn=qn: e.copy(qn[:], PPQ[:, 256:512].rearrange("p (dp x) -> p dp x", dp=4)), reads=[PPQ], writes=[qn])
                    qc, pc, xc = qn, pn, xn
                    yield
                yield
                mm_all(PD, lambda dp: (256 + dp * 64, 256 + dp * 64 + 64), lambda dp, r: xc[r, dp, 0:64], lambda dp, r: ArbT[r, dp, :], [xc, ArbT])
                kb.op("dve", lambda e: e.tensor_tensor(RAT[:], Rb[:], PD[:, 256:512].rearrange("p (dp x) -> p dp x", dp=4), ALU.add),
                      reads=[Rb, PD], writes=[RAT])
                mm_all(PE_, lambda dp: (dp * 64, dp * 64 + 64), lambda dp, r: xc[r, dp, 0:64], lambda dp, r: Btok[r, dp, :], [xc, Btok])
                kb.op("pool", lambda e: e.tensor_tensor(DG[:], id2[:, :].unsqueeze(1).broadcast_to([128, 4, 64]),
                                                        Wtot[:, :].unsqueeze(2).broadcast_to([128, 4, 64]), ALU.mult), reads=[id2, Wtot], writes=[DG])
                kb.op("dve", lambda e: e.tensor_tensor(McT[:], DG[:], PE_[:, 0:256].rearrange("p (dp x) -> p dp x", dp=4), ALU.add),
                      reads=[DG, PE_], writes=[McT])
                for dp in range(4):
                    for hp in range(2):
                        r = HP[hp]
                        c0 = 256 + dp * 64
                        kb.op("pe", lambda e, dp=dp, r=r, c0=c0: e.matmul(PE_[r, c0:c0 + 64], Btok[r, dp, :], xc[r, dp, 64:128], start=True, stop=False),
                              reads=[Btok, xc], writes=[PE_])
                        kb.op("pe", lambda e, dp=dp, r=r, c0=c0: e.matmul(PE_[r, c0:c0 + 64], Ktok[r, dp, :], vt[r, dp, :], start=False, stop=True),
                              reads=[Ktok, vt], writes=[PE_])
                kb.op("act", lambda e: e.copy(NcS[:], PE_[:, 256:512].rearrange("p (dp x) -> p dp x", dp=4)), reads=[PE_], writes=[NcS])
                yield
                PYs = [PA, PB]
                for dp in range(4):
                    for hp in range(2):
                        r = HP[hp]
                        PY = PYs[hp]
                        c0 = dp * 64
                        kb.op("pe", lambda e, dp=dp, r=r, c0=c0, PY=PY: e.matmul(PY[0:64, c0:c0 + 64], ST[r, dp, :], RAT[r, dp, :], start=True, stop=False),
                              reads=[ST, RAT], writes=[PY])
                        kb.op("pe", lambda e, dp=dp, r=r, c0=c0, PY=PY: e.matmul(PY[0:64, c0:c0 + 64], xc[r, dp, 64:128], ArbT[r, dp, :], start=False, stop=False),
                              reads=[xc, ArbT], writes=[PY])
                        kb.op("pe", lambda e, dp=dp, r=r, c0=c0, PY=PY: e.matmul(PY[0:64, c0:c0 + 64], vt[r, dp, :], ArkT[r, dp, :], start=False, stop=True),
                              reads=[vt, ArkT], writes=[PY])
                for d in range(2):
                    c0 = cidx[d] * CH
                    yv = ysb[d][:, :].rearrange("v (pr hp t) -> v pr hp t", pr=2, hp=2)
                    for hp in range(2):
                        kb.op("act", lambda e, d=d, hp=hp, yv=yv: e.copy(
                            yv[:, :, hp, :], PYs[hp][0:64, d * 128:(d + 1) * 128].rearrange("v (pr t) -> v pr t", pr=2)), reads=[PYs[hp]], writes=[ysb[d]])
                    dst = RS["YF" if d == 0 else "YB"]
                    kb.dma("sp", dst.t.rearrange("(h v) t -> v h t", v=64)[:, :, c0:c0 + CH],
                           ysb[d][:, :].rearrange("v (h t) -> v h t", h=4), reads=[ysb[d]], writes=[Buf()])
                PSS = PT1
                mm_all(PSS, lambda dp: (dp * 64, dp * 64 + 64), lambda dp, r: McT[r, dp, :], lambda dp, r: ST[r, dp, :], [McT, ST])
                kb.op("dve", lambda e: e.tensor_tensor(ST[:], NcS[:], PSS[:, 0:256].rearrange("p (dp x) -> p dp x", dp=4), ALU.add),
                      reads=[NcS, PSS], writes=[ST])


            def lockstep(gens):
                gens = list(gens)
                while gens:
                    nxt = []
                    for g_ in gens:
                        try:
                            next(g_)
                            nxt.append(g_)
                        except StopIteration:
                            pass
                    gens = nxt
            for ci in range(0, NCH, 2):
                lockstep([chunk_gen(ci, SETS[0]), chunk_gen(ci + 1, SETS[1])])

    def phase_rwkv_chunked2(l):
        CH = 64
        NCH = T // CH
        with kb.scope():
            def ldc(nm, shape):
                t = kb.sb("k" + nm, shape)
                kb.dma("sp", t[:], CT[nm].t, reads=[CT[nm]], writes=[t])
                return t
            MsB = ldc("rw_msb", [128, 2, 128]); MTsB = ldc("rw_mtsb", [128, 2, 128]); MTi = ldc("rw_mti", [128, 2, 64])
            identr = kb.sb("cidr", [128, 128], F32R)
            kb.op("dve", lambda e: e.tensor_copy(identr[:], ident_f[:]), reads=[ident_f], writes=[identr])
            ones = kb.sb("rones", [128, 64])
            kb.op("pool", lambda e: e.memset(ones[:], 1.0), writes=[ones])
            def bd(nm, n=1, dt=F32R):
                ts = [kb.sb(f"{nm}{i}", [128, 4, 128], dt) for i in range(n)]
                for t in ts:
                    kb.op("pool", lambda e, t=t: e.memset(t[:].bitcast(F32) if dt == F32R else t[:], 0.0), writes=[t])
                return ts
            def t4(nm, n=1, w=64, dt=F32):
                return [kb.sb(f"{nm}{i}", [128, 4, w], dt) for i in range(n)]
            f32 = lambda ap: ap.bitcast(F32)
            names = ("AL", "W", "B", "KD", "RT")
            IN = {n: t4("di" + n, 2) for n in names}
            VT = bd("dVT", 2, F32)
            VTr = bd("dVTr")[0]
            ST = bd("dST")[0]
            CS = t4("dCS")[0]; TOT = kb.sb("dTOT", [128, 4]); TMP = t4("dTMP")[0]
            Epos = t4("dEp")[0]; Eneg = t4("dEn")[0]; Eprev = t4("dEv")[0]; Etot = t4("dEt")[0]; Wtot = kb.sb("dWt", [128, 4])
            Ab = bd("dAb")[0]; Bb = bd("dBb")[0]; Kb = bd("dKb")[0]; Bt = bd("dBt")[0]; Kt = bd("dKt")[0]
            Rb = t4("dRb", 1, 64, F32R)[0]
            Q = bd("dQ", 2); P = bd("dP", 2); AkvT = bd("dAkvT")[0]
            ArbT = t4("dArbT", 1, 64, F32R)[0]; ArkT = t4("dArkT", 1, 64, F32R)[0]; RAT = t4("dRAT", 1, 64, F32R)[0]
            X = [kb.sb(f"dX{i}", [128, 4, 256], F32R) for i in range(2)]
            Btok = bd("dBtok")[0]; Ktok = bd("dKtok")[0]; McT = bd("dMcT")[0]
            NcS = bd("dNcS", 1, F32)[0]; DG = bd("dDG", 1, F32)[0]
            ysb = [kb.sb(f"dysb{d}", [128, 2, 64]) for d in range(2)]
            border = [3, 2, 1, 0] + list(range(NCH - 1, 3, -1))
            H0, H1 = slice(0, 64), slice(64, 128)
            B0, B1, B2, B3, B4, B5, B6, B7 = PS

            def mm4(ps, c0, w, lhs, rhs, reads, start=True, stop=True):
                for dp in range(4):
                    kb.op("pe", lambda e, dp=dp: e.matmul(ps[:, c0 + dp * w:c0 + (dp + 1) * w], lhs(dp), rhs(dp), start=start, stop=stop),
                          reads=reads, writes=[ps])

            def v4(ap):
                return ap.rearrange("p (d pr) x -> p d pr x", d=2)

            def mk(m, w):
                return m[:, :, :].unsqueeze(2).broadcast_to([128, 2, 2, w])

            def pv(ps, c0, w):
                return ps[:, c0:c0 + 4 * w].rearrange("p (dp x) -> p dp x", dp=4)

            for ci in range(NCH):
                cidx = [ci, border[ci]]
                i2 = ci % 2
                vt = VT[i2]
                for d in range(2):
                    c0 = cidx[d] * CH
                    q_ = "sp" if d == 0 else "pool"
                    for n in names:
                        src = RS[n if n in ("AL", "RT") else f"{n}{d}"]
                        kb.dma(q_, IN[n][i2][:, 2 * d:2 * d + 2, :],
                               src.t.rearrange("(pr q) t -> q pr t", q=128)[:, :, c0:c0 + CH], reads=[src], writes=[IN[n][i2]])
                    for hp in range(2):
                        kb.dma(q_, vt[hp * 64:(hp + 1) * 64, 2 * d:2 * d + 2, hp * 64:(hp + 1) * 64],
                               RS["VTOK"][c0:c0 + CH, :].rearrange("t (pr hp v) -> t pr hp v", pr=2, hp=2)[:, :, hp, :],
                               reads=[RS["VTOK"]], writes=[vt])
                al, lw, be, kd, rt = IN["AL"][i2], IN["W"][i2], IN["B"][i2], IN["KD"][i2], IN["RT"][i2]
                kb.op("act", lambda e: e.copy(VTr[:], vt[:]), reads=[vt], writes=[VTr])
                for dp in range(4):
                    kb.op("dve", lambda e, dp=dp: e.tensor_tensor_scan(CS[:, dp, :], ones[:, :], lw[:, dp, :], 0.0, ALU.mult, ALU.add),
                          reads=[ones, lw], writes=[CS])
                kb.op("dve", lambda e: e.tensor_copy(TOT[:, :], CS[:, :, CH - 1]), reads=[CS], writes=[TOT])
                kb.op("dve", lambda e: e.tensor_tensor(CS[:, 2:4, :], lw[:, 2:4, :], CS[:, 2:4, :], ALU.subtract), reads=[lw, CS], writes=[CS])
                kb.op("dve", lambda e: e.tensor_tensor(CS[:, 2:4, :], CS[:, 2:4, :], TOT[:, 2:4].unsqueeze(2).broadcast_to([128, 2, CH]), ALU.add),
                      reads=[CS, TOT], writes=[CS])
                kb.op("act", lambda e: e.activation(Epos[:], CS[:], AF.Exp), reads=[CS], writes=[Epos])
                kb.op("act", lambda e: e.activation(Eneg[:], CS[:], AF.Exp, scale=-1.0), reads=[CS], writes=[Eneg])
                kb.op("pool", lambda e: e.tensor_tensor(TMP[:], CS[:], lw[:], ALU.subtract), reads=[CS, lw], writes=[TMP])
                kb.op("act", lambda e: e.activation(Eprev[:], TMP[:], AF.Exp), reads=[TMP], writes=[Eprev])
                kb.op("pool", lambda e: e.tensor_tensor(Etot[:], TOT[:, :].unsqueeze(2).broadcast_to([128, 4, CH]), CS[:], ALU.subtract),
                      reads=[TOT, CS], writes=[Etot])
                kb.op("act", lambda e: e.activation(Etot[:], Etot[:], AF.Exp), reads=[Etot], writes=[Etot])
                kb.op("act", lambda e: e.activation(Wtot[:], TOT[:], AF.Exp), reads=[TOT], writes=[Wtot])
                for k_, (dst, a_, b_) in enumerate(((Ab, al, Eprev), (Bb, be, Eneg), (Kb, kd, Eneg), (Bt, be, Etot), (Kt, kd, Etot))):
                    for hi, r in enumerate((H0, H1)):
                        eng = "dve" if (k_ + hi) % 2 == 0 else "pool"
                        kb.op(eng, lambda e, dst=dst, a_=a_, b_=b_, r=r: e.tensor_tensor(dst[r, :, r.start:r.start + 64], a_[r, :, :], b_[r, :, :], ALU.mult),
                              reads=[a_, b_], writes=[dst])
                kb.op("pool", lambda e: e.tensor_tensor(Rb[:], rt[:], Epos[:], ALU.mult), reads=[rt, Epos], writes=[Rb])
                mm4(B0, 0, 128, lambda dp: Bb[:, dp, :], lambda dp: Ab[:, dp, :], [Bb, Ab])
                mm4(B1, 0, 128, lambda dp: Kb[:, dp, :], lambda dp: Ab[:, dp, :], [Kb, Ab])
                mm4(B2, 0, 128, lambda dp: Ab[:, dp, :], lambda dp: Bb[:, dp, :], [Ab, Bb])
                mm4(B3, 0, 64, lambda dp: Bb[:, dp, :], lambda dp: Rb[:, dp, :], [Bb, Rb])
                mm4(B3, 256, 64, lambda dp: Kb[:, dp, :], lambda dp: Rb[:, dp, :], [Kb, Rb])
                q0, p0, x0 = Q[0], P[0], X[0]
                kb.op("dve", lambda e: e.tensor_tensor(v4(q0[:]), v4(pv(B0, 0, 128)), mk(MTsB, 128), ALU.mult), reads=[B0, MTsB], writes=[q0])
                kb.op("dve", lambda e: e.tensor_tensor(v4(AkvT[:]), v4(pv(B1, 0, 128)), mk(MTsB, 128), ALU.mult), reads=[B1, MTsB], writes=[AkvT])
                kb.op("dve", lambda e: e.tensor_tensor(v4(p0[:]), v4(pv(B2, 0, 128)), mk(MsB, 128), ALU.mult), reads=[B2, MsB], writes=[p0])
                kb.op("dve", lambda e: e.tensor_tensor(v4(ArbT[:]), v4(pv(B3, 0, 64)), mk(MTi, 64), ALU.mult), reads=[B3, MTi], writes=[ArbT])
                kb.op("dve", lambda e: e.tensor_tensor(v4(ArkT[:]), v4(pv(B3, 256, 64)), mk(MTi, 64), ALU.mult), reads=[B3, MTi], writes=[ArkT])
                mm4(B4, 0, 128, lambda dp: Ab[:, dp, :], lambda dp: identr[:, :], [Ab, identr])
                mm4(B6, 0, 128, lambda dp: Bt[:, dp, :], lambda dp: identr[:, :], [Bt, identr])
                mm4(B7, 0, 128, lambda dp: Kt[:, dp, :], lambda dp: identr[:, :], [Kt, identr])
                mm4(B5, 0, 128, lambda dp: AkvT[:, dp, :], lambda dp: VTr[:, dp, :], [AkvT, VTr])
                kb.op("act", lambda e: e.copy(x0[:, :, 0:128], pv(B4, 0, 128)), reads=[B4], writes=[x0])
                kb.op("act", lambda e: e.copy(Btok[:], pv(B6, 0, 128)), reads=[B6], writes=[Btok])
                kb.op("act", lambda e: e.copy(Ktok[:], pv(B7, 0, 128)), reads=[B7], writes=[Ktok])
                kb.op("act", lambda e: e.copy(x0[:, :, 128:256], pv(B5, 0, 128)), reads=[B5], writes=[x0])
                qc, pc, xc = Q[0], P[0], X[0]
                for j in range(6):
                    qn, pn, xn = Q[(j + 1) % 2], P[(j + 1) % 2], X[(j + 1) % 2]
                    for hf, bank in ((0, B4), (1, B5)):
                        for dq in range(2):
                            dp = hf * 2 + dq
                            kb.op("pe", lambda e, dp=dp, dq=dq, bank=bank: e.matmul(bank[:, dq * 256:(dq + 1) * 256], qc[:, dp, :], xc[:, dp, :],
                                                                                    start=True, stop=True), reads=[qc, xc], writes=[bank])
                        kb.op("dve", lambda e, hf=hf, bank=bank, xn=xn, xc=xc: e.tensor_tensor(
                            xn[:, 2 * hf:2 * hf + 2, :], f32(xc[:, 2 * hf:2 * hf + 2, :]), bank[:, :].rearrange("p (dq x) -> p dq x", dq=2), ALU.add),
                            reads=[xc, bank], writes=[xn])
                    if j < 5:
                        mm4(B6, 0, 128, lambda dp: qc[:, dp, :], lambda dp: pc[:, dp, :], [qc, pc])
                        mm4(B7, 0, 128, lambda dp: pc[:, dp, :], lambda dp: qc[:, dp, :], [qc, pc])
                        kb.op("act", lambda e, pn=pn: e.copy(pn[:], pv(B6, 0, 128)), reads=[B6], writes=[pn])
                        kb.op("act", lambda e, qn=qn: e.copy(qn[:], pv(B7, 0, 128)), reads=[B7], writes=[qn])
                    qc, pc, xc = qn, pn, xn
                mm4(B3, 0, 64, lambda dp: xc[:, dp, 0:128], lambda dp: ArbT[:, dp, :], [xc, ArbT])
                kb.op("dve", lambda e: e.tensor_tensor(RAT[:], f32(Rb[:]), pv(B3, 0, 64), ALU.add), reads=[Rb, B3], writes=[RAT])
                mm4(B2, 0, 128, lambda dp: xc[:, dp, 0:128], lambda dp: Btok[:, dp, :], [xc, Btok])
                kb.op("pool", lambda e: e.tensor_tensor(DG[:], ident_f[:, :].unsqueeze(1).broadcast_to([128, 4, 128]),
                                                        Wtot[:, :].unsqueeze(2).broadcast_to([128, 4, 128]), ALU.mult), reads=[ident_f, Wtot], writes=[DG])
                kb.op("dve", lambda e: e.tensor_tensor(McT[:], DG[:], pv(B2, 0, 128), ALU.add), reads=[DG, B2], writes=[McT])
                for dp in range(4):
                    kb.op("pe", lambda e, dp=dp: e.matmul(B0[:, dp * 128:(dp + 1) * 128], Btok[:, dp, :], xc[:, dp, 128:256], start=True, stop=False),
                          reads=[Btok, xc], writes=[B0])
                    kb.op("pe", lambda e, dp=dp: e.matmul(B0[:, dp * 128:(dp + 1) * 128], Ktok[:, dp, :], VTr[:, dp, :], start=False, stop=True),
                          reads=[Ktok, VTr], writes=[B0])
                kb.op("act", lambda e: e.copy(NcS[:], pv(B0, 0, 128)), reads=[B0], writes=[NcS])
                for dp in range(4):
                    c0 = dp * 64
                    kb.op("pe", lambda e, dp=dp, c0=c0: e.matmul(B1[:, c0:c0 + 64], ST[:, dp, :], RAT[:, dp, :], start=True, stop=False),
                          reads=[ST, RAT], writes=[B1])
                    kb.op("pe", lambda e, dp=dp, c0=c0: e.matmul(B1[:, c0:c0 + 64], xc[:, dp, 128:256], ArbT[:, dp, :], start=False, stop=False),
                          reads=[xc, ArbT], writes=[B1])
                    kb.op("pe", lambda e, dp=dp, c0=c0: e.matmul(B1[:, c0:c0 + 64], VTr[:, dp, :], ArkT[:, dp, :], start=False, stop=True),
                          reads=[VTr, ArkT], writes=[B1])
                for d in range(2):
                    c0 = cidx[d] * CH
                    kb.op("act", lambda e, d=d: e.copy(ysb[d][:, :, :], B1[:, d * 128:(d + 1) * 128].rearrange("p (pr t) -> p pr t", pr=2)),
                          reads=[B1], writes=[ysb[d]])
                    dst = RS["YF" if d == 0 else "YB"]
                    kb.dma("sp", dst.t.rearrange("(pr q) t -> q pr t", q=128)[:, :, c0:c0 + CH], ysb[d][:, :, :], reads=[ysb[d]], writes=[Buf()])
                mm4(B6, 0, 128, lambda dp: McT[:, dp, :], lambda dp: ST[:, dp, :], [McT, ST])
                kb.op("dve", lambda e: e.tensor_tensor(ST[:], NcS[:], pv(B6, 0, 128), ALU.add), reads=[NcS, B6], writes=[ST])

    def phase_rwkv_out(l, with_ctx):
        with kb.scope():
            rk_ = colvec("rrk", W["rw_r_k"][l, :], W["rw_r_k"], [128, 2], "(j p) -> p j", p=128)
            lg_ = colvec("rlg", W["rw_ln_g"][l, :], W["rw_ln_g"], [128, 2], "(j p) -> p j", p=128)
            lb_ = colvec("rlb", W["rw_ln_b"][l, :], W["rw_ln_b"], [128, 2], "(j p) -> p j", p=128)
            nm = ("YF", "YB", "RT", "KD0", "KD1", "VT")
            tl = [{n: kb.sb(f"o{n}{i}", [128, 512]) for n in nm} for i in range(2)]
            sg = [kb.sb(f"osg{i}", [128, 512], BF16) for i in range(2)]
            ob = [kb.sb(f"oob{i}", [128, 512], BF16) for i in range(2)]
            wk = [[kb.sb(f"owk{k}{i}", [128, 512]) for k in range(3)] for i in range(2)]
            it = 0
            for pr in range(2):
                rows = slice(pr * 128, (pr + 1) * 128)
                for (t0, nt) in TCH:
                    if not with_ctx and t0 + nt <= C:
                        continue
                    t_, s_, o_, (a_, b_, c_) = tl[it % 2], sg[it % 2], ob[it % 2], wk[it % 2]
                    for k, n in enumerate(nm):
                        kb.dma("sp" if k % 2 == 0 else "pool", t_[n][:, 0:nt], RS[n][rows, t0:t0 + nt], reads=[RS[n]], writes=[t_[n]])
                    kb.dma("sp", s_[:, 0:nt], RS["SGT"][rows, t0:t0 + nt], reads=[RS["SGT"]], writes=[s_])
                    y = t_["YF"]
                    kb.op("dve", lambda e: e.tensor_tensor(y[:, 0:nt], y[:, 0:nt], t_["YB"][:, 0:nt], ALU.add), reads=[y, t_["YB"]], writes=[y])
                    p1, p2, p3 = PS[(3 * it) % 8], PS[(3 * it + 1) % 8], PS[(3 * it + 2) % 8]
                    kb.op("pe", lambda e: e.matmul(p1[:, 0:nt], blk64[:], y[:, 0:nt], start=True, stop=True), reads=[blk64, y], writes=[p1])
                    kb.op("dve", lambda e: e.scalar_tensor_tensor(a_[:, 0:nt], p1[:, 0:nt], -1.0 / 64, y[:, 0:nt], ALU.mult, ALU.add),
                          reads=[p1, y], writes=[a_])
                    kb.op("act", lambda e: e.activation(b_[:, 0:nt], a_[:, 0:nt], AF.Square), reads=[a_], writes=[b_])
                    kb.op("pe", lambda e: e.matmul(p2[:, 0:nt], blk64[:], b_[:, 0:nt], start=True, stop=True), reads=[blk64, b_], writes=[p2])
                    kb.op("dve", lambda e: e.tensor_scalar(b_[:, 0:nt], p2[:, 0:nt], 1.0 / 64, 64e-5, ALU.mult, ALU.add), reads=[p2], writes=[b_])
                    kb.op("act", lambda e: e.sqrt(b_[:, 0:nt], b_[:, 0:nt]), reads=[b_], writes=[b_])
                    kb.op("dve", lambda e: e.reciprocal(b_[:, 0:nt], b_[:, 0:nt]), reads=[b_], writes=[b_])
                    kb.op("dve", lambda e: e.tensor_tensor(a_[:, 0:nt], a_[:, 0:nt], b_[:, 0:nt], ALU.mult), reads=[a_, b_], writes=[a_])
                    kb.op("dve", lambda e: e.tensor_scalar(a_[:, 0:nt], a_[:, 0:nt], lg_[:, pr:pr + 1], lb_[:, pr:pr + 1], ALU.mult, ALU.add),
                          reads=[a_, lg_, lb_], writes=[a_])
                    kb.op("pool", lambda e: e.tensor_tensor(c_[:, 0:nt], t_["KD0"][:, 0:nt], t_["KD1"][:, 0:nt], ALU.add),
                          reads=[t_["KD0"], t_["KD1"]], writes=[c_])
                    kb.op("dve", lambda e: e.scalar_tensor_tensor(c_[:, 0:nt], t_["RT"][:, 0:nt], rk_[:, pr:pr + 1], c_[:, 0:nt], ALU.mult, ALU.mult),
                          reads=[t_["RT"], rk_, c_], writes=[c_])
                    kb.op("pe", lambda e: e.matmul(p3[:, 0:nt], blk64[:], c_[:, 0:nt], start=True, stop=True), reads=[blk64, c_], writes=[p3])
                    kb.op("dve", lambda e: e.tensor_tensor(c_[:, 0:nt], p3[:, 0:nt], t_["VT"][:, 0:nt], ALU.mult), reads=[p3, t_["VT"]], writes=[c_])
                    kb.op("dve", lambda e: e.tensor_tensor(a_[:, 0:nt], a_[:, 0:nt], c_[:, 0:nt], ALU.add), reads=[a_, c_], writes=[a_])
                    kb.op("pool", lambda e: e.tensor_tensor(o_[:, 0:nt], a_[:, 0:nt], s_[:, 0:nt], ALU.mult), reads=[a_, s_], writes=[o_])
                    kb.dma("sp", mixT[256 + pr * 128:256 + (pr + 1) * 128, t0:t0 + nt], o_[:, 0:nt], reads=[o_], writes=[Buf()])
                    it += 1


    SEGS = {"L": dict(Ls=L, A=32, cbw=32, off=C, ut="UTL"), "C": dict(Ls=C, A=2, cbw=64, off=0, ut="UTC")}

    def phase_hyena_prep(l, with_ctx):
        with kb.scope():
            stage = kb.sb("hstage", [128, 8, 128])
            wts = [kb.sb(f"hwt{i}", [128, 8, 128], BF16) for i in range(2)]
            cw = kb.sb("hcw", [128, 6, 3])
            for k in range(3):
                kb.dma("sp", cw[:, :, k], W["hy_conv"][l, k, :].rearrange("(j p) -> p j", p=128), reads=[W["hy_conv"]], writes=[cw], slow=True)
            ncw = kb.sb("hncw", [128, 6, 3])
            kb.op("dve", lambda e: e.tensor_scalar(ncw[:], cw[:], -1.0, None, ALU.mult), reads=[cw], writes=[ncw])
            Zraw = kb.sb("hZraw", [128, T + 2])
            Zout = kb.sb("hZout", [128, T])
            kb.op("pool", lambda e: e.memset(Zraw[:, 0:1], 0.0), writes=[Zraw])
            kb.op("pool", lambda e: e.memset(Zraw[:, T + 1:T + 2], 0.0), writes=[Zraw])
            ub = kb.sb("hub", [128, 32 * 128])
            tG = [kb.sb(f"htG{i}", [128, 512], BF16) for i in range(2)]
            for oi, jt in enumerate(range(8)):
                wt = wts[oi % 2]
                c0 = HY0 + jt * 128 if jt < 6 else HYG0 + (jt - 6) * 128
                load_w(l, wt, c0, 128, stage)
                if jt >= 6:
                    for ci, (t0, nt) in enumerate(TCH):
                        p = PS[ci % 4]
                        proj_fm(p, wt, 0, 128, t0, nt)
                        g = tG[ci % 2]
                        kb.op("act", lambda e, p=p, g=g, nt=nt: e.activation(g[:, 0:nt], p[:, 0:nt], AF.Silu), reads=[p], writes=[g])
                        kb.dma("sp", HS["SG"][(jt - 6) * 128:(jt - 5) * 128, t0:t0 + nt], g[:, 0:nt], reads=[g], writes=[Buf()])
                    continue
                conv_tile(l, wt, cw, ncw, jt, Zraw, Zout)
                arr, half = jt // 2, jt % 2
                for sn in (("L", "C") if with_ctx else ("L",)):
                    sg = SEGS[sn]
                    A, cbw, off = sg["A"], sg["cbw"], sg["off"]
                    G = 128 // A
                    ncg = 128 // G
                    ubv = ub[:, 0:A * 128].rearrange("p (g a c) -> p g a c", g=ncg, a=A)
                    for a in range(A):
                        p = PS[4 + (a // 4) % 4]
                        kb.op("pe", lambda e, p=p, a=a, A=A, off=off: e.transpose(
                            p[:, (a % 4) * 128:(a % 4 + 1) * 128], Zout[:, off + a:off + a + 127 * A + 1:A], ident_f[:]),
                            reads=[Zout, ident_f], writes=[p])
                        if a % 4 == 3 or a == A - 1:
                            a0 = (a // 4) * 4
                            na = a - a0 + 1
                            kb.op("act", lambda e, p=p, a0=a0, na=na, G=G: e.copy(
                                ubv[:, :, a0:a0 + na, :], p[:, 0:na * 128].rearrange("p (a g c) -> p g a c", a=na, c=G)), reads=[p], writes=[ub])
                    nb = 128 // cbw
                    bsz = A * cbw
                    for b in range(nb):
                        dst = HS[sg["ut"]][arr, half * nb + b, :, :]
                        kb.dma("sp" if b % 2 == 0 else "pool", dst, ub[:, b * bsz:(b + 1) * bsz], reads=[ub], writes=[Buf()])

    def cmul(dre, dim_, sre, sim, tre, tim, conj, srcb, tabb, dstb, tmp):
        t1, t2 = tmp
        sh = tuple(slice(None) for _ in range(1))
        kb.op("dve", lambda e: e.tensor_tensor(t1, sre, tre, ALU.mult), reads=srcb + tabb, writes=[dstb[2]])
        kb.op("dve", lambda e: e.tensor_tensor(t2, sim, tim, ALU.mult), reads=srcb + tabb, writes=[dstb[3]])
        kb.op("pool", lambda e: e.tensor_tensor(dre, t1, t2, ALU.add if conj else ALU.subtract), reads=[dstb[2], dstb[3]], writes=[dstb[0]])
        kb.op("dve", lambda e: e.tensor_tensor(t1, sim, tre, ALU.mult), reads=srcb + tabb + [dstb[0]], writes=[dstb[2]])
        kb.op("dve", lambda e: e.tensor_tensor(t2, sre, tim, ALU.mult), reads=srcb + tabb + [dstb[0]], writes=[dstb[3]])
        kb.op("pool", lambda e: e.tensor_tensor(dim_, t1, t2, ALU.subtract if conj else ALU.add), reads=[dstb[2], dstb[3]], writes=[dstb[1]])

    def phase_hyena_main(l, with_ctx):
        PI = math.pi
        with kb.scope():
            fw1 = kb.sb("hfw1", [33, 64])
            fw2 = kb.sb("hfw2", [64, 64])
            fw3 = kb.sb("hfw3", [64, 1024])
            kb.dma("sp", fw1[:], W["hy_fw1"][l, :, :], reads=[W["hy_fw1"]], writes=[fw1])
            kb.dma("sp", fw2[:], W["hy_fw2"][l, :, :], reads=[W["hy_fw2"]], writes=[fw2])
            kb.dma("sp", fw3[:], W["hy_fw3"][l, :, :], reads=[W["hy_fw3"]], writes=[fw3])
            fb1 = colvec("hfb1", W["hy_fb1"][l, :], W["hy_fb1"], [64, 1], "(d o) -> d o", o=1)
            fb2 = colvec("hfb2", W["hy_fb2"][l, :], W["hy_fb2"], [64, 1], "(d o) -> d o", o=1)
            frq = colvec("hfrq", W["hy_freq"][l, :], W["hy_freq"], [64, 1], "(d o) -> d o", o=1)
            brow = kb.sb("hbrow", [1, 512])
            kb.dma("sp", brow[:], W["hy_bias"][l, :, :].rearrange("o c -> (o c)").rearrange("(x n) -> x n", x=1), reads=[W["hy_bias"]], writes=[brow])
            for sn in (("L", "C") if with_ctx else ("L",)):
                sg = SEGS[sn]
                Ls, A, cbw, off = sg["Ls"], sg["A"], sg["cbw"], sg["off"]
                G = 128 // A
                N = 2 * Ls
                ngr = cbw // G
                nblk = 256 // cbw
                pre = f"hy{sn}_"
                with kb.scope():
                    def ld(nm, shape):
                        t = kb.sb("k" + nm, shape)
                        src = CT[pre + nm]
                        kb.dma("sp", t[:], src.t, reads=[src], writes=[t])
                        return t
                    def ldr(nm, shape):
                        tr = kb.sb("r" + nm, shape, F32R)
                        with kb.scope():
                            t32 = ld(nm, shape)
                            kb.op("dve", lambda e: e.tensor_copy(tr[:], t32[:]), reads=[t32], writes=[tr])
                        return tr
                    F256 = ldr("F256", [128, 2, 512]); TWC = ld("TWC", [128, 256]); TWS = ld("TWS", [128, 256])
                    Dre = ldr("Dre", [128, 128]); Dim = ldr("Dim", [128, 128]); nDim = ldr("nDim", [128, 128])
                    E1 = ldr("E1", [128, 256]); E2 = ldr("E2", [128, 256])
                    TW2C = ld("TW2C", [128, 2, 128]); TW2S = ld("TW2S", [128, 2, 128])
                    IC = ldr("IC", [128, 2, 128]); IS = ldr("IS", [128, 2, 128])
                    h2T = kb.sb("h2T", [64, N])
                    with kb.scope():
                        zT = kb.sb("zT", [33, N])
                        kb.dma("sp", zT[:], CT[pre + "zT"].t, reads=[CT[pre + "zT"]], writes=[zT])
                        h1T = kb.sb("h1T", [64, N])
                        arg = [kb.sb(f"harg{i}", [64, 512]) for i in range(2)]
                        wr = [kb.sb(f"hwr{i}", [64, 512]) for i in range(2)]
                        for (src, K_, wgt, bcol, dst) in ((zT, 33, fw1, fb1, h1T), (h1T, 64, fw2, fb2, h2T)):
                            for ci, n0 in enumerate(range(0, N, 512)):
                                p = PS[ci % 4]
                                ag = arg[ci % 2]
                                kb.op("pe", lambda e: e.matmul(p[0:64, :], wgt[0:K_, :], src[0:K_, n0:n0 + 512], start=True, stop=True),
                                      reads=[wgt, src], writes=[p])
                                kb.op("dve", lambda e: e.tensor_scalar(ag[:, :], p[0:64, :], bcol[:, 0:1], frq[:, 0:1], ALU.add, ALU.mult),
                                      reads=[p, bcol, frq], writes=[ag])
                                for _w in range(2):
                                    kb.op("dve", lambda e: e.tensor_scalar(wr[0][:, :], ag[:, :], PI, -2 * PI, ALU.is_gt, ALU.mult), reads=[ag], writes=[wr[0]])
                                    kb.op("dve", lambda e: e.tensor_scalar(wr[1][:, :], ag[:, :], -PI, 2 * PI, ALU.is_lt, ALU.mult), reads=[ag], writes=[wr[1]])
                                    kb.op("dve", lambda e: e.tensor_tensor(ag[:, :], ag[:, :], wr[0][:, :], ALU.add), reads=[ag, wr[0]], writes=[ag])
                                    kb.op("dve", lambda e: e.tensor_tensor(ag[:, :], ag[:, :], wr[1][:, :], ALU.add), reads=[ag, wr[1]], writes=[ag])
                                kb.op("act", lambda e: e.activation(dst[:, n0:n0 + 512], ag[:, :], AF.Sin), reads=[ag], writes=[dst])
                    KT1 = kb.sb("KT", [128, 2, ngr, A, G])
                    KTr1 = kb.sb("KTr", [128, 2, ngr, A, G], F32R)
                    KT = [KT1, KT1]
                    KTr = [KTr1, KTr1]
                    uvr = kb.sb("huvr", [128, ngr, A * G], F32R)
                    KS = [kb.sb(f"KS{o}", [128, ngr, 512]) for o in range(2)]
                    DECt = kb.sb("DECt", [128, 2, ngr, A, G])
                    part = kb.sb("hpart", [128, cbw])
                    rn = kb.sb("hrn", [128, cbw])
                    ex = kb.sb("hex", [1, cbw])
                    uv = kb.sb("huv", [128, ngr, A * G]); x1 = kb.sb("hx1", [128, ngr, A * G]); x2 = kb.sb("hx2", [128, ngr, A * G])
                    u2 = kb.sb("hu2", [128, ngr, A * G], F32R); res = kb.sb("hres", [128, A, cbw])
                    dts = (F32R, F32R, F32, F32)
                    NL = 4 if ngr >= 4 else 2
                    BpS = [[kb.sb(f"hBp{b}{i}", [128, 256], dts[i]) for i in range(4)] for b in range(NL)]
                    BpbS = [[Buf() for _ in range(4)] for b in range(NL)]
                    YpS = [[kb.sb(f"hYp{b}{i}", [128, 256], dts[i]) for i in range(4)] for b in range(NL)]
                    YpbS = [[Buf() for _ in range(4)] for b in range(NL)]
                    GpS = [[kb.sb(f"hGp{b}{i}", [128, 2, 128], dts[i]) for i in range(4)] for b in range(NL)]
                    GpbS = [[Buf() for _ in range(4)] for b in range(NL)]
                    fctr = [0]
                    sgm = kb.sb("hsgm", [cbw, Ls], BF16)

                    def fwd_fft(lhs_chunks, lhs_bufs, psB, psX):
                        n = len(lhs_chunks)
                        fctr[0] += 1
                        Bp, Bpb = BpS[fctr[0] % NL], BpbS[fctr[0] % NL]
                        for i, (ap, hf) in enumerate(lhs_chunks):
                            kb.op("pe", lambda e, ap=ap, hf=hf, i=i: e.matmul(psB[:, :], ap, F256[:, hf, :], start=(i == 0), stop=(i == n - 1)),
                                  reads=lhs_bufs + [F256], writes=[psB])
                        yield
                        cmul(Bp[0][:, :], Bp[1][:, :], psB[:, 0:256], psB[:, 256:512], TWC[:, :], TWS[:, :], True,
                             [psB], [TWC, TWS], Bpb, (Bp[2][:, :], Bp[3][:, :]))
                        yield
                        kb.op("pe", lambda e: e.matmul(psX[:, 0:256], Dre[:, :], Bp[0][:, :], start=True, stop=False), reads=[Dre, Bpb[0]], writes=[psX])
                        kb.op("pe", lambda e: e.matmul(psX[:, 0:256], nDim[:, :], Bp[1][:, :], start=False, stop=True), reads=[nDim, Bpb[1]], writes=[psX])
                        kb.op("pe", lambda e: e.matmul(psX[:, 256:512], Dim[:, :], Bp[0][:, :], start=True, stop=False), reads=[Dim, Bpb[0]], writes=[psX])
                        kb.op("pe", lambda e: e.matmul(psX[:, 256:512], Dre[:, :], Bp[1][:, :], start=False, stop=True), reads=[Dre, Bpb[1]], writes=[psX])

                    def conv_group(src, src_b, g, o, mulv, mul_b, dst_ap, dst_b, it):
                        ln = it % NL
                        if NL == 4:
                            psB, psX, psG, psy = PS[2 * ln], PS[2 * ln + 1], PS[2 * ln], PS[2 * ln + 1]
                        else:
                            psB, psX, psG, psy = PS[it % 2], PS[2 + it % 2], PS[4 + it % 2], PS[6 + it % 2]
                        Yp, Ypb, Gp, Gpb = YpS[ln], YpbS[ln], GpS[ln], GpbS[ln]
                        yield from fwd_fft([(src[:, g, :], 0)], [src_b], psB, psX)
                        yield
                        cmul(Yp[0][:, :], Yp[1][:, :], psX[:, 0:256], psX[:, 256:512], KS[o][:, g, 0:256], KS[o][:, g, 256:512], False,
                             [psX], [KS[o]], Ypb, (Yp[2][:, :], Yp[3][:, :]))
                        yield
                        for chn in range(2):
                            fs = slice(chn * 128, (chn + 1) * 128)
                            kb.op("pe", lambda e, fs=fs, chn=chn: e.matmul(psG[:, chn * 256:(chn + 1) * 256], Yp[0][:, fs], E1[:, :], start=True, stop=False),
                                  reads=[Ypb[0], E1], writes=[psG])
                            kb.op("pe", lambda e, fs=fs, chn=chn: e.matmul(psG[:, chn * 256:(chn + 1) * 256], Yp[1][:, fs], E2[:, :], start=False, stop=True),
                                  reads=[Ypb[1], E2], writes=[psG])
                        yield
                        pg = psG[:, :].rearrange("p (ch ri c) -> p ch ri c", ch=2, ri=2)
                        cmul(Gp[0][:, :, :], Gp[1][:, :, :], pg[:, :, 0, :], pg[:, :, 1, :], TW2C[:, :, :], TW2S[:, :, :], False,
                             [psG], [TW2C, TW2S], Gpb, (Gp[2][:, :, :], Gp[3][:, :, :]))
                        yield
                        k = 0
                        for chn in range(2):
                            for (tab, gsrc, gb) in ((IC, Gp[0], Gpb[0]), (IS, Gp[1], Gpb[1])):
                                kb.op("pe", lambda e, chn=chn, tab=tab, gsrc=gsrc, k=k: e.matmul(
                                    psy[:, 0:128], tab[:, chn, :], gsrc[:, chn, :], start=(k == 0), stop=(k == 3)), reads=[tab, gb], writes=[psy])
                                k += 1
                        yield
                        kb.op("dve", lambda e: e.tensor_tensor(dst_ap, psy[:, 0:128].rearrange("p (c a) -> p a c", a=A),
                                                               mulv[:, g, :].rearrange("p (a c) -> p a c", c=G), ALU.mult),
                              reads=[psy, mul_b], writes=[dst_b])

                    def lockstep(gens):
                        gens = list(gens)
                        while gens:
                            nxt = []
                            for g_ in gens:
                                try:
                                    next(g_)
                                    nxt.append(g_)
                                except StopIteration:
                                    pass
                            gens = nxt

                    def spec_group(o, g, it):
                        if NL == 4:
                            psB, psX = PS[2 * (it % 4)], PS[2 * (it % 4) + 1]
                        else:
                            psB, psX = PS[it % 2], PS[2 + it % 2]
                        yield from fwd_fft([(KTr[o][:, 0, g, :, :].rearrange("p a c -> p (a c)"), 0),
                                            (KTr[o][:, 1, g, :, :].rearrange("p a c -> p (a c)"), 1)], [KTr[o]], psB, psX)
                        yield
                        kb.op("act", lambda e: e.copy(KS[o][:, g, :], psX[:, :]), reads=[psX], writes=[KS[o]])

                    git = 0
                    for cb in range(nblk):
                        kb.dma("sp", DECt[:].rearrange("p h g a c -> p (h g a c)"), CT[pre + "DEC"][cb, :, :], reads=[CT[pre + "DEC"]], writes=[DECt])
                        for ai, tile_ in enumerate((uv, x1, x2)):
                            kb.dma("pool", tile_[:].rearrange("p g x -> p (g x)"), HS[sg["ut"]][ai, cb, :, :], reads=[HS[sg["ut"]]], writes=[tile_])
                        for o in range(2):
                            for hf in range(2):
                                col0 = o * 512 + hf * 256 + cb * cbw
                                npb = 512 // cbw
                                for a in range(A):
                                    p = PS[(a // npb) % 4]
                                    kb.op("pe", lambda e, p=p, a=a, hf=hf, col0=col0, npb=npb: e.matmul(
                                        p[:, (a % npb) * cbw:(a % npb + 1) * cbw], h2T[0:64, hf * 128 * A + a:hf * 128 * A + a + 127 * A + 1:A],
                                        fw3[0:64, col0:col0 + cbw], start=True, stop=True), reads=[h2T, fw3], writes=[p])
                                    if a % npb == npb - 1 or a == A - 1:
                                        a0 = (a // npb) * npb
                                        na = a - a0 + 1
                                        kb.op("dve", lambda e, p=p, a0=a0, na=na, hf=hf, o=o: e.tensor_tensor(
                                            KT[o][:, hf, :, a0:a0 + na, :], p[:, 0:na * cbw].rearrange("p (a g c) -> p g a c", a=na, c=G),
                                            DECt[:, hf, :, a0:a0 + na, :], ALU.mult), reads=[p, DECt], writes=[KT[o]])
                            kb.op("dve", lambda e, o=o: e.tensor_reduce(part[:, :].rearrange("p (g c) -> p g c", c=G),
                                                                        KT[o][:, :, :, :, :].rearrange("p h g a c -> p g c h a"), AX.XY, ALU.add,
                                                                        apply_absolute_value=True), reads=[KT[o]], writes=[part])
                            pe_ = PS[4]
                            kb.op("pe", lambda e, o=o: e.matmul(pe_[0:1, 0:cbw], h2T[0:64, 0:1], fw3[0:64, o * 512 + 256 + cb * cbw:o * 512 + 256 + (cb + 1) * cbw],
                                                                start=True, stop=True), reads=[h2T, fw3], writes=[pe_])
                            kb.op("act", lambda e: e.activation(ex[0:1, :], pe_[0:1, 0:cbw], AF.Abs), reads=[pe_], writes=[ex])
                            kb.op("dve", lambda e: e.tensor_tensor(part[0:1, :], part[0:1, :], ex[0:1, :], ALU.add), reads=[part, ex], writes=[part])
                            pt_ = PS[5]
                            kb.op("pe", lambda e: e.matmul(pt_[:, 0:cbw], ones_f[:, :], part[:, :], start=True, stop=True), reads=[ones_f, part], writes=[pt_])
                            kb.op("dve", lambda e: e.reciprocal(rn[:, :], pt_[:, 0:cbw]), reads=[pt_], writes=[rn])
                            for hf in range(2):
                                kb.op("dve", lambda e, o=o, hf=hf: e.tensor_tensor(
                                    KTr[o][:, hf, :, :, :], KT[o][:, hf, :, :, :],
                                    rn[:, :].rearrange("p (g c) -> p g c", c=G).unsqueeze(2).broadcast_to([128, ngr, A, G]), ALU.mult),
                                    reads=[KT[o], rn], writes=[KTr[o]])
                            kb.op("dve", lambda e, o=o: e.tensor_tensor(
                                KTr[o][0:1, 0, :, 0, :], KTr[o][0:1, 0, :, 0, :].bitcast(F32),
                                brow[0:1, o * 256 + cb * cbw:o * 256 + (cb + 1) * cbw].rearrange("p (g c) -> p g c", c=G), ALU.add),
                                reads=[KTr[o], brow], writes=[KTr[o]])
                            for g in range(0, ngr, NL):
                                gg = [g_ for g_ in range(g, min(ngr, g + NL))]
                                lockstep([spec_group(o, g_, git + k_) for k_, g_ in enumerate(gg)])
                                git += len(gg)
                        kb.op("act", lambda e: e.copy(uvr[:], uv[:]), reads=[uv], writes=[uvr])
                        for g in range(0, ngr, NL):
                            gg = [g_ for g_ in range(g, min(ngr, g + NL))]
                            lockstep([conv_group(uvr, uvr, g_, 0, x1, x1, u2[:, g_, :].rearrange("p (a c) -> p a c", c=G), u2, git + k_)
                                      for k_, g_ in enumerate(gg)])
                            git += len(gg)
                        for g in range(0, ngr, NL):
                            gg = [g_ for g_ in range(g, min(ngr, g + NL))]
                            lockstep([conv_group(u2, u2, g_, 1, x2, x2, res[:, :, g_ * G:(g_ + 1) * G], res, git + k_)
                                      for k_, g_ in enumerate(gg)])
                            git += len(gg)
                        kb.dma("sp", sgm[:], HS["SG"][cb * cbw:(cb + 1) * cbw, off:off + Ls], reads=[HS["SG"]], writes=[sgm])
                        Fv = sgm[:, :].rearrange("c (p a) -> c p a", a=A)
                        for a in range(A):
                            p = PS[4 + (a // 4) % 4]
                            kb.op("pe", lambda e, p=p, a=a: e.transpose(p[0:cbw, (a % 4) * 128:(a % 4 + 1) * 128], res[:, a, :], ident_f[:]),
                                  reads=[res, ident_f], writes=[p])
                            if a % 4 == 3 or a == A - 1:
                                a0 = (a // 4) * 4
                                na = a - a0 + 1
                                kb.op("dve", lambda e, p=p, a0=a0, na=na: e.tensor_tensor(
                                    Fv[:, :, a0:a0 + na], p[0:cbw, 0:na * 128].rearrange("c (a p) -> c p a", p=128), Fv[:, :, a0:a0 + na], ALU.mult),
                                    reads=[p, sgm], writes=[sgm])
                        kb.dma("sp", mixT[cb * cbw:(cb + 1) * cbw, off:off + Ls], sgm[:, :], reads=[sgm], writes=[Buf()])

    dbgn = [n for n, _ in dbg]
    for l in range(depth):
        last = (l == DEPTH - 1)
        with kb.scope():
            hT = kb.sb("hT", [128, 8, T], BF16)
            G1 = kb.sb("G1", [128, 2, D])
            SH = kb.sb("SH", [128, 2, D])
            phase_mod(l)
            phase_norm(l)
            if "noattn" not in dbgn:
                if os.environ.get("ATTN_ONLY", "") != "dense":
                    phase_attn(l, False, not last)
                if os.environ.get("ATTN_ONLY", "") != "window":
                    phase_attn(l, True, not last)
            if "norw" not in dbgn:
                phase_rwkv_prep(l)
            if "nohy" not in dbgn:
                phase_hyena_prep(l, not last)
            if "hT" in dbgn:
                tmp = kb.sb("dbghT", [128, T])
                for j in range(8):
                    kb.op("dve", lambda e, j=j, tmp=tmp: e.tensor_copy(tmp[:], hT[:, j, :]), reads=[hT], writes=[tmp])
                    kb.dma("sp", dbg_t["hT"][:, j, :], tmp[:], reads=[tmp], writes=[dbg_t["hT"]])
        if "norw" not in dbgn:
            {0: phase_rwkv_chunked, 1: phase_rwkv_chunked2, 3: phase_rwkv_chunked3}[RW_V2](l)
            phase_rwkv_out(l, not last)
        if "nohy" not in dbgn:
            phase_hyena_main(l, not last)
        if "noout" not in dbgn:
            phase_out(l, last)
    for n, s_ in dbg:
        if n == "xres":
            with kb.scope():
                tx = kb.sb("dbgx", [128, D])
                for i in range(NT):
                    kb.dma("sp", tx[:], xres[i * 128:(i + 1) * 128, :], reads=[xres_b[i]], writes=[tx])
                    kb.dma("sp", dbg_t[n][i * 128:(i + 1) * 128, :], tx[:], reads=[tx], writes=[dbg_t[n]])
        if n == "mixT":
            with kb.scope():
                tmpb = kb.sb("dbgmb", [128, T], BF16)
                tmpf = kb.sb("dbgmf", [128, T])
                for j in range(8):
                    kb.dma("sp", tmpb[:], mixT[j * 128:(j + 1) * 128, :], reads=[mixT], writes=[tmpb])
                    kb.op("dve", lambda e, tmpb=tmpb, tmpf=tmpf: e.tensor_copy(tmpf[:], tmpb[:]), reads=[tmpb], writes=[tmpf])
                    kb.dma("sp", dbg_t[n][j * 128:(j + 1) * 128, :], tmpf[:], reads=[tmpf], writes=[dbg_t[n]])
    kb.finish()
    kb.es.close()
    return kb, cst


_PROG = {}


def kernel(**inputs):
    if "p" not in _PROG:
        _PROG["p"] = build()
    kb, cst = _PROG["p"]
    f = lambda a: np.ascontiguousarray(np.asarray(a, dtype=np.float32))
    shared = {}
    for n in inputs:
        if n in ("x", "c", "ctx", "c_ctx"):
            continue
        shared[n] = f(inputs[n])
    shared["c_ctx"] = f(inputs["c_ctx"])
    for n, a in cst.items():
        shared["k_" + n] = np.ascontiguousarray(a)
    x, c, ctx = f(inputs["x"]), f(inputs["c"]), f(inputs["ctx"])
    B = x.shape[0]
    in_maps = []
    for b in range(B):
        m = dict(shared)
        m["x"] = np.ascontiguousarray(x[b])
        m["c"] = np.ascontiguousarray(c[b])
        m["ctx"] = np.ascontiguousarray(ctx[b])
        in_maps.append(m)
    res = run_bass_kernel_spmd(kb.nc, in_maps, core_ids=list(range(B)))
    return np.stack([np.asarray(res.results[b]["out"], dtype=np.float32) for b in range(B)], axis=0)
```

```python
import contextlib
import math
import numpy as np
import ml_dtypes
import concourse.bass as bass
import concourse.mybir as mybir
from concourse.bass_utils import run_bass_kernel_spmd

F32 = mybir.dt.float32
BF16 = mybir.dt.bfloat16
F32R = mybir.dt.float32r
ALU = mybir.AluOpType
AF = mybir.ActivationFunctionType
AX = mybir.AxisListType

D = 1024
L = 4096
C = 256
T = L + C
NT = T // 128
DEPTH = 4
D_IN = 3712
HY0, HYG0, RW0, RWG0, WA0, WAG0, FA0, FAG0 = 0, 768, 1024, 1920, 2176, 2688, 2944, 3456
EPS = 1e-6
NSLOT = 24
import os
RW_STAGE = int(os.environ.get('RW_STAGE', '99'))
INLINE_WAIT = int(os.environ.get('INLINE_WAIT', '1'))
POOL_DMA_TO_SP = int(os.environ.get('POOL_DMA_TO_SP', '1'))
RW_V2 = int(os.environ.get('RW_V2', '3'))


class Buf:
    def __init__(self, name=""):
        self.name = name
        self.w = None
        self.r = {}

    def wdeps(self):
        return [self.w] if self.w is not None else []

    def rdeps(self):
        return list(self.r.values())

    def add_reader(self, tok):
        k = tok[:2]
        if k not in self.r or self.r[k][2] < tok[2]:
            self.r[k] = tok

    def set_writer(self, tok):
        self.w = tok
        self.r = {}


class Tile(Buf):
    def __init__(self, name, t):
        super().__init__(name)
        self.t = t

    def __getitem__(self, key):
        return self.t[key]


class KB:
    def __init__(self):
        self.nc = bass.Bass("TRN2", target_bir_lowering=False)
        nc = self.nc
        self.es = contextlib.ExitStack()
        self.eng = {"pe": nc.tensor, "act": nc.scalar, "dve": nc.vector, "pool": nc.gpsimd, "sp": nc.sync}
        self.sem = {}
        self.cnt = {}
        self.waited = {e: {} for e in self.eng}
        for e in self.eng:
            self.sem[e] = self.es.enter_context(nc.semaphore("s_" + e))
            self.cnt[e] = 0
        self.slots = {}
        self.slot_i = {}
        for q in ("sp", "act", "pool"):
            self.slots[q] = [[self.es.enter_context(nc.semaphore(f"d_{q}{i}")), 0] for i in range(NSLOT)]
            self.slot_i[q] = 0
        self.n_ins = 0

    def sb(self, name, shape, dt=F32):
        self.uid = getattr(self, "uid", 0) + 1
        name = f"{name}_{self.uid}"
        return Tile(name, self.es.enter_context(self.nc.sbuf_tensor(name, list(shape), dt)))

    def ps(self, name, shape, dt=F32):
        return Tile(name, self.es.enter_context(self.nc.psum_tensor(name, list(shape), dt)))

    def dram(self, name, shape, dt=F32, kind="Internal"):
        t = self.nc.dram_tensor(name, list(shape), dt, kind=kind)
        b = Tile(name, t.ap())
        return b

    def _tok_sem(self, tok):
        if tok[0] == "e":
            return ("e", tok[1]), self.sem[tok[1]], tok[2]
        return ("d", tok[1]), self.slots[tok[1][0]][tok[1][1]][0], tok[2]

    def _wait(self, e, toks, defer=False):
        need = {}
        for tok in toks:
            if tok is None:
                continue
            key, sem, val = self._tok_sem(tok)
            if tok[0] == "e" and tok[1] == e and e == "pe":
                continue
            if self.waited[e].get(key, 0) >= val:
                continue
            if key not in need or need[key][1] < val:
                need[key] = (sem, val)
        items = list(need.items())
        inline = None
        if defer and INLINE_WAIT and items:
            inline = items.pop()
        for key, (sem, val) in items:
            self.eng[e].wait_ge(sem, val)
            self.waited[e][key] = val
        return inline

    def op(self, e, fn, reads=(), writes=()):
        toks = []
        for b in reads:
            toks += b.wdeps()
        for b in writes:
            toks += b.wdeps() + b.rdeps()
        inline = self._wait(e, toks, defer=True)
        ins = fn(self.eng[e])
        if inline is not None:
            key, (sem, val) = inline
            ins._wait_ge(sem, val)
            self.waited[e][key] = val
        self.cnt[e] += 1
        ins.then_inc(self.sem[e], 1)
        tok = ("e", e, self.cnt[e])
        for b in reads:
            b.add_reader(tok)
        for b in writes:
            b.set_writer(tok)
        self.n_ins += 1
        return ins

    def dma(self, q, out, in_, reads=(), writes=(), slow=False):
        if q == "pool" and POOL_DMA_TO_SP:
            q = "sp"
        i = self.slot_i[q]
        self.slot_i[q] = (i + 1) % NSLOT
        slot = self.slots[q][i]
        toks = []
        if slot[1] > 0:
            toks.append(("d", (q, i), slot[1]))
        for b in reads:
            toks += b.wdeps()
        for b in writes:
            toks += b.wdeps() + b.rdeps()
        self._wait(q, toks)
        if slow:
            ins = self.eng[q].dma_start(out=out, in_=in_, allow_slow_non_contiguous=True)
        else:
            ins = self.eng[q].dma_start(out=out, in_=in_)
        ins.then_inc(slot[0], 16)
        slot[1] += 16
        tok = ("d", (q, i), slot[1])
        for b in reads:
            b.add_reader(tok)
        for b in writes:
            b.set_writer(tok)
        self.n_ins += 1
        return ins

    def barrier(self):
        toks = [("e", e, self.cnt[e]) for e in self.eng if self.cnt[e] > 0]
        for q in self.slots:
            for i, s in enumerate(self.slots[q]):
                if s[1] > 0:
                    toks.append(("d", (q, i), s[1]))
        for e in self.eng:
            self._wait(e, toks)

    def finish(self):
        self.barrier()

    @contextlib.contextmanager
    def scope(self):
        es = contextlib.ExitStack()
        old = self.es
        self.es = es
        try:
            yield
        finally:
            self.barrier()
            self.es = old
            es.close()


def host_consts():
    cst = {}
    cst["ident_bf"] = np.eye(128, dtype=np.float32).astype(ml_dtypes.bfloat16)
    cst["ident_f"] = np.eye(128, dtype=np.float32)
    blk = np.zeros((128, 128), np.float32)
    blk[:64, :64] = 1.0
    blk[64:, 64:] = 1.0
    cst["blk64"] = blk
    cst["ones_f"] = np.ones((128, 128), np.float32)
    t = np.arange(L)
    row = (t // 64).astype(np.float32)
    col = (t % 64).astype(np.float32)
    inv = (10000.0 ** (-np.arange(16, dtype=np.float32) / 16)).astype(np.float32)
    cosT = np.zeros((128, L), np.float32)
    sinT = np.zeros((128, L), np.float32)
    perm = np.zeros((128, 128), np.float32)
    for p in range(128):
        d = p % 64
        sec, half, f = d // 32, (d % 32) // 16, d % 16
        pos = row if sec == 0 else col
        ang = (pos * inv[f]).astype(np.float32)
        cosT[p] = np.cos(ang)
        sinT[p] = np.sin(ang)
        if half == 0:
            perm[p + 16, p] = -1.0
        else:
            perm[p - 16, p] = 1.0
    cst["rope_cos"] = cosT
    cst["rope_sin"] = sinT
    cst["rope_perm"] = perm
    i = np.arange(128)[:, None]
    j = np.arange(384)[None, :]
    cst["wmask"] = np.where((j >= i) & (j <= i + 256), 0.0, -1e30).astype(np.float32)
    ii = np.arange(64)
    ms = np.zeros((128, 2, 64), np.float32); mts = np.zeros((128, 2, 64), np.float32); mti = np.zeros((128, 2, 64), np.float32)
    for hp in range(2):
        rows = slice(hp * 64, hp * 64 + 64)
        ms[rows, 0, :] = (ii[None, :] < ii[:, None]); ms[rows, 1, :] = (ii[None, :] > ii[:, None])
        mts[rows, 0, :] = (ii[:, None] < ii[None, :]); mts[rows, 1, :] = (ii[:, None] > ii[None, :])
        mti[rows, 0, :] = (ii[:, None] <= ii[None, :]); mti[rows, 1, :] = (ii[:, None] >= ii[None, :])
    cst["rw_ms"] = ms; cst["rw_mts"] = mts; cst["rw_mti"] = mti
    cst["rw_msb"] = np.concatenate([ms, ms], 2); cst["rw_mtsb"] = np.concatenate([mts, mts], 2)
    cst["rw_id2"] = np.concatenate([np.eye(64, dtype=np.float32)] * 2, 0)
    cst.update(hy_consts(L, 32, 32, "L"))
    cst.update(hy_consts(C, 2, 64, "C"))
    return cst


def hy_consts(Ls, A, cbw, tag):
    G = 128 // A
    N = 2 * Ls
    out = {}
    p = np.arange(128)
    f1 = np.arange(256)
    F = np.zeros((128, 2, 512), np.float64)
    for h in range(2):
        pp = h * 128 + p
        ang = 2 * np.pi * ((pp[:, None] * f1[None, :]) % 256) / 256
        F[:, h, 0:256] = np.cos(ang)
        F[:, h, 256:512] = -np.sin(ang)
    out["F256"] = F
    a_of_row = np.arange(128) // G
    th = 2 * np.pi * ((a_of_row[:, None] * f1[None, :]) % N) / N
    out["TWC"] = np.cos(th)
    out["TWS"] = np.sin(th)
    Dre = np.zeros((128, 128)); Dim = np.zeros((128, 128))
    E1 = np.zeros((128, 256)); E2 = np.zeros((128, 256))
    for a in range(A):
        for c in range(G):
            for f2 in range(A):
                ph = 2 * np.pi * ((a * f2) % A) / A
                Dre[a * G + c, c * A + f2] = np.cos(ph)
                Dim[a * G + c, c * A + f2] = -np.sin(ph)
                E1[c * A + f2, c * A + a] = np.cos(ph)
                E1[c * A + f2, 128 + c * A + a] = np.sin(ph)
                E2[c * A + f2, c * A + a] = -np.sin(ph)
                E2[c * A + f2, 128 + c * A + a] = np.cos(ph)
    out["Dre"] = Dre; out["Dim"] = Dim; out["nDim"] = -Dim; out["E1"] = E1; out["E2"] = E2
    a_of_col = np.arange(128) % A
    TW2C = np.zeros((128, 2, 128)); TW2S = np.zeros((128, 2, 128))
    IC = np.zeros((128, 2, 128)); IS = np.zeros((128, 2, 128))
    for ch in range(2):
        ff = ch * 128 + np.arange(128)
        th2 = 2 * np.pi * ((ff[:, None] * a_of_col[None, :]) % N) / N
        TW2C[:, ch, :] = np.cos(th2) / N
        TW2S[:, ch, :] = np.sin(th2) / N
        ph = 2 * np.pi * ((ff[:, None] * p[None, :]) % 256) / 256
        IC[:, ch, :] = np.cos(ph)
        IS[:, ch, :] = -np.sin(ph)
    out["TW2C"] = TW2C; out["TW2S"] = TW2S; out["IC"] = IC; out["IS"] = IS
    tp = np.arange(N)
    pos = np.where(tp < Ls, tp, N - tp).astype(np.float64)
    tn = (pos / (Ls - 1)).astype(np.float32)
    w = ((2.0 * math.pi / Ls) * pos).astype(np.float32)
    fb = np.linspace(1e-4, 15.0, 16, dtype=np.float32)
    zT = np.zeros((33, N), np.float32)
    zT[0] = tn
    zT[1:17] = np.cos(fb[:, None] * w[None, :])
    zT[17:33] = np.sin(fb[:, None] * w[None, :])
    out["zT"] = zT
    deltas = np.abs(np.linspace(math.log(1e-2) / 1.5, math.log(1e-2) / 0.3, 256, dtype=np.float32))
    dec = np.exp(-tn[:, None] * deltas[None, :]).astype(np.float32)
    dec[Ls, :] = 0.0
    nblk = 256 // cbw
    ngr = cbw // G
    DEC = np.zeros((nblk, 128, 2, ngr, A, G), np.float32)
    for h in range(2):
        for a in range(A):
            tpp = A * (h * 128 + p) + a
            for b in range(nblk):
                DEC[b, :, h, :, a, :] = dec[tpp, b * cbw:(b + 1) * cbw].reshape(128, ngr, G)
    out["DEC"] = DEC.reshape(nblk, 128, 2 * ngr * A * G)
    return {f"hy{tag}_{k}": np.ascontiguousarray(v.astype(np.float32)) for k, v in out.items()}

CONST_SPECS = None


def build(depth=DEPTH, dbg=()):
    kb = KB()
    nc = kb.nc
    cst = host_consts()
    def inp(name, shape, dt=F32):
        return kb.dram(name, shape, dt, kind="ExternalInput")

    x_in = inp("x", [L, D])
    c_in = inp("c", [D])
    ctx_in = inp("ctx", [C, D])
    cctx_in = inp("c_ctx", [D])
    W = {}
    wspec = {
        "mod_w": [DEPTH, D, 3 * D], "mod_b": [DEPTH, 3 * D], "norm_g": [DEPTH, D], "w_in": [DEPTH, D, D_IN],
        "w_out": [DEPTH, D, D], "wa_sink": [DEPTH, 4], "fa_q_norm": [DEPTH, 64], "fa_k_norm": [DEPTH, 64],
        "final_g": [D],
        "rw_conv": [DEPTH, 3, 896], "rw_w0": [DEPTH, 2, 256], "rw_w_up": [DEPTH, 2, 64, 256], "rw_a0": [DEPTH, 2, 256],
        "rw_a_up": [DEPTH, 2, 64, 256], "rw_k_k": [DEPTH, 256], "rw_k_a": [DEPTH, 256], "rw_r_k": [DEPTH, 256],
        "rw_ln_g": [DEPTH, 256], "rw_ln_b": [DEPTH, 256],
        "hy_conv": [DEPTH, 3, 768], "hy_fw1": [DEPTH, 33, 64], "hy_fb1": [DEPTH, 64], "hy_freq": [DEPTH, 64],
        "hy_fw2": [DEPTH, 64, 64], "hy_fb2": [DEPTH, 64], "hy_fw3": [DEPTH, 64, 1024], "hy_bias": [DEPTH, 2, 256],
    }
    for n, s in wspec.items():
        W[n] = inp(n, s)
    CT = {}
    for n, a in cst.items():
        CT[n] = inp("k_" + n, list(a.shape), BF16 if a.dtype == ml_dtypes.bfloat16 else F32)
    out = kb.dram("out", [L, D], F32, kind="ExternalOutput")
    xres = kb.dram("xres", [T, D], F32)
    mixT = kb.dram("mixT", [D, T], BF16)
    RS = {}
    for n in ("RT", "VT", "AL", "W0", "W1", "B0", "B1", "KD0", "KD1", "YF", "YB"):
        RS[n] = kb.dram("rs_" + n, [256, T])
    RS["VTOK"] = kb.dram("rs_VTOK", [T, 256])
    RS["SGT"] = kb.dram("rs_SGT", [256, T], BF16)
    HS = {"SG": kb.dram("hs_SG", [256, T], BF16),
          "UTL": kb.dram("hs_UTL", [3, 8, 128, 32 * 32]), "UTC": kb.dram("hs_UTC", [3, 4, 128, 2 * 64])}
    dbg_t = {}
    for n, s in dbg:
        dbg_t[n] = kb.dram("dbg_" + n, s, F32, kind="ExternalOutput")

    ident_bf = kb.sb("ident_bf", [128, 128], BF16)
    ident_f = kb.sb("ident_f", [128, 128])
    blk64 = kb.sb("blk64", [128, 128])
    ones_f = kb.sb("ones_f", [128, 128])
    for tl, n in ((ident_bf, "ident_bf"), (ident_f, "ident_f"), (blk64, "blk64"), (ones_f, "ones_f")):
        kb.dma("sp", tl[:], CT[n][:, :], reads=[CT[n]], writes=[tl])
    hT = G1 = SH = None
    GT = kb.sb("GT", [128, 2, D])
    PS = [kb.ps(f"ps{i}", [128, 512]) for i in range(8)]

    xres_b = [Buf(f"xres{i}") for i in range(NT)]

    def x_src(l, i):
        if l == 0:
            if i < 2:
                return ctx_in[i * 128:(i + 1) * 128, :], ctx_in
            return x_in[(i - 2) * 128:(i - 1) * 128, :], x_in
        return xres[i * 128:(i + 1) * 128, :], xres_b[i]

    def phase_mod(l):
        with kb.scope():
            cc = kb.sb("cc", [128, 2, 8])
            sc = kb.sb("sc", [128, 2, 8])
            mw = [kb.sb(f"mw{i}", [128, 8, 512]) for i in range(2)]
            mb = kb.sb("mb", [128, 3 * D])
            ng = kb.sb("ng", [128, D])
            modr = kb.sb("modr", [128, 2, 3 * D])
            kb.dma("sp", cc[:, 0, :], c_in.t.rearrange("(j p) -> p j", p=128), reads=[c_in], writes=[cc], slow=True)
            kb.dma("sp", cc[:, 1, :], cctx_in.t.rearrange("(j p) -> p j", p=128), reads=[cctx_in], writes=[cc], slow=True)
            kb.dma("sp", mb[:], W["mod_b"][l, :].partition_broadcast(128), reads=[W["mod_b"]], writes=[mb])
            kb.dma("sp", ng[:], W["norm_g"][l, :].partition_broadcast(128), reads=[W["norm_g"]], writes=[ng])
            kb.op("act", lambda e: e.activation(sc[:], cc[:], AF.Silu), reads=[cc], writes=[sc])
            for n in range(6):
                m = mw[n % 2]
                kb.dma("sp" if n % 2 == 0 else "pool", m[:],
                       W["mod_w"][l, :, n * 512:(n + 1) * 512].rearrange("(j p) n -> p j n", p=128),
                       reads=[W["mod_w"]], writes=[m])
                for i in range(2):
                    p = PS[(2 * n + i) % 8]
                    for j in range(8):
                        kb.op("pe", lambda e, p=p, i=i, j=j, m=m: e.matmul(
                            p[:, :], sc[:, i, j:j + 1].broadcast_to([128, 128]), m[:, j, :],
                            start=(j == 0), stop=(j == 7)), reads=[sc, m], writes=[p])
                    kb.op("dve", lambda e, p=p, i=i, n=n: e.tensor_tensor(
                        modr[:, i, n * 512:(n + 1) * 512], p[:, :], mb[:, n * 512:(n + 1) * 512], ALU.add),
                        reads=[p, mb], writes=[modr])
            for i in range(2):
                kb.op("dve", lambda e, i=i: e.scalar_tensor_tensor(
                    G1[:, i, :], modr[:, i, D:2 * D], 1.0, ng[:], ALU.add, ALU.mult), reads=[modr, ng], writes=[G1])
                kb.op("act", lambda e, i=i: e.copy(SH[:, i, :], modr[:, i, 0:D]), reads=[modr], writes=[SH])
                kb.op("act", lambda e, i=i: e.copy(GT[:, i, :], modr[:, i, 2 * D:3 * D]), reads=[modr], writes=[GT])

    def phase_norm(l):
        NLN = 4
        with kb.scope():
            xt = [kb.sb(f"xt{i}", [128, D]) for i in range(NLN)]
            junks = [kb.sb(f"junk{i}", [128, D]) for i in range(NLN)]
            hf = [kb.sb(f"hf{i}", [128, D]) for i in range(NLN)]
            hb = [kb.sb(f"hb{i}", [128, D], BF16) for i in range(NLN)]
            st = [kb.sb(f"st{i}", [128, 4]) for i in range(NLN)]

            def tile_gen(i):
                ln = i % NLN
                x, s, h, hbt, junk = xt[ln], st[ln], hf[ln], hb[ln], junks[ln]
                sel = 1 if i < 2 else 0
                src, srcb = x_src(l, i)
                kb.dma("sp", x[:], src, reads=[srcb], writes=[x])
                yield
                kb.op("act", lambda e: e.activation(junk[:], x[:], AF.Square, accum_out=s[:, 0:1]), reads=[x], writes=[junk, s])
                yield
                kb.op("dve", lambda e: e.tensor_scalar(s[:, 1:2], s[:, 0:1], 1.0 / D, EPS, ALU.mult, ALU.add), reads=[s], writes=[s])
                yield
                kb.op("act", lambda e: e.sqrt(s[:, 2:3], s[:, 1:2]), reads=[s], writes=[s])
                yield
                kb.op("dve", lambda e: e.reciprocal(s[:, 3:4], s[:, 2:3]), reads=[s], writes=[s])
                kb.op("dve", lambda e: e.scalar_tensor_tensor(h[:], x[:], s[:, 3:4], G1[:, sel, :], ALU.mult, ALU.mult), reads=[x, s, G1], writes=[h])
                yield
                kb.op("pool", lambda e: e.tensor_tensor(hbt[:], h[:], SH[:, sel, :], ALU.add), reads=[h, SH], writes=[hbt])
                yield
                p = PS[ln]
                pv = p[:, :].bitcast(BF16)
                for j in range(8):
                    kb.op("pe", lambda e, j=j: e.transpose(pv[:, j * 128:(j + 1) * 128], hbt[:, j * 128:(j + 1) * 128], ident_bf[:]),
                          reads=[hbt, ident_bf], writes=[p])
                yield
                kb.op("act", lambda e: e.copy(hT[:, :, i * 128:(i + 1) * 128], pv.rearrange("p (j t) -> p j t", j=8)), reads=[p], writes=[hT])

            for i0 in range(0, NT, NLN):
                gens = [tile_gen(i) for i in range(i0, min(NT, i0 + NLN))]
                while gens:
                    nxt = []
                    for g_ in gens:
                        try:
                            next(g_)
                            nxt.append(g_)
                        except StopIteration:
                            pass
                    gens = nxt

    def load_w(l, dst, col0, ncols, stage, q="sp"):
        kb.dma(q, stage[:, :, 0:ncols], W["w_in"][l, :, col0:col0 + ncols].rearrange("(j p) n -> p j n", p=128),
               reads=[W["w_in"]], writes=[stage])
        kb.op("pool", lambda e: e.tensor_copy(dst[:, :, 0:ncols], stage[:, :, 0:ncols]), reads=[stage], writes=[dst])

    def proj_fm(p, wt, c0, nc_, t0, nt):
        for j in range(8):
            kb.op("pe", lambda e, j=j: e.matmul(p[0:nc_, 0:nt], wt[:, j, c0:c0 + nc_], hT[:, j, t0:t0 + nt],
                                                start=(j == 0), stop=(j == 7)), reads=[wt, hT], writes=[p])

    def proj_tm(p, wt, c0, nc_, i):
        for j in range(8):
            kb.op("pe", lambda e, j=j: e.matmul(p[:, 0:nc_], hT[:, j, i * 128:(i + 1) * 128], wt[:, j, c0:c0 + nc_],
                                                start=(j == 0), stop=(j == 7)), reads=[wt, hT], writes=[p])

    TCH = [(t0, min(512, T - t0)) for t0 in range(0, T, 512)]

    def qk_prep(l, es_tiles, wt, c0, dst, dst_j, gvec, norm, rope):
        raw, sq, rs, rot = es_tiles
        for ci, (t0, nt) in enumerate(TCH):
            p = PS[ci % 2]
            proj_fm(p, wt, c0, 128, t0, nt)
            if norm:
                kb.op("act", lambda e, p=p, nt=nt: e.activation(sq[:, 0:nt], p[:, 0:nt], AF.Square), reads=[p], writes=[sq])
                p2 = PS[2 + ci % 2]
                kb.op("pe", lambda e, p2=p2, nt=nt: e.matmul(p2[:, 0:nt], blk64[:], sq[:, 0:nt], start=True, stop=True),
                      reads=[blk64, sq], writes=[p2])
                kb.op("dve", lambda e, p2=p2, nt=nt: e.tensor_scalar(rs[:, 0:nt], p2[:, 0:nt], 1.0 / 64, EPS, ALU.mult, ALU.add),
                      reads=[p2], writes=[rs])
                kb.op("act", lambda e, nt=nt: e.sqrt(rs[:, 0:nt], rs[:, 0:nt]), reads=[rs], writes=[rs])
                kb.op("dve", lambda e, nt=nt: e.reciprocal(rs[:, 0:nt], rs[:, 0:nt]), reads=[rs], writes=[rs])
                kb.op("dve", lambda e, p=p, nt=nt: e.scalar_tensor_tensor(
                    raw[:, 0:nt], p[:, 0:nt], gvec[:, 0:1], rs[:, 0:nt], ALU.mult, ALU.mult), reads=[p, gvec, rs], writes=[raw])
            else:
                kb.op("act", lambda e, p=p, nt=nt: e.copy(raw[:, 0:nt], p[:, 0:nt]), reads=[p], writes=[raw])
            lat0 = 0
            if t0 < C:
                lat0 = C - t0
                kb.op("pool", lambda e, t0=t0, lat0=lat0: e.tensor_copy(dst[:, dst_j, t0:t0 + lat0], raw[:, 0:lat0]),
                      reads=[raw], writes=[dst])
            if not rope:
                if nt > lat0:
                    kb.op("pool", lambda e, t0=t0, lat0=lat0, nt=nt: e.tensor_copy(
                        dst[:, dst_j, t0 + lat0:t0 + nt], raw[:, lat0:nt]), reads=[raw], writes=[dst])
                continue
            p3 = PS[4 + ci % 2]
            n_l = nt - lat0
            lp = t0 + lat0 - C
            kb.op("pe", lambda e, p3=p3, lat0=lat0, nt=nt: e.matmul(p3[:, lat0:nt], rope_perm[:], raw[:, lat0:nt], start=True, stop=True),
                  reads=[rope_perm, raw], writes=[p3])
            kb.op("dve", lambda e, p3=p3, lat0=lat0, nt=nt, lp=lp, n_l=n_l: e.tensor_tensor(
                rot[:, lat0:nt], p3[:, lat0:nt], rope_sin[:, lp:lp + n_l], ALU.mult), reads=[p3, rope_sin], writes=[rot])
            kb.op("pool", lambda e, lat0=lat0, nt=nt, lp=lp, n_l=n_l: e.tensor_tensor(
                raw[:, lat0:nt], raw[:, lat0:nt], rope_cos[:, lp:lp + n_l], ALU.mult), reads=[raw, rope_cos], writes=[raw])
            kb.op("dve", lambda e, t0=t0, lat0=lat0, nt=nt: e.tensor_tensor(
                dst[:, dst_j, t0 + lat0:t0 + nt], raw[:, lat0:nt], rot[:, lat0:nt], ALU.add), reads=[raw, rot], writes=[dst])

    rope_cos = rope_sin = rope_perm = None

    def phase_attn(l, dense, with_ctx):
        nonlocal rope_cos, rope_sin, rope_perm
        base = FA0 if dense else WA0
        gbase = FAG0 if dense else WAG0
        mrow = 768 if dense else 512
        with kb.scope():
            wt = kb.sb("wt", [128, 8, 768], BF16)
            gq = kb.sb("gq", [128, 1])
            gk = kb.sb("gk", [128, 1])
            sink = kb.sb("sink", [128, 4])
            if dense:
                for hh in range(2):
                    kb.dma("sp", gq[hh * 64:(hh + 1) * 64, :], W["fa_q_norm"][l, :].rearrange("(d o) -> d o", o=1),
                           reads=[W["fa_q_norm"]], writes=[gq], slow=True)
                    kb.dma("sp", gk[hh * 64:(hh + 1) * 64, :], W["fa_k_norm"][l, :].rearrange("(d o) -> d o", o=1),
                           reads=[W["fa_k_norm"]], writes=[gk], slow=True)
            else:
                kb.dma("sp", sink[:], W["wa_sink"][l, :].partition_broadcast(128), reads=[W["wa_sink"]], writes=[sink])
            QT = kb.sb("QT", [128, 2, T], BF16)
            KT = kb.sb("KT", [128, 1, T], BF16)
            VW = 65 if dense else 64
            Vt = kb.sb("Vt", [128, NT, 2, VW], BF16)
            SG = None
            with kb.scope():
                rope_cos = kb.sb("rope_cos", [128, L])
                rope_sin = kb.sb("rope_sin", [128, L])
                rope_perm = kb.sb("rope_perm", [128, 128])
                kb.dma("sp", rope_cos[:], CT["rope_cos"][:, :], reads=[CT["rope_cos"]], writes=[rope_cos])
                kb.dma("pool", rope_sin[:], CT["rope_sin"][:, :], reads=[CT["rope_sin"]], writes=[rope_sin])
                kb.dma("sp", rope_perm[:], CT["rope_perm"][:, :], reads=[CT["rope_perm"]], writes=[rope_perm])
                stage = kb.sb("wstage", [128, 8, 256])
                w4 = W["w_in"][l, :, base:base + 256].rearrange("(j p) (h d) -> p j h d", p=128, d=64)
                st4 = stage[:, :, 0:256].rearrange("p j (h d) -> p j h d", d=64)
                for hi, h in enumerate((0, 2, 1, 3)):
                    kb.dma("sp", st4[:, :, hi, :], w4[:, :, h, :], reads=[W["w_in"]], writes=[stage])
                kb.op("pool", lambda e: e.tensor_copy(wt[:, :, 0:256], stage[:]), reads=[stage], writes=[wt])
                kb.dma("pool", stage[:], W["w_in"][l, :, base + 256:base + 512].rearrange("(j p) n -> p j n", p=128),
                       reads=[W["w_in"]], writes=[stage])
                kb.op("pool", lambda e: e.tensor_copy(wt[:, :, 256:512], stage[:]), reads=[stage], writes=[wt])
                kb.dma("sp", stage[:], W["w_in"][l, :, gbase:gbase + 256].rearrange("(j p) n -> p j n", p=128),
                       reads=[W["w_in"]], writes=[stage])
                kb.op("pool", lambda e: e.tensor_copy(wt[:, :, 512:768], stage[:]), reads=[stage], writes=[wt])
                tl = (kb.sb("qraw", [128, 512]), kb.sb("qsq", [128, 512]), kb.sb("qrs", [128, 512]), kb.sb("qrot", [128, 512]))
                qk_prep(l, tl, wt, 0, QT, 0, gq, dense, True)
                qk_prep(l, tl, wt, 128, QT, 1, gq, dense, True)
                qk_prep(l, tl, wt, 256, KT, 0, gk, dense, True)
            if dense:
                kb.op("pool", lambda e: e.memset(Vt[:, :, :, 64:65], 1.0), writes=[Vt])
            for i in range(NT):
                p = PS[i % 2]
                proj_tm(p, wt, 384, 128, i)
                kb.op("act", lambda e, p=p, i=i: e.copy(Vt[:, i, :, 0:64], p[:, 0:128].rearrange("p (k d) -> p k d", d=64)),
                      reads=[p], writes=[Vt])
            with kb.scope():
                if dense:
                    attn_dense(l, wt, QT, KT, Vt, mrow, with_ctx)
                else:
                    attn_window(l, wt, QT, KT, Vt, sink, mrow, with_ctx)

    def attn_dense(l, wt, QT, KT, Vt, mrow, with_ctx):
        pt = [kb.sb(f"pt{i}", [128, 512], BF16) for i in range(8)]
        osb = [kb.sb(f"osb{i}", [128, 512]) for i in range(2)]
        rc = [kb.sb(f"rc{i}", [128, 512]) for i in range(2)]
        ob = [kb.sb(f"ob{i}", [128, 512], BF16) for i in range(2)]
        sgt = [kb.sb(f"sgt{i}", [64, 512], BF16) for i in range(2)]
        chunks = []
        if with_ctx:
            chunks.append((0, C, 0, 2))
        for t0 in range(C, T, 512):
            chunks.append((t0, 512, 0, NT))
        for pr in range(2):
            heads_ = (pr, pr + 2)
            for (t0, nt, kb0, kb1) in chunks:
                po = [PS[4], PS[5]]
                pg = [PS[6], PS[7]]
                for s_, h in enumerate(heads_):
                    for j in range(8):
                        kb.op("pe", lambda e, j=j, s_=s_, h=h: e.matmul(
                            pg[s_][0:64, 0:nt], wt[:, j, 512 + 64 * h:576 + 64 * h], hT[:, j, t0:t0 + nt], start=(j == 0), stop=(j == 7)),
                            reads=[wt, hT], writes=[pg[s_]])
                    kb.op("act", lambda e, s_=s_: e.activation(sgt[s_][0:64, 0:nt], pg[s_][0:64, 0:nt], AF.Silu), reads=[pg[s_]], writes=[sgt[s_]])

                def pv_(kbi):
                    for s_ in range(2):
                        ptt = pt[(2 * kbi + s_) % 8]
                        kb.op("pe", lambda e, s_=s_, ptt=ptt: e.matmul(
                            po[s_][0:65, 0:nt], Vt[:, kbi, s_, 0:65], ptt[:, 0:nt], start=(kbi == kb0), stop=(kbi == kb1 - 1)),
                            reads=[Vt, ptt], writes=[po[s_]])
                LA = 1
                for kbi in range(kb0, kb1):
                    for s_ in range(2):
                        ks = slice(64 * s_, 64 * s_ + 64)
                        psS = PS[(2 * kbi + s_) % 4]
                        kb.op("pe", lambda e, psS=psS, ks=ks: e.matmul(
                            psS[:, 0:nt], KT[ks, 0, kbi * 128:(kbi + 1) * 128], QT[ks, pr, t0:t0 + nt], start=True, stop=True),
                            reads=[KT, QT], writes=[psS])
                    for s_ in range(2):
                        psS = PS[(2 * kbi + s_) % 4]
                        ptt = pt[(2 * kbi + s_) % 8]
                        kb.op("act", lambda e, psS=psS, ptt=ptt: e.activation(ptt[:, 0:nt], psS[:, 0:nt], AF.Exp, scale=0.125),
                              reads=[psS], writes=[ptt])
                    if kbi - LA >= kb0:
                        pv_(kbi - LA)
                for kbi in range(max(kb0, kb1 - LA), kb1):
                    pv_(kbi)
                for s_, h in enumerate(heads_):
                    o_s, r_c, o_b, sg = osb[s_], rc[s_], ob[s_], sgt[s_]
                    pb = pg[s_]
                    kb.op("dve", lambda e, s_=s_, r_c=r_c: e.reciprocal(r_c[64:65, 0:nt], po[s_][64:65, 0:nt]), reads=[po[s_]], writes=[r_c])
                    kb.op("act", lambda e, s_=s_, o_s=o_s: e.copy(o_s[0:64, 0:nt], po[s_][0:64, 0:nt]), reads=[po[s_]], writes=[o_s])
                    kb.op("pe", lambda e, pb=pb, r_c=r_c: e.matmul(pb[0:64, 0:nt], ones_f[64:65, 0:64], r_c[64:65, 0:nt], start=True, stop=True),
                          reads=[ones_f, r_c], writes=[pb])
                    kb.op("dve", lambda e, o_s=o_s, pb=pb: e.tensor_tensor(o_s[0:64, 0:nt], o_s[0:64, 0:nt], pb[0:64, 0:nt], ALU.mult),
                          reads=[o_s, pb], writes=[o_s])
                    kb.op("pool", lambda e, o_s=o_s, o_b=o_b, sg=sg: e.tensor_tensor(o_b[0:64, 0:nt], o_s[0:64, 0:nt], sg[0:64, 0:nt], ALU.mult),
                          reads=[o_s, sg], writes=[o_b])
                    kb.dma("sp", mixT[mrow + 64 * h:mrow + 64 * h + 64, t0:t0 + nt], o_b[0:64, 0:nt], reads=[o_b], writes=[Buf()])

    def attn_window(l, wt, QT, KT, Vt, sink, mrow, with_ctx):
        wmask = kb.sb("wmask", [128, 384])
        kb.dma("sp", wmask[:], CT["wmask"][:, :], reads=[CT["wmask"]], writes=[wmask])
        nsink = kb.sb("nsink", [128, 4])
        kb.op("dve", lambda e: e.tensor_scalar(nsink[:], sink[:], -1.0, None, ALU.mult), reads=[sink], writes=[nsink])
        S = [kb.sb(f"wS{i}", [128, 640]) for i in range(4)]
        P = [kb.sb(f"wP{i}", [128, 640]) for i in range(4)]
        Pn = [kb.sb(f"wPn{i}", [128, 640], BF16) for i in range(4)]
        PT = [kb.sb(f"wPT{i}", [128, 640], BF16) for i in range(4)]
        st = [kb.sb(f"wst{i}", [128, 8]) for i in range(4)]
        sgt = [kb.sb(f"wsg{i}", [64, 128], BF16) for i in range(4)]
        ob = [kb.sb(f"wob{i}", [64, 128], BF16) for i in range(4)]
        it = 0
        for i in range(0 if with_ctx else 2, NT):
            if i < 2:
                loc = []
            else:
                loc = list(range(max(2, i - 1), min(NT - 1, i + 1) + 1))
            nl = 128 * len(loc)
            m0 = 128 if (i >= 2 and i - 1 < 2) else 0
            nk = nl + C
            ktiles = loc + [0, 1]
            def unit(h, it):
                kv, pr = h // 2, h % 2
                ks = slice(64 * kv, 64 * kv + 64)
                s_, p_, pn_, pt_, st_, sg, o_b = S[it % 4], P[it % 4], Pn[it % 4], PT[it % 4], st[it % 4], sgt[it % 4], ob[it % 4]
                psA, psB = PS[2 * (it % 4)], PS[2 * (it % 4) + 1]
                psT, psOG = psA, psB
                psO, psG = psOG, psOG
                q_ap = QT[ks, pr, i * 128:(i + 1) * 128]
                if nl:
                    k0 = loc[0] * 128
                    kb.op("pe", lambda e: e.matmul(psA[:, 0:nl], q_ap, KT[ks, 0, k0:k0 + nl], start=True, stop=True),
                          reads=[QT, KT], writes=[psA])
                    kb.op("dve", lambda e: e.tensor_tensor(s_[:, 0:nl], psA[:, 0:nl], wmask[:, m0:m0 + nl], ALU.add),
                          reads=[psA, wmask], writes=[s_])
                kb.op("pe", lambda e: e.matmul(psB[:, 0:C], q_ap, KT[ks, 0, 0:C], start=True, stop=True),
                      reads=[QT, KT], writes=[psB])
                kb.op("act", lambda e: e.copy(s_[:, nl:nk], psB[:, 0:C]), reads=[psB], writes=[s_])
                yield
                kb.op("dve", lambda e: e.reduce_max(st_[:, 0:1], s_[:, 0:nk], AX.X), reads=[s_], writes=[st_])
                kb.op("dve", lambda e: e.tensor_scalar(st_[:, 1:2], st_[:, 0:1], -0.125, nsink[:, h:h + 1], ALU.mult, ALU.min),
                      reads=[st_, nsink], writes=[st_])
                kb.op("act", lambda e: e.activation(p_[:, 0:nk], s_[:, 0:nk], AF.Exp, bias=st_[:, 1:2], scale=0.125,
                                                    accum_out=st_[:, 2:3]), reads=[s_, st_], writes=[p_, st_])
                kb.op("act", lambda e: e.activation(st_[:, 3:4], sink[:, h:h + 1], AF.Exp, bias=st_[:, 1:2], scale=1.0),
                      reads=[sink, st_], writes=[st_])
                kb.op("dve", lambda e: e.tensor_tensor(st_[:, 4:5], st_[:, 2:3], st_[:, 3:4], ALU.add), reads=[st_], writes=[st_])
                kb.op("dve", lambda e: e.reciprocal(st_[:, 5:6], st_[:, 4:5]), reads=[st_], writes=[st_])
                kb.op("dve", lambda e: e.tensor_scalar(pn_[:, 0:nk], p_[:, 0:nk], st_[:, 5:6], None, ALU.mult),
                      reads=[p_, st_], writes=[pn_])
                yield
                pv = psT[:, :].bitcast(BF16)
                nb = nk // 128
                for b in range(nb):
                    kb.op("pe", lambda e, b=b: e.transpose(pv[:, b * 128:(b + 1) * 128], pn_[:, b * 128:(b + 1) * 128], ident_bf[:]),
                          reads=[pn_, ident_bf], writes=[psT])
                yield
                kb.op("act", lambda e: e.copy(pt_[:, 0:nk], pv[:, 0:nk]), reads=[psT], writes=[pt_])
                yield
                for b in range(nb):
                    kb.op("pe", lambda e, b=b: e.matmul(psO[0:64, 0:128], Vt[:, ktiles[b], kv, 0:64], pt_[:, b * 128:(b + 1) * 128],
                                                        start=(b == 0), stop=(b == nb - 1)), reads=[Vt, pt_], writes=[psO])
                for j in range(8):
                    kb.op("pe", lambda e, j=j: e.matmul(
                        psG[0:64, 128:256], wt[:, j, 512 + 64 * h:576 + 64 * h], hT[:, j, i * 128:(i + 1) * 128],
                        start=(j == 0), stop=(j == 7)), reads=[wt, hT], writes=[psG])
                yield
                kb.op("act", lambda e: e.activation(sg[:, :], psG[0:64, 128:256], AF.Silu), reads=[psG], writes=[sg])
                kb.op("dve", lambda e: e.tensor_tensor(o_b[:, :], psO[0:64, 0:128], sg[:, :], ALU.mult), reads=[psO, sg], writes=[o_b])
                kb.dma("sp", mixT[mrow + 64 * h:mrow + 64 * h + 64, i * 128:(i + 1) * 128], o_b[:, :], reads=[o_b], writes=[Buf()])

            for h0 in (0,):
                gens = [unit(h_, it + k_) for k_, h_ in enumerate((0, 2, 1, 3))]
                it += 4
                while gens:
                    nxt = []
                    for g_ in gens:
                        try:
                            next(g_)
                            nxt.append(g_)
                        except StopIteration:
                            pass
                    gens = nxt

    def phase_out(l, last):
        with kb.scope():
            wo = kb.sb("wo", [128, 8, D], BF16)
            stage = kb.sb("wostage", [128, 8, 256])
            for q in range(4):
                kb.dma("sp", stage[:], W["w_out"][l, :, q * 256:(q + 1) * 256].rearrange("(j p) n -> p j n", p=128),
                       reads=[W["w_out"]], writes=[stage])
                kb.op("pool", lambda e, q=q: e.tensor_copy(wo[:, :, q * 256:(q + 1) * 256], stage[:]), reads=[stage], writes=[wo])
            fg = kb.sb("fg", [128, D])
            if last:
                kb.dma("sp", fg[:], W["final_g"][:].partition_broadcast(128), reads=[W["final_g"]], writes=[fg])
            mt = [kb.sb(f"mt{i}", [128, 8, 128], BF16) for i in range(2)]
            xt = [kb.sb(f"oxt{i}", [128, D]) for i in range(2)]
            xn = [kb.sb(f"oxn{i}", [128, D]) for i in range(2)]
            tmp = [kb.sb(f"otmp{i}", [128, 512]) for i in range(2)]
            st = [kb.sb(f"ost{i}", [128, 4]) for i in range(2)]
            junk = kb.sb("ojunk", [128, D])
            mixv = mixT.t.rearrange("(j p) t -> p j t", p=128)
            for it, i in enumerate(range(2 if last else 0, NT)):
                m, x, xo, s = mt[it % 2], xt[it % 2], xn[it % 2], st[it % 2]
                sel = 1 if i < 2 else 0
                kb.dma("sp", m[:], mixv[:, :, i * 128:(i + 1) * 128], reads=[mixT], writes=[m])
                src, srcb = x_src(l, i)
                kb.dma("pool", x[:], src, reads=[srcb], writes=[x])
                for hf in range(2):
                    p = PS[(2 * it + hf) % 8]
                    tp = tmp[hf]
                    for j in range(8):
                        kb.op("pe", lambda e, j=j, p=p, m=m, hf=hf: e.matmul(p[:, :], m[:, j, :], wo[:, j, hf * 512:(hf + 1) * 512],
                                                                     start=(j == 0), stop=(j == 7)), reads=[m, wo], writes=[p])
                    kb.op("dve", lambda e, p=p, tp=tp, hf=hf, sel=sel: e.tensor_tensor(
                        tp[:], p[:, :], GT[:, sel, hf * 512:(hf + 1) * 512], ALU.mult), reads=[p, GT], writes=[tp])
                    kb.op("pool", lambda e, tp=tp, hf=hf, x=x, xo=xo: e.tensor_tensor(
                        xo[:, hf * 512:(hf + 1) * 512], x[:, hf * 512:(hf + 1) * 512], tp[:], ALU.add), reads=[x, tp], writes=[xo])
                if not last:
                    kb.dma("sp", xres[i * 128:(i + 1) * 128, :], xo[:], reads=[xo], writes=[xres_b[i]])
                else:
                    kb.op("act", lambda e, xo=xo, s=s: e.activation(junk[:], xo[:], AF.Square, accum_out=s[:, 0:1]),
                          reads=[xo], writes=[junk, s])
                    kb.op("dve", lambda e, s=s: e.tensor_scalar(s[:, 1:2], s[:, 0:1], 1.0 / D, EPS, ALU.mult, ALU.add),
                          reads=[s], writes=[s])
                    kb.op("act", lambda e, s=s: e.sqrt(s[:, 2:3], s[:, 1:2]), reads=[s], writes=[s])
                    kb.op("dve", lambda e, s=s: e.reciprocal(s[:, 3:4], s[:, 2:3]), reads=[s], writes=[s])
                    kb.op("dve", lambda e, xo=xo, s=s, x=x: e.scalar_tensor_tensor(
                        x[:], xo[:], s[:, 3:4], fg[:], ALU.mult, ALU.mult), reads=[xo, s, fg], writes=[x])
                    kb.dma("sp", out[(i - 2) * 128:(i - 1) * 128, :], x[:], reads=[x], writes=[Buf()])


    def conv_tile(l, wt, cw, ncw, jt, Zraw, Zout):
        for ci, (t0, nt) in enumerate(TCH):
            p = PS[ci % 4]
            proj_fm(p, wt, 0, 128, t0, nt)
            kb.op("act", lambda e, p=p, t0=t0, nt=nt: e.copy(Zraw[:, 1 + t0:1 + t0 + nt], p[:, 0:nt]), reads=[p], writes=[Zraw])
        kb.op("dve", lambda e: e.tensor_scalar(Zout[:, :], Zraw[:, 1:T + 1], cw[:, jt, 1:2], None, ALU.mult), reads=[Zraw, cw], writes=[Zout])
        kb.op("dve", lambda e: e.scalar_tensor_tensor(Zout[:, :], Zraw[:, 0:T], cw[:, jt, 0:1], Zout[:, :], ALU.mult, ALU.add),
              reads=[Zraw, cw, Zout], writes=[Zout])
        kb.op("dve", lambda e: e.scalar_tensor_tensor(Zout[:, :], Zraw[:, 2:T + 2], cw[:, jt, 2:3], Zout[:, :], ALU.mult, ALU.add),
              reads=[Zraw, cw, Zout], writes=[Zout])
        kb.op("dve", lambda e: e.scalar_tensor_tensor(Zout[:, C - 1:C], Zraw[:, C + 1:C + 2], ncw[:, jt, 2:3], Zout[:, C - 1:C], ALU.mult, ALU.add),
              reads=[Zraw, ncw, Zout], writes=[Zout])
        kb.op("dve", lambda e: e.scalar_tensor_tensor(Zout[:, C:C + 1], Zraw[:, C:C + 1], ncw[:, jt, 0:1], Zout[:, C:C + 1], ALU.mult, ALU.add),
              reads=[Zraw, ncw, Zout], writes=[Zout])

    def colvec(name, src_ap, srcb, shape, rearr, **kw):
        t = kb.sb(name, shape)
        kb.dma("sp", t[:], src_ap.rearrange(rearr, **kw), reads=[srcb], writes=[t], slow=True)
        return t

    def phase_rwkv_prep(l):
        with kb.scope():
            stage = kb.sb("rstage", [128, 8, 128])
            wts = [kb.sb(f"rwt{i}", [128, 8, 128], BF16) for i in range(2)]
            cw = kb.sb("rcw", [128, 7, 3])
            for k in range(3):
                kb.dma("sp", cw[:, :, k], W["rw_conv"][l, k, :].rearrange("(j p) -> p j", p=128), reads=[W["rw_conv"]], writes=[cw], slow=True)
            ncw = kb.sb("rncw", [128, 7, 3])
            kb.op("dve", lambda e: e.tensor_scalar(ncw[:], cw[:], -1.0, None, ALU.mult), reads=[cw], writes=[ncw])
            kk_ = colvec("rkk", W["rw_k_k"][l, :], W["rw_k_k"], [128, 2], "(j p) -> p j", p=128)
            ka_ = colvec("rka", W["rw_k_a"][l, :], W["rw_k_a"], [128, 2], "(j p) -> p j", p=128)
            omka = kb.sb("romka", [128, 2])
            kb.op("dve", lambda e: e.tensor_scalar(omka[:], ka_[:], -1.0, 1.0, ALU.mult, ALU.add), reads=[ka_], writes=[omka])
            w0_ = kb.sb("rw0", [128, 2, 2])
            a0_ = kb.sb("ra0", [128, 2, 2])
            for d in range(2):
                kb.dma("sp", w0_[:, d, :], W["rw_w0"][l, d, :].rearrange("(j p) -> p j", p=128), reads=[W["rw_w0"]], writes=[w0_], slow=True)
                kb.dma("sp", a0_[:, d, :], W["rw_a0"][l, d, :].rearrange("(j p) -> p j", p=128), reads=[W["rw_a0"]], writes=[a0_], slow=True)
            wup = kb.sb("rwup", [128, 2, 256])
            kb.dma("sp", wup[0:64, :, :], W["rw_w_up"][l, :, :, :].rearrange("d k n -> k d n"), reads=[W["rw_w_up"]], writes=[wup])
            kb.dma("sp", wup[64:128, :, :], W["rw_a_up"][l, :, :, :].rearrange("d k n -> k d n"), reads=[W["rw_a_up"]], writes=[wup])
            Zraw = kb.sb("rZraw", [128, T + 2])
            Zout = kb.sb("rZout", [128, T])
            Z6 = kb.sb("rZ6", [128, T])
            kb.op("pool", lambda e: e.memset(Zraw[:, 0:1], 0.0), writes=[Zraw])
            kb.op("pool", lambda e: e.memset(Zraw[:, T + 1:T + 2], 0.0), writes=[Zraw])
            tA = [kb.sb(f"rtA{i}", [128, 512]) for i in range(2)]
            tB = [kb.sb(f"rtB{i}", [128, 512]) for i in range(2)]
            tC = [kb.sb(f"rtC{i}", [128, 512]) for i in range(2)]
            tD = [kb.sb(f"rtD{i}", [128, 512]) for i in range(2)]
            tE = [kb.sb(f"rtE{i}", [128, 512]) for i in range(2)]
            tG = [kb.sb(f"rtG{i}", [128, 512], BF16) for i in range(2)]
            vt_ = [kb.sb(f"rvt{i}", [128, 128]) for i in range(2)]
            order = [6, 0, 1, 4, 5, 2, 3, 7, 8]
            for oi, jt in enumerate(order):
                wt = wts[oi % 2]
                c0 = RW0 + jt * 128 if jt < 7 else RWG0 + (jt - 7) * 128
                load_w(l, wt, c0, 128, stage)
                if jt >= 7:
                    for ci, (t0, nt) in enumerate(TCH):
                        p = PS[ci % 4]
                        proj_fm(p, wt, 0, 128, t0, nt)
                        g = tG[ci % 2]
                        kb.op("act", lambda e, p=p, g=g, nt=nt: e.activation(g[:, 0:nt], p[:, 0:nt], AF.Silu), reads=[p], writes=[g])
                        kb.dma("sp", RS["SGT"][(jt - 7) * 128:(jt - 6) * 128, t0:t0 + nt], g[:, 0:nt], reads=[g], writes=[Buf()])
                    continue
                conv_tile(l, wt, cw, ncw, jt, Zraw, Z6 if jt == 6 else Zout)
                if jt == 6:
                    kb.op("act", lambda e: e.activation(Z6[0:64, :], Z6[0:64, :], AF.Tanh), reads=[Z6], writes=[Z6])
                elif jt in (0, 1):
                    kb.dma("sp", RS["RT"][jt * 128:(jt + 1) * 128, :], Zout[:, :], reads=[Zout], writes=[Buf()])
                elif jt in (4, 5):
                    kb.dma("sp", RS["VT"][(jt - 4) * 128:(jt - 3) * 128, :], Zout[:, :], reads=[Zout], writes=[Buf()])
                    for i in range(NT):
                        p = PS[4 + i % 2]
                        kb.op("pe", lambda e, p=p, i=i: e.transpose(p[:, 0:128], Zout[:, i * 128:(i + 1) * 128], ident_f[:]),
                              reads=[Zout, ident_f], writes=[p])
                        v = vt_[i % 2]
                        kb.op("act", lambda e, p=p, v=v: e.copy(v[:, :], p[:, 0:128]), reads=[p], writes=[v])
                        kb.dma("pool", RS["VTOK"][i * 128:(i + 1) * 128, (jt - 4) * 128:(jt - 3) * 128], v[:, :], reads=[v], writes=[Buf()])
                else:
                    pt = jt - 2
                    rows = slice(pt * 128, (pt + 1) * 128)
                    for ci, (t0, nt) in enumerate(TCH):
                        a_, b_, c_, d_, e_ = tA[ci % 2], tB[ci % 2], tC[ci % 2], tD[ci % 2], tE[ci % 2]
                        zc = Zout[:, t0:t0 + nt]
                        kb.op("dve", lambda e: e.tensor_scalar(a_[:, 0:nt], zc, kk_[:, pt:pt + 1], None, ALU.mult), reads=[Zout, kk_], writes=[a_])
                        kb.op("act", lambda e: e.activation(b_[:, 0:nt], a_[:, 0:nt], AF.Square), reads=[a_], writes=[b_])
                        p = PS[ci % 2]
                        kb.op("pe", lambda e: e.matmul(p[:, 0:nt], blk64[:], b_[:, 0:nt], start=True, stop=True), reads=[blk64, b_], writes=[p])
                        kb.op("act", lambda e: e.sqrt(b_[:, 0:nt], p[:, 0:nt]), reads=[p], writes=[b_])
                        kb.op("dve", lambda e: e.tensor_scalar(b_[:, 0:nt], b_[:, 0:nt], 1e-12, None, ALU.max), reads=[b_], writes=[b_])
                        kb.op("dve", lambda e: e.reciprocal(b_[:, 0:nt], b_[:, 0:nt]), reads=[b_], writes=[b_])
                        kb.op("dve", lambda e: e.scalar_tensor_tensor(a_[:, 0:nt], a_[:, 0:nt], -1.0, b_[:, 0:nt], ALU.mult, ALU.mult),
                              reads=[a_, b_], writes=[a_])
                        kb.dma("sp", RS["AL"][rows, t0:t0 + nt], a_[:, 0:nt], reads=[a_], writes=[Buf()])
                        for d in range(2):
                            pa = PS[2 + d]
                            kb.op("pe", lambda e: e.matmul(pa[:, 0:nt], wup[64:128, d, pt * 128:(pt + 1) * 128], Z6[64:128, t0:t0 + nt],
                                                           start=True, stop=True), reads=[wup, Z6], writes=[pa])
                            kb.op("act", lambda e: e.activation(c_[:, 0:nt], pa[:, 0:nt], AF.Sigmoid, bias=a0_[:, d, pt:pt + 1]),
                                  reads=[pa, a0_], writes=[c_])
                            kb.op("dve", lambda e: e.scalar_tensor_tensor(d_[:, 0:nt], c_[:, 0:nt], -1.0, a_[:, 0:nt], ALU.mult, ALU.mult),
                                  reads=[c_, a_], writes=[d_])
                            kb.dma("sp", RS[f"B{d}"][rows, t0:t0 + nt], d_[:, 0:nt], reads=[d_], writes=[Buf()])
                            kb.op("dve", lambda e: e.tensor_scalar(c_[:, 0:nt], c_[:, 0:nt], ka_[:, pt:pt + 1], omka[:, pt:pt + 1], ALU.mult, ALU.add),
                                  reads=[c_, ka_, omka], writes=[c_])
                            kb.op("dve", lambda e: e.tensor_tensor(e_[:, 0:nt], c_[:, 0:nt], zc, ALU.mult), reads=[c_, Zout], writes=[e_])
                            kb.dma("pool", RS[f"KD{d}"][rows, t0:t0 + nt], e_[:, 0:nt], reads=[e_], writes=[Buf()])
                            pw = PS[4 + d]
                            kb.op("pe", lambda e: e.matmul(pw[:, 0:nt], wup[0:64, d, pt * 128:(pt + 1) * 128], Z6[0:64, t0:t0 + nt],
                                                           start=True, stop=True), reads=[wup, Z6], writes=[pw])
                            kb.op("act", lambda e: e.activation(c_[:, 0:nt], pw[:, 0:nt], AF.Sigmoid, bias=w0_[:, d, pt:pt + 1]),
                                  reads=[pw, w0_], writes=[c_])
                            kb.op("dve", lambda e: e.tensor_scalar(d_[:, 0:nt], c_[:, 0:nt], -math.exp(-0.5), None, ALU.mult),
                                  reads=[c_], writes=[d_])
                            kb.dma("pool", RS[f"W{d}"][rows, t0:t0 + nt], d_[:, 0:nt], reads=[d_], writes=[Buf()])

    def phase_rwkv_scan(l):
        with kb.scope():
            ST = [kb.sb(f"ST{d}", [128, 2, 64]) for d in range(2)]
            for d in range(2):
                kb.op("pool", lambda e, d=d: e.memset(ST[d][:], 0.0), writes=[ST[d]])
            names = ("AL", "W", "B", "KD", "RT")
            ch = [[{n: kb.sb(f"c{n}{d}{i}", [128, 2, 128]) for n in names} for i in range(2)] for d in range(2)]
            vch = [[kb.sb(f"cV{d}{i}", [128, 256]) for i in range(2)] for d in range(2)]
            t1 = [kb.sb(f"st1{d}", [128, 2, 64]) for d in range(2)]
            t2 = [kb.sb(f"st2{d}", [128, 2, 64]) for d in range(2)]
            ysb = [kb.sb(f"ysb{d}", [64, 512]) for d in range(2)]
            psSA, psV, psY = [PS[0], PS[1]], [PS[2], PS[3]], [PS[4], PS[5]]
            border = [1, 0] + list(range(NT - 1, 1, -1))
            for ci in range(NT):
                cidx = [ci, border[ci]]
                cur = []
                for d in range(2):
                    c0 = cidx[d] * 128
                    tl_ = ch[d][ci % 2]
                    for n in names:
                        src = RS[n if n in ("AL", "RT") else f"{n}{d}"]
                        kb.dma("sp" if d == 0 else "pool", tl_[n][:],
                               src.t.rearrange("(pr q) t -> q pr t", q=128)[:, :, c0:c0 + 128], reads=[src], writes=[tl_[n]])
                    vv = vch[d][ci % 2]
                    kb.dma("sp" if d == 0 else "pool", vv[:], RS["VTOK"][c0:c0 + 128, :], reads=[RS["VTOK"]], writes=[vv])
                    cur.append((tl_, vv))
                for tl in range(128):
                    for d in range(2):
                        col = tl if d == 0 else 127 - tl
                        tl_, vv = cur[d]
                        S_, sa, pv, py = ST[d], psSA[d], psV[d], psY[d]
                        for pr in range(2):
                            for hp in range(2):
                                rows = slice(64 * hp, 64 * hp + 64)
                                kb.op("pe", lambda e, pr=pr, rows=rows: e.matmul(
                                    sa[rows, pr * 64:(pr + 1) * 64], tl_["AL"][rows, pr, col:col + 1].broadcast_to([64, 64]),
                                    S_[rows, pr, :], start=True, stop=True), reads=[tl_["AL"], S_], writes=[sa])
                        for pr in range(2):
                            for hp in range(2):
                                rows = slice(64 * hp, 64 * hp + 64)
                                h = 2 * pr + hp
                                kb.op("pe", lambda e, pr=pr, rows=rows, h=h: e.matmul(
                                    pv[rows, pr * 64:(pr + 1) * 64], ident_f[:, col:col + 1].broadcast_to([128, 64]),
                                    vv[:, h * 64:(h + 1) * 64], start=True, stop=True), reads=[ident_f, vv], writes=[pv])
                        for pr in range(2):
                            kb.op("dve", lambda e, pr=pr: e.tensor_scalar(
                                t1[d][:, pr, :], sa[:, pr * 64:(pr + 1) * 64], tl_["B"][:, pr, col:col + 1], None, ALU.mult),
                                reads=[sa, tl_["B"]], writes=[t1[d]])
                            kb.op("dve", lambda e, pr=pr: e.scalar_tensor_tensor(
                                t2[d][:, pr, :], pv[:, pr * 64:(pr + 1) * 64], tl_["KD"][:, pr, col:col + 1], t1[d][:, pr, :], ALU.mult, ALU.add),
                                reads=[pv, tl_["KD"], t1[d]], writes=[t2[d]])
                            kb.op("dve", lambda e, pr=pr: e.scalar_tensor_tensor(
                                S_[:, pr, :], S_[:, pr, :], tl_["W"][:, pr, col:col + 1], t2[d][:, pr, :], ALU.mult, ALU.add),
                                reads=[S_, tl_["W"], t2[d]], writes=[S_])
                        for pr in range(2):
                            for hp in range(2):
                                rows = slice(64 * hp, 64 * hp + 64)
                                h = 2 * pr + hp
                                kb.op("pe", lambda e, pr=pr, rows=rows, h=h: e.matmul(
                                    py[0:64, h * 128 + col:h * 128 + col + 1], S_[rows, pr, :], tl_["RT"][rows, pr, col:col + 1],
                                    start=True, stop=True), reads=[S_, tl_["RT"]], writes=[py])
                for d in range(2):
                    c0 = cidx[d] * 128
                    kb.op("act", lambda e, d=d: e.copy(ysb[d][:, :], psY[d][0:64, :]), reads=[psY[d]], writes=[ysb[d]])
                    dst = RS["YF" if d == 0 else "YB"]
                    kb.dma("sp", dst.t.rearrange("(h v) t -> v h t", v=64)[:, :, c0:c0 + 128],
                           ysb[d][:, :].rearrange("v (h t) -> v h t", h=4), reads=[ysb[d]], writes=[Buf()])


    def phase_rwkv_chunked(l):
        CH = 64
        NCH = T // CH
        with kb.scope():
            def ldc(nm, shape):
                t = kb.sb("k" + nm, shape)
                kb.dma("sp", t[:], CT[nm].t, reads=[CT[nm]], writes=[t])
                return t
            Ms = ldc("rw_ms", [128, 2, 64]); MTs = ldc("rw_mts", [128, 2, 64]); MTi = ldc("rw_mti", [128, 2, 64])
            id2 = ldc("rw_id2", [128, 64])
            ones = kb.sb("rones", [128, 64])
            kb.op("pool", lambda e: e.memset(ones[:], 1.0), writes=[ones])
            ST = kb.sb("cST", [128, 4, 64])
            kb.op("pool", lambda e: e.memset(ST[:], 0.0), writes=[ST])
            names = ("AL", "W", "B", "KD", "RT")
            def t4(nm, n=2, w=64):
                return [kb.sb(f"{nm}{i}", [128, 4, w]) for i in range(n)]
            IN = {n: t4("ci" + n) for n in names}
            VTK = t4("cVTK")
            CS = t4("cCS", 1)[0]; TOT = kb.sb("cTOT", [128, 4]); TMP = t4("cTMP", 1)[0]
            Epos = t4("cEp", 1)[0]; Eneg = t4("cEn", 1)[0]; Eprev = t4("cEv", 1)[0]; Etot = t4("cEt", 1)[0]; Wtot = kb.sb("cWt", [128, 4])
            Ab = t4("cAb", 1)[0]; Bb = t4("cBb", 1)[0]; Kb = t4("cKb", 1)[0]; Rb = t4("cRb", 1)[0]; Bt = t4("cBt", 1)[0]; Kt = t4("cKt", 1)[0]
            Q = t4("cQ"); P = t4("cP"); ArbT = t4("cArbT", 1)[0]; AkvT = t4("cAkvT", 1)[0]; ArkT = t4("cArkT", 1)[0]
            X = t4("cX", 2, 128); Btok = t4("cBtok", 1)[0]; Ktok = t4("cKtok", 1)[0]
            RAT = t4("cRAT", 1)[0]; McT = t4("cMcT", 1)[0]; NcS = t4("cNcS", 1)[0]; DG = t4("cDG", 1)[0]
            ysb = [kb.sb(f"cysb{d}", [64, 256]) for d in range(2)]
            border = [3, 2, 1, 0] + list(range(NCH - 1, 3, -1))
            DP = [(d, pr) for d in range(2) for pr in range(2)]
            HP = [slice(0, 64), slice(64, 128)]

            def mm_all(ps, col_fn, lhs_fn, rhs_fn, reads, start=True, stop=True, w=None):
                for dp in range(4):
                    for hp in range(2):
                        r = HP[hp]
                        c0, c1 = col_fn(dp)
                        kb.op("pe", lambda e, dp=dp, r=r, c0=c0, c1=c1: e.matmul(ps[r, c0:c1], lhs_fn(dp, r), rhs_fn(dp, r), start=start, stop=stop),
                              reads=reads, writes=[ps])

            for ci in range(NCH):
                cidx = [ci, border[ci]]
                i2 = ci % 2
                for d in range(2):
                    c0 = cidx[d] * CH
                    for n in names:
                        src = RS[n if n in ("AL", "RT") else f"{n}{d}"]
                        kb.dma("sp" if d == 0 else "pool", IN[n][i2][:, 2 * d:2 * d + 2, :],
                               src.t.rearrange("(pr q) t -> q pr t", q=128)[:, :, c0:c0 + CH], reads=[src], writes=[IN[n][i2]])
                    for hp in range(2):
                        kb.dma("sp" if d == 0 else "pool", VTK[i2][HP[hp], 2 * d:2 * d + 2, :],
                               RS["VTOK"][c0:c0 + CH, :].rearrange("t (pr hp v) -> t pr hp v", pr=2, hp=2)[:, :, hp, :],
                               reads=[RS["VTOK"]], writes=[VTK[i2]])
                al, lw, be, kd, rt, vt = IN["AL"][i2], IN["W"][i2], IN["B"][i2], IN["KD"][i2], IN["RT"][i2], VTK[i2]
                if RW_STAGE <= 1:
                    continue
                for dp in range(4):
                    kb.op("dve", lambda e, dp=dp: e.tensor_tensor_scan(CS[:, dp, :], ones[:, :], lw[:, dp, :], 0.0, ALU.mult, ALU.add),
                          reads=[ones, lw], writes=[CS])
                kb.op("dve", lambda e: e.tensor_copy(TOT[:, :], CS[:, :, CH - 1]), reads=[CS], writes=[TOT])
                kb.op("dve", lambda e: e.tensor_tensor(CS[:, 2:4, :], lw[:, 2:4, :], CS[:, 2:4, :], ALU.subtract), reads=[lw, CS], writes=[CS])
                kb.op("dve", lambda e: e.tensor_tensor(CS[:, 2:4, :], CS[:, 2:4, :], TOT[:, 2:4].unsqueeze(2).broadcast_to([128, 2, CH]), ALU.add),
                      reads=[CS, TOT], writes=[CS])
                kb.op("act", lambda e: e.activation(Epos[:], CS[:], AF.Exp), reads=[CS], writes=[Epos])
                kb.op("act", lambda e: e.activation(Eneg[:], CS[:], AF.Exp, scale=-1.0), reads=[CS], writes=[Eneg])
                kb.op("pool", lambda e: e.tensor_tensor(TMP[:], CS[:], lw[:], ALU.subtract), reads=[CS, lw], writes=[TMP])
                kb.op("act", lambda e: e.activation(Eprev[:], TMP[:], AF.Exp), reads=[TMP], writes=[Eprev])
                kb.op("dve", lambda e: e.tensor_tensor(Etot[:], TOT[:, :].unsqueeze(2).broadcast_to([128, 4, CH]), CS[:], ALU.subtract),
                      reads=[TOT, CS], writes=[Etot])
                kb.op("act", lambda e: e.activation(Etot[:], Etot[:], AF.Exp), reads=[Etot], writes=[Etot])
                kb.op("act", lambda e: e.activation(Wtot[:], TOT[:], AF.Exp), reads=[TOT], writes=[Wtot])
                kb.op("dve", lambda e: e.tensor_tensor(Ab[:], al[:], Eprev[:], ALU.mult), reads=[al, Eprev], writes=[Ab])
                kb.op("pool", lambda e: e.tensor_tensor(Bb[:], be[:], Eneg[:], ALU.mult), reads=[be, Eneg], writes=[Bb])
                kb.op("dve", lambda e: e.tensor_tensor(Kb[:], kd[:], Eneg[:], ALU.mult), reads=[kd, Eneg], writes=[Kb])
                kb.op("pool", lambda e: e.tensor_tensor(Rb[:], rt[:], Epos[:], ALU.mult), reads=[rt, Epos], writes=[Rb])
                kb.op("dve", lambda e: e.tensor_tensor(Bt[:], be[:], Etot[:], ALU.mult), reads=[be, Etot], writes=[Bt])
                kb.op("pool", lambda e: e.tensor_tensor(Kt[:], kd[:], Etot[:], ALU.mult), reads=[kd, Etot], writes=[Kt])
                if RW_STAGE <= 2:
                    continue
                PA, PB, PC, PT1, PD, PX, PPQ, PE_ = PS
                mm_all(PA, lambda dp: (dp * 128, dp * 128 + 64), lambda dp, r: Bb[r, dp, :], lambda dp, r: Ab[r, dp, :], [Bb, Ab])
                mm_all(PA, lambda dp: (dp * 128 + 64, dp * 128 + 128), lambda dp, r: Bb[r, dp, :], lambda dp, r: Rb[r, dp, :], [Bb, Rb])
                mm_all(PB, lambda dp: (dp * 128, dp * 128 + 64), lambda dp, r: Kb[r, dp, :], lambda dp, r: Ab[r, dp, :], [Kb, Ab])
                mm_all(PB, lambda dp: (dp * 128 + 64, dp * 128 + 128), lambda dp, r: Kb[r, dp, :], lambda dp, r: Rb[r, dp, :], [Kb, Rb])
                mm_all(PC, lambda dp: (dp * 64, dp * 64 + 64), lambda dp, r: Ab[r, dp, :], lambda dp, r: Bb[r, dp, :], [Ab, Bb])
                q0, p0 = Q[0], P[0]
                pav = PA[:, :].rearrange("p (dp x) -> p dp x", dp=4)
                pbv = PB[:, :].rearrange("p (dp x) -> p dp x", dp=4)
                def mk(m):
                    return m[:, :, :].unsqueeze(2).broadcast_to([128, 2, 2, 64])
                def v4(ap):
                    return ap.rearrange("p (d pr) x -> p d pr x", d=2)
                kb.op("dve", lambda e: e.tensor_tensor(v4(q0[:]), v4(pav[:, :, 0:64]), mk(MTs), ALU.mult), reads=[PA, MTs], writes=[q0])
                kb.op("dve", lambda e: e.tensor_tensor(v4(ArbT[:]), v4(pav[:, :, 64:128]), mk(MTi), ALU.mult), reads=[PA, MTi], writes=[ArbT])
                kb.op("dve", lambda e: e.tensor_tensor(v4(AkvT[:]), v4(pbv[:, :, 0:64]), mk(MTs), ALU.mult), reads=[PB, MTs], writes=[AkvT])
                kb.op("dve", lambda e: e.tensor_tensor(v4(ArkT[:]), v4(pbv[:, :, 64:128]), mk(MTi), ALU.mult), reads=[PB, MTi], writes=[ArkT])
                kb.op("dve", lambda e: e.tensor_tensor(v4(p0[:]), v4(PC[:, 0:256].rearrange("p (dp x) -> p dp x", dp=4)), mk(Ms), ALU.mult),
                      reads=[PC, Ms], writes=[p0])
                if RW_STAGE <= 3:
                    continue
                def idb(r):
                    return ident_f[r, r.start:r.start + 64]
                mm_all(PT1, lambda dp: (dp * 128, dp * 128 + 64), lambda dp, r: Ab[r, dp, :], lambda dp, r: idb(r), [Ab, ident_f])
                mm_all(PT1, lambda dp: (dp * 128 + 64, dp * 128 + 128), lambda dp, r: Bt[r, dp, :], lambda dp, r: idb(r), [Bt, ident_f])
                mm_all(PC, lambda dp: (256 + dp * 64, 256 + dp * 64 + 64), lambda dp, r: Kt[r, dp, :], lambda dp, r: idb(r), [Kt, ident_f])
                x0 = X[0]
                pt1v = PT1[:, :].rearrange("p (dp x) -> p dp x", dp=4)
                kb.op("act", lambda e: e.copy(x0[:, :, 0:64], pt1v[:, :, 0:64]), reads=[PT1], writes=[x0])
                kb.op("act", lambda e: e.copy(Btok[:], pt1v[:, :, 64:128]), reads=[PT1], writes=[Btok])
                kb.op("act", lambda e: e.copy(Ktok[:], PC[:, 256:512].rearrange("p (dp x) -> p dp x", dp=4)), reads=[PC], writes=[Ktok])
                if RW_STAGE <= 4:
                    continue
                mm_all(PD, lambda dp: (dp * 64, dp * 64 + 64), lambda dp, r: AkvT[r, dp, :], lambda dp, r: vt[r, dp, :], [AkvT, vt])
                kb.op("act", lambda e: e.copy(x0[:, :, 64:128], PD[:, 0:256].rearrange("p (dp x) -> p dp x", dp=4)), reads=[PD], writes=[x0])
                if RW_STAGE <= 5:
                    continue
                qc, pc, xc = Q[0], P[0], X[0]
                for j in range(6):
                    qn, pn, xn = Q[(j + 1) % 2], P[(j + 1) % 2], X[(j + 1) % 2]
                    mm_all(PX, lambda dp: (dp * 128, dp * 128 + 128), lambda dp, r: qc[r, dp, :], lambda dp, r: xc[r, dp, :], [qc, xc])
                    kb.op("dve", lambda e, xn=xn, xc=xc: e.tensor_tensor(xn[:], xc[:], PX[:, :].rearrange("p (dp x) -> p dp x", dp=4), ALU.add),
                          reads=[xc, PX], writes=[xn])
                    if j < 5:
                        mm_all(PPQ, lambda dp: (dp * 64, dp * 64 + 64), lambda dp, r: qc[r, dp, :], lambda dp, r: pc[r, dp, :], [qc, pc])
                        mm_all(PPQ, lambda dp: (256 + dp * 64, 256 + dp * 64 + 64), lambda dp, r: pc[r, dp, :], lambda dp, r: qc[r, dp, :], [qc, pc])
                        kb.op("act", lambda e, pn=pn: e.copy(pn[:], PPQ[:, 0:256].rearrange("p (dp x) -> p dp x", dp=4)), reads=[PPQ], writes=[pn])
                        kb.op("act", lambda e, qn=qn: e.copy(qn[:], PPQ[:, 256:512].rearrange("p (dp x) -> p dp x", dp=4)), reads=[PPQ], writes=[qn])
                    qc, pc, xc = qn, pn, xn
                if RW_STAGE <= 6:
                    continue
                mm_all(PD, lambda dp: (256 + dp * 64, 256 + dp * 64 + 64), lambda dp, r: xc[r, dp, 0:64], lambda dp, r: ArbT[r, dp, :], [xc, ArbT])
                kb.op("dve", lambda e: e.tensor_tensor(RAT[:], Rb[:], PD[:, 256:512].rearrange("p (dp x) -> p dp x", dp=4), ALU.add),
                      reads=[Rb, PD], writes=[RAT])
                mm_all(PE_, lambda dp: (dp * 64, dp * 64 + 64), lambda dp, r: xc[r, dp, 0:64], lambda dp, r: Btok[r, dp, :], [xc, Btok])
                kb.op("pool", lambda e: e.tensor_tensor(DG[:], id2[:, :].unsqueeze(1).broadcast_to([128, 4, 64]),
                                                        Wtot[:, :].unsqueeze(2).broadcast_to([128, 4, 64]), ALU.mult), reads=[id2, Wtot], writes=[DG])
                kb.op("dve", lambda e: e.tensor_tensor(McT[:], DG[:], PE_[:, 0:256].rearrange("p (dp x) -> p dp x", dp=4), ALU.add),
                      reads=[DG, PE_], writes=[McT])
                for dp in range(4):
                    for hp in range(2):
                        r = HP[hp]
                        c0 = 256 + dp * 64
                        kb.op("pe", lambda e, dp=dp, r=r, c0=c0: e.matmul(PE_[r, c0:c0 + 64], Btok[r, dp, :], xc[r, dp, 64:128], start=True, stop=False),
                              reads=[Btok, xc], writes=[PE_])
                        kb.op("pe", lambda e, dp=dp, r=r, c0=c0: e.matmul(PE_[r, c0:c0 + 64], Ktok[r, dp, :], vt[r, dp, :], start=False, stop=True),
                              reads=[Ktok, vt], writes=[PE_])
                kb.op("act", lambda e: e.copy(NcS[:], PE_[:, 256:512].rearrange("p (dp x) -> p dp x", dp=4)), reads=[PE_], writes=[NcS])
                if RW_STAGE <= 7:
                    continue
                PYs = [PA, PT1]
                for dp in range(4):
                    for hp in range(2):
                        r = HP[hp]
                        PY = PYs[hp]
                        c0 = dp * 64
                        kb.op("pe", lambda e, dp=dp, r=r, c0=c0, PY=PY: e.matmul(PY[0:64, c0:c0 + 64], ST[r, dp, :], RAT[r, dp, :], start=True, stop=False),
                              reads=[ST, RAT], writes=[PY])
                        kb.op("pe", lambda e, dp=dp, r=r, c0=c0, PY=PY: e.matmul(PY[0:64, c0:c0 + 64], xc[r, dp, 64:128], ArbT[r, dp, :], start=False, stop=False),
                              reads=[xc, ArbT], writes=[PY])
                        kb.op("pe", lambda e, dp=dp, r=r, c0=c0, PY=PY: e.matmul(PY[0:64, c0:c0 + 64], vt[r, dp, :], ArkT[r, dp, :], start=False, stop=True),
                              reads=[vt, ArkT], writes=[PY])
                for d in range(2):
                    c0 = cidx[d] * CH
                    yv = ysb[d][:, :].rearrange("v (pr hp t) -> v pr hp t", pr=2, hp=2)
                    for hp in range(2):
                        kb.op("act", lambda e, d=d, hp=hp, yv=yv: e.copy(
                            yv[:, :, hp, :], PYs[hp][0:64, d * 128:(d + 1) * 128].rearrange("v (pr t) -> v pr t", pr=2)), reads=[PYs[hp]], writes=[ysb[d]])
                    dst = RS["YF" if d == 0 else "YB"]
                    kb.dma("sp", dst.t.rearrange("(h v) t -> v h t", v=64)[:, :, c0:c0 + CH],
                           ysb[d][:, :].rearrange("v (h t) -> v h t", h=4), reads=[ysb[d]], writes=[Buf()])
                if RW_STAGE <= 8:
                    continue
                PSS = PB
                mm_all(PSS, lambda dp: (dp * 64, dp * 64 + 64), lambda dp, r: McT[r, dp, :], lambda dp, r: ST[r, dp, :], [McT, ST])
                kb.op("dve", lambda e: e.tensor_tensor(ST[:], NcS[:], PSS[:, 0:256].rearrange("p (dp x) -> p dp x", dp=4), ALU.add),
                      reads=[NcS, PSS], writes=[ST])


    def phase_rwkv_chunked3(l):
        CH = 64
        NCH = T // CH
        with kb.scope():
            def ldc(nm, shape):
                t = kb.sb("k" + nm, shape)
                kb.dma("sp", t[:], CT[nm].t, reads=[CT[nm]], writes=[t])
                return t
            Ms = ldc("rw_ms", [128, 2, 64]); MTs = ldc("rw_mts", [128, 2, 64]); MTi = ldc("rw_mti", [128, 2, 64])
            id2 = ldc("rw_id2", [128, 64])
            ones = kb.sb("rones", [128, 64])
            kb.op("pool", lambda e: e.memset(ones[:], 1.0), writes=[ones])
            ST = kb.sb("cST", [128, 4, 64])
            kb.op("pool", lambda e: e.memset(ST[:], 0.0), writes=[ST])
            names = ("AL", "W", "B", "KD", "RT")
            import types
            def alloc_set(si):
                S = types.SimpleNamespace()
                def t4(nm, n=2, w=64):
                    return [kb.sb(f"{nm}s{si}_{i}", [128, 4, w]) for i in range(n)]
                S.IN = {n: t4("ci" + n, 1)[0] for n in names}
                S.VTK = t4("cVTK", 1)[0]
                S.CS = t4("cCS", 1)[0]; S.TOT = kb.sb(f"cTOT{si}", [128, 4]); S.TMP = t4("cTMP", 1)[0]
                S.Epos = t4("cEp", 1)[0]; S.Eneg = t4("cEn", 1)[0]; S.Eprev = t4("cEv", 1)[0]; S.Etot = t4("cEt", 1)[0]; S.Wtot = kb.sb(f"cWt{si}", [128, 4])
                S.Ab = t4("cAb", 1)[0]; S.Bb = t4("cBb", 1)[0]; S.Kb = t4("cKb", 1)[0]; S.Rb = t4("cRb", 1)[0]; S.Bt = t4("cBt", 1)[0]; S.Kt = t4("cKt", 1)[0]
                S.Q = t4("cQ"); S.P = t4("cP"); S.ArbT = t4("cArbT", 1)[0]; S.AkvT = t4("cAkvT", 1)[0]; S.ArkT = t4("cArkT", 1)[0]
                S.X = t4("cX", 2, 128); S.Btok = t4("cBtok", 1)[0]; S.Ktok = t4("cKtok", 1)[0]
                S.RAT = t4("cRAT", 1)[0]; S.McT = t4("cMcT", 1)[0]; S.NcS = t4("cNcS", 1)[0]; S.DG = t4("cDG", 1)[0]
                S.ysb = [kb.sb(f"cysb{si}_{d}", [64, 256]) for d in range(2)]
                S.banks = PS[4 * si:4 * si + 4]
                return S
            SETS = [alloc_set(0), alloc_set(1)]
            border = [3, 2, 1, 0] + list(range(NCH - 1, 3, -1))
            DP = [(d, pr) for d in range(2) for pr in range(2)]
            HP = [slice(0, 64), slice(64, 128)]

            def mm_all(ps, col_fn, lhs_fn, rhs_fn, reads, start=True, stop=True, w=None):
                for dp in range(4):
                    for hp in range(2):
                        r = HP[hp]
                        c0, c1 = col_fn(dp)
                        kb.op("pe", lambda e, dp=dp, r=r, c0=c0, c1=c1: e.matmul(ps[r, c0:c1], lhs_fn(dp, r), rhs_fn(dp, r), start=start, stop=stop),
                              reads=reads, writes=[ps])

            def chunk_gen(ci, S):
                cidx = [ci, border[ci]]
                IN, VTK = S.IN, S.VTK
                for d in range(2):
                    c0 = cidx[d] * CH
                    for n in names:
                        src = RS[n if n in ("AL", "RT") else f"{n}{d}"]
                        kb.dma("sp" if d == 0 else "pool", IN[n][:, 2 * d:2 * d + 2, :],
                               src.t.rearrange("(pr q) t -> q pr t", q=128)[:, :, c0:c0 + CH], reads=[src], writes=[IN[n]])
                    for hp in range(2):
                        kb.dma("sp" if d == 0 else "pool", VTK[HP[hp], 2 * d:2 * d + 2, :],
                               RS["VTOK"][c0:c0 + CH, :].rearrange("t (pr hp v) -> t pr hp v", pr=2, hp=2)[:, :, hp, :],
                               reads=[RS["VTOK"]], writes=[VTK])
                al, lw, be, kd, rt, vt = IN["AL"], IN["W"], IN["B"], IN["KD"], IN["RT"], VTK
                CS, TOT, TMP, Epos, Eneg, Eprev, Etot, Wtot = S.CS, S.TOT, S.TMP, S.Epos, S.Eneg, S.Eprev, S.Etot, S.Wtot
                Ab, Bb, Kb, Rb, Bt, Kt, Q, P, ArbT, AkvT, ArkT = S.Ab, S.Bb, S.Kb, S.Rb, S.Bt, S.Kt, S.Q, S.P, S.ArbT, S.AkvT, S.ArkT
                X, Btok, Ktok, RAT, McT, NcS, DG, ysb = S.X, S.Btok, S.Ktok, S.RAT, S.McT, S.NcS, S.DG, S.ysb
                yield
                for dp in range(4):
                    kb.op("dve", lambda e, dp=dp: e.tensor_tensor_scan(CS[:, dp, :], ones[:, :], lw[:, dp, :], 0.0, ALU.mult, ALU.add),
                          reads=[ones, lw], writes=[CS])
                kb.op("dve", lambda e: e.tensor_copy(TOT[:, :], CS[:, :, CH - 1]), reads=[CS], writes=[TOT])
                kb.op("dve", lambda e: e.tensor_tensor(CS[:, 2:4, :], lw[:, 2:4, :], CS[:, 2:4, :], ALU.subtract), reads=[lw, CS], writes=[CS])
                kb.op("dve", lambda e: e.tensor_tensor(CS[:, 2:4, :], CS[:, 2:4, :], TOT[:, 2:4].unsqueeze(2).broadcast_to([128, 2, CH]), ALU.add),
                      reads=[CS, TOT], writes=[CS])
                kb.op("act", lambda e: e.activation(Epos[:], CS[:], AF.Exp), reads=[CS], writes=[Epos])
                kb.op("act", lambda e: e.activation(Eneg[:], CS[:], AF.Exp, scale=-1.0), reads=[CS], writes=[Eneg])
                kb.op("pool", lambda e: e.tensor_tensor(TMP[:], CS[:], lw[:], ALU.subtract), reads=[CS, lw], writes=[TMP])
                kb.op("act", lambda e: e.activation(Eprev[:], TMP[:], AF.Exp), reads=[TMP], writes=[Eprev])
                kb.op("dve", lambda e: e.tensor_tensor(Etot[:], TOT[:, :].unsqueeze(2).broadcast_to([128, 4, CH]), CS[:], ALU.subtract),
                      reads=[TOT, CS], writes=[Etot])
                kb.op("act", lambda e: e.activation(Etot[:], Etot[:], AF.Exp), reads=[Etot], writes=[Etot])
                kb.op("act", lambda e: e.activation(Wtot[:], TOT[:], AF.Exp), reads=[TOT], writes=[Wtot])
                kb.op("dve", lambda e: e.tensor_tensor(Ab[:], al[:], Eprev[:], ALU.mult), reads=[al, Eprev], writes=[Ab])
                kb.op("pool", lambda e: e.tensor_tensor(Bb[:], be[:], Eneg[:], ALU.mult), reads=[be, Eneg], writes=[Bb])
                kb.op("dve", lambda e: e.tensor_tensor(Kb[:], kd[:], Eneg[:], ALU.mult), reads=[kd, Eneg], writes=[Kb])
                kb.op("pool", lambda e: e.tensor_tensor(Rb[:], rt[:], Epos[:], ALU.mult), reads=[rt, Epos], writes=[Rb])
                kb.op("dve", lambda e: e.tensor_tensor(Bt[:], be[:], Etot[:], ALU.mult), reads=[be, Etot], writes=[Bt])
                kb.op("pool", lambda e: e.tensor_tensor(Kt[:], kd[:], Etot[:], ALU.mult), reads=[kd, Etot], writes=[Kt])
                yield
                PA, PB, PC, PT1 = S.banks
                PD, PX, PPQ, PE_ = PA, PB, PC, PT1
                mm_all(PA, lambda dp: (dp * 128, dp * 128 + 64), lambda dp, r: Bb[r, dp, :], lambda dp, r: Ab[r, dp, :], [Bb, Ab])
                mm_all(PA, lambda dp: (dp * 128 + 64, dp * 128 + 128), lambda dp, r: Bb[r, dp, :], lambda dp, r: Rb[r, dp, :], [Bb, Rb])
                mm_all(PB, lambda dp: (dp * 128, dp * 128 + 64), lambda dp, r: Kb[r, dp, :], lambda dp, r: Ab[r, dp, :], [Kb, Ab])
                mm_all(PB, lambda dp: (dp * 128 + 64, dp * 128 + 128), lambda dp, r: Kb[r, dp, :], lambda dp, r: Rb[r, dp, :], [Kb, Rb])
                mm_all(PC, lambda dp: (dp * 64, dp * 64 + 64), lambda dp, r: Ab[r, dp, :], lambda dp, r: Bb[r, dp, :], [Ab, Bb])
                q0, p0 = Q[0], P[0]
                pav = PA[:, :].rearrange("p (dp x) -> p dp x", dp=4)
                pbv = PB[:, :].rearrange("p (dp x) -> p dp x", dp=4)
                def mk(m):
                    return m[:, :, :].unsqueeze(2).broadcast_to([128, 2, 2, 64])
                def v4(ap):
                    return ap.rearrange("p (d pr) x -> p d pr x", d=2)
                kb.op("dve", lambda e: e.tensor_tensor(v4(q0[:]), v4(pav[:, :, 0:64]), mk(MTs), ALU.mult), reads=[PA, MTs], writes=[q0])
                kb.op("dve", lambda e: e.tensor_tensor(v4(ArbT[:]), v4(pav[:, :, 64:128]), mk(MTi), ALU.mult), reads=[PA, MTi], writes=[ArbT])
                kb.op("dve", lambda e: e.tensor_tensor(v4(AkvT[:]), v4(pbv[:, :, 0:64]), mk(MTs), ALU.mult), reads=[PB, MTs], writes=[AkvT])
                kb.op("dve", lambda e: e.tensor_tensor(v4(ArkT[:]), v4(pbv[:, :, 64:128]), mk(MTi), ALU.mult), reads=[PB, MTi], writes=[ArkT])
                kb.op("dve", lambda e: e.tensor_tensor(v4(p0[:]), v4(PC[:, 0:256].rearrange("p (dp x) -> p dp x", dp=4)), mk(Ms), ALU.mult),
                      reads=[PC, Ms], writes=[p0])
                yield
                def idb(r):
                    return ident_f[r, r.start:r.start + 64]
                mm_all(PT1, lambda dp: (dp * 128, dp * 128 + 64), lambda dp, r: Ab[r, dp, :], lambda dp, r: idb(r), [Ab, ident_f])
                mm_all(PT1, lambda dp: (dp * 128 + 64, dp * 128 + 128), lambda dp, r: Bt[r, dp, :], lambda dp, r: idb(r), [Bt, ident_f])
                mm_all(PC, lambda dp: (256 + dp * 64, 256 + dp * 64 + 64), lambda dp, r: Kt[r, dp, :], lambda dp, r: idb(r), [Kt, ident_f])
                x0 = X[0]
                pt1v = PT1[:, :].rearrange("p (dp x) -> p dp x", dp=4)
                kb.op("act", lambda e: e.copy(x0[:, :, 0:64], pt1v[:, :, 0:64]), reads=[PT1], writes=[x0])
                kb.op("act", lambda e: e.copy(Btok[:], pt1v[:, :, 64:128]), reads=[PT1], writes=[Btok])
                kb.op("act", lambda e: e.copy(Ktok[:], PC[:, 256:512].rearrange("p (dp x) -> p dp x", dp=4)), reads=[PC], writes=[Ktok])
                yield
                mm_all(PD, lambda dp: (dp * 64, dp * 64 + 64), lambda dp, r: AkvT[r, dp, :], lambda dp, r: vt[r, dp, :], [AkvT, vt])
                kb.op("act", lambda e: e.copy(x0[:, :, 64:128], PD[:, 0:256].rearrange("p (dp x) -> p dp x", dp=4)), reads=[PD], writes=[x0])
                yield
                qc, pc, xc = Q[0], P[0], X[0]
                for j in range(6):
                    qn, pn, xn = Q[(j + 1) % 2], P[(j + 1) % 2], X[(j + 1) % 2]
                    mm_all(PX, lambda dp: (dp * 128, dp * 128 + 128), lambda dp, r: qc[r, dp, :], lambda dp, r: xc[r, dp, :], [qc, xc])
                    kb.op("dve", lambda e, xn=xn, xc=xc: e.tensor_tensor(xn[:], xc[:], PX[:, :].rearrange("p (dp x) -> p dp x", dp=4), ALU.add),
                          reads=[xc, PX], writes=[xn])
                    if j < 5:
                        mm_all(PPQ, lambda dp: (dp * 64, dp * 64 + 64), lambda dp, r: qc[r, dp, :], lambda dp, r: pc[r, dp, :], [qc, pc])
                        mm_all(PPQ, lambda dp: (256 + dp * 64, 256 + dp * 64 + 64), lambda dp, r: pc[r, dp, :], lambda dp, r: qc[r, dp, :], [qc, pc])
                        kb.op("act", lambda e, pn=pn: e.copy(pn[:], PPQ[:, 0:256].rearrange("p (dp x) -> p dp x", dp=4)), reads=[PPQ], writes=[pn])
                        kb.op("act", lambda e, qn=qn: e.copy(qn[:], PPQ[:, 256:512].rearrange("p (dp x) -> p dp x", dp=4)), reads=[PPQ], writes=[qn])
                    qc, pc, xc = qn, pn, xn
                    yield
                yield
                mm_all(PD, lambda dp: (256 + dp * 64, 256 + dp * 64 + 64), lambda dp, r: xc[r, dp, 0:64], lambda dp, r: ArbT[r, dp, :], [xc, ArbT])
                kb.op("dve", lambda e: e.tensor_tensor(RAT[:], Rb[:], PD[:, 256:512].rearrange("p (dp x) -> p dp x", dp=4), ALU.add),
                      reads=[Rb, PD], writes=[RAT])
                mm_all(PE_, lambda dp: (dp * 64, dp * 64 + 64), lambda dp, r: xc[r, dp, 0:64], lambda dp, r: Btok[r, dp, :], [xc, Btok])
                kb.op("pool", lambda e: e.tensor_tensor(DG[:], id2[:, :].unsqueeze(1).broadcast_to([128, 4, 64]),
                                                        Wtot[:, :].unsqueeze(2).broadcast_to([128, 4, 64]), ALU.mult), reads=[id2, Wtot], writes=[DG])
                kb.op("dve", lambda e: e.tensor_tensor(McT[:], DG[:], PE_[:, 0:256].rearrange("p (dp x) -> p dp x", dp=4), ALU.add),
                      reads=[DG, PE_], writes=[McT])
                for dp in range(4):
                    for hp in range(2):
                        r = HP[hp]
                        c0 = 256 + dp * 64
                        kb.op("pe", lambda e, dp=dp, r=r, c0=c0: e.matmul(PE_[r, c0:c0 + 64], Btok[r, dp, :], xc[r, dp, 64:128], start=True, stop=False),
                              reads=[Btok, xc], writes=[PE_])
                        kb.op("pe", lambda e, dp=dp, r=r, c0=c0: e.matmul(PE_[r, c0:c0 + 64], Ktok[r, dp, :], vt[r, dp, :], start=False, stop=True),
                              reads=[Ktok, vt], writes=[PE_])
                kb.op("act", lambda e: e.copy(NcS[:], PE_[:, 256:512].rearrange("p (dp x) -> p dp x", dp=4)), reads=[PE_], writes=[NcS])
                yield
                PYs = [PA, PB]
                for dp in range(4):
                    for hp in range(2):
                        r = HP[hp]
                        PY = PYs[hp]
                        c0 = dp * 64
                        kb.op("pe", lambda e, dp=dp, r=r, c0=c0, PY=PY: e.matmul(PY[0:64, c0:c0 + 64], ST[r, dp, :], RAT[r, dp, :], start=True, stop=False),
                              reads=[ST, RAT], writes=[PY])
                        kb.op("pe", lambda e, dp=dp, r=r, c0=c0, PY=PY: e.matmul(PY[0:64, c0:c0 + 64], xc[r, dp, 64:128], ArbT[r, dp, :], start=False, stop=False),
                              reads=[xc, ArbT], writes=[PY])
                        kb.op("pe", lambda e, dp=dp, r=r, c0=c0, PY=PY: e.matmul(PY[0:64, c0:c0 + 64], vt[r, dp, :], ArkT[r, dp, :], start=False, stop=True),
                              reads=[vt, ArkT], writes=[PY])
                for d in range(2):
                    c0 = cidx[d] * CH
                    yv = ysb[d][:, :].rearrange("v (pr hp t) -> v pr hp t", pr=2, hp=2)
                    for hp in range(2):
                        kb.op("act", lambda e, d=d, hp=hp, yv=yv: e.copy(
                            yv[:, :, hp, :], PYs[hp][0:64, d * 128:(d + 1) * 128].rearrange("v (pr t) -> v pr t", pr=2)), reads=[PYs[hp]], writes=[ysb[d]])
                    dst = RS["YF" if d == 0 else "YB"]
                    kb.dma("sp", dst.t.rearrange("(h v) t -> v h t", v=64)[:, :, c0:c0 + CH],
                           ysb[d][:, :].rearrange("v (h t) -> v h t", h=4), reads=[ysb[d]], writes=[Buf()])
                PSS = PT1
                mm_all(PSS, lambda dp: (dp * 64, dp * 64 + 64), lambda dp, r: McT[r, dp, :], lambda dp, r: ST[r, dp, :], [McT, ST])
                kb.op("dve", lambda e: e.tensor_tensor(ST[:], NcS[:], PSS[:, 0:256].rearrange("p (dp x) -> p dp x", dp=4), ALU.add),
                      reads=[NcS, PSS], writes=[ST])


            def lockstep(gens):
                gens = list(gens)
                while gens:
                    nxt = []
                    for g_ in gens:
                        try:
                            next(g_)
                            nxt.append(g_)
                        except StopIteration:
                            pass
                    gens = nxt
            for ci in range(0, NCH, 2):
                lockstep([chunk_gen(ci, SETS[0]), chunk_gen(ci + 1, SETS[1])])

    def phase_rwkv_chunked2(l):
        CH = 64
        NCH = T // CH
        with kb.scope():
            def ldc(nm, shape):
                t = kb.sb("k" + nm, shape)
                kb.dma("sp", t[:], CT[nm].t, reads=[CT[nm]], writes=[t])
                return t
            MsB = ldc("rw_msb", [128, 2, 128]); MTsB = ldc("rw_mtsb", [128, 2, 128]); MTi = ldc("rw_mti", [128, 2, 64])
            identr = kb.sb("cidr", [128, 128], F32R)
            kb.op("dve", lambda e: e.tensor_copy(identr[:], ident_f[:]), reads=[ident_f], writes=[identr])
            ones = kb.sb("rones", [128, 64])
            kb.op("pool", lambda e: e.memset(ones[:], 1.0), writes=[ones])
            def bd(nm, n=1, dt=F32R):
                ts = [kb.sb(f"{nm}{i}", [128, 4, 128], dt) for i in range(n)]
                for t in ts:
                    kb.op("pool", lambda e, t=t: e.memset(t[:].bitcast(F32) if dt == F32R else t[:], 0.0), writes=[t])
                return ts
            def t4(nm, n=1, w=64, dt=F32):
                return [kb.sb(f"{nm}{i}", [128, 4, w], dt) for i in range(n)]
            f32 = lambda ap: ap.bitcast(F32)
            names = ("AL", "W", "B", "KD", "RT")
            IN = {n: t4("di" + n, 2) for n in names}
            VT = bd("dVT", 2, F32)
            VTr = bd("dVTr")[0]
            ST = bd("dST")[0]
            CS = t4("dCS")[0]; TOT = kb.sb("dTOT", [128, 4]); TMP = t4("dTMP")[0]
            Epos = t4("dEp")[0]; Eneg = t4("dEn")[0]; Eprev = t4("dEv")[0]; Etot = t4("dEt")[0]; Wtot = kb.sb("dWt", [128, 4])
            Ab = bd("dAb")[0]; Bb = bd("dBb")[0]; Kb = bd("dKb")[0]; Bt = bd("dBt")[0]; Kt = bd("dKt")[0]
            Rb = t4("dRb", 1, 64, F32R)[0]
            Q = bd("dQ", 2); P = bd("dP", 2); AkvT = bd("dAkvT")[0]
            ArbT = t4("dArbT", 1, 64, F32R)[0]; ArkT = t4("dArkT", 1, 64, F32R)[0]; RAT = t4("dRAT", 1, 64, F32R)[0]
            X = [kb.sb(f"dX{i}", [128, 4, 256], F32R) for i in range(2)]
            Btok = bd("dBtok")[0]; Ktok = bd("dKtok")[0]; McT = bd("dMcT")[0]
            NcS = bd("dNcS", 1, F32)[0]; DG = bd("dDG", 1, F32)[0]
            ysb = [kb.sb(f"dysb{d}", [128, 2, 64]) for d in range(2)]
            border = [3, 2, 1, 0] + list(range(NCH - 1, 3, -1))
            H0, H1 = slice(0, 64), slice(64, 128)
            B0, B1, B2, B3, B4, B5, B6, B7 = PS

            def mm4(ps, c0, w, lhs, rhs, reads, start=True, stop=True):
                for dp in range(4):
                    kb.op("pe", lambda e, dp=dp: e.matmul(ps[:, c0 + dp * w:c0 + (dp + 1) * w], lhs(dp), rhs(dp), start=start, stop=stop),
                          reads=reads, writes=[ps])

            def v4(ap):
                return ap.rearrange("p (d pr) x -> p d pr x", d=2)

            def mk(m, w):
                return m[:, :, :].unsqueeze(2).broadcast_to([128, 2, 2, w])

            def pv(ps, c0, w):
                return ps[:, c0:c0 + 4 * w].rearrange("p (dp x) -> p dp x", dp=4)

            for ci in range(NCH):
                cidx = [ci, border[ci]]
                i2 = ci % 2
                vt = VT[i2]
                for d in range(2):
                    c0 = cidx[d] * CH
                    q_ = "sp" if d == 0 else "pool"
                    for n in names:
                        src = RS[n if n in ("AL", "RT") else f"{n}{d}"]
                        kb.dma(q_, IN[n][i2][:, 2 * d:2 * d + 2, :],
                               src.t.rearrange("(pr q) t -> q pr t", q=128)[:, :, c0:c0 + CH], reads=[src], writes=[IN[n][i2]])
                    for hp in range(2):
                        kb.dma(q_, vt[hp * 64:(hp + 1) * 64, 2 * d:2 * d + 2, hp * 64:(hp + 1) * 64],
                               RS["VTOK"][c0:c0 + CH, :].rearrange("t (pr hp v) -> t pr hp v", pr=2, hp=2)[:, :, hp, :],
                               reads=[RS["VTOK"]], writes=[vt])
                al, lw, be, kd, rt = IN["AL"][i2], IN["W"][i2], IN["B"][i2], IN["KD"][i2], IN["RT"][i2]
                kb.op("act", lambda e: e.copy(VTr[:], vt[:]), reads=[vt], writes=[VTr])
                for dp in range(4):
                    kb.op("dve", lambda e, dp=dp: e.tensor_tensor_scan(CS[:, dp, :], ones[:, :], lw[:, dp, :], 0.0, ALU.mult, ALU.add),
                          reads=[ones, lw], writes=[CS])
                kb.op("dve", lambda e: e.tensor_copy(TOT[:, :], CS[:, :, CH - 1]), reads=[CS], writes=[TOT])
                kb.op("dve", lambda e: e.tensor_tensor(CS[:, 2:4, :], lw[:, 2:4, :], CS[:, 2:4, :], ALU.subtract), reads=[lw, CS], writes=[CS])
                kb.op("dve", lambda e: e.tensor_tensor(CS[:, 2:4, :], CS[:, 2:4, :], TOT[:, 2:4].unsqueeze(2).broadcast_to([128, 2, CH]), ALU.add),
                      reads=[CS, TOT], writes=[CS])
                kb.op("act", lambda e: e.activation(Epos[:], CS[:], AF.Exp), reads=[CS], writes=[Epos])
                kb.op("act", lambda e: e.activation(Eneg[:], CS[:], AF.Exp, scale=-1.0), reads=[CS], writes=[Eneg])
                kb.op("pool", lambda e: e.tensor_tensor(TMP[:], CS[:], lw[:], ALU.subtract), reads=[CS, lw], writes=[TMP])
                kb.op("act", lambda e: e.activation(Eprev[:], TMP[:], AF.Exp), reads=[TMP], writes=[Eprev])
                kb.op("pool", lambda e: e.tensor_tensor(Etot[:], TOT[:, :].unsqueeze(2).broadcast_to([128, 4, CH]), CS[:], ALU.subtract),
                      reads=[TOT, CS], writes=[Etot])
                kb.op("act", lambda e: e.activation(Etot[:], Etot[:], AF.Exp), reads=[Etot], writes=[Etot])
                kb.op("act", lambda e: e.activation(Wtot[:], TOT[:], AF.Exp), reads=[TOT], writes=[Wtot])
                for k_, (dst, a_, b_) in enumerate(((Ab, al, Eprev), (Bb, be, Eneg), (Kb, kd, Eneg), (Bt, be, Etot), (Kt, kd, Etot))):
                    for hi, r in enumerate((H0, H1)):
                        eng = "dve" if (k_ + hi) % 2 == 0 else "pool"
                        kb.op(eng, lambda e, dst=dst, a_=a_, b_=b_, r=r: e.tensor_tensor(dst[r, :, r.start:r.start + 64], a_[r, :, :], b_[r, :, :], ALU.mult),
                              reads=[a_, b_], writes=[dst])
                kb.op("pool", lambda e: e.tensor_tensor(Rb[:], rt[:], Epos[:], ALU.mult), reads=[rt, Epos], writes=[Rb])
                mm4(B0, 0, 128, lambda dp: Bb[:, dp, :], lambda dp: Ab[:, dp, :], [Bb, Ab])
                mm4(B1, 0, 128, lambda dp: Kb[:, dp, :], lambda dp: Ab[:, dp, :], [Kb, Ab])
                mm4(B2, 0, 128, lambda dp: Ab[:, dp, :], lambda dp: Bb[:, dp, :], [Ab, Bb])
                mm4(B3, 0, 64, lambda dp: Bb[:, dp, :], lambda dp: Rb[:, dp, :], [Bb, Rb])
                mm4(B3, 256, 64, lambda dp: Kb[:, dp, :], lambda dp: Rb[:, dp, :], [Kb, Rb])
                q0, p0, x0 = Q[0], P[0], X[0]
                kb.op("dve", lambda e: e.tensor_tensor(v4(q0[:]), v4(pv(B0, 0, 128)), mk(MTsB, 128), ALU.mult), reads=[B0, MTsB], writes=[q0])
                kb.op("dve", lambda e: e.tensor_tensor(v4(AkvT[:]), v4(pv(B1, 0, 128)), mk(MTsB, 128), ALU.mult), reads=[B1, MTsB], writes=[AkvT])
                kb.op("dve", lambda e: e.tensor_tensor(v4(p0[:]), v4(pv(B2, 0, 128)), mk(MsB, 128), ALU.mult), reads=[B2, MsB], writes=[p0])
                kb.op("dve", lambda e: e.tensor_tensor(v4(ArbT[:]), v4(pv(B3, 0, 64)), mk(MTi, 64), ALU.mult), reads=[B3, MTi], writes=[ArbT])
                kb.op("dve", lambda e: e.tensor_tensor(v4(ArkT[:]), v4(pv(B3, 256, 64)), mk(MTi, 64), ALU.mult), reads=[B3, MTi], writes=[ArkT])
                mm4(B4, 0, 128, lambda dp: Ab[:, dp, :], lambda dp: identr[:, :], [Ab, identr])
                mm4(B6, 0, 128, lambda dp: Bt[:, dp, :], lambda dp: identr[:, :], [Bt, identr])
                mm4(B7, 0, 128, lambda dp: Kt[:, dp, :], lambda dp: identr[:, :], [Kt, identr])
                mm4(B5, 0, 128, lambda dp: AkvT[:, dp, :], lambda dp: VTr[:, dp, :], [AkvT, VTr])
                kb.op("act", lambda e: e.copy(x0[:, :, 0:128], pv(B4, 0, 128)), reads=[B4], writes=[x0])
                kb.op("act", lambda e: e.copy(Btok[:], pv(B6, 0, 128)), reads=[B6], writes=[Btok])
                kb.op("act", lambda e: e.copy(Ktok[:], pv(B7, 0, 128)), reads=[B7], writes=[Ktok])
                kb.op("act", lambda e: e.copy(x0[:, :, 128:256], pv(B5, 0, 128)), reads=[B5], writes=[x0])
                qc, pc, xc = Q[0], P[0], X[0]
                for j in range(6):
                    qn, pn, xn = Q[(j + 1) % 2], P[(j + 1) % 2], X[(j + 1) % 2]
                    for hf, bank in ((0, B4), (1, B5)):
                        for dq in range(2):
                            dp = hf * 2 + dq
                            kb.op("pe", lambda e, dp=dp, dq=dq, bank=bank: e.matmul(bank[:, dq * 256:(dq + 1) * 256], qc[:, dp, :], xc[:, dp, :],
                                                                                    start=True, stop=True), reads=[qc, xc], writes=[bank])
                        kb.op("dve", lambda e, hf=hf, bank=bank, xn=xn, xc=xc: e.tensor_tensor(
                            xn[:, 2 * hf:2 * hf + 2, :], f32(xc[:, 2 * hf:2 * hf + 2, :]), bank[:, :].rearrange("p (dq x) -> p dq x", dq=2), ALU.add),
                            reads=[xc, bank], writes=[xn])
                    if j < 5:
                        mm4(B6, 0, 128, lambda dp: qc[:, dp, :], lambda dp: pc[:, dp, :], [qc, pc])
                        mm4(B7, 0, 128, lambda dp: pc[:, dp, :], lambda dp: qc[:, dp, :], [qc, pc])
                        kb.op("act", lambda e, pn=pn: e.copy(pn[:], pv(B6, 0, 128)), reads=[B6], writes=[pn])
                        kb.op("act", lambda e, qn=qn: e.copy(qn[:], pv(B7, 0, 128)), reads=[B7], writes=[qn])
                    qc, pc, xc = qn, pn, xn
                mm4(B3, 0, 64, lambda dp: xc[:, dp, 0:128], lambda dp: ArbT[:, dp, :], [xc, ArbT])
                kb.op("dve", lambda e: e.tensor_tensor(RAT[:], f32(Rb[:]), pv(B3, 0, 64), ALU.add), reads=[Rb, B3], writes=[RAT])
                mm4(B2, 0, 128, lambda dp: xc[:, dp, 0:128], lambda dp: Btok[:, dp, :], [xc, Btok])
                kb.op("pool", lambda e: e.tensor_tensor(DG[:], ident_f[:, :].unsqueeze(1).broadcast_to([128, 4, 128]),
                                                        Wtot[:, :].unsqueeze(2).broadcast_to([128, 4, 128]), ALU.mult), reads=[ident_f, Wtot], writes=[DG])
                kb.op("dve", lambda e: e.tensor_tensor(McT[:], DG[:], pv(B2, 0, 128), ALU.add), reads=[DG, B2], writes=[McT])
                for dp in range(4):
                    kb.op("pe", lambda e, dp=dp: e.matmul(B0[:, dp * 128:(dp + 1) * 128], Btok[:, dp, :], xc[:, dp, 128:256], start=True, stop=False),
                          reads=[Btok, xc], writes=[B0])
                    kb.op("pe", lambda e, dp=dp: e.matmul(B0[:, dp * 128:(dp + 1) * 128], Ktok[:, dp, :], VTr[:, dp, :], start=False, stop=True),
                          reads=[Ktok, VTr], writes=[B0])
                kb.op("act", lambda e: e.copy(NcS[:], pv(B0, 0, 128)), reads=[B0], writes=[NcS])
                for dp in range(4):
                    c0 = dp * 64
                    kb.op("pe", lambda e, dp=dp, c0=c0: e.matmul(B1[:, c0:c0 + 64], ST[:, dp, :], RAT[:, dp, :], start=True, stop=False),
                          reads=[ST, RAT], writes=[B1])
                    kb.op("pe", lambda e, dp=dp, c0=c0: e.matmul(B1[:, c0:c0 + 64], xc[:, dp, 128:256], ArbT[:, dp, :], start=False, stop=False),
                          reads=[xc, ArbT], writes=[B1])
                    kb.op("pe", lambda e, dp=dp, c0=c0: e.matmul(B1[:, c0:c0 + 64], VTr[:, dp, :], ArkT[:, dp, :], start=False, stop=True),
                          reads=[VTr, ArkT], writes=[B1])
                for d in range(2):
                    c0 = cidx[d] * CH
                    kb.op("act", lambda e, d=d: e.copy(ysb[d][:, :, :], B1[:, d * 128:(d + 1) * 128].rearrange("p (pr t) -> p pr t", pr=2)),
                          reads=[B1], writes=[ysb[d]])
                    dst = RS["YF" if d == 0 else "YB"]
                    kb.dma("sp", dst.t.rearrange("(pr q) t -> q pr t", q=128)[:, :, c0:c0 + CH], ysb[d][:, :, :], reads=[ysb[d]], writes=[Buf()])
                mm4(B6, 0, 128, lambda dp: McT[:, dp, :], lambda dp: ST[:, dp, :], [McT, ST])
                kb.op("dve", lambda e: e.tensor_tensor(ST[:], NcS[:], pv(B6, 0, 128), ALU.add), reads=[NcS, B6], writes=[ST])

    def phase_rwkv_out(l, with_ctx):
        with kb.scope():
            rk_ = colvec("rrk", W["rw_r_k"][l, :], W["rw_r_k"], [128, 2], "(j p) -> p j", p=128)
            lg_ = colvec("rlg", W["rw_ln_g"][l, :], W["rw_ln_g"], [128, 2], "(j p) -> p j", p=128)
            lb_ = colvec("rlb", W["rw_ln_b"][l, :], W["rw_ln_b"], [128, 2], "(j p) -> p j", p=128)
            nm = ("YF", "YB", "RT", "KD0", "KD1", "VT")
            tl = [{n: kb.sb(f"o{n}{i}", [128, 512]) for n in nm} for i in range(2)]
            sg = [kb.sb(f"osg{i}", [128, 512], BF16) for i in range(2)]
            ob = [kb.sb(f"oob{i}", [128, 512], BF16) for i in range(2)]
            wk = [[kb.sb(f"owk{k}{i}", [128, 512]) for k in range(3)] for i in range(2)]
            it = 0
            for pr in range(2):
                rows = slice(pr * 128, (pr + 1) * 128)
                for (t0, nt) in TCH:
                    if not with_ctx and t0 + nt <= C:
                        continue
                    t_, s_, o_, (a_, b_, c_) = tl[it % 2], sg[it % 2], ob[it % 2], wk[it % 2]
                    for k, n in enumerate(nm):
                        kb.dma("sp" if k % 2 == 0 else "pool", t_[n][:, 0:nt], RS[n][rows, t0:t0 + nt], reads=[RS[n]], writes=[t_[n]])
                    kb.dma("sp", s_[:, 0:nt], RS["SGT"][rows, t0:t0 + nt], reads=[RS["SGT"]], writes=[s_])
                    y = t_["YF"]
                    kb.op("dve", lambda e: e.tensor_tensor(y[:, 0:nt], y[:, 0:nt], t_["YB"][:, 0:nt], ALU.add), reads=[y, t_["YB"]], writes=[y])
                    p1, p2, p3 = PS[(3 * it) % 8], PS[(3 * it + 1) % 8], PS[(3 * it + 2) % 8]
                    kb.op("pe", lambda e: e.matmul(p1[:, 0:nt], blk64[:], y[:, 0:nt], start=True, stop=True), reads=[blk64, y], writes=[p1])
                    kb.op("dve", lambda e: e.scalar_tensor_tensor(a_[:, 0:nt], p1[:, 0:nt], -1.0 / 64, y[:, 0:nt], ALU.mult, ALU.add),
                          reads=[p1, y], writes=[a_])
                    kb.op("act", lambda e: e.activation(b_[:, 0:nt], a_[:, 0:nt], AF.Square), reads=[a_], writes=[b_])
                    kb.op("pe", lambda e: e.matmul(p2[:, 0:nt], blk64[:], b_[:, 0:nt], start=True, stop=True), reads=[blk64, b_], writes=[p2])
                    kb.op("dve", lambda e: e.tensor_scalar(b_[:, 0:nt], p2[:, 0:nt], 1.0 / 64, 64e-5, ALU.mult, ALU.add), reads=[p2], writes=[b_])
                    kb.op("act", lambda e: e.sqrt(b_[:, 0:nt], b_[:, 0:nt]), reads=[b_], writes=[b_])
                    kb.op("dve", lambda e: e.reciprocal(b_[:, 0:nt], b_[:, 0:nt]), reads=[b_], writes=[b_])
                    kb.op("dve", lambda e: e.tensor_tensor(a_[:, 0:nt], a_[:, 0:nt], b_[:, 0:nt], ALU.mult), reads=[a_, b_], writes=[a_])
                    kb.op("dve", lambda e: e.tensor_scalar(a_[:, 0:nt], a_[:, 0:nt], lg_[:, pr:pr + 1], lb_[:, pr:pr + 1], ALU.mult, ALU.add),
                          reads=[a_, lg_, lb_], writes=[a_])
                    kb.op("pool", lambda e: e.tensor_tensor(c_[:, 0:nt], t_["KD0"][:, 0:nt], t_["KD1"][:, 0:nt], ALU.add),
                          reads=[t_["KD0"], t_["KD1"]], writes=[c_])
                    kb.op("dve", lambda e: e.scalar_tensor_tensor(c_[:, 0:nt], t_["RT"][:, 0:nt], rk_[:, pr:pr + 1], c_[:, 0:nt], ALU.mult, ALU.mult),
                          reads=[t_["RT"], rk_, c_], writes=[c_])
                    kb.op("pe", lambda e: e.matmul(p3[:, 0:nt], blk64[:], c_[:, 0:nt], start=True, stop=True), reads=[blk64, c_], writes=[p3])
                    kb.op("dve", lambda e: e.tensor_tensor(c_[:, 0:nt], p3[:, 0:nt], t_["VT"][:, 0:nt], ALU.mult), reads=[p3, t_["VT"]], writes=[c_])
                    kb.op("dve", lambda e: e.tensor_tensor(a_[:, 0:nt], a_[:, 0:nt], c_[:, 0:nt], ALU.add), reads=[a_, c_], writes=[a_])
                    kb.op("pool", lambda e: e.tensor_tensor(o_[:, 0:nt], a_[:, 0:nt], s_[:, 0:nt], ALU.mult), reads=[a_, s_], writes=[o_])
                    kb.dma("sp", mixT[256 + pr * 128:256 + (pr + 1) * 128, t0:t0 + nt], o_[:, 0:nt], reads=[o_], writes=[Buf()])
                    it += 1


    SEGS = {"L": dict(Ls=L, A=32, cbw=32, off=C, ut="UTL"), "C": dict(Ls=C, A=2, cbw=64, off=0, ut="UTC")}

    def phase_hyena_prep(l, with_ctx):
        with kb.scope():
            stage = kb.sb("hstage", [128, 8, 128])
            wts = [kb.sb(f"hwt{i}", [128, 8, 128], BF16) for i in range(2)]
            cw = kb.sb("hcw", [128, 6, 3])
            for k in range(3):
                kb.dma("sp", cw[:, :, k], W["hy_conv"][l, k, :].rearrange("(j p) -> p j", p=128), reads=[W["hy_conv"]], writes=[cw], slow=True)
            ncw = kb.sb("hncw", [128, 6, 3])
            kb.op("dve", lambda e: e.tensor_scalar(ncw[:], cw[:], -1.0, None, ALU.mult), reads=[cw], writes=[ncw])
            Zraw = kb.sb("hZraw", [128, T + 2])
            Zout = kb.sb("hZout", [128, T])
            kb.op("pool", lambda e: e.memset(Zraw[:, 0:1], 0.0), writes=[Zraw])
            kb.op("pool", lambda e: e.memset(Zraw[:, T + 1:T + 2], 0.0), writes=[Zraw])
            ub = kb.sb("hub", [128, 32 * 128])
            tG = [kb.sb(f"htG{i}", [128, 512], BF16) for i in range(2)]
            for oi, jt in enumerate(range(8)):
                wt = wts[oi % 2]
                c0 = HY0 + jt * 128 if jt < 6 else HYG0 + (jt - 6) * 128
                load_w(l, wt, c0, 128, stage)
                if jt >= 6:
                    for ci, (t0, nt) in enumerate(TCH):
                        p = PS[ci % 4]
                        proj_fm(p, wt, 0, 128, t0, nt)
                        g = tG[ci % 2]
                        kb.op("act", lambda e, p=p, g=g, nt=nt: e.activation(g[:, 0:nt], p[:, 0:nt], AF.Silu), reads=[p], writes=[g])
                        kb.dma("sp", HS["SG"][(jt - 6) * 128:(jt - 5) * 128, t0:t0 + nt], g[:, 0:nt], reads=[g], writes=[Buf()])
                    continue
                conv_tile(l, wt, cw, ncw, jt, Zraw, Zout)
                arr, half = jt // 2, jt % 2
                for sn in (("L", "C") if with_ctx else ("L",)):
                    sg = SEGS[sn]
                    A, cbw, off = sg["A"], sg["cbw"], sg["off"]
                    G = 128 // A
                    ncg = 128 // G
                    ubv = ub[:, 0:A * 128].rearrange("p (g a c) -> p g a c", g=ncg, a=A)
                    for a in range(A):
                        p = PS[4 + (a // 4) % 4]
                        kb.op("pe", lambda e, p=p, a=a, A=A, off=off: e.transpose(
                            p[:, (a % 4) * 128:(a % 4 + 1) * 128], Zout[:, off + a:off + a + 127 * A + 1:A], ident_f[:]),
                            reads=[Zout, ident_f], writes=[p])
                        if a % 4 == 3 or a == A - 1:
                            a0 = (a // 4) * 4
                            na = a - a0 + 1
                            kb.op("act", lambda e, p=p, a0=a0, na=na, G=G: e.copy(
                                ubv[:, :, a0:a0 + na, :], p[:, 0:na * 128].rearrange("p (a g c) -> p g a c", a=na, c=G)), reads=[p], writes=[ub])
                    nb = 128 // cbw
                    bsz = A * cbw
                    for b in range(nb):
                        dst = HS[sg["ut"]][arr, half * nb + b, :, :]
                        kb.dma("sp" if b % 2 == 0 else "pool", dst, ub[:, b * bsz:(b + 1) * bsz], reads=[ub], writes=[Buf()])

    def cmul(dre, dim_, sre, sim, tre, tim, conj, srcb, tabb, dstb, tmp):
        t1, t2 = tmp
        sh = tuple(slice(None) for _ in range(1))
        kb.op("dve", lambda e: e.tensor_tensor(t1, sre, tre, ALU.mult), reads=srcb + tabb, writes=[dstb[2]])
        kb.op("dve", lambda e: e.tensor_tensor(t2, sim, tim, ALU.mult), reads=srcb + tabb, writes=[dstb[3]])
        kb.op("pool", lambda e: e.tensor_tensor(dre, t1, t2, ALU.add if conj else ALU.subtract), reads=[dstb[2], dstb[3]], writes=[dstb[0]])
        kb.op("dve", lambda e: e.tensor_tensor(t1, sim, tre, ALU.mult), reads=srcb + tabb + [dstb[0]], writes=[dstb[2]])
        kb.op("dve", lambda e: e.tensor_tensor(t2, sre, tim, ALU.mult), reads=srcb + tabb + [dstb[0]], writes=[dstb[3]])
        kb.op("pool", lambda e: e.tensor_tensor(dim_, t1, t2, ALU.subtract if conj else ALU.add), reads=[dstb[2], dstb[3]], writes=[dstb[1]])

    def phase_hyena_main(l, with_ctx):
        PI = math.pi
        with kb.scope():
            fw1 = kb.sb("hfw1", [33, 64])
            fw2 = kb.sb("hfw2", [64, 64])
            fw3 = kb.sb("hfw3", [64, 1024])
            kb.dma("sp", fw1[:], W["hy_fw1"][l, :, :], reads=[W["hy_fw1"]], writes=[fw1])
            kb.dma("sp", fw2[:], W["hy_fw2"][l, :, :], reads=[W["hy_fw2"]], writes=[fw2])
            kb.dma("sp", fw3[:], W["hy_fw3"][l, :, :], reads=[W["hy_fw3"]], writes=[fw3])
            fb1 = colvec("hfb1", W["hy_fb1"][l, :], W["hy_fb1"], [64, 1], "(d o) -> d o", o=1)
            fb2 = colvec("hfb2", W["hy_fb2"][l, :], W["hy_fb2"], [64, 1], "(d o) -> d o", o=1)
            frq = colvec("hfrq", W["hy_freq"][l, :], W["hy_freq"], [64, 1], "(d o) -> d o", o=1)
            brow = kb.sb("hbrow", [1, 512])
            kb.dma("sp", brow[:], W["hy_bias"][l, :, :].rearrange("o c -> (o c)").rearrange("(x n) -> x n", x=1), reads=[W["hy_bias"]], writes=[brow])
            for sn in (("L", "C") if with_ctx else ("L",)):
                sg = SEGS[sn]
                Ls, A, cbw, off = sg["Ls"], sg["A"], sg["cbw"], sg["off"]
                G = 128 // A
                N = 2 * Ls
                ngr = cbw // G
                nblk = 256 // cbw
                pre = f"hy{sn}_"
                with kb.scope():
                    def ld(nm, shape):
                        t = kb.sb("k" + nm, shape)
                        src = CT[pre + nm]
                        kb.dma("sp", t[:], src.t, reads=[src], writes=[t])
                        return t
                    def ldr(nm, shape):
                        tr = kb.sb("r" + nm, shape, F32R)
                        with kb.scope():
                            t32 = ld(nm, shape)
                            kb.op("dve", lambda e: e.tensor_copy(tr[:], t32[:]), reads=[t32], writes=[tr])
                        return tr
                    F256 = ldr("F256", [128, 2, 512]); TWC = ld("TWC", [128, 256]); TWS = ld("TWS", [128, 256])
                    Dre = ldr("Dre", [128, 128]); Dim = ldr("Dim", [128, 128]); nDim = ldr("nDim", [128, 128])
                    E1 = ldr("E1", [128, 256]); E2 = ldr("E2", [128, 256])
                    TW2C = ld("TW2C", [128, 2, 128]); TW2S = ld("TW2S", [128, 2, 128])
                    IC = ldr("IC", [128, 2, 128]); IS = ldr("IS", [128, 2, 128])
                    h2T = kb.sb("h2T", [64, N])
                    with kb.scope():
                        zT = kb.sb("zT", [33, N])
                        kb.dma("sp", zT[:], CT[pre + "zT"].t, reads=[CT[pre + "zT"]], writes=[zT])
                        h1T = kb.sb("h1T", [64, N])
                        arg = [kb.sb(f"harg{i}", [64, 512]) for i in range(2)]
                        wr = [kb.sb(f"hwr{i}", [64, 512]) for i in range(2)]
                        for (src, K_, wgt, bcol, dst) in ((zT, 33, fw1, fb1, h1T), (h1T, 64, fw2, fb2, h2T)):
                            for ci, n0 in enumerate(range(0, N, 512)):
                                p = PS[ci % 4]
                                ag = arg[ci % 2]
                                kb.op("pe", lambda e: e.matmul(p[0:64, :], wgt[0:K_, :], src[0:K_, n0:n0 + 512], start=True, stop=True),
                                      reads=[wgt, src], writes=[p])
                                kb.op("dve", lambda e: e.tensor_scalar(ag[:, :], p[0:64, :], bcol[:, 0:1], frq[:, 0:1], ALU.add, ALU.mult),
                                      reads=[p, bcol, frq], writes=[ag])
                                for _w in range(2):
                                    kb.op("dve", lambda e: e.tensor_scalar(wr[0][:, :], ag[:, :], PI, -2 * PI, ALU.is_gt, ALU.mult), reads=[ag], writes=[wr[0]])
                                    kb.op("dve", lambda e: e.tensor_scalar(wr[1][:, :], ag[:, :], -PI, 2 * PI, ALU.is_lt, ALU.mult), reads=[ag], writes=[wr[1]])
                                    kb.op("dve", lambda e: e.tensor_tensor(ag[:, :], ag[:, :], wr[0][:, :], ALU.add), reads=[ag, wr[0]], writes=[ag])
                                    kb.op("dve", lambda e: e.tensor_tensor(ag[:, :], ag[:, :], wr[1][:, :], ALU.add), reads=[ag, wr[1]], writes=[ag])
                                kb.op("act", lambda e: e.activation(dst[:, n0:n0 + 512], ag[:, :], AF.Sin), reads=[ag], writes=[dst])
                    KT1 = kb.sb("KT", [128, 2, ngr, A, G])
                    KTr1 = kb.sb("KTr", [128, 2, ngr, A, G], F32R)
                    KT = [KT1, KT1]
                    KTr = [KTr1, KTr1]
                    uvr = kb.sb("huvr", [128, ngr, A * G], F32R)
                    KS = [kb.sb(f"KS{o}", [128, ngr, 512]) for o in range(2)]
                    DECt = kb.sb("DECt", [128, 2, ngr, A, G])
                    part = kb.sb("hpart", [128, cbw])
                    rn = kb.sb("hrn", [128, cbw])
                    ex = kb.sb("hex", [1, cbw])
                    uv = kb.sb("huv", [128, ngr, A * G]); x1 = kb.sb("hx1", [128, ngr, A * G]); x2 = kb.sb("hx2", [128, ngr, A * G])
                    u2 = kb.sb("hu2", [128, ngr, A * G], F32R); res = kb.sb("hres", [128, A, cbw])
                    dts = (F32R, F32R, F32, F32)
                    NL = 4 if ngr >= 4 else 2
                    BpS = [[kb.sb(f"hBp{b}{i}", [128, 256], dts[i]) for i in range(4)] for b in range(NL)]
                    BpbS = [[Buf() for _ in range(4)] for b in range(NL)]
                    YpS = [[kb.sb(f"hYp{b}{i}", [128, 256], dts[i]) for i in range(4)] for b in range(NL)]
                    YpbS = [[Buf() for _ in range(4)] for b in range(NL)]
                    GpS = [[kb.sb(f"hGp{b}{i}", [128, 2, 128], dts[i]) for i in range(4)] for b in range(NL)]
                    GpbS = [[Buf() for _ in range(4)] for b in range(NL)]
                    fctr = [0]
                    sgm = kb.sb("hsgm", [cbw, Ls], BF16)

                    def fwd_fft(lhs_chunks, lhs_bufs, psB, psX):
                        n = len(lhs_chunks)
                        fctr[0] += 1
                        Bp, Bpb = BpS[fctr[0] % NL], BpbS[fctr[0] % NL]
                        for i, (ap, hf) in enumerate(lhs_chunks):
                            kb.op("pe", lambda e, ap=ap, hf=hf, i=i: e.matmul(psB[:, :], ap, F256[:, hf, :], start=(i == 0), stop=(i == n - 1)),
                                  reads=lhs_bufs + [F256], writes=[psB])
                        yield
                        cmul(Bp[0][:, :], Bp[1][:, :], psB[:, 0:256], psB[:, 256:512], TWC[:, :], TWS[:, :], True,
                             [psB], [TWC, TWS], Bpb, (Bp[2][:, :], Bp[3][:, :]))
                        yield
                        kb.op("pe", lambda e: e.matmul(psX[:, 0:256], Dre[:, :], Bp[0][:, :], start=True, stop=False), reads=[Dre, Bpb[0]], writes=[psX])
                        kb.op("pe", lambda e: e.matmul(psX[:, 0:256], nDim[:, :], Bp[1][:, :], start=False, stop=True), reads=[nDim, Bpb[1]], writes=[psX])
                        kb.op("pe", lambda e: e.matmul(psX[:, 256:512], Dim[:, :], Bp[0][:, :], start=True, stop=False), reads=[Dim, Bpb[0]], writes=[psX])
                        kb.op("pe", lambda e: e.matmul(psX[:, 256:512], Dre[:, :], Bp[1][:, :], start=False, stop=True), reads=[Dre, Bpb[1]], writes=[psX])

                    def conv_group(src, src_b, g, o, mulv, mul_b, dst_ap, dst_b, it):
                        ln = it % NL
                        if NL == 4:
                            psB, psX, psG, psy = PS[2 * ln], PS[2 * ln + 1], PS[2 * ln], PS[2 * ln + 1]
                        else:
                            psB, psX, psG, psy = PS[it % 2], PS[2 + it % 2], PS[4 + it % 2], PS[6 + it % 2]
                        Yp, Ypb, Gp, Gpb = YpS[ln], YpbS[ln], GpS[ln], GpbS[ln]
                        yield from fwd_fft([(src[:, g, :], 0)], [src_b], psB, psX)
                        yield
                        cmul(Yp[0][:, :], Yp[1][:, :], psX[:, 0:256], psX[:, 256:512], KS[o][:, g, 0:256], KS[o][:, g, 256:512], False,
                             [psX], [KS[o]], Ypb, (Yp[2][:, :], Yp[3][:, :]))
                        yield
                        for chn in range(2):
                            fs = slice(chn * 128, (chn + 1) * 128)
                            kb.op("pe", lambda e, fs=fs, chn=chn: e.matmul(psG[:, chn * 256:(chn + 1) * 256], Yp[0][:, fs], E1[:, :], start=True, stop=False),
                                  reads=[Ypb[0], E1], writes=[psG])
                            kb.op("pe", lambda e, fs=fs, chn=chn: e.matmul(psG[:, chn * 256:(chn + 1) * 256], Yp[1][:, fs], E2[:, :], start=False, stop=True),
                                  reads=[Ypb[1], E2], writes=[psG])
                        yield
                        pg = psG[:, :].rearrange("p (ch ri c) -> p ch ri c", ch=2, ri=2)
                        cmul(Gp[0][:, :, :], Gp[1][:, :, :], pg[:, :, 0, :], pg[:, :, 1, :], TW2C[:, :, :], TW2S[:, :, :], False,
                             [psG], [TW2C, TW2S], Gpb, (Gp[2][:, :, :], Gp[3][:, :, :]))
                        yield
                        k = 0
                        for chn in range(2):
                            for (tab, gsrc, gb) in ((IC, Gp[0], Gpb[0]), (IS, Gp[1], Gpb[1])):
                                kb.op("pe", lambda e, chn=chn, tab=tab, gsrc=gsrc, k=k: e.matmul(
                                    psy[:, 0:128], tab[:, chn, :], gsrc[:, chn, :], start=(k == 0), stop=(k == 3)), reads=[tab, gb], writes=[psy])
                                k += 1
                        yield
                        kb.op("dve", lambda e: e.tensor_tensor(dst_ap, psy[:, 0:128].rearrange("p (c a) -> p a c", a=A),
                                                               mulv[:, g, :].rearrange("p (a c) -> p a c", c=G), ALU.mult),
                              reads=[psy, mul_b], writes=[dst_b])

                    def lockstep(gens):
                        gens = list(gens)
                        while gens:
                            nxt = []
                            for g_ in gens:
                                try:
                                    next(g_)
                                    nxt.append(g_)
                                except StopIteration:
                                    pass
                            gens = nxt

                    def spec_group(o, g, it):
                        if NL == 4:
                            psB, psX = PS[2 * (it % 4)], PS[2 * (it % 4) + 1]
                        else:
                            psB, psX = PS[it % 2], PS[2 + it % 2]
                        yield from fwd_fft([(KTr[o][:, 0, g, :, :].rearrange("p a c -> p (a c)"), 0),
                                            (KTr[o][:, 1, g, :, :].rearrange("p a c -> p (a c)"), 1)], [KTr[o]], psB, psX)
                        yield
                        kb.op("act", lambda e: e.copy(KS[o][:, g, :], psX[:, :]), reads=[psX], writes=[KS[o]])

                    git = 0
                    for cb in range(nblk):
                        kb.dma("sp", DECt[:].rearrange("p h g a c -> p (h g a c)"), CT[pre + "DEC"][cb, :, :], reads=[CT[pre + "DEC"]], writes=[DECt])
                        for ai, tile_ in enumerate((uv, x1, x2)):
                            kb.dma("pool", tile_[:].rearrange("p g x -> p (g x)"), HS[sg["ut"]][ai, cb, :, :], reads=[HS[sg["ut"]]], writes=[tile_])
                        for o in range(2):
                            for hf in range(2):
                                col0 = o * 512 + hf * 256 + cb * cbw
                                npb = 512 // cbw
                                for a in range(A):
                                    p = PS[(a // npb) % 4]
                                    kb.op("pe", lambda e, p=p, a=a, hf=hf, col0=col0, npb=npb: e.matmul(
                                        p[:, (a % npb) * cbw:(a % npb + 1) * cbw], h2T[0:64, hf * 128 * A + a:hf * 128 * A + a + 127 * A + 1:A],
                                        fw3[0:64, col0:col0 + cbw], start=True, stop=True), reads=[h2T, fw3], writes=[p])
                                    if a % npb == npb - 1 or a == A - 1:
                                        a0 = (a // npb) * npb
                                        na = a - a0 + 1
                                        kb.op("dve", lambda e, p=p, a0=a0, na=na, hf=hf, o=o: e.tensor_tensor(
                                            KT[o][:, hf, :, a0:a0 + na, :], p[:, 0:na * cbw].rearrange("p (a g c) -> p g a c", a=na, c=G),
                                            DECt[:, hf, :, a0:a0 + na, :], ALU.mult), reads=[p, DECt], writes=[KT[o]])
                            kb.op("dve", lambda e, o=o: e.tensor_reduce(part[:, :].rearrange("p (g c) -> p g c", c=G),
                                                                        KT[o][:, :, :, :, :].rearrange("p h g a c -> p g c h a"), AX.XY, ALU.add,
                                                                        apply_absolute_value=True), reads=[KT[o]], writes=[part])
                            pe_ = PS[4]
                            kb.op("pe", lambda e, o=o: e.matmul(pe_[0:1, 0:cbw], h2T[0:64, 0:1], fw3[0:64, o * 512 + 256 + cb * cbw:o * 512 + 256 + (cb + 1) * cbw],
                                                                start=True, stop=True), reads=[h2T, fw3], writes=[pe_])
                            kb.op("act", lambda e: e.activation(ex[0:1, :], pe_[0:1, 0:cbw], AF.Abs), reads=[pe_], writes=[ex])
                            kb.op("dve", lambda e: e.tensor_tensor(part[0:1, :], part[0:1, :], ex[0:1, :], ALU.add), reads=[part, ex], writes=[part])
                            pt_ = PS[5]
                            kb.op("pe", lambda e: e.matmul(pt_[:, 0:cbw], ones_f[:, :], part[:, :], start=True, stop=True), reads=[ones_f, part], writes=[pt_])
                            kb.op("dve", lambda e: e.reciprocal(rn[:, :], pt_[:, 0:cbw]), reads=[pt_], writes=[rn])
                            for hf in range(2):
                                kb.op("dve", lambda e, o=o, hf=hf: e.tensor_tensor(
                                    KTr[o][:, hf, :, :, :], KT[o][:, hf, :, :, :],
                                    rn[:, :].rearrange("p (g c) -> p g c", c=G).unsqueeze(2).broadcast_to([128, ngr, A, G]), ALU.mult),
                                    reads=[KT[o], rn], writes=[KTr[o]])
                            kb.op("dve", lambda e, o=o: e.tensor_tensor(
                                KTr[o][0:1, 0, :, 0, :], KTr[o][0:1, 0, :, 0, :].bitcast(F32),
                                brow[0:1, o * 256 + cb * cbw:o * 256 + (cb + 1) * cbw].rearrange("p (g c) -> p g c", c=G), ALU.add),
                                reads=[KTr[o], brow], writes=[KTr[o]])
                            for g in range(0, ngr, NL):
                                gg = [g_ for g_ in range(g, min(ngr, g + NL))]
                                lockstep([spec_group(o, g_, git + k_) for k_, g_ in enumerate(gg)])
                                git += len(gg)
                        kb.op("act", lambda e: e.copy(uvr[:], uv[:]), reads=[uv], writes=[uvr])
                        for g in range(0, ngr, NL):
                            gg = [g_ for g_ in range(g, min(ngr, g + NL))]
                            lockstep([conv_group(uvr, uvr, g_, 0, x1, x1, u2[:, g_, :].rearrange("p (a c) -> p a c", c=G), u2, git + k_)
                                      for k_, g_ in enumerate(gg)])
                            git += len(gg)
                        for g in range(0, ngr, NL):
                            gg = [g_ for g_ in range(g, min(ngr, g + NL))]
                            lockstep([conv_group(u2, u2, g_, 1, x2, x2, res[:, :, g_ * G:(g_ + 1) * G], res, git + k_)
                                      for k_, g_ in enumerate(gg)])
                            git += len(gg)
                        kb.dma("sp", sgm[:], HS["SG"][cb * cbw:(cb + 1) * cbw, off:off + Ls], reads=[HS["SG"]], writes=[sgm])
                        Fv = sgm[:, :].rearrange("c (p a) -> c p a", a=A)
                        for a in range(A):
                            p = PS[4 + (a // 4) % 4]
                            kb.op("pe", lambda e, p=p, a=a: e.transpose(p[0:cbw, (a % 4) * 128:(a % 4 + 1) * 128], res[:, a, :], ident_f[:]),
                                  reads=[res, ident_f], writes=[p])
                            if a % 4 == 3 or a == A - 1:
                                a0 = (a // 4) * 4
                                na = a - a0 + 1
                                kb.op("dve", lambda e, p=p, a0=a0, na=na: e.tensor_tensor(
                                    Fv[:, :, a0:a0 + na], p[0:cbw, 0:na * 128].rearrange("c (a p) -> c p a", p=128), Fv[:, :, a0:a0 + na], ALU.mult),
                                    reads=[p, sgm], writes=[sgm])
                        kb.dma("sp", mixT[cb * cbw:(cb + 1) * cbw, off:off + Ls], sgm[:, :], reads=[sgm], writes=[Buf()])

    dbgn = [n for n, _ in dbg]
    for l in range(depth):
        last = (l == DEPTH - 1)
        with kb.scope():
            hT = kb.sb("hT", [128, 8, T], BF16)
            G1 = kb.sb("G1", [128, 2, D])
            SH = kb.sb("SH", [128, 2, D])
            phase_mod(l)
            phase_norm(l)
            if "noattn" not in dbgn:
                if os.environ.get("ATTN_ONLY", "") != "dense":
                    phase_attn(l, False, not last)
                if os.environ.get("ATTN_ONLY", "") != "window":
                    phase_attn(l, True, not last)
            if "norw" not in dbgn:
                phase_rwkv_prep(l)
            if "nohy" not in dbgn:
                phase_hyena_prep(l, not last)
            if "hT" in dbgn:
                tmp = kb.sb("dbghT", [128, T])
                for j in range(8):
                    kb.op("dve", lambda e, j=j, tmp=tmp: e.tensor_copy(tmp[:], hT[:, j, :]), reads=[hT], writes=[tmp])
                    kb.dma("sp", dbg_t["hT"][:, j, :], tmp[:], reads=[tmp], writes=[dbg_t["hT"]])
        if "norw" not in dbgn:
            {0: phase_rwkv_chunked, 1: phase_rwkv_chunked2, 3: phase_rwkv_chunked3}[RW_V2](l)
            phase_rwkv_out(l, not last)
        if "nohy" not in dbgn:
            phase_hyena_main(l, not last)
        if "noout" not in dbgn:
            phase_out(l, last)
    for n, s_ in dbg:
        if n == "xres":
            with kb.scope():
                tx = kb.sb("dbgx", [128, D])
                for i in range(NT):
                    kb.dma("sp", tx[:], xres[i * 128:(i + 1) * 128, :], reads=[xres_b[i]], writes=[tx])
                    kb.dma("sp", dbg_t[n][i * 128:(i + 1) * 128, :], tx[:], reads=[tx], writes=[dbg_t[n]])
        if n == "mixT":
            with kb.scope():
                tmpb = kb.sb("dbgmb", [128, T], BF16)
                tmpf = kb.sb("dbgmf", [128, T])
                for j in range(8):
                    kb.dma("sp", tmpb[:], mixT[j * 128:(j + 1) * 128, :], reads=[mixT], writes=[tmpb])
                    kb.op("dve", lambda e, tmpb=tmpb, tmpf=tmpf: e.tensor_copy(tmpf[:], tmpb[:]), reads=[tmpb], writes=[tmpf])
                    kb.dma("sp", dbg_t[n][j * 128:(j + 1) * 128, :], tmpf[:], reads=[tmpf], writes=[dbg_t[n]])
    kb.finish()
    kb.es.close()
    return kb, cst


_PROG = {}


def kernel(**inputs):
    if "p" not in _PROG:
        _PROG["p"] = build()
    kb, cst = _PROG["p"]
    f = lambda a: np.ascontiguousarray(np.asarray(a, dtype=np.float32))
    shared = {}
    for n in inputs:
        if n in ("x", "c", "ctx", "c_ctx"):
            continue
        shared[n] = f(inputs[n])
    shared["c_ctx"] = f(inputs["c_ctx"])
    for n, a in cst.items():
        shared["k_" + n] = np.ascontiguousarray(a)
    x, c, ctx = f(inputs["x"]), f(inputs["c"]), f(inputs["ctx"])
    B = x.shape[0]
    in_maps = []
    for b in range(B):
        m = dict(shared)
        m["x"] = np.ascontiguousarray(x[b])
        m["c"] = np.ascontiguousarray(c[b])
        m["ctx"] = np.ascontiguousarray(ctx[b])
        in_maps.append(m)
    res = run_bass_kernel_spmd(kb.nc, in_maps, core_ids=list(range(B)))
    return np.stack([np.asarray(res.results[b]["out"], dtype=np.float32) for b in range(B)], axis=0)
```

```python
import contextlib
import math
import numpy as np
import ml_dtypes
import concourse.bass as bass
import concourse.mybir as mybir
from concourse.bass_utils import run_bass_kernel_spmd

F32 = mybir.dt.float32
BF16 = mybir.dt.bfloat16
F32R = mybir.dt.float32r
ALU = mybir.AluOpType
AF = mybir.ActivationFunctionType
AX = mybir.AxisListType

D = 1024
L = 4096
C = 256
T = L + C
NT = T // 128
DEPTH = 4
D_IN = 3712
HY0, HYG0, RW0, RWG0, WA0, WAG0, FA0, FAG0 = 0, 768, 1024, 1920, 2176, 2688, 2944, 3456
EPS = 1e-6
NSLOT = 24
import os
RW_STAGE = int(os.environ.get('RW_STAGE', '99'))
INLINE_WAIT = int(os.environ.get('INLINE_WAIT', '1'))
POOL_DMA_TO_SP = int(os.environ.get('POOL_DMA_TO_SP', '1'))
RW_V2 = int(os.environ.get('RW_V2', '3'))


class Buf:
    def __init__(self, name=""):
        self.name = name
        self.w = None
        self.r = {}

    def wdeps(self):
        return [self.w] if self.w is not None else []

    def rdeps(self):
        return list(self.r.values())

    def add_reader(self, tok):
        k = tok[:2]
        if k not in self.r or self.r[k][2] < tok[2]:
            self.r[k] = tok

    def set_writer(self, tok):
        self.w = tok
        self.r = {}


class Tile(Buf):
    def __init__(self, name, t):
        super().__init__(name)
        self.t = t

    def __getitem__(self, key):
        return self.t[key]


class KB:
    def __init__(self):
        self.nc = bass.Bass("TRN2", target_bir_lowering=False)
        nc = self.nc
        self.es = contextlib.ExitStack()
        self.eng = {"pe": nc.tensor, "act": nc.scalar, "dve": nc.vector, "pool": nc.gpsimd, "sp": nc.sync}
        self.sem = {}
        self.cnt = {}
        self.waited = {e: {} for e in self.eng}
        for e in self.eng:
            self.sem[e] = self.es.enter_context(nc.semaphore("s_" + e))
            self.cnt[e] = 0
        self.slots = {}
        self.slot_i = {}
        for q in ("sp", "act", "pool"):
            self.slots[q] = [[self.es.enter_context(nc.semaphore(f"d_{q}{i}")), 0] for i in range(NSLOT)]
            self.slot_i[q] = 0
        self.n_ins = 0

    def sb(self, name, shape, dt=F32):
        self.uid = getattr(self, "uid", 0) + 1
        name = f"{name}_{self.uid}"
        return Tile(name, self.es.enter_context(self.nc.sbuf_tensor(name, list(shape), dt)))

    def ps(self, name, shape, dt=F32):
        return Tile(name, self.es.enter_context(self.nc.psum_tensor(name, list(shape), dt)))

    def dram(self, name, shape, dt=F32, kind="Internal"):
        t = self.nc.dram_tensor(name, list(shape), dt, kind=kind)
        b = Tile(name, t.ap())
        return b

    def _tok_sem(self, tok):
        if tok[0] == "e":
            return ("e", tok[1]), self.sem[tok[1]], tok[2]
        return ("d", tok[1]), self.slots[tok[1][0]][tok[1][1]][0], tok[2]

    def _wait(self, e, toks, defer=False):
        need = {}
        for tok in toks:
            if tok is None:
                continue
            key, sem, val = self._tok_sem(tok)
            if tok[0] == "e" and tok[1] == e and e == "pe":
                continue
            if self.waited[e].get(key, 0) >= val:
                continue
            if key not in need or need[key][1] < val:
                need[key] = (sem, val)
        items = list(need.items())
        inline = None
        if defer and INLINE_WAIT and items:
            inline = items.pop()
        for key, (sem, val) in items:
            self.eng[e].wait_ge(sem, val)
            self.waited[e][key] = val
        return inline

    def op(self, e, fn, reads=(), writes=()):
        toks = []
        for b in reads:
            toks += b.wdeps()
        for b in writes:
            toks += b.wdeps() + b.rdeps()
        inline = self._wait(e, toks, defer=True)
        ins = fn(self.eng[e])
        if inline is not None:
            key, (sem, val) = inline
            ins._wait_ge(sem, val)
            self.waited[e][key] = val
        self.cnt[e] += 1
        ins.then_inc(self.sem[e], 1)
        tok = ("e", e, self.cnt[e])
        for b in reads:
            b.add_reader(tok)
        for b in writes:
            b.set_writer(tok)
        self.n_ins += 1
        return ins

    def dma(self, q, out, in_, reads=(), writes=(), slow=False):
        if q == "pool" and POOL_DMA_TO_SP:
            q = "sp"
        i = self.slot_i[q]
        self.slot_i[q] = (i + 1) % NSLOT
        slot = self.slots[q][i]
        toks = []
        if slot[1] > 0:
            toks.append(("d", (q, i), slot[1]))
        for b in reads:
            toks += b.wdeps()
        for b in writes:
            toks += b.wdeps() + b.rdeps()
        self._wait(q, toks)
        if slow:
            ins = self.eng[q].dma_start(out=out, in_=in_, allow_slow_non_contiguous=True)
        else:
            ins = self.eng[q].dma_start(out=out, in_=in_)
        ins.then_inc(slot[0], 16)
        slot[1] += 16
        tok = ("d", (q, i), slot[1])
        for b in reads:
            b.add_reader(tok)
        for b in writes:
            b.set_writer(tok)
        self.n_ins += 1
        return ins

    def barrier(self):
        toks = [("e", e, self.cnt[e]) for e in self.eng if self.cnt[e] > 0]
        for q in self.slots:
            for i, s in enumerate(self.slots[q]):
                if s[1] > 0:
                    toks.append(("d", (q, i), s[1]))
        for e in self.eng:
            self._wait(e, toks)

    def finish(self):
        self.barrier()

    @contextlib.contextmanager
    def scope(self):
        es = contextlib.ExitStack()
        old = self.es
        self.es = es
        try:
            yield
        finally:
            self.barrier()
            self.es = old
            es.close()


def host_consts():
    cst = {}
    cst["ident_bf"] = np.eye(128, dtype=np.float32).astype(ml_dtypes.bfloat16)
    cst["ident_f"] = np.eye(128, dtype=np.float32)
    blk = np.zeros((128, 128), np.float32)
    blk[:64, :64] = 1.0
    blk[64:, 64:] = 1.0
    cst["blk64"] = blk
    cst["ones_f"] = np.ones((128, 128), np.float32)
    t = np.arange(L)
    row = (t // 64).astype(np.float32)
    col = (t % 64).astype(np.float32)
    inv = (10000.0 ** (-np.arange(16, dtype=np.float32) / 16)).astype(np.float32)
    cosT = np.zeros((128, L), np.float32)
    sinT = np.zeros((128, L), np.float32)
    perm = np.zeros((128, 128), np.float32)
    for p in range(128):
        d = p % 64
        sec, half, f = d // 32, (d % 32) // 16, d % 16
        pos = row if sec == 0 else col
        ang = (pos * inv[f]).astype(np.float32)
        cosT[p] = np.cos(ang)
        sinT[p] = np.sin(ang)
        if half == 0:
            perm[p + 16, p] = -1.0
        else:
            perm[p - 16, p] = 1.0
    cst["rope_cos"] = cosT
    cst["rope_sin"] = sinT
    cst["rope_perm"] = perm
    i = np.arange(128)[:, None]
    j = np.arange(384)[None, :]
    cst["wmask"] = np.where((j >= i) & (j <= i + 256), 0.0, -1e30).astype(np.float32)
    ii = np.arange(64)
    ms = np.zeros((128, 2, 64), np.float32); mts = np.zeros((128, 2, 64), np.float32); mti = np.zeros((128, 2, 64), np.float32)
    for hp in range(2):
        rows = slice(hp * 64, hp * 64 + 64)
        ms[rows, 0, :] = (ii[None, :] < ii[:, None]); ms[rows, 1, :] = (ii[None, :] > ii[:, None])
        mts[rows, 0, :] = (ii[:, None] < ii[None, :]); mts[rows, 1, :] = (ii[:, None] > ii[None, :])
        mti[rows, 0, :] = (ii[:, None] <= ii[None, :]); mti[rows, 1, :] = (ii[:, None] >= ii[None, :])
    cst["rw_ms"] = ms; cst["rw_mts"] = mts; cst["rw_mti"] = mti
    cst["rw_msb"] = np.concatenate([ms, ms], 2); cst["rw_mtsb"] = np.concatenate([mts, mts], 2)
    cst["rw_id2"] = np.concatenate([np.eye(64, dtype=np.float32)] * 2, 0)
    cst.update(hy_consts(L, 32, 32, "L"))
    cst.update(hy_consts(C, 2, 64, "C"))
    return cst


def hy_consts(Ls, A, cbw, tag):
    G = 128 // A
    N = 2 * Ls
    out = {}
    p = np.arange(128)
    f1 = np.arange(256)
    F = np.zeros((128, 2, 512), np.float64)
    for h in range(2):
        pp = h * 128 + p
        ang = 2 * np.pi * ((pp[:, None] * f1[None, :]) % 256) / 256
        F[:, h, 0:256] = np.cos(ang)
        F[:, h, 256:512] = -np.sin(ang)
    out["F256"] = F
    a_of_row = np.arange(128) // G
    th = 2 * np.pi * ((a_of_row[:, None] * f1[None, :]) % N) / N
    out["TWC"] = np.cos(th)
    out["TWS"] = np.sin(th)
    Dre = np.zeros((128, 128)); Dim = np.zeros((128, 128))
    E1 = np.zeros((128, 256)); E2 = np.zeros((128, 256))
    for a in range(A):
        for c in range(G):
            for f2 in range(A):
                ph = 2 * np.pi * ((a * f2) % A) / A
                Dre[a * G + c, c * A + f2] = np.cos(ph)
                Dim[a * G + c, c * A + f2] = -np.sin(ph)
                E1[c * A + f2, c * A + a] = np.cos(ph)
                E1[c * A + f2, 128 + c * A + a] = np.sin(ph)
                E2[c * A + f2, c * A + a] = -np.sin(ph)
                E2[c * A + f2, 128 + c * A + a] = np.cos(ph)
    out["Dre"] = Dre; out["Dim"] = Dim; out["nDim"] = -Dim; out["E1"] = E1; out["E2"] = E2
    a_of_col = np.arange(128) % A
    TW2C = np.zeros((128, 2, 128)); TW2S = np.zeros((128, 2, 128))
    IC = np.zeros((128, 2, 128)); IS = np.zeros((128, 2, 128))
    for ch in range(2):
        ff = ch * 128 + np.arange(128)
        th2 = 2 * np.pi * ((ff[:, None] * a_of_col[None, :]) % N) / N
        TW2C[:, ch, :] = np.cos(th2) / N
        TW2S[:, ch, :] = np.sin(th2) / N
        ph = 2 * np.pi * ((ff[:, None] * p[None, :]) % 256) / 256
        IC[:, ch, :] = np.cos(ph)
        IS[:, ch, :] = -np.sin(ph)
    out["TW2C"] = TW2C; out["TW2S"] = TW2S; out["IC"] = IC; out["IS"] = IS
    tp = np.arange(N)
    pos = np.where(tp < Ls, tp, N - tp).astype(np.float64)
    tn = (pos / (Ls - 1)).astype(np.float32)
    w = ((2.0 * math.pi / Ls) * pos).astype(np.float32)
    fb = np.linspace(1e-4, 15.0, 16, dtype=np.float32)
    zT = np.zeros((33, N), np.float32)
    zT[0] = tn
    zT[1:17] = np.cos(fb[:, None] * w[None, :])
    zT[17:33] = np.sin(fb[:, None] * w[None, :])
    out["zT"] = zT
    deltas = np.abs(np.linspace(math.log(1e-2) / 1.5, math.log(1e-2) / 0.3, 256, dtype=np.float32))
    dec = np.exp(-tn[:, None] * deltas[None, :]).astype(np.float32)
    dec[Ls, :] = 0.0
    nblk = 256 // cbw
    ngr = cbw // G
    DEC = np.zeros((nblk, 128, 2, ngr, A, G), np.float32)
    for h in range(2):
        for a in range(A):
            tpp = A * (h * 128 + p) + a
            for b in range(nblk):
                DEC[b, :, h, :, a, :] = dec[tpp, b * cbw:(b + 1) * cbw].reshape(128, ngr, G)
    out["DEC"] = DEC.reshape(nblk, 128, 2 * ngr * A * G)
    return {f"hy{tag}_{k}": np.ascontiguousarray(v.astype(np.float32)) for k, v in out.items()}

CONST_SPECS = None


def build(depth=DEPTH, dbg=()):
    kb = KB()
    nc = kb.nc
    cst = host_consts()
    def inp(name, shape, dt=F32):
        return kb.dram(name, shape, dt, kind="ExternalInput")

    x_in = inp("x", [L, D])
    c_in = inp("c", [D])
    ctx_in = inp("ctx", [C, D])
    cctx_in = inp("c_ctx", [D])
    W = {}
    wspec = {
        "mod_w": [DEPTH, D, 3 * D], "mod_b": [DEPTH, 3 * D], "norm_g": [DEPTH, D], "w_in": [DEPTH, D, D_IN],
        "w_out": [DEPTH, D, D], "wa_sink": [DEPTH, 4], "fa_q_norm": [DEPTH, 64], "fa_k_norm": [DEPTH, 64],
        "final_g": [D],
        "rw_conv": [DEPTH, 3, 896], "rw_w0": [DEPTH, 2, 256], "rw_w_up": [DEPTH, 2, 64, 256], "rw_a0": [DEPTH, 2, 256],
        "rw_a_up": [DEPTH, 2, 64, 256], "rw_k_k": [DEPTH, 256], "rw_k_a": [DEPTH, 256], "rw_r_k": [DEPTH, 256],
        "rw_ln_g": [DEPTH, 256], "rw_ln_b": [DEPTH, 256],
        "hy_conv": [DEPTH, 3, 768], "hy_fw1": [DEPTH, 33, 64], "hy_fb1": [DEPTH, 64], "hy_freq": [DEPTH, 64],
        "hy_fw2": [DEPTH, 64, 64], "hy_fb2": [DEPTH, 64], "hy_fw3": [DEPTH, 64, 1024], "hy_bias": [DEPTH, 2, 256],
    }
    for n, s in wspec.items():
        W[n] = inp(n, s)
    CT = {}
    for n, a in cst.items():
        CT[n] = inp("k_" + n, list(a.shape), BF16 if a.dtype == ml_dtypes.bfloat16 else F32)
    out = kb.dram("out", [L, D], F32, kind="ExternalOutput")
    xres = kb.dram("xres", [T, D], F32)
    mixT = kb.dram("mixT", [D, T], BF16)
    RS = {}
    for n in ("RT", "VT", "AL", "W0", "W1", "B0", "B1", "KD0", "KD1", "YF", "YB"):
        RS[n] = kb.dram("rs_" + n, [256, T])
    RS["VTOK"] = kb.dram("rs_VTOK", [T, 256])
    RS["SGT"] = kb.dram("rs_SGT", [256, T], BF16)
    HS = {"SG": kb.dram("hs_SG", [256, T], BF16),
          "UTL": kb.dram("hs_UTL", [3, 8, 128, 32 * 32]), "UTC": kb.dram("hs_UTC", [3, 4, 128, 2 * 64])}
    dbg_t = {}
    for n, s in dbg:
        dbg_t[n] = kb.dram("dbg_" + n, s, F32, kind="ExternalOutput")

    ident_bf = kb.sb("ident_bf", [128, 128], BF16)
    ident_f = kb.sb("ident_f", [128, 128])
    blk64 = kb.sb("blk64", [128, 128])
    ones_f = kb.sb("ones_f", [128, 128])
    for tl, n in ((ident_bf, "ident_bf"), (ident_f, "ident_f"), (blk64, "blk64"), (ones_f, "ones_f")):
        kb.dma("sp", tl[:], CT[n][:, :], reads=[CT[n]], writes=[tl])
    hT = G1 = SH = None
    GT = kb.sb("GT", [128, 2, D])
    PS = [kb.ps(f"ps{i}", [128, 512]) for i in range(8)]

    xres_b = [Buf(f"xres{i}") for i in range(NT)]

    def x_src(l, i):
        if l == 0:
            if i < 2:
                return ctx_in[i * 128:(i + 1) * 128, :], ctx_in
            return x_in[(i - 2) * 128:(i - 1) * 128, :], x_in
        return xres[i * 128:(i + 1) * 128, :], xres_b[i]

    def phase_mod(l):
        with kb.scope():
            cc = kb.sb("cc", [128, 2, 8])
            sc = kb.sb("sc", [128, 2, 8])
            mw = [kb.sb(f"mw{i}", [128, 8, 512]) for i in range(2)]
            mb = kb.sb("mb", [128, 3 * D])
            ng = kb.sb("ng", [128, D])
            modr = kb.sb("modr", [128, 2, 3 * D])
            kb.dma("sp", cc[:, 0, :], c_in.t.rearrange("(j p) -> p j", p=128), reads=[c_in], writes=[cc], slow=True)
            kb.dma("sp", cc[:, 1, :], cctx_in.t.rearrange("(j p) -> p j", p=128), reads=[cctx_in], writes=[cc], slow=True)
            kb.dma("sp", mb[:], W["mod_b"][l, :].partition_broadcast(128), reads=[W["mod_b"]], writes=[mb])
            kb.dma("sp", ng[:], W["norm_g"][l, :].partition_broadcast(128), reads=[W["norm_g"]], writes=[ng])
            kb.op("act", lambda e: e.activation(sc[:], cc[:], AF.Silu), reads=[cc], writes=[sc])
            for n in range(6):
                m = mw[n % 2]
                kb.dma("sp" if n % 2 == 0 else "pool", m[:],
                       W["mod_w"][l, :, n * 512:(n + 1) * 512].rearrange("(j p) n -> p j n", p=128),
                       reads=[W["mod_w"]], writes=[m])
                for i in range(2):
                    p = PS[(2 * n + i) % 8]
                    for j in range(8):
                        kb.op("pe", lambda e, p=p, i=i, j=j, m=m: e.matmul(
                            p[:, :], sc[:, i, j:j + 1].broadcast_to([128, 128]), m[:, j, :],
                            start=(j == 0), stop=(j == 7)), reads=[sc, m], writes=[p])
                    kb.op("dve", lambda e, p=p, i=i, n=n: e.tensor_tensor(
                        modr[:, i, n * 512:(n + 1) * 512], p[:, :], mb[:, n * 512:(n + 1) * 512], ALU.add),
                        reads=[p, mb], writes=[modr])
            for i in range(2):
                kb.op("dve", lambda e, i=i: e.scalar_tensor_tensor(
                    G1[:, i, :], modr[:, i, D:2 * D], 1.0, ng[:], ALU.add, ALU.mult), reads=[modr, ng], writes=[G1])
                kb.op("act", lambda e, i=i: e.copy(SH[:, i, :], modr[:, i, 0:D]), reads=[modr], writes=[SH])
                kb.op("act", lambda e, i=i: e.copy(GT[:, i, :], modr[:, i, 2 * D:3 * D]), reads=[modr], writes=[GT])

    def phase_norm(l):
        NLN = 4
        with kb.scope():
            xt = [kb.sb(f"xt{i}", [128, D]) for i in range(NLN)]
            junks = [kb.sb(f"junk{i}", [128, D]) for i in range(NLN)]
            hf = [kb.sb(f"hf{i}", [128, D]) for i in range(NLN)]
            hb = [kb.sb(f"hb{i}", [128, D], BF16) for i in range(NLN)]
            st = [kb.sb(f"st{i}", [128, 4]) for i in range(NLN)]

            def tile_gen(i):
                ln = i % NLN
                x, s, h, hbt, junk = xt[ln], st[ln], hf[ln], hb[ln], junks[ln]
                sel = 1 if i < 2 else 0
                src, srcb = x_src(l, i)
                kb.dma("sp", x[:], src, reads=[srcb], writes=[x])
                yield
                kb.op("act", lambda e: e.activation(junk[:], x[:], AF.Square, accum_out=s[:, 0:1]), reads=[x], writes=[junk, s])
                yield
                kb.op("dve", lambda e: e.tensor_scalar(s[:, 1:2], s[:, 0:1], 1.0 / D, EPS, ALU.mult, ALU.add), reads=[s], writes=[s])
                yield
                kb.op("act", lambda e: e.sqrt(s[:, 2:3], s[:, 1:2]), reads=[s], writes=[s])
                yield
                kb.op("dve", lambda e: e.reciprocal(s[:, 3:4], s[:, 2:3]), reads=[s], writes=[s])
                kb.op("dve", lambda e: e.scalar_tensor_tensor(h[:], x[:], s[:, 3:4], G1[:, sel, :], ALU.mult, ALU.mult), reads=[x, s, G1], writes=[h])
                yield
                kb.op("pool", lambda e: e.tensor_tensor(hbt[:], h[:], SH[:, sel, :], ALU.add), reads=[h, SH], writes=[hbt])
                yield
                p = PS[ln]
                pv = p[:, :].bitcast(BF16)
                for j in range(8):
                    kb.op("pe", lambda e, j=j: e.transpose(pv[:, j * 128:(j + 1) * 128], hbt[:, j * 128:(j + 1) * 128], ident_bf[:]),
                          reads=[hbt, ident_bf], writes=[p])
                yield
                kb.op("act", lambda e: e.copy(hT[:, :, i * 128:(i + 1) * 128], pv.rearrange("p (j t) -> p j t", j=8)), reads=[p], writes=[hT])

            for i0 in range(0, NT, NLN):
                gens = [tile_gen(i) for i in range(i0, min(NT, i0 + NLN))]
                while gens:
                    nxt = []
                    for g_ in gens:
                        try:
                            next(g_)
                            nxt.append(g_)
                        except StopIteration:
                            pass
                    gens = nxt

    def load_w(l, dst, col0, ncols, stage, q="sp"):
        kb.dma(q, stage[:, :, 0:ncols], W["w_in"][l, :, col0:col0 + ncols].rearrange("(j p) n -> p j n", p=128),
               reads=[W["w_in"]], writes=[stage])
        kb.op("pool", lambda e: e.tensor_copy(dst[:, :, 0:ncols], stage[:, :, 0:ncols]), reads=[stage], writes=[dst])

    def proj_fm(p, wt, c0, nc_, t0, nt):
        for j in range(8):
            kb.op("pe", lambda e, j=j: e.matmul(p[0:nc_, 0:nt], wt[:, j, c0:c0 + nc_], hT[:, j, t0:t0 + nt],
                                                start=(j == 0), stop=(j == 7)), reads=[wt, hT], writes=[p])

    def proj_tm(p, wt, c0, nc_, i):
        for j in range(8):
            kb.op("pe", lambda e, j=j: e.matmul(p[:, 0:nc_], hT[:, j, i * 128:(i + 1) * 128], wt[:, j, c0:c0 + nc_],
                                                start=(j == 0), stop=(j == 7)), reads=[wt, hT], writes=[p])

    TCH = [(t0, min(512, T - t0)) for t0 in range(0, T, 512)]

    def qk_prep(l, es_tiles, wt, c0, dst, dst_j, gvec, norm, rope):
        raw, sq, rs, rot = es_tiles
        for ci, (t0, nt) in enumerate(TCH):
            p = PS[ci % 2]
            proj_fm(p, wt, c0, 128, t0, nt)
            if norm:
                kb.op("act", lambda e, p=p, nt=nt: e.activation(sq[:, 0:nt], p[:, 0:nt], AF.Square), reads=[p], writes=[sq])
                p2 = PS[2 + ci % 2]
                kb.op("pe", lambda e, p2=p2, nt=nt: e.matmul(p2[:, 0:nt], blk64[:], sq[:, 0:nt], start=True, stop=True),
                      reads=[blk64, sq], writes=[p2])
                kb.op("dve", lambda e, p2=p2, nt=nt: e.tensor_scalar(rs[:, 0:nt], p2[:, 0:nt], 1.0 / 64, EPS, ALU.mult, ALU.add),
                      reads=[p2], writes=[rs])
                kb.op("act", lambda e, nt=nt: e.sqrt(rs[:, 0:nt], rs[:, 0:nt]), reads=[rs], writes=[rs])
                kb.op("dve", lambda e, nt=nt: e.reciprocal(rs[:, 0:nt], rs[:, 0:nt]), reads=[rs], writes=[rs])
                kb.op("dve", lambda e, p=p, nt=nt: e.scalar_tensor_tensor(
                    raw[:, 0:nt], p[:, 0:nt], gvec[:, 0:1], rs[:, 0:nt], ALU.mult, ALU.mult), reads=[p, gvec, rs], writes=[raw])
            else:
                kb.op("act", lambda e, p=p, nt=nt: e.copy(raw[:, 0:nt], p[:, 0:nt]), reads=[p], writes=[raw])
            lat0 = 0
            if t0 < C:
                lat0 = C - t0
                kb.op("pool", lambda e, t0=t0, lat0=lat0: e.tensor_copy(dst[:, dst_j, t0:t0 + lat0], raw[:, 0:lat0]),
                      reads=[raw], writes=[dst])
            if not rope:
                if nt > lat0:
                    kb.op("pool", lambda e, t0=t0, lat0=lat0, nt=nt: e.tensor_copy(
                        dst[:, dst_j, t0 + lat0:t0 + nt], raw[:, lat0:nt]), reads=[raw], writes=[dst])
                continue
            p3 = PS[4 + ci % 2]
            n_l = nt - lat0
            lp = t0 + lat0 - C
            kb.op("pe", lambda e, p3=p3, lat0=lat0, nt=nt: e.matmul(p3[:, lat0:nt], rope_perm[:], raw[:, lat0:nt], start=True, stop=True),
                  reads=[rope_perm, raw], writes=[p3])
            kb.op("dve", lambda e, p3=p3, lat0=lat0, nt=nt, lp=lp, n_l=n_l: e.tensor_tensor(
                rot[:, lat0:nt], p3[:, lat0:nt], rope_sin[:, lp:lp + n_l], ALU.mult), reads=[p3, rope_sin], writes=[rot])
            kb.op("pool", lambda e, lat0=lat0, nt=nt, lp=lp, n_l=n_l: e.tensor_tensor(
                raw[:, lat0:nt], raw[:, lat0:nt], rope_cos[:, lp:lp + n_l], ALU.mult), reads=[raw, rope_cos], writes=[raw])
            kb.op("dve", lambda e, t0=t0, lat0=lat0, nt=nt: e.tensor_tensor(
                dst[:, dst_j, t0 + lat0:t0 + nt], raw[:, lat0:nt], rot[:, lat0:nt], ALU.add), reads=[raw, rot], writes=[dst])

    rope_cos = rope_sin = rope_perm = None

    def phase_attn(l, dense, with_ctx):
        nonlocal rope_cos, rope_sin, rope_perm
        base = FA0 if dense else WA0
        gbase = FAG0 if dense else WAG0
        mrow = 768 if dense else 512
        with kb.scope():
            wt = kb.sb("wt", [128, 8, 768], BF16)
            gq = kb.sb("gq", [128, 1])
            gk = kb.sb("gk", [128, 1])
            sink = kb.sb("sink", [128, 4])
            if dense:
                for hh in range(2):
                    kb.dma("sp", gq[hh * 64:(hh + 1) * 64, :], W["fa_q_norm"][l, :].rearrange("(d o) -> d o", o=1),
                           reads=[W["fa_q_norm"]], writes=[gq], slow=True)
                    kb.dma("sp", gk[hh * 64:(hh + 1) * 64, :], W["fa_k_norm"][l, :].rearrange("(d o) -> d o", o=1),
                           reads=[W["fa_k_norm"]], writes=[gk], slow=True)
            else:
                kb.dma("sp", sink[:], W["wa_sink"][l, :].partition_broadcast(128), reads=[W["wa_sink"]], writes=[sink])
            QT = kb.sb("QT", [128, 2, T], BF16)
            KT = kb.sb("KT", [128, 1, T], BF16)
            VW = 65 if dense else 64
            Vt = kb.sb("Vt", [128, NT, 2, VW], BF16)
            SG = None
            with kb.scope():
                rope_cos = kb.sb("rope_cos", [128, L])
                rope_sin = kb.sb("rope_sin", [128, L])
                rope_perm = kb.sb("rope_perm", [128, 128])
                kb.dma("sp", rope_cos[:], CT["rope_cos"][:, :], reads=[CT["rope_cos"]], writes=[rope_cos])
                kb.dma("pool", rope_sin[:], CT["rope_sin"][:, :], reads=[CT["rope_sin"]], writes=[rope_sin])
                kb.dma("sp", rope_perm[:], CT["rope_perm"][:, :], reads=[CT["rope_perm"]], writes=[rope_perm])
                stage = kb.sb("wstage", [128, 8, 256])
                w4 = W["w_in"][l, :, base:base + 256].rearrange("(j p) (h d) -> p j h d", p=128, d=64)
                st4 = stage[:, :, 0:256].rearrange("p j (h d) -> p j h d", d=64)
                for hi, h in enumerate((0, 2, 1, 3)):
                    kb.dma("sp", st4[:, :, hi, :], w4[:, :, h, :], reads=[W["w_in"]], writes=[stage])
                kb.op("pool", lambda e: e.tensor_copy(wt[:, :, 0:256], stage[:]), reads=[stage], writes=[wt])
                kb.dma("pool", stage[:], W["w_in"][l, :, base + 256:base + 512].rearrange("(j p) n -> p j n", p=128),
                       reads=[W["w_in"]], writes=[stage])
                kb.op("pool", lambda e: e.tensor_copy(wt[:, :, 256:512], stage[:]), reads=[stage], writes=[wt])
                kb.dma("sp", stage[:], W["w_in"][l, :, gbase:gbase + 256].rearrange("(j p) n -> p j n", p=128),
                       reads=[W["w_in"]], writes=[stage])
                kb.op("pool", lambda e: e.tensor_copy(wt[:, :, 512:768], stage[:]), reads=[stage], writes=[wt])
                tl = (kb.sb("qraw", [128, 512]), kb.sb("qsq", [128, 512]), kb.sb("qrs", [128, 512]), kb.sb("qrot", [128, 512]))
                qk_prep(l, tl, wt, 0, QT, 0, gq, dense, True)
                qk_prep(l, tl, wt, 128, QT, 1, gq, dense, True)
                qk_prep(l, tl, wt, 256, KT, 0, gk, dense, True)
            if dense:
                kb.op("pool", lambda e: e.memset(Vt[:, :, :, 64:65], 1.0), writes=[Vt])
            for i in range(NT):
                p = PS[i % 2]
                proj_tm(p, wt, 384, 128, i)
                kb.op("act", lambda e, p=p, i=i: e.copy(Vt[:, i, :, 0:64], p[:, 0:128].rearrange("p (k d) -> p k d", d=64)),
                      reads=[p], writes=[Vt])
            with kb.scope():
                if dense:
                    attn_dense(l, wt, QT, KT, Vt, mrow, with_ctx)
                else:
                    attn_window(l, wt, QT, KT, Vt, sink, mrow, with_ctx)

    def attn_dense(l, wt, QT, KT, Vt, mrow, with_ctx):
        pt = [kb.sb(f"pt{i}", [128, 512], BF16) for i in range(8)]
        osb = [kb.sb(f"osb{i}", [128, 512]) for i in range(2)]
        rc = [kb.sb(f"rc{i}", [128, 512]) for i in range(2)]
        ob = [kb.sb(f"ob{i}", [128, 512], BF16) for i in range(2)]
        sgt = [kb.sb(f"sgt{i}", [64, 512], BF16) for i in range(2)]
        chunks = []
        if with_ctx:
            chunks.append((0, C, 0, 2))
        for t0 in range(C, T, 512):
            chunks.append((t0, 512, 0, NT))
        for pr in range(2):
            heads_ = (pr, pr + 2)
            for (t0, nt, kb0, kb1) in chunks:
                po = [PS[4], PS[5]]
                pg = [PS[6], PS[7]]
                for s_, h in enumerate(heads_):
                    for j in range(8):
                        kb.op("pe", lambda e, j=j, s_=s_, h=h: e.matmul(
                            pg[s_][0:64, 0:nt], wt[:, j, 512 + 64 * h:576 + 64 * h], hT[:, j, t0:t0 + nt], start=(j == 0), stop=(j == 7)),
                            reads=[wt, hT], writes=[pg[s_]])
                    kb.op("act", lambda e, s_=s_: e.activation(sgt[s_][0:64, 0:nt], pg[s_][0:64, 0:nt], AF.Silu), reads=[pg[s_]], writes=[sgt[s_]])

                def pv_(kbi):
                    for s_ in range(2):
                        ptt = pt[(2 * kbi + s_) % 8]
                        kb.op("pe", lambda e, s_=s_, ptt=ptt: e.matmul(
                            po[s_][0:65, 0:nt], Vt[:, kbi, s_, 0:65], ptt[:, 0:nt], start=(kbi == kb0), stop=(kbi == kb1 - 1)),
                            reads=[Vt, ptt], writes=[po[s_]])
                LA = 1
                for kbi in range(kb0, kb1):
                    for s_ in range(2):
                        ks = slice(64 * s_, 64 * s_ + 64)
                        psS = PS[(2 * kbi + s_) % 4]
                        kb.op("pe", lambda e, psS=psS, ks=ks: e.matmul(
                            psS[:, 0:nt], KT[ks, 0, kbi * 128:(kbi + 1) * 128], QT[ks, pr, t0:t0 + nt], start=True, stop=True),
                            reads=[KT, QT], writes=[psS])
                    for s_ in range(2):
                        psS = PS[(2 * kbi + s_) % 4]
                        ptt = pt[(2 * kbi + s_) % 8]
                        kb.op("act", lambda e, psS=psS, ptt=ptt: e.activation(ptt[:, 0:nt], psS[:, 0:nt], AF.Exp, scale=0.125),
                              reads=[psS], writes=[ptt])
                    if kbi - LA >= kb0:
                        pv_(kbi - LA)
                for kbi in range(max(kb0, kb1 - LA), kb1):
                    pv_(kbi)
                for s_, h in enumerate(heads_):
                    o_s, r_c, o_b, sg = osb[s_], rc[s_], ob[s_], sgt[s_]
                    pb = pg[s_]
                    kb.op("dve", lambda e, s_=s_, r_c=r_c: e.reciprocal(r_c[64:65, 0:nt], po[s_][64:65, 0:nt]), reads=[po[s_]], writes=[r_c])
                    kb.op("act", lambda e, s_=s_, o_s=o_s: e.copy(o_s[0:64, 0:nt], po[s_][0:64, 0:nt]), reads=[po[s_]], writes=[o_s])
                    kb.op("pe", lambda e, pb=pb, r_c=r_c: e.matmul(pb[0:64, 0:nt], ones_f[64:65, 0:64], r_c[64:65, 0:nt], start=True, stop=True),
                          reads=[ones_f, r_c], writes=[pb])
                    kb.op("dve", lambda e, o_s=o_s, pb=pb: e.tensor_tensor(o_s[0:64, 0:nt], o_s[0:64, 0:nt], pb[0:64, 0:nt], ALU.mult),
                          reads=[o_s, pb], writes=[o_s])
                    kb.op("pool", lambda e, o_s=o_s, o_b=o_b, sg=sg: e.tensor_tensor(o_b[0:64, 0:nt], o_s[0:64, 0:nt], sg[0:64, 0:nt], ALU.mult),
                          reads=[o_s, sg], writes=[o_b])
                    kb.dma("sp", mixT[mrow + 64 * h:mrow + 64 * h + 64, t0:t0 + nt], o_b[0:64, 0:nt], reads=[o_b], writes=[Buf()])

    def attn_window(l, wt, QT, KT, Vt, sink, mrow, with_ctx):
        wmask = kb.sb("wmask", [128, 384])
        kb.dma("sp", wmask[:], CT["wmask"][:, :], reads=[CT["wmask"]], writes=[wmask])
        nsink = kb.sb("nsink", [128, 4])
        kb.op("dve", lambda e: e.tensor_scalar(nsink[:], sink[:], -1.0, None, ALU.mult), reads=[sink], writes=[nsink])
        S = [kb.sb(f"wS{i}", [128, 640]) for i in range(4)]
        P = [kb.sb(f"wP{i}", [128, 640]) for i in range(4)]
        Pn = [kb.sb(f"wPn{i}", [128, 640], BF16) for i in range(4)]
        PT = [kb.sb(f"wPT{i}", [128, 640], BF16) for i in range(4)]
        st = [kb.sb(f"wst{i}", [128, 8]) for i in range(4)]
        sgt = [kb.sb(f"wsg{i}", [64, 128], BF16) for i in range(4)]
        ob = [kb.sb(f"wob{i}", [64, 128], BF16) for i in range(4)]
        it = 0
        for i in range(0 if with_ctx else 2, NT):
            if i < 2:
                loc = []
            else:
                loc = list(range(max(2, i - 1), min(NT - 1, i + 1) + 1))
            nl = 128 * len(loc)
            m0 = 128 if (i >= 2 and i - 1 < 2) else 0
            nk = nl + C
            ktiles = loc + [0, 1]
            def unit(h, it):
                kv, pr = h // 2, h % 2
                ks = slice(64 * kv, 64 * kv + 64)
                s_, p_, pn_, pt_, st_, sg, o_b = S[it % 4], P[it % 4], Pn[it % 4], PT[it % 4], st[it % 4], sgt[it % 4], ob[it % 4]
                psA, psB = PS[2 * (it % 4)], PS[2 * (it % 4) + 1]
                psT, psOG = psA, psB
                psO, psG = psOG, psOG
                q_ap = QT[ks, pr, i * 128:(i + 1) * 128]
                if nl:
                    k0 = loc[0] * 128
                    kb.op("pe", lambda e: e.matmul(psA[:, 0:nl], q_ap, KT[ks, 0, k0:k0 + nl], start=True, stop=True),
                          reads=[QT, KT], writes=[psA])
                    kb.op("dve", lambda e: e.tensor_tensor(s_[:, 0:nl], psA[:, 0:nl], wmask[:, m0:m0 + nl], ALU.add),
                          reads=[psA, wmask], writes=[s_])
                kb.op("pe", lambda e: e.matmul(psB[:, 0:C], q_ap, KT[ks, 0, 0:C], start=True, stop=True),
                      reads=[QT, KT], writes=[psB])
                kb.op("act", lambda e: e.copy(s_[:, nl:nk], psB[:, 0:C]), reads=[psB], writes=[s_])
                yield
                kb.op("dve", lambda e: e.reduce_max(st_[:, 0:1], s_[:, 0:nk], AX.X), reads=[s_], writes=[st_])
                kb.op("dve", lambda e: e.tensor_scalar(st_[:, 1:2], st_[:, 0:1], -0.125, nsink[:, h:h + 1], ALU.mult, ALU.min),
                      reads=[st_, nsink], writes=[st_])
                kb.op("act", lambda e: e.activation(p_[:, 0:nk], s_[:, 0:nk], AF.Exp, bias=st_[:, 1:2], scale=0.125,
                                                    accum_out=st_[:, 2:3]), reads=[s_, st_], writes=[p_, st_])
                kb.op("act", lambda e: e.activation(st_[:, 3:4], sink[:, h:h + 1], AF.Exp, bias=st_[:, 1:2], scale=1.0),
                      reads=[sink, st_], writes=[st_])
                kb.op("dve", lambda e: e.tensor_tensor(st_[:, 4:5], st_[:, 2:3], st_[:, 3:4], ALU.add), reads=[st_], writes=[st_])
                kb.op("dve", lambda e: e.reciprocal(st_[:, 5:6], st_[:, 4:5]), reads=[st_], writes=[st_])
                kb.op("dve", lambda e: e.tensor_scalar(pn_[:, 0:nk], p_[:, 0:nk], st_[:, 5:6], None, ALU.mult),
                      reads=[p_, st_], writes=[pn_])
                yield
                pv = psT[:, :].bitcast(BF16)
                nb = nk // 128
                for b in range(nb):
                    kb.op("pe", lambda e, b=b: e.transpose(pv[:, b * 128:(b + 1) * 128], pn_[:, b * 128:(b + 1) * 128], ident_bf[:]),
                          reads=[pn_, ident_bf], writes=[psT])
                yield
                kb.op("act", lambda e: e.copy(pt_[:, 0:nk], pv[:, 0:nk]), reads=[psT], writes=[pt_])
                yield
                for b in range(nb):
                    kb.op("pe", lambda e, b=b: e.matmul(psO[0:64, 0:128], Vt[:, ktiles[b], kv, 0:64], pt_[:, b * 128:(b + 1) * 128],
                                                        start=(b == 0), stop=(b == nb - 1)), reads=[Vt, pt_], writes=[psO])
                for j in range(8):
                    kb.op("pe", lambda e, j=j: e.matmul(
                        psG[0:64, 128:256], wt[:, j, 512 + 64 * h:576 + 64 * h], hT[:, j, i * 128:(i + 1) * 128],
                        start=(j == 0), stop=(j == 7)), reads=[wt, hT], writes=[psG])
                yield
                kb.op("act", lambda e: e.activation(sg[:, :], psG[0:64, 128:256], AF.Silu), reads=[psG], writes=[sg])
                kb.op("dve", lambda e: e.tensor_tensor(o_b[:, :], psO[0:64, 0:128], sg[:, :], ALU.mult), reads=[psO, sg], writes=[o_b])
                kb.dma("sp", mixT[mrow + 64 * h:mrow + 64 * h + 64, i * 128:(i + 1) * 128], o_b[:, :], reads=[o_b], writes=[Buf()])

            for h0 in (0,):
                gens = [unit(h_, it + k_) for k_, h_ in enumerate((0, 2, 1, 3))]
                it += 4
                while gens:
                    nxt = []
                    for g_ in gens:
                        try:
                            next(g_)
                            nxt.append(g_)
                        except StopIteration:
                            pass
                    gens = nxt

    def phase_out(l, last):
        with kb.scope():
            wo = kb.sb("wo", [128, 8, D], BF16)
            stage = kb.sb("wostage", [128, 8, 256])
            for q in range(4):
                kb.dma("sp", stage[:], W["w_out"][l, :, q * 256:(q + 1) * 256].rearrange("(j p) n -> p j n", p=128),
                       reads=[W["w_out"]], writes=[stage])
                kb.op("pool", lambda e, q=q: e.tensor_copy(wo[:, :, q * 256:(q + 1) * 256], stage[:]), reads=[stage], writes=[wo])
            fg = kb.sb("fg", [128, D])
            if last:
                kb.dma("sp", fg[:], W["final_g"][:].partition_broadcast(128), reads=[W["final_g"]], writes=[fg])
            mt = [kb.sb(f"mt{i}", [128, 8, 128], BF16) for i in range(2)]
            xt = [kb.sb(f"oxt{i}", [128, D]) for i in range(2)]
            xn = [kb.sb(f"oxn{i}", [128, D]) for i in range(2)]
            tmp = [kb.sb(f"otmp{i}", [128, 512]) for i in range(2)]
            st = [kb.sb(f"ost{i}", [128, 4]) for i in range(2)]
            junk = kb.sb("ojunk", [128, D])
            mixv = mixT.t.rearrange("(j p) t -> p j t", p=128)
            for it, i in enumerate(range(2 if last else 0, NT)):
                m, x, xo, s = mt[it % 2], xt[it % 2], xn[it % 2], st[it % 2]
                sel = 1 if i < 2 else 0
                kb.dma("sp", m[:], mixv[:, :, i * 128:(i + 1) * 128], reads=[mixT], writes=[m])
                src, srcb = x_src(l, i)
                kb.dma("pool", x[:], src, reads=[srcb], writes=[x])
                for hf in range(2):
                    p = PS[(2 * it + hf) % 8]
                    tp = tmp[hf]
                    for j in range(8):
                        kb.op("pe", lambda e, j=j, p=p, m=m, hf=hf: e.matmul(p[:, :], m[:, j, :], wo[:, j, hf * 512:(hf + 1) * 512],
                                                                     start=(j == 0), stop=(j == 7)), reads=[m, wo], writes=[p])
                    kb.op("dve", lambda e, p=p, tp=tp, hf=hf, sel=sel: e.tensor_tensor(
                        tp[:], p[:, :], GT[:, sel, hf * 512:(hf + 1) * 512], ALU.mult), reads=[p, GT], writes=[tp])
                    kb.op("pool", lambda e, tp=tp, hf=hf, x=x, xo=xo: e.tensor_tensor(
                        xo[:, hf * 512:(hf + 1) * 512], x[:, hf * 512:(hf + 1) * 512], tp[:], ALU.add), reads=[x, tp], writes=[xo])
                if not last:
                    kb.dma("sp", xres[i * 128:(i + 1) * 128, :], xo[:], reads=[xo], writes=[xres_b[i]])
                else:
                    kb.op("act", lambda e, xo=xo, s=s: e.activation(junk[:], xo[:], AF.Square, accum_out=s[:, 0:1]),
                          reads=[xo], writes=[junk, s])
                    kb.op("dve", lambda e, s=s: e.tensor_scalar(s[:, 1:2], s[:, 0:1], 1.0 / D, EPS, ALU.mult, ALU.add),
                          reads=[s], writes=[s])
                    kb.op("act", lambda e, s=s: e.sqrt(s[:, 2:3], s[:, 1:2]), reads=[s], writes=[s])
                    kb.op("dve", lambda e, s=s: e.reciprocal(s[:, 3:4], s[:, 2:3]), reads=[s], writes=[s])
                    kb.op("dve", lambda e, xo=xo, s=s, x=x: e.scalar_tensor_tensor(
                        x[:], xo[:], s[:, 3:4], fg[:], ALU.mult, ALU.mult), reads=[xo, s, fg], writes=[x])
                    kb.dma("sp", out[(i - 2) * 128:(i - 1) * 128, :], x[:], reads=[x], writes=[Buf()])


    def conv_tile(l, wt, cw, ncw, jt, Zraw, Zout):
        for ci, (t0, nt) in enumerate(TCH):
            p = PS[ci % 4]
            proj_fm(p, wt, 0, 128, t0, nt)
            kb.op("act", lambda e, p=p, t0=t0, nt=nt: e.copy(Zraw[:, 1 + t0:1 + t0 + nt], p[:, 0:nt]), reads=[p], writes=[Zraw])
        kb.op("dve", lambda e: e.tensor_scalar(Zout[:, :], Zraw[:, 1:T + 1], cw[:, jt, 1:2], None, ALU.mult), reads=[Zraw, cw], writes=[Zout])
        kb.op("dve", lambda e: e.scalar_tensor_tensor(Zout[:, :], Zraw[:, 0:T], cw[:, jt, 0:1], Zout[:, :], ALU.mult, ALU.add),
              reads=[Zraw, cw, Zout], writes=[Zout])
        kb.op("dve", lambda e: e.scalar_tensor_tensor(Zout[:, :], Zraw[:, 2:T + 2], cw[:, jt, 2:3], Zout[:, :], ALU.mult, ALU.add),
              reads=[Zraw, cw, Zout], writes=[Zout])
        kb.op("dve", lambda e: e.scalar_tensor_tensor(Zout[:, C - 1:C], Zraw[:, C + 1:C + 2], ncw[:, jt, 2:3], Zout[:, C - 1:C], ALU.mult, ALU.add),
              reads=[Zraw, ncw, Zout], writes=[Zout])
        kb.op("dve", lambda e: e.scalar_tensor_tensor(Zout[:, C:C + 1], Zraw[:, C:C + 1], ncw[:, jt, 0:1], Zout[:, C:C + 1], ALU.mult, ALU.add),
              reads=[Zraw, ncw, Zout], writes=[Zout])

    def colvec(name, src_ap, srcb, shape, rearr, **kw):
        t = kb.sb(name, shape)
        kb.dma("sp", t[:], src_ap.rearrange(rearr, **kw), reads=[srcb], writes=[t], slow=True)
        return t

    def phase_rwkv_prep(l):
        with kb.scope():
            stage = kb.sb("rstage", [128, 8, 128])
            wts = [kb.sb(f"rwt{i}", [128, 8, 128], BF16) for i in range(2)]
            cw = kb.sb("rcw", [128, 7, 3])
            for k in range(3):
                kb.dma("sp", cw[:, :, k], W["rw_conv"][l, k, :].rearrange("(j p) -> p j", p=128), reads=[W["rw_conv"]], writes=[cw], slow=True)
            ncw = kb.sb("rncw", [128, 7, 3])
            kb.op("dve", lambda e: e.tensor_scalar(ncw[:], cw[:], -1.0, None, ALU.mult), reads=[cw], writes=[ncw])
            kk_ = colvec("rkk", W["rw_k_k"][l, :], W["rw_k_k"], [128, 2], "(j p) -> p j", p=128)
            ka_ = colvec("rka", W["rw_k_a"][l, :], W["rw_k_a"], [128, 2], "(j p) -> p j", p=128)
            omka = kb.sb("romka", [128, 2])
            kb.op("dve", lambda e: e.tensor_scalar(omka[:], ka_[:], -1.0, 1.0, ALU.mult, ALU.add), reads=[ka_], writes=[omka])
            w0_ = kb.sb("rw0", [128, 2, 2])
            a0_ = kb.sb("ra0", [128, 2, 2])
            for d in range(2):
                kb.dma("sp", w0_[:, d, :], W["rw_w0"][l, d, :].rearrange("(j p) -> p j", p=128), reads=[W["rw_w0"]], writes=[w0_], slow=True)
                kb.dma("sp", a0_[:, d, :], W["rw_a0"][l, d, :].rearrange("(j p) -> p j", p=128), reads=[W["rw_a0"]], writes=[a0_], slow=True)
            wup = kb.sb("rwup", [128, 2, 256])
            kb.dma("sp", wup[0:64, :, :], W["rw_w_up"][l, :, :, :].rearrange("d k n -> k d n"), reads=[W["rw_w_up"]], writes=[wup])
            kb.dma("sp", wup[64:128, :, :], W["rw_a_up"][l, :, :, :].rearrange("d k n -> k d n"), reads=[W["rw_a_up"]], writes=[wup])
            Zraw = kb.sb("rZraw", [128, T + 2])
            Zout = kb.sb("rZout", [128, T])
            Z6 = kb.sb("rZ6", [128, T])
            kb.op("pool", lambda e: e.memset(Zraw[:, 0:1], 0.0), writes=[Zraw])
            kb.op("pool", lambda e: e.memset(Zraw[:, T + 1:T + 2], 0.0), writes=[Zraw])
            tA = [kb.sb(f"rtA{i}", [128, 512]) for i in range(2)]
            tB = [kb.sb(f"rtB{i}", [128, 512]) for i in range(2)]
            tC = [kb.sb(f"rtC{i}", [128, 512]) for i in range(2)]
            tD = [kb.sb(f"rtD{i}", [128, 512]) for i in range(2)]
            tE = [kb.sb(f"rtE{i}", [128, 512]) for i in range(2)]
            tG = [kb.sb(f"rtG{i}", [128, 512], BF16) for i in range(2)]
            vt_ = [kb.sb(f"rvt{i}", [128, 128]) for i in range(2)]
            order = [6, 0, 1, 4, 5, 2, 3, 7, 8]
            for oi, jt in enumerate(order):
                wt = wts[oi % 2]
                c0 = RW0 + jt * 128 if jt < 7 else RWG0 + (jt - 7) * 128
                load_w(l, wt, c0, 128, stage)
                if jt >= 7:
                    for ci, (t0, nt) in enumerate(TCH):
                        p = PS[ci % 4]
                        proj_fm(p, wt, 0, 128, t0, nt)
                        g = tG[ci % 2]
                        kb.op("act", lambda e, p=p, g=g, nt=nt: e.activation(g[:, 0:nt], p[:, 0:nt], AF.Silu), reads=[p], writes=[g])
                        kb.dma("sp", RS["SGT"][(jt - 7) * 128:(jt - 6) * 128, t0:t0 + nt], g[:, 0:nt], reads=[g], writes=[Buf()])
                    continue
                conv_tile(l, wt, cw, ncw, jt, Zraw, Z6 if jt == 6 else Zout)
                if jt == 6:
                    kb.op("act", lambda e: e.activation(Z6[0:64, :], Z6[0:64, :], AF.Tanh), reads=[Z6], writes=[Z6])
                elif jt in (0, 1):
                    kb.dma("sp", RS["RT"][jt * 128:(jt + 1) * 128, :], Zout[:, :], reads=[Zout], writes=[Buf()])
                elif jt in (4, 5):
                    kb.dma("sp", RS["VT"][(jt - 4) * 128:(jt - 3) * 128, :], Zout[:, :], reads=[Zout], writes=[Buf()])
                    for i in range(NT):
                        p = PS[4 + i % 2]
                        kb.op("pe", lambda e, p=p, i=i: e.transpose(p[:, 0:128], Zout[:, i * 128:(i + 1) * 128], ident_f[:]),
                              reads=[Zout, ident_f], writes=[p])
                        v = vt_[i % 2]
                        kb.op("act", lambda e, p=p, v=v: e.copy(v[:, :], p[:, 0:128]), reads=[p], writes=[v])
                        kb.dma("pool", RS["VTOK"][i * 128:(i + 1) * 128, (jt - 4) * 128:(jt - 3) * 128], v[:, :], reads=[v], writes=[Buf()])
                else:
                    pt = jt - 2
                    rows = slice(pt * 128, (pt + 1) * 128)
                    for ci, (t0, nt) in enumerate(TCH):
                        a_, b_, c_, d_, e_ = tA[ci % 2], tB[ci % 2], tC[ci % 2], tD[ci % 2], tE[ci % 2]
                        zc = Zout[:, t0:t0 + nt]
                        kb.op("dve", lambda e: e.tensor_scalar(a_[:, 0:nt], zc, kk_[:, pt:pt + 1], None, ALU.mult), reads=[Zout, kk_], writes=[a_])
                        kb.op("act", lambda e: e.activation(b_[:, 0:nt], a_[:, 0:nt], AF.Square), reads=[a_], writes=[b_])
                        p = PS[ci % 2]
                        kb.op("pe", lambda e: e.matmul(p[:, 0:nt], blk64[:], b_[:, 0:nt], start=True, stop=True), reads=[blk64, b_], writes=[p])
                        kb.op("act", lambda e: e.sqrt(b_[:, 0:nt], p[:, 0:nt]), reads=[p], writes=[b_])
                        kb.op("dve", lambda e: e.tensor_scalar(b_[:, 0:nt], b_[:, 0:nt], 1e-12, None, ALU.max), reads=[b_], writes=[b_])
                        kb.op("dve", lambda e: e.reciprocal(b_[:, 0:nt], b_[:, 0:nt]), reads=[b_], writes=[b_])
                        kb.op("dve", lambda e: e.scalar_tensor_tensor(a_[:, 0:nt], a_[:, 0:nt], -1.0, b_[:, 0:nt], ALU.mult, ALU.mult),
                              reads=[a_, b_], writes=[a_])
                        kb.dma("sp", RS["AL"][rows, t0:t0 + nt], a_[:, 0:nt], reads=[a_], writes=[Buf()])
                        for d in range(2):
                            pa = PS[2 + d]
                            kb.op("pe", lambda e: e.matmul(pa[:, 0:nt], wup[64:128, d, pt * 128:(pt + 1) * 128], Z6[64:128, t0:t0 + nt],
                                                           start=True, stop=True), reads=[wup, Z6], writes=[pa])
                            kb.op("act", lambda e: e.activation(c_[:, 0:nt], pa[:, 0:nt], AF.Sigmoid, bias=a0_[:, d, pt:pt + 1]),
                                  reads=[pa, a0_], writes=[c_])
                            kb.op("dve", lambda e: e.scalar_tensor_tensor(d_[:, 0:nt], c_[:, 0:nt], -1.0, a_[:, 0:nt], ALU.mult, ALU.mult),
                                  reads=[c_, a_], writes=[d_])
                            kb.dma("sp", RS[f"B{d}"][rows, t0:t0 + nt], d_[:, 0:nt], reads=[d_], writes=[Buf()])
                            kb.op("dve", lambda e: e.tensor_scalar(c_[:, 0:nt], c_[:, 0:nt], ka_[:, pt:pt + 1], omka[:, pt:pt + 1], ALU.mult, ALU.add),
                                  reads=[c_, ka_, omka], writes=[c_])
                            kb.op("dve", lambda e: e.tensor_tensor(e_[:, 0:nt], c_[:, 0:nt], zc, ALU.mult), reads=[c_, Zout], writes=[e_])
                            kb.dma("pool", RS[f"KD{d}"][rows, t0:t0 + nt], e_[:, 0:nt], reads=[e_], writes=[Buf()])
                            pw = PS[4 + d]
                            kb.op("pe", lambda e: e.matmul(pw[:, 0:nt], wup[0:64, d, pt * 128:(pt + 1) * 128], Z6[0:64, t0:t0 + nt],
                                                           start=True, stop=True), reads=[wup, Z6], writes=[pw])
                            kb.op("act", lambda e: e.activation(c_[:, 0:nt], pw[:, 0:nt], AF.Sigmoid, bias=w0_[:, d, pt:pt + 1]),
                                  reads=[pw, w0_], writes=[c_])
                            kb.op("dve", lambda e: e.tensor_scalar(d_[:, 0:nt], c_[:, 0:nt], -math.exp(-0.5), None, ALU.mult),
                                  reads=[c_], writes=[d_])
                            kb.dma("pool", RS[f"W{d}"][rows, t0:t0 + nt], d_[:, 0:nt], reads=[d_], writes=[Buf()])

    def phase_rwkv_scan(l):
        with kb.scope():
            ST = [kb.sb(f"ST{d}", [128, 2, 64]) for d in range(2)]
            for d in range(2):
                kb.op("pool", lambda e, d=d: e.memset(ST[d][:], 0.0), writes=[ST[d]])
            names = ("AL", "W", "B", "KD", "RT")
            ch = [[{n: kb.sb(f"c{n}{d}{i}", [128, 2, 128]) for n in names} for i in range(2)] for d in range(2)]
            vch = [[kb.sb(f"cV{d}{i}", [128, 256]) for i in range(2)] for d in range(2)]
            t1 = [kb.sb(f"st1{d}", [128, 2, 64]) for d in range(2)]
            t2 = [kb.sb(f"st2{d}", [128, 2, 64]) for d in range(2)]
            ysb = [kb.sb(f"ysb{d}", [64, 512]) for d in range(2)]
            psSA, psV, psY = [PS[0], PS[1]], [PS[2], PS[3]], [PS[4], PS[5]]
            border = [1, 0] + list(range(NT - 1, 1, -1))
            for ci in range(NT):
                cidx = [ci, border[ci]]
                cur = []
                for d in range(2):
                    c0 = cidx[d] * 128
                    tl_ = ch[d][ci % 2]
                    for n in names:
                        src = RS[n if n in ("AL", "RT") else f"{n}{d}"]
                        kb.dma("sp" if d == 0 else "pool", tl_[n][:],
                               src.t.rearrange("(pr q) t -> q pr t", q=128)[:, :, c0:c0 + 128], reads=[src], writes=[tl_[n]])
                    vv = vch[d][ci % 2]
                    kb.dma("sp" if d == 0 else "pool", vv[:], RS["VTOK"][c0:c0 + 128, :], reads=[RS["VTOK"]], writes=[vv])
                    cur.append((tl_, vv))
                for tl in range(128):
                    for d in range(2):
                        col = tl if d == 0 else 127 - tl
                        tl_, vv = cur[d]
                        S_, sa, pv, py = ST[d], psSA[d], psV[d], psY[d]
                        for pr in range(2):
                            for hp in range(2):
                                rows = slice(64 * hp, 64 * hp + 64)
                                kb.op("pe", lambda e, pr=pr, rows=rows: e.matmul(
                                    sa[rows, pr * 64:(pr + 1) * 64], tl_["AL"][rows, pr, col:col + 1].broadcast_to([64, 64]),
                                    S_[rows, pr, :], start=True, stop=True), reads=[tl_["AL"], S_], writes=[sa])
                        for pr in range(2):
                            for hp in range(2):
                                rows = slice(64 * hp, 64 * hp + 64)
                                h = 2 * pr + hp
                                kb.op("pe", lambda e, pr=pr, rows=rows, h=h: e.matmul(
                                    pv[rows, pr * 64:(pr + 1) * 64], ident_f[:, col:col + 1].broadcast_to([128, 64]),
                                    vv[:, h * 64:(h + 1) * 64], start=True, stop=True), reads=[ident_f, vv], writes=[pv])
                        for pr in range(2):
                            kb.op("dve", lambda e, pr=pr: e.tensor_scalar(
                                t1[d][:, pr, :], sa[:, pr * 64:(pr + 1) * 64], tl_["B"][:, pr, col:col + 1], None, ALU.mult),
                                reads=[sa, tl_["B"]], writes=[t1[d]])
                            kb.op("dve", lambda e, pr=pr: e.scalar_tensor_tensor(
                                t2[d][:, pr, :], pv[:, pr * 64:(pr + 1) * 64], tl_["KD"][:, pr, col:col + 1], t1[d][:, pr, :], ALU.mult, ALU.add),
                                reads=[pv, tl_["KD"], t1[d]], writes=[t2[d]])
                            kb.op("dve", lambda e, pr=pr: e.scalar_tensor_tensor(
                                S_[:, pr, :], S_[:, pr, :], tl_["W"][:, pr, col:col + 1], t2[d][:, pr, :], ALU.mult, ALU.add),
                                reads=[S_, tl_["W"], t2[d]], writes=[S_])
                        for pr in range(2):
                            for hp in range(2):
                                rows = slice(64 * hp, 64 * hp + 64)
                                h = 2 * pr + hp
                                kb.op("pe", lambda e, pr=pr, rows=rows, h=h: e.matmul(
                                    py[0:64, h * 128 + col:h * 128 + col + 1], S_[rows, pr, :], tl_["RT"][rows, pr, col:col + 1],
                                    start=True, stop=True), reads=[S_, tl_["RT"]], writes=[py])
                for d in range(2):
                    c0 = cidx[d] * 128
                    kb.op("act", lambda e, d=d: e.copy(ysb[d][:, :], psY[d][0:64, :]), reads=[psY[d]], writes=[ysb[d]])
                    dst = RS["YF" if d == 0 else "YB"]
                    kb.dma("sp", dst.t.rearrange("(h v) t -> v h t", v=64)[:, :, c0:c0 + 128],
                           ysb[d][:, :].rearrange("v (h t) -> v h t", h=4), reads=[ysb[d]], writes=[Buf()])


    def phase_rwkv_chunked(l):
        CH = 64
        NCH = T // CH
        with kb.scope():
            def ldc(nm, shape):
                t = kb.sb("k" + nm, shape)
                kb.dma("sp", t[:], CT[nm].t, reads=[CT[nm]], writes=[t])
                return t
            Ms = ldc("rw_ms", [128, 2, 64]); MTs = ldc("rw_mts", [128, 2, 64]); MTi = ldc("rw_mti", [128, 2, 64])
            id2 = ldc("rw_id2", [128, 64])
            ones = kb.sb("rones", [128, 64])
            kb.op("pool", lambda e: e.memset(ones[:], 1.0), writes=[ones])
            ST = kb.sb("cST", [128, 4, 64])
            kb.op("pool", lambda e: e.memset(ST[:], 0.0), writes=[ST])
            names = ("AL", "W", "B", "KD", "RT")
            def t4(nm, n=2, w=64):
                return [kb.sb(f"{nm}{i}", [128, 4, w]) for i in range(n)]
            IN = {n: t4("ci" + n) for n in names}
            VTK = t4("cVTK")
            CS = t4("cCS", 1)[0]; TOT = kb.sb("cTOT", [128, 4]); TMP = t4("cTMP", 1)[0]
            Epos = t4("cEp", 1)[0]; Eneg = t4("cEn", 1)[0]; Eprev = t4("cEv", 1)[0]; Etot = t4("cEt", 1)[0]; Wtot = kb.sb("cWt", [128, 4])
            Ab = t4("cAb", 1)[0]; Bb = t4("cBb", 1)[0]; Kb = t4("cKb", 1)[0]; Rb = t4("cRb", 1)[0]; Bt = t4("cBt", 1)[0]; Kt = t4("cKt", 1)[0]
            Q = t4("cQ"); P = t4("cP"); ArbT = t4("cArbT", 1)[0]; AkvT = t4("cAkvT", 1)[0]; ArkT = t4("cArkT", 1)[0]
            X = t4("cX", 2, 128); Btok = t4("cBtok", 1)[0]; Ktok = t4("cKtok", 1)[0]
            RAT = t4("cRAT", 1)[0]; McT = t4("cMcT", 1)[0]; NcS = t4("cNcS", 1)[0]; DG = t4("cDG", 1)[0]
            ysb = [kb.sb(f"cysb{d}", [64, 256]) for d in range(2)]
            border = [3, 2, 1, 0] + list(range(NCH - 1, 3, -1))
            DP = [(d, pr) for d in range(2) for pr in range(2)]
            HP = [slice(0, 64), slice(64, 128)]

            def mm_all(ps, col_fn, lhs_fn, rhs_fn, reads, start=True, stop=True, w=None):
                for dp in range(4):
                    for hp in range(2):
                        r = HP[hp]
                        c0, c1 = col_fn(dp)
                        kb.op("pe", lambda e, dp=dp, r=r, c0=c0, c1=c1: e.matmul(ps[r, c0:c1], lhs_fn(dp, r), rhs_fn(dp, r), start=start, stop=stop),
                              reads=reads, writes=[ps])

            for ci in range(NCH):
                cidx = [ci, border[ci]]
                i2 = ci % 2
                for d in range(2):
                    c0 = cidx[d] * CH
                    for n in names:
                        src = RS[n if n in ("AL", "RT") else f"{n}{d}"]
                        kb.dma("sp" if d == 0 else "pool", IN[n][i2][:, 2 * d:2 * d + 2, :],
                               src.t.rearrange("(pr q) t -> q pr t", q=128)[:, :, c0:c0 + CH], reads=[src], writes=[IN[n][i2]])
                    for hp in range(2):
                        kb.dma("sp" if d == 0 else "pool", VTK[i2][HP[hp], 2 * d:2 * d + 2, :],
                               RS["VTOK"][c0:c0 + CH, :].rearrange("t (pr hp v) -> t pr hp v", pr=2, hp=2)[:, :, hp, :],
                               reads=[RS["VTOK"]], writes=[VTK[i2]])
                al, lw, be, kd, rt, vt = IN["AL"][i2], IN["W"][i2], IN["B"][i2], IN["KD"][i2], IN["RT"][i2], VTK[i2]
                if RW_STAGE <= 1:
                    continue
                for dp in range(4):
                    kb.op("dve", lambda e, dp=dp: e.tensor_tensor_scan(CS[:, dp, :], ones[:, :], lw[:, dp, :], 0.0, ALU.mult, ALU.add),
                          reads=[ones, lw], writes=[CS])
                kb.op("dve", lambda e: e.tensor_copy(TOT[:, :], CS[:, :, CH - 1]), reads=[CS], writes=[TOT])
                kb.op("dve", lambda e: e.tensor_tensor(CS[:, 2:4, :], lw[:, 2:4, :], CS[:, 2:4, :], ALU.subtract), reads=[lw, CS], writes=[CS])
                kb.op("dve", lambda e: e.tensor_tensor(CS[:, 2:4, :], CS[:, 2:4, :], TOT[:, 2:4].unsqueeze(2).broadcast_to([128, 2, CH]), ALU.add),
                      reads=[CS, TOT], writes=[CS])
                kb.op("act", lambda e: e.activation(Epos[:], CS[:], AF.Exp), reads=[CS], writes=[Epos])
                kb.op("act", lambda e: e.activation(Eneg[:], CS[:], AF.Exp, scale=-1.0), reads=[CS], writes=[Eneg])
                kb.op("pool", lambda e: e.tensor_tensor(TMP[:], CS[:], lw[:], ALU.subtract), reads=[CS, lw], writes=[TMP])
                kb.op("act", lambda e: e.activation(Eprev[:], TMP[:], AF.Exp), reads=[TMP], writes=[Eprev])
                kb.op("dve", lambda e: e.tensor_tensor(Etot[:], TOT[:, :].unsqueeze(2).broadcast_to([128, 4, CH]), CS[:], ALU.subtract),
                      reads=[TOT, CS], writes=[Etot])
                kb.op("act", lambda e: e.activation(Etot[:], Etot[:], AF.Exp), reads=[Etot], writes=[Etot])
                kb.op("act", lambda e: e.activation(Wtot[:], TOT[:], AF.Exp), reads=[TOT], writes=[Wtot])
                kb.op("dve", lambda e: e.tensor_tensor(Ab[:], al[:], Eprev[:], ALU.mult), reads=[al, Eprev], writes=[Ab])
                kb.op("pool", lambda e: e.tensor_tensor(Bb[:], be[:], Eneg[:], ALU.mult), reads=[be, Eneg], writes=[Bb])
                kb.op("dve", lambda e: e.tensor_tensor(Kb[:], kd[:], Eneg[:], ALU.mult), reads=[kd, Eneg], writes=[Kb])
                kb.op("pool", lambda e: e.tensor_tensor(Rb[:], rt[:], Epos[:], ALU.mult), reads=[rt, Epos], writes=[Rb])
                kb.op("dve", lambda e: e.tensor_tensor(Bt[:], be[:], Etot[:], ALU.mult), reads=[be, Etot], writes=[Bt])
                kb.op("pool", lambda e: e.tensor_tensor(Kt[:], kd[:], Etot[:], ALU.mult), reads=[kd, Etot], writes=[Kt])
                if RW_STAGE <= 2:
                    continue
                PA, PB, PC, PT1, PD, PX, PPQ, PE_ = PS
                mm_all(PA, lambda dp: (dp * 128, dp * 128 + 64), lambda dp, r: Bb[r, dp, :], lambda dp, r: Ab[r, dp, :], [Bb, Ab])
                mm_all(PA, lambda dp: (dp * 128 + 64, dp * 128 + 128), lambda dp, r: Bb[r, dp, :], lambda dp, r: Rb[r, dp, :], [Bb, Rb])
                mm_all(PB, lambda dp: (dp * 128, dp * 128 + 64), lambda dp, r: Kb[r, dp, :], lambda dp, r: Ab[r, dp, :], [Kb, Ab])
                mm_all(PB, lambda dp: (dp * 128 + 64, dp * 128 + 128), lambda dp, r: Kb[r, dp, :], lambda dp, r: Rb[r, dp, :], [Kb, Rb])
                mm_all(PC, lambda dp: (dp * 64, dp * 64 + 64), lambda dp, r: Ab[r, dp, :], lambda dp, r: Bb[r, dp, :], [Ab, Bb])
                q0, p0 = Q[0], P[0]
                pav = PA[:, :].rearrange("p (dp x) -> p dp x", dp=4)
                pbv = PB[:, :].rearrange("p (dp x) -> p dp x", dp=4)
                def mk(m):
                    return m[:, :, :].unsqueeze(2).broadcast_to([128, 2, 2, 64])
                def v4(ap):
                    return ap.rearrange("p (d pr) x -> p d pr x", d=2)
                kb.op("dve", lambda e: e.tensor_tensor(v4(q0[:]), v4(pav[:, :, 0:64]), mk(MTs), ALU.mult), reads=[PA, MTs], writes=[q0])
                kb.op("dve", lambda e: e.tensor_tensor(v4(ArbT[:]), v4(pav[:, :, 64:128]), mk(MTi), ALU.mult), reads=[PA, MTi], writes=[ArbT])
                kb.op("dve", lambda e: e.tensor_tensor(v4(AkvT[:]), v4(pbv[:, :, 0:64]), mk(MTs), ALU.mult), reads=[PB, MTs], writes=[AkvT])
                kb.op("dve", lambda e: e.tensor_tensor(v4(ArkT[:]), v4(pbv[:, :, 64:128]), mk(MTi), ALU.mult), reads=[PB, MTi], writes=[ArkT])
                kb.op("dve", lambda e: e.tensor_tensor(v4(p0[:]), v4(PC[:, 0:256].rearrange("p (dp x) -> p dp x", dp=4)), mk(Ms), ALU.mult),
                      reads=[PC, Ms], writes=[p0])
                if RW_STAGE <= 3:
                    continue
                def idb(r):
                    return ident_f[r, r.start:r.start + 64]
                mm_all(PT1, lambda dp: (dp * 128, dp * 128 + 64), lambda dp, r: Ab[r, dp, :], lambda dp, r: idb(r), [Ab, ident_f])
                mm_all(PT1, lambda dp: (dp * 128 + 64, dp * 128 + 128), lambda dp, r: Bt[r, dp, :], lambda dp, r: idb(r), [Bt, ident_f])
                mm_all(PC, lambda dp: (256 + dp * 64, 256 + dp * 64 + 64), lambda dp, r: Kt[r, dp, :], lambda dp, r: idb(r), [Kt, ident_f])
                x0 = X[0]
                pt1v = PT1[:, :].rearrange("p (dp x) -> p dp x", dp=4)
                kb.op("act", lambda e: e.copy(x0[:, :, 0:64], pt1v[:, :, 0:64]), reads=[PT1], writes=[x0])
                kb.op("act", lambda e: e.copy(Btok[:], pt1v[:, :, 64:128]), reads=[PT1], writes=[Btok])
                kb.op("act", lambda e: e.copy(Ktok[:], PC[:, 256:512].rearrange("p (dp x) -> p dp x", dp=4)), reads=[PC], writes=[Ktok])
                if RW_STAGE <= 4:
                    continue
                mm_all(PD, lambda dp: (dp * 64, dp * 64 + 64), lambda dp, r: AkvT[r, dp, :], lambda dp, r: vt[r, dp, :], [AkvT, vt])
                kb.op("act", lambda e: e.copy(x0[:, :, 64:128], PD[:, 0:256].rearrange("p (dp x) -> p dp x", dp=4)), reads=[PD], writes=[x0])
                if RW_STAGE <= 5:
                    continue
                qc, pc, xc = Q[0], P[0], X[0]
                for j in range(6):
                    qn, pn, xn = Q[(j + 1) % 2], P[(j + 1) % 2], X[(j + 1) % 2]
                    mm_all(PX, lambda dp: (dp * 128, dp * 128 + 128), lambda dp, r: qc[r, dp, :], lambda dp, r: xc[r, dp, :], [qc, xc])
                    kb.op("dve", lambda e, xn=xn, xc=xc: e.tensor_tensor(xn[:], xc[:], PX[:, :].rearrange("p (dp x) -> p dp x", dp=4), ALU.add),
                          reads=[xc, PX], writes=[xn])
                    if j < 5:
                        mm_all(PPQ, lambda dp: (dp * 64, dp * 64 + 64), lambda dp, r: qc[r, dp, :], lambda dp, r: pc[r, dp, :], [qc, pc])
                        mm_all(PPQ, lambda dp: (256 + dp * 64, 256 + dp * 64 + 64), lambda dp, r: pc[r, dp, :], lambda dp, r: qc[r, dp, :], [qc, pc])
                        kb.op("act", lambda e, pn=pn: e.copy(pn[:], PPQ[:, 0:256].rearrange("p (dp x) -> p dp x", dp=4)), reads=[PPQ], writes=[pn])
                        kb.op("act", lambda e, qn=qn: e.copy(qn[:], PPQ[:, 256:512].rearrange("p (dp x) -> p dp x", dp=4)), reads=[PPQ], writes=[qn])
                    qc, pc, xc = qn, pn, xn
                if RW_STAGE <= 6:
                    continue
                mm_all(PD, lambda dp: (256 + dp * 64, 256 + dp * 64 + 64), lambda dp, r: xc[r, dp, 0:64], lambda dp, r: ArbT[r, dp, :], [xc, ArbT])
                kb.op("dve", lambda e: e.tensor_tensor(RAT[:], Rb[:], PD[:, 256:512].rearrange("p (dp x) -> p dp x", dp=4), ALU.add),
                      reads=[Rb, PD], writes=[RAT])
                mm_all(PE_, lambda dp: (dp * 64, dp * 64 + 64), lambda dp, r: xc[r, dp, 0:64], lambda dp, r: Btok[r, dp, :], [xc, Btok])
                kb.op("pool", lambda e: e.tensor_tensor(DG[:], id2[:, :].unsqueeze(1).broadcast_to([128, 4, 64]),
                                                        Wtot[:, :].unsqueeze(2).broadcast_to([128, 4, 64]), ALU.mult), reads=[id2, Wtot], writes=[DG])
                kb.op("dve", lambda e: e.tensor_tensor(McT[:], DG[:], PE_[:, 0:256].rearrange("p (dp x) -> p dp x", dp=4), ALU.add),
                      reads=[DG, PE_], writes=[McT])
                for dp in range(4):
                    for hp in range(2):
                        r = HP[hp]
                        c0 = 256 + dp * 64
                        kb.op("pe", lambda e, dp=dp, r=r, c0=c0: e.matmul(PE_[r, c0:c0 + 64], Btok[r, dp, :], xc[r, dp, 64:128], start=True, stop=False),
                              reads=[Btok, xc], writes=[PE_])
                        kb.op("pe", lambda e, dp=dp, r=r, c0=c0: e.matmul(PE_[r, c0:c0 + 64], Ktok[r, dp, :], vt[r, dp, :], start=False, stop=True),
                              reads=[Ktok, vt], writes=[PE_])
                kb.op("act", lambda e: e.copy(NcS[:], PE_[:, 256:512].rearrange("p (dp x) -> p dp x", dp=4)), reads=[PE_], writes=[NcS])
                if RW_STAGE <= 7:
                    continue
                PYs = [PA, PT1]
                for dp in range(4):
                    for hp in range(2):
                        r = HP[hp]
                        PY = PYs[hp]
                        c0 = dp * 64
                        kb.op("pe", lambda e, dp=dp, r=r, c0=c0, PY=PY: e.matmul(PY[0:64, c0:c0 + 64], ST[r, dp, :], RAT[r, dp, :], start=True, stop=False),
                              reads=[ST, RAT], writes=[PY])
                        kb.op("pe", lambda e, dp=dp, r=r, c0=c0, PY=PY: e.matmul(PY[0:64, c0:c0 + 64], xc[r, dp, 64:128], ArbT[r, dp, :], start=False, stop=False),
                              reads=[xc, ArbT], writes=[PY])
                        kb.op("pe", lambda e, dp=dp, r=r, c0=c0, PY=PY: e.matmul(PY[0:64, c0:c0 + 64], vt[r, dp, :], ArkT[r, dp, :], start=False, stop=True),
                              reads=[vt, ArkT], writes=[PY])
                for d in range(2):
                    c0 = cidx[d] * CH
                    yv = ysb[d][:, :].rearrange("v (pr hp t) -> v pr hp t", pr=2, hp=2)
                    for hp in range(2):
                        kb.op("act", lambda e, d=d, hp=hp, yv=yv: e.copy(
                            yv[:, :, hp, :], PYs[hp][0:64, d * 128:(d + 1) * 128].rearrange("v (pr t) -> v pr t", pr=2)), reads=[PYs[hp]], writes=[ysb[d]])
                    dst = RS["YF" if d == 0 else "YB"]
                    kb.dma("sp", dst.t.rearrange("(h v) t -> v h t", v=64)[:, :, c0:c0 + CH],
                           ysb[d][:, :].rearrange("v (h t) -> v h t", h=4), reads=[ysb[d]], writes=[Buf()])
                if RW_STAGE <= 8:
                    continue
                PSS = PB
                mm_all(PSS, lambda dp: (dp * 64, dp * 64 + 64), lambda dp, r: McT[r, dp, :], lambda dp, r: ST[r, dp, :], [McT, ST])
                kb.op("dve", lambda e: e.tensor_tensor(ST[:], NcS[:], PSS[:, 0:256].rearrange("p (dp x) -> p dp x", dp=4), ALU.add),
                      reads=[NcS, PSS], writes=[ST])


    def phase_rwkv_chunked3(l):
        CH = 64
        NCH = T // CH
        with kb.scope():
            def ldc(nm, shape):
                t = kb.sb("k" + nm, shape)
                kb.dma("sp", t[:], CT[nm].t, reads=[CT[nm]], writes=[t])
                return t
            Ms = ldc("rw_ms", [128, 2, 64]); MTs = ldc("rw_mts", [128, 2, 64]); MTi = ldc("rw_mti", [128, 2, 64])
            id2 = ldc("rw_id2", [128, 64])
            ones = kb.sb("rones", [128, 64])
            kb.op("pool", lambda e: e.memset(ones[:], 1.0), writes=[ones])
            ST = kb.sb("cST", [128, 4, 64])
            kb.op("pool", lambda e: e.memset(ST[:], 0.0), writes=[ST])
            names = ("AL", "W", "B", "KD", "RT")
            import types
            def alloc_set(si):
                S = types.SimpleNamespace()
                def t4(nm, n=2, w=64):
                    return [kb.sb(f"{nm}s{si}_{i}", [128, 4, w]) for i in range(n)]
                S.IN = {n: t4("ci" + n, 1)[0] for n in names}
                S.VTK = t4("cVTK", 1)[0]
                S.CS = t4("cCS", 1)[0]; S.TOT = kb.sb(f"cTOT{si}", [128, 4]); S.TMP = t4("cTMP", 1)[0]
                S.Epos = t4("cEp", 1)[0]; S.Eneg = t4("cEn", 1)[0]; S.Eprev = t4("cEv", 1)[0]; S.Etot = t4("cEt", 1)[0]; S.Wtot = kb.sb(f"cWt{si}", [128, 4])
                S.Ab = t4("cAb", 1)[0]; S.Bb = t4("cBb", 1)[0]; S.Kb = t4("cKb", 1)[0]; S.Rb = t4("cRb", 1)[0]; S.Bt = t4("cBt", 1)[0]; S.Kt = t4("cKt", 1)[0]
                S.Q = t4("cQ"); S.P = t4("cP"); S.ArbT = t4("cArbT", 1)[0]; S.AkvT = t4("cAkvT", 1)[0]; S.ArkT = t4("cArkT", 1)[0]
                S.X = t4("cX", 2, 128); S.Btok = t4("cBtok", 1)[0]; S.Ktok = t4("cKtok", 1)[0]
                S.RAT = t4("cRAT", 1)[0]; S.McT = t4("cMcT", 1)[0]; S.NcS = t4("cNcS", 1)[0]; S.DG = t4("cDG", 1)[0]
                S.ysb = [kb.sb(f"cysb{si}_{d}", [64, 256]) for d in range(2)]
                S.banks = PS[4 * si:4 * si + 4]
                return S
            SETS = [alloc_set(0), alloc_set(1)]
            border = [3, 2, 1, 0] + list(range(NCH - 1, 3, -1))
            DP = [(d, pr) for d in range(2) for pr in range(2)]
            HP = [slice(0, 64), slice(64, 128)]

            def mm_all(ps, col_fn, lhs_fn, rhs_fn, reads, start=True, stop=True, w=None):
                for dp in range(4):
                    for hp in range(2):
                        r = HP[hp]
                        c0, c1 = col_fn(dp)
                        kb.op("pe", lambda e, dp=dp, r=r, c0=c0, c1=c1: e.matmul(ps[r, c0:c1], lhs_fn(dp, r), rhs_fn(dp, r), start=start, stop=stop),
                              reads=reads, writes=[ps])

            def chunk_gen(ci, S):
                cidx = [ci, border[ci]]
                IN, VTK = S.IN, S.VTK
                for d in range(2):
                    c0 = cidx[d] * CH
                    for n in names:
                        src = RS[n if n in ("AL", "RT") else f"{n}{d}"]
                        kb.dma("sp" if d == 0 else "pool", IN[n][:, 2 * d:2 * d + 2, :],
                               src.t.rearrange("(pr q) t -> q pr t", q=128)[:, :, c0:c0 + CH], reads=[src], writes=[IN[n]])
                    for hp in range(2):
                        kb.dma("sp" if d == 0 else "pool", VTK[HP[hp], 2 * d:2 * d + 2, :],
                               RS["VTOK"][c0:c0 + CH, :].rearrange("t (pr hp v) -> t pr hp v", pr=2, hp=2)[:, :, hp, :],
                               reads=[RS["VTOK"]], writes=[VTK])
                al, lw, be, kd, rt, vt = IN["AL"], IN["W"], IN["B"], IN["KD"], IN["RT"], VTK
                CS, TOT, TMP, Epos, Eneg, Eprev, Etot, Wtot = S.CS, S.TOT, S.TMP, S.Epos, S.Eneg, S.Eprev, S.Etot, S.Wtot
                Ab, Bb, Kb, Rb, Bt, Kt, Q, P, ArbT, AkvT, ArkT = S.Ab, S.Bb, S.Kb, S.Rb, S.Bt, S.Kt, S.Q, S.P, S.ArbT, S.AkvT, S.ArkT
                X, Btok, Ktok, RAT, McT, NcS, DG, ysb = S.X, S.Btok, S.Ktok, S.RAT, S.McT, S.NcS, S.DG, S.ysb
                yield
                for dp in range(4):
                    kb.op("dve", lambda e, dp=dp: e.tensor_tensor_scan(CS[:, dp, :], ones[:, :], lw[:, dp, :], 0.0, ALU.mult, ALU.add),
                          reads=[ones, lw], writes=[CS])
                kb.op("dve", lambda e: e.tensor_copy(TOT[:, :], CS[:, :, CH - 1]), reads=[CS], writes=[TOT])
                kb.op("dve", lambda e: e.tensor_tensor(CS[:, 2:4, :], lw[:, 2:4, :], CS[:, 2:4, :], ALU.subtract), reads=[lw, CS], writes=[CS])
                kb.op("dve", lambda e: e.tensor_tensor(CS[:, 2:4, :], CS[:, 2:4, :], TOT[:, 2:4].unsqueeze(2).broadcast_to([128, 2, CH]), ALU.add),
                      reads=[CS, TOT], writes=[CS])
                kb.op("act", lambda e: e.activation(Epos[:], CS[:], AF.Exp), reads=[CS], writes=[Epos])
                kb.op("act", lambda e: e.activation(Eneg[:], CS[:], AF.Exp, scale=-1.0), reads=[CS], writes=[Eneg])
                kb.op("pool", lambda e: e.tensor_tensor(TMP[:], CS[:], lw[:], ALU.subtract), reads=[CS, lw], writes=[TMP])
                kb.op("act", lambda e: e.activation(Eprev[:], TMP[:], AF.Exp), reads=[TMP], writes=[Eprev])
                kb.op("dve", lambda e: e.tensor_tensor(Etot[:], TOT[:, :].unsqueeze(2).broadcast_to([128, 4, CH]), CS[:], ALU.subtract),
                      reads=[TOT, CS], writes=[Etot])
                kb.op("act", lambda e: e.activation(Etot[:], Etot[:], AF.Exp), reads=[Etot], writes=[Etot])
                kb.op("act", lambda e: e.activation(Wtot[:], TOT[:], AF.Exp), reads=[TOT], writes=[Wtot])
                kb.op("dve", lambda e: e.tensor_tensor(Ab[:], al[:], Eprev[:], ALU.mult), reads=[al, Eprev], writes=[Ab])
                kb.op("pool", lambda e: e.tensor_tensor(Bb[:], be[:], Eneg[:], ALU.mult), reads=[be, Eneg], writes=[Bb])
                kb.op("dve", lambda e: e.tensor_tensor(Kb[:], kd[:], Eneg[:], ALU.mult), reads=[kd, Eneg], writes=[Kb])
                kb.op("pool", lambda e: e.tensor_tensor(Rb[:], rt[:], Epos[:], ALU.mult), reads=[rt, Epos], writes=[Rb])
                kb.op("dve", lambda e: e.tensor_tensor(Bt[:], be[:], Etot[:], ALU.mult), reads=[be, Etot], writes=[Bt])
                kb.op("pool", lambda e: e.tensor_tensor(Kt[:], kd[:], Etot[:], ALU.mult), reads=[kd, Etot], writes=[Kt])
                yield
                PA, PB, PC, PT1 = S.banks
                PD, PX, PPQ, PE_ = PA, PB, PC, PT1
                mm_all(PA, lambda dp: (dp * 128, dp * 128 + 64), lambda dp, r: Bb[r, dp, :], lambda dp, r: Ab[r, dp, :], [Bb, Ab])
                mm_all(PA, lambda dp: (dp * 128 + 64, dp * 128 + 128), lambda dp, r: Bb[r, dp, :], lambda dp, r: Rb[r, dp, :], [Bb, Rb])
                mm_all(PB, lambda dp: (dp * 128, dp * 128 + 64), lambda dp, r: Kb[r, dp, :], lambda dp, r: Ab[r, dp, :], [Kb, Ab])
                mm_all(PB, lambda dp: (dp * 128 + 64, dp * 128 + 128), lambda dp, r: Kb[r, dp, :], lambda dp, r: Rb[r, dp, :], [Kb, Rb])
                mm_all(PC, lambda dp: (dp * 64, dp * 64 + 64), lambda dp, r: Ab[r, dp, :], lambda dp, r: Bb[r, dp, :], [Ab, Bb])
                q0, p0 = Q[0], P[0]
                pav = PA[:, :].rearrange("p (dp x) -> p dp x", dp=4)
                pbv = PB[:, :].rearrange("p (dp x) -> p dp x", dp=4)
                def mk(m):
                    return m[:, :, :].unsqueeze(2).broadcast_to([128, 2, 2, 64])
                def v4(ap):
                    return ap.rearrange("p (d pr) x -> p d pr x", d=2)
                kb.op("dve", lambda e: e.tensor_tensor(v4(q0[:]), v4(pav[:, :, 0:64]), mk(MTs), ALU.mult), reads=[PA, MTs], writes=[q0])
                kb.op("dve", lambda e: e.tensor_tensor(v4(ArbT[:]), v4(pav[:, :, 64:128]), mk(MTi), ALU.mult), reads=[PA, MTi], writes=[ArbT])
                kb.op("dve", lambda e: e.tensor_tensor(v4(AkvT[:]), v4(pbv[:, :, 0:64]), mk(MTs), ALU.mult), reads=[PB, MTs], writes=[AkvT])
                kb.op("dve", lambda e: e.tensor_tensor(v4(ArkT[:]), v4(pbv[:, :, 64:128]), mk(MTi), ALU.mult), reads=[PB, MTi], writes=[ArkT])
                kb.op("dve", lambda e: e.tensor_tensor(v4(p0[:]), v4(PC[:, 0:256].rearrange("p (dp x) -> p dp x", dp=4)), mk(Ms), ALU.mult),
                      reads=[PC, Ms], writes=[p0])
                yield
                def idb(r):
                    return ident_f[r, r.start:r.start + 64]
                mm_all(PT1, lambda dp: (dp * 128, dp * 128 + 64), lambda dp, r: Ab[r, dp, :], lambda dp, r: idb(r), [Ab, ident_f])
                mm_all(PT1, lambda dp: (dp * 128 + 64, dp * 128 + 128), lambda dp, r: Bt[r, dp, :], lambda dp, r: idb(r), [Bt, ident_f])
                mm_all(PC, lambda dp: (256 + dp * 64, 256 + dp * 64 + 64), lambda dp, r: Kt[r, dp, :], lambda dp, r: idb(r), [Kt, ident_f])
                x0 = X[0]
                pt1v = PT1[:, :].rearrange("p (dp x) -> p dp x", dp=4)
                kb.op("act", lambda e: e.copy(x0[:, :, 0:64], pt1v[:, :, 0:64]), reads=[PT1], writes=[x0])
                kb.op("act", lambda e: e.copy(Btok[:], pt1v[:, :, 64:128]), reads=[PT1], writes=[Btok])
                kb.op("act", lambda e: e.copy(Ktok[:], PC[:, 256:512].rearrange("p (dp x) -> p dp x", dp=4)), reads=[PC], writes=[Ktok])
                yield
                mm_all(PD, lambda dp: (dp * 64, dp * 64 + 64), lambda dp, r: AkvT[r, dp, :], lambda dp, r: vt[r, dp, :], [AkvT, vt])
                kb.op("act", lambda e: e.copy(x0[:, :, 64:128], PD[:, 0:256].rearrange("p (dp x) -> p dp x", dp=4)), reads=[PD], writes=[x0])
                yield
                qc, pc, xc = Q[0], P[0], X[0]
                for j in range(6):
                    qn, pn, xn = Q[(j + 1) % 2], P[(j + 1) % 2], X[(j + 1) % 2]
                    mm_all(PX, lambda dp: (dp * 128, dp * 128 + 128), lambda dp, r: qc[r, dp, :], lambda dp, r: xc[r, dp, :], [qc, xc])
                    kb.op("dve", lambda e, xn=xn, xc=xc: e.tensor_tensor(xn[:], xc[:], PX[:, :].rearrange("p (dp x) -> p dp x", dp=4), ALU.add),
                          reads=[xc, PX], writes=[xn])
                    if j < 5:
                        mm_all(PPQ, lambda dp: (dp * 64, dp * 64 + 64), lambda dp, r: qc[r, dp, :], lambda dp, r: pc[r, dp, :], [qc, pc])
                        mm_all(PPQ, lambda dp: (256 + dp * 64, 256 + dp * 64 + 64), lambda dp, r: pc[r, dp, :], lambda dp, r: qc[r, dp, :], [qc, pc])
                        kb.op("act", lambda e, pn=pn: e.copy(pn[:], PPQ[:, 0:256].rearrange("p (dp x) -> p dp x", dp=4)), reads=[PPQ], writes=[pn])
                        kb.op("act", lambda e, qn=qn: e.copy(qn[:], PPQ[:, 256:512].rearrange("p (dp x) -> p dp x", dp=4)), reads=[PPQ], writes=[qn])
                    qc, pc, xc = qn, pn, xn
                    yield
                yield
                mm_all(PD, lambda dp: (256 + dp * 64, 256 + dp * 64 + 64), lambda dp, r: xc[r, dp, 0:64], lambda dp, r: ArbT[r, dp, :], [xc, ArbT])
                kb.op("dve", lambda e: e.tensor_tensor(RAT[:], Rb[:], PD[:, 256:512].rearrange("p (dp x) -> p dp x", dp=4), ALU.add),
                      reads=[Rb, PD], writes=[RAT])
                mm_all(PE_, lambda dp: (dp * 64, dp * 64 + 64), lambda dp, r: xc[r, dp, 0:64], lambda dp, r: Btok[r, dp, :], [xc, Btok])
                kb.op("pool", lambda e: e.tensor_tensor(DG[:], id2[:, :].unsqueeze(1).broadcast_to([128, 4, 64]),
                                                        Wtot[:, :].unsqueeze(2).broadcast_to([128, 4, 64]), ALU.mult), reads=[id2, Wtot], writes=[DG])
                kb.op("dve", lambda e: e.tensor_tensor(McT[:], DG[:], PE_[:, 0:256].rearrange("p (dp x) -> p dp x", dp=4), ALU.add),
                      reads=[DG, PE_], writes=[McT])
                for dp in range(4):
                    c0 = 256 + dp * 64
                    for hp in range(2):
                        r = HP[hp]
                        kb.op("pe", lambda e, dp=dp, r=r, c0=c0: e.matmul(PE_[r, c0:c0 + 64], Btok[r, dp, :], xc[r, dp, 64:128], start=True, stop=False),
                              reads=[Btok, xc], writes=[PE_])
                    for hp in range(2):
                        r = HP[hp]
                        kb.op("pe", lambda e, dp=dp, r=r, c0=c0: e.matmul(PE_[r, c0:c0 + 64], Ktok[r, dp, :], vt[r, dp, :], start=False, stop=True),
                              reads=[Ktok, vt], writes=[PE_])
                kb.op("act", lambda e: e.copy(NcS[:], PE_[:, 256:512].rearrange("p (dp x) -> p dp x", dp=4)), reads=[PE_], writes=[NcS])
                yield
                PYs = [PA, PB]
                for dp in range(4):
                    c0 = dp * 64
                    for k_, (lh, rh, rd) in enumerate(((lambda r: ST[r, dp, :], lambda r: RAT[r, dp, :], [ST, RAT]),
                                                       (lambda r: xc[r, dp, 64:128], lambda r: ArbT[r, dp, :], [xc, ArbT]),
                                                       (lambda r: vt[r, dp, :], lambda r: ArkT[r, dp, :], [vt, ArkT]))):
                        for hp in range(2):
                            r = HP[hp]
                            PY = PYs[hp]
                            kb.op("pe", lambda e, r=r, c0=c0, PY=PY, lh=lh, rh=rh, k_=k_: e.matmul(
                                PY[0:64, c0:c0 + 64], lh(r), rh(r), start=(k_ == 0), stop=(k_ == 2)), reads=rd, writes=[PY])
                for d in range(2):
                    c0 = cidx[d] * CH
                    yv = ysb[d][:, :].rearrange("v (pr hp t) -> v pr hp t", pr=2, hp=2)
                    for hp in range(2):
                        kb.op("act", lambda e, d=d, hp=hp, yv=yv: e.copy(
                            yv[:, :, hp, :], PYs[hp][0:64, d * 128:(d + 1) * 128].rearrange("v (pr t) -> v pr t", pr=2)), reads=[PYs[hp]], writes=[ysb[d]])
                    dst = RS["YF" if d == 0 else "YB"]
                    kb.dma("sp", dst.t.rearrange("(h v) t -> v h t", v=64)[:, :, c0:c0 + CH],
                           ysb[d][:, :].rearrange("v (h t) -> v h t", h=4), reads=[ysb[d]], writes=[Buf()])
                PSS = PT1
                mm_all(PSS, lambda dp: (dp * 64, dp * 64 + 64), lambda dp, r: McT[r, dp, :], lambda dp, r: ST[r, dp, :], [McT, ST])
                kb.op("dve", lambda e: e.tensor_tensor(ST[:], NcS[:], PSS[:, 0:256].rearrange("p (dp x) -> p dp x", dp=4), ALU.add),
                      reads=[NcS, PSS], writes=[ST])


            def lockstep(gens):
                gens = list(gens)
                while gens:
                    nxt = []
                    for g_ in gens:
                        try:
                            next(g_)
                            nxt.append(g_)
                        except StopIteration:
                            pass
                    gens = nxt
            for ci in range(0, NCH, 2):
                lockstep([chunk_gen(ci, SETS[0]), chunk_gen(ci + 1, SETS[1])])

    def phase_rwkv_chunked2(l):
        CH = 64
        NCH = T // CH
        with kb.scope():
            def ldc(nm, shape):
                t = kb.sb("k" + nm, shape)
                kb.dma("sp", t[:], CT[nm].t, reads=[CT[nm]], writes=[t])
                return t
            MsB = ldc("rw_msb", [128, 2, 128]); MTsB = ldc("rw_mtsb", [128, 2, 128]); MTi = ldc("rw_mti", [128, 2, 64])
            identr = kb.sb("cidr", [128, 128], F32R)
            kb.op("dve", lambda e: e.tensor_copy(identr[:], ident_f[:]), reads=[ident_f], writes=[identr])
            ones = kb.sb("rones", [128, 64])
            kb.op("pool", lambda e: e.memset(ones[:], 1.0), writes=[ones])
            def bd(nm, n=1, dt=F32R):
                ts = [kb.sb(f"{nm}{i}", [128, 4, 128], dt) for i in range(n)]
                for t in ts:
                    kb.op("pool", lambda e, t=t: e.memset(t[:].bitcast(F32) if dt == F32R else t[:], 0.0), writes=[t])
                return ts
            def t4(nm, n=1, w=64, dt=F32):
                return [kb.sb(f"{nm}{i}", [128, 4, w], dt) for i in range(n)]
            f32 = lambda ap: ap.bitcast(F32)
            names = ("AL", "W", "B", "KD", "RT")
            IN = {n: t4("di" + n, 2) for n in names}
            VT = bd("dVT", 2, F32)
            VTr = bd("dVTr")[0]
            ST = bd("dST")[0]
            CS = t4("dCS")[0]; TOT = kb.sb("dTOT", [128, 4]); TMP = t4("dTMP")[0]
            Epos = t4("dEp")[0]; Eneg = t4("dEn")[0]; Eprev = t4("dEv")[0]; Etot = t4("dEt")[0]; Wtot = kb.sb("dWt", [128, 4])
            Ab = bd("dAb")[0]; Bb = bd("dBb")[0]; Kb = bd("dKb")[0]; Bt = bd("dBt")[0]; Kt = bd("dKt")[0]
            Rb = t4("dRb", 1, 64, F32R)[0]
            Q = bd("dQ", 2); P = bd("dP", 2); AkvT = bd("dAkvT")[0]
            ArbT = t4("dArbT", 1, 64, F32R)[0]; ArkT = t4("dArkT", 1, 64, F32R)[0]; RAT = t4("dRAT", 1, 64, F32R)[0]
            X = [kb.sb(f"dX{i}", [128, 4, 256], F32R) for i in range(2)]
            Btok = bd("dBtok")[0]; Ktok = bd("dKtok")[0]; McT = bd("dMcT")[0]
            NcS = bd("dNcS", 1, F32)[0]; DG = bd("dDG", 1, F32)[0]
            ysb = [kb.sb(f"dysb{d}", [128, 2, 64]) for d in range(2)]
            border = [3, 2, 1, 0] + list(range(NCH - 1, 3, -1))
            H0, H1 = slice(0, 64), slice(64, 128)
            B0, B1, B2, B3, B4, B5, B6, B7 = PS

            def mm4(ps, c0, w, lhs, rhs, reads, start=True, stop=True):
                for dp in range(4):
                    kb.op("pe", lambda e, dp=dp: e.matmul(ps[:, c0 + dp * w:c0 + (dp + 1) * w], lhs(dp), rhs(dp), start=start, stop=stop),
                          reads=reads, writes=[ps])

            def v4(ap):
                return ap.rearrange("p (d pr) x -> p d pr x", d=2)

            def mk(m, w):
                return m[:, :, :].unsqueeze(2).broadcast_to([128, 2, 2, w])

            def pv(ps, c0, w):
                return ps[:, c0:c0 + 4 * w].rearrange("p (dp x) -> p dp x", dp=4)

            for ci in range(NCH):
                cidx = [ci, border[ci]]
                i2 = ci % 2
                vt = VT[i2]
                for d in range(2):
                    c0 = cidx[d] * CH
                    q_ = "sp" if d == 0 else "pool"
                    for n in names:
                        src = RS[n if n in ("AL", "RT") else f"{n}{d}"]
                        kb.dma(q_, IN[n][i2][:, 2 * d:2 * d + 2, :],
                               src.t.rearrange("(pr q) t -> q pr t", q=128)[:, :, c0:c0 + CH], reads=[src], writes=[IN[n][i2]])
                    for hp in range(2):
                        kb.dma(q_, vt[hp * 64:(hp + 1) * 64, 2 * d:2 * d + 2, hp * 64:(hp + 1) * 64],
                               RS["VTOK"][c0:c0 + CH, :].rearrange("t (pr hp v) -> t pr hp v", pr=2, hp=2)[:, :, hp, :],
                               reads=[RS["VTOK"]], writes=[vt])
                al, lw, be, kd, rt = IN["AL"][i2], IN["W"][i2], IN["B"][i2], IN["KD"][i2], IN["RT"][i2]
                kb.op("act", lambda e: e.copy(VTr[:], vt[:]), reads=[vt], writes=[VTr])
                for dp in range(4):
                    kb.op("dve", lambda e, dp=dp: e.tensor_tensor_scan(CS[:, dp, :], ones[:, :], lw[:, dp, :], 0.0, ALU.mult, ALU.add),
                          reads=[ones, lw], writes=[CS])
                kb.op("dve", lambda e: e.tensor_copy(TOT[:, :], CS[:, :, CH - 1]), reads=[CS], writes=[TOT])
                kb.op("dve", lambda e: e.tensor_tensor(CS[:, 2:4, :], lw[:, 2:4, :], CS[:, 2:4, :], ALU.subtract), reads=[lw, CS], writes=[CS])
                kb.op("dve", lambda e: e.tensor_tensor(CS[:, 2:4, :], CS[:, 2:4, :], TOT[:, 2:4].unsqueeze(2).broadcast_to([128, 2, CH]), ALU.add),
                      reads=[CS, TOT], writes=[CS])
                kb.op("act", lambda e: e.activation(Epos[:], CS[:], AF.Exp), reads=[CS], writes=[Epos])
                kb.op("act", lambda e: e.activation(Eneg[:], CS[:], AF.Exp, scale=-1.0), reads=[CS], writes=[Eneg])
                kb.op("pool", lambda e: e.tensor_tensor(TMP[:], CS[:], lw[:], ALU.subtract), reads=[CS, lw], writes=[TMP])
                kb.op("act", lambda e: e.activation(Eprev[:], TMP[:], AF.Exp), reads=[TMP], writes=[Eprev])
                kb.op("pool", lambda e: e.tensor_tensor(Etot[:], TOT[:, :].unsqueeze(2).broadcast_to([128, 4, CH]), CS[:], ALU.subtract),
                      reads=[TOT, CS], writes=[Etot])
                kb.op("act", lambda e: e.activation(Etot[:], Etot[:], AF.Exp), reads=[Etot], writes=[Etot])
                kb.op("act", lambda e: e.activation(Wtot[:], TOT[:], AF.Exp), reads=[TOT], writes=[Wtot])
                for k_, (dst, a_, b_) in enumerate(((Ab, al, Eprev), (Bb, be, Eneg), (Kb, kd, Eneg), (Bt, be, Etot), (Kt, kd, Etot))):
                    for hi, r in enumerate((H0, H1)):
                        eng = "dve" if (k_ + hi) % 2 == 0 else "pool"
                        kb.op(eng, lambda e, dst=dst, a_=a_, b_=b_, r=r: e.tensor_tensor(dst[r, :, r.start:r.start + 64], a_[r, :, :], b_[r, :, :], ALU.mult),
                              reads=[a_, b_], writes=[dst])
                kb.op("pool", lambda e: e.tensor_tensor(Rb[:], rt[:], Epos[:], ALU.mult), reads=[rt, Epos], writes=[Rb])
                mm4(B0, 0, 128, lambda dp: Bb[:, dp, :], lambda dp: Ab[:, dp, :], [Bb, Ab])
                mm4(B1, 0, 128, lambda dp: Kb[:, dp, :], lambda dp: Ab[:, dp, :], [Kb, Ab])
                mm4(B2, 0, 128, lambda dp: Ab[:, dp, :], lambda dp: Bb[:, dp, :], [Ab, Bb])
                mm4(B3, 0, 64, lambda dp: Bb[:, dp, :], lambda dp: Rb[:, dp, :], [Bb, Rb])
                mm4(B3, 256, 64, lambda dp: Kb[:, dp, :], lambda dp: Rb[:, dp, :], [Kb, Rb])
                q0, p0, x0 = Q[0], P[0], X[0]
                kb.op("dve", lambda e: e.tensor_tensor(v4(q0[:]), v4(pv(B0, 0, 128)), mk(MTsB, 128), ALU.mult), reads=[B0, MTsB], writes=[q0])
                kb.op("dve", lambda e: e.tensor_tensor(v4(AkvT[:]), v4(pv(B1, 0, 128)), mk(MTsB, 128), ALU.mult), reads=[B1, MTsB], writes=[AkvT])
                kb.op("dve", lambda e: e.tensor_tensor(v4(p0[:]), v4(pv(B2, 0, 128)), mk(MsB, 128), ALU.mult), reads=[B2, MsB], writes=[p0])
                kb.op("dve", lambda e: e.tensor_tensor(v4(ArbT[:]), v4(pv(B3, 0, 64)), mk(MTi, 64), ALU.mult), reads=[B3, MTi], writes=[ArbT])
                kb.op("dve", lambda e: e.tensor_tensor(v4(ArkT[:]), v4(pv(B3, 256, 64)), mk(MTi, 64), ALU.mult), reads=[B3, MTi], writes=[ArkT])
                mm4(B4, 0, 128, lambda dp: Ab[:, dp, :], lambda dp: identr[:, :], [Ab, identr])
                mm4(B6, 0, 128, lambda dp: Bt[:, dp, :], lambda dp: identr[:, :], [Bt, identr])
                mm4(B7, 0, 128, lambda dp: Kt[:, dp, :], lambda dp: identr[:, :], [Kt, identr])
                mm4(B5, 0, 128, lambda dp: AkvT[:, dp, :], lambda dp: VTr[:, dp, :], [AkvT, VTr])
                kb.op("act", lambda e: e.copy(x0[:, :, 0:128], pv(B4, 0, 128)), reads=[B4], writes=[x0])
                kb.op("act", lambda e: e.copy(Btok[:], pv(B6, 0, 128)), reads=[B6], writes=[Btok])
                kb.op("act", lambda e: e.copy(Ktok[:], pv(B7, 0, 128)), reads=[B7], writes=[Ktok])
                kb.op("act", lambda e: e.copy(x0[:, :, 128:256], pv(B5, 0, 128)), reads=[B5], writes=[x0])
                qc, pc, xc = Q[0], P[0], X[0]
                for j in range(6):
                    qn, pn, xn = Q[(j + 1) % 2], P[(j + 1) % 2], X[(j + 1) % 2]
                    for hf, bank in ((0, B4), (1, B5)):
                        for dq in range(2):
                            dp = hf * 2 + dq
                            kb.op("pe", lambda e, dp=dp, dq=dq, bank=bank: e.matmul(bank[:, dq * 256:(dq + 1) * 256], qc[:, dp, :], xc[:, dp, :],
                                                                                    start=True, stop=True), reads=[qc, xc], writes=[bank])
                        kb.op("dve", lambda e, hf=hf, bank=bank, xn=xn, xc=xc: e.tensor_tensor(
                            xn[:, 2 * hf:2 * hf + 2, :], f32(xc[:, 2 * hf:2 * hf + 2, :]), bank[:, :].rearrange("p (dq x) -> p dq x", dq=2), ALU.add),
                            reads=[xc, bank], writes=[xn])
                    if j < 5:
                        mm4(B6, 0, 128, lambda dp: qc[:, dp, :], lambda dp: pc[:, dp, :], [qc, pc])
                        mm4(B7, 0, 128, lambda dp: pc[:, dp, :], lambda dp: qc[:, dp, :], [qc, pc])
                        kb.op("act", lambda e, pn=pn: e.copy(pn[:], pv(B6, 0, 128)), reads=[B6], writes=[pn])
                        kb.op("act", lambda e, qn=qn: e.copy(qn[:], pv(B7, 0, 128)), reads=[B7], writes=[qn])
                    qc, pc, xc = qn, pn, xn
                mm4(B3, 0, 64, lambda dp: xc[:, dp, 0:128], lambda dp: ArbT[:, dp, :], [xc, ArbT])
                kb.op("dve", lambda e: e.tensor_tensor(RAT[:], f32(Rb[:]), pv(B3, 0, 64), ALU.add), reads=[Rb, B3], writes=[RAT])
                mm4(B2, 0, 128, lambda dp: xc[:, dp, 0:128], lambda dp: Btok[:, dp, :], [xc, Btok])
                kb.op("pool", lambda e: e.tensor_tensor(DG[:], ident_f[:, :].unsqueeze(1).broadcast_to([128, 4, 128]),
                                                        Wtot[:, :].unsqueeze(2).broadcast_to([128, 4, 128]), ALU.mult), reads=[ident_f, Wtot], writes=[DG])
                kb.op("dve", lambda e: e.tensor_tensor(McT[:], DG[:], pv(B2, 0, 128), ALU.add), reads=[DG, B2], writes=[McT])
                for dp in range(4):
                    kb.op("pe", lambda e, dp=dp: e.matmul(B0[:, dp * 128:(dp + 1) * 128], Btok[:, dp, :], xc[:, dp, 128:256], start=True, stop=False),
                          reads=[Btok, xc], writes=[B0])
                    kb.op("pe", lambda e, dp=dp: e.matmul(B0[:, dp * 128:(dp + 1) * 128], Ktok[:, dp, :], VTr[:, dp, :], start=False, stop=True),
                          reads=[Ktok, VTr], writes=[B0])
                kb.op("act", lambda e: e.copy(NcS[:], pv(B0, 0, 128)), reads=[B0], writes=[NcS])
                for dp in range(4):
                    c0 = dp * 64
                    kb.op("pe", lambda e, dp=dp, c0=c0: e.matmul(B1[:, c0:c0 + 64], ST[:, dp, :], RAT[:, dp, :], start=True, stop=False),
                          reads=[ST, RAT], writes=[B1])
                    kb.op("pe", lambda e, dp=dp, c0=c0: e.matmul(B1[:, c0:c0 + 64], xc[:, dp, 128:256], ArbT[:, dp, :], start=False, stop=False),
                          reads=[xc, ArbT], writes=[B1])
                    kb.op("pe", lambda e, dp=dp, c0=c0: e.matmul(B1[:, c0:c0 + 64], VTr[:, dp, :], ArkT[:, dp, :], start=False, stop=True),
                          reads=[VTr, ArkT], writes=[B1])
                for d in range(2):
                    c0 = cidx[d] * CH
                    kb.op("act", lambda e, d=d: e.copy(ysb[d][:, :, :], B1[:, d * 128:(d + 1) * 128].rearrange("p (pr t) -> p pr t", pr=2)),
                          reads=[B1], writes=[ysb[d]])
                    dst = RS["YF" if d == 0 else "YB"]
                    kb.dma("sp", dst.t.rearrange("(pr q) t -> q pr t", q=128)[:, :, c0:c0 + CH], ysb[d][:, :, :], reads=[ysb[d]], writes=[Buf()])
                mm4(B6, 0, 128, lambda dp: McT[:, dp, :], lambda dp: ST[:, dp, :], [McT, ST])
                kb.op("dve", lambda e: e.tensor_tensor(ST[:], NcS[:], pv(B6, 0, 128), ALU.add), reads=[NcS, B6], writes=[ST])

    def phase_rwkv_out(l, with_ctx):
        with kb.scope():
            rk_ = colvec("rrk", W["rw_r_k"][l, :], W["rw_r_k"], [128, 2], "(j p) -> p j", p=128)
            lg_ = colvec("rlg", W["rw_ln_g"][l, :], W["rw_ln_g"], [128, 2], "(j p) -> p j", p=128)
            lb_ = colvec("rlb", W["rw_ln_b"][l, :], W["rw_ln_b"], [128, 2], "(j p) -> p j", p=128)
            nm = ("YF", "YB", "RT", "KD0", "KD1", "VT")
            tl = [{n: kb.sb(f"o{n}{i}", [128, 512]) for n in nm} for i in range(2)]
            sg = [kb.sb(f"osg{i}", [128, 512], BF16) for i in range(2)]
            ob = [kb.sb(f"oob{i}", [128, 512], BF16) for i in range(2)]
            wk = [[kb.sb(f"owk{k}{i}", [128, 512]) for k in range(3)] for i in range(2)]
            it = 0
            for pr in range(2):
                rows = slice(pr * 128, (pr + 1) * 128)
                for (t0, nt) in TCH:
                    if not with_ctx and t0 + nt <= C:
                        continue
                    t_, s_, o_, (a_, b_, c_) = tl[it % 2], sg[it % 2], ob[it % 2], wk[it % 2]
                    for k, n in enumerate(nm):
                        kb.dma("sp" if k % 2 == 0 else "pool", t_[n][:, 0:nt], RS[n][rows, t0:t0 + nt], reads=[RS[n]], writes=[t_[n]])
                    kb.dma("sp", s_[:, 0:nt], RS["SGT"][rows, t0:t0 + nt], reads=[RS["SGT"]], writes=[s_])
                    y = t_["YF"]
                    kb.op("dve", lambda e: e.tensor_tensor(y[:, 0:nt], y[:, 0:nt], t_["YB"][:, 0:nt], ALU.add), reads=[y, t_["YB"]], writes=[y])
                    p1, p2, p3 = PS[(3 * it) % 8], PS[(3 * it + 1) % 8], PS[(3 * it + 2) % 8]
                    kb.op("pe", lambda e: e.matmul(p1[:, 0:nt], blk64[:], y[:, 0:nt], start=True, stop=True), reads=[blk64, y], writes=[p1])
                    kb.op("dve", lambda e: e.scalar_tensor_tensor(a_[:, 0:nt], p1[:, 0:nt], -1.0 / 64, y[:, 0:nt], ALU.mult, ALU.add),
                          reads=[p1, y], writes=[a_])
                    kb.op("act", lambda e: e.activation(b_[:, 0:nt], a_[:, 0:nt], AF.Square), reads=[a_], writes=[b_])
                    kb.op("pe", lambda e: e.matmul(p2[:, 0:nt], blk64[:], b_[:, 0:nt], start=True, stop=True), reads=[blk64, b_], writes=[p2])
                    kb.op("dve", lambda e: e.tensor_scalar(b_[:, 0:nt], p2[:, 0:nt], 1.0 / 64, 64e-5, ALU.mult, ALU.add), reads=[p2], writes=[b_])
                    kb.op("act", lambda e: e.sqrt(b_[:, 0:nt], b_[:, 0:nt]), reads=[b_], writes=[b_])
                    kb.op("dve", lambda e: e.reciprocal(b_[:, 0:nt], b_[:, 0:nt]), reads=[b_], writes=[b_])
                    kb.op("dve", lambda e: e.tensor_tensor(a_[:, 0:nt], a_[:, 0:nt], b_[:, 0:nt], ALU.mult), reads=[a_, b_], writes=[a_])
                    kb.op("dve", lambda e: e.tensor_scalar(a_[:, 0:nt], a_[:, 0:nt], lg_[:, pr:pr + 1], lb_[:, pr:pr + 1], ALU.mult, ALU.add),
                          reads=[a_, lg_, lb_], writes=[a_])
                    kb.op("pool", lambda e: e.tensor_tensor(c_[:, 0:nt], t_["KD0"][:, 0:nt], t_["KD1"][:, 0:nt], ALU.add),
                          reads=[t_["KD0"], t_["KD1"]], writes=[c_])
                    kb.op("dve", lambda e: e.scalar_tensor_tensor(c_[:, 0:nt], t_["RT"][:, 0:nt], rk_[:, pr:pr + 1], c_[:, 0:nt], ALU.mult, ALU.mult),
                          reads=[t_["RT"], rk_, c_], writes=[c_])
                    kb.op("pe", lambda e: e.matmul(p3[:, 0:nt], blk64[:], c_[:, 0:nt], start=True, stop=True), reads=[blk64, c_], writes=[p3])
                    kb.op("dve", lambda e: e.tensor_tensor(c_[:, 0:nt], p3[:, 0:nt], t_["VT"][:, 0:nt], ALU.mult), reads=[p3, t_["VT"]], writes=[c_])
                    kb.op("dve", lambda e: e.tensor_tensor(a_[:, 0:nt], a_[:, 0:nt], c_[:, 0:nt], ALU.add), reads=[a_, c_], writes=[a_])
                    kb.op("pool", lambda e: e.tensor_tensor(o_[:, 0:nt], a_[:, 0:nt], s_[:, 0:nt], ALU.mult), reads=[a_, s_], writes=[o_])
                    kb.dma("sp", mixT[256 + pr * 128:256 + (pr + 1) * 128, t0:t0 + nt], o_[:, 0:nt], reads=[o_], writes=[Buf()])
                    it += 1


    SEGS = {"L": dict(Ls=L, A=32, cbw=32, off=C, ut="UTL"), "C": dict(Ls=C, A=2, cbw=64, off=0, ut="UTC")}

    def phase_hyena_prep(l, with_ctx):
        with kb.scope():
            stage = kb.sb("hstage", [128, 8, 128])
            wts = [kb.sb(f"hwt{i}", [128, 8, 128], BF16) for i in range(2)]
            cw = kb.sb("hcw", [128, 6, 3])
            for k in range(3):
                kb.dma("sp", cw[:, :, k], W["hy_conv"][l, k, :].rearrange("(j p) -> p j", p=128), reads=[W["hy_conv"]], writes=[cw], slow=True)
            ncw = kb.sb("hncw", [128, 6, 3])
            kb.op("dve", lambda e: e.tensor_scalar(ncw[:], cw[:], -1.0, None, ALU.mult), reads=[cw], writes=[ncw])
            Zraw = kb.sb("hZraw", [128, T + 2])
            Zout = kb.sb("hZout", [128, T])
            kb.op("pool", lambda e: e.memset(Zraw[:, 0:1], 0.0), writes=[Zraw])
            kb.op("pool", lambda e: e.memset(Zraw[:, T + 1:T + 2], 0.0), writes=[Zraw])
            ub = kb.sb("hub", [128, 32 * 128])
            tG = [kb.sb(f"htG{i}", [128, 512], BF16) for i in range(2)]
            for oi, jt in enumerate(range(8)):
                wt = wts[oi % 2]
                c0 = HY0 + jt * 128 if jt < 6 else HYG0 + (jt - 6) * 128
                load_w(l, wt, c0, 128, stage)
                if jt >= 6:
                    for ci, (t0, nt) in enumerate(TCH):
                        p = PS[ci % 4]
                        proj_fm(p, wt, 0, 128, t0, nt)
                        g = tG[ci % 2]
                        kb.op("act", lambda e, p=p, g=g, nt=nt: e.activation(g[:, 0:nt], p[:, 0:nt], AF.Silu), reads=[p], writes=[g])
                        kb.dma("sp", HS["SG"][(jt - 6) * 128:(jt - 5) * 128, t0:t0 + nt], g[:, 0:nt], reads=[g], writes=[Buf()])
                    continue
                conv_tile(l, wt, cw, ncw, jt, Zraw, Zout)
                arr, half = jt // 2, jt % 2
                for sn in (("L", "C") if with_ctx else ("L",)):
                    sg = SEGS[sn]
                    A, cbw, off = sg["A"], sg["cbw"], sg["off"]
                    G = 128 // A
                    ncg = 128 // G
                    ubv = ub[:, 0:A * 128].rearrange("p (g a c) -> p g a c", g=ncg, a=A)
                    for a in range(A):
                        p = PS[4 + (a // 4) % 4]
                        kb.op("pe", lambda e, p=p, a=a, A=A, off=off: e.transpose(
                            p[:, (a % 4) * 128:(a % 4 + 1) * 128], Zout[:, off + a:off + a + 127 * A + 1:A], ident_f[:]),
                            reads=[Zout, ident_f], writes=[p])
                        if a % 4 == 3 or a == A - 1:
                            a0 = (a // 4) * 4
                            na = a - a0 + 1
                            kb.op("act", lambda e, p=p, a0=a0, na=na, G=G: e.copy(
                                ubv[:, :, a0:a0 + na, :], p[:, 0:na * 128].rearrange("p (a g c) -> p g a c", a=na, c=G)), reads=[p], writes=[ub])
                    nb = 128 // cbw
                    bsz = A * cbw
                    for b in range(nb):
                        dst = HS[sg["ut"]][arr, half * nb + b, :, :]
                        kb.dma("sp" if b % 2 == 0 else "pool", dst, ub[:, b * bsz:(b + 1) * bsz], reads=[ub], writes=[Buf()])

    def cmul(dre, dim_, sre, sim, tre, tim, conj, srcb, tabb, dstb, tmp):
        t1, t2 = tmp
        sh = tuple(slice(None) for _ in range(1))
        kb.op("dve", lambda e: e.tensor_tensor(t1, sre, tre, ALU.mult), reads=srcb + tabb, writes=[dstb[2]])
        kb.op("dve", lambda e: e.tensor_tensor(t2, sim, tim, ALU.mult), reads=srcb + tabb, writes=[dstb[3]])
        kb.op("pool", lambda e: e.tensor_tensor(dre, t1, t2, ALU.add if conj else ALU.subtract), reads=[dstb[2], dstb[3]], writes=[dstb[0]])
        kb.op("dve", lambda e: e.tensor_tensor(t1, sim, tre, ALU.mult), reads=srcb + tabb + [dstb[0]], writes=[dstb[2]])
        kb.op("dve", lambda e: e.tensor_tensor(t2, sre, tim, ALU.mult), reads=srcb + tabb + [dstb[0]], writes=[dstb[3]])
        kb.op("pool", lambda e: e.tensor_tensor(dim_, t1, t2, ALU.subtract if conj else ALU.add), reads=[dstb[2], dstb[3]], writes=[dstb[1]])

    def phase_hyena_main(l, with_ctx):
        PI = math.pi
        with kb.scope():
            fw1 = kb.sb("hfw1", [33, 64])
            fw2 = kb.sb("hfw2", [64, 64])
            fw3 = kb.sb("hfw3", [64, 1024])
            kb.dma("sp", fw1[:], W["hy_fw1"][l, :, :], reads=[W["hy_fw1"]], writes=[fw1])
            kb.dma("sp", fw2[:], W["hy_fw2"][l, :, :], reads=[W["hy_fw2"]], writes=[fw2])
            kb.dma("sp", fw3[:], W["hy_fw3"][l, :, :], reads=[W["hy_fw3"]], writes=[fw3])
            fb1 = colvec("hfb1", W["hy_fb1"][l, :], W["hy_fb1"], [64, 1], "(d o) -> d o", o=1)
            fb2 = colvec("hfb2", W["hy_fb2"][l, :], W["hy_fb2"], [64, 1], "(d o) -> d o", o=1)
            frq = colvec("hfrq", W["hy_freq"][l, :], W["hy_freq"], [64, 1], "(d o) -> d o", o=1)
            brow = kb.sb("hbrow", [1, 512])
            kb.dma("sp", brow[:], W["hy_bias"][l, :, :].rearrange("o c -> (o c)").rearrange("(x n) -> x n", x=1), reads=[W["hy_bias"]], writes=[brow])
            for sn in (("L", "C") if with_ctx else ("L",)):
                sg = SEGS[sn]
                Ls, A, cbw, off = sg["Ls"], sg["A"], sg["cbw"], sg["off"]
                G = 128 // A
                N = 2 * Ls
                ngr = cbw // G
                nblk = 256 // cbw
                pre = f"hy{sn}_"
                with kb.scope():
                    def ld(nm, shape):
                        t = kb.sb("k" + nm, shape)
                        src = CT[pre + nm]
                        kb.dma("sp", t[:], src.t, reads=[src], writes=[t])
                        return t
                    def ldr(nm, shape):
                        tr = kb.sb("r" + nm, shape, F32R)
                        with kb.scope():
                            t32 = ld(nm, shape)
                            kb.op("dve", lambda e: e.tensor_copy(tr[:], t32[:]), reads=[t32], writes=[tr])
                        return tr
                    F256 = ldr("F256", [128, 2, 512]); TWC = ld("TWC", [128, 256]); TWS = ld("TWS", [128, 256])
                    Dre = ldr("Dre", [128, 128]); Dim = ldr("Dim", [128, 128]); nDim = ldr("nDim", [128, 128])
                    E1 = ldr("E1", [128, 256]); E2 = ldr("E2", [128, 256])
                    TW2C = ld("TW2C", [128, 2, 128]); TW2S = ld("TW2S", [128, 2, 128])
                    IC = ldr("IC", [128, 2, 128]); IS = ldr("IS", [128, 2, 128])
                    h2T = kb.sb("h2T", [64, N])
                    with kb.scope():
                        zT = kb.sb("zT", [33, N])
                        kb.dma("sp", zT[:], CT[pre + "zT"].t, reads=[CT[pre + "zT"]], writes=[zT])
                        h1T = kb.sb("h1T", [64, N])
                        arg = [kb.sb(f"harg{i}", [64, 512]) for i in range(2)]
                        wr = [kb.sb(f"hwr{i}", [64, 512]) for i in range(2)]
                        for (src, K_, wgt, bcol, dst) in ((zT, 33, fw1, fb1, h1T), (h1T, 64, fw2, fb2, h2T)):
                            for ci, n0 in enumerate(range(0, N, 512)):
                                p = PS[ci % 4]
                                ag = arg[ci % 2]
                                kb.op("pe", lambda e: e.matmul(p[0:64, :], wgt[0:K_, :], src[0:K_, n0:n0 + 512], start=True, stop=True),
                                      reads=[wgt, src], writes=[p])
                                kb.op("dve", lambda e: e.tensor_scalar(ag[:, :], p[0:64, :], bcol[:, 0:1], frq[:, 0:1], ALU.add, ALU.mult),
                                      reads=[p, bcol, frq], writes=[ag])
                                for _w in range(2):
                                    kb.op("dve", lambda e: e.tensor_scalar(wr[0][:, :], ag[:, :], PI, -2 * PI, ALU.is_gt, ALU.mult), reads=[ag], writes=[wr[0]])
                                    kb.op("dve", lambda e: e.tensor_scalar(wr[1][:, :], ag[:, :], -PI, 2 * PI, ALU.is_lt, ALU.mult), reads=[ag], writes=[wr[1]])
                                    kb.op("dve", lambda e: e.tensor_tensor(ag[:, :], ag[:, :], wr[0][:, :], ALU.add), reads=[ag, wr[0]], writes=[ag])
                                    kb.op("dve", lambda e: e.tensor_tensor(ag[:, :], ag[:, :], wr[1][:, :], ALU.add), reads=[ag, wr[1]], writes=[ag])
                                kb.op("act", lambda e: e.activation(dst[:, n0:n0 + 512], ag[:, :], AF.Sin), reads=[ag], writes=[dst])
                    KT1 = kb.sb("KT", [128, 2, ngr, A, G])
                    KTr1 = kb.sb("KTr", [128, 2, ngr, A, G], F32R)
                    KT = [KT1, KT1]
                    KTr = [KTr1, KTr1]
                    uvr = kb.sb("huvr", [128, ngr, A * G], F32R)
                    KS = [kb.sb(f"KS{o}", [128, ngr, 512]) for o in range(2)]
                    DECt = kb.sb("DECt", [128, 2, ngr, A, G])
                    part = kb.sb("hpart", [128, cbw])
                    rn = kb.sb("hrn", [128, cbw])
                    ex = kb.sb("hex", [1, cbw])
                    uv = kb.sb("huv", [128, ngr, A * G]); x1 = kb.sb("hx1", [128, ngr, A * G]); x2 = kb.sb("hx2", [128, ngr, A * G])
                    u2 = kb.sb("hu2", [128, ngr, A * G], F32R); res = kb.sb("hres", [128, A, cbw])
                    dts = (F32R, F32R, F32, F32)
                    NL = 4 if ngr >= 4 else 2
                    BpS = [[kb.sb(f"hBp{b}{i}", [128, 256], dts[i]) for i in range(4)] for b in range(NL)]
                    BpbS = [[Buf() for _ in range(4)] for b in range(NL)]
                    YpS = [[kb.sb(f"hYp{b}{i}", [128, 256], dts[i]) for i in range(4)] for b in range(NL)]
                    YpbS = [[Buf() for _ in range(4)] for b in range(NL)]
                    GpS = [[kb.sb(f"hGp{b}{i}", [128, 2, 128], dts[i]) for i in range(4)] for b in range(NL)]
                    GpbS = [[Buf() for _ in range(4)] for b in range(NL)]
                    fctr = [0]
                    sgm = kb.sb("hsgm", [cbw, Ls], BF16)

                    def fwd_fft(lhs_chunks, lhs_bufs, psB, psX):
                        n = len(lhs_chunks)
                        fctr[0] += 1
                        Bp, Bpb = BpS[fctr[0] % NL], BpbS[fctr[0] % NL]
                        for i, (ap, hf) in enumerate(lhs_chunks):
                            kb.op("pe", lambda e, ap=ap, hf=hf, i=i: e.matmul(psB[:, :], ap, F256[:, hf, :], start=(i == 0), stop=(i == n - 1)),
                                  reads=lhs_bufs + [F256], writes=[psB])
                        yield
                        cmul(Bp[0][:, :], Bp[1][:, :], psB[:, 0:256], psB[:, 256:512], TWC[:, :], TWS[:, :], True,
                             [psB], [TWC, TWS], Bpb, (Bp[2][:, :], Bp[3][:, :]))
                        yield
                        kb.op("pe", lambda e: e.matmul(psX[:, 0:256], Dre[:, :], Bp[0][:, :], start=True, stop=False), reads=[Dre, Bpb[0]], writes=[psX])
                        kb.op("pe", lambda e: e.matmul(psX[:, 0:256], nDim[:, :], Bp[1][:, :], start=False, stop=True), reads=[nDim, Bpb[1]], writes=[psX])
                        kb.op("pe", lambda e: e.matmul(psX[:, 256:512], Dim[:, :], Bp[0][:, :], start=True, stop=False), reads=[Dim, Bpb[0]], writes=[psX])
                        kb.op("pe", lambda e: e.matmul(psX[:, 256:512], Dre[:, :], Bp[1][:, :], start=False, stop=True), reads=[Dre, Bpb[1]], writes=[psX])

                    def conv_group(src, src_b, g, o, mulv, mul_b, dst_ap, dst_b, it):
                        ln = it % NL
                        if NL == 4:
                            psB, psX, psG, psy = PS[2 * ln], PS[2 * ln + 1], PS[2 * ln], PS[2 * ln + 1]
                        else:
                            psB, psX, psG, psy = PS[it % 2], PS[2 + it % 2], PS[4 + it % 2], PS[6 + it % 2]
                        Yp, Ypb, Gp, Gpb = YpS[ln], YpbS[ln], GpS[ln], GpbS[ln]
                        yield from fwd_fft([(src[:, g, :], 0)], [src_b], psB, psX)
                        yield
                        cmul(Yp[0][:, :], Yp[1][:, :], psX[:, 0:256], psX[:, 256:512], KS[o][:, g, 0:256], KS[o][:, g, 256:512], False,
                             [psX], [KS[o]], Ypb, (Yp[2][:, :], Yp[3][:, :]))
                        yield
                        for chn in range(2):
                            fs = slice(chn * 128, (chn + 1) * 128)
                            kb.op("pe", lambda e, fs=fs, chn=chn: e.matmul(psG[:, chn * 256:(chn + 1) * 256], Yp[0][:, fs], E1[:, :], start=True, stop=False),
                                  reads=[Ypb[0], E1], writes=[psG])
                            kb.op("pe", lambda e, fs=fs, chn=chn: e.matmul(psG[:, chn * 256:(chn + 1) * 256], Yp[1][:, fs], E2[:, :], start=False, stop=True),
                                  reads=[Ypb[1], E2], writes=[psG])
                        yield
                        pg = psG[:, :].rearrange("p (ch ri c) -> p ch ri c", ch=2, ri=2)
                        cmul(Gp[0][:, :, :], Gp[1][:, :, :], pg[:, :, 0, :], pg[:, :, 1, :], TW2C[:, :, :], TW2S[:, :, :], False,
                             [psG], [TW2C, TW2S], Gpb, (Gp[2][:, :, :], Gp[3][:, :, :]))
                        yield
                        k = 0
                        for chn in range(2):
                            for (tab, gsrc, gb) in ((IC, Gp[0], Gpb[0]), (IS, Gp[1], Gpb[1])):
                                kb.op("pe", lambda e, chn=chn, tab=tab, gsrc=gsrc, k=k: e.matmul(
                                    psy[:, 0:128], tab[:, chn, :], gsrc[:, chn, :], start=(k == 0), stop=(k == 3)), reads=[tab, gb], writes=[psy])
                                k += 1
                        yield
                        kb.op("dve", lambda e: e.tensor_tensor(dst_ap, psy[:, 0:128].rearrange("p (c a) -> p a c", a=A),
                                                               mulv[:, g, :].rearrange("p (a c) -> p a c", c=G), ALU.mult),
                              reads=[psy, mul_b], writes=[dst_b])

                    def lockstep(gens):
                        gens = list(gens)
                        while gens:
                            nxt = []
                            for g_ in gens:
                                try:
                                    next(g_)
                                    nxt.append(g_)
                                except StopIteration:
                                    pass
                            gens = nxt

                    def spec_group(o, g, it):
                        if NL == 4:
                            psB, psX = PS[2 * (it % 4)], PS[2 * (it % 4) + 1]
                        else:
                            psB, psX = PS[it % 2], PS[2 + it % 2]
                        yield from fwd_fft([(KTr[o][:, 0, g, :, :].rearrange("p a c -> p (a c)"), 0),
                                            (KTr[o][:, 1, g, :, :].rearrange("p a c -> p (a c)"), 1)], [KTr[o]], psB, psX)
                        yield
                        kb.op("act", lambda e: e.copy(KS[o][:, g, :], psX[:, :]), reads=[psX], writes=[KS[o]])

                    git = 0
                    for cb in range(nblk):
                        kb.dma("sp", DECt[:].rearrange("p h g a c -> p (h g a c)"), CT[pre + "DEC"][cb, :, :], reads=[CT[pre + "DEC"]], writes=[DECt])
                        for ai, tile_ in enumerate((uv, x1, x2)):
                            kb.dma("pool", tile_[:].rearrange("p g x -> p (g x)"), HS[sg["ut"]][ai, cb, :, :], reads=[HS[sg["ut"]]], writes=[tile_])
                        for o in range(2):
                            for hf in range(2):
                                col0 = o * 512 + hf * 256 + cb * cbw
                                npb = 512 // cbw
                                for a in range(A):
                                    p = PS[(a // npb) % 4]
                                    kb.op("pe", lambda e, p=p, a=a, hf=hf, col0=col0, npb=npb: e.matmul(
                                        p[:, (a % npb) * cbw:(a % npb + 1) * cbw], h2T[0:64, hf * 128 * A + a:hf * 128 * A + a + 127 * A + 1:A],
                                        fw3[0:64, col0:col0 + cbw], start=True, stop=True), reads=[h2T, fw3], writes=[p])
                                    if a % npb == npb - 1 or a == A - 1:
                                        a0 = (a // npb) * npb
                                        na = a - a0 + 1
                                        kb.op("dve", lambda e, p=p, a0=a0, na=na, hf=hf, o=o: e.tensor_tensor(
                                            KT[o][:, hf, :, a0:a0 + na, :], p[:, 0:na * cbw].rearrange("p (a g c) -> p g a c", a=na, c=G),
                                            DECt[:, hf, :, a0:a0 + na, :], ALU.mult), reads=[p, DECt], writes=[KT[o]])
                            kb.op("dve", lambda e, o=o: e.tensor_reduce(part[:, :].rearrange("p (g c) -> p g c", c=G),
                                                                        KT[o][:, :, :, :, :].rearrange("p h g a c -> p g c h a"), AX.XY, ALU.add,
                                                                        apply_absolute_value=True), reads=[KT[o]], writes=[part])
                            pe_ = PS[4]
                            kb.op("pe", lambda e, o=o: e.matmul(pe_[0:1, 0:cbw], h2T[0:64, 0:1], fw3[0:64, o * 512 + 256 + cb * cbw:o * 512 + 256 + (cb + 1) * cbw],
                                                                start=True, stop=True), reads=[h2T, fw3], writes=[pe_])
                            kb.op("act", lambda e: e.activation(ex[0:1, :], pe_[0:1, 0:cbw], AF.Abs), reads=[pe_], writes=[ex])
                            kb.op("dve", lambda e: e.tensor_tensor(part[0:1, :], part[0:1, :], ex[0:1, :], ALU.add), reads=[part, ex], writes=[part])
                            pt_ = PS[5]
                            kb.op("pe", lambda e: e.matmul(pt_[:, 0:cbw], ones_f[:, :], part[:, :], start=True, stop=True), reads=[ones_f, part], writes=[pt_])
                            kb.op("dve", lambda e: e.reciprocal(rn[:, :], pt_[:, 0:cbw]), reads=[pt_], writes=[rn])
                            for hf in range(2):
                                kb.op("dve", lambda e, o=o, hf=hf: e.tensor_tensor(
                                    KTr[o][:, hf, :, :, :], KT[o][:, hf, :, :, :],
                                    rn[:, :].rearrange("p (g c) -> p g c", c=G).unsqueeze(2).broadcast_to([128, ngr, A, G]), ALU.mult),
                                    reads=[KT[o], rn], writes=[KTr[o]])
                            kb.op("dve", lambda e, o=o: e.tensor_tensor(
                                KTr[o][0:1, 0, :, 0, :], KTr[o][0:1, 0, :, 0, :].bitcast(F32),
                                brow[0:1, o * 256 + cb * cbw:o * 256 + (cb + 1) * cbw].rearrange("p (g c) -> p g c", c=G), ALU.add),
                                reads=[KTr[o], brow], writes=[KTr[o]])
                            for g in range(0, ngr, NL):
                                gg = [g_ for g_ in range(g, min(ngr, g + NL))]
                                lockstep([spec_group(o, g_, git + k_) for k_, g_ in enumerate(gg)])
                                git += len(gg)
                        kb.op("act", lambda e: e.copy(uvr[:], uv[:]), reads=[uv], writes=[uvr])
                        for g in range(0, ngr, NL):
                            gg = [g_ for g_ in range(g, min(ngr, g + NL))]
                            lockstep([conv_group(uvr, uvr, g_, 0, x1, x1, u2[:, g_, :].rearrange("p (a c) -> p a c", c=G), u2, git + k_)
                                      for k_, g_ in enumerate(gg)])
                            git += len(gg)
                        for g in range(0, ngr, NL):
                            gg = [g_ for g_ in range(g, min(ngr, g + NL))]
                            lockstep([conv_group(u2, u2, g_, 1, x2, x2, res[:, :, g_ * G:(g_ + 1) * G], res, git + k_)
                                      for k_, g_ in enumerate(gg)])
                            git += len(gg)
                        kb.dma("sp", sgm[:], HS["SG"][cb * cbw:(cb + 1) * cbw, off:off + Ls], reads=[HS["SG"]], writes=[sgm])
                        Fv = sgm[:, :].rearrange("c (p a) -> c p a", a=A)
                        for a in range(A):
                            p = PS[4 + (a // 4) % 4]
                            kb.op("pe", lambda e, p=p, a=a: e.transpose(p[0:cbw, (a % 4) * 128:(a % 4 + 1) * 128], res[:, a, :], ident_f[:]),
                                  reads=[res, ident_f], writes=[p])
                            if a % 4 == 3 or a == A - 1:
                                a0 = (a // 4) * 4
                                na = a - a0 + 1
                                kb.op("dve", lambda e, p=p, a0=a0, na=na: e.tensor_tensor(
                                    Fv[:, :, a0:a0 + na], p[0:cbw, 0:na * 128].rearrange("c (a p) -> c p a", p=128), Fv[:, :, a0:a0 + na], ALU.mult),
                                    reads=[p, sgm], writes=[sgm])
                        kb.dma("sp", mixT[cb * cbw:(cb + 1) * cbw, off:off + Ls], sgm[:, :], reads=[sgm], writes=[Buf()])

    dbgn = [n for n, _ in dbg]
    for l in range(depth):
        last = (l == DEPTH - 1)
        with kb.scope():
            hT = kb.sb("hT", [128, 8, T], BF16)
            G1 = kb.sb("G1", [128, 2, D])
            SH = kb.sb("SH", [128, 2, D])
            phase_mod(l)
            phase_norm(l)
            if "noattn" not in dbgn:
                if os.environ.get("ATTN_ONLY", "") != "dense":
                    phase_attn(l, False, not last)
                if os.environ.get("ATTN_ONLY", "") != "window":
                    phase_attn(l, True, not last)
            if "norw" not in dbgn:
                phase_rwkv_prep(l)
            if "nohy" not in dbgn:
                phase_hyena_prep(l, not last)
            if "hT" in dbgn:
                tmp = kb.sb("dbghT", [128, T])
                for j in range(8):
                    kb.op("dve", lambda e, j=j, tmp=tmp: e.tensor_copy(tmp[:], hT[:, j, :]), reads=[hT], writes=[tmp])
                    kb.dma("sp", dbg_t["hT"][:, j, :], tmp[:], reads=[tmp], writes=[dbg_t["hT"]])
        if "norw" not in dbgn:
            {0: phase_rwkv_chunked, 1: phase_rwkv_chunked2, 3: phase_rwkv_chunked3}[RW_V2](l)
            phase_rwkv_out(l, not last)
        if "nohy" not in dbgn:
            phase_hyena_main(l, not last)
        if "noout" not in dbgn:
            phase_out(l, last)
    for n, s_ in dbg:
        if n == "xres":
            with kb.scope():
                tx = kb.sb("dbgx", [128, D])
                for i in range(NT):
                    kb.dma("sp", tx[:], xres[i * 128:(i + 1) * 128, :], reads=[xres_b[i]], writes=[tx])
                    kb.dma("sp", dbg_t[n][i * 128:(i + 1) * 128, :], tx[:], reads=[tx], writes=[dbg_t[n]])
        if n == "mixT":
            with kb.scope():
                tmpb = kb.sb("dbgmb", [128, T], BF16)
                tmpf = kb.sb("dbgmf", [128, T])
                for j in range(8):
                    kb.dma("sp", tmpb[:], mixT[j * 128:(j + 1) * 128, :], reads=[mixT], writes=[tmpb])
                    kb.op("dve", lambda e, tmpb=tmpb, tmpf=tmpf: e.tensor_copy(tmpf[:], tmpb[:]), reads=[tmpb], writes=[tmpf])
                    kb.dma("sp", dbg_t[n][j * 128:(j + 1) * 128, :], tmpf[:], reads=[tmpf], writes=[dbg_t[n]])
    kb.finish()
    kb.es.close()
    return kb, cst


_PROG = {}


def kernel(**inputs):
    if "p" not in _PROG:
        _PROG["p"] = build()
    kb, cst = _PROG["p"]
    f = lambda a: np.ascontiguousarray(np.asarray(a, dtype=np.float32))
    shared = {}
    for n in inputs:
        if n in ("x", "c", "ctx", "c_ctx"):
            continue
        shared[n] = f(inputs[n])
    shared["c_ctx"] = f(inputs["c_ctx"])
    for n, a in cst.items():
        shared["k_" + n] = np.ascontiguousarray(a)
    x, c, ctx = f(inputs["x"]), f(inputs["c"]), f(inputs["ctx"])
    B = x.shape[0]
    in_maps = []
    for b in range(B):
        m = dict(shared)
        m["x"] = np.ascontiguousarray(x[b])
        m["c"] = np.ascontiguousarray(c[b])
        m["ctx"] = np.ascontiguousarray(ctx[b])
        in_maps.append(m)
    res = run_bass_kernel_spmd(kb.nc, in_maps, core_ids=list(range(B)))
    return np.stack([np.asarray(res.results[b]["out"], dtype=np.float32) for b in range(B)], axis=0)
```

```python
import contextlib
import math
import numpy as np
import ml_dtypes
import concourse.bass as bass
import concourse.mybir as mybir
from concourse.bass_utils import run_bass_kernel_spmd

F32 = mybir.dt.float32
BF16 = mybir.dt.bfloat16
F32R = mybir.dt.float32r
ALU = mybir.AluOpType
AF = mybir.ActivationFunctionType
AX = mybir.AxisListType

D = 1024
L = 4096
C = 256
T = L + C
NT = T // 128
DEPTH = 4
D_IN = 3712
HY0, HYG0, RW0, RWG0, WA0, WAG0, FA0, FAG0 = 0, 768, 1024, 1920, 2176, 2688, 2944, 3456
EPS = 1e-6
NSLOT = 24
import os
RW_STAGE = int(os.environ.get('RW_STAGE', '99'))
INLINE_WAIT = int(os.environ.get('INLINE_WAIT', '1'))
POOL_DMA_TO_SP = int(os.environ.get('POOL_DMA_TO_SP', '1'))
RW_V2 = int(os.environ.get('RW_V2', '3'))


class Buf:
    def __init__(self, name=""):
        self.name = name
        self.w = None
        self.r = {}

    def wdeps(self):
        return [self.w] if self.w is not None else []

    def rdeps(self):
        return list(self.r.values())

    def add_reader(self, tok):
        k = tok[:2]
        if k not in self.r or self.r[k][2] < tok[2]:
            self.r[k] = tok

    def set_writer(self, tok):
        self.w = tok
        self.r = {}


class Tile(Buf):
    def __init__(self, name, t):
        super().__init__(name)
        self.t = t

    def __getitem__(self, key):
        return self.t[key]


class KB:
    def __init__(self):
        self.nc = bass.Bass("TRN2", target_bir_lowering=False)
        nc = self.nc
        self.es = contextlib.ExitStack()
        self.eng = {"pe": nc.tensor, "act": nc.scalar, "dve": nc.vector, "pool": nc.gpsimd, "sp": nc.sync}
        self.sem = {}
        self.cnt = {}
        self.waited = {e: {} for e in self.eng}
        for e in self.eng:
            self.sem[e] = self.es.enter_context(nc.semaphore("s_" + e))
            self.cnt[e] = 0
        self.slots = {}
        self.slot_i = {}
        for q in ("sp", "act", "pool"):
            self.slots[q] = [[self.es.enter_context(nc.semaphore(f"d_{q}{i}")), 0] for i in range(NSLOT)]
            self.slot_i[q] = 0
        self.n_ins = 0

    def sb(self, name, shape, dt=F32):
        self.uid = getattr(self, "uid", 0) + 1
        name = f"{name}_{self.uid}"
        return Tile(name, self.es.enter_context(self.nc.sbuf_tensor(name, list(shape), dt)))

    def ps(self, name, shape, dt=F32):
        return Tile(name, self.es.enter_context(self.nc.psum_tensor(name, list(shape), dt)))

    def dram(self, name, shape, dt=F32, kind="Internal"):
        t = self.nc.dram_tensor(name, list(shape), dt, kind=kind)
        b = Tile(name, t.ap())
        return b

    def _tok_sem(self, tok):
        if tok[0] == "e":
            return ("e", tok[1]), self.sem[tok[1]], tok[2]
        return ("d", tok[1]), self.slots[tok[1][0]][tok[1][1]][0], tok[2]

    def _wait(self, e, toks, defer=False):
        need = {}
        for tok in toks:
            if tok is None:
                continue
            key, sem, val = self._tok_sem(tok)
            if tok[0] == "e" and tok[1] == e and e == "pe":
                continue
            if self.waited[e].get(key, 0) >= val:
                continue
            if key not in need or need[key][1] < val:
                need[key] = (sem, val)
        items = list(need.items())
        inline = None
        if defer and INLINE_WAIT and items:
            inline = items.pop()
        for key, (sem, val) in items:
            self.eng[e].wait_ge(sem, val)
            self.waited[e][key] = val
        return inline

    def op(self, e, fn, reads=(), writes=()):
        toks = []
        for b in reads:
            toks += b.wdeps()
        for b in writes:
            toks += b.wdeps() + b.rdeps()
        inline = self._wait(e, toks, defer=True)
        ins = fn(self.eng[e])
        if inline is not None:
            key, (sem, val) = inline
            ins._wait_ge(sem, val)
            self.waited[e][key] = val
        self.cnt[e] += 1
        ins.then_inc(self.sem[e], 1)
        tok = ("e", e, self.cnt[e])
        for b in reads:
            b.add_reader(tok)
        for b in writes:
            b.set_writer(tok)
        self.n_ins += 1
        return ins

    def dma(self, q, out, in_, reads=(), writes=(), slow=False):
        if q == "pool" and POOL_DMA_TO_SP:
            q = "sp"
        i = self.slot_i[q]
        self.slot_i[q] = (i + 1) % NSLOT
        slot = self.slots[q][i]
        toks = []
        if slot[1] > 0:
            toks.append(("d", (q, i), slot[1]))
        for b in reads:
            toks += b.wdeps()
        for b in writes:
            toks += b.wdeps() + b.rdeps()
        self._wait(q, toks)
        if slow:
            ins = self.eng[q].dma_start(out=out, in_=in_, allow_slow_non_contiguous=True)
        else:
            ins = self.eng[q].dma_start(out=out, in_=in_)
        ins.then_inc(slot[0], 16)
        slot[1] += 16
        tok = ("d", (q, i), slot[1])
        for b in reads:
            b.add_reader(tok)
        for b in writes:
            b.set_writer(tok)
        self.n_ins += 1
        return ins

    def barrier(self):
        toks = [("e", e, self.cnt[e]) for e in self.eng if self.cnt[e] > 0]
        for q in self.slots:
            for i, s in enumerate(self.slots[q]):
                if s[1] > 0:
                    toks.append(("d", (q, i), s[1]))
        for e in self.eng:
            self._wait(e, toks)

    def finish(self):
        self.barrier()

    @contextlib.contextmanager
    def scope(self):
        es = contextlib.ExitStack()
        old = self.es
        self.es = es
        try:
            yield
        finally:
            self.barrier()
            self.es = old
            es.close()


def host_consts():
    cst = {}
    cst["ident_bf"] = np.eye(128, dtype=np.float32).astype(ml_dtypes.bfloat16)
    cst["ident_f"] = np.eye(128, dtype=np.float32)
    blk = np.zeros((128, 128), np.float32)
    blk[:64, :64] = 1.0
    blk[64:, 64:] = 1.0
    cst["blk64"] = blk
    cst["ones_f"] = np.ones((128, 128), np.float32)
    t = np.arange(L)
    row = (t // 64).astype(np.float32)
    col = (t % 64).astype(np.float32)
    inv = (10000.0 ** (-np.arange(16, dtype=np.float32) / 16)).astype(np.float32)
    cosT = np.zeros((128, L), np.float32)
    sinT = np.zeros((128, L), np.float32)
    perm = np.zeros((128, 128), np.float32)
    for p in range(128):
        d = p % 64
        sec, half, f = d // 32, (d % 32) // 16, d % 16
        pos = row if sec == 0 else col
        ang = (pos * inv[f]).astype(np.float32)
        cosT[p] = np.cos(ang)
        sinT[p] = np.sin(ang)
        if half == 0:
            perm[p + 16, p] = -1.0
        else:
            perm[p - 16, p] = 1.0
    cst["rope_cos"] = cosT
    cst["rope_sin"] = sinT
    cst["rope_perm"] = perm
    i = np.arange(128)[:, None]
    j = np.arange(384)[None, :]
    cst["wmask"] = np.where((j >= i) & (j <= i + 256), 0.0, -1e30).astype(np.float32)
    ii = np.arange(64)
    ms = np.zeros((128, 2, 64), np.float32); mts = np.zeros((128, 2, 64), np.float32); mti = np.zeros((128, 2, 64), np.float32)
    for hp in range(2):
        rows = slice(hp * 64, hp * 64 + 64)
        ms[rows, 0, :] = (ii[None, :] < ii[:, None]); ms[rows, 1, :] = (ii[None, :] > ii[:, None])
        mts[rows, 0, :] = (ii[:, None] < ii[None, :]); mts[rows, 1, :] = (ii[:, None] > ii[None, :])
        mti[rows, 0, :] = (ii[:, None] <= ii[None, :]); mti[rows, 1, :] = (ii[:, None] >= ii[None, :])
    cst["rw_ms"] = ms; cst["rw_mts"] = mts; cst["rw_mti"] = mti
    cst["rw_msb"] = np.concatenate([ms, ms], 2); cst["rw_mtsb"] = np.concatenate([mts, mts], 2)
    cst["rw_id2"] = np.concatenate([np.eye(64, dtype=np.float32)] * 2, 0)
    cst.update(hy_consts(L, 32, 32, "L"))
    cst.update(hy_consts(C, 2, 64, "C"))
    return cst


def hy_consts(Ls, A, cbw, tag):
    G = 128 // A
    N = 2 * Ls
    out = {}
    p = np.arange(128)
    f1 = np.arange(256)
    F = np.zeros((128, 2, 512), np.float64)
    for h in range(2):
        pp = h * 128 + p
        ang = 2 * np.pi * ((pp[:, None] * f1[None, :]) % 256) / 256
        F[:, h, 0:256] = np.cos(ang)
        F[:, h, 256:512] = -np.sin(ang)
    out["F256"] = F
    a_of_row = np.arange(128) // G
    th = 2 * np.pi * ((a_of_row[:, None] * f1[None, :]) % N) / N
    out["TWC"] = np.cos(th)
    out["TWS"] = np.sin(th)
    Dre = np.zeros((128, 128)); Dim = np.zeros((128, 128))
    E1 = np.zeros((128, 256)); E2 = np.zeros((128, 256))
    for a in range(A):
        for c in range(G):
            for f2 in range(A):
                ph = 2 * np.pi * ((a * f2) % A) / A
                Dre[a * G + c, c * A + f2] = np.cos(ph)
                Dim[a * G + c, c * A + f2] = -np.sin(ph)
                E1[c * A + f2, c * A + a] = np.cos(ph)
                E1[c * A + f2, 128 + c * A + a] = np.sin(ph)
                E2[c * A + f2, c * A + a] = -np.sin(ph)
                E2[c * A + f2, 128 + c * A + a] = np.cos(ph)
    out["Dre"] = Dre; out["Dim"] = Dim; out["nDim"] = -Dim; out["E1"] = E1; out["E2"] = E2
    a_of_col = np.arange(128) % A
    TW2C = np.zeros((128, 2, 128)); TW2S = np.zeros((128, 2, 128))
    IC = np.zeros((128, 2, 128)); IS = np.zeros((128, 2, 128))
    for ch in range(2):
        ff = ch * 128 + np.arange(128)
        th2 = 2 * np.pi * ((ff[:, None] * a_of_col[None, :]) % N) / N
        TW2C[:, ch, :] = np.cos(th2) / N
        TW2S[:, ch, :] = np.sin(th2) / N
        ph = 2 * np.pi * ((ff[:, None] * p[None, :]) % 256) / 256
        IC[:, ch, :] = np.cos(ph)
        IS[:, ch, :] = -np.sin(ph)
    out["TW2C"] = TW2C; out["TW2S"] = TW2S; out["IC"] = IC; out["IS"] = IS
    tp = np.arange(N)
    pos = np.where(tp < Ls, tp, N - tp).astype(np.float64)
    tn = (pos / (Ls - 1)).astype(np.float32)
    w = ((2.0 * math.pi / Ls) * pos).astype(np.float32)
    fb = np.linspace(1e-4, 15.0, 16, dtype=np.float32)
    zT = np.zeros((33, N), np.float32)
    zT[0] = tn
    zT[1:17] = np.cos(fb[:, None] * w[None, :])
    zT[17:33] = np.sin(fb[:, None] * w[None, :])
    out["zT"] = zT
    deltas = np.abs(np.linspace(math.log(1e-2) / 1.5, math.log(1e-2) / 0.3, 256, dtype=np.float32))
    dec = np.exp(-tn[:, None] * deltas[None, :]).astype(np.float32)
    dec[Ls, :] = 0.0
    nblk = 256 // cbw
    ngr = cbw // G
    DEC = np.zeros((nblk, 128, 2, ngr, A, G), np.float32)
    for h in range(2):
        for a in range(A):
            tpp = A * (h * 128 + p) + a
            for b in range(nblk):
                DEC[b, :, h, :, a, :] = dec[tpp, b * cbw:(b + 1) * cbw].reshape(128, ngr, G)
    out["DEC"] = DEC.reshape(nblk, 128, 2 * ngr * A * G)
    return {f"hy{tag}_{k}": np.ascontiguousarray(v.astype(np.float32)) for k, v in out.items()}

CONST_SPECS = None


def build(depth=DEPTH, dbg=()):
    kb = KB()
    nc = kb.nc
    cst = host_consts()
    def inp(name, shape, dt=F32):
        return kb.dram(name, shape, dt, kind="ExternalInput")

    x_in = inp("x", [L, D])
    c_in = inp("c", [D])
    ctx_in = inp("ctx", [C, D])
    cctx_in = inp("c_ctx", [D])
    W = {}
    wspec = {
        "mod_w": [DEPTH, D, 3 * D], "mod_b": [DEPTH, 3 * D], "norm_g": [DEPTH, D], "w_in": [DEPTH, D, D_IN],
        "w_out": [DEPTH, D, D], "wa_sink": [DEPTH, 4], "fa_q_norm": [DEPTH, 64], "fa_k_norm": [DEPTH, 64],
        "final_g": [D],
        "rw_conv": [DEPTH, 3, 896], "rw_w0": [DEPTH, 2, 256], "rw_w_up": [DEPTH, 2, 64, 256], "rw_a0": [DEPTH, 2, 256],
        "rw_a_up": [DEPTH, 2, 64, 256], "rw_k_k": [DEPTH, 256], "rw_k_a": [DEPTH, 256], "rw_r_k": [DEPTH, 256],
        "rw_ln_g": [DEPTH, 256], "rw_ln_b": [DEPTH, 256],
        "hy_conv": [DEPTH, 3, 768], "hy_fw1": [DEPTH, 33, 64], "hy_fb1": [DEPTH, 64], "hy_freq": [DEPTH, 64],
        "hy_fw2": [DEPTH, 64, 64], "hy_fb2": [DEPTH, 64], "hy_fw3": [DEPTH, 64, 1024], "hy_bias": [DEPTH, 2, 256],
    }
    for n, s in wspec.items():
        W[n] = inp(n, s)
    CT = {}
    for n, a in cst.items():
        CT[n] = inp("k_" + n, list(a.shape), BF16 if a.dtype == ml_dtypes.bfloat16 else F32)
    out = kb.dram("out", [L, D], F32, kind="ExternalOutput")
    xres = kb.dram("xres", [T, D], F32)
    mixT = kb.dram("mixT", [D, T], BF16)
    RS = {}
    for n in ("RT", "VT", "AL", "W0", "W1", "B0", "B1", "KD0", "KD1", "YF", "YB"):
        RS[n] = kb.dram("rs_" + n, [256, T])
    RS["VTOK"] = kb.dram("rs_VTOK", [T, 256])
    RS["SGT"] = kb.dram("rs_SGT", [256, T], BF16)
    HS = {"SG": kb.dram("hs_SG", [256, T], BF16),
          "UTL": kb.dram("hs_UTL", [3, 8, 128, 32 * 32]), "UTC": kb.dram("hs_UTC", [3, 4, 128, 2 * 64])}
    dbg_t = {}
    for n, s in dbg:
        dbg_t[n] = kb.dram("dbg_" + n, s, F32, kind="ExternalOutput")

    ident_bf = kb.sb("ident_bf", [128, 128], BF16)
    ident_f = kb.sb("ident_f", [128, 128])
    blk64 = kb.sb("blk64", [128, 128])
    ones_f = kb.sb("ones_f", [128, 128])
    for tl, n in ((ident_bf, "ident_bf"), (ident_f, "ident_f"), (blk64, "blk64"), (ones_f, "ones_f")):
        kb.dma("sp", tl[:], CT[n][:, :], reads=[CT[n]], writes=[tl])
    hT = G1 = SH = None
    GT = kb.sb("GT", [128, 2, D])
    PS = [kb.ps(f"ps{i}", [128, 512]) for i in range(8)]

    xres_b = [Buf(f"xres{i}") for i in range(NT)]

    def x_src(l, i):
        if l == 0:
            if i < 2:
                return ctx_in[i * 128:(i + 1) * 128, :], ctx_in
            return x_in[(i - 2) * 128:(i - 1) * 128, :], x_in
        return xres[i * 128:(i + 1) * 128, :], xres_b[i]

    def phase_mod(l):
        with kb.scope():
            cc = kb.sb("cc", [128, 2, 8])
            sc = kb.sb("sc", [128, 2, 8])
            mw = [kb.sb(f"mw{i}", [128, 8, 512]) for i in range(2)]
            mb = kb.sb("mb", [128, 3 * D])
            ng = kb.sb("ng", [128, D])
            modr = kb.sb("modr", [128, 2, 3 * D])
            kb.dma("sp", cc[:, 0, :], c_in.t.rearrange("(j p) -> p j", p=128), reads=[c_in], writes=[cc], slow=True)
            kb.dma("sp", cc[:, 1, :], cctx_in.t.rearrange("(j p) -> p j", p=128), reads=[cctx_in], writes=[cc], slow=True)
            kb.dma("sp", mb[:], W["mod_b"][l, :].partition_broadcast(128), reads=[W["mod_b"]], writes=[mb])
            kb.dma("sp", ng[:], W["norm_g"][l, :].partition_broadcast(128), reads=[W["norm_g"]], writes=[ng])
            kb.op("act", lambda e: e.activation(sc[:], cc[:], AF.Silu), reads=[cc], writes=[sc])
            for n in range(6):
                m = mw[n % 2]
                kb.dma("sp" if n % 2 == 0 else "pool", m[:],
                       W["mod_w"][l, :, n * 512:(n + 1) * 512].rearrange("(j p) n -> p j n", p=128),
                       reads=[W["mod_w"]], writes=[m])
                for i in range(2):
                    p = PS[(2 * n + i) % 8]
                    for j in range(8):
                        kb.op("pe", lambda e, p=p, i=i, j=j, m=m: e.matmul(
                            p[:, :], sc[:, i, j:j + 1].broadcast_to([128, 128]), m[:, j, :],
                            start=(j == 0), stop=(j == 7)), reads=[sc, m], writes=[p])
                    kb.op("dve", lambda e, p=p, i=i, n=n: e.tensor_tensor(
                        modr[:, i, n * 512:(n + 1) * 512], p[:, :], mb[:, n * 512:(n + 1) * 512], ALU.add),
                        reads=[p, mb], writes=[modr])
            for i in range(2):
                kb.op("dve", lambda e, i=i: e.scalar_tensor_tensor(
                    G1[:, i, :], modr[:, i, D:2 * D], 1.0, ng[:], ALU.add, ALU.mult), reads=[modr, ng], writes=[G1])
                kb.op("act", lambda e, i=i: e.copy(SH[:, i, :], modr[:, i, 0:D]), reads=[modr], writes=[SH])
                kb.op("act", lambda e, i=i: e.copy(GT[:, i, :], modr[:, i, 2 * D:3 * D]), reads=[modr], writes=[GT])

    def phase_norm(l):
        NLN = 4
        with kb.scope():
            xt = [kb.sb(f"xt{i}", [128, D]) for i in range(NLN)]
            junks = [kb.sb(f"junk{i}", [128, D]) for i in range(NLN)]
            hf = [kb.sb(f"hf{i}", [128, D]) for i in range(NLN)]
            hb = [kb.sb(f"hb{i}", [128, D], BF16) for i in range(NLN)]
            st = [kb.sb(f"st{i}", [128, 4]) for i in range(NLN)]

            def tile_gen(i):
                ln = i % NLN
                x, s, h, hbt, junk = xt[ln], st[ln], hf[ln], hb[ln], junks[ln]
                sel = 1 if i < 2 else 0
                src, srcb = x_src(l, i)
                kb.dma("sp", x[:], src, reads=[srcb], writes=[x])
                yield
                kb.op("act", lambda e: e.activation(junk[:], x[:], AF.Square, accum_out=s[:, 0:1]), reads=[x], writes=[junk, s])
                yield
                kb.op("dve", lambda e: e.tensor_scalar(s[:, 1:2], s[:, 0:1], 1.0 / D, EPS, ALU.mult, ALU.add), reads=[s], writes=[s])
                yield
                kb.op("act", lambda e: e.sqrt(s[:, 2:3], s[:, 1:2]), reads=[s], writes=[s])
                yield
                kb.op("dve", lambda e: e.reciprocal(s[:, 3:4], s[:, 2:3]), reads=[s], writes=[s])
                kb.op("dve", lambda e: e.scalar_tensor_tensor(h[:], x[:], s[:, 3:4], G1[:, sel, :], ALU.mult, ALU.mult), reads=[x, s, G1], writes=[h])
                yield
                kb.op("pool", lambda e: e.tensor_tensor(hbt[:], h[:], SH[:, sel, :], ALU.add), reads=[h, SH], writes=[hbt])
                yield
                p = PS[ln]
                pv = p[:, :].bitcast(BF16)
                for j in range(8):
                    kb.op("pe", lambda e, j=j: e.transpose(pv[:, j * 128:(j + 1) * 128], hbt[:, j * 128:(j + 1) * 128], ident_bf[:]),
                          reads=[hbt, ident_bf], writes=[p])
                yield
                kb.op("act", lambda e: e.copy(hT[:, :, i * 128:(i + 1) * 128], pv.rearrange("p (j t) -> p j t", j=8)), reads=[p], writes=[hT])

            for i0 in range(0, NT, NLN):
                gens = [tile_gen(i) for i in range(i0, min(NT, i0 + NLN))]
                while gens:
                    nxt = []
                    for g_ in gens:
                        try:
                            next(g_)
                            nxt.append(g_)
                        except StopIteration:
                            pass
                    gens = nxt

    def load_w(l, dst, col0, ncols, stage, q="sp"):
        kb.dma(q, stage[:, :, 0:ncols], W["w_in"][l, :, col0:col0 + ncols].rearrange("(j p) n -> p j n", p=128),
               reads=[W["w_in"]], writes=[stage])
        kb.op("pool", lambda e: e.tensor_copy(dst[:, :, 0:ncols], stage[:, :, 0:ncols]), reads=[stage], writes=[dst])

    def proj_fm(p, wt, c0, nc_, t0, nt):
        for j in range(8):
            kb.op("pe", lambda e, j=j: e.matmul(p[0:nc_, 0:nt], wt[:, j, c0:c0 + nc_], hT[:, j, t0:t0 + nt],
                                                start=(j == 0), stop=(j == 7)), reads=[wt, hT], writes=[p])

    def proj_tm(p, wt, c0, nc_, i):
        for j in range(8):
            kb.op("pe", lambda e, j=j: e.matmul(p[:, 0:nc_], hT[:, j, i * 128:(i + 1) * 128], wt[:, j, c0:c0 + nc_],
                                                start=(j == 0), stop=(j == 7)), reads=[wt, hT], writes=[p])

    TCH = [(t0, min(512, T - t0)) for t0 in range(0, T, 512)]

    def qk_prep(l, es_tiles, wt, c0, dst, dst_j, gvec, norm, rope):
        raw, sq, rs, rot = es_tiles
        for ci, (t0, nt) in enumerate(TCH):
            p = PS[ci % 2]
            proj_fm(p, wt, c0, 128, t0, nt)
            if norm:
                kb.op("act", lambda e, p=p, nt=nt: e.activation(sq[:, 0:nt], p[:, 0:nt], AF.Square), reads=[p], writes=[sq])
                p2 = PS[2 + ci % 2]
                kb.op("pe", lambda e, p2=p2, nt=nt: e.matmul(p2[:, 0:nt], blk64[:], sq[:, 0:nt], start=True, stop=True),
                      reads=[blk64, sq], writes=[p2])
                kb.op("dve", lambda e, p2=p2, nt=nt: e.tensor_scalar(rs[:, 0:nt], p2[:, 0:nt], 1.0 / 64, EPS, ALU.mult, ALU.add),
                      reads=[p2], writes=[rs])
                kb.op("act", lambda e, nt=nt: e.sqrt(rs[:, 0:nt], rs[:, 0:nt]), reads=[rs], writes=[rs])
                kb.op("dve", lambda e, nt=nt: e.reciprocal(rs[:, 0:nt], rs[:, 0:nt]), reads=[rs], writes=[rs])
                kb.op("dve", lambda e, p=p, nt=nt: e.scalar_tensor_tensor(
                    raw[:, 0:nt], p[:, 0:nt], gvec[:, 0:1], rs[:, 0:nt], ALU.mult, ALU.mult), reads=[p, gvec, rs], writes=[raw])
            else:
                kb.op("act", lambda e, p=p, nt=nt: e.copy(raw[:, 0:nt], p[:, 0:nt]), reads=[p], writes=[raw])
            lat0 = 0
            if t0 < C:
                lat0 = C - t0
                kb.op("pool", lambda e, t0=t0, lat0=lat0: e.tensor_copy(dst[:, dst_j, t0:t0 + lat0], raw[:, 0:lat0]),
                      reads=[raw], writes=[dst])
            if not rope:
                if nt > lat0:
                    kb.op("pool", lambda e, t0=t0, lat0=lat0, nt=nt: e.tensor_copy(
                        dst[:, dst_j, t0 + lat0:t0 + nt], raw[:, lat0:nt]), reads=[raw], writes=[dst])
                continue
            p3 = PS[4 + ci % 2]
            n_l = nt - lat0
            lp = t0 + lat0 - C
            kb.op("pe", lambda e, p3=p3, lat0=lat0, nt=nt: e.matmul(p3[:, lat0:nt], rope_perm[:], raw[:, lat0:nt], start=True, stop=True),
                  reads=[rope_perm, raw], writes=[p3])
            kb.op("dve", lambda e, p3=p3, lat0=lat0, nt=nt, lp=lp, n_l=n_l: e.tensor_tensor(
                rot[:, lat0:nt], p3[:, lat0:nt], rope_sin[:, lp:lp + n_l], ALU.mult), reads=[p3, rope_sin], writes=[rot])
            kb.op("pool", lambda e, lat0=lat0, nt=nt, lp=lp, n_l=n_l: e.tensor_tensor(
                raw[:, lat0:nt], raw[:, lat0:nt], rope_cos[:, lp:lp + n_l], ALU.mult), reads=[raw, rope_cos], writes=[raw])
            kb.op("dve", lambda e, t0=t0, lat0=lat0, nt=nt: e.tensor_tensor(
                dst[:, dst_j, t0 + lat0:t0 + nt], raw[:, lat0:nt], rot[:, lat0:nt], ALU.add), reads=[raw, rot], writes=[dst])

    rope_cos = rope_sin = rope_perm = None

    def phase_attn(l, dense, with_ctx):
        nonlocal rope_cos, rope_sin, rope_perm
        base = FA0 if dense else WA0
        gbase = FAG0 if dense else WAG0
        mrow = 768 if dense else 512
        with kb.scope():
            wt = kb.sb("wt", [128, 8, 768], BF16)
            gq = kb.sb("gq", [128, 1])
            gk = kb.sb("gk", [128, 1])
            sink = kb.sb("sink", [128, 4])
            if dense:
                for hh in range(2):
                    kb.dma("sp", gq[hh * 64:(hh + 1) * 64, :], W["fa_q_norm"][l, :].rearrange("(d o) -> d o", o=1),
                           reads=[W["fa_q_norm"]], writes=[gq], slow=True)
                    kb.dma("sp", gk[hh * 64:(hh + 1) * 64, :], W["fa_k_norm"][l, :].rearrange("(d o) -> d o", o=1),
                           reads=[W["fa_k_norm"]], writes=[gk], slow=True)
            else:
                kb.dma("sp", sink[:], W["wa_sink"][l, :].partition_broadcast(128), reads=[W["wa_sink"]], writes=[sink])
            QT = kb.sb("QT", [128, 2, T], BF16)
            KT = kb.sb("KT", [128, 1, T], BF16)
            VW = 65 if dense else 64
            Vt = kb.sb("Vt", [128, NT, 2, VW], BF16)
            SG = None
            with kb.scope():
                rope_cos = kb.sb("rope_cos", [128, L])
                rope_sin = kb.sb("rope_sin", [128, L])
                rope_perm = kb.sb("rope_perm", [128, 128])
                kb.dma("sp", rope_cos[:], CT["rope_cos"][:, :], reads=[CT["rope_cos"]], writes=[rope_cos])
                kb.dma("pool", rope_sin[:], CT["rope_sin"][:, :], reads=[CT["rope_sin"]], writes=[rope_sin])
                kb.dma("sp", rope_perm[:], CT["rope_perm"][:, :], reads=[CT["rope_perm"]], writes=[rope_perm])
                stage = kb.sb("wstage", [128, 8, 256])
                w4 = W["w_in"][l, :, base:base + 256].rearrange("(j p) (h d) -> p j h d", p=128, d=64)
                st4 = stage[:, :, 0:256].rearrange("p j (h d) -> p j h d", d=64)
                for hi, h in enumerate((0, 2, 1, 3)):
                    kb.dma("sp", st4[:, :, hi, :], w4[:, :, h, :], reads=[W["w_in"]], writes=[stage])
                kb.op("pool", lambda e: e.tensor_copy(wt[:, :, 0:256], stage[:]), reads=[stage], writes=[wt])
                kb.dma("pool", stage[:], W["w_in"][l, :, base + 256:base + 512].rearrange("(j p) n -> p j n", p=128),
                       reads=[W["w_in"]], writes=[stage])
                kb.op("pool", lambda e: e.tensor_copy(wt[:, :, 256:512], stage[:]), reads=[stage], writes=[wt])
                kb.dma("sp", stage[:], W["w_in"][l, :, gbase:gbase + 256].rearrange("(j p) n -> p j n", p=128),
                       reads=[W["w_in"]], writes=[stage])
                kb.op("pool", lambda e: e.tensor_copy(wt[:, :, 512:768], stage[:]), reads=[stage], writes=[wt])
                tl = (kb.sb("qraw", [128, 512]), kb.sb("qsq", [128, 512]), kb.sb("qrs", [128, 512]), kb.sb("qrot", [128, 512]))
                qk_prep(l, tl, wt, 0, QT, 0, gq, dense, True)
                qk_prep(l, tl, wt, 128, QT, 1, gq, dense, True)
                qk_prep(l, tl, wt, 256, KT, 0, gk, dense, True)
            if dense:
                kb.op("pool", lambda e: e.memset(Vt[:, :, :, 64:65], 1.0), writes=[Vt])
            for i in range(NT):
                p = PS[i % 2]
                proj_tm(p, wt, 384, 128, i)
                kb.op("act", lambda e, p=p, i=i: e.copy(Vt[:, i, :, 0:64], p[:, 0:128].rearrange("p (k d) -> p k d", d=64)),
                      reads=[p], writes=[Vt])
            with kb.scope():
                if dense:
                    attn_dense(l, wt, QT, KT, Vt, mrow, with_ctx)
                else:
                    attn_window(l, wt, QT, KT, Vt, sink, mrow, with_ctx)

    def attn_dense(l, wt, QT, KT, Vt, mrow, with_ctx):
        pt = [kb.sb(f"pt{i}", [128, 512], BF16) for i in range(8)]
        osb = [kb.sb(f"osb{i}", [128, 512]) for i in range(2)]
        rc = [kb.sb(f"rc{i}", [128, 512]) for i in range(2)]
        ob = [kb.sb(f"ob{i}", [128, 512], BF16) for i in range(2)]
        sgt = [kb.sb(f"sgt{i}", [64, 512], BF16) for i in range(2)]
        chunks = []
        if with_ctx:
            chunks.append((0, C, 0, 2))
        for t0 in range(C, T, 512):
            chunks.append((t0, 512, 0, NT))
        for pr in range(2):
            heads_ = (pr, pr + 2)
            for (t0, nt, kb0, kb1) in chunks:
                po = [PS[4], PS[5]]
                pg = [PS[6], PS[7]]
                for s_, h in enumerate(heads_):
                    for j in range(8):
                        kb.op("pe", lambda e, j=j, s_=s_, h=h: e.matmul(
                            pg[s_][0:64, 0:nt], wt[:, j, 512 + 64 * h:576 + 64 * h], hT[:, j, t0:t0 + nt], start=(j == 0), stop=(j == 7)),
                            reads=[wt, hT], writes=[pg[s_]])
                    kb.op("act", lambda e, s_=s_: e.activation(sgt[s_][0:64, 0:nt], pg[s_][0:64, 0:nt], AF.Silu), reads=[pg[s_]], writes=[sgt[s_]])

                def pv_(kbi):
                    for s_ in range(2):
                        ptt = pt[(2 * kbi + s_) % 8]
                        kb.op("pe", lambda e, s_=s_, ptt=ptt: e.matmul(
                            po[s_][0:65, 0:nt], Vt[:, kbi, s_, 0:65], ptt[:, 0:nt], start=(kbi == kb0), stop=(kbi == kb1 - 1)),
                            reads=[Vt, ptt], writes=[po[s_]])
                LA = 1
                for kbi in range(kb0, kb1):
                    for s_ in range(2):
                        ks = slice(64 * s_, 64 * s_ + 64)
                        psS = PS[(2 * kbi + s_) % 4]
                        kb.op("pe", lambda e, psS=psS, ks=ks: e.matmul(
                            psS[:, 0:nt], KT[ks, 0, kbi * 128:(kbi + 1) * 128], QT[ks, pr, t0:t0 + nt], start=True, stop=True),
                            reads=[KT, QT], writes=[psS])
                    for s_ in range(2):
                        psS = PS[(2 * kbi + s_) % 4]
                        ptt = pt[(2 * kbi + s_) % 8]
                        kb.op("act", lambda e, psS=psS, ptt=ptt: e.activation(ptt[:, 0:nt], psS[:, 0:nt], AF.Exp, scale=0.125),
                              reads=[psS], writes=[ptt])
                    if kbi - LA >= kb0:
                        pv_(kbi - LA)
                for kbi in range(max(kb0, kb1 - LA), kb1):
                    pv_(kbi)
                for s_, h in enumerate(heads_):
                    o_s, r_c, o_b, sg = osb[s_], rc[s_], ob[s_], sgt[s_]
                    pb = pg[s_]
                    kb.op("dve", lambda e, s_=s_, r_c=r_c: e.reciprocal(r_c[64:65, 0:nt], po[s_][64:65, 0:nt]), reads=[po[s_]], writes=[r_c])
                    kb.op("act", lambda e, s_=s_, o_s=o_s: e.copy(o_s[0:64, 0:nt], po[s_][0:64, 0:nt]), reads=[po[s_]], writes=[o_s])
                    kb.op("pe", lambda e, pb=pb, r_c=r_c: e.matmul(pb[0:64, 0:nt], ones_f[64:65, 0:64], r_c[64:65, 0:nt], start=True, stop=True),
                          reads=[ones_f, r_c], writes=[pb])
                    kb.op("dve", lambda e, o_s=o_s, pb=pb: e.tensor_tensor(o_s[0:64, 0:nt], o_s[0:64, 0:nt], pb[0:64, 0:nt], ALU.mult),
                          reads=[o_s, pb], writes=[o_s])
                    kb.op("pool", lambda e, o_s=o_s, o_b=o_b, sg=sg: e.tensor_tensor(o_b[0:64, 0:nt], o_s[0:64, 0:nt], sg[0:64, 0:nt], ALU.mult),
                          reads=[o_s, sg], writes=[o_b])
                    kb.dma("sp", mixT[mrow + 64 * h:mrow + 64 * h + 64, t0:t0 + nt], o_b[0:64, 0:nt], reads=[o_b], writes=[Buf()])

    def attn_window(l, wt, QT, KT, Vt, sink, mrow, with_ctx):
        wmask = kb.sb("wmask", [128, 384])
        kb.dma("sp", wmask[:], CT["wmask"][:, :], reads=[CT["wmask"]], writes=[wmask])
        nsink = kb.sb("nsink", [128, 4])
        kb.op("dve", lambda e: e.tensor_scalar(nsink[:], sink[:], -1.0, None, ALU.mult), reads=[sink], writes=[nsink])
        S = [kb.sb(f"wS{i}", [128, 640]) for i in range(4)]
        P = [kb.sb(f"wP{i}", [128, 640]) for i in range(4)]
        Pn = [kb.sb(f"wPn{i}", [128, 640], BF16) for i in range(4)]
        PT = [kb.sb(f"wPT{i}", [128, 640], BF16) for i in range(4)]
        st = [kb.sb(f"wst{i}", [128, 8]) for i in range(4)]
        sgt = [kb.sb(f"wsg{i}", [64, 128], BF16) for i in range(4)]
        ob = [kb.sb(f"wob{i}", [64, 128], BF16) for i in range(4)]
        it = 0
        for i in range(0 if with_ctx else 2, NT):
            if i < 2:
                loc = []
            else:
                loc = list(range(max(2, i - 1), min(NT - 1, i + 1) + 1))
            nl = 128 * len(loc)
            m0 = 128 if (i >= 2 and i - 1 < 2) else 0
            nk = nl + C
            ktiles = loc + [0, 1]
            def unit(h, it):
                kv, pr = h // 2, h % 2
                ks = slice(64 * kv, 64 * kv + 64)
                s_, p_, pn_, pt_, st_, sg, o_b = S[it % 4], P[it % 4], Pn[it % 4], PT[it % 4], st[it % 4], sgt[it % 4], ob[it % 4]
                psA, psB = PS[2 * (it % 4)], PS[2 * (it % 4) + 1]
                psT, psOG = psA, psB
                psO, psG = psOG, psOG
                q_ap = QT[ks, pr, i * 128:(i + 1) * 128]
                if nl:
                    k0 = loc[0] * 128
                    kb.op("pe", lambda e: e.matmul(psA[:, 0:nl], q_ap, KT[ks, 0, k0:k0 + nl], start=True, stop=True),
                          reads=[QT, KT], writes=[psA])
                    kb.op("dve", lambda e: e.tensor_tensor(s_[:, 0:nl], psA[:, 0:nl], wmask[:, m0:m0 + nl], ALU.add),
                          reads=[psA, wmask], writes=[s_])
                kb.op("pe", lambda e: e.matmul(psB[:, 0:C], q_ap, KT[ks, 0, 0:C], start=True, stop=True),
                      reads=[QT, KT], writes=[psB])
                kb.op("act", lambda e: e.copy(s_[:, nl:nk], psB[:, 0:C]), reads=[psB], writes=[s_])
                yield
                kb.op("dve", lambda e: e.reduce_max(st_[:, 0:1], s_[:, 0:nk], AX.X), reads=[s_], writes=[st_])
                kb.op("dve", lambda e: e.tensor_scalar(st_[:, 1:2], st_[:, 0:1], -0.125, nsink[:, h:h + 1], ALU.mult, ALU.min),
                      reads=[st_, nsink], writes=[st_])
                kb.op("act", lambda e: e.activation(p_[:, 0:nk], s_[:, 0:nk], AF.Exp, bias=st_[:, 1:2], scale=0.125,
                                                    accum_out=st_[:, 2:3]), reads=[s_, st_], writes=[p_, st_])
                kb.op("act", lambda e: e.activation(st_[:, 3:4], sink[:, h:h + 1], AF.Exp, bias=st_[:, 1:2], scale=1.0),
                      reads=[sink, st_], writes=[st_])
                kb.op("dve", lambda e: e.tensor_tensor(st_[:, 4:5], st_[:, 2:3], st_[:, 3:4], ALU.add), reads=[st_], writes=[st_])
                kb.op("dve", lambda e: e.reciprocal(st_[:, 5:6], st_[:, 4:5]), reads=[st_], writes=[st_])
                kb.op("dve", lambda e: e.tensor_scalar(pn_[:, 0:nk], p_[:, 0:nk], st_[:, 5:6], None, ALU.mult),
                      reads=[p_, st_], writes=[pn_])
                yield
                pv = psT[:, :].bitcast(BF16)
                nb = nk // 128
                for b in range(nb):
                    kb.op("pe", lambda e, b=b: e.transpose(pv[:, b * 128:(b + 1) * 128], pn_[:, b * 128:(b + 1) * 128], ident_bf[:]),
                          reads=[pn_, ident_bf], writes=[psT])
                yield
                kb.op("act", lambda e: e.copy(pt_[:, 0:nk], pv[:, 0:nk]), reads=[psT], writes=[pt_])
                yield
                for b in range(nb):
                    kb.op("pe", lambda e, b=b: e.matmul(psO[0:64, 0:128], Vt[:, ktiles[b], kv, 0:64], pt_[:, b * 128:(b + 1) * 128],
                                                        start=(b == 0), stop=(b == nb - 1)), reads=[Vt, pt_], writes=[psO])
                for j in range(8):
                    kb.op("pe", lambda e, j=j: e.matmul(
                        psG[0:64, 128:256], wt[:, j, 512 + 64 * h:576 + 64 * h], hT[:, j, i * 128:(i + 1) * 128],
                        start=(j == 0), stop=(j == 7)), reads=[wt, hT], writes=[psG])
                yield
                kb.op("act", lambda e: e.activation(sg[:, :], psG[0:64, 128:256], AF.Silu), reads=[psG], writes=[sg])
                kb.op("dve", lambda e: e.tensor_tensor(o_b[:, :], psO[0:64, 0:128], sg[:, :], ALU.mult), reads=[psO, sg], writes=[o_b])
                kb.dma("sp", mixT[mrow + 64 * h:mrow + 64 * h + 64, i * 128:(i + 1) * 128], o_b[:, :], reads=[o_b], writes=[Buf()])

            for h0 in (0,):
                gens = [unit(h_, it + k_) for k_, h_ in enumerate((0, 2, 1, 3))]
                it += 4
                while gens:
                    nxt = []
                    for g_ in gens:
                        try:
                            next(g_)
                            nxt.append(g_)
                        except StopIteration:
                            pass
                    gens = nxt

    def phase_out(l, last):
        with kb.scope():
            wo = kb.sb("wo", [128, 8, D], BF16)
            stage = kb.sb("wostage", [128, 8, 256])
            for q in range(4):
                kb.dma("sp", stage[:], W["w_out"][l, :, q * 256:(q + 1) * 256].rearrange("(j p) n -> p j n", p=128),
                       reads=[W["w_out"]], writes=[stage])
                kb.op("pool", lambda e, q=q: e.tensor_copy(wo[:, :, q * 256:(q + 1) * 256], stage[:]), reads=[stage], writes=[wo])
            fg = kb.sb("fg", [128, D])
            if last:
                kb.dma("sp", fg[:], W["final_g"][:].partition_broadcast(128), reads=[W["final_g"]], writes=[fg])
            mt = [kb.sb(f"mt{i}", [128, 8, 128], BF16) for i in range(2)]
            xt = [kb.sb(f"oxt{i}", [128, D]) for i in range(2)]
            xn = [kb.sb(f"oxn{i}", [128, D]) for i in range(2)]
            tmp = [kb.sb(f"otmp{i}", [128, 512]) for i in range(2)]
            st = [kb.sb(f"ost{i}", [128, 4]) for i in range(2)]
            junk = kb.sb("ojunk", [128, D])
            mixv = mixT.t.rearrange("(j p) t -> p j t", p=128)
            for it, i in enumerate(range(2 if last else 0, NT)):
                m, x, xo, s = mt[it % 2], xt[it % 2], xn[it % 2], st[it % 2]
                sel = 1 if i < 2 else 0
                kb.dma("sp", m[:], mixv[:, :, i * 128:(i + 1) * 128], reads=[mixT], writes=[m])
                src, srcb = x_src(l, i)
                kb.dma("pool", x[:], src, reads=[srcb], writes=[x])
                for hf in range(2):
                    p = PS[(2 * it + hf) % 8]
                    tp = tmp[hf]
                    for j in range(8):
                        kb.op("pe", lambda e, j=j, p=p, m=m, hf=hf: e.matmul(p[:, :], m[:, j, :], wo[:, j, hf * 512:(hf + 1) * 512],
                                                                     start=(j == 0), stop=(j == 7)), reads=[m, wo], writes=[p])
                    kb.op("dve", lambda e, p=p, tp=tp, hf=hf, sel=sel: e.tensor_tensor(
                        tp[:], p[:, :], GT[:, sel, hf * 512:(hf + 1) * 512], ALU.mult), reads=[p, GT], writes=[tp])
                    kb.op("pool", lambda e, tp=tp, hf=hf, x=x, xo=xo: e.tensor_tensor(
                        xo[:, hf * 512:(hf + 1) * 512], x[:, hf * 512:(hf + 1) * 512], tp[:], ALU.add), reads=[x, tp], writes=[xo])
                if not last:
                    kb.dma("sp", xres[i * 128:(i + 1) * 128, :], xo[:], reads=[xo], writes=[xres_b[i]])
                else:
                    kb.op("act", lambda e, xo=xo, s=s: e.activation(junk[:], xo[:], AF.Square, accum_out=s[:, 0:1]),
                          reads=[xo], writes=[junk, s])
                    kb.op("dve", lambda e, s=s: e.tensor_scalar(s[:, 1:2], s[:, 0:1], 1.0 / D, EPS, ALU.mult, ALU.add),
                          reads=[s], writes=[s])
                    kb.op("act", lambda e, s=s: e.sqrt(s[:, 2:3], s[:, 1:2]), reads=[s], writes=[s])
                    kb.op("dve", lambda e, s=s: e.reciprocal(s[:, 3:4], s[:, 2:3]), reads=[s], writes=[s])
                    kb.op("dve", lambda e, xo=xo, s=s, x=x: e.scalar_tensor_tensor(
                        x[:], xo[:], s[:, 3:4], fg[:], ALU.mult, ALU.mult), reads=[xo, s, fg], writes=[x])
                    kb.dma("sp", out[(i - 2) * 128:(i - 1) * 128, :], x[:], reads=[x], writes=[Buf()])


    def conv_tile(l, wt, cw, ncw, jt, Zraw, Zout):
        for ci, (t0, nt) in enumerate(TCH):
            p = PS[ci % 4]
            proj_fm(p, wt, 0, 128, t0, nt)
            kb.op("act", lambda e, p=p, t0=t0, nt=nt: e.copy(Zraw[:, 1 + t0:1 + t0 + nt], p[:, 0:nt]), reads=[p], writes=[Zraw])
        kb.op("dve", lambda e: e.tensor_scalar(Zout[:, :], Zraw[:, 1:T + 1], cw[:, jt, 1:2], None, ALU.mult), reads=[Zraw, cw], writes=[Zout])
        kb.op("dve", lambda e: e.scalar_tensor_tensor(Zout[:, :], Zraw[:, 0:T], cw[:, jt, 0:1], Zout[:, :], ALU.mult, ALU.add),
              reads=[Zraw, cw, Zout], writes=[Zout])
        kb.op("dve", lambda e: e.scalar_tensor_tensor(Zout[:, :], Zraw[:, 2:T + 2], cw[:, jt, 2:3], Zout[:, :], ALU.mult, ALU.add),
              reads=[Zraw, cw, Zout], writes=[Zout])
        kb.op("dve", lambda e: e.scalar_tensor_tensor(Zout[:, C - 1:C], Zraw[:, C + 1:C + 2], ncw[:, jt, 2:3], Zout[:, C - 1:C], ALU.mult, ALU.add),
              reads=[Zraw, ncw, Zout], writes=[Zout])
        kb.op("dve", lambda e: e.scalar_tensor_tensor(Zout[:, C:C + 1], Zraw[:, C:C + 1], ncw[:, jt, 0:1], Zout[:, C:C + 1], ALU.mult, ALU.add),
              reads=[Zraw, ncw, Zout], writes=[Zout])

    def colvec(name, src_ap, srcb, shape, rearr, **kw):
        t = kb.sb(name, shape)
        kb.dma("sp", t[:], src_ap.rearrange(rearr, **kw), reads=[srcb], writes=[t], slow=True)
        return t

    def phase_rwkv_prep(l):
        with kb.scope():
            stage = kb.sb("rstage", [128, 8, 128])
            wts = [kb.sb(f"rwt{i}", [128, 8, 128], BF16) for i in range(2)]
            cw = kb.sb("rcw", [128, 7, 3])
            for k in range(3):
                kb.dma("sp", cw[:, :, k], W["rw_conv"][l, k, :].rearrange("(j p) -> p j", p=128), reads=[W["rw_conv"]], writes=[cw], slow=True)
            ncw = kb.sb("rncw", [128, 7, 3])
            kb.op("dve", lambda e: e.tensor_scalar(ncw[:], cw[:], -1.0, None, ALU.mult), reads=[cw], writes=[ncw])
            kk_ = colvec("rkk", W["rw_k_k"][l, :], W["rw_k_k"], [128, 2], "(j p) -> p j", p=128)
            ka_ = colvec("rka", W["rw_k_a"][l, :], W["rw_k_a"], [128, 2], "(j p) -> p j", p=128)
            omka = kb.sb("romka", [128, 2])
            kb.op("dve", lambda e: e.tensor_scalar(omka[:], ka_[:], -1.0, 1.0, ALU.mult, ALU.add), reads=[ka_], writes=[omka])
            w0_ = kb.sb("rw0", [128, 2, 2])
            a0_ = kb.sb("ra0", [128, 2, 2])
            for d in range(2):
                kb.dma("sp", w0_[:, d, :], W["rw_w0"][l, d, :].rearrange("(j p) -> p j", p=128), reads=[W["rw_w0"]], writes=[w0_], slow=True)
                kb.dma("sp", a0_[:, d, :], W["rw_a0"][l, d, :].rearrange("(j p) -> p j", p=128), reads=[W["rw_a0"]], writes=[a0_], slow=True)
            wup = kb.sb("rwup", [128, 2, 256])
            kb.dma("sp", wup[0:64, :, :], W["rw_w_up"][l, :, :, :].rearrange("d k n -> k d n"), reads=[W["rw_w_up"]], writes=[wup])
            kb.dma("sp", wup[64:128, :, :], W["rw_a_up"][l, :, :, :].rearrange("d k n -> k d n"), reads=[W["rw_a_up"]], writes=[wup])
            Zraw = kb.sb("rZraw", [128, T + 2])
            Zout = kb.sb("rZout", [128, T])
            Z6 = kb.sb("rZ6", [128, T])
            kb.op("pool", lambda e: e.memset(Zraw[:, 0:1], 0.0), writes=[Zraw])
            kb.op("pool", lambda e: e.memset(Zraw[:, T + 1:T + 2], 0.0), writes=[Zraw])
            tA = [kb.sb(f"rtA{i}", [128, 512]) for i in range(2)]
            tB = [kb.sb(f"rtB{i}", [128, 512]) for i in range(2)]
            tC = [kb.sb(f"rtC{i}", [128, 512]) for i in range(2)]
            tD = [kb.sb(f"rtD{i}", [128, 512]) for i in range(2)]
            tE = [kb.sb(f"rtE{i}", [128, 512]) for i in range(2)]
            tG = [kb.sb(f"rtG{i}", [128, 512], BF16) for i in range(2)]
            vt_ = [kb.sb(f"rvt{i}", [128, 128]) for i in range(2)]
            order = [6, 0, 1, 4, 5, 2, 3, 7, 8]
            for oi, jt in enumerate(order):
                wt = wts[oi % 2]
                c0 = RW0 + jt * 128 if jt < 7 else RWG0 + (jt - 7) * 128
                load_w(l, wt, c0, 128, stage)
                if jt >= 7:
                    for ci, (t0, nt) in enumerate(TCH):
                        p = PS[ci % 4]
                        proj_fm(p, wt, 0, 128, t0, nt)
                        g = tG[ci % 2]
                        kb.op("act", lambda e, p=p, g=g, nt=nt: e.activation(g[:, 0:nt], p[:, 0:nt], AF.Silu), reads=[p], writes=[g])
                        kb.dma("sp", RS["SGT"][(jt - 7) * 128:(jt - 6) * 128, t0:t0 + nt], g[:, 0:nt], reads=[g], writes=[Buf()])
                    continue
                conv_tile(l, wt, cw, ncw, jt, Zraw, Z6 if jt == 6 else Zout)
                if jt == 6:
                    kb.op("act", lambda e: e.activation(Z6[0:64, :], Z6[0:64, :], AF.Tanh), reads=[Z6], writes=[Z6])
                elif jt in (0, 1):
                    kb.dma("sp", RS["RT"][jt * 128:(jt + 1) * 128, :], Zout[:, :], reads=[Zout], writes=[Buf()])
                elif jt in (4, 5):
                    kb.dma("sp", RS["VT"][(jt - 4) * 128:(jt - 3) * 128, :], Zout[:, :], reads=[Zout], writes=[Buf()])
                    for i in range(NT):
                        p = PS[4 + i % 2]
                        kb.op("pe", lambda e, p=p, i=i: e.transpose(p[:, 0:128], Zout[:, i * 128:(i + 1) * 128], ident_f[:]),
                              reads=[Zout, ident_f], writes=[p])
                        v = vt_[i % 2]
                        kb.op("act", lambda e, p=p, v=v: e.copy(v[:, :], p[:, 0:128]), reads=[p], writes=[v])
                        kb.dma("pool", RS["VTOK"][i * 128:(i + 1) * 128, (jt - 4) * 128:(jt - 3) * 128], v[:, :], reads=[v], writes=[Buf()])
                else:
                    pt = jt - 2
                    rows = slice(pt * 128, (pt + 1) * 128)
                    for ci, (t0, nt) in enumerate(TCH):
                        a_, b_, c_, d_, e_ = tA[ci % 2], tB[ci % 2], tC[ci % 2], tD[ci % 2], tE[ci % 2]
                        zc = Zout[:, t0:t0 + nt]
                        kb.op("dve", lambda e: e.tensor_scalar(a_[:, 0:nt], zc, kk_[:, pt:pt + 1], None, ALU.mult), reads=[Zout, kk_], writes=[a_])
                        kb.op("act", lambda e: e.activation(b_[:, 0:nt], a_[:, 0:nt], AF.Square), reads=[a_], writes=[b_])
                        p = PS[ci % 2]
                        kb.op("pe", lambda e: e.matmul(p[:, 0:nt], blk64[:], b_[:, 0:nt], start=True, stop=True), reads=[blk64, b_], writes=[p])
                        kb.op("act", lambda e: e.sqrt(b_[:, 0:nt], p[:, 0:nt]), reads=[p], writes=[b_])
                        kb.op("dve", lambda e: e.tensor_scalar(b_[:, 0:nt], b_[:, 0:nt], 1e-12, None, ALU.max), reads=[b_], writes=[b_])
                        kb.op("dve", lambda e: e.reciprocal(b_[:, 0:nt], b_[:, 0:nt]), reads=[b_], writes=[b_])
                        kb.op("dve", lambda e: e.scalar_tensor_tensor(a_[:, 0:nt], a_[:, 0:nt], -1.0, b_[:, 0:nt], ALU.mult, ALU.mult),
                              reads=[a_, b_], writes=[a_])
                        kb.dma("sp", RS["AL"][rows, t0:t0 + nt], a_[:, 0:nt], reads=[a_], writes=[Buf()])
                        for d in range(2):
                            pa = PS[2 + d]
                            kb.op("pe", lambda e: e.matmul(pa[:, 0:nt], wup[64:128, d, pt * 128:(pt + 1) * 128], Z6[64:128, t0:t0 + nt],
                                                           start=True, stop=True), reads=[wup, Z6], writes=[pa])
                            kb.op("act", lambda e: e.activation(c_[:, 0:nt], pa[:, 0:nt], AF.Sigmoid, bias=a0_[:, d, pt:pt + 1]),
                                  reads=[pa, a0_], writes=[c_])
                            kb.op("dve", lambda e: e.scalar_tensor_tensor(d_[:, 0:nt], c_[:, 0:nt], -1.0, a_[:, 0:nt], ALU.mult, ALU.mult),
                                  reads=[c_, a_], writes=[d_])
                            kb.dma("sp", RS[f"B{d}"][rows, t0:t0 + nt], d_[:, 0:nt], reads=[d_], writes=[Buf()])
                            kb.op("dve", lambda e: e.tensor_scalar(c_[:, 0:nt], c_[:, 0:nt], ka_[:, pt:pt + 1], omka[:, pt:pt + 1], ALU.mult, ALU.add),
                                  reads=[c_, ka_, omka], writes=[c_])
                            kb.op("dve", lambda e: e.tensor_tensor(e_[:, 0:nt], c_[:, 0:nt], zc, ALU.mult), reads=[c_, Zout], writes=[e_])
                            kb.dma("pool", RS[f"KD{d}"][rows, t0:t0 + nt], e_[:, 0:nt], reads=[e_], writes=[Buf()])
                            pw = PS[4 + d]
                            kb.op("pe", lambda e: e.matmul(pw[:, 0:nt], wup[0:64, d, pt * 128:(pt + 1) * 128], Z6[0:64, t0:t0 + nt],
                                                           start=True, stop=True), reads=[wup, Z6], writes=[pw])
                            kb.op("act", lambda e: e.activation(c_[:, 0:nt], pw[:, 0:nt], AF.Sigmoid, bias=w0_[:, d, pt:pt + 1]),
                                  reads=[pw, w0_], writes=[c_])
                            kb.op("dve", lambda e: e.tensor_scalar(d_[:, 0:nt], c_[:, 0:nt], -math.exp(-0.5), None, ALU.mult),
                                  reads=[c_], writes=[d_])
                            kb.dma("pool", RS[f"W{d}"][rows, t0:t0 + nt], d_[:, 0:nt], reads=[d_], writes=[Buf()])

    def phase_rwkv_scan(l):
        with kb.scope():
            ST = [kb.sb(f"ST{d}", [128, 2, 64]) for d in range(2)]
            for d in range(2):
                kb.op("pool", lambda e, d=d: e.memset(ST[d][:], 0.0), writes=[ST[d]])
            names = ("AL", "W", "B", "KD", "RT")
            ch = [[{n: kb.sb(f"c{n}{d}{i}", [128, 2, 128]) for n in names} for i in range(2)] for d in range(2)]
            vch = [[kb.sb(f"cV{d}{i}", [128, 256]) for i in range(2)] for d in range(2)]
            t1 = [kb.sb(f"st1{d}", [128, 2, 64]) for d in range(2)]
            t2 = [kb.sb(f"st2{d}", [128, 2, 64]) for d in range(2)]
            ysb = [kb.sb(f"ysb{d}", [64, 512]) for d in range(2)]
            psSA, psV, psY = [PS[0], PS[1]], [PS[2], PS[3]], [PS[4], PS[5]]
            border = [1, 0] + list(range(NT - 1, 1, -1))
            for ci in range(NT):
                cidx = [ci, border[ci]]
                cur = []
                for d in range(2):
                    c0 = cidx[d] * 128
                    tl_ = ch[d][ci % 2]
                    for n in names:
                        src = RS[n if n in ("AL", "RT") else f"{n}{d}"]
                        kb.dma("sp" if d == 0 else "pool", tl_[n][:],
                               src.t.rearrange("(pr q) t -> q pr t", q=128)[:, :, c0:c0 + 128], reads=[src], writes=[tl_[n]])
                    vv = vch[d][ci % 2]
                    kb.dma("sp" if d == 0 else "pool", vv[:], RS["VTOK"][c0:c0 + 128, :], reads=[RS["VTOK"]], writes=[vv])
                    cur.append((tl_, vv))
                for tl in range(128):
                    for d in range(2):
                        col = tl if d == 0 else 127 - tl
                        tl_, vv = cur[d]
                        S_, sa, pv, py = ST[d], psSA[d], psV[d], psY[d]
                        for pr in range(2):
                            for hp in range(2):
                                rows = slice(64 * hp, 64 * hp + 64)
                                kb.op("pe", lambda e, pr=pr, rows=rows: e.matmul(
                                    sa[rows, pr * 64:(pr + 1) * 64], tl_["AL"][rows, pr, col:col + 1].broadcast_to([64, 64]),
                                    S_[rows, pr, :], start=True, stop=True), reads=[tl_["AL"], S_], writes=[sa])
                        for pr in range(2):
                            for hp in range(2):
                                rows = slice(64 * hp, 64 * hp + 64)
                                h = 2 * pr + hp
                                kb.op("pe", lambda e, pr=pr, rows=rows, h=h: e.matmul(
                                    pv[rows, pr * 64:(pr + 1) * 64], ident_f[:, col:col + 1].broadcast_to([128, 64]),
                                    vv[:, h * 64:(h + 1) * 64], start=True, stop=True), reads=[ident_f, vv], writes=[pv])
                        for pr in range(2):
                            kb.op("dve", lambda e, pr=pr: e.tensor_scalar(
                                t1[d][:, pr, :], sa[:, pr * 64:(pr + 1) * 64], tl_["B"][:, pr, col:col + 1], None, ALU.mult),
                                reads=[sa, tl_["B"]], writes=[t1[d]])
                            kb.op("dve", lambda e, pr=pr: e.scalar_tensor_tensor(
                                t2[d][:, pr, :], pv[:, pr * 64:(pr + 1) * 64], tl_["KD"][:, pr, col:col + 1], t1[d][:, pr, :], ALU.mult, ALU.add),
                                reads=[pv, tl_["KD"], t1[d]], writes=[t2[d]])
                            kb.op("dve", lambda e, pr=pr: e.scalar_tensor_tensor(
                                S_[:, pr, :], S_[:, pr, :], tl_["W"][:, pr, col:col + 1], t2[d][:, pr, :], ALU.mult, ALU.add),
                                reads=[S_, tl_["W"], t2[d]], writes=[S_])
                        for pr in range(2):
                            for hp in range(2):
                                rows = slice(64 * hp, 64 * hp + 64)
                                h = 2 * pr + hp
                                kb.op("pe", lambda e, pr=pr, rows=rows, h=h: e.matmul(
                                    py[0:64, h * 128 + col:h * 128 + col + 1], S_[rows, pr, :], tl_["RT"][rows, pr, col:col + 1],
                                    start=True, stop=True), reads=[S_, tl_["RT"]], writes=[py])
                for d in range(2):
                    c0 = cidx[d] * 128
                    kb.op("act", lambda e, d=d: e.copy(ysb[d][:, :], psY[d][0:64, :]), reads=[psY[d]], writes=[ysb[d]])
                    dst = RS["YF" if d == 0 else "YB"]
                    kb.dma("sp", dst.t.rearrange("(h v) t -> v h t", v=64)[:, :, c0:c0 + 128],
                           ysb[d][:, :].rearrange("v (h t) -> v h t", h=4), reads=[ysb[d]], writes=[Buf()])


    def phase_rwkv_chunked(l):
        CH = 64
        NCH = T // CH
        with kb.scope():
            def ldc(nm, shape):
                t = kb.sb("k" + nm, shape)
                kb.dma("sp", t[:], CT[nm].t, reads=[CT[nm]], writes=[t])
                return t
            Ms = ldc("rw_ms", [128, 2, 64]); MTs = ldc("rw_mts", [128, 2, 64]); MTi = ldc("rw_mti", [128, 2, 64])
            id2 = ldc("rw_id2", [128, 64])
            ones = kb.sb("rones", [128, 64])
            kb.op("pool", lambda e: e.memset(ones[:], 1.0), writes=[ones])
            ST = kb.sb("cST", [128, 4, 64])
            kb.op("pool", lambda e: e.memset(ST[:], 0.0), writes=[ST])
            names = ("AL", "W", "B", "KD", "RT")
            def t4(nm, n=2, w=64):
                return [kb.sb(f"{nm}{i}", [128, 4, w]) for i in range(n)]
            IN = {n: t4("ci" + n) for n in names}
            VTK = t4("cVTK")
            CS = t4("cCS", 1)[0]; TOT = kb.sb("cTOT", [128, 4]); TMP = t4("cTMP", 1)[0]
            Epos = t4("cEp", 1)[0]; Eneg = t4("cEn", 1)[0]; Eprev = t4("cEv", 1)[0]; Etot = t4("cEt", 1)[0]; Wtot = kb.sb("cWt", [128, 4])
            Ab = t4("cAb", 1)[0]; Bb = t4("cBb", 1)[0]; Kb = t4("cKb", 1)[0]; Rb = t4("cRb", 1)[0]; Bt = t4("cBt", 1)[0]; Kt = t4("cKt", 1)[0]
            Q = t4("cQ"); P = t4("cP"); ArbT = t4("cArbT", 1)[0]; AkvT = t4("cAkvT", 1)[0]; ArkT = t4("cArkT", 1)[0]
            X = t4("cX", 2, 128); Btok = t4("cBtok", 1)[0]; Ktok = t4("cKtok", 1)[0]
            RAT = t4("cRAT", 1)[0]; McT = t4("cMcT", 1)[0]; NcS = t4("cNcS", 1)[0]; DG = t4("cDG", 1)[0]
            ysb = [kb.sb(f"cysb{d}", [64, 256]) for d in range(2)]
            border = [3, 2, 1, 0] + list(range(NCH - 1, 3, -1))
            DP = [(d, pr) for d in range(2) for pr in range(2)]
            HP = [slice(0, 64), slice(64, 128)]

            def mm_all(ps, col_fn, lhs_fn, rhs_fn, reads, start=True, stop=True, w=None):
                for dp in range(4):
                    for hp in range(2):
                        r = HP[hp]
                        c0, c1 = col_fn(dp)
                        kb.op("pe", lambda e, dp=dp, r=r, c0=c0, c1=c1: e.matmul(ps[r, c0:c1], lhs_fn(dp, r), rhs_fn(dp, r), start=start, stop=stop),
                              reads=reads, writes=[ps])

            for ci in range(NCH):
                cidx = [ci, border[ci]]
                i2 = ci % 2
                for d in range(2):
                    c0 = cidx[d] * CH
                    for n in names:
                        src = RS[n if n in ("AL", "RT") else f"{n}{d}"]
                        kb.dma("sp" if d == 0 else "pool", IN[n][i2][:, 2 * d:2 * d + 2, :],
                               src.t.rearrange("(pr q) t -> q pr t", q=128)[:, :, c0:c0 + CH], reads=[src], writes=[IN[n][i2]])
                    for hp in range(2):
                        kb.dma("sp" if d == 0 else "pool", VTK[i2][HP[hp], 2 * d:2 * d + 2, :],
                               RS["VTOK"][c0:c0 + CH, :].rearrange("t (pr hp v) -> t pr hp v", pr=2, hp=2)[:, :, hp, :],
                               reads=[RS["VTOK"]], writes=[VTK[i2]])
                al, lw, be, kd, rt, vt = IN["AL"][i2], IN["W"][i2], IN["B"][i2], IN["KD"][i2], IN["RT"][i2], VTK[i2]
                if RW_STAGE <= 1:
                    continue
                for dp in range(4):
                    kb.op("dve", lambda e, dp=dp: e.tensor_tensor_scan(CS[:, dp, :], ones[:, :], lw[:, dp, :], 0.0, ALU.mult, ALU.add),
                          reads=[ones, lw], writes=[CS])
                kb.op("dve", lambda e: e.tensor_copy(TOT[:, :], CS[:, :, CH - 1]), reads=[CS], writes=[TOT])
                kb.op("dve", lambda e: e.tensor_tensor(CS[:, 2:4, :], lw[:, 2:4, :], CS[:, 2:4, :], ALU.subtract), reads=[lw, CS], writes=[CS])
                kb.op("dve", lambda e: e.tensor_tensor(CS[:, 2:4, :], CS[:, 2:4, :], TOT[:, 2:4].unsqueeze(2).broadcast_to([128, 2, CH]), ALU.add),
                      reads=[CS, TOT], writes=[CS])
                kb.op("act", lambda e: e.activation(Epos[:], CS[:], AF.Exp), reads=[CS], writes=[Epos])
                kb.op("act", lambda e: e.activation(Eneg[:], CS[:], AF.Exp, scale=-1.0), reads=[CS], writes=[Eneg])
                kb.op("pool", lambda e: e.tensor_tensor(TMP[:], CS[:], lw[:], ALU.subtract), reads=[CS, lw], writes=[TMP])
                kb.op("act", lambda e: e.activation(Eprev[:], TMP[:], AF.Exp), reads=[TMP], writes=[Eprev])
                kb.op("dve", lambda e: e.tensor_tensor(Etot[:], TOT[:, :].unsqueeze(2).broadcast_to([128, 4, CH]), CS[:], ALU.subtract),
                      reads=[TOT, CS], writes=[Etot])
                kb.op("act", lambda e: e.activation(Etot[:], Etot[:], AF.Exp), reads=[Etot], writes=[Etot])
                kb.op("act", lambda e: e.activation(Wtot[:], TOT[:], AF.Exp), reads=[TOT], writes=[Wtot])
                kb.op("dve", lambda e: e.tensor_tensor(Ab[:], al[:], Eprev[:], ALU.mult), reads=[al, Eprev], writes=[Ab])
                kb.op("pool", lambda e: e.tensor_tensor(Bb[:], be[:], Eneg[:], ALU.mult), reads=[be, Eneg], writes=[Bb])
                kb.op("dve", lambda e: e.tensor_tensor(Kb[:], kd[:], Eneg[:], ALU.mult), reads=[kd, Eneg], writes=[Kb])
                kb.op("pool", lambda e: e.tensor_tensor(Rb[:], rt[:], Epos[:], ALU.mult), reads=[rt, Epos], writes=[Rb])
                kb.op("dve", lambda e: e.tensor_tensor(Bt[:], be[:], Etot[:], ALU.mult), reads=[be, Etot], writes=[Bt])
                kb.op("pool", lambda e: e.tensor_tensor(Kt[:], kd[:], Etot[:], ALU.mult), reads=[kd, Etot], writes=[Kt])
                if RW_STAGE <= 2:
                    continue
                PA, PB, PC, PT1, PD, PX, PPQ, PE_ = PS
                mm_all(PA, lambda dp: (dp * 128, dp * 128 + 64), lambda dp, r: Bb[r, dp, :], lambda dp, r: Ab[r, dp, :], [Bb, Ab])
                mm_all(PA, lambda dp: (dp * 128 + 64, dp * 128 + 128), lambda dp, r: Bb[r, dp, :], lambda dp, r: Rb[r, dp, :], [Bb, Rb])
                mm_all(PB, lambda dp: (dp * 128, dp * 128 + 64), lambda dp, r: Kb[r, dp, :], lambda dp, r: Ab[r, dp, :], [Kb, Ab])
                mm_all(PB, lambda dp: (dp * 128 + 64, dp * 128 + 128), lambda dp, r: Kb[r, dp, :], lambda dp, r: Rb[r, dp, :], [Kb, Rb])
                mm_all(PC, lambda dp: (dp * 64, dp * 64 + 64), lambda dp, r: Ab[r, dp, :], lambda dp, r: Bb[r, dp, :], [Ab, Bb])
                q0, p0 = Q[0], P[0]
                pav = PA[:, :].rearrange("p (dp x) -> p dp x", dp=4)
                pbv = PB[:, :].rearrange("p (dp x) -> p dp x", dp=4)
                def mk(m):
                    return m[:, :, :].unsqueeze(2).broadcast_to([128, 2, 2, 64])
                def v4(ap):
                    return ap.rearrange("p (d pr) x -> p d pr x", d=2)
                kb.op("dve", lambda e: e.tensor_tensor(v4(q0[:]), v4(pav[:, :, 0:64]), mk(MTs), ALU.mult), reads=[PA, MTs], writes=[q0])
                kb.op("dve", lambda e: e.tensor_tensor(v4(ArbT[:]), v4(pav[:, :, 64:128]), mk(MTi), ALU.mult), reads=[PA, MTi], writes=[ArbT])
                kb.op("dve", lambda e: e.tensor_tensor(v4(AkvT[:]), v4(pbv[:, :, 0:64]), mk(MTs), ALU.mult), reads=[PB, MTs], writes=[AkvT])
                kb.op("dve", lambda e: e.tensor_tensor(v4(ArkT[:]), v4(pbv[:, :, 64:128]), mk(MTi), ALU.mult), reads=[PB, MTi], writes=[ArkT])
                kb.op("dve", lambda e: e.tensor_tensor(v4(p0[:]), v4(PC[:, 0:256].rearrange("p (dp x) -> p dp x", dp=4)), mk(Ms), ALU.mult),
                      reads=[PC, Ms], writes=[p0])
                if RW_STAGE <= 3:
                    continue
                def idb(r):
                    return ident_f[r, r.start:r.start + 64]
                mm_all(PT1, lambda dp: (dp * 128, dp * 128 + 64), lambda dp, r: Ab[r, dp, :], lambda dp, r: idb(r), [Ab, ident_f])
                mm_all(PT1, lambda dp: (dp * 128 + 64, dp * 128 + 128), lambda dp, r: Bt[r, dp, :], lambda dp, r: idb(r), [Bt, ident_f])
                mm_all(PC, lambda dp: (256 + dp * 64, 256 + dp * 64 + 64), lambda dp, r: Kt[r, dp, :], lambda dp, r: idb(r), [Kt, ident_f])
                x0 = X[0]
                pt1v = PT1[:, :].rearrange("p (dp x) -> p dp x", dp=4)
                kb.op("act", lambda e: e.copy(x0[:, :, 0:64], pt1v[:, :, 0:64]), reads=[PT1], writes=[x0])
                kb.op("act", lambda e: e.copy(Btok[:], pt1v[:, :, 64:128]), reads=[PT1], writes=[Btok])
                kb.op("act", lambda e: e.copy(Ktok[:], PC[:, 256:512].rearrange("p (dp x) -> p dp x", dp=4)), reads=[PC], writes=[Ktok])
                if RW_STAGE <= 4:
                    continue
                mm_all(PD, lambda dp: (dp * 64, dp * 64 + 64), lambda dp, r: AkvT[r, dp, :], lambda dp, r: vt[r, dp, :], [AkvT, vt])
                kb.op("act", lambda e: e.copy(x0[:, :, 64:128], PD[:, 0:256].rearrange("p (dp x) -> p dp x", dp=4)), reads=[PD], writes=[x0])
                if RW_STAGE <= 5:
                    continue
                qc, pc, xc = Q[0], P[0], X[0]
                for j in range(6):
                    qn, pn, xn = Q[(j + 1) % 2], P[(j + 1) % 2], X[(j + 1) % 2]
                    mm_all(PX, lambda dp: (dp * 128, dp * 128 + 128), lambda dp, r: qc[r, dp, :], lambda dp, r: xc[r, dp, :], [qc, xc])
                    kb.op("dve", lambda e, xn=xn, xc=xc: e.tensor_tensor(xn[:], xc[:], PX[:, :].rearrange("p (dp x) -> p dp x", dp=4), ALU.add),
                          reads=[xc, PX], writes=[xn])
                    if j < 5:
                        mm_all(PPQ, lambda dp: (dp * 64, dp * 64 + 64), lambda dp, r: qc[r, dp, :], lambda dp, r: pc[r, dp, :], [qc, pc])
                        mm_all(PPQ, lambda dp: (256 + dp * 64, 256 + dp * 64 + 64), lambda dp, r: pc[r, dp, :], lambda dp, r: qc[r, dp, :], [qc, pc])
                        kb.op("act", lambda e, pn=pn: e.copy(pn[:], PPQ[:, 0:256].rearrange("p (dp x) -> p dp x", dp=4)), reads=[PPQ], writes=[pn])
                        kb.op("act", lambda e, qn=qn: e.copy(qn[:], PPQ[:, 256:512].rearrange("p (dp x) -> p dp x", dp=4)), reads=[PPQ], writes=[qn])
                    qc, pc, xc = qn, pn, xn
                if RW_STAGE <= 6:
                    continue
                mm_all(PD, lambda dp: (256 + dp * 64, 256 + dp * 64 + 64), lambda dp, r: xc[r, dp, 0:64], lambda dp, r: ArbT[r, dp, :], [xc, ArbT])
                kb.op("dve", lambda e: e.tensor_tensor(RAT[:], Rb[:], PD[:, 256:512].rearrange("p (dp x) -> p dp x", dp=4), ALU.add),
                      reads=[Rb, PD], writes=[RAT])
                mm_all(PE_, lambda dp: (dp * 64, dp * 64 + 64), lambda dp, r: xc[r, dp, 0:64], lambda dp, r: Btok[r, dp, :], [xc, Btok])
                kb.op("pool", lambda e: e.tensor_tensor(DG[:], id2[:, :].unsqueeze(1).broadcast_to([128, 4, 64]),
                                                        Wtot[:, :].unsqueeze(2).broadcast_to([128, 4, 64]), ALU.mult), reads=[id2, Wtot], writes=[DG])
                kb.op("dve", lambda e: e.tensor_tensor(McT[:], DG[:], PE_[:, 0:256].rearrange("p (dp x) -> p dp x", dp=4), ALU.add),
                      reads=[DG, PE_], writes=[McT])
                for dp in range(4):
                    for hp in range(2):
                        r = HP[hp]
                        c0 = 256 + dp * 64
                        kb.op("pe", lambda e, dp=dp, r=r, c0=c0: e.matmul(PE_[r, c0:c0 + 64], Btok[r, dp, :], xc[r, dp, 64:128], start=True, stop=False),
                              reads=[Btok, xc], writes=[PE_])
                        kb.op("pe", lambda e, dp=dp, r=r, c0=c0: e.matmul(PE_[r, c0:c0 + 64], Ktok[r, dp, :], vt[r, dp, :], start=False, stop=True),
                              reads=[Ktok, vt], writes=[PE_])
                kb.op("act", lambda e: e.copy(NcS[:], PE_[:, 256:512].rearrange("p (dp x) -> p dp x", dp=4)), reads=[PE_], writes=[NcS])
                if RW_STAGE <= 7:
                    continue
                PYs = [PA, PT1]
                for dp in range(4):
                    for hp in range(2):
                        r = HP[hp]
                        PY = PYs[hp]
                        c0 = dp * 64
                        kb.op("pe", lambda e, dp=dp, r=r, c0=c0, PY=PY: e.matmul(PY[0:64, c0:c0 + 64], ST[r, dp, :], RAT[r, dp, :], start=True, stop=False),
                              reads=[ST, RAT], writes=[PY])
                        kb.op("pe", lambda e, dp=dp, r=r, c0=c0, PY=PY: e.matmul(PY[0:64, c0:c0 + 64], xc[r, dp, 64:128], ArbT[r, dp, :], start=False, stop=False),
                              reads=[xc, ArbT], writes=[PY])
                        kb.op("pe", lambda e, dp=dp, r=r, c0=c0, PY=PY: e.matmul(PY[0:64, c0:c0 + 64], vt[r, dp, :], ArkT[r, dp, :], start=False, stop=True),
                              reads=[vt, ArkT], writes=[PY])
                for d in range(2):
                    c0 = cidx[d] * CH
                    yv = ysb[d][:, :].rearrange("v (pr hp t) -> v pr hp t", pr=2, hp=2)
                    for hp in range(2):
                        kb.op("act", lambda e, d=d, hp=hp, yv=yv: e.copy(
                            yv[:, :, hp, :], PYs[hp][0:64, d * 128:(d + 1) * 128].rearrange("v (pr t) -> v pr t", pr=2)), reads=[PYs[hp]], writes=[ysb[d]])
                    dst = RS["YF" if d == 0 else "YB"]
                    kb.dma("sp", dst.t.rearrange("(h v) t -> v h t", v=64)[:, :, c0:c0 + CH],
                           ysb[d][:, :].rearrange("v (h t) -> v h t", h=4), reads=[ysb[d]], writes=[Buf()])
                if RW_STAGE <= 8:
                    continue
                PSS = PB
                mm_all(PSS, lambda dp: (dp * 64, dp * 64 + 64), lambda dp, r: McT[r, dp, :], lambda dp, r: ST[r, dp, :], [McT, ST])
                kb.op("dve", lambda e: e.tensor_tensor(ST[:], NcS[:], PSS[:, 0:256].rearrange("p (dp x) -> p dp x", dp=4), ALU.add),
                      reads=[NcS, PSS], writes=[ST])


    def phase_rwkv_chunked3(l):
        CH = 64
        NCH = T // CH
        with kb.scope():
            def ldc(nm, shape):
                t = kb.sb("k" + nm, shape)
                kb.dma("sp", t[:], CT[nm].t, reads=[CT[nm]], writes=[t])
                return t
            Ms = ldc("rw_ms", [128, 2, 64]); MTs = ldc("rw_mts", [128, 2, 64]); MTi = ldc("rw_mti", [128, 2, 64])
            id2 = ldc("rw_id2", [128, 64])
            ones = kb.sb("rones", [128, 64])
            kb.op("pool", lambda e: e.memset(ones[:], 1.0), writes=[ones])
            ST = kb.sb("cST", [128, 4, 64])
            kb.op("pool", lambda e: e.memset(ST[:], 0.0), writes=[ST])
            names = ("AL", "W", "B", "KD", "RT")
            import types
            def alloc_set(si):
                S = types.SimpleNamespace()
                def t4(nm, n=2, w=64):
                    return [kb.sb(f"{nm}s{si}_{i}", [128, 4, w]) for i in range(n)]
                S.IN = {n: t4("ci" + n, 1)[0] for n in names}
                S.VTK = t4("cVTK", 1)[0]
                S.CS = t4("cCS", 1)[0]; S.TOT = kb.sb(f"cTOT{si}", [128, 4]); S.TMP = t4("cTMP", 1)[0]
                S.Epos = t4("cEp", 1)[0]; S.Eneg = t4("cEn", 1)[0]; S.Eprev = t4("cEv", 1)[0]; S.Etot = t4("cEt", 1)[0]; S.Wtot = kb.sb(f"cWt{si}", [128, 4])
                S.Ab = t4("cAb", 1)[0]; S.Bb = t4("cBb", 1)[0]; S.Kb = t4("cKb", 1)[0]; S.Rb = t4("cRb", 1)[0]; S.Bt = t4("cBt", 1)[0]; S.Kt = t4("cKt", 1)[0]
                S.Q = t4("cQ"); S.P = t4("cP"); S.ArbT = t4("cArbT", 1)[0]; S.AkvT = t4("cAkvT", 1)[0]; S.ArkT = t4("cArkT", 1)[0]
                S.X = t4("cX", 2, 128); S.Btok = t4("cBtok", 1)[0]; S.Ktok = t4("cKtok", 1)[0]
                S.RAT = t4("cRAT", 1)[0]; S.McT = t4("cMcT", 1)[0]; S.NcS = t4("cNcS", 1)[0]; S.DG = t4("cDG", 1)[0]
                S.ysb = [kb.sb(f"cysb{si}_{d}", [64, 256]) for d in range(2)]
                S.banks = PS[4 * si:4 * si + 4]
                return S
            SETS = [alloc_set(0), alloc_set(1)]
            border = [3, 2, 1, 0] + list(range(NCH - 1, 3, -1))
            DP = [(d, pr) for d in range(2) for pr in range(2)]
            HP = [slice(0, 64), slice(64, 128)]

            def mm_all(ps, col_fn, lhs_fn, rhs_fn, reads, start=True, stop=True, w=None):
                for dp in range(4):
                    for hp in range(2):
                        r = HP[hp]
                        c0, c1 = col_fn(dp)
                        kb.op("pe", lambda e, dp=dp, r=r, c0=c0, c1=c1: e.matmul(ps[r, c0:c1], lhs_fn(dp, r), rhs_fn(dp, r), start=start, stop=stop),
                              reads=reads, writes=[ps])

            def chunk_gen(ci, S):
                cidx = [ci, border[ci]]
                IN, VTK = S.IN, S.VTK
                for d in range(2):
                    c0 = cidx[d] * CH
                    for n in names:
                        src = RS[n if n in ("AL", "RT") else f"{n}{d}"]
                        kb.dma("sp" if d == 0 else "pool", IN[n][:, 2 * d:2 * d + 2, :],
                               src.t.rearrange("(pr q) t -> q pr t", q=128)[:, :, c0:c0 + CH], reads=[src], writes=[IN[n]])
                    for hp in range(2):
                        kb.dma("sp" if d == 0 else "pool", VTK[HP[hp], 2 * d:2 * d + 2, :],
                               RS["VTOK"][c0:c0 + CH, :].rearrange("t (pr hp v) -> t pr hp v", pr=2, hp=2)[:, :, hp, :],
                               reads=[RS["VTOK"]], writes=[VTK])
                al, lw, be, kd, rt, vt = IN["AL"], IN["W"], IN["B"], IN["KD"], IN["RT"], VTK
                CS, TOT, TMP, Epos, Eneg, Eprev, Etot, Wtot = S.CS, S.TOT, S.TMP, S.Epos, S.Eneg, S.Eprev, S.Etot, S.Wtot
                Ab, Bb, Kb, Rb, Bt, Kt, Q, P, ArbT, AkvT, ArkT = S.Ab, S.Bb, S.Kb, S.Rb, S.Bt, S.Kt, S.Q, S.P, S.ArbT, S.AkvT, S.ArkT
                X, Btok, Ktok, RAT, McT, NcS, DG, ysb = S.X, S.Btok, S.Ktok, S.RAT, S.McT, S.NcS, S.DG, S.ysb
                yield
                for dp in range(4):
                    kb.op("dve", lambda e, dp=dp: e.tensor_tensor_scan(CS[:, dp, :], ones[:, :], lw[:, dp, :], 0.0, ALU.mult, ALU.add),
                          reads=[ones, lw], writes=[CS])
                kb.op("dve", lambda e: e.tensor_copy(TOT[:, :], CS[:, :, CH - 1]), reads=[CS], writes=[TOT])
                kb.op("dve", lambda e: e.tensor_tensor(CS[:, 2:4, :], lw[:, 2:4, :], CS[:, 2:4, :], ALU.subtract), reads=[lw, CS], writes=[CS])
                kb.op("dve", lambda e: e.tensor_tensor(CS[:, 2:4, :], CS[:, 2:4, :], TOT[:, 2:4].unsqueeze(2).broadcast_to([128, 2, CH]), ALU.add),
                      reads=[CS, TOT], writes=[CS])
                kb.op("act", lambda e: e.activation(Epos[:], CS[:], AF.Exp), reads=[CS], writes=[Epos])
                kb.op("act", lambda e: e.activation(Eneg[:], CS[:], AF.Exp, scale=-1.0), reads=[CS], writes=[Eneg])
                kb.op("pool", lambda e: e.tensor_tensor(TMP[:], CS[:], lw[:], ALU.subtract), reads=[CS, lw], writes=[TMP])
                kb.op("act", lambda e: e.activation(Eprev[:], TMP[:], AF.Exp), reads=[TMP], writes=[Eprev])
                kb.op("dve", lambda e: e.tensor_tensor(Etot[:], TOT[:, :].unsqueeze(2).broadcast_to([128, 4, CH]), CS[:], ALU.subtract),
                      reads=[TOT, CS], writes=[Etot])
                kb.op("act", lambda e: e.activation(Etot[:], Etot[:], AF.Exp), reads=[Etot], writes=[Etot])
                kb.op("act", lambda e: e.activation(Wtot[:], TOT[:], AF.Exp), reads=[TOT], writes=[Wtot])
                kb.op("dve", lambda e: e.tensor_tensor(Ab[:], al[:], Eprev[:], ALU.mult), reads=[al, Eprev], writes=[Ab])
                kb.op("pool", lambda e: e.tensor_tensor(Bb[:], be[:], Eneg[:], ALU.mult), reads=[be, Eneg], writes=[Bb])
                kb.op("dve", lambda e: e.tensor_tensor(Kb[:], kd[:], Eneg[:], ALU.mult), reads=[kd, Eneg], writes=[Kb])
                kb.op("pool", lambda e: e.tensor_tensor(Rb[:], rt[:], Epos[:], ALU.mult), reads=[rt, Epos], writes=[Rb])
                kb.op("dve", lambda e: e.tensor_tensor(Bt[:], be[:], Etot[:], ALU.mult), reads=[be, Etot], writes=[Bt])
                kb.op("pool", lambda e: e.tensor_tensor(Kt[:], kd[:], Etot[:], ALU.mult), reads=[kd, Etot], writes=[Kt])
                yield
                PA, PB, PC, PT1 = S.banks
                PD, PX, PPQ, PE_ = PA, PB, PC, PT1
                mm_all(PA, lambda dp: (dp * 128, dp * 128 + 64), lambda dp, r: Bb[r, dp, :], lambda dp, r: Ab[r, dp, :], [Bb, Ab])
                mm_all(PA, lambda dp: (dp * 128 + 64, dp * 128 + 128), lambda dp, r: Bb[r, dp, :], lambda dp, r: Rb[r, dp, :], [Bb, Rb])
                mm_all(PB, lambda dp: (dp * 128, dp * 128 + 64), lambda dp, r: Kb[r, dp, :], lambda dp, r: Ab[r, dp, :], [Kb, Ab])
                mm_all(PB, lambda dp: (dp * 128 + 64, dp * 128 + 128), lambda dp, r: Kb[r, dp, :], lambda dp, r: Rb[r, dp, :], [Kb, Rb])
                mm_all(PC, lambda dp: (dp * 64, dp * 64 + 64), lambda dp, r: Ab[r, dp, :], lambda dp, r: Bb[r, dp, :], [Ab, Bb])
                q0, p0 = Q[0], P[0]
                pav = PA[:, :].rearrange("p (dp x) -> p dp x", dp=4)
                pbv = PB[:, :].rearrange("p (dp x) -> p dp x", dp=4)
                def mk(m):
                    return m[:, :, :].unsqueeze(2).broadcast_to([128, 2, 2, 64])
                def v4(ap):
                    return ap.rearrange("p (d pr) x -> p d pr x", d=2)
                kb.op("dve", lambda e: e.tensor_tensor(v4(q0[:]), v4(pav[:, :, 0:64]), mk(MTs), ALU.mult), reads=[PA, MTs], writes=[q0])
                kb.op("dve", lambda e: e.tensor_tensor(v4(ArbT[:]), v4(pav[:, :, 64:128]), mk(MTi), ALU.mult), reads=[PA, MTi], writes=[ArbT])
                kb.op("dve", lambda e: e.tensor_tensor(v4(AkvT[:]), v4(pbv[:, :, 0:64]), mk(MTs), ALU.mult), reads=[PB, MTs], writes=[AkvT])
                kb.op("dve", lambda e: e.tensor_tensor(v4(ArkT[:]), v4(pbv[:, :, 64:128]), mk(MTi), ALU.mult), reads=[PB, MTi], writes=[ArkT])
                kb.op("dve", lambda e: e.tensor_tensor(v4(p0[:]), v4(PC[:, 0:256].rearrange("p (dp x) -> p dp x", dp=4)), mk(Ms), ALU.mult),
                      reads=[PC, Ms], writes=[p0])
                yield
                def idb(r):
                    return ident_f[r, r.start:r.start + 64]
                mm_all(PT1, lambda dp: (dp * 128, dp * 128 + 64), lambda dp, r: Ab[r, dp, :], lambda dp, r: idb(r), [Ab, ident_f])
                mm_all(PT1, lambda dp: (dp * 128 + 64, dp * 128 + 128), lambda dp, r: Bt[r, dp, :], lambda dp, r: idb(r), [Bt, ident_f])
                mm_all(PC, lambda dp: (256 + dp * 64, 256 + dp * 64 + 64), lambda dp, r: Kt[r, dp, :], lambda dp, r: idb(r), [Kt, ident_f])
                x0 = X[0]
                pt1v = PT1[:, :].rearrange("p (dp x) -> p dp x", dp=4)
                kb.op("act", lambda e: e.copy(x0[:, :, 0:64], pt1v[:, :, 0:64]), reads=[PT1], writes=[x0])
                kb.op("act", lambda e: e.copy(Btok[:], pt1v[:, :, 64:128]), reads=[PT1], writes=[Btok])
                kb.op("act", lambda e: e.copy(Ktok[:], PC[:, 256:512].rearrange("p (dp x) -> p dp x", dp=4)), reads=[PC], writes=[Ktok])
                yield
                mm_all(PD, lambda dp: (dp * 64, dp * 64 + 64), lambda dp, r: AkvT[r, dp, :], lambda dp, r: vt[r, dp, :], [AkvT, vt])
                kb.op("act", lambda e: e.copy(x0[:, :, 64:128], PD[:, 0:256].rearrange("p (dp x) -> p dp x", dp=4)), reads=[PD], writes=[x0])
                yield
                qc, pc, xc = Q[0], P[0], X[0]
                for j in range(6):
                    qn, pn, xn = Q[(j + 1) % 2], P[(j + 1) % 2], X[(j + 1) % 2]
                    mm_all(PX, lambda dp: (dp * 128, dp * 128 + 128), lambda dp, r: qc[r, dp, :], lambda dp, r: xc[r, dp, :], [qc, xc])
                    kb.op("dve", lambda e, xn=xn, xc=xc: e.tensor_tensor(xn[:], xc[:], PX[:, :].rearrange("p (dp x) -> p dp x", dp=4), ALU.add),
                          reads=[xc, PX], writes=[xn])
                    if j < 5:
                        mm_all(PPQ, lambda dp: (dp * 64, dp * 64 + 64), lambda dp, r: qc[r, dp, :], lambda dp, r: pc[r, dp, :], [qc, pc])
                        mm_all(PPQ, lambda dp: (256 + dp * 64, 256 + dp * 64 + 64), lambda dp, r: pc[r, dp, :], lambda dp, r: qc[r, dp, :], [qc, pc])
                        kb.op("act", lambda e, pn=pn: e.copy(pn[:], PPQ[:, 0:256].rearrange("p (dp x) -> p dp x", dp=4)), reads=[PPQ], writes=[pn])
                        kb.op("act", lambda e, qn=qn: e.copy(qn[:], PPQ[:, 256:512].rearrange("p (dp x) -> p dp x", dp=4)), reads=[PPQ], writes=[qn])
                    qc, pc, xc = qn, pn, xn
                    yield
                yield
                mm_all(PD, lambda dp: (256 + dp * 64, 256 + dp * 64 + 64), lambda dp, r: xc[r, dp, 0:64], lambda dp, r: ArbT[r, dp, :], [xc, ArbT])
                kb.op("dve", lambda e: e.tensor_tensor(RAT[:], Rb[:], PD[:, 256:512].rearrange("p (dp x) -> p dp x", dp=4), ALU.add),
                      reads=[Rb, PD], writes=[RAT])
                mm_all(PE_, lambda dp: (dp * 64, dp * 64 + 64), lambda dp, r: xc[r, dp, 0:64], lambda dp, r: Btok[r, dp, :], [xc, Btok])
                kb.op("pool", lambda e: e.tensor_tensor(DG[:], id2[:, :].unsqueeze(1).broadcast_to([128, 4, 64]),
                                                        Wtot[:, :].unsqueeze(2).broadcast_to([128, 4, 64]), ALU.mult), reads=[id2, Wtot], writes=[DG])
                kb.op("dve", lambda e: e.tensor_tensor(McT[:], DG[:], PE_[:, 0:256].rearrange("p (dp x) -> p dp x", dp=4), ALU.add),
                      reads=[DG, PE_], writes=[McT])
                for dp in range(4):
                    for hp in range(2):
                        r = HP[hp]
                        c0 = 256 + dp * 64
                        kb.op("pe", lambda e, dp=dp, r=r, c0=c0: e.matmul(PE_[r, c0:c0 + 64], Btok[r, dp, :], xc[r, dp, 64:128], start=True, stop=False),
                              reads=[Btok, xc], writes=[PE_])
                        kb.op("pe", lambda e, dp=dp, r=r, c0=c0: e.matmul(PE_[r, c0:c0 + 64], Ktok[r, dp, :], vt[r, dp, :], start=False, stop=True),
                              reads=[Ktok, vt], writes=[PE_])
                kb.op("act", lambda e: e.copy(NcS[:], PE_[:, 256:512].rearrange("p (dp x) -> p dp x", dp=4)), reads=[PE_], writes=[NcS])
                yield
                PYs = [PA, PB]
                for dp in range(4):
                    for hp in range(2):
                        r = HP[hp]
                        PY = PYs[hp]
                        c0 = dp * 64
                        kb.op("pe", lambda e, dp=dp, r=r, c0=c0, PY=PY: e.matmul(PY[0:64, c0:c0 + 64], ST[r, dp, :], RAT[r, dp, :], start=True, stop=False),
                              reads=[ST, RAT], writes=[PY])
                        kb.op("pe", lambda e, dp=dp, r=r, c0=c0, PY=PY: e.matmul(PY[0:64, c0:c0 + 64], xc[r, dp, 64:128], ArbT[r, dp, :], start=False, stop=False),
                              reads=[xc, ArbT], writes=[PY])
                        kb.op("pe", lambda e, dp=dp, r=r, c0=c0, PY=PY: e.matmul(PY[0:64, c0:c0 + 64], vt[r, dp, :], ArkT[r, dp, :], start=False, stop=True),
                              reads=[vt, ArkT], writes=[PY])
                for d in range(2):
                    c0 = cidx[d] * CH
                    yv = ysb[d][:, :].rearrange("v (pr hp t) -> v pr hp t", pr=2, hp=2)
                    for hp in range(2):
                        kb.op("act", lambda e, d=d, hp=hp, yv=yv: e.copy(
                            yv[:, :, hp, :], PYs[hp][0:64, d * 128:(d + 1) * 128].rearrange("v (pr t) -> v pr t", pr=2)), reads=[PYs[hp]], writes=[ysb[d]])
                    dst = RS["YF" if d == 0 else "YB"]
                    kb.dma("sp", dst.t.rearrange("(h v) t -> v h t", v=64)[:, :, c0:c0 + CH],
                           ysb[d][:, :].rearrange("v (h t) -> v h t", h=4), reads=[ysb[d]], writes=[Buf()])
                PSS = PT1
                mm_all(PSS, lambda dp: (dp * 64, dp * 64 + 64), lambda dp, r: McT[r, dp, :], lambda dp, r: ST[r, dp, :], [McT, ST])
                kb.op("dve", lambda e: e.tensor_tensor(ST[:], NcS[:], PSS[:, 0:256].rearrange("p (dp x) -> p dp x", dp=4), ALU.add),
                      reads=[NcS, PSS], writes=[ST])


            def lockstep(gens):
                gens = list(gens)
                while gens:
                    nxt = []
                    for g_ in gens:
                        try:
                            next(g_)
                            nxt.append(g_)
                        except StopIteration:
                            pass
                    gens = nxt
            for ci in range(0, NCH, 2):
                lockstep([chunk_gen(ci, SETS[0]), chunk_gen(ci + 1, SETS[1])])

    def phase_rwkv_chunked2(l):
        CH = 64
        NCH = T // CH
        with kb.scope():
            def ldc(nm, shape):
                t = kb.sb("k" + nm, shape)
                kb.dma("sp", t[:], CT[nm].t, reads=[CT[nm]], writes=[t])
                return t
            MsB = ldc("rw_msb", [128, 2, 128]); MTsB = ldc("rw_mtsb", [128, 2, 128]); MTi = ldc("rw_mti", [128, 2, 64])
            identr = kb.sb("cidr", [128, 128], F32R)
            kb.op("dve", lambda e: e.tensor_copy(identr[:], ident_f[:]), reads=[ident_f], writes=[identr])
            ones = kb.sb("rones", [128, 64])
            kb.op("pool", lambda e: e.memset(ones[:], 1.0), writes=[ones])
            def bd(nm, n=1, dt=F32R):
                ts = [kb.sb(f"{nm}{i}", [128, 4, 128], dt) for i in range(n)]
                for t in ts:
                    kb.op("pool", lambda e, t=t: e.memset(t[:].bitcast(F32) if dt == F32R else t[:], 0.0), writes=[t])
                return ts
            def t4(nm, n=1, w=64, dt=F32):
                return [kb.sb(f"{nm}{i}", [128, 4, w], dt) for i in range(n)]
            f32 = lambda ap: ap.bitcast(F32)
            names = ("AL", "W", "B", "KD", "RT")
            IN = {n: t4("di" + n, 2) for n in names}
            VT = bd("dVT", 2, F32)
            VTr = bd("dVTr")[0]
            ST = bd("dST")[0]
            CS = t4("dCS")[0]; TOT = kb.sb("dTOT", [128, 4]); TMP = t4("dTMP")[0]
            Epos = t4("dEp")[0]; Eneg = t4("dEn")[0]; Eprev = t4("dEv")[0]; Etot = t4("dEt")[0]; Wtot = kb.sb("dWt", [128, 4])
            Ab = bd("dAb")[0]; Bb = bd("dBb")[0]; Kb = bd("dKb")[0]; Bt = bd("dBt")[0]; Kt = bd("dKt")[0]
            Rb = t4("dRb", 1, 64, F32R)[0]
            Q = bd("dQ", 2); P = bd("dP", 2); AkvT = bd("dAkvT")[0]
            ArbT = t4("dArbT", 1, 64, F32R)[0]; ArkT = t4("dArkT", 1, 64, F32R)[0]; RAT = t4("dRAT", 1, 64, F32R)[0]
            X = [kb.sb(f"dX{i}", [128, 4, 256], F32R) for i in range(2)]
            Btok = bd("dBtok")[0]; Ktok = bd("dKtok")[0]; McT = bd("dMcT")[0]
            NcS = bd("dNcS", 1, F32)[0]; DG = bd("dDG", 1, F32)[0]
            ysb = [kb.sb(f"dysb{d}", [128, 2, 64]) for d in range(2)]
            border = [3, 2, 1, 0] + list(range(NCH - 1, 3, -1))
            H0, H1 = slice(0, 64), slice(64, 128)
            B0, B1, B2, B3, B4, B5, B6, B7 = PS

            def mm4(ps, c0, w, lhs, rhs, reads, start=True, stop=True):
                for dp in range(4):
                    kb.op("pe", lambda e, dp=dp: e.matmul(ps[:, c0 + dp * w:c0 + (dp + 1) * w], lhs(dp), rhs(dp), start=start, stop=stop),
                          reads=reads, writes=[ps])

            def v4(ap):
                return ap.rearrange("p (d pr) x -> p d pr x", d=2)

            def mk(m, w):
                return m[:, :, :].unsqueeze(2).broadcast_to([128, 2, 2, w])

            def pv(ps, c0, w):
                return ps[:, c0:c0 + 4 * w].rearrange("p (dp x) -> p dp x", dp=4)

            for ci in range(NCH):
                cidx = [ci, border[ci]]
                i2 = ci % 2
                vt = VT[i2]
                for d in range(2):
                    c0 = cidx[d] * CH
                    q_ = "sp" if d == 0 else "pool"
                    for n in names:
                        src = RS[n if n in ("AL", "RT") else f"{n}{d}"]
                        kb.dma(q_, IN[n][i2][:, 2 * d:2 * d + 2, :],
                               src.t.rearrange("(pr q) t -> q pr t", q=128)[:, :, c0:c0 + CH], reads=[src], writes=[IN[n][i2]])
                    for hp in range(2):
                        kb.dma(q_, vt[hp * 64:(hp + 1) * 64, 2 * d:2 * d + 2, hp * 64:(hp + 1) * 64],
                               RS["VTOK"][c0:c0 + CH, :].rearrange("t (pr hp v) -> t pr hp v", pr=2, hp=2)[:, :, hp, :],
                               reads=[RS["VTOK"]], writes=[vt])
                al, lw, be, kd, rt = IN["AL"][i2], IN["W"][i2], IN["B"][i2], IN["KD"][i2], IN["RT"][i2]
                kb.op("act", lambda e: e.copy(VTr[:], vt[:]), reads=[vt], writes=[VTr])
                for dp in range(4):
                    kb.op("dve", lambda e, dp=dp: e.tensor_tensor_scan(CS[:, dp, :], ones[:, :], lw[:, dp, :], 0.0, ALU.mult, ALU.add),
                          reads=[ones, lw], writes=[CS])
                kb.op("dve", lambda e: e.tensor_copy(TOT[:, :], CS[:, :, CH - 1]), reads=[CS], writes=[TOT])
                kb.op("dve", lambda e: e.tensor_tensor(CS[:, 2:4, :], lw[:, 2:4, :], CS[:, 2:4, :], ALU.subtract), reads=[lw, CS], writes=[CS])
                kb.op("dve", lambda e: e.tensor_tensor(CS[:, 2:4, :], CS[:, 2:4, :], TOT[:, 2:4].unsqueeze(2).broadcast_to([128, 2, CH]), ALU.add),
                      reads=[CS, TOT], writes=[CS])
                kb.op("act", lambda e: e.activation(Epos[:], CS[:], AF.Exp), reads=[CS], writes=[Epos])
                kb.op("act", lambda e: e.activation(Eneg[:], CS[:], AF.Exp, scale=-1.0), reads=[CS], writes=[Eneg])
                kb.op("pool", lambda e: e.tensor_tensor(TMP[:], CS[:], lw[:], ALU.subtract), reads=[CS, lw], writes=[TMP])
                kb.op("act", lambda e: e.activation(Eprev[:], TMP[:], AF.Exp), reads=[TMP], writes=[Eprev])
                kb.op("pool", lambda e: e.tensor_tensor(Etot[:], TOT[:, :].unsqueeze(2).broadcast_to([128, 4, CH]), CS[:], ALU.subtract),
                      reads=[TOT, CS], writes=[Etot])
                kb.op("act", lambda e: e.activation(Etot[:], Etot[:], AF.Exp), reads=[Etot], writes=[Etot])
                kb.op("act", lambda e: e.activation(Wtot[:], TOT[:], AF.Exp), reads=[TOT], writes=[Wtot])
                for k_, (dst, a_, b_) in enumerate(((Ab, al, Eprev), (Bb, be, Eneg), (Kb, kd, Eneg), (Bt, be, Etot), (Kt, kd, Etot))):
                    for hi, r in enumerate((H0, H1)):
                        eng = "dve" if (k_ + hi) % 2 == 0 else "pool"
                        kb.op(eng, lambda e, dst=dst, a_=a_, b_=b_, r=r: e.tensor_tensor(dst[r, :, r.start:r.start + 64], a_[r, :, :], b_[r, :, :], ALU.mult),
                              reads=[a_, b_], writes=[dst])
                kb.op("pool", lambda e: e.tensor_tensor(Rb[:], rt[:], Epos[:], ALU.mult), reads=[rt, Epos], writes=[Rb])
                mm4(B0, 0, 128, lambda dp: Bb[:, dp, :], lambda dp: Ab[:, dp, :], [Bb, Ab])
                mm4(B1, 0, 128, lambda dp: Kb[:, dp, :], lambda dp: Ab[:, dp, :], [Kb, Ab])
                mm4(B2, 0, 128, lambda dp: Ab[:, dp, :], lambda dp: Bb[:, dp, :], [Ab, Bb])
                mm4(B3, 0, 64, lambda dp: Bb[:, dp, :], lambda dp: Rb[:, dp, :], [Bb, Rb])
                mm4(B3, 256, 64, lambda dp: Kb[:, dp, :], lambda dp: Rb[:, dp, :], [Kb, Rb])
                q0, p0, x0 = Q[0], P[0], X[0]
                kb.op("dve", lambda e: e.tensor_tensor(v4(q0[:]), v4(pv(B0, 0, 128)), mk(MTsB, 128), ALU.mult), reads=[B0, MTsB], writes=[q0])
                kb.op("dve", lambda e: e.tensor_tensor(v4(AkvT[:]), v4(pv(B1, 0, 128)), mk(MTsB, 128), ALU.mult), reads=[B1, MTsB], writes=[AkvT])
                kb.op("dve", lambda e: e.tensor_tensor(v4(p0[:]), v4(pv(B2, 0, 128)), mk(MsB, 128), ALU.mult), reads=[B2, MsB], writes=[p0])
                kb.op("dve", lambda e: e.tensor_tensor(v4(ArbT[:]), v4(pv(B3, 0, 64)), mk(MTi, 64), ALU.mult), reads=[B3, MTi], writes=[ArbT])
                kb.op("dve", lambda e: e.tensor_tensor(v4(ArkT[:]), v4(pv(B3, 256, 64)), mk(MTi, 64), ALU.mult), reads=[B3, MTi], writes=[ArkT])
                mm4(B4, 0, 128, lambda dp: Ab[:, dp, :], lambda dp: identr[:, :], [Ab, identr])
                mm4(B6, 0, 128, lambda dp: Bt[:, dp, :], lambda dp: identr[:, :], [Bt, identr])
                mm4(B7, 0, 128, lambda dp: Kt[:, dp, :], lambda dp: identr[:, :], [Kt, identr])
                mm4(B5, 0, 128, lambda dp: AkvT[:, dp, :], lambda dp: VTr[:, dp, :], [AkvT, VTr])
                kb.op("act", lambda e: e.copy(x0[:, :, 0:128], pv(B4, 0, 128)), reads=[B4], writes=[x0])
                kb.op("act", lambda e: e.copy(Btok[:], pv(B6, 0, 128)), reads=[B6], writes=[Btok])
                kb.op("act", lambda e: e.copy(Ktok[:], pv(B7, 0, 128)), reads=[B7], writes=[Ktok])
                kb.op("act", lambda e: e.copy(x0[:, :, 128:256], pv(B5, 0, 128)), reads=[B5], writes=[x0])
                qc, pc, xc = Q[0], P[0], X[0]
                for j in range(6):
                    qn, pn, xn = Q[(j + 1) % 2], P[(j + 1) % 2], X[(j + 1) % 2]
                    for hf, bank in ((0, B4), (1, B5)):
                        for dq in range(2):
                            dp = hf * 2 + dq
                            kb.op("pe", lambda e, dp=dp, dq=dq, bank=bank: e.matmul(bank[:, dq * 256:(dq + 1) * 256], qc[:, dp, :], xc[:, dp, :],
                                                                                    start=True, stop=True), reads=[qc, xc], writes=[bank])
                        kb.op("dve", lambda e, hf=hf, bank=bank, xn=xn, xc=xc: e.tensor_tensor(
                            xn[:, 2 * hf:2 * hf + 2, :], f32(xc[:, 2 * hf:2 * hf + 2, :]), bank[:, :].rearrange("p (dq x) -> p dq x", dq=2), ALU.add),
                            reads=[xc, bank], writes=[xn])
                    if j < 5:
                        mm4(B6, 0, 128, lambda dp: qc[:, dp, :], lambda dp: pc[:, dp, :], [qc, pc])
                        mm4(B7, 0, 128, lambda dp: pc[:, dp, :], lambda dp: qc[:, dp, :], [qc, pc])
                        kb.op("act", lambda e, pn=pn: e.copy(pn[:], pv(B6, 0, 128)), reads=[B6], writes=[pn])
                        kb.op("act", lambda e, qn=qn: e.copy(qn[:], pv(B7, 0, 128)), reads=[B7], writes=[qn])
                    qc, pc, xc = qn, pn, xn
                mm4(B3, 0, 64, lambda dp: xc[:, dp, 0:128], lambda dp: ArbT[:, dp, :], [xc, ArbT])
                kb.op("dve", lambda e: e.tensor_tensor(RAT[:], f32(Rb[:]), pv(B3, 0, 64), ALU.add), reads=[Rb, B3], writes=[RAT])
                mm4(B2, 0, 128, lambda dp: xc[:, dp, 0:128], lambda dp: Btok[:, dp, :], [xc, Btok])
                kb.op("pool", lambda e: e.tensor_tensor(DG[:], ident_f[:, :].unsqueeze(1).broadcast_to([128, 4, 128]),
                                                        Wtot[:, :].unsqueeze(2).broadcast_to([128, 4, 128]), ALU.mult), reads=[ident_f, Wtot], writes=[DG])
                kb.op("dve", lambda e: e.tensor_tensor(McT[:], DG[:], pv(B2, 0, 128), ALU.add), reads=[DG, B2], writes=[McT])
                for dp in range(4):
                    kb.op("pe", lambda e, dp=dp: e.matmul(B0[:, dp * 128:(dp + 1) * 128], Btok[:, dp, :], xc[:, dp, 128:256], start=True, stop=False),
                          reads=[Btok, xc], writes=[B0])
                    kb.op("pe", lambda e, dp=dp: e.matmul(B0[:, dp * 128:(dp + 1) * 128], Ktok[:, dp, :], VTr[:, dp, :], start=False, stop=True),
                          reads=[Ktok, VTr], writes=[B0])
                kb.op("act", lambda e: e.copy(NcS[:], pv(B0, 0, 128)), reads=[B0], writes=[NcS])
                for dp in range(4):
                    c0 = dp * 64
                    kb.op("pe", lambda e, dp=dp, c0=c0: e.matmul(B1[:, c0:c0 + 64], ST[:, dp, :], RAT[:, dp, :], start=True, stop=False),
                          reads=[ST, RAT], writes=[B1])
                    kb.op("pe", lambda e, dp=dp, c0=c0: e.matmul(B1[:, c0:c0 + 64], xc[:, dp, 128:256], ArbT[:, dp, :], start=False, stop=False),
                          reads=[xc, ArbT], writes=[B1])
                    kb.op("pe", lambda e, dp=dp, c0=c0: e.matmul(B1[:, c0:c0 + 64], VTr[:, dp, :], ArkT[:, dp, :], start=False, stop=True),
                          reads=[VTr, ArkT], writes=[B1])
                for d in range(2):
                    c0 = cidx[d] * CH
                    kb.op("act", lambda e, d=d: e.copy(ysb[d][:, :, :], B1[:, d * 128:(d + 1) * 128].rearrange("p (pr t) -> p pr t", pr=2)),
                          reads=[B1], writes=[ysb[d]])
                    dst = RS["YF" if d == 0 else "YB"]
                    kb.dma("sp", dst.t.rearrange("(pr q) t -> q pr t", q=128)[:, :, c0:c0 + CH], ysb[d][:, :, :], reads=[ysb[d]], writes=[Buf()])
                mm4(B6, 0, 128, lambda dp: McT[:, dp, :], lambda dp: ST[:, dp, :], [McT, ST])
                kb.op("dve", lambda e: e.tensor_tensor(ST[:], NcS[:], pv(B6, 0, 128), ALU.add), reads=[NcS, B6], writes=[ST])

    def phase_rwkv_out(l, with_ctx):
        with kb.scope():
            rk_ = colvec("rrk", W["rw_r_k"][l, :], W["rw_r_k"], [128, 2], "(j p) -> p j", p=128)
            lg_ = colvec("rlg", W["rw_ln_g"][l, :], W["rw_ln_g"], [128, 2], "(j p) -> p j", p=128)
            lb_ = colvec("rlb", W["rw_ln_b"][l, :], W["rw_ln_b"], [128, 2], "(j p) -> p j", p=128)
            nm = ("YF", "YB", "RT", "KD0", "KD1", "VT")
            tl = [{n: kb.sb(f"o{n}{i}", [128, 512]) for n in nm} for i in range(2)]
            sg = [kb.sb(f"osg{i}", [128, 512], BF16) for i in range(2)]
            ob = [kb.sb(f"oob{i}", [128, 512], BF16) for i in range(2)]
            wk = [[kb.sb(f"owk{k}{i}", [128, 512]) for k in range(3)] for i in range(2)]
            it = 0
            for pr in range(2):
                rows = slice(pr * 128, (pr + 1) * 128)
                for (t0, nt) in TCH:
                    if not with_ctx and t0 + nt <= C:
                        continue
                    t_, s_, o_, (a_, b_, c_) = tl[it % 2], sg[it % 2], ob[it % 2], wk[it % 2]
                    for k, n in enumerate(nm):
                        kb.dma("sp" if k % 2 == 0 else "pool", t_[n][:, 0:nt], RS[n][rows, t0:t0 + nt], reads=[RS[n]], writes=[t_[n]])
                    kb.dma("sp", s_[:, 0:nt], RS["SGT"][rows, t0:t0 + nt], reads=[RS["SGT"]], writes=[s_])
                    y = t_["YF"]
                    kb.op("dve", lambda e: e.tensor_tensor(y[:, 0:nt], y[:, 0:nt], t_["YB"][:, 0:nt], ALU.add), reads=[y, t_["YB"]], writes=[y])
                    p1, p2, p3 = PS[(3 * it) % 8], PS[(3 * it + 1) % 8], PS[(3 * it + 2) % 8]
                    kb.op("pe", lambda e: e.matmul(p1[:, 0:nt], blk64[:], y[:, 0:nt], start=True, stop=True), reads=[blk64, y], writes=[p1])
                    kb.op("dve", lambda e: e.scalar_tensor_tensor(a_[:, 0:nt], p1[:, 0:nt], -1.0 / 64, y[:, 0:nt], ALU.mult, ALU.add),
                          reads=[p1, y], writes=[a_])
                    kb.op("act", lambda e: e.activation(b_[:, 0:nt], a_[:, 0:nt], AF.Square), reads=[a_], writes=[b_])
                    kb.op("pe", lambda e: e.matmul(p2[:, 0:nt], blk64[:], b_[:, 0:nt], start=True, stop=True), reads=[blk64, b_], writes=[p2])
                    kb.op("dve", lambda e: e.tensor_scalar(b_[:, 0:nt], p2[:, 0:nt], 1.0 / 64, 64e-5, ALU.mult, ALU.add), reads=[p2], writes=[b_])
                    kb.op("act", lambda e: e.sqrt(b_[:, 0:nt], b_[:, 0:nt]), reads=[b_], writes=[b_])
                    kb.op("dve", lambda e: e.reciprocal(b_[:, 0:nt], b_[:, 0:nt]), reads=[b_], writes=[b_])
                    kb.op("dve", lambda e: e.tensor_tensor(a_[:, 0:nt], a_[:, 0:nt], b_[:, 0:nt], ALU.mult), reads=[a_, b_], writes=[a_])
                    kb.op("dve", lambda e: e.tensor_scalar(a_[:, 0:nt], a_[:, 0:nt], lg_[:, pr:pr + 1], lb_[:, pr:pr + 1], ALU.mult, ALU.add),
                          reads=[a_, lg_, lb_], writes=[a_])
                    kb.op("pool", lambda e: e.tensor_tensor(c_[:, 0:nt], t_["KD0"][:, 0:nt], t_["KD1"][:, 0:nt], ALU.add),
                          reads=[t_["KD0"], t_["KD1"]], writes=[c_])
                    kb.op("dve", lambda e: e.scalar_tensor_tensor(c_[:, 0:nt], t_["RT"][:, 0:nt], rk_[:, pr:pr + 1], c_[:, 0:nt], ALU.mult, ALU.mult),
                          reads=[t_["RT"], rk_, c_], writes=[c_])
                    kb.op("pe", lambda e: e.matmul(p3[:, 0:nt], blk64[:], c_[:, 0:nt], start=True, stop=True), reads=[blk64, c_], writes=[p3])
                    kb.op("dve", lambda e: e.tensor_tensor(c_[:, 0:nt], p3[:, 0:nt], t_["VT"][:, 0:nt], ALU.mult), reads=[p3, t_["VT"]], writes=[c_])
                    kb.op("dve", lambda e: e.tensor_tensor(a_[:, 0:nt], a_[:, 0:nt], c_[:, 0:nt], ALU.add), reads=[a_, c_], writes=[a_])
                    kb.op("pool", lambda e: e.tensor_tensor(o_[:, 0:nt], a_[:, 0:nt], s_[:, 0:nt], ALU.mult), reads=[a_, s_], writes=[o_])
                    kb.dma("sp", mixT[256 + pr * 128:256 + (pr + 1) * 128, t0:t0 + nt], o_[:, 0:nt], reads=[o_], writes=[Buf()])
                    it += 1


    SEGS = {"L": dict(Ls=L, A=32, cbw=32, off=C, ut="UTL"), "C": dict(Ls=C, A=2, cbw=64, off=0, ut="UTC")}

    def phase_hyena_prep(l, with_ctx):
        with kb.scope():
            stage = kb.sb("hstage", [128, 8, 128])
            wts = [kb.sb(f"hwt{i}", [128, 8, 128], BF16) for i in range(2)]
            cw = kb.sb("hcw", [128, 6, 3])
            for k in range(3):
                kb.dma("sp", cw[:, :, k], W["hy_conv"][l, k, :].rearrange("(j p) -> p j", p=128), reads=[W["hy_conv"]], writes=[cw], slow=True)
            ncw = kb.sb("hncw", [128, 6, 3])
            kb.op("dve", lambda e: e.tensor_scalar(ncw[:], cw[:], -1.0, None, ALU.mult), reads=[cw], writes=[ncw])
            Zraw = kb.sb("hZraw", [128, T + 2])
            Zout = kb.sb("hZout", [128, T])
            kb.op("pool", lambda e: e.memset(Zraw[:, 0:1], 0.0), writes=[Zraw])
            kb.op("pool", lambda e: e.memset(Zraw[:, T + 1:T + 2], 0.0), writes=[Zraw])
            ub = kb.sb("hub", [128, 32 * 128])
            tG = [kb.sb(f"htG{i}", [128, 512], BF16) for i in range(2)]
            for oi, jt in enumerate(range(8)):
                wt = wts[oi % 2]
                c0 = HY0 + jt * 128 if jt < 6 else HYG0 + (jt - 6) * 128
                load_w(l, wt, c0, 128, stage)
                if jt >= 6:
                    for ci, (t0, nt) in enumerate(TCH):
                        p = PS[ci % 4]
                        proj_fm(p, wt, 0, 128, t0, nt)
                        g = tG[ci % 2]
                        kb.op("act", lambda e, p=p, g=g, nt=nt: e.activation(g[:, 0:nt], p[:, 0:nt], AF.Silu), reads=[p], writes=[g])
                        kb.dma("sp", HS["SG"][(jt - 6) * 128:(jt - 5) * 128, t0:t0 + nt], g[:, 0:nt], reads=[g], writes=[Buf()])
                    continue
                conv_tile(l, wt, cw, ncw, jt, Zraw, Zout)
                arr, half = jt // 2, jt % 2
                for sn in (("L", "C") if with_ctx else ("L",)):
                    sg = SEGS[sn]
                    A, cbw, off = sg["A"], sg["cbw"], sg["off"]
                    G = 128 // A
                    ncg = 128 // G
                    ubv = ub[:, 0:A * 128].rearrange("p (g a c) -> p g a c", g=ncg, a=A)
                    for a in range(A):
                        p = PS[4 + (a // 4) % 4]
                        kb.op("pe", lambda e, p=p, a=a, A=A, off=off: e.transpose(
                            p[:, (a % 4) * 128:(a % 4 + 1) * 128], Zout[:, off + a:off + a + 127 * A + 1:A], ident_f[:]),
                            reads=[Zout, ident_f], writes=[p])
                        if a % 4 == 3 or a == A - 1:
                            a0 = (a // 4) * 4
                            na = a - a0 + 1
                            kb.op("act", lambda e, p=p, a0=a0, na=na, G=G: e.copy(
                                ubv[:, :, a0:a0 + na, :], p[:, 0:na * 128].rearrange("p (a g c) -> p g a c", a=na, c=G)), reads=[p], writes=[ub])
                    nb = 128 // cbw
                    bsz = A * cbw
                    for b in range(nb):
                        dst = HS[sg["ut"]][arr, half * nb + b, :, :]
                        kb.dma("sp" if b % 2 == 0 else "pool", dst, ub[:, b * bsz:(b + 1) * bsz], reads=[ub], writes=[Buf()])

    def cmul(dre, dim_, sre, sim, tre, tim, conj, srcb, tabb, dstb, tmp):
        t1, t2 = tmp
        sh = tuple(slice(None) for _ in range(1))
        kb.op("dve", lambda e: e.tensor_tensor(t1, sre, tre, ALU.mult), reads=srcb + tabb, writes=[dstb[2]])
        kb.op("dve", lambda e: e.tensor_tensor(t2, sim, tim, ALU.mult), reads=srcb + tabb, writes=[dstb[3]])
        kb.op("pool", lambda e: e.tensor_tensor(dre, t1, t2, ALU.add if conj else ALU.subtract), reads=[dstb[2], dstb[3]], writes=[dstb[0]])
        kb.op("dve", lambda e: e.tensor_tensor(t1, sim, tre, ALU.mult), reads=srcb + tabb + [dstb[0]], writes=[dstb[2]])
        kb.op("dve", lambda e: e.tensor_tensor(t2, sre, tim, ALU.mult), reads=srcb + tabb + [dstb[0]], writes=[dstb[3]])
        kb.op("pool", lambda e: e.tensor_tensor(dim_, t1, t2, ALU.subtract if conj else ALU.add), reads=[dstb[2], dstb[3]], writes=[dstb[1]])

    def phase_hyena_main(l, with_ctx):
        PI = math.pi
        with kb.scope():
            fw1 = kb.sb("hfw1", [33, 64])
            fw2 = kb.sb("hfw2", [64, 64])
            fw3 = kb.sb("hfw3", [64, 1024])
            kb.dma("sp", fw1[:], W["hy_fw1"][l, :, :], reads=[W["hy_fw1"]], writes=[fw1])
            kb.dma("sp", fw2[:], W["hy_fw2"][l, :, :], reads=[W["hy_fw2"]], writes=[fw2])
            kb.dma("sp", fw3[:], W["hy_fw3"][l, :, :], reads=[W["hy_fw3"]], writes=[fw3])
            fb1 = colvec("hfb1", W["hy_fb1"][l, :], W["hy_fb1"], [64, 1], "(d o) -> d o", o=1)
            fb2 = colvec("hfb2", W["hy_fb2"][l, :], W["hy_fb2"], [64, 1], "(d o) -> d o", o=1)
            frq = colvec("hfrq", W["hy_freq"][l, :], W["hy_freq"], [64, 1], "(d o) -> d o", o=1)
            brow = kb.sb("hbrow", [1, 512])
            kb.dma("sp", brow[:], W["hy_bias"][l, :, :].rearrange("o c -> (o c)").rearrange("(x n) -> x n", x=1), reads=[W["hy_bias"]], writes=[brow])
            for sn in (("L", "C") if with_ctx else ("L",)):
                sg = SEGS[sn]
                Ls, A, cbw, off = sg["Ls"], sg["A"], sg["cbw"], sg["off"]
                G = 128 // A
                N = 2 * Ls
                ngr = cbw // G
                nblk = 256 // cbw
                pre = f"hy{sn}_"
                with kb.scope():
                    def ld(nm, shape):
                        t = kb.sb("k" + nm, shape)
                        src = CT[pre + nm]
                        kb.dma("sp", t[:], src.t, reads=[src], writes=[t])
                        return t
                    cstage = kb.sb("kcstage", [128, 1024])

                    def ldr(nm, shape):
                        tr = kb.sb("r" + nm, shape, F32R)
                        n_ = 1
                        for d_ in shape[1:]:
                            n_ *= d_
                        src = CT[pre + nm]
                        names_ = " ".join(f"d{i}" for i in range(len(shape) - 1))
                        flat = lambda ap: ap if len(shape) == 2 else ap.rearrange(f"p {names_} -> p ({names_})")
                        kb.dma("sp", cstage[:, 0:n_], flat(src.t), reads=[src], writes=[cstage])
                        kb.op("dve", lambda e: e.tensor_copy(flat(tr[:]), cstage[:, 0:n_]), reads=[cstage], writes=[tr])
                        return tr
                    F256 = ldr("F256", [128, 2, 512]); TWC = ld("TWC", [128, 256]); TWS = ld("TWS", [128, 256])
                    Dre = ldr("Dre", [128, 128]); Dim = ldr("Dim", [128, 128]); nDim = ldr("nDim", [128, 128])
                    E1 = ldr("E1", [128, 256]); E2 = ldr("E2", [128, 256])
                    TW2C = ld("TW2C", [128, 2, 128]); TW2S = ld("TW2S", [128, 2, 128])
                    IC = ldr("IC", [128, 2, 128]); IS = ldr("IS", [128, 2, 128])
                    h2T = kb.sb("h2T", [64, N])
                    with kb.scope():
                        zT = kb.sb("zT", [33, N])
                        kb.dma("sp", zT[:], CT[pre + "zT"].t, reads=[CT[pre + "zT"]], writes=[zT])
                        h1T = kb.sb("h1T", [64, N])
                        arg = [kb.sb(f"harg{i}", [64, 512]) for i in range(2)]
                        wr = [kb.sb(f"hwr{i}", [64, 512]) for i in range(2)]
                        for (src, K_, wgt, bcol, dst) in ((zT, 33, fw1, fb1, h1T), (h1T, 64, fw2, fb2, h2T)):
                            for ci, n0 in enumerate(range(0, N, 512)):
                                p = PS[ci % 4]
                                ag = arg[ci % 2]
                                kb.op("pe", lambda e: e.matmul(p[0:64, :], wgt[0:K_, :], src[0:K_, n0:n0 + 512], start=True, stop=True),
                                      reads=[wgt, src], writes=[p])
                                kb.op("dve", lambda e: e.tensor_scalar(ag[:, :], p[0:64, :], bcol[:, 0:1], frq[:, 0:1], ALU.add, ALU.mult),
                                      reads=[p, bcol, frq], writes=[ag])
                                for _w in range(2):
                                    kb.op("dve", lambda e: e.tensor_scalar(wr[0][:, :], ag[:, :], PI, -2 * PI, ALU.is_gt, ALU.mult), reads=[ag], writes=[wr[0]])
                                    kb.op("dve", lambda e: e.tensor_scalar(wr[1][:, :], ag[:, :], -PI, 2 * PI, ALU.is_lt, ALU.mult), reads=[ag], writes=[wr[1]])
                                    kb.op("dve", lambda e: e.tensor_tensor(ag[:, :], ag[:, :], wr[0][:, :], ALU.add), reads=[ag, wr[0]], writes=[ag])
                                    kb.op("dve", lambda e: e.tensor_tensor(ag[:, :], ag[:, :], wr[1][:, :], ALU.add), reads=[ag, wr[1]], writes=[ag])
                                kb.op("act", lambda e: e.activation(dst[:, n0:n0 + 512], ag[:, :], AF.Sin), reads=[ag], writes=[dst])
                    KT1 = kb.sb("KT", [128, 2, ngr, A, G])
                    KTr1 = kb.sb("KTr", [128, 2, ngr, A, G], F32R)
                    KT = [KT1, KT1]
                    KTr = [KTr1, KTr1]
                    uvr = kb.sb("huvr", [128, ngr, A * G], F32R)
                    KS = [kb.sb(f"KS{o}", [128, ngr, 512]) for o in range(2)]
                    DECt = kb.sb("DECt", [128, 2, ngr, A, G])
                    part = kb.sb("hpart", [128, cbw])
                    rn = kb.sb("hrn", [128, cbw])
                    ex = kb.sb("hex", [1, cbw])
                    uv = kb.sb("huv", [128, ngr, A * G]); x1 = kb.sb("hx1", [128, ngr, A * G]); x2 = kb.sb("hx2", [128, ngr, A * G])
                    u2 = kb.sb("hu2", [128, ngr, A * G], F32R); res = kb.sb("hres", [128, A, cbw])
                    dts = (F32R, F32R, F32, F32)
                    NL = 4 if ngr >= 4 else 2
                    BpS = [[kb.sb(f"hBp{b}{i}", [128, 256], dts[i]) for i in range(4)] for b in range(NL)]
                    BpbS = [[Buf() for _ in range(4)] for b in range(NL)]
                    YpS = [[kb.sb(f"hYp{b}{i}", [128, 256], dts[i]) for i in range(4)] for b in range(NL)]
                    YpbS = [[Buf() for _ in range(4)] for b in range(NL)]
                    GpS = [[kb.sb(f"hGp{b}{i}", [128, 2, 128], dts[i]) for i in range(4)] for b in range(NL)]
                    GpbS = [[Buf() for _ in range(4)] for b in range(NL)]
                    fctr = [0]
                    sgm = kb.sb("hsgm", [cbw, Ls], BF16)

                    def fwd_fft(lhs_chunks, lhs_bufs, psB, psX):
                        n = len(lhs_chunks)
                        fctr[0] += 1
                        Bp, Bpb = BpS[fctr[0] % NL], BpbS[fctr[0] % NL]
                        for i, (ap, hf) in enumerate(lhs_chunks):
                            kb.op("pe", lambda e, ap=ap, hf=hf, i=i: e.matmul(psB[:, :], ap, F256[:, hf, :], start=(i == 0), stop=(i == n - 1)),
                                  reads=lhs_bufs + [F256], writes=[psB])
                        yield
                        cmul(Bp[0][:, :], Bp[1][:, :], psB[:, 0:256], psB[:, 256:512], TWC[:, :], TWS[:, :], True,
                             [psB], [TWC, TWS], Bpb, (Bp[2][:, :], Bp[3][:, :]))
                        yield
                        kb.op("pe", lambda e: e.matmul(psX[:, 0:256], Dre[:, :], Bp[0][:, :], start=True, stop=False), reads=[Dre, Bpb[0]], writes=[psX])
                        kb.op("pe", lambda e: e.matmul(psX[:, 0:256], nDim[:, :], Bp[1][:, :], start=False, stop=True), reads=[nDim, Bpb[1]], writes=[psX])
                        kb.op("pe", lambda e: e.matmul(psX[:, 256:512], Dim[:, :], Bp[0][:, :], start=True, stop=False), reads=[Dim, Bpb[0]], writes=[psX])
                        kb.op("pe", lambda e: e.matmul(psX[:, 256:512], Dre[:, :], Bp[1][:, :], start=False, stop=True), reads=[Dre, Bpb[1]], writes=[psX])

                    def conv_group(src, src_b, g, o, mulv, mul_b, dst_ap, dst_b, it):
                        ln = it % NL
                        if NL == 4:
                            psB, psX, psG, psy = PS[2 * ln], PS[2 * ln + 1], PS[2 * ln], PS[2 * ln + 1]
                        else:
                            psB, psX, psG, psy = PS[it % 2], PS[2 + it % 2], PS[4 + it % 2], PS[6 + it % 2]
                        Yp, Ypb, Gp, Gpb = YpS[ln], YpbS[ln], GpS[ln], GpbS[ln]
                        yield from fwd_fft([(src[:, g, :], 0)], [src_b], psB, psX)
                        yield
                        cmul(Yp[0][:, :], Yp[1][:, :], psX[:, 0:256], psX[:, 256:512], KS[o][:, g, 0:256], KS[o][:, g, 256:512], False,
                             [psX], [KS[o]], Ypb, (Yp[2][:, :], Yp[3][:, :]))
                        yield
                        for chn in range(2):
                            fs = slice(chn * 128, (chn + 1) * 128)
                            kb.op("pe", lambda e, fs=fs, chn=chn: e.matmul(psG[:, chn * 256:(chn + 1) * 256], Yp[0][:, fs], E1[:, :], start=True, stop=False),
                                  reads=[Ypb[0], E1], writes=[psG])
                            kb.op("pe", lambda e, fs=fs, chn=chn: e.matmul(psG[:, chn * 256:(chn + 1) * 256], Yp[1][:, fs], E2[:, :], start=False, stop=True),
                                  reads=[Ypb[1], E2], writes=[psG])
                        yield
                        pg = psG[:, :].rearrange("p (ch ri c) -> p ch ri c", ch=2, ri=2)
                        cmul(Gp[0][:, :, :], Gp[1][:, :, :], pg[:, :, 0, :], pg[:, :, 1, :], TW2C[:, :, :], TW2S[:, :, :], False,
                             [psG], [TW2C, TW2S], Gpb, (Gp[2][:, :, :], Gp[3][:, :, :]))
                        yield
                        k = 0
                        for chn in range(2):
                            for (tab, gsrc, gb) in ((IC, Gp[0], Gpb[0]), (IS, Gp[1], Gpb[1])):
                                kb.op("pe", lambda e, chn=chn, tab=tab, gsrc=gsrc, k=k: e.matmul(
                                    psy[:, 0:128], tab[:, chn, :], gsrc[:, chn, :], start=(k == 0), stop=(k == 3)), reads=[tab, gb], writes=[psy])
                                k += 1
                        yield
                        kb.op("dve", lambda e: e.tensor_tensor(dst_ap, psy[:, 0:128].rearrange("p (c a) -> p a c", a=A),
                                                               mulv[:, g, :].rearrange("p (a c) -> p a c", c=G), ALU.mult),
                              reads=[psy, mul_b], writes=[dst_b])

                    def lockstep(gens):
                        gens = list(gens)
                        while gens:
                            nxt = []
                            for g_ in gens:
                                try:
                                    next(g_)
                                    nxt.append(g_)
                                except StopIteration:
                                    pass
                            gens = nxt

                    def spec_group(o, g, it):
                        if NL == 4:
                            psB, psX = PS[2 * (it % 4)], PS[2 * (it % 4) + 1]
                        else:
                            psB, psX = PS[it % 2], PS[2 + it % 2]
                        yield from fwd_fft([(KTr[o][:, 0, g, :, :].rearrange("p a c -> p (a c)"), 0),
                                            (KTr[o][:, 1, g, :, :].rearrange("p a c -> p (a c)"), 1)], [KTr[o]], psB, psX)
                        yield
                        kb.op("act", lambda e: e.copy(KS[o][:, g, :], psX[:, :]), reads=[psX], writes=[KS[o]])

                    git = 0
                    for cb in range(nblk):
                        kb.dma("sp", DECt[:].rearrange("p h g a c -> p (h g a c)"), CT[pre + "DEC"][cb, :, :], reads=[CT[pre + "DEC"]], writes=[DECt])
                        for ai, tile_ in enumerate((uv, x1, x2)):
                            kb.dma("pool", tile_[:].rearrange("p g x -> p (g x)"), HS[sg["ut"]][ai, cb, :, :], reads=[HS[sg["ut"]]], writes=[tile_])
                        for o in range(2):
                            for hf in range(2):
                                col0 = o * 512 + hf * 256 + cb * cbw
                                npb = 512 // cbw
                                for a in range(A):
                                    p = PS[(a // npb) % 4]
                                    kb.op("pe", lambda e, p=p, a=a, hf=hf, col0=col0, npb=npb: e.matmul(
                                        p[:, (a % npb) * cbw:(a % npb + 1) * cbw], h2T[0:64, hf * 128 * A + a:hf * 128 * A + a + 127 * A + 1:A],
                                        fw3[0:64, col0:col0 + cbw], start=True, stop=True), reads=[h2T, fw3], writes=[p])
                                    if a % npb == npb - 1 or a == A - 1:
                                        a0 = (a // npb) * npb
                                        na = a - a0 + 1
                                        kb.op("dve", lambda e, p=p, a0=a0, na=na, hf=hf, o=o: e.tensor_tensor(
                                            KT[o][:, hf, :, a0:a0 + na, :], p[:, 0:na * cbw].rearrange("p (a g c) -> p g a c", a=na, c=G),
                                            DECt[:, hf, :, a0:a0 + na, :], ALU.mult), reads=[p, DECt], writes=[KT[o]])
                            kb.op("dve", lambda e, o=o: e.tensor_reduce(part[:, :].rearrange("p (g c) -> p g c", c=G),
                                                                        KT[o][:, :, :, :, :].rearrange("p h g a c -> p g c h a"), AX.XY, ALU.add,
                                                                        apply_absolute_value=True), reads=[KT[o]], writes=[part])
                            pe_ = PS[4]
                            kb.op("pe", lambda e, o=o: e.matmul(pe_[0:1, 0:cbw], h2T[0:64, 0:1], fw3[0:64, o * 512 + 256 + cb * cbw:o * 512 + 256 + (cb + 1) * cbw],
                                                                start=True, stop=True), reads=[h2T, fw3], writes=[pe_])
                            kb.op("act", lambda e: e.activation(ex[0:1, :], pe_[0:1, 0:cbw], AF.Abs), reads=[pe_], writes=[ex])
                            kb.op("dve", lambda e: e.tensor_tensor(part[0:1, :], part[0:1, :], ex[0:1, :], ALU.add), reads=[part, ex], writes=[part])
                            pt_ = PS[5]
                            kb.op("pe", lambda e: e.matmul(pt_[:, 0:cbw], ones_f[:, :], part[:, :], start=True, stop=True), reads=[ones_f, part], writes=[pt_])
                            kb.op("dve", lambda e: e.reciprocal(rn[:, :], pt_[:, 0:cbw]), reads=[pt_], writes=[rn])
                            for hf in range(2):
                                kb.op("dve", lambda e, o=o, hf=hf: e.tensor_tensor(
                                    KTr[o][:, hf, :, :, :], KT[o][:, hf, :, :, :],
                                    rn[:, :].rearrange("p (g c) -> p g c", c=G).unsqueeze(2).broadcast_to([128, ngr, A, G]), ALU.mult),
                                    reads=[KT[o], rn], writes=[KTr[o]])
                            kb.op("dve", lambda e, o=o: e.tensor_tensor(
                                KTr[o][0:1, 0, :, 0, :], KTr[o][0:1, 0, :, 0, :].bitcast(F32),
                                brow[0:1, o * 256 + cb * cbw:o * 256 + (cb + 1) * cbw].rearrange("p (g c) -> p g c", c=G), ALU.add),
                                reads=[KTr[o], brow], writes=[KTr[o]])
                            for g in range(0, ngr, NL):
                                gg = [g_ for g_ in range(g, min(ngr, g + NL))]
                                lockstep([spec_group(o, g_, git + k_) for k_, g_ in enumerate(gg)])
                                git += len(gg)
                        kb.op("act", lambda e: e.copy(uvr[:], uv[:]), reads=[uv], writes=[uvr])
                        for g in range(0, ngr, NL):
                            gg = [g_ for g_ in range(g, min(ngr, g + NL))]
                            lockstep([conv_group(uvr, uvr, g_, 0, x1, x1, u2[:, g_, :].rearrange("p (a c) -> p a c", c=G), u2, git + k_)
                                      for k_, g_ in enumerate(gg)])
                            git += len(gg)
                        for g in range(0, ngr, NL):
                            gg = [g_ for g_ in range(g, min(ngr, g + NL))]
                            lockstep([conv_group(u2, u2, g_, 1, x2, x2, res[:, :, g_ * G:(g_ + 1) * G], res, git + k_)
                                      for k_, g_ in enumerate(gg)])
                            git += len(gg)
                        kb.dma("sp", sgm[:], HS["SG"][cb * cbw:(cb + 1) * cbw, off:off + Ls], reads=[HS["SG"]], writes=[sgm])
                        Fv = sgm[:, :].rearrange("c (p a) -> c p a", a=A)
                        for a in range(A):
                            p = PS[4 + (a // 4) % 4]
                            kb.op("pe", lambda e, p=p, a=a: e.transpose(p[0:cbw, (a % 4) * 128:(a % 4 + 1) * 128], res[:, a, :], ident_f[:]),
                                  reads=[res, ident_f], writes=[p])
                            if a % 4 == 3 or a == A - 1:
                                a0 = (a // 4) * 4
                                na = a - a0 + 1
                                kb.op("dve", lambda e, p=p, a0=a0, na=na: e.tensor_tensor(
                                    Fv[:, :, a0:a0 + na], p[0:cbw, 0:na * 128].rearrange("c (a p) -> c p a", p=128), Fv[:, :, a0:a0 + na], ALU.mult),
                                    reads=[p, sgm], writes=[sgm])
                        kb.dma("sp", mixT[cb * cbw:(cb + 1) * cbw, off:off + Ls], sgm[:, :], reads=[sgm], writes=[Buf()])

    dbgn = [n for n, _ in dbg]
    for l in range(depth):
        last = (l == DEPTH - 1)
        with kb.scope():
            hT = kb.sb("hT", [128, 8, T], BF16)
            G1 = kb.sb("G1", [128, 2, D])
            SH = kb.sb("SH", [128, 2, D])
            phase_mod(l)
            phase_norm(l)
            if "noattn" not in dbgn:
                if os.environ.get("ATTN_ONLY", "") != "dense":
                    phase_attn(l, False, not last)
                if os.environ.get("ATTN_ONLY", "") != "window":
                    phase_attn(l, True, not last)
            if "norw" not in dbgn:
                phase_rwkv_prep(l)
            if "nohy" not in dbgn:
                phase_hyena_prep(l, not last)
            if "hT" in dbgn:
                tmp = kb.sb("dbghT", [128, T])
                for j in range(8):
                    kb.op("dve", lambda e, j=j, tmp=tmp: e.tensor_copy(tmp[:], hT[:, j, :]), reads=[hT], writes=[tmp])
                    kb.dma("sp", dbg_t["hT"][:, j, :], tmp[:], reads=[tmp], writes=[dbg_t["hT"]])
        if "norw" not in dbgn:
            {0: phase_rwkv_chunked, 1: phase_rwkv_chunked2, 3: phase_rwkv_chunked3}[RW_V2](l)
            phase_rwkv_out(l, not last)
        if "nohy" not in dbgn:
            phase_hyena_main(l, not last)
        if "noout" not in dbgn:
            phase_out(l, last)
    for n, s_ in dbg:
        if n == "xres":
            with kb.scope():
                tx = kb.sb("dbgx", [128, D])
                for i in range(NT):
                    kb.dma("sp", tx[:], xres[i * 128:(i + 1) * 128, :], reads=[xres_b[i]], writes=[tx])
                    kb.dma("sp", dbg_t[n][i * 128:(i + 1) * 128, :], tx[:], reads=[tx], writes=[dbg_t[n]])
        if n == "mixT":
            with kb.scope():
                tmpb = kb.sb("dbgmb", [128, T], BF16)
                tmpf = kb.sb("dbgmf", [128, T])
                for j in range(8):
                    kb.dma("sp", tmpb[:], mixT[j * 128:(j + 1) * 128, :], reads=[mixT], writes=[tmpb])
                    kb.op("dve", lambda e, tmpb=tmpb, tmpf=tmpf: e.tensor_copy(tmpf[:], tmpb[:]), reads=[tmpb], writes=[tmpf])
                    kb.dma("sp", dbg_t[n][j * 128:(j + 1) * 128, :], tmpf[:], reads=[tmpf], writes=[dbg_t[n]])
    kb.finish()
    kb.es.close()
    return kb, cst


_PROG = {}


def kernel(**inputs):
    if "p" not in _PROG:
        _PROG["p"] = build()
    kb, cst = _PROG["p"]
    f = lambda a: np.ascontiguousarray(np.asarray(a, dtype=np.float32))
    shared = {}
    for n in inputs:
        if n in ("x", "c", "ctx", "c_ctx"):
            continue
        shared[n] = f(inputs[n])
    shared["c_ctx"] = f(inputs["c_ctx"])
    for n, a in cst.items():
        shared["k_" + n] = np.ascontiguousarray(a)
    x, c, ctx = f(inputs["x"]), f(inputs["c"]), f(inputs["ctx"])
    B = x.shape[0]
    in_maps = []
    for b in range(B):
        m = dict(shared)
        m["x"] = np.ascontiguousarray(x[b])
        m["c"] = np.ascontiguousarray(c[b])
        m["ctx"] = np.ascontiguousarray(ctx[b])
        in_maps.append(m)
    res = run_bass_kernel_spmd(kb.nc, in_maps, core_ids=list(range(B)))
    return np.stack([np.asarray(res.results[b]["out"], dtype=np.float32) for b in range(B)], axis=0)
```
